# Optimizing a Trainium2 kernel written in Bass

```python
import jax, jax.numpy as jnp
from jax import lax
import numpy as np

D_MODEL = 1024
BATCH = 8
SEQ = 2048
DEPTH = 1
DEC_BATCH = 128
DEC_SEQ = 1
PAST_LEN = 16384
PAGE_SIZE = 128

POOL_WINDOWS = (2, 4, 8, 16)
POOL_GROUPS = len(POOL_WINDOWS)
D_POOL = D_MODEL
POOL_GW = D_POOL // POOL_GROUPS
POOL_BUF = max(POOL_WINDOWS) - 1
SSD_EXPAND = 2
D_INNER = SSD_EXPAND * D_MODEL
SSD_HEADDIM = 64
SSD_HEADS = D_INNER // SSD_HEADDIM
SSD_GROUPS = 8
SSD_HPG = SSD_HEADS // SSD_GROUPS
D_STATE = 128
CONV_W = 4
CONV_DIM = D_INNER + 2 * SSD_GROUPS * D_STATE
SSD_CHUNK = 128
D_FF = ((8 * D_MODEL // 3 + 255) // 256) * 256
N_BRANCH = 2
N_MOD = 6
COL_Z = D_POOL
COL_XBC = COL_Z + D_INNER
COL_DT = COL_XBC + CONV_DIM
COL_GATE = COL_DT + SSD_HEADS
IN_COLS = COL_GATE + N_BRANCH * D_MODEL
EPS = 1e-6

kernel_name = "hybrid_pool_ssd_adaln_decoder_step"


def rmsnorm(x, w):
    xf = x.astype(jnp.float32)
    y = xf * lax.rsqrt(jnp.mean(xf * xf, axis=-1, keepdims=True) + EPS) * w.astype(jnp.float32)
    return y.astype(x.dtype)


def pool_mixer(u, prev, pos0, w_pool, pool_scale):
    bsz, L, _ = u.shape
    full = jnp.concatenate([prev.astype(u.dtype), u], axis=1)
    cs = jnp.cumsum(full.astype(jnp.float32), axis=1)
    cs = jnp.pad(cs, ((0, 0), (1, 0), (0, 0)))
    pos = pos0 + jnp.arange(L)
    hi = cs[:, POOL_BUF + 1:POOL_BUF + 1 + L]
    outs = []
    for g, w in enumerate(POOL_WINDOWS):
        sl = slice(g * POOL_GW, (g + 1) * POOL_GW)
        lo = cs[:, POOL_BUF + 1 - w:POOL_BUF + 1 - w + L, sl]
        cnt = jnp.minimum(pos + 1, w).astype(jnp.float32)[None, :, None]
        outs.append((hi[:, :, sl] - lo) / cnt)
    pooled = jnp.concatenate(outs, axis=-1) - u.astype(jnp.float32)
    pg = pooled.reshape(bsz, L, POOL_GROUPS, POOL_GW)
    y = jnp.einsum("blgi,gio->blgo", pg, w_pool.astype(jnp.float32)).reshape(bsz, L, D_MODEL)
    y = y * pool_scale.astype(jnp.float32)
    return y.astype(u.dtype), full[:, -POOL_BUF:]


def causal_conv(xbc, prev, w, b):
    L = xbc.shape[1]
    full = jnp.concatenate([prev.astype(xbc.dtype), xbc], axis=1)
    out = b
    for k in range(CONV_W):
        out = out + full[:, k:k + L] * w[k]
    return jax.nn.silu(out), full[:, -(CONV_W - 1):]


def ssd_scan(xh, dt, A, Bm, Cm, h0):
    bsz, L = xh.shape[:2]
    Q = min(SSD_CHUNK, L)
    pad = (-L) % Q
    xh, dt, Bm, Cm = (t.astype(jnp.float32) for t in (xh, dt, Bm, Cm))
    if pad:
        padw = lambda t: jnp.pad(t, [(0, 0), (0, pad)] + [(0, 0)] * (t.ndim - 2))
        xh, dt, Bm, Cm = padw(xh), padw(dt), padw(Bm), padw(Cm)
    nc = (L + pad) // Q
    x = xh.reshape(bsz, nc, Q, SSD_GROUPS, SSD_HPG, SSD_HEADDIM)
    dtc = dt.reshape(bsz, nc, Q, SSD_GROUPS, SSD_HPG)
    Bc = Bm.reshape(bsz, nc, Q, SSD_GROUPS, D_STATE)
    Cc = Cm.reshape(bsz, nc, Q, SSD_GROUPS, D_STATE)
    dA = dtc * A.astype(jnp.float32).reshape(SSD_GROUPS, SSD_HPG)
    Acs = jnp.cumsum(dA, axis=2)
    xdt = x * dtc[..., None]
    mask = jnp.tril(jnp.ones((Q, Q), dtype=bool))[None, None, :, :, None, None]
    seg = Acs[:, :, :, None] - Acs[:, :, None]
    decay = jnp.exp(jnp.where(mask, seg, -jnp.inf))
    CB = jnp.einsum("bclgn,bcsgn->bclsg", Cc, Bc)
    scores = CB[..., None] * decay
    y_diag = jnp.einsum("bclsgr,bcsgrp->bclgrp", scores, xdt)
    decay_to_end = jnp.exp(Acs[:, :, -1:] - Acs)
    chunk_states = jnp.einsum("bclgn,bclgr,bclgrp->bcgrpn", Bc, decay_to_end, xdt)
    chunk_decay = jnp.exp(Acs[:, :, -1])
    h0r = h0.astype(jnp.float32).reshape(bsz, SSD_GROUPS, SSD_HPG, SSD_HEADDIM, D_STATE)

    def step(h, inp):
        st, dec = inp
        return h * dec[..., None, None] + st, h

    h_last, h_prev = lax.scan(step, h0r, (jnp.moveaxis(chunk_states, 1, 0), jnp.moveaxis(chunk_decay, 1, 0)))
    h_prev = jnp.moveaxis(h_prev, 0, 1)
    y_off = jnp.einsum("bclgn,bcgrpn,bclgr->bclgrp", Cc, h_prev, jnp.exp(Acs))
    y = (y_diag + y_off).reshape(bsz, nc * Q, SSD_HEADS, SSD_HEADDIM)[:, :L]
    return y, h_last.reshape(bsz, SSD_HEADS, SSD_HEADDIM, D_STATE)


def decoder_layer(x, c, pool_prev, conv_prev, ssm_prev, pos0, w_ada, b_ada, norm1_w, w_in, w_pool, pool_scale,
                  conv_w, conv_b, dt_bias, A_log, D_skip, ssd_norm_w, w_ssd_proj, w_out, norm2_w, w_ffn_in, w_ffn_out):
    bsz, L, _ = x.shape
    mod = jax.nn.silu(c) @ w_ada + b_ada
    sh1, sc1, g1, sh2, sc2, g2 = (m[:, None, :] for m in jnp.split(mod, N_MOD, axis=-1))
    h = rmsnorm(x, norm1_w) * (1 + sc1) + sh1
    proj = h @ w_in
    u, z, xbc, dt_raw, gates = jnp.split(proj, [COL_Z, COL_XBC, COL_DT, COL_GATE], axis=-1)
    a_out, pool_new = pool_mixer(u, pool_prev, pos0, w_pool, pool_scale)
    xbc, conv_new = causal_conv(xbc, conv_prev, conv_w, conv_b)
    xs, Bm, Cm = jnp.split(xbc, [D_INNER, D_INNER + SSD_GROUPS * D_STATE], axis=-1)
    dt = jax.nn.softplus(dt_raw.astype(jnp.float32) + dt_bias.astype(jnp.float32))
    A = -jnp.exp(A_log.astype(jnp.float32))
    xh = xs.reshape(bsz, L, SSD_HEADS, SSD_HEADDIM)
    y, ssm_new = ssd_scan(xh, dt, A, Bm.reshape(bsz, L, SSD_GROUPS, D_STATE),
                          Cm.reshape(bsz, L, SSD_GROUPS, D_STATE), ssm_prev)
    y = y + D_skip.astype(jnp.float32)[:, None] * xh.astype(jnp.float32)
    y = y.reshape(bsz, L, D_INNER).astype(x.dtype)
    y = rmsnorm(y * jax.nn.silu(z), ssd_norm_w)
    b_out = y @ w_ssd_proj
    gate_a, gate_b = jnp.split(jax.nn.sigmoid(gates), N_BRANCH, axis=-1)
    mix = (gate_a * a_out + gate_b * b_out) @ w_out
    x = x + g1 * mix
    h2 = rmsnorm(x, norm2_w) * (1 + sc2) + sh2
    gt, up = jnp.split(h2 @ w_ffn_in, 2, axis=-1)
    x = x + g2 * ((jax.nn.silu(gt) * up) @ w_ffn_out)
    return x, pool_new, conv_new, ssm_new.astype(ssm_prev.dtype)


def setup_inputs(seed: int = 0) -> dict:
    key = jax.random.key(seed)
    ks = iter(jax.random.split(key, 32))

    def nrm(shape, scale):
        return scale * jax.random.normal(next(ks), shape, jnp.float32)

    def unif(shape, lo, hi):
        return jax.random.uniform(next(ks), shape, jnp.float32, lo, hi)

    dt0 = jnp.exp(unif((DEPTH, SSD_HEADS), float(np.log(1e-3)), float(np.log(1e-1))))
    return {
        "x_prompt": nrm((BATCH, SEQ, D_MODEL), 1.0),
        "x_sample": nrm((DEC_BATCH, DEC_SEQ, D_MODEL), 1.0),
        "c_prompt": nrm((BATCH, D_MODEL), 1.0),
        "c_sample": nrm((DEC_BATCH, D_MODEL), 1.0),
        "state_pool": nrm((DEPTH, DEC_BATCH, POOL_BUF, D_POOL), 1.0),
        "state_conv": nrm((DEPTH, DEC_BATCH, CONV_W - 1, CONV_DIM), 1.0),
        "state_ssm": nrm((DEPTH, DEC_BATCH, SSD_HEADS, SSD_HEADDIM, D_STATE), 0.3),
        "w_ada": nrm((DEPTH, D_MODEL, N_MOD * D_MODEL), 0.5 * D_MODEL ** -0.5),
        "b_ada": nrm((DEPTH, N_MOD * D_MODEL), 0.01),
        "norm1_w": 1.0 + nrm((DEPTH, D_MODEL), 0.05),
        "w_in": nrm((DEPTH, D_MODEL, IN_COLS), D_MODEL ** -0.5),
        "w_pool": nrm((DEPTH, POOL_GROUPS, POOL_GW, POOL_GW), POOL_GW ** -0.5),
        "pool_scale": 0.5 + nrm((DEPTH, D_MODEL), 0.05),
        "conv_w": nrm((DEPTH, CONV_W, CONV_DIM), CONV_W ** -0.5),
        "conv_b": nrm((DEPTH, CONV_DIM), 0.01),
        "dt_bias": dt0 + jnp.log(-jnp.expm1(-dt0)),
        "A_log": jnp.log(unif((DEPTH, SSD_HEADS), 1.0, 16.0)),
        "D_skip": 1.0 + nrm((DEPTH, SSD_HEADS), 0.1),
        "ssd_norm_w": 1.0 + nrm((DEPTH, D_INNER), 0.05),
        "w_ssd_proj": nrm((DEPTH, D_INNER, D_MODEL), D_INNER ** -0.5),
        "w_out": nrm((DEPTH, D_MODEL, D_MODEL), D_MODEL ** -0.5),
        "norm2_w": 1.0 + nrm((DEPTH, D_MODEL), 0.05),
        "w_ffn_in": nrm((DEPTH, D_MODEL, 2 * D_FF), D_MODEL ** -0.5),
        "w_ffn_out": nrm((DEPTH, D_FF, D_MODEL), D_FF ** -0.5),
        "final_norm_w": 1.0 + nrm((D_MODEL,), 0.05),
    }


def reference(x_prompt, x_sample, c_prompt, c_sample, state_pool, state_conv, state_ssm, w_ada, b_ada, norm1_w,
              w_in, w_pool, pool_scale, conv_w, conv_b, dt_bias, A_log, D_skip, ssd_norm_w, w_ssd_proj, w_out,
              norm2_w, w_ffn_in, w_ffn_out, final_norm_w):
    xp, xs = x_prompt, x_sample
    pool_p, conv_p, ssm_p, pool_s, conv_s, ssm_s = [], [], [], [], [], []
    for l in range(DEPTH):
        lw = (w_ada[l], b_ada[l], norm1_w[l], w_in[l], w_pool[l], pool_scale[l], conv_w[l], conv_b[l],
              dt_bias[l], A_log[l], D_skip[l], ssd_norm_w[l], w_ssd_proj[l], w_out[l], norm2_w[l],
              w_ffn_in[l], w_ffn_out[l])
        zp = jnp.zeros((BATCH, POOL_BUF, D_POOL), xp.dtype)
        zc = jnp.zeros((BATCH, CONV_W - 1, CONV_DIM), xp.dtype)
        zs = jnp.zeros((BATCH, SSD_HEADS, SSD_HEADDIM, D_STATE), state_ssm.dtype)
        xp, pn, cn, sn = decoder_layer(xp, c_prompt, zp, zc, zs, 0, *lw)
        pool_p.append(pn); conv_p.append(cn); ssm_p.append(sn)
        xs, pn, cn, sn = decoder_layer(xs, c_sample, state_pool[l], state_conv[l], state_ssm[l], PAST_LEN, *lw)
        pool_s.append(pn); conv_s.append(cn); ssm_s.append(sn)
    y_prompt = rmsnorm(xp, final_norm_w)
    y_sample = rmsnorm(xs, final_norm_w)
    return (y_prompt, y_sample, jnp.stack(pool_p), jnp.stack(conv_p), jnp.stack(ssm_p),
            jnp.stack(pool_s), jnp.stack(conv_s), jnp.stack(ssm_s))
```

```python
import os
from contextlib import ExitStack

import numpy as np
import concourse.bass as bass
import concourse.mybir as mybir
from concourse.bass_utils import run_bass_kernel_spmd

F32 = mybir.dt.float32
BF16 = mybir.dt.bfloat16
AF = mybir.ActivationFunctionType
ALU = mybir.AluOpType

ENGS = ("pe", "act", "dve", "pool", "sp")

D = 1024
SEQ = 2048
TB = 512
NBLK = SEQ // TB
NS = 16
DI = 2048
NH = 32
HP = 64
NG = 8
DST = 128
CONV = 4096
DFF = 2816
IN_COLS = 9248
EPS = 1e-6

PV_N1W, PV_PSC, PV_CW, PV_CB, PV_N2W, PV_BADA, PV_DF, PV_N = 0, 8, 16, 144, 176, 184, 232, 248
RB_FNW, RB_DSK, RB_ALOG, RB_DTB, RB_N = 0, 1024, 1056, 1088, 1120


class Op:
    __slots__ = ("eng", "fn", "reads", "writes", "dkey", "dgroup", "idx", "waits",
                 "sig", "cnt", "eidx")

    def __init__(self, eng, fn, reads, writes, dkey=None, dgroup=None):
        self.eng = eng
        self.fn = fn
        self.reads = reads
        self.writes = writes
        self.dkey = dkey
        self.dgroup = dgroup
        self.waits = {}
        self.sig = False
        self.cnt = 0


class Prog:
    def __init__(self, nc):
        self.nc = nc
        self.ops = []

    def op(self, eng, fn, reads=(), writes=()):
        o = Op(eng, fn, list(reads), list(writes))
        o.idx = len(self.ops)
        self.ops.append(o)
        return o

    def dma(self, eng, fn, reads=(), writes=(), key=None, group=None):
        o = Op(eng, fn, list(reads), list(writes), dkey=key, dgroup=group)
        o.idx = len(self.ops)
        self.ops.append(o)
        return o

    def analyze(self):
        ops = self.ops
        ecount = {e: 0 for e in ENGS}
        for o in ops:
            o.eidx = ecount[o.eng]
            ecount[o.eng] += 1
        key_ops = {}
        for o in ops:
            if o.dkey is not None:
                key_ops.setdefault(o.dkey, []).append(o)
        dma_cum, dma_prev = {}, {}
        for k, lst in key_ops.items():
            groups = []
            for o in lst:
                if groups and o.dgroup is not None and groups[-1][0] == o.dgroup:
                    groups[-1][1].append(o)
                else:
                    groups.append((o.dgroup, [o]))
            cum = 0
            for g, gl in groups:
                prev = cum
                cum += len(gl)
                for o in gl:
                    dma_cum[o.idx] = cum
                    dma_prev[o.idx] = prev
        self.keys = sorted(key_ops.keys())
        recs = {}
        waited = {}
        pend = []
        for o in ops:
            d = set()
            for (buf, lo, hi) in o.reads:
                for r in recs.get(buf, ()):
                    if r[3] and r[0] < hi and lo < r[1]:
                        d.add(r[2])
            for (buf, lo, hi) in o.writes:
                for r in recs.get(buf, ()):
                    if r[0] < hi and lo < r[1]:
                        d.add(r[2])
            d.discard(o.idx)
            for (buf, lo, hi) in o.writes:
                lst = recs.setdefault(buf, [])
                lst[:] = [r for r in lst if not (lo <= r[0] and r[1] <= hi)]
                lst.append([lo, hi, o.idx, True])
            for (buf, lo, hi) in o.reads:
                lst = recs.setdefault(buf, [])
                lst[:] = [r for r in lst if not ((not r[3]) and ops[r[2]].eng == o.eng
                                                 and ops[r[2]].dkey is None and o.dkey is None
                                                 and lo <= r[0] and r[1] <= hi)]
                lst.append([lo, hi, o.idx, False])
            need = {}
            for di in d:
                p = ops[di]
                if p.dkey is not None:
                    sk = ("d", p.dkey)
                    val = 16 * dma_cum[p.idx]
                    if need.get(sk, 0) < val:
                        need[sk] = val
                    continue
                if p.eng == o.eng and o.dkey is None:
                    if o.eng in ("pe", "sp"):
                        continue
                sk = ("e", p.eng)
                cur = need.get(sk)
                if cur is None or cur.eidx < p.eidx:
                    need[sk] = p
            if o.dkey is not None and dma_prev[o.idx] > 0:
                sk = ("d", o.dkey)
                val = 16 * dma_prev[o.idx]
                if need.get(sk, 0) < val:
                    need[sk] = val
            o.waits = need
            for sk, v in need.items():
                if sk[0] == "e":
                    v.sig = True
        cnt = {e: 0 for e in ENGS}
        for o in ops:
            if o.dkey is None and o.sig:
                cnt[o.eng] += 1
            o.cnt = cnt[o.eng]
        for o in ops:
            final = {}
            for sk, v in o.waits.items():
                val = v.cnt if sk[0] == "e" else v
                wk = (o.eng, sk)
                if waited.get(wk, 0) >= val:
                    continue
                waited[wk] = val
                final[sk] = val
            o.waits = final

    def emit(self, sems_e, sems_d):
        nc = self.nc
        per = {e: [o for o in self.ops if o.eng == e] for e in ENGS}

        def run(engname, eng):
            for o in per[engname]:
                for sk, val in o.waits.items():
                    sem = sems_e[sk[1]] if sk[0] == "e" else sems_d[sk[1]]
                    eng.wait_ge(sem, val)
                if o.fn is None:
                    continue
                ins = o.fn(eng)
                if o.dkey is not None:
                    ins.then_inc(sems_d[o.dkey], 16)
                elif o.sig:
                    ins.then_inc(sems_e[o.eng], 1)

        with nc.Block() as block:
            @block.tensor
            def _(e):
                run("pe", e)

            @block.scalar
            def _(e):
                run("act", e)

            @block.vector
            def _(e):
                run("dve", e)

            @block.gpsimd
            def _(e):
                run("pool", e)

            @block.sync
            def _(e):
                run("sp", e)


def R(name, lo=0, hi=1):
    return (name, lo, hi)


def build_nc(nblk=NBLK, do_sample=True):
    nc = bass.Bass("TRN2", target_bir_lowering=False)

    def din(name, shape):
        return nc.dram_tensor(name, list(shape), F32, kind="ExternalInput").ap()

    def dout(name, shape):
        return nc.dram_tensor(name, list(shape), F32, kind="ExternalOutput").ap()

    xp = din("xp", [SEQ, D])
    xsm = din("xsm", [NS, D])
    cT = din("cT", [128, 8, 17])
    pvec = din("pvec", [128, PV_N])
    rowb_in = din("rowb", [128, RB_N])
    snw_in = din("snw", [128, DI])
    bg_in = din("bg", [128, 2 * D])
    w_ada = din("w_ada", [128, 8, 6 * D])
    w_in = din("w_in", [128, 8, IN_COLS])
    w_pool = din("w_pool", [128, 4, 2, 256])
    w_ssd = din("w_ssd", [128, 16, D])
    w_out = din("w_out", [128, 8, D])
    w_ffi = din("w_ffi", [128, 8, 2 * DFF])
    w_ffo = din("w_ffo", [128, 22, D])
    st_pool = din("st_pool", [128, 8, NS, 15])
    st_conv = din("st_conv", [128, 32, NS, 3])
    st_ssm = din("st_ssm", [NS, 128, DI])
    o_y = dout("o_y", [SEQ, D])
    o_ys = dout("o_ys", [NS, D])
    o_pp = dout("o_pp", [128, 8, 15])
    o_cp = dout("o_cp", [128, 32, 3])
    o_sp = dout("o_sp", [128, DI])
    o_ps = dout("o_ps", [128, 8, NS, 15])
    o_cs = dout("o_cs", [128, 32, NS, 3])
    o_ss = dout("o_ss", [NS, 128, DI])
    dbg_out = {}

    es = ExitStack()
    with es:
        def sb(name, shape, dt=F32):
            return es.enter_context(nc.sbuf_tensor("s_" + name, list(shape), dt))

        P = Prog(nc)
        dma_keys = set()

        def DMA(eng, out, in_, key, reads=(), writes=(), group=None):
            dma_keys.add(key)
            P.dma(eng, lambda e: e.dma_start(out=out, in_=in_), reads=reads, writes=writes, key=key, group=group)

        pb = [es.enter_context(nc.psum_tensor("pb%d" % i, [128, 512], F32)) for i in range(8)]

        def PBF(i):
            return pb[i][:].bitcast(BF16)

        def PR(i, lo=0, hi=512):
            return ("pb%d" % i, 0, 512)

        sA = sb("sA", [128, 16 + TB])
        sB = sb("sB", [128, 16 + TB])
        identf = sB[:, 0:128]
        triuf = sB[:, 128:256]
        nmf = sA[:, 0:512].rearrange("p (a b) -> p a b", a=4)
        identb = sb("identb", [128, 128], BF16)
        onesb = sb("onesb", [128, 128], BF16)
        triub = sb("triub", [128, 128], BF16)
        nmb = sb("nmb", [128, 4, 128], BF16)
        epsc = sb("epsc", [128, 1])
        invc = sb("invc", [128, 4, 16])
        pv = sb("pv", [128, PV_N])
        rowb = sb("rowb", [128, RB_N])
        snwb = sb("snwb", [128, DI], BF16)
        anegb = sb("anegb", [128, 32])
        cTs = sb("cTs", [128, 8, 17])
        silucT = sb("silucT", [128, 8, 17], BF16)
        modT = sb("modT", [128, 48, 17])
        a1T = sb("a1T", [128, 8, 17])
        a2T = sb("a2T", [128, 8, 17])
        g1B = sb("g1B", [128, D])
        g2B = sb("g2B", [128, D])
        wdt = sb("wdt", [128, 8, 32], BF16)
        diagD = sb("diagD", [128, 16, 128], BF16)
        NWB = 2
        wbuf = [sb("wbuf%d" % i, [128, 4096], BF16) for i in range(NWB)]

        P.op("pool", lambda e: e.memset(identf, 0.0), writes=[R("sB")])
        P.op("pool", lambda e: e.affine_select(out=identf, in_=identf, pattern=[[-1, 128]], compare_op=ALU.not_equal,
                                               fill=1.0, base=0, channel_multiplier=1), reads=[R("sB")], writes=[R("sB")])
        P.op("dve", lambda e: e.tensor_copy(out=identb[:], in_=identf), reads=[R("sB")], writes=[R("identb")])
        P.op("pool", lambda e: e.memset(onesb[:], 1.0), writes=[R("onesb")])
        P.op("pool", lambda e: e.memset(triuf, 1.0), writes=[R("sB")])
        P.op("pool", lambda e: e.affine_select(out=triuf, in_=triuf, pattern=[[1, 128]], compare_op=ALU.is_ge,
                                               fill=0.0, base=0, channel_multiplier=-1), reads=[R("sB")], writes=[R("sB")])
        P.op("dve", lambda e: e.tensor_copy(out=triub[:], in_=triuf), reads=[R("sB")], writes=[R("triub")])
        P.op("pool", lambda e: e.memset(nmf, 0.0), writes=[R("sA")])
        P.op("pool", lambda e: e.affine_select(out=nmf, in_=nmf, pattern=[[0, 4], [1, 128]], compare_op=ALU.is_ge,
                                               fill=-30000.0, base=0, channel_multiplier=-1), reads=[R("sA")], writes=[R("sA")])
        P.op("dve", lambda e: e.tensor_copy(out=nmb[:], in_=nmf), reads=[R("sA")], writes=[R("nmb")])
        P.op("pool", lambda e: e.memset(epsc[:], EPS), writes=[R("epsc")])
        for g in range(4):
            w = 2 ** (g + 1)
            P.op("pool", lambda e, g=g, w=w: e.memset(invc[:, g, :], 1.0 / w), writes=[R("invc")])
            for t in range(w - 1):
                P.op("pool", lambda e, g=g, t=t: e.memset(invc[:, g, t:t + 1], 1.0 / (t + 1)), writes=[R("invc")])

        DMA("sp", pv[:], pvec, "ld_pv", writes=[R("pv")])
        DMA("sp", rowb[:], rowb_in, "ld_rowb", writes=[R("rowb")])
        DMA("sp", cTs[:], cT, "ld_c", writes=[R("cTs")])
        DMA("sp", g1B[:], bg_in[:, 0:D], "ld_g1", writes=[R("g1B", 0, 2)])
        DMA("sp", g2B[:], bg_in[:, D:2 * D], "ld_g2", writes=[R("g2B", 0, 2)])
        DMA("pool", snwb[:], snw_in, "ld_snw", writes=[R("snwb")])
        DMA("pool", wdt[:], w_in[:, :, 7168:7200], "ld_wdt", writes=[R("wdt")])
        P.op("dve", lambda e: e.tensor_tensor(out=diagD[:], in0=identb[:].unsqueeze(1).to_broadcast([128, 16, 128]),
                                              in1=pv[:, PV_DF:PV_DF + 16].unsqueeze(2).to_broadcast([128, 16, 128]), op=ALU.mult),
             reads=[R("identb"), R("pv")], writes=[R("diagD")])

        P.op("act", lambda e: e.activation(out=anegb[:], in_=rowb[:, RB_ALOG:RB_ALOG + 32], func=AF.Exp), reads=[R("rowb")], writes=[R("anegb")])
        P.op("dve", lambda e: e.tensor_scalar(out=anegb[:], in0=anegb[:], scalar1=-1.0, scalar2=None, op0=ALU.mult), reads=[R("anegb")], writes=[R("anegb")])

        wstate = {"i": 0}

        NSCR = 42
        wsc = nc.dram_tensor("wsc", [NSCR, 128, 4096], BF16).ap()
        scr_idx = {}

        def load_w(src_ap, nk, ncol, tid=None):
            i = wstate["i"] % NWB
            wstate["i"] += 1
            wt = wbuf[i]
            view = wt[:, 0:nk * ncol].rearrange("p (k c) -> p k c", k=nk)
            if tid is None:
                DMA("pool", view, src_ap, "w%d" % i, writes=[R("wbuf%d" % i)])
            elif tid not in scr_idx:
                k = len(scr_idx)
                scr_idx[tid] = k
                DMA("pool", view, src_ap, "w%d" % i, writes=[R("wbuf%d" % i)])
                DMA("sp", wsc[k, :, 0:nk * ncol], wt[:, 0:nk * ncol], "ws%d" % i, reads=[R("wbuf%d" % i)], writes=[R("wsc", k, k + 1)])
            else:
                k = scr_idx[tid]
                DMA("sp", wt[:, 0:nk * ncol], wsc[k, :, 0:nk * ncol], "wh%d" % i, reads=[R("wsc", k, k + 1)], writes=[R("wbuf%d" % i)])
            return view, R("wbuf%d" % i)

        rot = {"i": 0}

        def next_bank(banks=(0, 1, 2, 3)):
            b = banks[rot["i"] % len(banks)]
            rot["i"] += 1
            return b

        P.op("act", lambda e: e.activation(out=silucT[:], in_=cTs[:], func=AF.Silu), reads=[R("cTs")], writes=[R("silucT")])
        for t in range(12):
            wt, wr = load_w(w_ada[:, :, t * 512:(t + 1) * 512], 8, 512)
            bk = 4 + (t % 2)
            for j in range(4):
                for kc in range(8):
                    P.op("pe", lambda e, bk=bk, j=j, kc=kc, wt=wt: e.matmul(pb[bk][:, j * 17:(j + 1) * 17], lhsT=wt[:, kc, j * 128:(j + 1) * 128],
                                                                             rhs=silucT[:, kc, :], start=(kc == 0), stop=(kc == 7)),
                         reads=[wr, R("silucT")], writes=[PR(bk, j * 17, j * 17 + 17)])
            P.op("dve", lambda e, bk=bk, t=t: e.tensor_tensor(out=modT[:, 4 * t:4 * t + 4, :], in0=pb[bk][:, 0:68].rearrange("p (a b) -> p a b", a=4),
                                                               in1=pv[:, PV_BADA + 4 * t:PV_BADA + 4 * t + 4].unsqueeze(2).to_broadcast([128, 4, 17]), op=ALU.add),
                 reads=[PR(bk, 0, 68), R("pv")], writes=[R("modT", 4 * t, 4 * t + 4)])
            if t in (4, 5, 10, 11):
                gB = g1B if t < 6 else g2B
                gname = "g1B" if t < 6 else "g2B"
                half = t % 2
                for kc in range(8):
                    P.op("pe", lambda e, kc=kc, wt=wt: e.matmul(pb[6][:, 0:512], lhsT=silucT[:, kc, 0:1].to_broadcast([128, 128]), rhs=wt[:, kc, :],
                                                                  start=(kc == 0), stop=(kc == 7)),
                         reads=[wr, R("silucT")], writes=[PR(6)])
                P.op("dve", lambda e, gB=gB, half=half: e.tensor_tensor(out=gB[:, half * 512:(half + 1) * 512], in0=pb[6][:, 0:512],
                                                                         in1=gB[:, half * 512:(half + 1) * 512], op=ALU.add),
                     reads=[PR(6), R(gname, half, half + 1)], writes=[R(gname, half, half + 1)])
        for (aT, an, sc0, nw0) in ((a1T, "a1T", 8, PV_N1W), (a2T, "a2T", 32, PV_N2W)):
            P.op("dve", lambda e, aT=aT, sc0=sc0: e.tensor_scalar(out=aT[:], in0=modT[:, sc0:sc0 + 8, :], scalar1=1.0, scalar2=None, op0=ALU.add),
                 reads=[R("modT", sc0, sc0 + 8)], writes=[R(an)])
            P.op("dve", lambda e, aT=aT, nw0=nw0: e.tensor_tensor(out=aT[:], in0=aT[:], in1=pv[:, nw0:nw0 + 8].unsqueeze(2).to_broadcast([128, 8, 17]), op=ALU.mult),
                 reads=[R(an), R("pv")], writes=[R(an)])

        xtok = sb("xtok", [128, 4, D])
        ssq = sb("ssq", [128, 8])
        rsq = sb("rsq", [128, 8])
        hT = sb("hT", [128, 8, TB], BF16)
        ubuf = sb("ubuf", [128, 8, 16 + TB])
        gT = sb("gT", [128, 8, TB], BF16)
        gaT = gT
        gbT = gT
        uhist = sb("uhist", [128, 8, 15])
        amT = sb("amT", [128, 8, TB], BF16)
        sz = sb("sz", [128, 4, DI], BF16)
        xpre = [sb("xpre%d" % i, [128, 3 + TB], BF16) for i in range(2)]
        dg = [sb("dg%d" % i, [128, 4, 128], BF16) for i in range(2)]
        chist = sb("chist", [128, 32, 3], BF16)
        cst = sb("cst", [128, 32, 3])
        xbc = sb("xbc", [128, 32, TB], BF16)
        dtx = sb("dtx", [128, 4, 32])
        dtt = sb("dtt", [128, 4, 32])
        dtu = sb("dtu", [128, 4, 32])
        dt_tok = sb("dt_tok", [128, 4, 32])
        dA_bf = sb("dA_bf", [128, 4, 32], BF16)
        negAcs = sb("negAcs", [128, 32])
        Eacs = sb("Eacs", [128, 32])
        dte = sb("dte", [128, 32])
        cdB = sb("cdB", [128, 32])
        xdtb = [sb("xdt%d" % i, [128, DI], BF16) for i in range(2)]
        x_tok = xdtb[1]
        xdt = xdtb[0]
        xdte = sb("xdte", [128, DI], BF16)
        B_tok = sb("B_tok", [128, 1024], BF16)
        CBs = sb("CBs", [128, 8, 128], BF16)
        dec = [sb("dec%d" % i, [128, 8, 128], BF16) for i in range(2)]
        scr = dec
        ybuf = sb("ybuf", [128, DI])
        ytmp = hT[:].rearrange("p a b -> p (a b)").bitcast(F32)
        xn = ybuf[:].bitcast(BF16).rearrange("p (a b) -> p a b", a=4)
        junk = xdte
        pooled = sz[:].rearrange("p a b -> p (a b)")[:, 0:8 * TB].rearrange("p (a b) -> p a b", a=8)
        hst = sb("hst", [128, DI])
        hstb = sb("hstb", [128, DI], BF16)
        yst = [sb("yst%d" % i, [128, D]) for i in range(1)]
        ynT = ubuf[:].rearrange("p a b -> p (a b)").bitcast(BF16)[:, 0:16 * TB].rearrange("p (a b) -> p a b", a=16)
        fT = xbc[:].rearrange("p a b -> p (a b)")[:, 0:22 * TB].rearrange("p (a b) -> p a b", a=22)

        P.op("pool", lambda e: e.memset(chist[:], 0.0), writes=[R("chist", 0, 32)])
        P.op("pool", lambda e: e.memset(hst[:], 0.0), writes=[R("hst", 0, 8)])
        P.op("pool", lambda e: e.memset(hstb[:], 0.0), writes=[R("hstb", 0, 8)])

        def rmsnorm_to_T(blk_name, aT, bcol0, ntok=128, ntile=4):
            for ti in range(ntile):
                P.op("act", lambda e, ti=ti: e.activation(out=junk[:, 0:D], in_=xtok[:, ti, :], func=AF.Square, accum_out=ssq[:, ti:ti + 1]),
                     reads=[R("xtok", ti, ti + 1)], writes=[R("xdte", 0, 8), R("ssq", ti, ti + 1)])
                P.op("act", lambda e, ti=ti: e.activation(out=rsq[:, 4 + ti:5 + ti], in_=ssq[:, ti:ti + 1], func=AF.Ln, scale=1.0 / D, bias=epsc[:, 0:1]),
                     reads=[R("ssq", ti, ti + 1), R("epsc")], writes=[R("rsq", 4 + ti, 5 + ti)])
                P.op("act", lambda e, ti=ti: e.activation(out=rsq[:, ti:ti + 1], in_=rsq[:, 4 + ti:5 + ti], func=AF.Exp, scale=-0.5),
                     reads=[R("rsq", 4 + ti, 5 + ti)], writes=[R("rsq", ti, ti + 1)])
                P.op("dve", lambda e, ti=ti: e.tensor_scalar(out=xn[:, ti, :], in0=xtok[:, ti, :], scalar1=rsq[:, ti:ti + 1], scalar2=None, op0=ALU.mult),
                     reads=[R("xtok", ti, ti + 1), R("rsq", ti, ti + 1)], writes=[R("ybuf", 2 * ti, 2 * ti + 2)])
            for kc in range(8):
                bk = kc // 2
                c0 = (kc % 2) * 512
                for ti in range(ntile):
                    P.op("pe", lambda e, bk=bk, c0=c0, ti=ti, kc=kc: e.transpose(out=PBF(bk)[:, c0 + ti * 128:c0 + (ti + 1) * 128],
                                                                                  in_=xn[:, ti, kc * 128:(kc + 1) * 128], identity=identb[:]),
                         reads=[R("ybuf", 2 * ti, 2 * ti + 2), R("identb")], writes=[PR(bk, c0 // 2 + ti * 64, c0 // 2 + (ti + 1) * 64)])
                P.op("dve", lambda e, bk=bk, c0=c0, kc=kc, aT=aT, bcol0=bcol0: e.tensor_scalar(
                    out=hT[:, kc, :], in0=PBF(bk)[:, c0:c0 + 512], scalar1=aT[:, kc, 0:1], scalar2=modT[:, bcol0 + kc, 0:1], op0=ALU.mult, op1=ALU.add),
                    reads=[PR(bk, c0 // 2, c0 // 2 + 256), R(blk_name), R("modT", bcol0 + kc, bcol0 + kc + 1)], writes=[R("hT", kc, kc + 1)])

        def proj_ws(wt, wr, j, nk, rhs_fn, rhs_regs, bank, ncols=TB, col0=0):
            for kc in range(nk):
                P.op("pe", lambda e, kc=kc: e.matmul(pb[bank][:, 0:ncols], lhsT=wt[:, kc, col0 + j * 128:col0 + (j + 1) * 128], rhs=rhs_fn(kc),
                                                      start=(kc == 0), stop=(kc == nk - 1)),
                     reads=[wr] + rhs_regs(kc), writes=[PR(bank, 0, ncols)])

        def proj_as(wt, wr, ti, nk, lhs_fn, lhs_regs, bank, kc0=0, first=True, last=True, nktot=None):
            for kc in range(nk):
                P.op("pe", lambda e, kc=kc: e.matmul(pb[bank][:, 0:512], lhsT=lhs_fn(kc0 + kc, ti), rhs=wt[:, kc, :],
                                                      start=(first and kc == 0), stop=(last and kc == nk - 1)),
                     reads=[wr] + lhs_regs(kc0 + kc), writes=[PR(bank)])

        hT_rhs = lambda kc: hT[:, kc, :]
        hT_regs = lambda kc: [R("hT", kc, kc + 1)]
        hT_lhs = lambda kc, ti: hT[:, kc, ti * 128:(ti + 1) * 128]

        for tb in range(nblk):
            DMA("sp", xtok[:], xp[tb * TB:(tb + 1) * TB, :].rearrange("(t p) d -> p t d", p=128), "ld_x", writes=[R("xtok", 0, 4)])
            rmsnorm_to_T("a1T", a1T, 0)

            for t in range(2):
                wt, wr = load_w(w_in[:, :, 7200 + t * 512:7200 + (t + 1) * 512], 8, 512, tid=('in', 7200 + t * 512))
                for j in range(4):
                    oc = 4 * t + j
                    bk = next_bank()
                    proj_ws(wt, wr, j, 8, hT_rhs, hT_regs, bk)
                    P.op("act", lambda e, bk=bk, oc=oc: e.activation(out=gaT[:, oc, :], in_=pb[bk][:, 0:TB], func=AF.Sigmoid),
                         reads=[PR(bk)], writes=[R("gT", oc, oc + 1)])
            if tb == 0:
                P.op("pool", lambda e: e.memset(ubuf[:, :, 0:16], 0.0), writes=[R("ubuf", 0, 8)])
            else:
                P.op("pool", lambda e: e.tensor_copy(out=ubuf[:, :, 1:16], in_=uhist[:]), reads=[R("uhist")], writes=[R("ubuf", 0, 8)])
            for t in range(2):
                wt, wr = load_w(w_in[:, :, t * 512:(t + 1) * 512], 8, 512, tid=('in', t * 512))
                for j in range(4):
                    oc = 4 * t + j
                    bk = next_bank()
                    proj_ws(wt, wr, j, 8, hT_rhs, hT_regs, bk)
                    P.op("act", lambda e, bk=bk, oc=oc: e.activation(out=ubuf[:, oc, 16:16 + TB], in_=pb[bk][:, 0:TB], func=AF.Copy),
                         reads=[PR(bk)], writes=[R("ubuf", oc, oc + 1)])
            for oc in range(8):
                g = oc // 2
                w = 2 ** (g + 1)
                U = ubuf[:, oc, :]
                ur = R("ubuf", oc, oc + 1)
                P.op("dve", lambda e, U=U: e.tensor_tensor(out=sA[:, 2:528], in0=U[:, 2:528], in1=U[:, 1:527], op=ALU.add), reads=[ur], writes=[R("sA")])
                cur, curname = sA, "sA"
                if g >= 1:
                    P.op("dve", lambda e: e.tensor_tensor(out=sB[:, 4:528], in0=sA[:, 4:528], in1=sA[:, 2:526], op=ALU.add), reads=[R("sA")], writes=[R("sB")])
                    cur, curname = sB, "sB"
                if g >= 2:
                    P.op("dve", lambda e: e.tensor_tensor(out=sA[:, 8:528], in0=sB[:, 8:528], in1=sB[:, 4:524], op=ALU.add), reads=[R("sB")], writes=[R("sA")])
                    cur, curname = sA, "sA"
                if g >= 3:
                    P.op("dve", lambda e: e.tensor_tensor(out=sB[:, 16:528], in0=sA[:, 16:528], in1=sA[:, 8:520], op=ALU.add), reads=[R("sA")], writes=[R("sB")])
                    cur, curname = sB, "sB"
                P.op("dve", lambda e, cur=cur, U=U, w=w, oc=oc: e.scalar_tensor_tensor(out=pooled[:, oc, :], in0=cur[:, 16:528], scalar=1.0 / w, in1=U[:, 16:528],
                                                                                        op0=ALU.mult, op1=ALU.subtract),
                     reads=[R(curname), ur], writes=[R("sz", 0, 2)])
                if tb == 0:
                    P.op("dve", lambda e, cur=cur, g=g: e.tensor_tensor(out=cur[:, 0:16], in0=cur[:, 16:32], in1=invc[:, g, :], op=ALU.mult),
                         reads=[R(curname), R("invc")], writes=[R(curname)])
                    P.op("dve", lambda e, cur=cur, U=U, oc=oc: e.tensor_tensor(out=pooled[:, oc, 0:16], in0=cur[:, 0:16], in1=U[:, 16:32], op=ALU.subtract),
                         reads=[R(curname), ur], writes=[R("sz", 0, 2)])
            wplt, wplr = load_w(w_pool.rearrange("p g k c -> p (g k) c"), 8, 256, tid=('pool',))
            for g in range(4):
                for j in range(2):
                    oc = 2 * g + j
                    bk = next_bank()
                    for k2 in range(2):
                        P.op("pe", lambda e, bk=bk, g=g, j=j, k2=k2: e.matmul(pb[bk][:, 0:TB], lhsT=wplt[:, 2 * g + k2, j * 128:(j + 1) * 128], rhs=pooled[:, 2 * g + k2, :],
                                                                                start=(k2 == 0), stop=(k2 == 1)),
                             reads=[wplr, R("sz", 0, 2)], writes=[PR(bk)])
                    P.op("dve", lambda e, bk=bk, oc=oc: e.scalar_tensor_tensor(out=amT[:, oc, :], in0=pb[bk][:, 0:TB], scalar=pv[:, PV_PSC + oc:PV_PSC + oc + 1],
                                                                                in1=gaT[:, oc, :], op0=ALU.mult, op1=ALU.mult),
                         reads=[PR(bk), R("pv"), R("gT", oc, oc + 1)], writes=[R("amT", oc, oc + 1)])
            if tb == nblk - 1:
                DMA("sp", o_pp, ubuf[:, :, 513:528], "st_pp", reads=[R("ubuf", 0, 8)], writes=[R("o_pp")])
            else:
                P.op("pool", lambda e: e.tensor_copy(out=uhist[:], in_=ubuf[:, :, 513:528]), reads=[R("ubuf", 0, 8)], writes=[R("uhist")])

            for t in range(4):
                wt, wr = load_w(w_in[:, :, 1024 + t * 512:1024 + (t + 1) * 512], 8, 512, tid=('in', 1024 + t * 512))
                for ti in range(4):
                    bk = next_bank()
                    proj_as(wt, wr, ti, 8, hT_lhs, hT_regs, bk)
                    P.op("act", lambda e, bk=bk, ti=ti, t=t: e.activation(out=sz[:, ti, t * 512:(t + 1) * 512], in_=pb[bk][:, 0:512], func=AF.Silu),
                         reads=[PR(bk)], writes=[R("sz", ti, ti + 1)])
            for t in range(8):
                wt, wr = load_w(w_in[:, :, 3072 + t * 512:3072 + (t + 1) * 512], 8, 512, tid=('in', 3072 + t * 512))
                for j in range(4):
                    oc = 4 * t + j
                    sl = oc % 2
                    bk = next_bank()
                    proj_ws(wt, wr, j, 8, hT_rhs, hT_regs, bk)
                    xpr = R("xpre%d" % sl)
                    P.op("dve", lambda e, sl=sl, oc=oc: e.tensor_copy(out=xpre[sl][:, 0:3], in_=chist[:, oc, :]), reads=[R("chist", oc, oc + 1)], writes=[xpr])
                    P.op("act", lambda e, sl=sl, bk=bk: e.activation(out=xpre[sl][:, 3:3 + TB], in_=pb[bk][:, 0:TB], func=AF.Copy), reads=[PR(bk)], writes=[xpr])
                    if tb == nblk - 1:
                        P.op("dve", lambda e, bk=bk, oc=oc: e.tensor_copy(out=cst[:, oc, :], in_=pb[bk][:, TB - 3:TB]), reads=[PR(bk)], writes=[R("cst", oc, oc + 1)])
                    else:
                        P.op("dve", lambda e, sl=sl, oc=oc: e.tensor_copy(out=chist[:, oc, :], in_=xpre[sl][:, TB:TB + 3]), reads=[xpr], writes=[R("chist", oc, oc + 1)])
                    P.op("pool", lambda e, sl=sl, oc=oc: e.tensor_tensor(out=dg[sl][:], in0=identb[:].unsqueeze(1).to_broadcast([128, 4, 128]),
                                                                          in1=pv[:, PV_CW + oc:PV_CW + oc + 97:32].unsqueeze(2).to_broadcast([128, 4, 128]), op=ALU.mult),
                         reads=[R("identb"), R("pv")], writes=[R("dg%d" % sl, 0, 4)])
                    cbk = 4 + sl
                    for k in range(4):
                        P.op("pe", lambda e, sl=sl, k=k, cbk=cbk: e.matmul(pb[cbk][:, 0:TB], lhsT=dg[sl][:, k, :], rhs=xpre[sl][:, k:k + TB], start=(k == 0), stop=(k == 3)),
                             reads=[R("dg%d" % sl, k, k + 1), xpr], writes=[PR(cbk)])
                    P.op("act", lambda e, cbk=cbk, oc=oc: e.activation(out=xbc[:, oc, :], in_=pb[cbk][:, 0:TB], func=AF.Silu, bias=pv[:, PV_CB + oc:PV_CB + oc + 1], scale=1.0),
                         reads=[PR(cbk), R("pv")], writes=[R("xbc", oc, oc + 1)])
            if tb == nblk - 1:
                DMA("sp", o_cp, cst[:], "st_cp", reads=[R("cst", 0, 32)], writes=[R("o_cp")])
            for ti in range(4):
                for kc in range(8):
                    P.op("pe", lambda e, ti=ti, kc=kc: e.matmul(pb[6][:, ti * 32:(ti + 1) * 32], lhsT=hT[:, kc, ti * 128:(ti + 1) * 128], rhs=wdt[:, kc, :],
                                                                  start=(kc == 0), stop=(kc == 7)),
                         reads=[R("hT", kc, kc + 1), R("wdt")], writes=[PR(6, ti * 32, ti * 32 + 32)])
            P.op("dve", lambda e: e.tensor_tensor(out=dtx[:], in0=pb[6][:, 0:128].rearrange("p (a b) -> p a b", a=4),
                                                  in1=rowb[:, RB_DTB:RB_DTB + 32].unsqueeze(1).to_broadcast([128, 4, 32]), op=ALU.add),
                 reads=[PR(6, 0, 128), R("rowb")], writes=[R("dtx")])
            P.op("act", lambda e: e.activation(out=dtt[:], in_=dtx[:], func=AF.Abs), reads=[R("dtx")], writes=[R("dtt")])
            P.op("act", lambda e: e.activation(out=dtu[:], in_=dtt[:], func=AF.Exp, scale=-1.0), reads=[R("dtt")], writes=[R("dtu")])
            P.op("act", lambda e: e.activation(out=dtt[:], in_=dtu[:], func=AF.Ln, bias=1.0, scale=1.0), reads=[R("dtu")], writes=[R("dtt")])
            P.op("dve", lambda e: e.scalar_tensor_tensor(out=dt_tok[:], in0=dtx[:], scalar=0.0, in1=dtt[:], op0=ALU.max, op1=ALU.add),
                 reads=[R("dtx"), R("dtt")], writes=[R("dt_tok")])
            P.op("dve", lambda e: e.tensor_tensor(out=dA_bf[:], in0=dt_tok[:], in1=anegb[:].unsqueeze(1).to_broadcast([128, 4, 32]), op=ALU.mult),
                 reads=[R("dt_tok"), R("anegb")], writes=[R("dA_bf")])
            for t in range(2):
                wt, wr = load_w(w_in[:, :, 8224 + t * 512:8224 + (t + 1) * 512], 8, 512, tid=('in', 8224 + t * 512))
                for j in range(4):
                    oc = 4 * t + j
                    bk = next_bank()
                    proj_ws(wt, wr, j, 8, hT_rhs, hT_regs, bk)
                    P.op("act", lambda e, bk=bk, oc=oc: e.activation(out=gbT[:, oc, :], in_=pb[bk][:, 0:TB], func=AF.Sigmoid),
                         reads=[PR(bk)], writes=[R("gT", oc, oc + 1)])


            def ssd_acd(ci):
                tsl = slice(ci * 128, (ci + 1) * 128)
                dAc = dA_bf[:, ci, :]
                P.op("pe", lambda e: e.matmul(pb[7][:, 0:32], lhsT=triub[:], rhs=dAc, start=True, stop=True), reads=[R("triub"), R("dA_bf")], writes=[PR(7)])
                P.op("pe", lambda e: e.matmul(pb[7][:, 32:64], lhsT=onesb[:], rhs=dAc, start=True, stop=True), reads=[R("onesb"), R("dA_bf")], writes=[PR(7)])
                P.op("dve", lambda e: e.tensor_scalar(out=negAcs[:], in0=pb[7][:, 0:32], scalar1=-1.0, scalar2=None, op0=ALU.mult), reads=[PR(7)], writes=[R("negAcs")])
                P.op("act", lambda e: e.activation(out=Eacs[:], in_=pb[7][:, 0:32], func=AF.Exp), reads=[PR(7)], writes=[R("Eacs")])
                P.op("dve", lambda e: e.tensor_tensor(out=dte[:], in0=pb[7][:, 32:64], in1=negAcs[:], op=ALU.add), reads=[PR(7), R("negAcs")], writes=[R("dte")])
                P.op("act", lambda e: e.activation(out=dte[:], in_=dte[:], func=AF.Exp), reads=[R("dte")], writes=[R("dte")])
                P.op("act", lambda e: e.activation(out=cdB[:], in_=pb[7][:, 32:64], func=AF.Exp), reads=[PR(7)], writes=[R("cdB")])
                for g in range(NG):
                    P.op("pe", lambda e, g=g: e.transpose(out=PBF(2)[:, g * 128:(g + 1) * 128], in_=xbc[:, 16 + g, tsl], identity=identb[:]),
                         reads=[R("xbc", 16 + g, 17 + g), R("identb")], writes=[PR(2)])
                P.op("act", lambda e: e.activation(out=B_tok[:], in_=PBF(2)[:, 0:1024], func=AF.Copy), reads=[PR(2)], writes=[R("B_tok")])
                for g in range(NG):
                    bk = 3 + g // 4
                    c0 = (g % 4) * 128
                    P.op("pe", lambda e, g=g, bk=bk, c0=c0: e.matmul(pb[bk][:, c0:c0 + 128], lhsT=xbc[:, 16 + g, tsl], rhs=xbc[:, 24 + g, tsl], start=True, stop=True),
                         reads=[R("xbc", 16 + g, 17 + g), R("xbc", 24 + g, 25 + g)], writes=[PR(bk)])
                for q in range(2):
                    P.op("act", lambda e, q=q: e.activation(out=CBs[:, q * 4:(q + 1) * 4, :], in_=pb[3 + q][:, 0:512].rearrange("p (a b) -> p a b", a=4), func=AF.Copy),
                         reads=[PR(3 + q)], writes=[R("CBs", q * 4, q * 4 + 4)])

            def ssd_b(ci):
                tsl = slice(ci * 128, (ci + 1) * 128)
                xd = xdtb[ci % 2]
                xr = R("xdt%d" % (ci % 2))
                for fc in range(16):
                    bk = fc // 8
                    c0 = (fc % 8) * 128
                    P.op("pe", lambda e, bk=bk, c0=c0, fc=fc: e.transpose(out=PBF(bk)[:, c0:c0 + 128], in_=xbc[:, fc, tsl], identity=identb[:]),
                         reads=[R("xbc", fc, fc + 1), R("identb")], writes=[PR(bk)])
                for bk in range(2):
                    P.op("dve", lambda e, bk=bk: e.tensor_tensor(out=xd[:, bk * 1024:(bk + 1) * 1024].rearrange("p (h q) -> p h q", h=16),
                                                                 in0=PBF(bk)[:, 0:1024].rearrange("p (h q) -> p h q", h=16),
                                                                 in1=dt_tok[:, ci, bk * 16:(bk + 1) * 16].unsqueeze(2).to_broadcast([128, 16, HP]), op=ALU.mult),
                         reads=[PR(bk), R("dt_tok")], writes=[xr])
                P.op("dve", lambda e: e.tensor_tensor(out=xdte[:].rearrange("p (h q) -> p h q", h=NH), in0=xd[:].rearrange("p (h q) -> p h q", h=NH),
                                                      in1=dte[:].unsqueeze(2).to_broadcast([128, NH, HP]), op=ALU.mult),
                     reads=[xr, R("dte")], writes=[R("xdte", 0, 8)])

            def ssd_decay(ci, hq):
                sl = hq % 2
                dbanks = (5, 6) if hq % 2 == 0 else (3, 4)
                for half in range(2):
                    bk = dbanks[half]
                    for j in range(4):
                        h = hq * 8 + half * 4 + j
                        P.op("pe", lambda e, bk=bk, j=j: e.matmul(pb[bk][:, j * 128:(j + 1) * 128], lhsT=identb[:], rhs=nmb[:, 0, :], start=True, stop=False),
                             reads=[R("identb"), R("nmb")], writes=[PR(bk)])
                        P.op("pe", lambda e, bk=bk, j=j, h=h: e.matmul(pb[bk][:, j * 128:(j + 1) * 128], lhsT=dA_bf[:, ci, h:h + 1].to_broadcast([128, 128]),
                                                                        rhs=triub[:], start=False, stop=True),
                             reads=[R("dA_bf"), R("triub")], writes=[PR(bk)])
                    for j in range(4):
                        h = hq * 8 + half * 4 + j
                        P.op("act", lambda e, bk=bk, j=j, h=h, half=half: e.activation(out=dec[sl][:, half * 4 + j, :], in_=pb[bk][:, j * 128:(j + 1) * 128],
                                                                                       func=AF.Exp, bias=negAcs[:, h:h + 1], scale=1.0),
                             reads=[PR(bk), R("negAcs")], writes=[R("dec%d" % sl, half * 4 + j, half * 4 + j + 1)])

            def ssd_y(ci, hq):
                tsl = slice(ci * 128, (ci + 1) * 128)
                sl = hq % 2
                P.op("dve", lambda e: e.tensor_tensor(out=scr[sl][:].rearrange("p (g j) l -> p g j l", g=2), in0=dec[sl][:].rearrange("p (g j) l -> p g j l", g=2),
                                                      in1=CBs[:, 2 * hq:2 * hq + 2, :].unsqueeze(2).to_broadcast([128, 2, 4, 128]), op=ALU.mult),
                     reads=[R("dec%d" % sl, 0, 8), R("CBs", 2 * hq, 2 * hq + 2)], writes=[R("dec%d" % sl, 0, 8)])
                bA = (0, 2)[hq % 2]
                bB = (1, 7)[hq % 2]
                for jj in range(8):
                    h = 8 * hq + jj
                    P.op("pe", lambda e, jj=jj, h=h: e.matmul(pb[bA][:, jj * 64:(jj + 1) * 64], lhsT=xbc[:, h // 2, tsl], rhs=diagD[:, h // 2, (h % 2) * 64:(h % 2 + 1) * 64], start=True, stop=False),
                         reads=[R("xbc", h // 2, h // 2 + 1), R("diagD")], writes=[PR(bA)])
                    P.op("pe", lambda e, jj=jj, h=h: e.matmul(pb[bA][:, jj * 64:(jj + 1) * 64], lhsT=scr[sl][:, jj, :], rhs=xdtb[ci % 2][:, h * 64:(h + 1) * 64], start=False, stop=True),
                         reads=[R("dec%d" % sl, jj, jj + 1), R("xdt%d" % (ci % 2))], writes=[PR(bA)])
                for gg in range(2):
                    g = 2 * hq + gg
                    P.op("pe", lambda e, g=g, gg=gg: e.matmul(pb[bB][:, gg * 256:(gg + 1) * 256], lhsT=xbc[:, 24 + g, tsl], rhs=hstb[:, g * 256:(g + 1) * 256], start=True, stop=True),
                         reads=[R("xbc", 24 + g, 25 + g), R("hstb", g, g + 1)], writes=[PR(bB)])
                ysl = slice(hq * 512, (hq + 1) * 512)
                yr = R("ybuf", 2 * hq, 2 * hq + 2)
                P.op("dve", lambda e: e.tensor_tensor(out=ybuf[:, ysl].rearrange("p (h q) -> p h q", h=8), in0=pb[bB][:, 0:512].rearrange("p (h q) -> p h q", h=8),
                                                      in1=Eacs[:, 8 * hq:8 * hq + 8].unsqueeze(2).to_broadcast([128, 8, HP]), op=ALU.mult),
                     reads=[PR(bB), R("Eacs")], writes=[yr])
                P.op("dve", lambda e: e.tensor_tensor(out=ybuf[:, ysl], in0=pb[bA][:, 0:512], in1=ybuf[:, ysl], op=ALU.add), reads=[PR(bA), yr], writes=[yr])
                P.op("dve", lambda e: e.tensor_tensor(out=ybuf[:, ysl], in0=ybuf[:, ysl], in1=sz[:, ci, ysl], op=ALU.mult), reads=[yr, R("sz", ci, ci + 1)], writes=[yr])

            def ssd_g(ci):
                for q in range(4):
                    sbk = 3 + q
                    hsl = slice(q * 512, (q + 1) * 512)
                    hr = R("hst", 2 * q, 2 * q + 2)
                    for gg in range(2):
                        g = 2 * q + gg
                        P.op("pe", lambda e, g=g, gg=gg, sbk=sbk: e.matmul(pb[sbk][:, gg * 256:(gg + 1) * 256], lhsT=B_tok[:, g * 128:(g + 1) * 128], rhs=xdte[:, g * 256:(g + 1) * 256], start=True, stop=True),
                             reads=[R("B_tok"), R("xdte", g, g + 1)], writes=[PR(sbk)])
                    P.op("pool", lambda e, q=q, hsl=hsl: e.tensor_tensor(out=hst[:, hsl].rearrange("p (h q) -> p h q", h=8), in0=hst[:, hsl].rearrange("p (h q) -> p h q", h=8),
                                                                         in1=cdB[:, 8 * q:8 * q + 8].unsqueeze(2).to_broadcast([128, 8, HP]), op=ALU.mult),
                         reads=[hr, R("cdB")], writes=[hr])
                    P.op("dve", lambda e, sbk=sbk, hsl=hsl: e.tensor_tensor(out=hst[:, hsl], in0=pb[sbk][:, 0:512], in1=hst[:, hsl], op=ALU.add), reads=[PR(sbk), hr], writes=[hr])
                    P.op("act", lambda e, hsl=hsl: e.activation(out=hstb[:, hsl], in_=hst[:, hsl], func=AF.Copy), reads=[hr], writes=[R("hstb", 2 * q, 2 * q + 2)])

            def ssd_h(ci):
                tsl = slice(ci * 128, (ci + 1) * 128)
                ynb2 = xdtb[ci % 2]
                xr = R("xdt%d" % (ci % 2))
                P.op("act", lambda e: e.activation(out=ynb2[:], in_=ybuf[:], func=AF.Square, accum_out=ssq[:, 4:5]), reads=[R("ybuf", 0, 8)], writes=[xr, R("ssq", 4, 5)])
                P.op("act", lambda e: e.activation(out=ssq[:, 5:6], in_=ssq[:, 4:5], func=AF.Ln, scale=1.0 / DI, bias=epsc[:, 0:1]), reads=[R("ssq", 4, 5), R("epsc")], writes=[R("ssq", 5, 6)])
                P.op("act", lambda e: e.activation(out=ssq[:, 6:7], in_=ssq[:, 5:6], func=AF.Exp, scale=-0.5), reads=[R("ssq", 5, 6)], writes=[R("ssq", 6, 7)])
                P.op("dve", lambda e: e.scalar_tensor_tensor(out=ynb2[:], in0=ybuf[:], scalar=ssq[:, 6:7], in1=snwb[:], op0=ALU.mult, op1=ALU.mult),
                     reads=[R("ybuf", 0, 8), R("ssq", 6, 7), R("snwb")], writes=[xr])
                for fc in range(16):
                    bk = fc // 8
                    c0 = (fc % 8) * 128
                    P.op("pe", lambda e, bk=bk, c0=c0, fc=fc: e.transpose(out=PBF(bk)[:, c0:c0 + 128], in_=ynb2[:, fc * 128:(fc + 1) * 128], identity=identb[:]),
                         reads=[xr, R("identb")], writes=[PR(bk)])
                for q in range(2):
                    P.op("act", lambda e, q=q: e.activation(out=ynT[:, q * 8:(q + 1) * 8, tsl], in_=PBF(q)[:, 0:1024].rearrange("p (a b) -> p a b", a=8), func=AF.Copy),
                         reads=[PR(q)], writes=[R("ubuf", 0, 8)])

            ssd_acd(0)
            ssd_b(0)
            for ci in range(4):
                if ci == 0:
                    ssd_decay(ci, 0)
                for hq in range(4):
                    if hq + 1 < 4:
                        ssd_decay(ci, hq + 1)
                    ssd_y(ci, hq)
                ssd_g(ci)
                if ci + 1 < 4:
                    ssd_acd(ci + 1)
                    ssd_b(ci + 1)
                    ssd_decay(ci + 1, 0)
                ssd_h(ci)
            if tb == nblk - 1:
                DMA("sp", o_sp, hst[:], "st_sp", reads=[R("hst", 0, 8)], writes=[R("o_sp")])

            for t in range(4):
                wt, wr = load_w(w_ssd[:, :, t * 256:(t + 1) * 256], 16, 256, tid=('ssd', t))
                for j in range(2):
                    oc = 2 * t + j
                    bk = next_bank()
                    proj_ws(wt, wr, j, 16, lambda kc: ynT[:, kc, :], lambda kc: [R("ubuf", 0, 8)], bk)
                    P.op("dve", lambda e, bk=bk, oc=oc: e.tensor_tensor(out=sA[:, 0:TB], in0=pb[bk][:, 0:TB], in1=gbT[:, oc, :], op=ALU.mult),
                         reads=[PR(bk), R("gT", oc, oc + 1)], writes=[R("sA")])
                    P.op("dve", lambda e, oc=oc: e.tensor_tensor(out=hT[:, oc, :], in0=sA[:, 0:TB], in1=amT[:, oc, :], op=ALU.add),
                         reads=[R("sA"), R("amT", oc, oc + 1)], writes=[R("hT", oc, oc + 1)])
            for t in range(2):
                wt, wr = load_w(w_out[:, :, t * 512:(t + 1) * 512], 8, 512, tid=('out', t))
                for ti in range(4):
                    bk = next_bank()
                    proj_as(wt, wr, ti, 8, hT_lhs, hT_regs, bk)
                    P.op("dve", lambda e, bk=bk, t=t: e.tensor_tensor(out=sB[:, 0:512], in0=pb[bk][:, 0:512], in1=g1B[:, t * 512:(t + 1) * 512], op=ALU.mult),
                         reads=[PR(bk), R("g1B", t, t + 1)], writes=[R("sB")])
                    P.op("dve", lambda e, ti=ti, t=t: e.tensor_tensor(out=xtok[:, ti, t * 512:(t + 1) * 512], in0=xtok[:, ti, t * 512:(t + 1) * 512], in1=sB[:, 0:512], op=ALU.add),
                         reads=[R("sB"), R("xtok", ti, ti + 1)], writes=[R("xtok", ti, ti + 1)])
            rmsnorm_to_T("a2T", a2T, 24)
            for t in range(11):
                wt, wr = load_w(w_ffi[:, :, t * 512:(t + 1) * 512], 8, 512, tid=('ffi', t))
                for j in range(2):
                    fc = 2 * t + j
                    bg = next_bank()
                    proj_ws(wt, wr, j, 8, hT_rhs, hT_regs, bg)
                    bu = next_bank()
                    proj_ws(wt, wr, j, 8, hT_rhs, hT_regs, bu, col0=256)
                    P.op("act", lambda e, bg=bg: e.activation(out=sA[:, 0:TB], in_=pb[bg][:, 0:TB], func=AF.Silu), reads=[PR(bg)], writes=[R("sA")])
                    P.op("dve", lambda e, bu=bu, fc=fc: e.tensor_tensor(out=fT[:, fc, :], in0=pb[bu][:, 0:TB], in1=sA[:, 0:TB], op=ALU.mult),
                         reads=[PR(bu), R("sA")], writes=[R("xbc", fc, fc + 1)])
            fT_lhs = lambda kc, ti: fT[:, kc, ti * 128:(ti + 1) * 128]
            fT_regs = lambda kc: [R("xbc", kc, kc + 1)]
            for half in range(2):
                for kg in range(3):
                    nk = 8 if kg < 2 else 6
                    wt, wr = load_w(w_ffo[:, kg * 8:kg * 8 + nk, half * 512:(half + 1) * 512], nk, 512, tid=('ffo', kg, half))
                    for ti in range(4):
                        proj_as(wt, wr, ti, nk, fT_lhs, fT_regs, 4 + ti, kc0=kg * 8, first=(kg == 0), last=(kg == 2))
                for ti in range(4):
                    P.op("dve", lambda e, ti=ti, half=half: e.tensor_tensor(out=sB[:, 0:512], in0=pb[4 + ti][:, 0:512], in1=g2B[:, half * 512:(half + 1) * 512], op=ALU.mult),
                         reads=[PR(4 + ti), R("g2B", half, half + 1)], writes=[R("sB")])
                    P.op("dve", lambda e, ti=ti, half=half: e.tensor_tensor(out=xtok[:, ti, half * 512:(half + 1) * 512], in0=xtok[:, ti, half * 512:(half + 1) * 512], in1=sB[:, 0:512], op=ALU.add),
                         reads=[R("sB"), R("xtok", ti, ti + 1)], writes=[R("xtok", ti, ti + 1)])
            for ti in range(4):
                ys = 0
                P.op("act", lambda e, ti=ti: e.activation(out=junk[:, 0:D], in_=xtok[:, ti, :], func=AF.Square, accum_out=ssq[:, ti:ti + 1]),
                     reads=[R("xtok", ti, ti + 1)], writes=[R("xdte", 0, 8), R("ssq", ti, ti + 1)])
                P.op("act", lambda e, ti=ti: e.activation(out=rsq[:, 4 + ti:5 + ti], in_=ssq[:, ti:ti + 1], func=AF.Ln, scale=1.0 / D, bias=epsc[:, 0:1]),
                     reads=[R("ssq", ti, ti + 1), R("epsc")], writes=[R("rsq", 4 + ti, 5 + ti)])
                P.op("act", lambda e, ti=ti: e.activation(out=rsq[:, ti:ti + 1], in_=rsq[:, 4 + ti:5 + ti], func=AF.Exp, scale=-0.5), reads=[R("rsq", 4 + ti, 5 + ti)], writes=[R("rsq", ti, ti + 1)])
                P.op("dve", lambda e, ti=ti, ys=ys: e.scalar_tensor_tensor(out=yst[ys][:], in0=xtok[:, ti, :], scalar=rsq[:, ti:ti + 1], in1=rowb[:, RB_FNW:RB_FNW + D], op0=ALU.mult, op1=ALU.mult),
                     reads=[R("xtok", ti, ti + 1), R("rsq", ti, ti + 1), R("rowb")], writes=[R("yst%d" % ys)])
                r0 = tb * TB + ti * 128
                DMA("sp", o_y[r0:r0 + 128, :], yst[ys][:], "st_y%d" % ys, reads=[R("yst%d" % ys)], writes=[R("o_y", tb * 4 + ti, tb * 4 + ti + 1)])


        if do_sample:
            Ssl = slice(1, 17)
            xbcF = xbc[:].rearrange("p a b -> p (a b)")
            stbuf = [xtok[:, 0:2, :].rearrange("p a b -> p (a b)"), xtok[:, 2:4, :].rearrange("p a b -> p (a b)")]
            streg = [R("xtok", 0, 2), R("xtok", 2, 4)]
            ubF = ubuf[:].rearrange("p a b -> p (a b)")
            stp = ubF[:, 0:1920].rearrange("p (c s r) -> p c s r", c=8, s=NS)
            newst = ubF[:, 1920:3840].rearrange("p (c s r) -> p c s r", c=8, s=NS)
            stc = ybuf[:, 0:1536].rearrange("p (c s r) -> p c s r", c=32, s=NS)
            newcst = hst[:, 0:1536].rearrange("p (c s r) -> p c s r", c=32, s=NS)
            gTf = gT[:].rearrange("p a b -> p (a b)").bitcast(F32)
            projS = gTf[:, 0:56 * NS].rearrange("p (c s) -> p c s", c=56)
            amF = amT[:].rearrange("p a b -> p (a b)").bitcast(F32)
            acc1 = amF[:, 0:512].rearrange("p (c s) -> p c s", c=32)
            acc2 = amF[:, 512:1024].rearrange("p (c s) -> p c s", c=32)
            amS = amF[:, 1024:1152].rearrange("p (c s) -> p c s", c=8)
            gaS = amF[:, 1152:1280].rearrange("p (c s) -> p c s", c=8)
            gbS = amF[:, 1280:1408].rearrange("p (c s) -> p c s", c=8)
            ptmp = amF[:, 1408:1536].rearrange("p (c s) -> p c s", c=8)
            sgS = amF[:, 1536:1568]
            xbcS = B_tok[:, 0:512].rearrange("p (c s) -> p c s", c=32)
            CBf = CBs[:].rearrange("p a b -> p (a b)")
            hTs = CBf[:, 0:128].rearrange("p (c s) -> p c s", c=8)
            mixTs = CBf[:, 128:256].rearrange("p (c s) -> p c s", c=8)
            pooledS = CBf[:, 256:384].rearrange("p (c s) -> p c s", c=8)
            ynTs = CBf[:, 384:640].rearrange("p (c s) -> p c s", c=16)
            fTs = CBf[:, 640:992].rearrange("p (c s) -> p c s", c=22)
            x_tokS = x_tok[0:NS, :]
            xdt_tokS = xdt[0:NS, :]
            szS = xdte[0:NS, :]
            xs_tok = yst[0][0:NS, :]
            szf = sz[:].rearrange("p a b -> p (a b)").bitcast(F32)
            ysS = szf[0:NS, 0:2048]
            gs1 = szf[0:NS, 2048:3072]
            gs2 = szf[0:NS, 3072:4096]
            xnS = dec[0][:].rearrange("p a b -> p (a b)")[0:NS, :]
            hTF = hT[:].rearrange("p a b -> p (a b)")
            ynS = hTF[0:NS, 0:2048]
            junkS = hTF[0:NS, 2048:4096]
            tmpDx = hTF[0:NS, :].bitcast(F32)
            decBs = sA[:, 0:512].rearrange("p (s h) -> p s h", s=NS)
            mask16 = sB[:, 0:256].rearrange("p (a b) -> p a b", a=NS)
            identfS = sB[:, 256:384]
            CmaskS = xbcF[:, 0:2048].rearrange("p (g s m) -> p g s m", g=NG, s=NS)
            ssS, rsS = ssq[0:NS, :], rsq[0:NS, :]
            RCB = R("CBs", 0, 8)
            RAM = R("amT", 0, 8)

            DMA("sp", xs_tok, xsm, "ld_xs", writes=[R("yst0")])
            DMA("sp", stp, st_pool, "ld_stp", writes=[R("ubuf", 0, 8)])
            DMA("sp", stc, st_conv, "ld_stc", writes=[R("ybuf", 0, 8)])
            P.op("pool", lambda e: e.memset(sB[:, 0:384], 0.0), writes=[R("sB")])
            P.op("pool", lambda e: e.affine_select(out=mask16, in_=mask16, pattern=[[1, NS], [-1, NS]], compare_op=ALU.not_equal, fill=1.0, base=0, channel_multiplier=0),
                 reads=[R("sB")], writes=[R("sB")])
            P.op("pool", lambda e: e.affine_select(out=identfS, in_=identfS, pattern=[[-1, 128]], compare_op=ALU.not_equal, fill=1.0, base=0, channel_multiplier=1),
                 reads=[R("sB")], writes=[R("sB")])
            for (gsv, c0, b0) in ((gs1, 16, 0), (gs2, 40, 2)):
                for c in range(8):
                    bk = b0 + c // 4
                    P.op("pe", lambda e, bk=bk, c=c, c0=c0: e.matmul(pb[bk][0:NS, (c % 4) * 128:(c % 4 + 1) * 128], lhsT=modT[:, c0 + c, Ssl], rhs=identfS, start=True, stop=True),
                         reads=[R("modT", c0 + c, c0 + c + 1), R("sB")], writes=[PR(bk)])
                for q in range(2):
                    P.op("act", lambda e, gsv=gsv, q=q, b0=b0: e.activation(out=gsv[:, q * 512:(q + 1) * 512], in_=pb[b0 + q][0:NS, 0:512], func=AF.Copy),
                         reads=[PR(b0 + q)], writes=[R("sz", 0, 4)])

            def s_norm_T(aT, bcol0):
                P.op("act", lambda e: e.activation(out=junkS[:, 0:D], in_=xs_tok, func=AF.Square, accum_out=ssS[:, 0:1]), reads=[R("yst0")], writes=[R("hT", 0, 8), R("ssq", 0, 8)])
                P.op("act", lambda e: e.activation(out=rsS[:, 4:5], in_=ssS[:, 0:1], func=AF.Ln, scale=1.0 / D, bias=epsc[0:NS, 0:1]), reads=[R("ssq", 0, 8), R("epsc")], writes=[R("rsq", 0, 8)])
                P.op("act", lambda e: e.activation(out=rsS[:, 0:1], in_=rsS[:, 4:5], func=AF.Exp, scale=-0.5), reads=[R("rsq", 0, 8)], writes=[R("rsq", 0, 8)])
                P.op("dve", lambda e: e.tensor_scalar(out=xnS, in0=xs_tok, scalar1=rsS[:, 0:1], scalar2=None, op0=ALU.mult), reads=[R("yst0"), R("rsq", 0, 8)], writes=[R("dec0", 0, 8)])
                for kc in range(8):
                    P.op("pe", lambda e, kc=kc: e.transpose(out=PBF(0)[:, kc * NS:(kc + 1) * NS], in_=xnS[:, kc * 128:(kc + 1) * 128], identity=identb[0:NS, 0:NS]),
                         reads=[R("dec0", 0, 8), R("identb")], writes=[PR(0)])
                P.op("dve", lambda e, aT=aT: e.tensor_tensor(out=ptmp, in0=PBF(0)[:, 0:128].rearrange("p (c s) -> p c s", c=8), in1=aT[:, :, Ssl], op=ALU.mult),
                     reads=[PR(0), R("a1T"), R("a2T")], writes=[RAM])
                P.op("dve", lambda e, bcol0=bcol0: e.tensor_tensor(out=hTs, in0=ptmp, in1=modT[:, bcol0:bcol0 + 8, Ssl], op=ALU.add),
                     reads=[RAM, R("modT", bcol0, bcol0 + 8)], writes=[RCB])

            s_norm_T(a1T, 0)
            ws_tiles = [(0, 0), (512, 4)] + [(3072 + 512 * t, 8 + 4 * t) for t in range(8)] + [(7200 + 512 * t, 40 + 4 * t) for t in range(4)]
            for (col0, cb0) in ws_tiles:
                wt, wr = load_w(w_in[:, :, col0:col0 + 512], 8, 512, tid=('in', col0))
                for j in range(4):
                    for kc in range(8):
                        P.op("pe", lambda e, j=j, kc=kc, wt=wt: e.matmul(pb[1][:, j * NS:(j + 1) * NS], lhsT=wt[:, kc, j * 128:(j + 1) * 128], rhs=hTs[:, kc, :], start=(kc == 0), stop=(kc == 7)),
                             reads=[wr, RCB], writes=[PR(1)])
                P.op("act", lambda e, cb0=cb0: e.activation(out=projS[:, cb0:cb0 + 4, :], in_=pb[1][:, 0:4 * NS].rearrange("p (c s) -> p c s", c=4), func=AF.Copy),
                     reads=[PR(1)], writes=[R("gT", 0, 8)])
            for t in range(4):
                wt, wr = load_w(w_in[:, :, 1024 + t * 512:1024 + (t + 1) * 512], 8, 512, tid=('in', 1024 + t * 512))
                for kc in range(8):
                    P.op("pe", lambda e, kc=kc, wt=wt: e.matmul(pb[2][0:NS, 0:512], lhsT=hTs[:, kc, :], rhs=wt[:, kc, :], start=(kc == 0), stop=(kc == 7)),
                         reads=[wr, RCB], writes=[PR(2)])
                P.op("act", lambda e, t=t: e.activation(out=szS[:, t * 512:(t + 1) * 512], in_=pb[2][0:NS, 0:512], func=AF.Silu), reads=[PR(2)], writes=[R("xdte", 0, 8)])
            for kc in range(8):
                P.op("pe", lambda e, kc=kc: e.matmul(pb[3][0:NS, 0:32], lhsT=hTs[:, kc, :], rhs=wdt[:, kc, :], start=(kc == 0), stop=(kc == 7)), reads=[RCB, R("wdt")], writes=[PR(3)])
            d_x, d_t, d_u, d_dt, d_dec = dtx[0:NS, 0, :], dtt[0:NS, 0, :], dtu[0:NS, 0, :], dt_tok[0:NS, 0, :], dtx[0:NS, 1, :]
            P.op("dve", lambda e: e.tensor_tensor(out=d_x, in0=pb[3][0:NS, 0:32], in1=rowb[0:NS, RB_DTB:RB_DTB + 32], op=ALU.add), reads=[PR(3), R("rowb")], writes=[R("dtx")])
            P.op("act", lambda e: e.activation(out=d_t, in_=d_x, func=AF.Abs), reads=[R("dtx")], writes=[R("dtt")])
            P.op("act", lambda e: e.activation(out=d_u, in_=d_t, func=AF.Exp, scale=-1.0), reads=[R("dtt")], writes=[R("dtu")])
            P.op("act", lambda e: e.activation(out=d_t, in_=d_u, func=AF.Ln, bias=1.0, scale=1.0), reads=[R("dtu")], writes=[R("dtt")])
            P.op("dve", lambda e: e.scalar_tensor_tensor(out=d_dt, in0=d_x, scalar=0.0, in1=d_t, op0=ALU.max, op1=ALU.add), reads=[R("dtx"), R("dtt")], writes=[R("dt_tok")])
            P.op("dve", lambda e: e.tensor_tensor(out=d_u, in0=d_dt, in1=anegb[0:NS, :], op=ALU.mult), reads=[R("dt_tok"), R("anegb")], writes=[R("dtu")])
            P.op("act", lambda e: e.activation(out=d_dec, in_=d_u, func=AF.Exp), reads=[R("dtu")], writes=[R("dtx")])
            for s_ in range(NS):
                P.op("pe", lambda e, s_=s_: e.matmul(pb[0][:, s_ * 32:(s_ + 1) * 32], lhsT=identfS[0:NS, s_:s_ + 1].to_broadcast([NS, 128]), rhs=d_dec, start=True, stop=True),
                     reads=[R("sB"), R("dtx")], writes=[PR(0)])
            P.op("act", lambda e: e.activation(out=sA[:, 0:512], in_=pb[0][:, 0:512], func=AF.Copy), reads=[PR(0)], writes=[R("sA")])
            for g in range(4):
                w = 2 ** (g + 1)
                ug = projS[:, 2 * g:2 * g + 2, :]
                P.op("dve", lambda e, g=g, w=w: e.reduce_sum(out=ptmp[:, 0:2, :], in_=stp[:, 2 * g:2 * g + 2, :, 15 - (w - 1):15], axis=mybir.AxisListType.X),
                     reads=[R("ubuf", 0, 8)], writes=[RAM])
                P.op("dve", lambda e, ug=ug: e.tensor_tensor(out=ptmp[:, 0:2, :], in0=ptmp[:, 0:2, :], in1=ug, op=ALU.add), reads=[RAM, R("gT", 0, 8)], writes=[RAM])
                P.op("dve", lambda e, ug=ug, g=g, w=w: e.scalar_tensor_tensor(out=pooledS[:, 2 * g:2 * g + 2, :], in0=ptmp[:, 0:2, :], scalar=1.0 / w, in1=ug, op0=ALU.mult, op1=ALU.subtract),
                     reads=[RAM, R("gT", 0, 8)], writes=[RCB])
            P.op("act", lambda e: e.activation(out=gaS, in_=projS[:, 40:48, :], func=AF.Sigmoid), reads=[R("gT", 0, 8)], writes=[RAM])
            P.op("act", lambda e: e.activation(out=gbS, in_=projS[:, 48:56, :], func=AF.Sigmoid), reads=[R("gT", 0, 8)], writes=[RAM])
            wplt, wplr = load_w(w_pool.rearrange("p g k c -> p (g k) c"), 8, 256, tid=('pool',))
            for g in range(4):
                for j in range(2):
                    oc = 2 * g + j
                    for k2 in range(2):
                        P.op("pe", lambda e, g=g, j=j, k2=k2, oc=oc: e.matmul(pb[2][:, oc * NS:(oc + 1) * NS], lhsT=wplt[:, 2 * g + k2, j * 128:(j + 1) * 128], rhs=pooledS[:, 2 * g + k2, :], start=(k2 == 0), stop=(k2 == 1)),
                             reads=[wplr, RCB], writes=[PR(2)])
            for oc in range(8):
                P.op("dve", lambda e, oc=oc: e.scalar_tensor_tensor(out=amS[:, oc, :], in0=pb[2][:, oc * NS:(oc + 1) * NS], scalar=pv[:, PV_PSC + oc:PV_PSC + oc + 1], in1=gaS[:, oc, :], op0=ALU.mult, op1=ALU.mult),
                     reads=[PR(2), R("pv"), RAM], writes=[RAM])
            P.op("pool", lambda e: e.tensor_copy(out=newst[:, :, :, 0:14], in_=stp[:, :, :, 1:15]), reads=[R("ubuf", 0, 8)], writes=[R("ubuf", 0, 8)])
            P.op("pool", lambda e: e.tensor_copy(out=newst[:, :, :, 14], in_=projS[:, 0:8, :]), reads=[R("gT", 0, 8)], writes=[R("ubuf", 0, 8)])
            DMA("sp", o_ps, newst, "st_ps", reads=[R("ubuf", 0, 8)], writes=[R("o_ps")])
            xnew = projS[:, 8:40, :]
            cwb = lambda k: pv[:, PV_CW + 32 * k:PV_CW + 32 * k + 32].unsqueeze(2).to_broadcast([128, 32, NS])
            P.op("dve", lambda e: e.tensor_tensor(out=acc1, in0=xnew, in1=cwb(3), op=ALU.mult), reads=[R("gT", 0, 8), R("pv")], writes=[RAM])
            for k in range(3):
                P.op("dve", lambda e, k=k: e.tensor_tensor(out=acc2, in0=stc[:, :, :, k], in1=cwb(k), op=ALU.mult), reads=[R("ybuf", 0, 8), R("pv")], writes=[RAM])
                P.op("dve", lambda e: e.tensor_tensor(out=acc1, in0=acc1, in1=acc2, op=ALU.add), reads=[RAM], writes=[RAM])
            P.op("dve", lambda e: e.tensor_tensor(out=acc1, in0=acc1, in1=pv[:, PV_CB:PV_CB + 32].unsqueeze(2).to_broadcast([128, 32, NS]), op=ALU.add), reads=[RAM, R("pv")], writes=[RAM])
            P.op("act", lambda e: e.activation(out=xbcS, in_=acc1, func=AF.Silu), reads=[RAM], writes=[R("B_tok")])
            P.op("pool", lambda e: e.tensor_copy(out=newcst[:, :, :, 0:2], in_=stc[:, :, :, 1:3]), reads=[R("ybuf", 0, 8)], writes=[R("hst", 0, 8)])
            P.op("pool", lambda e: e.tensor_copy(out=newcst[:, :, :, 2], in_=xnew), reads=[R("gT", 0, 8)], writes=[R("hst", 0, 8)])
            DMA("sp", o_cs, newcst, "st_cs", reads=[R("hst", 0, 8)], writes=[R("o_cs")])
            for fc in range(16):
                bk = 1 + fc // 8
                P.op("pe", lambda e, fc=fc, bk=bk: e.transpose(out=PBF(bk)[0:NS, (fc % 8) * 128:(fc % 8 + 1) * 128], in_=xbcS[:, fc, :], identity=identb[:]),
                     reads=[R("B_tok"), R("identb")], writes=[PR(bk)])
            for q in range(2):
                P.op("act", lambda e, q=q: e.activation(out=x_tokS[:, q * 1024:(q + 1) * 1024], in_=PBF(1 + q)[0:NS, 0:1024], func=AF.Copy), reads=[PR(1 + q)], writes=[R("xdt1")])
            P.op("dve", lambda e: e.tensor_tensor(out=xdt_tokS.rearrange("p (h q) -> p h q", h=NH), in0=x_tokS.rearrange("p (h q) -> p h q", h=NH),
                                                  in1=d_dt.unsqueeze(2).to_broadcast([NS, NH, HP]), op=ALU.mult), reads=[R("xdt1"), R("dt_tok")], writes=[R("xdt0")])
            P.op("dve", lambda e: e.tensor_tensor(out=CmaskS, in0=xbcS[:, 24:32, :].unsqueeze(3).to_broadcast([128, NG, NS, NS]),
                                                  in1=mask16.unsqueeze(1).to_broadcast([128, NG, NS, NS]), op=ALU.mult), reads=[R("B_tok"), R("sB")], writes=[R("xbc", 0, 32)])
            P.op("pool", lambda e: e.memset(ysS, 0.0), writes=[R("sz", 0, 4)])
            def samp_L(s_):
                sl = s_ % 2
                DMA("sp", stbuf[sl], st_ssm[s_], "ld_st%d" % sl, writes=[streg[sl]])

            def samp_A(s_):
                sl = s_ % 2
                buf = stbuf[sl]
                for q in range(4):
                    P.op("pe", lambda e, q=q: e.matmul(pb[q][:, 0:512], lhsT=identb[0:NS, s_:s_ + 1].to_broadcast([NS, 128]), rhs=xdt_tokS[:, q * 512:(q + 1) * 512], start=True, stop=True),
                         reads=[R("identb"), R("xdt0")], writes=[PR(q)])
                P.op("pool", lambda e: e.tensor_tensor(out=buf.rearrange("p (h q) -> p h q", h=NH), in0=buf.rearrange("p (h q) -> p h q", h=NH),
                                                       in1=decBs[:, s_, :].unsqueeze(2).to_broadcast([128, NH, HP]), op=ALU.mult), reads=[streg[sl], R("sA")], writes=[streg[sl]])
                for g in range(NG):
                    P.op("dve", lambda e, g=g: e.scalar_tensor_tensor(out=buf[:, g * 256:(g + 1) * 256], in0=pb[g // 2][:, (g % 2) * 256:(g % 2 + 1) * 256],
                                                                       scalar=xbcS[:, 16 + g, s_:s_ + 1], in1=buf[:, g * 256:(g + 1) * 256], op0=ALU.mult, op1=ALU.add),
                         reads=[PR(g // 2), R("B_tok"), streg[sl]], writes=[streg[sl]])
                P.op("act", lambda e: e.activation(out=hstb[:], in_=buf, func=AF.Copy), reads=[streg[sl]], writes=[R("hstb", 0, 8)])
                DMA("sp", o_ss[s_], buf, "st_ss%d" % sl, reads=[streg[sl]], writes=[R("o_ss", s_, s_ + 1)])

            def samp_A2(s_):
                for g in range(NG):
                    P.op("pe", lambda e, g=g: e.matmul(pb[4 + g // 2][0:NS, (g % 2) * 256:(g % 2 + 1) * 256], lhsT=CmaskS[:, g, s_, :], rhs=hstb[:, g * 256:(g + 1) * 256], start=True, stop=True),
                         reads=[R("xbc", 0, 32), R("hstb", 0, 8)], writes=[PR(4 + g // 2)])

            def samp_B(s_):
                for q in range(4):
                    P.op("dve", lambda e, q=q: e.tensor_tensor(out=ysS[:, q * 512:(q + 1) * 512], in0=pb[4 + q][0:NS, 0:512], in1=ysS[:, q * 512:(q + 1) * 512], op=ALU.add),
                         reads=[PR(4 + q), R("sz", 0, 4)], writes=[R("sz", 0, 4)])

            samp_L(0)
            samp_L(1)
            samp_A(0)
            samp_A2(0)
            for s_ in range(NS):
                if s_ + 1 < NS:
                    samp_A(s_ + 1)
                samp_B(s_)
                if s_ + 2 < NS:
                    samp_L(s_ + 2)
                if s_ + 1 < NS:
                    samp_A2(s_ + 1)
            P.op("dve", lambda e: e.tensor_tensor(out=tmpDx.rearrange("p (h q) -> p h q", h=NH), in0=x_tokS.rearrange("p (h q) -> p h q", h=NH),
                                                  in1=rowb[0:NS, RB_DSK:RB_DSK + 32].unsqueeze(2).to_broadcast([NS, NH, HP]), op=ALU.mult), reads=[R("xdt1"), R("rowb")], writes=[R("hT", 0, 8)])
            P.op("dve", lambda e: e.tensor_tensor(out=ysS, in0=ysS, in1=tmpDx, op=ALU.add), reads=[R("sz", 0, 4), R("hT", 0, 8)], writes=[R("sz", 0, 4)])
            P.op("dve", lambda e: e.tensor_tensor(out=ysS, in0=ysS, in1=szS, op=ALU.mult), reads=[R("sz", 0, 4), R("xdte", 0, 8)], writes=[R("sz", 0, 4)])
            P.op("act", lambda e: e.activation(out=junkS, in_=ysS, func=AF.Square, accum_out=ssS[:, 1:2]), reads=[R("sz", 0, 4)], writes=[R("hT", 0, 8), R("ssq", 0, 8)])
            P.op("act", lambda e: e.activation(out=rsS[:, 5:6], in_=ssS[:, 1:2], func=AF.Ln, scale=1.0 / DI, bias=epsc[0:NS, 0:1]), reads=[R("ssq", 0, 8), R("epsc")], writes=[R("rsq", 0, 8)])
            P.op("act", lambda e: e.activation(out=rsS[:, 1:2], in_=rsS[:, 5:6], func=AF.Exp, scale=-0.5), reads=[R("rsq", 0, 8)], writes=[R("rsq", 0, 8)])
            P.op("dve", lambda e: e.scalar_tensor_tensor(out=ynS, in0=ysS, scalar=rsS[:, 1:2], in1=snwb[0:NS, :], op0=ALU.mult, op1=ALU.mult),
                 reads=[R("sz", 0, 4), R("rsq", 0, 8), R("snwb")], writes=[R("hT", 0, 8)])
            for fc in range(16):
                P.op("pe", lambda e, fc=fc: e.transpose(out=PBF(0)[:, fc * NS:(fc + 1) * NS], in_=ynS[:, fc * 128:(fc + 1) * 128], identity=identb[0:NS, 0:NS]),
                     reads=[R("hT", 0, 8), R("identb")], writes=[PR(0)])
            P.op("act", lambda e: e.activation(out=ynTs, in_=PBF(0)[:, 0:256].rearrange("p (c s) -> p c s", c=16), func=AF.Copy), reads=[PR(0)], writes=[RCB])
            for t in range(4):
                wt, wr = load_w(w_ssd[:, :, t * 256:(t + 1) * 256], 16, 256, tid=('ssd', t))
                for j in range(2):
                    oc = 2 * t + j
                    for kc in range(16):
                        P.op("pe", lambda e, j=j, kc=kc, oc=oc, wt=wt: e.matmul(pb[1][:, oc * NS:(oc + 1) * NS], lhsT=wt[:, kc, j * 128:(j + 1) * 128], rhs=ynTs[:, kc, :], start=(kc == 0), stop=(kc == 15)),
                             reads=[wr, RCB], writes=[PR(1)])
            P.op("dve", lambda e: e.tensor_tensor(out=ptmp, in0=pb[1][:, 0:128].rearrange("p (c s) -> p c s", c=8), in1=gbS, op=ALU.mult), reads=[PR(1), RAM], writes=[RAM])
            P.op("dve", lambda e: e.tensor_tensor(out=mixTs, in0=ptmp, in1=amS, op=ALU.add), reads=[RAM], writes=[RCB])
            tmpR = ysS[:, 0:512]
            for t in range(2):
                wt, wr = load_w(w_out[:, :, t * 512:(t + 1) * 512], 8, 512, tid=('out', t))
                for kc in range(8):
                    P.op("pe", lambda e, kc=kc, wt=wt, t=t: e.matmul(pb[2 + t][0:NS, 0:512], lhsT=mixTs[:, kc, :], rhs=wt[:, kc, :], start=(kc == 0), stop=(kc == 7)), reads=[wr, RCB], writes=[PR(2 + t)])
                P.op("dve", lambda e, t=t: e.tensor_tensor(out=tmpR, in0=pb[2 + t][0:NS, 0:512], in1=gs1[:, t * 512:(t + 1) * 512], op=ALU.mult), reads=[PR(2 + t), R("sz", 0, 4)], writes=[R("sz", 0, 4)])
                P.op("dve", lambda e, t=t: e.tensor_tensor(out=xs_tok[:, t * 512:(t + 1) * 512], in0=xs_tok[:, t * 512:(t + 1) * 512], in1=tmpR, op=ALU.add), reads=[R("sz", 0, 4), R("yst0")], writes=[R("yst0")])
            s_norm_T(a2T, 24)
            for t in range(11):
                wt, wr = load_w(w_ffi[:, :, t * 512:(t + 1) * 512], 8, 512, tid=('ffi', t))
                for q in range(4):
                    for kc in range(8):
                        P.op("pe", lambda e, q=q, kc=kc, wt=wt: e.matmul(pb[1][:, q * NS:(q + 1) * NS], lhsT=wt[:, kc, q * 128:(q + 1) * 128], rhs=hTs[:, kc, :], start=(kc == 0), stop=(kc == 7)),
                             reads=[wr, RCB], writes=[PR(1)])
                P.op("act", lambda e: e.activation(out=sgS, in_=pb[1][:, 0:2 * NS], func=AF.Silu), reads=[PR(1)], writes=[RAM])
                P.op("dve", lambda e, t=t: e.tensor_tensor(out=fTs[:, 2 * t:2 * t + 2, :], in0=pb[1][:, 2 * NS:4 * NS].rearrange("p (c s) -> p c s", c=2), in1=sgS.rearrange("p (c s) -> p c s", c=2), op=ALU.mult),
                     reads=[PR(1), RAM], writes=[RCB])
            for half in range(2):
                for kg in range(3):
                    nk = 8 if kg < 2 else 6
                    wt, wr = load_w(w_ffo[:, kg * 8:kg * 8 + nk, half * 512:(half + 1) * 512], nk, 512, tid=('ffo', kg, half))
                    for kc in range(nk):
                        P.op("pe", lambda e, kc=kc, kg=kg, nk=nk, wt=wt, half=half: e.matmul(pb[2 + half][0:NS, 0:512], lhsT=fTs[:, kg * 8 + kc, :], rhs=wt[:, kc, :], start=(kg == 0 and kc == 0), stop=(kg == 2 and kc == nk - 1)),
                             reads=[wr, RCB], writes=[PR(2 + half)])
                P.op("dve", lambda e, half=half: e.tensor_tensor(out=tmpR, in0=pb[2 + half][0:NS, 0:512], in1=gs2[:, half * 512:(half + 1) * 512], op=ALU.mult), reads=[PR(2 + half), R("sz", 0, 4)], writes=[R("sz", 0, 4)])
                P.op("dve", lambda e, half=half: e.tensor_tensor(out=xs_tok[:, half * 512:(half + 1) * 512], in0=xs_tok[:, half * 512:(half + 1) * 512], in1=tmpR, op=ALU.add), reads=[R("sz", 0, 4), R("yst0")], writes=[R("yst0")])
            P.op("act", lambda e: e.activation(out=junkS[:, 0:D], in_=xs_tok, func=AF.Square, accum_out=ssS[:, 2:3]), reads=[R("yst0")], writes=[R("hT", 0, 8), R("ssq", 0, 8)])
            P.op("act", lambda e: e.activation(out=rsS[:, 6:7], in_=ssS[:, 2:3], func=AF.Ln, scale=1.0 / D, bias=epsc[0:NS, 0:1]), reads=[R("ssq", 0, 8), R("epsc")], writes=[R("rsq", 0, 8)])
            P.op("act", lambda e: e.activation(out=rsS[:, 2:3], in_=rsS[:, 6:7], func=AF.Exp, scale=-0.5), reads=[R("rsq", 0, 8)], writes=[R("rsq", 0, 8)])
            P.op("dve", lambda e: e.scalar_tensor_tensor(out=xs_tok, in0=xs_tok, scalar=rsS[:, 2:3], in1=rowb[0:NS, RB_FNW:RB_FNW + D], op0=ALU.mult, op1=ALU.mult),
                 reads=[R("yst0"), R("rsq", 0, 8), R("rowb")], writes=[R("yst0")])
            DMA("sp", o_ys, xs_tok, "st_ys", reads=[R("yst0")], writes=[R("o_ys")])

        P.op("sp", None, reads=[R("o_y", 0, 4 * NBLK), R("o_pp"), R("o_cp"), R("o_sp"), R("o_ys"), R("o_ps"), R("o_cs"), R("o_ss", 0, NS)])

        P.analyze()
        sems_e = {e: es.enter_context(nc.semaphore("se_" + e)) for e in ENGS}
        sems_d = {k: es.enter_context(nc.semaphore("sd_" + k)) for k in sorted(dma_keys)}
        P.emit(sems_e, sems_d)
    return nc


def _tile_k(w):
    K, N = w.shape
    return np.ascontiguousarray(w.reshape(K // 128, 128, N).transpose(1, 0, 2))


def _fm(v):
    return np.ascontiguousarray(v.reshape(-1, 128).T)


_NC_CACHE = {}


def kernel(x_prompt, x_sample, c_prompt, c_sample, state_pool, state_conv, state_ssm, w_ada, b_ada, norm1_w,
           w_in, w_pool, pool_scale, conv_w, conv_b, dt_bias, A_log, D_skip, ssd_norm_w, w_ssd_proj, w_out,
           norm2_w, w_ffn_in, w_ffn_out, final_norm_w):
    f = np.float32
    n = 8
    x_prompt = np.asarray(x_prompt, f)
    pvec = np.zeros((128, PV_N), f)
    pvec[:, PV_N1W:PV_N1W + 8] = _fm(np.asarray(norm1_w[0], f))
    pvec[:, PV_PSC:PV_PSC + 8] = _fm(np.asarray(pool_scale[0], f))
    cw = np.asarray(conv_w[0], f)
    for k in range(4):
        pvec[:, PV_CW + 32 * k:PV_CW + 32 * k + 32] = _fm(cw[k])
    pvec[:, PV_CB:PV_CB + 32] = _fm(np.asarray(conv_b[0], f))
    pvec[:, PV_N2W:PV_N2W + 8] = _fm(np.asarray(norm2_w[0], f))
    pvec[:, PV_BADA:PV_BADA + 48] = _fm(np.asarray(b_ada[0], f))
    pvec[:, PV_DF:PV_DF + 16] = _fm(np.repeat(np.asarray(D_skip[0], f), HP))
    rowb = np.zeros((128, RB_N), f)
    rowb[:, RB_FNW:RB_FNW + D] = np.asarray(final_norm_w, f)[None, :]
    bg = np.zeros((128, 2 * D), f)
    bg[:, 0:D] = np.asarray(b_ada[0], f)[None, 2 * D:3 * D]
    bg[:, D:2 * D] = np.asarray(b_ada[0], f)[None, 5 * D:6 * D]
    rowb[:, RB_DSK:RB_DSK + 32] = np.asarray(D_skip[0], f)[None, :]
    rowb[:, RB_ALOG:RB_ALOG + 32] = np.asarray(A_log[0], f)[None, :]
    rowb[:, RB_DTB:RB_DTB + 32] = np.asarray(dt_bias[0], f)[None, :]
    snw = np.ascontiguousarray(np.broadcast_to(np.asarray(ssd_norm_w[0], f)[None, :], (128, DI)))
    w_ada_t = _tile_k(np.asarray(w_ada[0], f))
    w_in_t = _tile_k(np.asarray(w_in[0], f))
    wp = np.asarray(w_pool[0], f)
    w_pool_t = np.ascontiguousarray(np.stack([_tile_k(wp[g]) for g in range(4)], axis=1))
    w_ssd_t = _tile_k(np.asarray(w_ssd_proj[0], f))
    w_out_t = _tile_k(np.asarray(w_out[0], f))
    wfi = np.asarray(w_ffn_in[0], f)
    perm = np.concatenate([np.concatenate([np.arange(256 * t, 256 * t + 256), DFF + np.arange(256 * t, 256 * t + 256)]) for t in range(11)])
    w_ffi_t = _tile_k(np.ascontiguousarray(wfi[:, perm]))
    w_ffo_t = _tile_k(np.asarray(w_ffn_out[0], f))

    in_maps = []
    for b in range(n):
        s0, s1 = NS * b, NS * (b + 1)
        c17 = np.concatenate([np.asarray(c_prompt[b:b + 1], f), np.asarray(c_sample[s0:s1], f)], axis=0)
        cT = np.ascontiguousarray(c17.T.reshape(8, 128, 17).transpose(1, 0, 2))
        sp = np.asarray(state_pool[0, s0:s1], f)
        sp_t = np.ascontiguousarray(sp.reshape(NS, 15, 8, 128).transpose(3, 2, 0, 1))
        sc = np.asarray(state_conv[0, s0:s1], f)
        sc_t = np.ascontiguousarray(sc.reshape(NS, 3, 32, 128).transpose(3, 2, 0, 1))
        ss = np.asarray(state_ssm[0, s0:s1], f)
        ss_t = np.ascontiguousarray(ss.reshape(NS, DI, DST).transpose(0, 2, 1))
        in_maps.append({
            "xp": np.ascontiguousarray(x_prompt[b]),
            "xsm": np.ascontiguousarray(np.asarray(x_sample[s0:s1, 0], f)),
            "cT": cT, "pvec": pvec, "rowb": rowb, "snw": snw, "bg": bg,
            "w_ada": w_ada_t, "w_in": w_in_t, "w_pool": w_pool_t, "w_ssd": w_ssd_t, "w_out": w_out_t,
            "w_ffi": w_ffi_t, "w_ffo": w_ffo_t,
            "st_pool": sp_t, "st_conv": sc_t, "st_ssm": ss_t,
        })
    nblk = int(os.environ.get("K_NBLK", NBLK))
    ncores = int(os.environ.get("K_CORES", n))
    if "nc" not in _NC_CACHE:
        _NC_CACHE["nc"] = build_nc(nblk=nblk)
    nc = _NC_CACHE["nc"]
    res = run_bass_kernel_spmd(nc, in_maps[:ncores], core_ids=list(range(ncores)))
    rs = list(res.results)
    while len(rs) < n:
        rs.append({k: np.zeros_like(v) for k, v in rs[0].items()})
    y_prompt = np.stack([rs[b]["o_y"] for b in range(n)], axis=0)
    y_sample = np.concatenate([rs[b]["o_ys"] for b in range(n)], axis=0)[:, None, :]
    pool_p = np.stack([rs[b]["o_pp"].transpose(2, 1, 0).reshape(15, D) for b in range(n)], axis=0)[None]
    conv_p = np.stack([rs[b]["o_cp"].transpose(2, 1, 0).reshape(3, CONV) for b in range(n)], axis=0)[None]
    ssm_p = np.stack([rs[b]["o_sp"].T.reshape(NH, HP, DST) for b in range(n)], axis=0)[None]
    pool_s = np.concatenate([rs[b]["o_ps"].transpose(2, 3, 1, 0).reshape(NS, 15, D) for b in range(n)], axis=0)[None]
    conv_s = np.concatenate([rs[b]["o_cs"].transpose(2, 3, 1, 0).reshape(NS, 3, CONV) for b in range(n)], axis=0)[None]
    ssm_s = np.concatenate([rs[b]["o_ss"].transpose(0, 2, 1).reshape(NS, NH, HP, DST) for b in range(n)], axis=0)[None]
    return (np.ascontiguousarray(y_prompt, dtype=f), np.ascontiguousarray(y_sample, dtype=f),
            np.ascontiguousarray(pool_p, dtype=f), np.ascontiguousarray(conv_p, dtype=f),
            np.ascontiguousarray(ssm_p, dtype=f), np.ascontiguousarray(pool_s, dtype=f),
            np.ascontiguousarray(conv_s, dtype=f), np.ascontiguousarray(ssm_s, dtype=f))
```

```python
import os
from contextlib import ExitStack

import numpy as np
import concourse.bass as bass
import concourse.mybir as mybir
from concourse.bass_utils import run_bass_kernel_spmd

F32 = mybir.dt.float32
BF16 = mybir.dt.bfloat16
AF = mybir.ActivationFunctionType
ALU = mybir.AluOpType

ENGS = ("pe", "act", "dve", "pool", "sp")

D = 1024
SEQ = 2048
TB = 512
NBLK = SEQ // TB
NS = 16
DI = 2048
NH = 32
HP = 64
NG = 8
DST = 128
CONV = 4096
DFF = 2816
IN_COLS = 9248
EPS = 1e-6

PV_N1W, PV_PSC, PV_CW, PV_CB, PV_N2W, PV_BADA, PV_DF, PV_N = 0, 8, 16, 144, 176, 184, 232, 248
RB_FNW, RB_DSK, RB_ALOG, RB_DTB, RB_N = 0, 1024, 1056, 1088, 1120


class Op:
    __slots__ = ("eng", "fn", "reads", "writes", "dkey", "dgroup", "idx", "waits",
                 "sig", "cnt", "eidx")

    def __init__(self, eng, fn, reads, writes, dkey=None, dgroup=None):
        self.eng = eng
        self.fn = fn
        self.reads = reads
        self.writes = writes
        self.dkey = dkey
        self.dgroup = dgroup
        self.waits = {}
        self.sig = False
        self.cnt = 0


class Prog:
    def __init__(self, nc):
        self.nc = nc
        self.ops = []

    def op(self, eng, fn, reads=(), writes=()):
        o = Op(eng, fn, list(reads), list(writes))
        o.idx = len(self.ops)
        self.ops.append(o)
        return o

    def dma(self, eng, fn, reads=(), writes=(), key=None, group=None):
        o = Op(eng, fn, list(reads), list(writes), dkey=key, dgroup=group)
        o.idx = len(self.ops)
        self.ops.append(o)
        return o

    def analyze(self):
        ops = self.ops
        ecount = {e: 0 for e in ENGS}
        for o in ops:
            o.eidx = ecount[o.eng]
            ecount[o.eng] += 1
        key_ops = {}
        for o in ops:
            if o.dkey is not None:
                key_ops.setdefault(o.dkey, []).append(o)
        dma_cum, dma_prev = {}, {}
        for k, lst in key_ops.items():
            groups = []
            for o in lst:
                if groups and o.dgroup is not None and groups[-1][0] == o.dgroup:
                    groups[-1][1].append(o)
                else:
                    groups.append((o.dgroup, [o]))
            cum = 0
            for g, gl in groups:
                prev = cum
                cum += len(gl)
                for o in gl:
                    dma_cum[o.idx] = cum
                    dma_prev[o.idx] = prev
        self.keys = sorted(key_ops.keys())
        recs = {}
        waited = {}
        pend = []
        for o in ops:
            d = set()
            for (buf, lo, hi) in o.reads:
                for r in recs.get(buf, ()):
                    if r[3] and r[0] < hi and lo < r[1]:
                        d.add(r[2])
            for (buf, lo, hi) in o.writes:
                for r in recs.get(buf, ()):
                    if r[0] < hi and lo < r[1]:
                        d.add(r[2])
            d.discard(o.idx)
            for (buf, lo, hi) in o.writes:
                lst = recs.setdefault(buf, [])
                lst[:] = [r for r in lst if not (lo <= r[0] and r[1] <= hi)]
                lst.append([lo, hi, o.idx, True])
            for (buf, lo, hi) in o.reads:
                lst = recs.setdefault(buf, [])
                lst[:] = [r for r in lst if not ((not r[3]) and ops[r[2]].eng == o.eng
                                                 and ops[r[2]].dkey is None and o.dkey is None
                                                 and lo <= r[0] and r[1] <= hi)]
                lst.append([lo, hi, o.idx, False])
            need = {}
            for di in d:
                p = ops[di]
                if p.dkey is not None:
                    sk = ("d", p.dkey)
                    val = 16 * dma_cum[p.idx]
                    if need.get(sk, 0) < val:
                        need[sk] = val
                    continue
                if p.eng == o.eng and o.dkey is None:
                    if o.eng in ("pe", "sp"):
                        continue
                sk = ("e", p.eng)
                cur = need.get(sk)
                if cur is None or cur.eidx < p.eidx:
                    need[sk] = p
            if o.dkey is not None and dma_prev[o.idx] > 0:
                sk = ("d", o.dkey)
                val = 16 * dma_prev[o.idx]
                if need.get(sk, 0) < val:
                    need[sk] = val
            o.waits = need
            for sk, v in need.items():
                if sk[0] == "e":
                    v.sig = True
        cnt = {e: 0 for e in ENGS}
        for o in ops:
            if o.dkey is None and o.sig:
                cnt[o.eng] += 1
            o.cnt = cnt[o.eng]
        for o in ops:
            final = {}
            for sk, v in o.waits.items():
                val = v.cnt if sk[0] == "e" else v
                wk = (o.eng, sk)
                if waited.get(wk, 0) >= val:
                    continue
                waited[wk] = val
                final[sk] = val
            o.waits = final

    def emit(self, sems_e, sems_d):
        nc = self.nc
        per = {e: [o for o in self.ops if o.eng == e] for e in ENGS}

        def run(engname, eng):
            for o in per[engname]:
                for sk, val in o.waits.items():
                    sem = sems_e[sk[1]] if sk[0] == "e" else sems_d[sk[1]]
                    eng.wait_ge(sem, val)
                if o.fn is None:
                    continue
                ins = o.fn(eng)
                if o.dkey is not None:
                    ins.then_inc(sems_d[o.dkey], 16)
                elif o.sig:
                    ins.then_inc(sems_e[o.eng], 1)

        with nc.Block() as block:
            @block.tensor
            def _(e):
                run("pe", e)

            @block.scalar
            def _(e):
                run("act", e)

            @block.vector
            def _(e):
                run("dve", e)

            @block.gpsimd
            def _(e):
                run("pool", e)

            @block.sync
            def _(e):
                run("sp", e)


def R(name, lo=0, hi=1):
    return (name, lo, hi)


def build_nc(nblk=NBLK, do_sample=True):
    nc = bass.Bass("TRN2", target_bir_lowering=False)

    def din(name, shape):
        return nc.dram_tensor(name, list(shape), F32, kind="ExternalInput").ap()

    def dout(name, shape):
        return nc.dram_tensor(name, list(shape), F32, kind="ExternalOutput").ap()

    xp = din("xp", [SEQ, D])
    xsm = din("xsm", [NS, D])
    cT = din("cT", [128, 8, 17])
    pvec = din("pvec", [128, PV_N])
    rowb_in = din("rowb", [128, RB_N])
    snw_in = din("snw", [128, DI])
    bg_in = din("bg", [128, 2 * D])
    w_ada = din("w_ada", [128, 8, 6 * D])
    w_in = din("w_in", [128, 8, IN_COLS])
    w_pool = din("w_pool", [128, 4, 2, 256])
    w_ssd = din("w_ssd", [128, 16, D])
    w_out = din("w_out", [128, 8, D])
    w_ffi = din("w_ffi", [128, 8, 2 * DFF])
    w_ffo = din("w_ffo", [128, 22, D])
    st_pool = din("st_pool", [128, 8, NS, 15])
    st_conv = din("st_conv", [128, 32, NS, 3])
    st_ssm = din("st_ssm", [NS, 128, DI])
    o_y = dout("o_y", [SEQ, D])
    o_ys = dout("o_ys", [NS, D])
    o_pp = dout("o_pp", [128, 8, 15])
    o_cp = dout("o_cp", [128, 32, 3])
    o_sp = dout("o_sp", [128, DI])
    o_ps = dout("o_ps", [128, 8, NS, 15])
    o_cs = dout("o_cs", [128, 32, NS, 3])
    o_ss = dout("o_ss", [NS, 128, DI])
    dbg_out = {}

    es = ExitStack()
    with es:
        def sb(name, shape, dt=F32):
            return es.enter_context(nc.sbuf_tensor("s_" + name, list(shape), dt))

        P = Prog(nc)
        dma_keys = set()

        def DMA(eng, out, in_, key, reads=(), writes=(), group=None):
            dma_keys.add(key)
            P.dma(eng, lambda e: e.dma_start(out=out, in_=in_), reads=reads, writes=writes, key=key, group=group)

        pb = [es.enter_context(nc.psum_tensor("pb%d" % i, [128, 512], F32)) for i in range(8)]

        def PBF(i):
            return pb[i][:].bitcast(BF16)

        def PR(i, lo=0, hi=512):
            return ("pb%d" % i, 0, 512)

        sA = sb("sA", [128, 16 + TB])
        sB = sb("sB", [128, 16 + TB])
        identf = sB[:, 0:128]
        triuf = sB[:, 128:256]
        nmf = sA[:, 0:512].rearrange("p (a b) -> p a b", a=4)
        identb = sb("identb", [128, 128], BF16)
        onesb = sb("onesb", [128, 128], BF16)
        triub = sb("triub", [128, 128], BF16)
        nmb = sb("nmb", [128, 4, 128], BF16)
        epsc = sb("epsc", [128, 1])
        invc = sb("invc", [128, 4, 16])
        pv = sb("pv", [128, PV_N])
        rowb = sb("rowb", [128, RB_N])
        snwb = sb("snwb", [128, DI], BF16)
        anegb = sb("anegb", [128, 32])
        cTs = sb("cTs", [128, 8, 17])
        silucT = sb("silucT", [128, 8, 17], BF16)
        modT = sb("modT", [128, 48, 17])
        a1T = sb("a1T", [128, 8, 17])
        a2T = sb("a2T", [128, 8, 17])
        g1B = sb("g1B", [128, D])
        g2B = sb("g2B", [128, D])
        wdt = sb("wdt", [128, 8, 32], BF16)
        diagD = sb("diagD", [128, 16, 128], BF16)
        NWB = 2
        wbuf = [sb("wbuf%d" % i, [128, 4096], BF16) for i in range(NWB)]

        P.op("pool", lambda e: e.memset(identf, 0.0), writes=[R("sB")])
        P.op("pool", lambda e: e.affine_select(out=identf, in_=identf, pattern=[[-1, 128]], compare_op=ALU.not_equal,
                                               fill=1.0, base=0, channel_multiplier=1), reads=[R("sB")], writes=[R("sB")])
        P.op("dve", lambda e: e.tensor_copy(out=identb[:], in_=identf), reads=[R("sB")], writes=[R("identb")])
        P.op("pool", lambda e: e.memset(onesb[:], 1.0), writes=[R("onesb")])
        P.op("pool", lambda e: e.memset(triuf, 1.0), writes=[R("sB")])
        P.op("pool", lambda e: e.affine_select(out=triuf, in_=triuf, pattern=[[1, 128]], compare_op=ALU.is_ge,
                                               fill=0.0, base=0, channel_multiplier=-1), reads=[R("sB")], writes=[R("sB")])
        P.op("dve", lambda e: e.tensor_copy(out=triub[:], in_=triuf), reads=[R("sB")], writes=[R("triub")])
        P.op("pool", lambda e: e.memset(nmf, 0.0), writes=[R("sA")])
        P.op("pool", lambda e: e.affine_select(out=nmf, in_=nmf, pattern=[[0, 4], [1, 128]], compare_op=ALU.is_ge,
                                               fill=-30000.0, base=0, channel_multiplier=-1), reads=[R("sA")], writes=[R("sA")])
        P.op("dve", lambda e: e.tensor_copy(out=nmb[:], in_=nmf), reads=[R("sA")], writes=[R("nmb")])
        P.op("pool", lambda e: e.memset(epsc[:], EPS), writes=[R("epsc")])
        for g in range(4):
            w = 2 ** (g + 1)
            P.op("pool", lambda e, g=g, w=w: e.memset(invc[:, g, :], 1.0 / w), writes=[R("invc")])
            for t in range(w - 1):
                P.op("pool", lambda e, g=g, t=t: e.memset(invc[:, g, t:t + 1], 1.0 / (t + 1)), writes=[R("invc")])

        DMA("sp", pv[:], pvec, "ld_pv", writes=[R("pv")])
        DMA("sp", rowb[:], rowb_in, "ld_rowb", writes=[R("rowb")])
        DMA("sp", cTs[:], cT, "ld_c", writes=[R("cTs")])
        DMA("sp", g1B[:], bg_in[:, 0:D], "ld_g1", writes=[R("g1B", 0, 2)])
        DMA("sp", g2B[:], bg_in[:, D:2 * D], "ld_g2", writes=[R("g2B", 0, 2)])
        DMA("pool", snwb[:], snw_in, "ld_snw", writes=[R("snwb")])
        DMA("pool", wdt[:], w_in[:, :, 7168:7200], "ld_wdt", writes=[R("wdt")])
        P.op("dve", lambda e: e.tensor_tensor(out=diagD[:], in0=identb[:].unsqueeze(1).to_broadcast([128, 16, 128]),
                                              in1=pv[:, PV_DF:PV_DF + 16].unsqueeze(2).to_broadcast([128, 16, 128]), op=ALU.mult),
             reads=[R("identb"), R("pv")], writes=[R("diagD")])

        P.op("act", lambda e: e.activation(out=anegb[:], in_=rowb[:, RB_ALOG:RB_ALOG + 32], func=AF.Exp), reads=[R("rowb")], writes=[R("anegb")])
        P.op("dve", lambda e: e.tensor_scalar(out=anegb[:], in0=anegb[:], scalar1=-1.0, scalar2=None, op0=ALU.mult), reads=[R("anegb")], writes=[R("anegb")])

        wstate = {"i": 0}

        NSCR = 42
        wsc = nc.dram_tensor("wsc", [NSCR, 128, 4096], BF16).ap()
        scr_idx = {}

        def load_w(src_ap, nk, ncol, tid=None):
            i = wstate["i"] % NWB
            wstate["i"] += 1
            wt = wbuf[i]
            view = wt[:, 0:nk * ncol].rearrange("p (k c) -> p k c", k=nk)
            if tid is None:
                DMA("pool", view, src_ap, "w%d" % i, writes=[R("wbuf%d" % i)])
            elif tid not in scr_idx:
                k = len(scr_idx)
                scr_idx[tid] = k
                DMA("pool", view, src_ap, "w%d" % i, writes=[R("wbuf%d" % i)])
                DMA("sp", wsc[k, :, 0:nk * ncol], wt[:, 0:nk * ncol], "ws%d" % i, reads=[R("wbuf%d" % i)], writes=[R("wsc", k, k + 1)])
            else:
                k = scr_idx[tid]
                DMA("sp", wt[:, 0:nk * ncol], wsc[k, :, 0:nk * ncol], "wh%d" % i, reads=[R("wsc", k, k + 1)], writes=[R("wbuf%d" % i)])
            return view, R("wbuf%d" % i)

        rot = {"i": 0}

        def next_bank(banks=(0, 1, 2, 3)):
            b = banks[rot["i"] % len(banks)]
            rot["i"] += 1
            return b

        P.op("act", lambda e: e.activation(out=silucT[:], in_=cTs[:], func=AF.Silu), reads=[R("cTs")], writes=[R("silucT")])
        for t in range(12):
            wt, wr = load_w(w_ada[:, :, t * 512:(t + 1) * 512], 8, 512)
            bk = 4 + (t % 2)
            for j in range(4):
                for kc in range(8):
                    P.op("pe", lambda e, bk=bk, j=j, kc=kc, wt=wt: e.matmul(pb[bk][:, j * 17:(j + 1) * 17], lhsT=wt[:, kc, j * 128:(j + 1) * 128],
                                                                             rhs=silucT[:, kc, :], start=(kc == 0), stop=(kc == 7)),
                         reads=[wr, R("silucT")], writes=[PR(bk, j * 17, j * 17 + 17)])
            P.op("dve", lambda e, bk=bk, t=t: e.tensor_tensor(out=modT[:, 4 * t:4 * t + 4, :], in0=pb[bk][:, 0:68].rearrange("p (a b) -> p a b", a=4),
                                                               in1=pv[:, PV_BADA + 4 * t:PV_BADA + 4 * t + 4].unsqueeze(2).to_broadcast([128, 4, 17]), op=ALU.add),
                 reads=[PR(bk, 0, 68), R("pv")], writes=[R("modT", 4 * t, 4 * t + 4)])
            if t in (4, 5, 10, 11):
                gB = g1B if t < 6 else g2B
                gname = "g1B" if t < 6 else "g2B"
                half = t % 2
                for kc in range(8):
                    P.op("pe", lambda e, kc=kc, wt=wt: e.matmul(pb[6][:, 0:512], lhsT=silucT[:, kc, 0:1].to_broadcast([128, 128]), rhs=wt[:, kc, :],
                                                                  start=(kc == 0), stop=(kc == 7)),
                         reads=[wr, R("silucT")], writes=[PR(6)])
                P.op("dve", lambda e, gB=gB, half=half: e.tensor_tensor(out=gB[:, half * 512:(half + 1) * 512], in0=pb[6][:, 0:512],
                                                                         in1=gB[:, half * 512:(half + 1) * 512], op=ALU.add),
                     reads=[PR(6), R(gname, half, half + 1)], writes=[R(gname, half, half + 1)])
        for (aT, an, sc0, nw0) in ((a1T, "a1T", 8, PV_N1W), (a2T, "a2T", 32, PV_N2W)):
            P.op("dve", lambda e, aT=aT, sc0=sc0: e.tensor_scalar(out=aT[:], in0=modT[:, sc0:sc0 + 8, :], scalar1=1.0, scalar2=None, op0=ALU.add),
                 reads=[R("modT", sc0, sc0 + 8)], writes=[R(an)])
            P.op("dve", lambda e, aT=aT, nw0=nw0: e.tensor_tensor(out=aT[:], in0=aT[:], in1=pv[:, nw0:nw0 + 8].unsqueeze(2).to_broadcast([128, 8, 17]), op=ALU.mult),
                 reads=[R(an), R("pv")], writes=[R(an)])

        xtok = sb("xtok", [128, 4, D])
        ssq = sb("ssq", [128, 8])
        rsq = sb("rsq", [128, 8])
        hT = sb("hT", [128, 8, TB], BF16)
        ubuf = sb("ubuf", [128, 8, 16 + TB])
        gT = sb("gT", [128, 8, TB], BF16)
        gaT = gT
        gbT = gT
        uhist = sb("uhist", [128, 8, 15])
        amT = sb("amT", [128, 8, TB], BF16)
        sz = sb("sz", [128, 4, DI], BF16)
        xpre = [sb("xpre%d" % i, [128, 3 + TB], BF16) for i in range(2)]
        dg = [sb("dg%d" % i, [128, 4, 128], BF16) for i in range(2)]
        chist = sb("chist", [128, 32, 3], BF16)
        cst = sb("cst", [128, 32, 3])
        xbc = sb("xbc", [128, 32, TB], BF16)
        dtx = sb("dtx", [128, 4, 32])
        dtt = sb("dtt", [128, 4, 32])
        dtu = sb("dtu", [128, 4, 32])
        dt_tok = sb("dt_tok", [128, 4, 32])
        dA_bf = sb("dA_bf", [128, 4, 32], BF16)
        negAcs = sb("negAcs", [128, 32])
        Eacs = sb("Eacs", [128, 32])
        dte = sb("dte", [128, 32])
        cdB = sb("cdB", [128, 32])
        xdtb = [sb("xdt%d" % i, [128, DI], BF16) for i in range(2)]
        x_tok = xdtb[1]
        xdt = xdtb[0]
        xdte = sb("xdte", [128, DI], BF16)
        B_tok = sb("B_tok", [128, 1024], BF16)
        CBs = sb("CBs", [128, 8, 128], BF16)
        dec = [sb("dec%d" % i, [128, 8, 128], BF16) for i in range(2)]
        scr = dec
        ybuf = sb("ybuf", [128, DI])
        ytmp = hT[:].rearrange("p a b -> p (a b)").bitcast(F32)
        xn = ybuf[:].bitcast(BF16).rearrange("p (a b) -> p a b", a=4)
        junk = xdte
        pooled = sz[:].rearrange("p a b -> p (a b)")[:, 0:8 * TB].rearrange("p (a b) -> p a b", a=8)
        hst = sb("hst", [128, DI])
        hstb = sb("hstb", [128, DI], BF16)
        yst = [sb("yst%d" % i, [128, D]) for i in range(1)]
        ynT = ubuf[:].rearrange("p a b -> p (a b)").bitcast(BF16)[:, 0:16 * TB].rearrange("p (a b) -> p a b", a=16)
        fT = xbc[:].rearrange("p a b -> p (a b)")[:, 0:22 * TB].rearrange("p (a b) -> p a b", a=22)

        P.op("pool", lambda e: e.memset(chist[:], 0.0), writes=[R("chist", 0, 32)])
        P.op("pool", lambda e: e.memset(hst[:], 0.0), writes=[R("hst", 0, 8)])
        P.op("pool", lambda e: e.memset(hstb[:], 0.0), writes=[R("hstb", 0, 8)])

        def rmsnorm_to_T(blk_name, aT, bcol0, ntok=128, ntile=4):
            for ti in range(ntile):
                P.op("act", lambda e, ti=ti: e.activation(out=junk[:, 0:D], in_=xtok[:, ti, :], func=AF.Square, accum_out=ssq[:, ti:ti + 1]),
                     reads=[R("xtok", ti, ti + 1)], writes=[R("xdte", 0, 8), R("ssq", ti, ti + 1)])
                P.op("act", lambda e, ti=ti: e.activation(out=rsq[:, 4 + ti:5 + ti], in_=ssq[:, ti:ti + 1], func=AF.Ln, scale=1.0 / D, bias=epsc[:, 0:1]),
                     reads=[R("ssq", ti, ti + 1), R("epsc")], writes=[R("rsq", 4 + ti, 5 + ti)])
                P.op("act", lambda e, ti=ti: e.activation(out=rsq[:, ti:ti + 1], in_=rsq[:, 4 + ti:5 + ti], func=AF.Exp, scale=-0.5),
                     reads=[R("rsq", 4 + ti, 5 + ti)], writes=[R("rsq", ti, ti + 1)])
                P.op("dve", lambda e, ti=ti: e.tensor_scalar(out=xn[:, ti, :], in0=xtok[:, ti, :], scalar1=rsq[:, ti:ti + 1], scalar2=None, op0=ALU.mult),
                     reads=[R("xtok", ti, ti + 1), R("rsq", ti, ti + 1)], writes=[R("ybuf", 2 * ti, 2 * ti + 2)])
            for kc in range(8):
                bk = kc // 2
                c0 = (kc % 2) * 512
                for ti in range(ntile):
                    P.op("pe", lambda e, bk=bk, c0=c0, ti=ti, kc=kc: e.transpose(out=PBF(bk)[:, c0 + ti * 128:c0 + (ti + 1) * 128],
                                                                                  in_=xn[:, ti, kc * 128:(kc + 1) * 128], identity=identb[:]),
                         reads=[R("ybuf", 2 * ti, 2 * ti + 2), R("identb")], writes=[PR(bk, c0 // 2 + ti * 64, c0 // 2 + (ti + 1) * 64)])
                P.op("dve", lambda e, bk=bk, c0=c0, kc=kc, aT=aT, bcol0=bcol0: e.tensor_scalar(
                    out=hT[:, kc, :], in0=PBF(bk)[:, c0:c0 + 512], scalar1=aT[:, kc, 0:1], scalar2=modT[:, bcol0 + kc, 0:1], op0=ALU.mult, op1=ALU.add),
                    reads=[PR(bk, c0 // 2, c0 // 2 + 256), R(blk_name), R("modT", bcol0 + kc, bcol0 + kc + 1)], writes=[R("hT", kc, kc + 1)])

        def proj_ws(wt, wr, j, nk, rhs_fn, rhs_regs, bank, ncols=TB, col0=0):
            for kc in range(nk):
                P.op("pe", lambda e, kc=kc: e.matmul(pb[bank][:, 0:ncols], lhsT=wt[:, kc, col0 + j * 128:col0 + (j + 1) * 128], rhs=rhs_fn(kc),
                                                      start=(kc == 0), stop=(kc == nk - 1)),
                     reads=[wr] + rhs_regs(kc), writes=[PR(bank, 0, ncols)])

        def proj_as(wt, wr, ti, nk, lhs_fn, lhs_regs, bank, kc0=0, first=True, last=True, nktot=None):
            for kc in range(nk):
                P.op("pe", lambda e, kc=kc: e.matmul(pb[bank][:, 0:512], lhsT=lhs_fn(kc0 + kc, ti), rhs=wt[:, kc, :],
                                                      start=(first and kc == 0), stop=(last and kc == nk - 1)),
                     reads=[wr] + lhs_regs(kc0 + kc), writes=[PR(bank)])

        hT_rhs = lambda kc: hT[:, kc, :]
        hT_regs = lambda kc: [R("hT", kc, kc + 1)]
        hT_lhs = lambda kc, ti: hT[:, kc, ti * 128:(ti + 1) * 128]

        for tb in range(nblk):
            DMA("sp", xtok[:], xp[tb * TB:(tb + 1) * TB, :].rearrange("(t p) d -> p t d", p=128), "ld_x", writes=[R("xtok", 0, 4)])
            rmsnorm_to_T("a1T", a1T, 0)

            for t in range(2):
                wt, wr = load_w(w_in[:, :, 7200 + t * 512:7200 + (t + 1) * 512], 8, 512, tid=('in', 7200 + t * 512))
                for j in range(4):
                    oc = 4 * t + j
                    bk = next_bank()
                    proj_ws(wt, wr, j, 8, hT_rhs, hT_regs, bk)
                    P.op("act", lambda e, bk=bk, oc=oc: e.activation(out=gaT[:, oc, :], in_=pb[bk][:, 0:TB], func=AF.Sigmoid),
                         reads=[PR(bk)], writes=[R("gT", oc, oc + 1)])
            if tb == 0:
                P.op("pool", lambda e: e.memset(ubuf[:, :, 0:16], 0.0), writes=[R("ubuf", 0, 8)])
            else:
                P.op("pool", lambda e: e.tensor_copy(out=ubuf[:, :, 1:16], in_=uhist[:]), reads=[R("uhist")], writes=[R("ubuf", 0, 8)])
            for t in range(2):
                wt, wr = load_w(w_in[:, :, t * 512:(t + 1) * 512], 8, 512, tid=('in', t * 512))
                for j in range(4):
                    oc = 4 * t + j
                    bk = next_bank()
                    proj_ws(wt, wr, j, 8, hT_rhs, hT_regs, bk)
                    P.op("act", lambda e, bk=bk, oc=oc: e.activation(out=ubuf[:, oc, 16:16 + TB], in_=pb[bk][:, 0:TB], func=AF.Copy),
                         reads=[PR(bk)], writes=[R("ubuf", oc, oc + 1)])
            for oc in range(8):
                g = oc // 2
                w = 2 ** (g + 1)
                U = ubuf[:, oc, :]
                ur = R("ubuf", oc, oc + 1)
                P.op("dve", lambda e, U=U: e.tensor_tensor(out=sA[:, 2:528], in0=U[:, 2:528], in1=U[:, 1:527], op=ALU.add), reads=[ur], writes=[R("sA")])
                cur, curname = sA, "sA"
                if g >= 1:
                    P.op("dve", lambda e: e.tensor_tensor(out=sB[:, 4:528], in0=sA[:, 4:528], in1=sA[:, 2:526], op=ALU.add), reads=[R("sA")], writes=[R("sB")])
                    cur, curname = sB, "sB"
                if g >= 2:
                    P.op("dve", lambda e: e.tensor_tensor(out=sA[:, 8:528], in0=sB[:, 8:528], in1=sB[:, 4:524], op=ALU.add), reads=[R("sB")], writes=[R("sA")])
                    cur, curname = sA, "sA"
                if g >= 3:
                    P.op("dve", lambda e: e.tensor_tensor(out=sB[:, 16:528], in0=sA[:, 16:528], in1=sA[:, 8:520], op=ALU.add), reads=[R("sA")], writes=[R("sB")])
                    cur, curname = sB, "sB"
                P.op("dve", lambda e, cur=cur, U=U, w=w, oc=oc: e.scalar_tensor_tensor(out=pooled[:, oc, :], in0=cur[:, 16:528], scalar=1.0 / w, in1=U[:, 16:528],
                                                                                        op0=ALU.mult, op1=ALU.subtract),
                     reads=[R(curname), ur], writes=[R("sz", 0, 2)])
                if tb == 0:
                    P.op("dve", lambda e, cur=cur, g=g: e.tensor_tensor(out=cur[:, 0:16], in0=cur[:, 16:32], in1=invc[:, g, :], op=ALU.mult),
                         reads=[R(curname), R("invc")], writes=[R(curname)])
                    P.op("dve", lambda e, cur=cur, U=U, oc=oc: e.tensor_tensor(out=pooled[:, oc, 0:16], in0=cur[:, 0:16], in1=U[:, 16:32], op=ALU.subtract),
                         reads=[R(curname), ur], writes=[R("sz", 0, 2)])
            wplt, wplr = load_w(w_pool.rearrange("p g k c -> p (g k) c"), 8, 256, tid=('pool',))
            for g in range(4):
                for j in range(2):
                    oc = 2 * g + j
                    bk = next_bank()
                    for k2 in range(2):
                        P.op("pe", lambda e, bk=bk, g=g, j=j, k2=k2: e.matmul(pb[bk][:, 0:TB], lhsT=wplt[:, 2 * g + k2, j * 128:(j + 1) * 128], rhs=pooled[:, 2 * g + k2, :],
                                                                                start=(k2 == 0), stop=(k2 == 1)),
                             reads=[wplr, R("sz", 0, 2)], writes=[PR(bk)])
                    P.op("dve", lambda e, bk=bk, oc=oc: e.scalar_tensor_tensor(out=amT[:, oc, :], in0=pb[bk][:, 0:TB], scalar=pv[:, PV_PSC + oc:PV_PSC + oc + 1],
                                                                                in1=gaT[:, oc, :], op0=ALU.mult, op1=ALU.mult),
                         reads=[PR(bk), R("pv"), R("gT", oc, oc + 1)], writes=[R("amT", oc, oc + 1)])
            if tb == nblk - 1:
                DMA("sp", o_pp, ubuf[:, :, 513:528], "st_pp", reads=[R("ubuf", 0, 8)], writes=[R("o_pp")])
            else:
                P.op("pool", lambda e: e.tensor_copy(out=uhist[:], in_=ubuf[:, :, 513:528]), reads=[R("ubuf", 0, 8)], writes=[R("uhist")])

            for t in range(4):
                wt, wr = load_w(w_in[:, :, 1024 + t * 512:1024 + (t + 1) * 512], 8, 512, tid=('in', 1024 + t * 512))
                for ti in range(4):
                    bk = next_bank()
                    proj_as(wt, wr, ti, 8, hT_lhs, hT_regs, bk)
                    P.op("act", lambda e, bk=bk, ti=ti, t=t: e.activation(out=sz[:, ti, t * 512:(t + 1) * 512], in_=pb[bk][:, 0:512], func=AF.Silu),
                         reads=[PR(bk)], writes=[R("sz", ti, ti + 1)])
            for t in range(8):
                wt, wr = load_w(w_in[:, :, 3072 + t * 512:3072 + (t + 1) * 512], 8, 512, tid=('in', 3072 + t * 512))
                for j in range(4):
                    oc = 4 * t + j
                    sl = oc % 2
                    bk = next_bank()
                    proj_ws(wt, wr, j, 8, hT_rhs, hT_regs, bk)
                    xpr = R("xpre%d" % sl)
                    P.op("dve", lambda e, sl=sl, oc=oc: e.tensor_copy(out=xpre[sl][:, 0:3], in_=chist[:, oc, :]), reads=[R("chist", oc, oc + 1)], writes=[xpr])
                    P.op("act", lambda e, sl=sl, bk=bk: e.activation(out=xpre[sl][:, 3:3 + TB], in_=pb[bk][:, 0:TB], func=AF.Copy), reads=[PR(bk)], writes=[xpr])
                    if tb == nblk - 1:
                        P.op("dve", lambda e, bk=bk, oc=oc: e.tensor_copy(out=cst[:, oc, :], in_=pb[bk][:, TB - 3:TB]), reads=[PR(bk)], writes=[R("cst", oc, oc + 1)])
                    else:
                        P.op("dve", lambda e, sl=sl, oc=oc: e.tensor_copy(out=chist[:, oc, :], in_=xpre[sl][:, TB:TB + 3]), reads=[xpr], writes=[R("chist", oc, oc + 1)])
                    P.op("pool", lambda e, sl=sl, oc=oc: e.tensor_tensor(out=dg[sl][:], in0=identb[:].unsqueeze(1).to_broadcast([128, 4, 128]),
                                                                          in1=pv[:, PV_CW + oc:PV_CW + oc + 97:32].unsqueeze(2).to_broadcast([128, 4, 128]), op=ALU.mult),
                         reads=[R("identb"), R("pv")], writes=[R("dg%d" % sl, 0, 4)])
                    cbk = 4 + sl
                    for k in range(4):
                        P.op("pe", lambda e, sl=sl, k=k, cbk=cbk: e.matmul(pb[cbk][:, 0:TB], lhsT=dg[sl][:, k, :], rhs=xpre[sl][:, k:k + TB], start=(k == 0), stop=(k == 3)),
                             reads=[R("dg%d" % sl, k, k + 1), xpr], writes=[PR(cbk)])
                    P.op("act", lambda e, cbk=cbk, oc=oc: e.activation(out=xbc[:, oc, :], in_=pb[cbk][:, 0:TB], func=AF.Silu, bias=pv[:, PV_CB + oc:PV_CB + oc + 1], scale=1.0),
                         reads=[PR(cbk), R("pv")], writes=[R("xbc", oc, oc + 1)])
            if tb == nblk - 1:
                DMA("sp", o_cp, cst[:], "st_cp", reads=[R("cst", 0, 32)], writes=[R("o_cp")])
            for ti in range(4):
                for kc in range(8):
                    P.op("pe", lambda e, ti=ti, kc=kc: e.matmul(pb[6][:, ti * 32:(ti + 1) * 32], lhsT=hT[:, kc, ti * 128:(ti + 1) * 128], rhs=wdt[:, kc, :],
                                                                  start=(kc == 0), stop=(kc == 7)),
                         reads=[R("hT", kc, kc + 1), R("wdt")], writes=[PR(6, ti * 32, ti * 32 + 32)])
            P.op("dve", lambda e: e.tensor_tensor(out=dtx[:], in0=pb[6][:, 0:128].rearrange("p (a b) -> p a b", a=4),
                                                  in1=rowb[:, RB_DTB:RB_DTB + 32].unsqueeze(1).to_broadcast([128, 4, 32]), op=ALU.add),
                 reads=[PR(6, 0, 128), R("rowb")], writes=[R("dtx")])
            P.op("act", lambda e: e.activation(out=dtt[:], in_=dtx[:], func=AF.Abs), reads=[R("dtx")], writes=[R("dtt")])
            P.op("act", lambda e: e.activation(out=dtu[:], in_=dtt[:], func=AF.Exp, scale=-1.0), reads=[R("dtt")], writes=[R("dtu")])
            P.op("act", lambda e: e.activation(out=dtt[:], in_=dtu[:], func=AF.Ln, bias=1.0, scale=1.0), reads=[R("dtu")], writes=[R("dtt")])
            P.op("dve", lambda e: e.scalar_tensor_tensor(out=dt_tok[:], in0=dtx[:], scalar=0.0, in1=dtt[:], op0=ALU.max, op1=ALU.add),
                 reads=[R("dtx"), R("dtt")], writes=[R("dt_tok")])
            P.op("dve", lambda e: e.tensor_tensor(out=dA_bf[:], in0=dt_tok[:], in1=anegb[:].unsqueeze(1).to_broadcast([128, 4, 32]), op=ALU.mult),
                 reads=[R("dt_tok"), R("anegb")], writes=[R("dA_bf")])
            for t in range(2):
                wt, wr = load_w(w_in[:, :, 8224 + t * 512:8224 + (t + 1) * 512], 8, 512, tid=('in', 8224 + t * 512))
                for j in range(4):
                    oc = 4 * t + j
                    bk = next_bank()
                    proj_ws(wt, wr, j, 8, hT_rhs, hT_regs, bk)
                    P.op("act", lambda e, bk=bk, oc=oc: e.activation(out=gbT[:, oc, :], in_=pb[bk][:, 0:TB], func=AF.Sigmoid),
                         reads=[PR(bk)], writes=[R("gT", oc, oc + 1)])


            def ssd_acd(ci):
                tsl = slice(ci * 128, (ci + 1) * 128)
                dAc = dA_bf[:, ci, :]
                P.op("pe", lambda e: e.matmul(pb[7][:, 0:32], lhsT=triub[:], rhs=dAc, start=True, stop=True), reads=[R("triub"), R("dA_bf")], writes=[PR(7)])
                P.op("pe", lambda e: e.matmul(pb[7][:, 32:64], lhsT=onesb[:], rhs=dAc, start=True, stop=True), reads=[R("onesb"), R("dA_bf")], writes=[PR(7)])
                P.op("dve", lambda e: e.tensor_scalar(out=negAcs[:], in0=pb[7][:, 0:32], scalar1=-1.0, scalar2=None, op0=ALU.mult), reads=[PR(7)], writes=[R("negAcs")])
                P.op("act", lambda e: e.activation(out=Eacs[:], in_=pb[7][:, 0:32], func=AF.Exp), reads=[PR(7)], writes=[R("Eacs")])
                P.op("dve", lambda e: e.tensor_tensor(out=dte[:], in0=pb[7][:, 32:64], in1=negAcs[:], op=ALU.add), reads=[PR(7), R("negAcs")], writes=[R("dte")])
                P.op("act", lambda e: e.activation(out=dte[:], in_=dte[:], func=AF.Exp), reads=[R("dte")], writes=[R("dte")])
                P.op("act", lambda e: e.activation(out=cdB[:], in_=pb[7][:, 32:64], func=AF.Exp), reads=[PR(7)], writes=[R("cdB")])
                for g in range(NG):
                    P.op("pe", lambda e, g=g: e.transpose(out=PBF(2)[:, g * 128:(g + 1) * 128], in_=xbc[:, 16 + g, tsl], identity=identb[:]),
                         reads=[R("xbc", 16 + g, 17 + g), R("identb")], writes=[PR(2)])
                P.op("act", lambda e: e.activation(out=B_tok[:], in_=PBF(2)[:, 0:1024], func=AF.Copy), reads=[PR(2)], writes=[R("B_tok")])
                for g in range(NG):
                    bk = 3 + g // 4
                    c0 = (g % 4) * 128
                    P.op("pe", lambda e, g=g, bk=bk, c0=c0: e.matmul(pb[bk][:, c0:c0 + 128], lhsT=xbc[:, 16 + g, tsl], rhs=xbc[:, 24 + g, tsl], start=True, stop=True),
                         reads=[R("xbc", 16 + g, 17 + g), R("xbc", 24 + g, 25 + g)], writes=[PR(bk)])
                for q in range(2):
                    P.op("act", lambda e, q=q: e.activation(out=CBs[:, q * 4:(q + 1) * 4, :], in_=pb[3 + q][:, 0:512].rearrange("p (a b) -> p a b", a=4), func=AF.Copy),
                         reads=[PR(3 + q)], writes=[R("CBs", q * 4, q * 4 + 4)])

            def ssd_b(ci):
                tsl = slice(ci * 128, (ci + 1) * 128)
                xd = xdtb[ci % 2]
                xr = R("xdt%d" % (ci % 2))
                for fc in range(16):
                    bk = fc // 8
                    c0 = (fc % 8) * 128
                    P.op("pe", lambda e, bk=bk, c0=c0, fc=fc: e.transpose(out=PBF(bk)[:, c0:c0 + 128], in_=xbc[:, fc, tsl], identity=identb[:]),
                         reads=[R("xbc", fc, fc + 1), R("identb")], writes=[PR(bk)])
                for bk in range(2):
                    P.op("dve", lambda e, bk=bk: e.tensor_tensor(out=xd[:, bk * 1024:(bk + 1) * 1024].rearrange("p (h q) -> p h q", h=16),
                                                                 in0=PBF(bk)[:, 0:1024].rearrange("p (h q) -> p h q", h=16),
                                                                 in1=dt_tok[:, ci, bk * 16:(bk + 1) * 16].unsqueeze(2).to_broadcast([128, 16, HP]), op=ALU.mult),
                         reads=[PR(bk), R("dt_tok")], writes=[xr])
                P.op("dve", lambda e: e.tensor_tensor(out=xdte[:].rearrange("p (h q) -> p h q", h=NH), in0=xd[:].rearrange("p (h q) -> p h q", h=NH),
                                                      in1=dte[:].unsqueeze(2).to_broadcast([128, NH, HP]), op=ALU.mult),
                     reads=[xr, R("dte")], writes=[R("xdte", 0, 8)])

            def ssd_decay(ci, hq):
                sl = hq % 2
                dbanks = (5, 6) if hq % 2 == 0 else (3, 4)
                for half in range(2):
                    bk = dbanks[half]
                    for j in range(4):
                        h = hq * 8 + half * 4 + j
                        P.op("pe", lambda e, bk=bk, j=j: e.matmul(pb[bk][:, j * 128:(j + 1) * 128], lhsT=identb[:], rhs=nmb[:, 0, :], start=True, stop=False),
                             reads=[R("identb"), R("nmb")], writes=[PR(bk)])
                        P.op("pe", lambda e, bk=bk, j=j, h=h: e.matmul(pb[bk][:, j * 128:(j + 1) * 128], lhsT=dA_bf[:, ci, h:h + 1].to_broadcast([128, 128]),
                                                                        rhs=triub[:], start=False, stop=True),
                             reads=[R("dA_bf"), R("triub")], writes=[PR(bk)])
                    for j in range(4):
                        h = hq * 8 + half * 4 + j
                        P.op("act", lambda e, bk=bk, j=j, h=h, half=half: e.activation(out=dec[sl][:, half * 4 + j, :], in_=pb[bk][:, j * 128:(j + 1) * 128],
                                                                                       func=AF.Exp, bias=negAcs[:, h:h + 1], scale=1.0),
                             reads=[PR(bk), R("negAcs")], writes=[R("dec%d" % sl, half * 4 + j, half * 4 + j + 1)])

            def ssd_y(ci, hq):
                tsl = slice(ci * 128, (ci + 1) * 128)
                sl = hq % 2
                P.op("dve", lambda e: e.tensor_tensor(out=scr[sl][:].rearrange("p (g j) l -> p g j l", g=2), in0=dec[sl][:].rearrange("p (g j) l -> p g j l", g=2),
                                                      in1=CBs[:, 2 * hq:2 * hq + 2, :].unsqueeze(2).to_broadcast([128, 2, 4, 128]), op=ALU.mult),
                     reads=[R("dec%d" % sl, 0, 8), R("CBs", 2 * hq, 2 * hq + 2)], writes=[R("dec%d" % sl, 0, 8)])
                bA = (0, 2)[hq % 2]
                bB = (1, 7)[hq % 2]
                for jj in range(8):
                    h = 8 * hq + jj
                    P.op("pe", lambda e, jj=jj, h=h: e.matmul(pb[bA][:, jj * 64:(jj + 1) * 64], lhsT=xbc[:, h // 2, tsl], rhs=diagD[:, h // 2, (h % 2) * 64:(h % 2 + 1) * 64], start=True, stop=False),
                         reads=[R("xbc", h // 2, h // 2 + 1), R("diagD")], writes=[PR(bA)])
                    P.op("pe", lambda e, jj=jj, h=h: e.matmul(pb[bA][:, jj * 64:(jj + 1) * 64], lhsT=scr[sl][:, jj, :], rhs=xdtb[ci % 2][:, h * 64:(h + 1) * 64], start=False, stop=True),
                         reads=[R("dec%d" % sl, jj, jj + 1), R("xdt%d" % (ci % 2))], writes=[PR(bA)])
                for gg in range(2):
                    g = 2 * hq + gg
                    P.op("pe", lambda e, g=g, gg=gg: e.matmul(pb[bB][:, gg * 256:(gg + 1) * 256], lhsT=xbc[:, 24 + g, tsl], rhs=hstb[:, g * 256:(g + 1) * 256], start=True, stop=True),
                         reads=[R("xbc", 24 + g, 25 + g), R("hstb", g, g + 1)], writes=[PR(bB)])
                ysl = slice(hq * 512, (hq + 1) * 512)
                yr = R("ybuf", 2 * hq, 2 * hq + 2)
                P.op("dve", lambda e: e.tensor_tensor(out=ybuf[:, ysl].rearrange("p (h q) -> p h q", h=8), in0=pb[bB][:, 0:512].rearrange("p (h q) -> p h q", h=8),
                                                      in1=Eacs[:, 8 * hq:8 * hq + 8].unsqueeze(2).to_broadcast([128, 8, HP]), op=ALU.mult),
                     reads=[PR(bB), R("Eacs")], writes=[yr])
                P.op("dve", lambda e: e.tensor_tensor(out=ybuf[:, ysl], in0=pb[bA][:, 0:512], in1=ybuf[:, ysl], op=ALU.add), reads=[PR(bA), yr], writes=[yr])
                P.op("dve", lambda e: e.tensor_tensor(out=ybuf[:, ysl], in0=ybuf[:, ysl], in1=sz[:, ci, ysl], op=ALU.mult), reads=[yr, R("sz", ci, ci + 1)], writes=[yr])

            def ssd_g(ci):
                for q in range(4):
                    sbk = 3 + q
                    hsl = slice(q * 512, (q + 1) * 512)
                    hr = R("hst", 2 * q, 2 * q + 2)
                    for gg in range(2):
                        g = 2 * q + gg
                        P.op("pe", lambda e, g=g, gg=gg, sbk=sbk: e.matmul(pb[sbk][:, gg * 256:(gg + 1) * 256], lhsT=B_tok[:, g * 128:(g + 1) * 128], rhs=xdte[:, g * 256:(g + 1) * 256], start=True, stop=True),
                             reads=[R("B_tok"), R("xdte", g, g + 1)], writes=[PR(sbk)])
                    P.op("pool", lambda e, q=q, hsl=hsl: e.tensor_tensor(out=hst[:, hsl].rearrange("p (h q) -> p h q", h=8), in0=hst[:, hsl].rearrange("p (h q) -> p h q", h=8),
                                                                         in1=cdB[:, 8 * q:8 * q + 8].unsqueeze(2).to_broadcast([128, 8, HP]), op=ALU.mult),
                         reads=[hr, R("cdB")], writes=[hr])
                    P.op("dve", lambda e, sbk=sbk, hsl=hsl: e.tensor_tensor(out=hst[:, hsl], in0=pb[sbk][:, 0:512], in1=hst[:, hsl], op=ALU.add), reads=[PR(sbk), hr], writes=[hr])
                    P.op("act", lambda e, hsl=hsl: e.activation(out=hstb[:, hsl], in_=hst[:, hsl], func=AF.Copy), reads=[hr], writes=[R("hstb", 2 * q, 2 * q + 2)])

            def ssd_h(ci):
                tsl = slice(ci * 128, (ci + 1) * 128)
                ynb2 = xdtb[ci % 2]
                xr = R("xdt%d" % (ci % 2))
                P.op("act", lambda e: e.activation(out=ynb2[:], in_=ybuf[:], func=AF.Square, accum_out=ssq[:, 4:5]), reads=[R("ybuf", 0, 8)], writes=[xr, R("ssq", 4, 5)])
                P.op("act", lambda e: e.activation(out=ssq[:, 5:6], in_=ssq[:, 4:5], func=AF.Ln, scale=1.0 / DI, bias=epsc[:, 0:1]), reads=[R("ssq", 4, 5), R("epsc")], writes=[R("ssq", 5, 6)])
                P.op("act", lambda e: e.activation(out=ssq[:, 6:7], in_=ssq[:, 5:6], func=AF.Exp, scale=-0.5), reads=[R("ssq", 5, 6)], writes=[R("ssq", 6, 7)])
                P.op("dve", lambda e: e.scalar_tensor_tensor(out=ynb2[:], in0=ybuf[:], scalar=ssq[:, 6:7], in1=snwb[:], op0=ALU.mult, op1=ALU.mult),
                     reads=[R("ybuf", 0, 8), R("ssq", 6, 7), R("snwb")], writes=[xr])
                for fc in range(16):
                    bk = fc // 8
                    c0 = (fc % 8) * 128
                    P.op("pe", lambda e, bk=bk, c0=c0, fc=fc: e.transpose(out=PBF(bk)[:, c0:c0 + 128], in_=ynb2[:, fc * 128:(fc + 1) * 128], identity=identb[:]),
                         reads=[xr, R("identb")], writes=[PR(bk)])
                for q in range(2):
                    P.op("act", lambda e, q=q: e.activation(out=ynT[:, q * 8:(q + 1) * 8, tsl], in_=PBF(q)[:, 0:1024].rearrange("p (a b) -> p a b", a=8), func=AF.Copy),
                         reads=[PR(q)], writes=[R("ubuf", 0, 8)])

            ssd_acd(0)
            ssd_b(0)
            for ci in range(4):
                if ci == 0:
                    ssd_decay(ci, 0)
                for hq in range(4):
                    if hq + 1 < 4:
                        ssd_decay(ci, hq + 1)
                    ssd_y(ci, hq)
                ssd_g(ci)
                if ci + 1 < 4:
                    ssd_acd(ci + 1)
                    ssd_b(ci + 1)
                    ssd_decay(ci + 1, 0)
                ssd_h(ci)
            if tb == nblk - 1:
                DMA("sp", o_sp, hst[:], "st_sp", reads=[R("hst", 0, 8)], writes=[R("o_sp")])

            for t in range(4):
                wt, wr = load_w(w_ssd[:, :, t * 256:(t + 1) * 256], 16, 256, tid=('ssd', t))
                for j in range(2):
                    oc = 2 * t + j
                    bk = next_bank()
                    proj_ws(wt, wr, j, 16, lambda kc: ynT[:, kc, :], lambda kc: [R("ubuf", 0, 8)], bk)
                    P.op("dve", lambda e, bk=bk, oc=oc: e.tensor_tensor(out=sA[:, 0:TB], in0=pb[bk][:, 0:TB], in1=gbT[:, oc, :], op=ALU.mult),
                         reads=[PR(bk), R("gT", oc, oc + 1)], writes=[R("sA")])
                    P.op("dve", lambda e, oc=oc: e.tensor_tensor(out=hT[:, oc, :], in0=sA[:, 0:TB], in1=amT[:, oc, :], op=ALU.add),
                         reads=[R("sA"), R("amT", oc, oc + 1)], writes=[R("hT", oc, oc + 1)])
            for t in range(2):
                wt, wr = load_w(w_out[:, :, t * 512:(t + 1) * 512], 8, 512, tid=('out', t))
                for ti in range(4):
                    bk = next_bank()
                    proj_as(wt, wr, ti, 8, hT_lhs, hT_regs, bk)
                    P.op("dve", lambda e, bk=bk, t=t: e.tensor_tensor(out=sB[:, 0:512], in0=pb[bk][:, 0:512], in1=g1B[:, t * 512:(t + 1) * 512], op=ALU.mult),
                         reads=[PR(bk), R("g1B", t, t + 1)], writes=[R("sB")])
                    P.op("dve", lambda e, ti=ti, t=t: e.tensor_tensor(out=xtok[:, ti, t * 512:(t + 1) * 512], in0=xtok[:, ti, t * 512:(t + 1) * 512], in1=sB[:, 0:512], op=ALU.add),
                         reads=[R("sB"), R("xtok", ti, ti + 1)], writes=[R("xtok", ti, ti + 1)])
            rmsnorm_to_T("a2T", a2T, 24)
            for t in range(11):
                wt, wr = load_w(w_ffi[:, :, t * 512:(t + 1) * 512], 8, 512, tid=('ffi', t))
                for j in range(2):
                    fc = 2 * t + j
                    bg = next_bank()
                    proj_ws(wt, wr, j, 8, hT_rhs, hT_regs, bg)
                    bu = next_bank()
                    proj_ws(wt, wr, j, 8, hT_rhs, hT_regs, bu, col0=256)
                    P.op("act", lambda e, bg=bg: e.activation(out=sA[:, 0:TB], in_=pb[bg][:, 0:TB], func=AF.Silu), reads=[PR(bg)], writes=[R("sA")])
                    P.op("dve", lambda e, bu=bu, fc=fc: e.tensor_tensor(out=fT[:, fc, :], in0=pb[bu][:, 0:TB], in1=sA[:, 0:TB], op=ALU.mult),
                         reads=[PR(bu), R("sA")], writes=[R("xbc", fc, fc + 1)])
            fT_lhs = lambda kc, ti: fT[:, kc, ti * 128:(ti + 1) * 128]
            fT_regs = lambda kc: [R("xbc", kc, kc + 1)]
            for half in range(2):
                for kg in range(3):
                    nk = 8 if kg < 2 else 6
                    wt, wr = load_w(w_ffo[:, kg * 8:kg * 8 + nk, half * 512:(half + 1) * 512], nk, 512, tid=('ffo', kg, half))
                    for ti in range(4):
                        proj_as(wt, wr, ti, nk, fT_lhs, fT_regs, 4 + ti, kc0=kg * 8, first=(kg == 0), last=(kg == 2))
                for ti in range(4):
                    P.op("dve", lambda e, ti=ti, half=half: e.tensor_tensor(out=sB[:, 0:512], in0=pb[4 + ti][:, 0:512], in1=g2B[:, half * 512:(half + 1) * 512], op=ALU.mult),
                         reads=[PR(4 + ti), R("g2B", half, half + 1)], writes=[R("sB")])
                    P.op("dve", lambda e, ti=ti, half=half: e.tensor_tensor(out=xtok[:, ti, half * 512:(half + 1) * 512], in0=xtok[:, ti, half * 512:(half + 1) * 512], in1=sB[:, 0:512], op=ALU.add),
                         reads=[R("sB"), R("xtok", ti, ti + 1)], writes=[R("xtok", ti, ti + 1)])
            for ti in range(4):
                ys = 0
                P.op("act", lambda e, ti=ti: e.activation(out=junk[:, 0:D], in_=xtok[:, ti, :], func=AF.Square, accum_out=ssq[:, ti:ti + 1]),
                     reads=[R("xtok", ti, ti + 1)], writes=[R("xdte", 0, 8), R("ssq", ti, ti + 1)])
                P.op("act", lambda e, ti=ti: e.activation(out=rsq[:, 4 + ti:5 + ti], in_=ssq[:, ti:ti + 1], func=AF.Ln, scale=1.0 / D, bias=epsc[:, 0:1]),
                     reads=[R("ssq", ti, ti + 1), R("epsc")], writes=[R("rsq", 4 + ti, 5 + ti)])
                P.op("act", lambda e, ti=ti: e.activation(out=rsq[:, ti:ti + 1], in_=rsq[:, 4 + ti:5 + ti], func=AF.Exp, scale=-0.5), reads=[R("rsq", 4 + ti, 5 + ti)], writes=[R("rsq", ti, ti + 1)])
                P.op("dve", lambda e, ti=ti, ys=ys: e.scalar_tensor_tensor(out=yst[ys][:], in0=xtok[:, ti, :], scalar=rsq[:, ti:ti + 1], in1=rowb[:, RB_FNW:RB_FNW + D], op0=ALU.mult, op1=ALU.mult),
                     reads=[R("xtok", ti, ti + 1), R("rsq", ti, ti + 1), R("rowb")], writes=[R("yst%d" % ys)])
                r0 = tb * TB + ti * 128
                DMA("sp", o_y[r0:r0 + 128, :], yst[ys][:], "st_y%d" % ys, reads=[R("yst%d" % ys)], writes=[R("o_y", tb * 4 + ti, tb * 4 + ti + 1)])


        if do_sample:
            Ssl = slice(1, 17)
            xbcF = xbc[:].rearrange("p a b -> p (a b)")
            stbuf = [xtok[:, 0:2, :].rearrange("p a b -> p (a b)"), xtok[:, 2:4, :].rearrange("p a b -> p (a b)")]
            streg = [R("xtok", 0, 2), R("xtok", 2, 4)]
            ubF = ubuf[:].rearrange("p a b -> p (a b)")
            stp = ubF[:, 0:1920].rearrange("p (c s r) -> p c s r", c=8, s=NS)
            newst = ubF[:, 1920:3840].rearrange("p (c s r) -> p c s r", c=8, s=NS)
            stc = ybuf[:, 0:1536].rearrange("p (c s r) -> p c s r", c=32, s=NS)
            newcst = hst[:, 0:1536].rearrange("p (c s r) -> p c s r", c=32, s=NS)
            gTf = gT[:].rearrange("p a b -> p (a b)").bitcast(F32)
            projS = gTf[:, 0:56 * NS].rearrange("p (c s) -> p c s", c=56)
            amF = amT[:].rearrange("p a b -> p (a b)").bitcast(F32)
            acc1 = amF[:, 0:512].rearrange("p (c s) -> p c s", c=32)
            acc2 = amF[:, 512:1024].rearrange("p (c s) -> p c s", c=32)
            amS = amF[:, 1024:1152].rearrange("p (c s) -> p c s", c=8)
            gaS = amF[:, 1152:1280].rearrange("p (c s) -> p c s", c=8)
            gbS = amF[:, 1280:1408].rearrange("p (c s) -> p c s", c=8)
            ptmp = amF[:, 1408:1536].rearrange("p (c s) -> p c s", c=8)
            sgS = amF[:, 1536:1568]
            xbcS = B_tok[:, 0:512].rearrange("p (c s) -> p c s", c=32)
            CBf = CBs[:].rearrange("p a b -> p (a b)")
            hTs = CBf[:, 0:128].rearrange("p (c s) -> p c s", c=8)
            mixTs = CBf[:, 128:256].rearrange("p (c s) -> p c s", c=8)
            pooledS = CBf[:, 256:384].rearrange("p (c s) -> p c s", c=8)
            ynTs = CBf[:, 384:640].rearrange("p (c s) -> p c s", c=16)
            fTs = CBf[:, 640:992].rearrange("p (c s) -> p c s", c=22)
            x_tokS = x_tok[0:NS, :]
            xdt_tokS = xdt[0:NS, :]
            szS = xdte[0:NS, :]
            xs_tok = yst[0][0:NS, :]
            szf = sz[:].rearrange("p a b -> p (a b)").bitcast(F32)
            ysS = szf[0:NS, 0:2048]
            gs1 = szf[0:NS, 2048:3072]
            gs2 = szf[0:NS, 3072:4096]
            xnS = dec[0][:].rearrange("p a b -> p (a b)")[0:NS, :]
            hTF = hT[:].rearrange("p a b -> p (a b)")
            ynS = hTF[0:NS, 0:2048]
            junkS = hTF[0:NS, 2048:4096]
            tmpDx = hTF[0:NS, :].bitcast(F32)
            decBs = sA[:, 0:512].rearrange("p (s h) -> p s h", s=NS)
            mask16 = sB[:, 0:256].rearrange("p (a b) -> p a b", a=NS)
            identfS = sB[:, 256:384]
            CmaskS = xbcF[:, 0:2048].rearrange("p (g s m) -> p g s m", g=NG, s=NS)
            ssS, rsS = ssq[0:NS, :], rsq[0:NS, :]
            RCB = R("CBs", 0, 8)
            RAM = R("amT", 0, 8)

            DMA("sp", xs_tok, xsm, "ld_xs", writes=[R("yst0")])
            DMA("sp", stp, st_pool, "ld_stp", writes=[R("ubuf", 0, 8)])
            DMA("sp", stc, st_conv, "ld_stc", writes=[R("ybuf", 0, 8)])
            P.op("pool", lambda e: e.memset(sB[:, 0:384], 0.0), writes=[R("sB")])
            P.op("pool", lambda e: e.affine_select(out=mask16, in_=mask16, pattern=[[1, NS], [-1, NS]], compare_op=ALU.not_equal, fill=1.0, base=0, channel_multiplier=0),
                 reads=[R("sB")], writes=[R("sB")])
            P.op("pool", lambda e: e.affine_select(out=identfS, in_=identfS, pattern=[[-1, 128]], compare_op=ALU.not_equal, fill=1.0, base=0, channel_multiplier=1),
                 reads=[R("sB")], writes=[R("sB")])
            for (gsv, c0, b0) in ((gs1, 16, 0), (gs2, 40, 2)):
                for c in range(8):
                    bk = b0 + c // 4
                    P.op("pe", lambda e, bk=bk, c=c, c0=c0: e.matmul(pb[bk][0:NS, (c % 4) * 128:(c % 4 + 1) * 128], lhsT=modT[:, c0 + c, Ssl], rhs=identfS, start=True, stop=True),
                         reads=[R("modT", c0 + c, c0 + c + 1), R("sB")], writes=[PR(bk)])
                for q in range(2):
                    P.op("act", lambda e, gsv=gsv, q=q, b0=b0: e.activation(out=gsv[:, q * 512:(q + 1) * 512], in_=pb[b0 + q][0:NS, 0:512], func=AF.Copy),
                         reads=[PR(b0 + q)], writes=[R("sz", 0, 4)])

            def s_norm_T(aT, bcol0):
                P.op("act", lambda e: e.activation(out=junkS[:, 0:D], in_=xs_tok, func=AF.Square, accum_out=ssS[:, 0:1]), reads=[R("yst0")], writes=[R("hT", 0, 8), R("ssq", 0, 8)])
                P.op("act", lambda e: e.activation(out=rsS[:, 4:5], in_=ssS[:, 0:1], func=AF.Ln, scale=1.0 / D, bias=epsc[0:NS, 0:1]), reads=[R("ssq", 0, 8), R("epsc")], writes=[R("rsq", 0, 8)])
                P.op("act", lambda e: e.activation(out=rsS[:, 0:1], in_=rsS[:, 4:5], func=AF.Exp, scale=-0.5), reads=[R("rsq", 0, 8)], writes=[R("rsq", 0, 8)])
                P.op("dve", lambda e: e.tensor_scalar(out=xnS, in0=xs_tok, scalar1=rsS[:, 0:1], scalar2=None, op0=ALU.mult), reads=[R("yst0"), R("rsq", 0, 8)], writes=[R("dec0", 0, 8)])
                for kc in range(8):
                    P.op("pe", lambda e, kc=kc: e.transpose(out=PBF(0)[:, kc * NS:(kc + 1) * NS], in_=xnS[:, kc * 128:(kc + 1) * 128], identity=identb[0:NS, 0:NS]),
                         reads=[R("dec0", 0, 8), R("identb")], writes=[PR(0)])
                P.op("dve", lambda e, aT=aT: e.tensor_tensor(out=ptmp, in0=PBF(0)[:, 0:128].rearrange("p (c s) -> p c s", c=8), in1=aT[:, :, Ssl], op=ALU.mult),
                     reads=[PR(0), R("a1T"), R("a2T")], writes=[RAM])
                P.op("dve", lambda e, bcol0=bcol0: e.tensor_tensor(out=hTs, in0=ptmp, in1=modT[:, bcol0:bcol0 + 8, Ssl], op=ALU.add),
                     reads=[RAM, R("modT", bcol0, bcol0 + 8)], writes=[RCB])

            s_norm_T(a1T, 0)
            ws_tiles = [(0, 0), (512, 4)] + [(3072 + 512 * t, 8 + 4 * t) for t in range(8)] + [(7200 + 512 * t, 40 + 4 * t) for t in range(4)]
            for (col0, cb0) in ws_tiles:
                wt, wr = load_w(w_in[:, :, col0:col0 + 512], 8, 512, tid=('in', col0))
                for j in range(4):
                    for kc in range(8):
                        P.op("pe", lambda e, j=j, kc=kc, wt=wt: e.matmul(pb[1][:, j * NS:(j + 1) * NS], lhsT=wt[:, kc, j * 128:(j + 1) * 128], rhs=hTs[:, kc, :], start=(kc == 0), stop=(kc == 7)),
                             reads=[wr, RCB], writes=[PR(1)])
                P.op("act", lambda e, cb0=cb0: e.activation(out=projS[:, cb0:cb0 + 4, :], in_=pb[1][:, 0:4 * NS].rearrange("p (c s) -> p c s", c=4), func=AF.Copy),
                     reads=[PR(1)], writes=[R("gT", 0, 8)])
            for t in range(4):
                wt, wr = load_w(w_in[:, :, 1024 + t * 512:1024 + (t + 1) * 512], 8, 512, tid=('in', 1024 + t * 512))
                for kc in range(8):
                    P.op("pe", lambda e, kc=kc, wt=wt: e.matmul(pb[2][0:NS, 0:512], lhsT=hTs[:, kc, :], rhs=wt[:, kc, :], start=(kc == 0), stop=(kc == 7)),
                         reads=[wr, RCB], writes=[PR(2)])
                P.op("act", lambda e, t=t: e.activation(out=szS[:, t * 512:(t + 1) * 512], in_=pb[2][0:NS, 0:512], func=AF.Silu), reads=[PR(2)], writes=[R("xdte", 0, 8)])
            for kc in range(8):
                P.op("pe", lambda e, kc=kc: e.matmul(pb[3][0:NS, 0:32], lhsT=hTs[:, kc, :], rhs=wdt[:, kc, :], start=(kc == 0), stop=(kc == 7)), reads=[RCB, R("wdt")], writes=[PR(3)])
            d_x, d_t, d_u, d_dt, d_dec = dtx[0:NS, 0, :], dtt[0:NS, 0, :], dtu[0:NS, 0, :], dt_tok[0:NS, 0, :], dtx[0:NS, 1, :]
            P.op("dve", lambda e: e.tensor_tensor(out=d_x, in0=pb[3][0:NS, 0:32], in1=rowb[0:NS, RB_DTB:RB_DTB + 32], op=ALU.add), reads=[PR(3), R("rowb")], writes=[R("dtx")])
            P.op("act", lambda e: e.activation(out=d_t, in_=d_x, func=AF.Abs), reads=[R("dtx")], writes=[R("dtt")])
            P.op("act", lambda e: e.activation(out=d_u, in_=d_t, func=AF.Exp, scale=-1.0), reads=[R("dtt")], writes=[R("dtu")])
            P.op("act", lambda e: e.activation(out=d_t, in_=d_u, func=AF.Ln, bias=1.0, scale=1.0), reads=[R("dtu")], writes=[R("dtt")])
            P.op("dve", lambda e: e.scalar_tensor_tensor(out=d_dt, in0=d_x, scalar=0.0, in1=d_t, op0=ALU.max, op1=ALU.add), reads=[R("dtx"), R("dtt")], writes=[R("dt_tok")])
            P.op("dve", lambda e: e.tensor_tensor(out=d_u, in0=d_dt, in1=anegb[0:NS, :], op=ALU.mult), reads=[R("dt_tok"), R("anegb")], writes=[R("dtu")])
            P.op("act", lambda e: e.activation(out=d_dec, in_=d_u, func=AF.Exp), reads=[R("dtu")], writes=[R("dtx")])
            for s_ in range(NS):
                P.op("pe", lambda e, s_=s_: e.matmul(pb[0][:, s_ * 32:(s_ + 1) * 32], lhsT=identfS[0:NS, s_:s_ + 1].to_broadcast([NS, 128]), rhs=d_dec, start=True, stop=True),
                     reads=[R("sB"), R("dtx")], writes=[PR(0)])
            P.op("act", lambda e: e.activation(out=sA[:, 0:512], in_=pb[0][:, 0:512], func=AF.Copy), reads=[PR(0)], writes=[R("sA")])
            for g in range(4):
                w = 2 ** (g + 1)
                ug = projS[:, 2 * g:2 * g + 2, :]
                P.op("dve", lambda e, g=g, w=w: e.reduce_sum(out=ptmp[:, 0:2, :], in_=stp[:, 2 * g:2 * g + 2, :, 15 - (w - 1):15], axis=mybir.AxisListType.X),
                     reads=[R("ubuf", 0, 8)], writes=[RAM])
                P.op("dve", lambda e, ug=ug: e.tensor_tensor(out=ptmp[:, 0:2, :], in0=ptmp[:, 0:2, :], in1=ug, op=ALU.add), reads=[RAM, R("gT", 0, 8)], writes=[RAM])
                P.op("dve", lambda e, ug=ug, g=g, w=w: e.scalar_tensor_tensor(out=pooledS[:, 2 * g:2 * g + 2, :], in0=ptmp[:, 0:2, :], scalar=1.0 / w, in1=ug, op0=ALU.mult, op1=ALU.subtract),
                     reads=[RAM, R("gT", 0, 8)], writes=[RCB])
            P.op("act", lambda e: e.activation(out=gaS, in_=projS[:, 40:48, :], func=AF.Sigmoid), reads=[R("gT", 0, 8)], writes=[RAM])
            P.op("act", lambda e: e.activation(out=gbS, in_=projS[:, 48:56, :], func=AF.Sigmoid), reads=[R("gT", 0, 8)], writes=[RAM])
            wplt, wplr = load_w(w_pool.rearrange("p g k c -> p (g k) c"), 8, 256, tid=('pool',))
            for g in range(4):
                for j in range(2):
                    oc = 2 * g + j
                    for k2 in range(2):
                        P.op("pe", lambda e, g=g, j=j, k2=k2, oc=oc: e.matmul(pb[2][:, oc * NS:(oc + 1) * NS], lhsT=wplt[:, 2 * g + k2, j * 128:(j + 1) * 128], rhs=pooledS[:, 2 * g + k2, :], start=(k2 == 0), stop=(k2 == 1)),
                             reads=[wplr, RCB], writes=[PR(2)])
            for oc in range(8):
                P.op("dve", lambda e, oc=oc: e.scalar_tensor_tensor(out=amS[:, oc, :], in0=pb[2][:, oc * NS:(oc + 1) * NS], scalar=pv[:, PV_PSC + oc:PV_PSC + oc + 1], in1=gaS[:, oc, :], op0=ALU.mult, op1=ALU.mult),
                     reads=[PR(2), R("pv"), RAM], writes=[RAM])
            P.op("pool", lambda e: e.tensor_copy(out=newst[:, :, :, 0:14], in_=stp[:, :, :, 1:15]), reads=[R("ubuf", 0, 8)], writes=[R("ubuf", 0, 8)])
            P.op("pool", lambda e: e.tensor_copy(out=newst[:, :, :, 14], in_=projS[:, 0:8, :]), reads=[R("gT", 0, 8)], writes=[R("ubuf", 0, 8)])
            DMA("sp", o_ps, newst, "st_ps", reads=[R("ubuf", 0, 8)], writes=[R("o_ps")])
            xnew = projS[:, 8:40, :]
            cwb = lambda k: pv[:, PV_CW + 32 * k:PV_CW + 32 * k + 32].unsqueeze(2).to_broadcast([128, 32, NS])
            P.op("dve", lambda e: e.tensor_tensor(out=acc1, in0=xnew, in1=cwb(3), op=ALU.mult), reads=[R("gT", 0, 8), R("pv")], writes=[RAM])
            for k in range(3):
                P.op("dve", lambda e, k=k: e.tensor_tensor(out=acc2, in0=stc[:, :, :, k], in1=cwb(k), op=ALU.mult), reads=[R("ybuf", 0, 8), R("pv")], writes=[RAM])
                P.op("dve", lambda e: e.tensor_tensor(out=acc1, in0=acc1, in1=acc2, op=ALU.add), reads=[RAM], writes=[RAM])
            P.op("dve", lambda e: e.tensor_tensor(out=acc1, in0=acc1, in1=pv[:, PV_CB:PV_CB + 32].unsqueeze(2).to_broadcast([128, 32, NS]), op=ALU.add), reads=[RAM, R("pv")], writes=[RAM])
            P.op("act", lambda e: e.activation(out=xbcS, in_=acc1, func=AF.Silu), reads=[RAM], writes=[R("B_tok")])
            P.op("pool", lambda e: e.tensor_copy(out=newcst[:, :, :, 0:2], in_=stc[:, :, :, 1:3]), reads=[R("ybuf", 0, 8)], writes=[R("hst", 0, 8)])
            P.op("pool", lambda e: e.tensor_copy(out=newcst[:, :, :, 2], in_=xnew), reads=[R("gT", 0, 8)], writes=[R("hst", 0, 8)])
            DMA("sp", o_cs, newcst, "st_cs", reads=[R("hst", 0, 8)], writes=[R("o_cs")])
            for fc in range(16):
                bk = 1 + fc // 8
                P.op("pe", lambda e, fc=fc, bk=bk: e.transpose(out=PBF(bk)[0:NS, (fc % 8) * 128:(fc % 8 + 1) * 128], in_=xbcS[:, fc, :], identity=identb[:]),
                     reads=[R("B_tok"), R("identb")], writes=[PR(bk)])
            for q in range(2):
                P.op("act", lambda e, q=q: e.activation(out=x_tokS[:, q * 1024:(q + 1) * 1024], in_=PBF(1 + q)[0:NS, 0:1024], func=AF.Copy), reads=[PR(1 + q)], writes=[R("xdt1")])
            P.op("dve", lambda e: e.tensor_tensor(out=xdt_tokS.rearrange("p (h q) -> p h q", h=NH), in0=x_tokS.rearrange("p (h q) -> p h q", h=NH),
                                                  in1=d_dt.unsqueeze(2).to_broadcast([NS, NH, HP]), op=ALU.mult), reads=[R("xdt1"), R("dt_tok")], writes=[R("xdt0")])
            P.op("dve", lambda e: e.tensor_tensor(out=CmaskS, in0=xbcS[:, 24:32, :].unsqueeze(3).to_broadcast([128, NG, NS, NS]),
                                                  in1=mask16.unsqueeze(1).to_broadcast([128, NG, NS, NS]), op=ALU.mult), reads=[R("B_tok"), R("sB")], writes=[R("xbc", 0, 32)])
            P.op("pool", lambda e: e.memset(ysS, 0.0), writes=[R("sz", 0, 4)])
            def samp_L(s_):
                sl = s_ % 2
                DMA("sp", stbuf[sl], st_ssm[s_], "ld_st%d" % sl, writes=[streg[sl]])

            def samp_A(s_):
                sl = s_ % 2
                buf = stbuf[sl]
                for q in range(4):
                    P.op("pe", lambda e, q=q: e.matmul(pb[q][:, 0:512], lhsT=identb[0:NS, s_:s_ + 1].to_broadcast([NS, 128]), rhs=xdt_tokS[:, q * 512:(q + 1) * 512], start=True, stop=True),
                         reads=[R("identb"), R("xdt0")], writes=[PR(q)])
                P.op("pool", lambda e: e.tensor_tensor(out=buf.rearrange("p (h q) -> p h q", h=NH), in0=buf.rearrange("p (h q) -> p h q", h=NH),
                                                       in1=decBs[:, s_, :].unsqueeze(2).to_broadcast([128, NH, HP]), op=ALU.mult), reads=[streg[sl], R("sA")], writes=[streg[sl]])
                for g in range(NG):
                    P.op("dve", lambda e, g=g: e.scalar_tensor_tensor(out=buf[:, g * 256:(g + 1) * 256], in0=pb[g // 2][:, (g % 2) * 256:(g % 2 + 1) * 256],
                                                                       scalar=xbcS[:, 16 + g, s_:s_ + 1], in1=buf[:, g * 256:(g + 1) * 256], op0=ALU.mult, op1=ALU.add),
                         reads=[PR(g // 2), R("B_tok"), streg[sl]], writes=[streg[sl]])
                P.op("act", lambda e: e.activation(out=hstb[:], in_=buf, func=AF.Copy), reads=[streg[sl]], writes=[R("hstb", 0, 8)])
                DMA("sp", o_ss[s_], buf, "st_ss%d" % sl, reads=[streg[sl]], writes=[R("o_ss", s_, s_ + 1)])

            def samp_A2(s_):
                for g in range(NG):
                    P.op("pe", lambda e, g=g: e.matmul(pb[4 + g // 2][0:NS, (g % 2) * 256:(g % 2 + 1) * 256], lhsT=CmaskS[:, g, s_, :], rhs=hstb[:, g * 256:(g + 1) * 256], start=True, stop=True),
                         reads=[R("xbc", 0, 32), R("hstb", 0, 8)], writes=[PR(4 + g // 2)])

            def samp_B(s_):
                for q in range(4):
                    P.op("dve", lambda e, q=q: e.tensor_tensor(out=ysS[:, q * 512:(q + 1) * 512], in0=pb[4 + q][0:NS, 0:512], in1=ysS[:, q * 512:(q + 1) * 512], op=ALU.add),
                         reads=[PR(4 + q), R("sz", 0, 4)], writes=[R("sz", 0, 4)])

            samp_L(0)
            samp_L(1)
            samp_A(0)
            samp_A2(0)
            for s_ in range(NS):
                if s_ + 2 < NS:
                    samp_L(s_ + 2)
                if s_ + 1 < NS:
                    samp_A(s_ + 1)
                samp_B(s_)
                if s_ + 1 < NS:
                    samp_A2(s_ + 1)
            P.op("dve", lambda e: e.tensor_tensor(out=tmpDx.rearrange("p (h q) -> p h q", h=NH), in0=x_tokS.rearrange("p (h q) -> p h q", h=NH),
                                                  in1=rowb[0:NS, RB_DSK:RB_DSK + 32].unsqueeze(2).to_broadcast([NS, NH, HP]), op=ALU.mult), reads=[R("xdt1"), R("rowb")], writes=[R("hT", 0, 8)])
            P.op("dve", lambda e: e.tensor_tensor(out=ysS, in0=ysS, in1=tmpDx, op=ALU.add), reads=[R("sz", 0, 4), R("hT", 0, 8)], writes=[R("sz", 0, 4)])
            P.op("dve", lambda e: e.tensor_tensor(out=ysS, in0=ysS, in1=szS, op=ALU.mult), reads=[R("sz", 0, 4), R("xdte", 0, 8)], writes=[R("sz", 0, 4)])
            P.op("act", lambda e: e.activation(out=junkS, in_=ysS, func=AF.Square, accum_out=ssS[:, 1:2]), reads=[R("sz", 0, 4)], writes=[R("hT", 0, 8), R("ssq", 0, 8)])
            P.op("act", lambda e: e.activation(out=rsS[:, 5:6], in_=ssS[:, 1:2], func=AF.Ln, scale=1.0 / DI, bias=epsc[0:NS, 0:1]), reads=[R("ssq", 0, 8), R("epsc")], writes=[R("rsq", 0, 8)])
            P.op("act", lambda e: e.activation(out=rsS[:, 1:2], in_=rsS[:, 5:6], func=AF.Exp, scale=-0.5), reads=[R("rsq", 0, 8)], writes=[R("rsq", 0, 8)])
            P.op("dve", lambda e: e.scalar_tensor_tensor(out=ynS, in0=ysS, scalar=rsS[:, 1:2], in1=snwb[0:NS, :], op0=ALU.mult, op1=ALU.mult),
                 reads=[R("sz", 0, 4), R("rsq", 0, 8), R("snwb")], writes=[R("hT", 0, 8)])
            for fc in range(16):
                P.op("pe", lambda e, fc=fc: e.transpose(out=PBF(0)[:, fc * NS:(fc + 1) * NS], in_=ynS[:, fc * 128:(fc + 1) * 128], identity=identb[0:NS, 0:NS]),
                     reads=[R("hT", 0, 8), R("identb")], writes=[PR(0)])
            P.op("act", lambda e: e.activation(out=ynTs, in_=PBF(0)[:, 0:256].rearrange("p (c s) -> p c s", c=16), func=AF.Copy), reads=[PR(0)], writes=[RCB])
            for t in range(4):
                wt, wr = load_w(w_ssd[:, :, t * 256:(t + 1) * 256], 16, 256, tid=('ssd', t))
                for j in range(2):
                    oc = 2 * t + j
                    for kc in range(16):
                        P.op("pe", lambda e, j=j, kc=kc, oc=oc, wt=wt: e.matmul(pb[1][:, oc * NS:(oc + 1) * NS], lhsT=wt[:, kc, j * 128:(j + 1) * 128], rhs=ynTs[:, kc, :], start=(kc == 0), stop=(kc == 15)),
                             reads=[wr, RCB], writes=[PR(1)])
            P.op("dve", lambda e: e.tensor_tensor(out=ptmp, in0=pb[1][:, 0:128].rearrange("p (c s) -> p c s", c=8), in1=gbS, op=ALU.mult), reads=[PR(1), RAM], writes=[RAM])
            P.op("dve", lambda e: e.tensor_tensor(out=mixTs, in0=ptmp, in1=amS, op=ALU.add), reads=[RAM], writes=[RCB])
            tmpR = ysS[:, 0:512]
            for t in range(2):
                wt, wr = load_w(w_out[:, :, t * 512:(t + 1) * 512], 8, 512, tid=('out', t))
                for kc in range(8):
                    P.op("pe", lambda e, kc=kc, wt=wt, t=t: e.matmul(pb[2 + t][0:NS, 0:512], lhsT=mixTs[:, kc, :], rhs=wt[:, kc, :], start=(kc == 0), stop=(kc == 7)), reads=[wr, RCB], writes=[PR(2 + t)])
                P.op("dve", lambda e, t=t: e.tensor_tensor(out=tmpR, in0=pb[2 + t][0:NS, 0:512], in1=gs1[:, t * 512:(t + 1) * 512], op=ALU.mult), reads=[PR(2 + t), R("sz", 0, 4)], writes=[R("sz", 0, 4)])
                P.op("dve", lambda e, t=t: e.tensor_tensor(out=xs_tok[:, t * 512:(t + 1) * 512], in0=xs_tok[:, t * 512:(t + 1) * 512], in1=tmpR, op=ALU.add), reads=[R("sz", 0, 4), R("yst0")], writes=[R("yst0")])
            s_norm_T(a2T, 24)
            for t in range(11):
                wt, wr = load_w(w_ffi[:, :, t * 512:(t + 1) * 512], 8, 512, tid=('ffi', t))
                for q in range(4):
                    for kc in range(8):
                        P.op("pe", lambda e, q=q, kc=kc, wt=wt: e.matmul(pb[1][:, q * NS:(q + 1) * NS], lhsT=wt[:, kc, q * 128:(q + 1) * 128], rhs=hTs[:, kc, :], start=(kc == 0), stop=(kc == 7)),
                             reads=[wr, RCB], writes=[PR(1)])
                P.op("act", lambda e: e.activation(out=sgS, in_=pb[1][:, 0:2 * NS], func=AF.Silu), reads=[PR(1)], writes=[RAM])
                P.op("dve", lambda e, t=t: e.tensor_tensor(out=fTs[:, 2 * t:2 * t + 2, :], in0=pb[1][:, 2 * NS:4 * NS].rearrange("p (c s) -> p c s", c=2), in1=sgS.rearrange("p (c s) -> p c s", c=2), op=ALU.mult),
                     reads=[PR(1), RAM], writes=[RCB])
            for half in range(2):
                for kg in range(3):
                    nk = 8 if kg < 2 else 6
                    wt, wr = load_w(w_ffo[:, kg * 8:kg * 8 + nk, half * 512:(half + 1) * 512], nk, 512, tid=('ffo', kg, half))
                    for kc in range(nk):
                        P.op("pe", lambda e, kc=kc, kg=kg, nk=nk, wt=wt, half=half: e.matmul(pb[2 + half][0:NS, 0:512], lhsT=fTs[:, kg * 8 + kc, :], rhs=wt[:, kc, :], start=(kg == 0 and kc == 0), stop=(kg == 2 and kc == nk - 1)),
                             reads=[wr, RCB], writes=[PR(2 + half)])
                P.op("dve", lambda e, half=half: e.tensor_tensor(out=tmpR, in0=pb[2 + half][0:NS, 0:512], in1=gs2[:, half * 512:(half + 1) * 512], op=ALU.mult), reads=[PR(2 + half), R("sz", 0, 4)], writes=[R("sz", 0, 4)])
                P.op("dve", lambda e, half=half: e.tensor_tensor(out=xs_tok[:, half * 512:(half + 1) * 512], in0=xs_tok[:, half * 512:(half + 1) * 512], in1=tmpR, op=ALU.add), reads=[R("sz", 0, 4), R("yst0")], writes=[R("yst0")])
            P.op("act", lambda e: e.activation(out=junkS[:, 0:D], in_=xs_tok, func=AF.Square, accum_out=ssS[:, 2:3]), reads=[R("yst0")], writes=[R("hT", 0, 8), R("ssq", 0, 8)])
            P.op("act", lambda e: e.activation(out=rsS[:, 6:7], in_=ssS[:, 2:3], func=AF.Ln, scale=1.0 / D, bias=epsc[0:NS, 0:1]), reads=[R("ssq", 0, 8), R("epsc")], writes=[R("rsq", 0, 8)])
            P.op("act", lambda e: e.activation(out=rsS[:, 2:3], in_=rsS[:, 6:7], func=AF.Exp, scale=-0.5), reads=[R("rsq", 0, 8)], writes=[R("rsq", 0, 8)])
            P.op("dve", lambda e: e.scalar_tensor_tensor(out=xs_tok, in0=xs_tok, scalar=rsS[:, 2:3], in1=rowb[0:NS, RB_FNW:RB_FNW + D], op0=ALU.mult, op1=ALU.mult),
                 reads=[R("yst0"), R("rsq", 0, 8), R("rowb")], writes=[R("yst0")])
            DMA("sp", o_ys, xs_tok, "st_ys", reads=[R("yst0")], writes=[R("o_ys")])

        P.op("sp", None, reads=[R("o_y", 0, 4 * NBLK), R("o_pp"), R("o_cp"), R("o_sp"), R("o_ys"), R("o_ps"), R("o_cs"), R("o_ss", 0, NS)])

        P.analyze()
        sems_e = {e: es.enter_context(nc.semaphore("se_" + e)) for e in ENGS}
        sems_d = {k: es.enter_context(nc.semaphore("sd_" + k)) for k in sorted(dma_keys)}
        P.emit(sems_e, sems_d)
    return nc


def _tile_k(w):
    K, N = w.shape
    return np.ascontiguousarray(w.reshape(K // 128, 128, N).transpose(1, 0, 2))


def _fm(v):
    return np.ascontiguousarray(v.reshape(-1, 128).T)


_NC_CACHE = {}


def kernel(x_prompt, x_sample, c_prompt, c_sample, state_pool, state_conv, state_ssm, w_ada, b_ada, norm1_w,
           w_in, w_pool, pool_scale, conv_w, conv_b, dt_bias, A_log, D_skip, ssd_norm_w, w_ssd_proj, w_out,
           norm2_w, w_ffn_in, w_ffn_out, final_norm_w):
    f = np.float32
    n = 8
    x_prompt = np.asarray(x_prompt, f)
    pvec = np.zeros((128, PV_N), f)
    pvec[:, PV_N1W:PV_N1W + 8] = _fm(np.asarray(norm1_w[0], f))
    pvec[:, PV_PSC:PV_PSC + 8] = _fm(np.asarray(pool_scale[0], f))
    cw = np.asarray(conv_w[0], f)
    for k in range(4):
        pvec[:, PV_CW + 32 * k:PV_CW + 32 * k + 32] = _fm(cw[k])
    pvec[:, PV_CB:PV_CB + 32] = _fm(np.asarray(conv_b[0], f))
    pvec[:, PV_N2W:PV_N2W + 8] = _fm(np.asarray(norm2_w[0], f))
    pvec[:, PV_BADA:PV_BADA + 48] = _fm(np.asarray(b_ada[0], f))
    pvec[:, PV_DF:PV_DF + 16] = _fm(np.repeat(np.asarray(D_skip[0], f), HP))
    rowb = np.zeros((128, RB_N), f)
    rowb[:, RB_FNW:RB_FNW + D] = np.asarray(final_norm_w, f)[None, :]
    bg = np.zeros((128, 2 * D), f)
    bg[:, 0:D] = np.asarray(b_ada[0], f)[None, 2 * D:3 * D]
    bg[:, D:2 * D] = np.asarray(b_ada[0], f)[None, 5 * D:6 * D]
    rowb[:, RB_DSK:RB_DSK + 32] = np.asarray(D_skip[0], f)[None, :]
    rowb[:, RB_ALOG:RB_ALOG + 32] = np.asarray(A_log[0], f)[None, :]
    rowb[:, RB_DTB:RB_DTB + 32] = np.asarray(dt_bias[0], f)[None, :]
    snw = np.ascontiguousarray(np.broadcast_to(np.asarray(ssd_norm_w[0], f)[None, :], (128, DI)))
    w_ada_t = _tile_k(np.asarray(w_ada[0], f))
    w_in_t = _tile_k(np.asarray(w_in[0], f))
    wp = np.asarray(w_pool[0], f)
    w_pool_t = np.ascontiguousarray(np.stack([_tile_k(wp[g]) for g in range(4)], axis=1))
    w_ssd_t = _tile_k(np.asarray(w_ssd_proj[0], f))
    w_out_t = _tile_k(np.asarray(w_out[0], f))
    wfi = np.asarray(w_ffn_in[0], f)
    perm = np.concatenate([np.concatenate([np.arange(256 * t, 256 * t + 256), DFF + np.arange(256 * t, 256 * t + 256)]) for t in range(11)])
    w_ffi_t = _tile_k(np.ascontiguousarray(wfi[:, perm]))
    w_ffo_t = _tile_k(np.asarray(w_ffn_out[0], f))

    in_maps = []
    for b in range(n):
        s0, s1 = NS * b, NS * (b + 1)
        c17 = np.concatenate([np.asarray(c_prompt[b:b + 1], f), np.asarray(c_sample[s0:s1], f)], axis=0)
        cT = np.ascontiguousarray(c17.T.reshape(8, 128, 17).transpose(1, 0, 2))
        sp = np.asarray(state_pool[0, s0:s1], f)
        sp_t = np.ascontiguousarray(sp.reshape(NS, 15, 8, 128).transpose(3, 2, 0, 1))
        sc = np.asarray(state_conv[0, s0:s1], f)
        sc_t = np.ascontiguousarray(sc.reshape(NS, 3, 32, 128).transpose(3, 2, 0, 1))
        ss = np.asarray(state_ssm[0, s0:s1], f)
        ss_t = np.ascontiguousarray(ss.reshape(NS, DI, DST).transpose(0, 2, 1))
        in_maps.append({
            "xp": np.ascontiguousarray(x_prompt[b]),
            "xsm": np.ascontiguousarray(np.asarray(x_sample[s0:s1, 0], f)),
            "cT": cT, "pvec": pvec, "rowb": rowb, "snw": snw, "bg": bg,
            "w_ada": w_ada_t, "w_in": w_in_t, "w_pool": w_pool_t, "w_ssd": w_ssd_t, "w_out": w_out_t,
            "w_ffi": w_ffi_t, "w_ffo": w_ffo_t,
            "st_pool": sp_t, "st_conv": sc_t, "st_ssm": ss_t,
        })
    nblk = int(os.environ.get("K_NBLK", NBLK))
    ncores = int(os.environ.get("K_CORES", n))
    if "nc" not in _NC_CACHE:
        _NC_CACHE["nc"] = build_nc(nblk=nblk)
    nc = _NC_CACHE["nc"]
    res = run_bass_kernel_spmd(nc, in_maps[:ncores], core_ids=list(range(ncores)))
    rs = list(res.results)
    while len(rs) < n:
        rs.append({k: np.zeros_like(v) for k, v in rs[0].items()})
    y_prompt = np.stack([rs[b]["o_y"] for b in range(n)], axis=0)
    y_sample = np.concatenate([rs[b]["o_ys"] for b in range(n)], axis=0)[:, None, :]
    pool_p = np.stack([rs[b]["o_pp"].transpose(2, 1, 0).reshape(15, D) for b in range(n)], axis=0)[None]
    conv_p = np.stack([rs[b]["o_cp"].transpose(2, 1, 0).reshape(3, CONV) for b in range(n)], axis=0)[None]
    ssm_p = np.stack([rs[b]["o_sp"].T.reshape(NH, HP, DST) for b in range(n)], axis=0)[None]
    pool_s = np.concatenate([rs[b]["o_ps"].transpose(2, 3, 1, 0).reshape(NS, 15, D) for b in range(n)], axis=0)[None]
    conv_s = np.concatenate([rs[b]["o_cs"].transpose(2, 3, 1, 0).reshape(NS, 3, CONV) for b in range(n)], axis=0)[None]
    ssm_s = np.concatenate([rs[b]["o_ss"].transpose(0, 2, 1).reshape(NS, NH, HP, DST) for b in range(n)], axis=0)[None]
    return (np.ascontiguousarray(y_prompt, dtype=f), np.ascontiguousarray(y_sample, dtype=f),
            np.ascontiguousarray(pool_p, dtype=f), np.ascontiguousarray(conv_p, dtype=f),
            np.ascontiguousarray(ssm_p, dtype=f), np.ascontiguousarray(pool_s, dtype=f),
            np.ascontiguousarray(conv_s, dtype=f), np.ascontiguousarray(ssm_s, dtype=f))
```

```python
import os
from contextlib import ExitStack

import numpy as np
import concourse.bass as bass
import concourse.mybir as mybir
from concourse.bass_utils import run_bass_kernel_spmd

F32 = mybir.dt.float32
BF16 = mybir.dt.bfloat16
AF = mybir.ActivationFunctionType
ALU = mybir.AluOpType

ENGS = ("pe", "act", "dve", "pool", "sp")

D = 1024
SEQ = 2048
TB = 512
NBLK = SEQ // TB
NS = 16
DI = 2048
NH = 32
HP = 64
NG = 8
DST = 128
CONV = 4096
DFF = 2816
IN_COLS = 9248
EPS = 1e-6

PV_N1W, PV_PSC, PV_CW, PV_CB, PV_N2W, PV_BADA, PV_DF, PV_N = 0, 8, 16, 144, 176, 184, 232, 248
RB_FNW, RB_DSK, RB_ALOG, RB_DTB, RB_N = 0, 1024, 1056, 1088, 1120


class Op:
    __slots__ = ("eng", "fn", "reads", "writes", "dkey", "dgroup", "idx", "waits",
                 "sig", "cnt", "eidx")

    def __init__(self, eng, fn, reads, writes, dkey=None, dgroup=None):
        self.eng = eng
        self.fn = fn
        self.reads = reads
        self.writes = writes
        self.dkey = dkey
        self.dgroup = dgroup
        self.waits = {}
        self.sig = False
        self.cnt = 0


class Prog:
    def __init__(self, nc):
        self.nc = nc
        self.ops = []

    def op(self, eng, fn, reads=(), writes=()):
        o = Op(eng, fn, list(reads), list(writes))
        o.idx = len(self.ops)
        self.ops.append(o)
        return o

    def dma(self, eng, fn, reads=(), writes=(), key=None, group=None):
        o = Op(eng, fn, list(reads), list(writes), dkey=key, dgroup=group)
        o.idx = len(self.ops)
        self.ops.append(o)
        return o

    def analyze(self):
        ops = self.ops
        ecount = {e: 0 for e in ENGS}
        for o in ops:
            o.eidx = ecount[o.eng]
            ecount[o.eng] += 1
        key_ops = {}
        for o in ops:
            if o.dkey is not None:
                key_ops.setdefault(o.dkey, []).append(o)
        dma_cum, dma_prev = {}, {}
        for k, lst in key_ops.items():
            groups = []
            for o in lst:
                if groups and o.dgroup is not None and groups[-1][0] == o.dgroup:
                    groups[-1][1].append(o)
                else:
                    groups.append((o.dgroup, [o]))
            cum = 0
            for g, gl in groups:
                prev = cum
                cum += len(gl)
                for o in gl:
                    dma_cum[o.idx] = cum
                    dma_prev[o.idx] = prev
        self.keys = sorted(key_ops.keys())
        recs = {}
        waited = {}
        pend = []
        for o in ops:
            d = set()
            for (buf, lo, hi) in o.reads:
                for r in recs.get(buf, ()):
                    if r[3] and r[0] < hi and lo < r[1]:
                        d.add(r[2])
            for (buf, lo, hi) in o.writes:
                for r in recs.get(buf, ()):
                    if r[0] < hi and lo < r[1]:
                        d.add(r[2])
            d.discard(o.idx)
            for (buf, lo, hi) in o.writes:
                lst = recs.setdefault(buf, [])
                lst[:] = [r for r in lst if not (lo <= r[0] and r[1] <= hi)]
                lst.append([lo, hi, o.idx, True])
            for (buf, lo, hi) in o.reads:
                lst = recs.setdefault(buf, [])
                lst[:] = [r for r in lst if not ((not r[3]) and ops[r[2]].eng == o.eng
                                                 and ops[r[2]].dkey is None and o.dkey is None
                                                 and lo <= r[0] and r[1] <= hi)]
                lst.append([lo, hi, o.idx, False])
            need = {}
            for di in d:
                p = ops[di]
                if p.dkey is not None:
                    sk = ("d", p.dkey)
                    val = 16 * dma_cum[p.idx]
                    if need.get(sk, 0) < val:
                        need[sk] = val
                    continue
                if p.eng == o.eng and o.dkey is None:
                    if o.eng in ("pe", "sp"):
                        continue
                sk = ("e", p.eng)
                cur = need.get(sk)
                if cur is None or cur.eidx < p.eidx:
                    need[sk] = p
            if o.dkey is not None and dma_prev[o.idx] > 0:
                sk = ("d", o.dkey)
                val = 16 * dma_prev[o.idx]
                if need.get(sk, 0) < val:
                    need[sk] = val
            o.waits = need
            for sk, v in need.items():
                if sk[0] == "e":
                    v.sig = True
        cnt = {e: 0 for e in ENGS}
        for o in ops:
            if o.dkey is None and o.sig:
                cnt[o.eng] += 1
            o.cnt = cnt[o.eng]
        for o in ops:
            final = {}
            for sk, v in o.waits.items():
                val = v.cnt if sk[0] == "e" else v
                wk = (o.eng, sk)
                if waited.get(wk, 0) >= val:
                    continue
                waited[wk] = val
                final[sk] = val
            o.waits = final

    def emit(self, sems_e, sems_d):
        nc = self.nc
        per = {e: [o for o in self.ops if o.eng == e] for e in ENGS}

        def run(engname, eng):
            for o in per[engname]:
                for sk, val in o.waits.items():
                    sem = sems_e[sk[1]] if sk[0] == "e" else sems_d[sk[1]]
                    eng.wait_ge(sem, val)
                if o.fn is None:
                    continue
                ins = o.fn(eng)
                if o.dkey is not None:
                    ins.then_inc(sems_d[o.dkey], 16)
                elif o.sig:
                    ins.then_inc(sems_e[o.eng], 1)

        with nc.Block() as block:
            @block.tensor
            def _(e):
                run("pe", e)

            @block.scalar
            def _(e):
                run("act", e)

            @block.vector
            def _(e):
                run("dve", e)

            @block.gpsimd
            def _(e):
                run("pool", e)

            @block.sync
            def _(e):
                run("sp", e)


def R(name, lo=0, hi=1):
    return (name, lo, hi)


def build_nc(nblk=NBLK, do_sample=True):
    nc = bass.Bass("TRN2", target_bir_lowering=False)

    def din(name, shape):
        return nc.dram_tensor(name, list(shape), F32, kind="ExternalInput").ap()

    def dout(name, shape):
        return nc.dram_tensor(name, list(shape), F32, kind="ExternalOutput").ap()

    xp = din("xp", [SEQ, D])
    xsm = din("xsm", [NS, D])
    cT = din("cT", [128, 8, 17])
    pvec = din("pvec", [128, PV_N])
    rowb_in = din("rowb", [128, RB_N])
    snw_in = din("snw", [128, DI])
    bg_in = din("bg", [128, 2 * D])
    w_ada = din("w_ada", [128, 8, 6 * D])
    w_in = din("w_in", [128, 8, IN_COLS])
    w_pool = din("w_pool", [128, 4, 2, 256])
    w_ssd = din("w_ssd", [128, 16, D])
    w_out = din("w_out", [128, 8, D])
    w_ffi = din("w_ffi", [128, 8, 2 * DFF])
    w_ffo = din("w_ffo", [128, 22, D])
    st_pool = din("st_pool", [128, 8, NS, 15])
    st_conv = din("st_conv", [128, 32, NS, 3])
    st_ssm = din("st_ssm", [NS, 128, DI])
    o_y = dout("o_y", [SEQ, D])
    o_ys = dout("o_ys", [NS, D])
    o_pp = dout("o_pp", [128, 8, 15])
    o_cp = dout("o_cp", [128, 32, 3])
    o_sp = dout("o_sp", [128, DI])
    o_ps = dout("o_ps", [128, 8, NS, 15])
    o_cs = dout("o_cs", [128, 32, NS, 3])
    o_ss = dout("o_ss", [NS, 128, DI])
    dbg_out = {}

    es = ExitStack()
    with es:
        def sb(name, shape, dt=F32):
            return es.enter_context(nc.sbuf_tensor("s_" + name, list(shape), dt))

        P = Prog(nc)
        dma_keys = set()

        def DMA(eng, out, in_, key, reads=(), writes=(), group=None):
            dma_keys.add(key)
            P.dma(eng, lambda e: e.dma_start(out=out, in_=in_), reads=reads, writes=writes, key=key, group=group)

        pb = [es.enter_context(nc.psum_tensor("pb%d" % i, [128, 512], F32)) for i in range(8)]

        def PBF(i):
            return pb[i][:].bitcast(BF16)

        def PR(i, lo=0, hi=512):
            return ("pb%d" % i, 0, 512)

        sA = sb("sA", [128, 16 + TB])
        sB = sb("sB", [128, 16 + TB])
        identf = sB[:, 0:128]
        triuf = sB[:, 128:256]
        nmf = sA[:, 0:512].rearrange("p (a b) -> p a b", a=4)
        identb = sb("identb", [128, 128], BF16)
        onesb = sb("onesb", [128, 128], BF16)
        triub = sb("triub", [128, 128], BF16)
        nmb = sb("nmb", [128, 4, 128], BF16)
        epsc = sb("epsc", [128, 1])
        invc = sb("invc", [128, 4, 16])
        pv = sb("pv", [128, PV_N])
        rowb = sb("rowb", [128, RB_N])
        snwb = sb("snwb", [128, DI], BF16)
        anegb = sb("anegb", [128, 32])
        cTs = sb("cTs", [128, 8, 17])
        silucT = sb("silucT", [128, 8, 17], BF16)
        modT = sb("modT", [128, 48, 17])
        a1T = sb("a1T", [128, 8, 17])
        a2T = sb("a2T", [128, 8, 17])
        g1B = sb("g1B", [128, D])
        g2B = sb("g2B", [128, D])
        wdt = sb("wdt", [128, 8, 32], BF16)
        diagD = sb("diagD", [128, 16, 128], BF16)
        NWB = 2
        wbuf = [sb("wbuf%d" % i, [128, 4096], BF16) for i in range(NWB)]

        P.op("pool", lambda e: e.memset(identf, 0.0), writes=[R("sB")])
        P.op("pool", lambda e: e.affine_select(out=identf, in_=identf, pattern=[[-1, 128]], compare_op=ALU.not_equal,
                                               fill=1.0, base=0, channel_multiplier=1), reads=[R("sB")], writes=[R("sB")])
        P.op("dve", lambda e: e.tensor_copy(out=identb[:], in_=identf), reads=[R("sB")], writes=[R("identb")])
        P.op("pool", lambda e: e.memset(onesb[:], 1.0), writes=[R("onesb")])
        P.op("pool", lambda e: e.memset(triuf, 1.0), writes=[R("sB")])
        P.op("pool", lambda e: e.affine_select(out=triuf, in_=triuf, pattern=[[1, 128]], compare_op=ALU.is_ge,
                                               fill=0.0, base=0, channel_multiplier=-1), reads=[R("sB")], writes=[R("sB")])
        P.op("dve", lambda e: e.tensor_copy(out=triub[:], in_=triuf), reads=[R("sB")], writes=[R("triub")])
        P.op("pool", lambda e: e.memset(nmf, 0.0), writes=[R("sA")])
        P.op("pool", lambda e: e.affine_select(out=nmf, in_=nmf, pattern=[[0, 4], [1, 128]], compare_op=ALU.is_ge,
                                               fill=-30000.0, base=0, channel_multiplier=-1), reads=[R("sA")], writes=[R("sA")])
        P.op("dve", lambda e: e.tensor_copy(out=nmb[:], in_=nmf), reads=[R("sA")], writes=[R("nmb")])
        P.op("pool", lambda e: e.memset(epsc[:], EPS), writes=[R("epsc")])
        for g in range(4):
            w = 2 ** (g + 1)
            P.op("pool", lambda e, g=g, w=w: e.memset(invc[:, g, :], 1.0 / w), writes=[R("invc")])
            for t in range(w - 1):
                P.op("pool", lambda e, g=g, t=t: e.memset(invc[:, g, t:t + 1], 1.0 / (t + 1)), writes=[R("invc")])

        DMA("sp", pv[:], pvec, "ld_pv", writes=[R("pv")])
        DMA("sp", rowb[:], rowb_in, "ld_rowb", writes=[R("rowb")])
        DMA("sp", cTs[:], cT, "ld_c", writes=[R("cTs")])
        DMA("sp", g1B[:], bg_in[:, 0:D], "ld_g1", writes=[R("g1B", 0, 2)])
        DMA("sp", g2B[:], bg_in[:, D:2 * D], "ld_g2", writes=[R("g2B", 0, 2)])
        DMA("pool", snwb[:], snw_in, "ld_snw", writes=[R("snwb")])
        DMA("pool", wdt[:], w_in[:, :, 7168:7200], "ld_wdt", writes=[R("wdt")])
        P.op("dve", lambda e: e.tensor_tensor(out=diagD[:], in0=identb[:].unsqueeze(1).to_broadcast([128, 16, 128]),
                                              in1=pv[:, PV_DF:PV_DF + 16].unsqueeze(2).to_broadcast([128, 16, 128]), op=ALU.mult),
             reads=[R("identb"), R("pv")], writes=[R("diagD")])

        P.op("act", lambda e: e.activation(out=anegb[:], in_=rowb[:, RB_ALOG:RB_ALOG + 32], func=AF.Exp), reads=[R("rowb")], writes=[R("anegb")])
        P.op("dve", lambda e: e.tensor_scalar(out=anegb[:], in0=anegb[:], scalar1=-1.0, scalar2=None, op0=ALU.mult), reads=[R("anegb")], writes=[R("anegb")])

        wstate = {"i": 0}

        NSCR = 42
        wsc = nc.dram_tensor("wsc", [NSCR, 128, 4096], BF16).ap()
        scr_idx = {}

        def load_w(src_ap, nk, ncol, tid=None):
            i = wstate["i"] % NWB
            wstate["i"] += 1
            wt = wbuf[i]
            view = wt[:, 0:nk * ncol].rearrange("p (k c) -> p k c", k=nk)
            if tid is None:
                DMA("pool", view, src_ap, "w%d" % i, writes=[R("wbuf%d" % i)])
            elif tid not in scr_idx:
                k = len(scr_idx)
                scr_idx[tid] = k
                DMA("pool", view, src_ap, "w%d" % i, writes=[R("wbuf%d" % i)])
                DMA("sp", wsc[k, :, 0:nk * ncol], wt[:, 0:nk * ncol], "ws%d" % i, reads=[R("wbuf%d" % i)], writes=[R("wsc", k, k + 1)])
            else:
                k = scr_idx[tid]
                DMA("sp", wt[:, 0:nk * ncol], wsc[k, :, 0:nk * ncol], "wh%d" % i, reads=[R("wsc", k, k + 1)], writes=[R("wbuf%d" % i)])
            return view, R("wbuf%d" % i)

        rot = {"i": 0}

        def next_bank(banks=(0, 1, 2, 3)):
            b = banks[rot["i"] % len(banks)]
            rot["i"] += 1
            return b

        P.op("act", lambda e: e.activation(out=silucT[:], in_=cTs[:], func=AF.Silu), reads=[R("cTs")], writes=[R("silucT")])
        for t in range(12):
            wt, wr = load_w(w_ada[:, :, t * 512:(t + 1) * 512], 8, 512)
            bk = 4 + (t % 2)
            for j in range(4):
                for kc in range(8):
                    P.op("pe", lambda e, bk=bk, j=j, kc=kc, wt=wt: e.matmul(pb[bk][:, j * 17:(j + 1) * 17], lhsT=wt[:, kc, j * 128:(j + 1) * 128],
                                                                             rhs=silucT[:, kc, :], start=(kc == 0), stop=(kc == 7)),
                         reads=[wr, R("silucT")], writes=[PR(bk, j * 17, j * 17 + 17)])
            P.op("dve", lambda e, bk=bk, t=t: e.tensor_tensor(out=modT[:, 4 * t:4 * t + 4, :], in0=pb[bk][:, 0:68].rearrange("p (a b) -> p a b", a=4),
                                                               in1=pv[:, PV_BADA + 4 * t:PV_BADA + 4 * t + 4].unsqueeze(2).to_broadcast([128, 4, 17]), op=ALU.add),
                 reads=[PR(bk, 0, 68), R("pv")], writes=[R("modT", 4 * t, 4 * t + 4)])
            if t in (4, 5, 10, 11):
                gB = g1B if t < 6 else g2B
                gname = "g1B" if t < 6 else "g2B"
                half = t % 2
                for kc in range(8):
                    P.op("pe", lambda e, kc=kc, wt=wt: e.matmul(pb[6][:, 0:512], lhsT=silucT[:, kc, 0:1].to_broadcast([128, 128]), rhs=wt[:, kc, :],
                                                                  start=(kc == 0), stop=(kc == 7)),
                         reads=[wr, R("silucT")], writes=[PR(6)])
                P.op("dve", lambda e, gB=gB, half=half: e.tensor_tensor(out=gB[:, half * 512:(half + 1) * 512], in0=pb[6][:, 0:512],
                                                                         in1=gB[:, half * 512:(half + 1) * 512], op=ALU.add),
                     reads=[PR(6), R(gname, half, half + 1)], writes=[R(gname, half, half + 1)])
        for (aT, an, sc0, nw0) in ((a1T, "a1T", 8, PV_N1W), (a2T, "a2T", 32, PV_N2W)):
            P.op("dve", lambda e, aT=aT, sc0=sc0: e.tensor_scalar(out=aT[:], in0=modT[:, sc0:sc0 + 8, :], scalar1=1.0, scalar2=None, op0=ALU.add),
                 reads=[R("modT", sc0, sc0 + 8)], writes=[R(an)])
            P.op("dve", lambda e, aT=aT, nw0=nw0: e.tensor_tensor(out=aT[:], in0=aT[:], in1=pv[:, nw0:nw0 + 8].unsqueeze(2).to_broadcast([128, 8, 17]), op=ALU.mult),
                 reads=[R(an), R("pv")], writes=[R(an)])

        bufX = [sb("bx%d" % i, [128, 4, D]) for i in range(2)]
        ssq = sb("ssq", [128, 8])
        rsq = sb("rsq", [128, 8])
        hT = sb("hT", [128, 8, TB], BF16)
        ubuf = sb("ubuf", [128, 8, 16 + TB])
        gT = sb("gT", [128, 8, TB], BF16)
        gaT = gT
        gbT = gT
        uhist = sb("uhist", [128, 8, 15])
        amT = sb("amT", [128, 8, TB], BF16)
        xpre = [sb("xpre%d" % i, [128, 3 + TB], BF16) for i in range(2)]
        dg = [sb("dg%d" % i, [128, 4, 128], BF16) for i in range(2)]
        chist = sb("chist", [128, 32, 3], BF16)
        cst = sb("cst", [128, 32, 3])
        xbc = sb("xbc", [128, 32, TB], BF16)
        dtx = sb("dtx", [128, 4, 32])
        dtt = sb("dtt", [128, 4, 32])
        dtu = sb("dtu", [128, 4, 32])
        dt_tok = sb("dt_tok", [128, 4, 32])
        dA_bf = sb("dA_bf", [128, 4, 32], BF16)
        negAcs = sb("negAcs", [128, 32])
        Eacs = sb("Eacs", [128, 32])
        dte = sb("dte", [128, 32])
        cdB = sb("cdB", [128, 32])
        xdtb = [sb("xdt%d" % i, [128, DI], BF16) for i in range(2)]
        x_tok = xdtb[1]
        xdt = xdtb[0]
        xdte = sb("xdte", [128, DI], BF16)
        B_tok = sb("B_tok", [128, 1024], BF16)
        CBs = sb("CBs", [128, 8, 128], BF16)
        dec = [sb("dec%d" % i, [128, 8, 128], BF16) for i in range(2)]
        scr = dec
        ybuf = sb("ybuf", [128, DI])
        ytmp = hT[:].rearrange("p a b -> p (a b)").bitcast(F32)
        xn = ybuf[:].bitcast(BF16).rearrange("p (a b) -> p a b", a=4)
        junk = xdte

        def role_views(k):
            xt = bufX[k]
            szv = bufX[1 - k][:].rearrange("p a b -> p (a b)").bitcast(BF16).rearrange("p (a b) -> p a b", a=4)
            pl = bufX[1 - k][:].rearrange("p a b -> p (a b)").bitcast(BF16)[:, 0:8 * TB].rearrange("p (a b) -> p a b", a=8)
            return xt, "bx%d" % k, szv, "bx%d" % (1 - k), pl
        hst = sb("hst", [128, DI])
        hstb = sb("hstb", [128, DI], BF16)
        yst = [sb("yst%d" % i, [128, D]) for i in range(1)]
        ynT = ubuf[:].rearrange("p a b -> p (a b)").bitcast(BF16)[:, 0:16 * TB].rearrange("p (a b) -> p a b", a=16)
        fT = xbc[:].rearrange("p a b -> p (a b)")[:, 0:22 * TB].rearrange("p (a b) -> p a b", a=22)

        P.op("pool", lambda e: e.memset(chist[:], 0.0), writes=[R("chist", 0, 32)])
        P.op("pool", lambda e: e.memset(hst[:], 0.0), writes=[R("hst", 0, 8)])
        P.op("pool", lambda e: e.memset(hstb[:], 0.0), writes=[R("hstb", 0, 8)])

        def rmsnorm_to_T(blk_name, aT, bcol0, xtok, XN, ntok=128, ntile=4):
            for ti in range(ntile):
                P.op("act", lambda e, ti=ti: e.activation(out=junk[:, 0:D], in_=xtok[:, ti, :], func=AF.Square, accum_out=ssq[:, ti:ti + 1]),
                     reads=[R(XN, ti, ti + 1)], writes=[R("xdte", 0, 8), R("ssq", ti, ti + 1)])
                P.op("act", lambda e, ti=ti: e.activation(out=rsq[:, 4 + ti:5 + ti], in_=ssq[:, ti:ti + 1], func=AF.Ln, scale=1.0 / D, bias=epsc[:, 0:1]),
                     reads=[R("ssq", ti, ti + 1), R("epsc")], writes=[R("rsq", 4 + ti, 5 + ti)])
                P.op("act", lambda e, ti=ti: e.activation(out=rsq[:, ti:ti + 1], in_=rsq[:, 4 + ti:5 + ti], func=AF.Exp, scale=-0.5),
                     reads=[R("rsq", 4 + ti, 5 + ti)], writes=[R("rsq", ti, ti + 1)])
                P.op("dve", lambda e, ti=ti: e.tensor_scalar(out=xn[:, ti, :], in0=xtok[:, ti, :], scalar1=rsq[:, ti:ti + 1], scalar2=None, op0=ALU.mult),
                     reads=[R(XN, ti, ti + 1), R("rsq", ti, ti + 1)], writes=[R("ybuf", 2 * ti, 2 * ti + 2)])
            for kc in range(8):
                bk = kc // 2
                c0 = (kc % 2) * 512
                for ti in range(ntile):
                    P.op("pe", lambda e, bk=bk, c0=c0, ti=ti, kc=kc: e.transpose(out=PBF(bk)[:, c0 + ti * 128:c0 + (ti + 1) * 128],
                                                                                  in_=xn[:, ti, kc * 128:(kc + 1) * 128], identity=identb[:]),
                         reads=[R("ybuf", 2 * ti, 2 * ti + 2), R("identb")], writes=[PR(bk, c0 // 2 + ti * 64, c0 // 2 + (ti + 1) * 64)])
                P.op("dve", lambda e, bk=bk, c0=c0, kc=kc, aT=aT, bcol0=bcol0: e.tensor_scalar(
                    out=hT[:, kc, :], in0=PBF(bk)[:, c0:c0 + 512], scalar1=aT[:, kc, 0:1], scalar2=modT[:, bcol0 + kc, 0:1], op0=ALU.mult, op1=ALU.add),
                    reads=[PR(bk, c0 // 2, c0 // 2 + 256), R(blk_name), R("modT", bcol0 + kc, bcol0 + kc + 1)], writes=[R("hT", kc, kc + 1)])

        def proj_ws(wt, wr, j, nk, rhs_fn, rhs_regs, bank, ncols=TB, col0=0):
            for kc in range(nk):
                P.op("pe", lambda e, kc=kc: e.matmul(pb[bank][:, 0:ncols], lhsT=wt[:, kc, col0 + j * 128:col0 + (j + 1) * 128], rhs=rhs_fn(kc),
                                                      start=(kc == 0), stop=(kc == nk - 1)),
                     reads=[wr] + rhs_regs(kc), writes=[PR(bank, 0, ncols)])

        def proj_as(wt, wr, ti, nk, lhs_fn, lhs_regs, bank, kc0=0, first=True, last=True, nktot=None):
            for kc in range(nk):
                P.op("pe", lambda e, kc=kc: e.matmul(pb[bank][:, 0:512], lhsT=lhs_fn(kc0 + kc, ti), rhs=wt[:, kc, :],
                                                      start=(first and kc == 0), stop=(last and kc == nk - 1)),
                     reads=[wr] + lhs_regs(kc0 + kc), writes=[PR(bank)])

        hT_rhs = lambda kc: hT[:, kc, :]
        hT_regs = lambda kc: [R("hT", kc, kc + 1)]
        hT_lhs = lambda kc, ti: hT[:, kc, ti * 128:(ti + 1) * 128]

        def emit_block(tb, xtok, XN, sz, SN, pooled):
            if tb == 0:
                DMA("sp", xtok[:], xp[0:TB, :].rearrange("(t p) d -> p t d", p=128), "ld_x", writes=[R(XN, 0, 4)])
            rmsnorm_to_T("a1T", a1T, 0, xtok, XN)

            for t in range(2):
                wt, wr = load_w(w_in[:, :, 7200 + t * 512:7200 + (t + 1) * 512], 8, 512, tid=('in', 7200 + t * 512))
                for j in range(4):
                    oc = 4 * t + j
                    bk = next_bank()
                    proj_ws(wt, wr, j, 8, hT_rhs, hT_regs, bk)
                    P.op("act", lambda e, bk=bk, oc=oc: e.activation(out=gaT[:, oc, :], in_=pb[bk][:, 0:TB], func=AF.Sigmoid),
                         reads=[PR(bk)], writes=[R("gT", oc, oc + 1)])
            if tb == 0:
                P.op("pool", lambda e: e.memset(ubuf[:, :, 0:16], 0.0), writes=[R("ubuf", 0, 8)])
            else:
                P.op("pool", lambda e: e.tensor_copy(out=ubuf[:, :, 1:16], in_=uhist[:]), reads=[R("uhist")], writes=[R("ubuf", 0, 8)])
            for t in range(2):
                wt, wr = load_w(w_in[:, :, t * 512:(t + 1) * 512], 8, 512, tid=('in', t * 512))
                for j in range(4):
                    oc = 4 * t + j
                    bk = next_bank()
                    proj_ws(wt, wr, j, 8, hT_rhs, hT_regs, bk)
                    P.op("act", lambda e, bk=bk, oc=oc: e.activation(out=ubuf[:, oc, 16:16 + TB], in_=pb[bk][:, 0:TB], func=AF.Copy),
                         reads=[PR(bk)], writes=[R("ubuf", oc, oc + 1)])
            for oc in range(8):
                g = oc // 2
                w = 2 ** (g + 1)
                U = ubuf[:, oc, :]
                ur = R("ubuf", oc, oc + 1)
                P.op("dve", lambda e, U=U: e.tensor_tensor(out=sA[:, 2:528], in0=U[:, 2:528], in1=U[:, 1:527], op=ALU.add), reads=[ur], writes=[R("sA")])
                cur, curname = sA, "sA"
                if g >= 1:
                    P.op("dve", lambda e: e.tensor_tensor(out=sB[:, 4:528], in0=sA[:, 4:528], in1=sA[:, 2:526], op=ALU.add), reads=[R("sA")], writes=[R("sB")])
                    cur, curname = sB, "sB"
                if g >= 2:
                    P.op("dve", lambda e: e.tensor_tensor(out=sA[:, 8:528], in0=sB[:, 8:528], in1=sB[:, 4:524], op=ALU.add), reads=[R("sB")], writes=[R("sA")])
                    cur, curname = sA, "sA"
                if g >= 3:
                    P.op("dve", lambda e: e.tensor_tensor(out=sB[:, 16:528], in0=sA[:, 16:528], in1=sA[:, 8:520], op=ALU.add), reads=[R("sA")], writes=[R("sB")])
                    cur, curname = sB, "sB"
                P.op("dve", lambda e, cur=cur, U=U, w=w, oc=oc: e.scalar_tensor_tensor(out=pooled[:, oc, :], in0=cur[:, 16:528], scalar=1.0 / w, in1=U[:, 16:528],
                                                                                        op0=ALU.mult, op1=ALU.subtract),
                     reads=[R(curname), ur], writes=[R(SN, 0, 2)])
                if tb == 0:
                    P.op("dve", lambda e, cur=cur, g=g: e.tensor_tensor(out=cur[:, 0:16], in0=cur[:, 16:32], in1=invc[:, g, :], op=ALU.mult),
                         reads=[R(curname), R("invc")], writes=[R(curname)])
                    P.op("dve", lambda e, cur=cur, U=U, oc=oc: e.tensor_tensor(out=pooled[:, oc, 0:16], in0=cur[:, 0:16], in1=U[:, 16:32], op=ALU.subtract),
                         reads=[R(curname), ur], writes=[R(SN, 0, 2)])
            wplt, wplr = load_w(w_pool.rearrange("p g k c -> p (g k) c"), 8, 256, tid=('pool',))
            for g in range(4):
                for j in range(2):
                    oc = 2 * g + j
                    bk = next_bank()
                    for k2 in range(2):
                        P.op("pe", lambda e, bk=bk, g=g, j=j, k2=k2: e.matmul(pb[bk][:, 0:TB], lhsT=wplt[:, 2 * g + k2, j * 128:(j + 1) * 128], rhs=pooled[:, 2 * g + k2, :],
                                                                                start=(k2 == 0), stop=(k2 == 1)),
                             reads=[wplr, R(SN, 0, 2)], writes=[PR(bk)])
                    P.op("dve", lambda e, bk=bk, oc=oc: e.scalar_tensor_tensor(out=amT[:, oc, :], in0=pb[bk][:, 0:TB], scalar=pv[:, PV_PSC + oc:PV_PSC + oc + 1],
                                                                                in1=gaT[:, oc, :], op0=ALU.mult, op1=ALU.mult),
                         reads=[PR(bk), R("pv"), R("gT", oc, oc + 1)], writes=[R("amT", oc, oc + 1)])
            if tb == nblk - 1:
                DMA("sp", o_pp, ubuf[:, :, 513:528], "st_pp", reads=[R("ubuf", 0, 8)], writes=[R("o_pp")])
            else:
                P.op("pool", lambda e: e.tensor_copy(out=uhist[:], in_=ubuf[:, :, 513:528]), reads=[R("ubuf", 0, 8)], writes=[R("uhist")])

            for t in range(4):
                wt, wr = load_w(w_in[:, :, 1024 + t * 512:1024 + (t + 1) * 512], 8, 512, tid=('in', 1024 + t * 512))
                for ti in range(4):
                    bk = next_bank()
                    proj_as(wt, wr, ti, 8, hT_lhs, hT_regs, bk)
                    P.op("act", lambda e, bk=bk, ti=ti, t=t: e.activation(out=sz[:, ti, t * 512:(t + 1) * 512], in_=pb[bk][:, 0:512], func=AF.Silu),
                         reads=[PR(bk)], writes=[R(SN, ti, ti + 1)])
            for t in range(8):
                wt, wr = load_w(w_in[:, :, 3072 + t * 512:3072 + (t + 1) * 512], 8, 512, tid=('in', 3072 + t * 512))
                for j in range(4):
                    oc = 4 * t + j
                    sl = oc % 2
                    bk = next_bank()
                    proj_ws(wt, wr, j, 8, hT_rhs, hT_regs, bk)
                    xpr = R("xpre%d" % sl)
                    P.op("dve", lambda e, sl=sl, oc=oc: e.tensor_copy(out=xpre[sl][:, 0:3], in_=chist[:, oc, :]), reads=[R("chist", oc, oc + 1)], writes=[xpr])
                    P.op("act", lambda e, sl=sl, bk=bk: e.activation(out=xpre[sl][:, 3:3 + TB], in_=pb[bk][:, 0:TB], func=AF.Copy), reads=[PR(bk)], writes=[xpr])
                    if tb == nblk - 1:
                        P.op("dve", lambda e, bk=bk, oc=oc: e.tensor_copy(out=cst[:, oc, :], in_=pb[bk][:, TB - 3:TB]), reads=[PR(bk)], writes=[R("cst", oc, oc + 1)])
                    else:
                        P.op("dve", lambda e, sl=sl, oc=oc: e.tensor_copy(out=chist[:, oc, :], in_=xpre[sl][:, TB:TB + 3]), reads=[xpr], writes=[R("chist", oc, oc + 1)])
                    P.op("pool", lambda e, sl=sl, oc=oc: e.tensor_tensor(out=dg[sl][:], in0=identb[:].unsqueeze(1).to_broadcast([128, 4, 128]),
                                                                          in1=pv[:, PV_CW + oc:PV_CW + oc + 97:32].unsqueeze(2).to_broadcast([128, 4, 128]), op=ALU.mult),
                         reads=[R("identb"), R("pv")], writes=[R("dg%d" % sl, 0, 4)])
                    cbk = 4 + sl
                    for k in range(4):
                        P.op("pe", lambda e, sl=sl, k=k, cbk=cbk: e.matmul(pb[cbk][:, 0:TB], lhsT=dg[sl][:, k, :], rhs=xpre[sl][:, k:k + TB], start=(k == 0), stop=(k == 3)),
                             reads=[R("dg%d" % sl, k, k + 1), xpr], writes=[PR(cbk)])
                    P.op("act", lambda e, cbk=cbk, oc=oc: e.activation(out=xbc[:, oc, :], in_=pb[cbk][:, 0:TB], func=AF.Silu, bias=pv[:, PV_CB + oc:PV_CB + oc + 1], scale=1.0),
                         reads=[PR(cbk), R("pv")], writes=[R("xbc", oc, oc + 1)])
            if tb == nblk - 1:
                DMA("sp", o_cp, cst[:], "st_cp", reads=[R("cst", 0, 32)], writes=[R("o_cp")])
            for ti in range(4):
                for kc in range(8):
                    P.op("pe", lambda e, ti=ti, kc=kc: e.matmul(pb[6][:, ti * 32:(ti + 1) * 32], lhsT=hT[:, kc, ti * 128:(ti + 1) * 128], rhs=wdt[:, kc, :],
                                                                  start=(kc == 0), stop=(kc == 7)),
                         reads=[R("hT", kc, kc + 1), R("wdt")], writes=[PR(6, ti * 32, ti * 32 + 32)])
            P.op("dve", lambda e: e.tensor_tensor(out=dtx[:], in0=pb[6][:, 0:128].rearrange("p (a b) -> p a b", a=4),
                                                  in1=rowb[:, RB_DTB:RB_DTB + 32].unsqueeze(1).to_broadcast([128, 4, 32]), op=ALU.add),
                 reads=[PR(6, 0, 128), R("rowb")], writes=[R("dtx")])
            P.op("act", lambda e: e.activation(out=dtt[:], in_=dtx[:], func=AF.Abs), reads=[R("dtx")], writes=[R("dtt")])
            P.op("act", lambda e: e.activation(out=dtu[:], in_=dtt[:], func=AF.Exp, scale=-1.0), reads=[R("dtt")], writes=[R("dtu")])
            P.op("act", lambda e: e.activation(out=dtt[:], in_=dtu[:], func=AF.Ln, bias=1.0, scale=1.0), reads=[R("dtu")], writes=[R("dtt")])
            P.op("dve", lambda e: e.scalar_tensor_tensor(out=dt_tok[:], in0=dtx[:], scalar=0.0, in1=dtt[:], op0=ALU.max, op1=ALU.add),
                 reads=[R("dtx"), R("dtt")], writes=[R("dt_tok")])
            P.op("dve", lambda e: e.tensor_tensor(out=dA_bf[:], in0=dt_tok[:], in1=anegb[:].unsqueeze(1).to_broadcast([128, 4, 32]), op=ALU.mult),
                 reads=[R("dt_tok"), R("anegb")], writes=[R("dA_bf")])
            for t in range(2):
                wt, wr = load_w(w_in[:, :, 8224 + t * 512:8224 + (t + 1) * 512], 8, 512, tid=('in', 8224 + t * 512))
                for j in range(4):
                    oc = 4 * t + j
                    bk = next_bank()
                    proj_ws(wt, wr, j, 8, hT_rhs, hT_regs, bk)
                    P.op("act", lambda e, bk=bk, oc=oc: e.activation(out=gbT[:, oc, :], in_=pb[bk][:, 0:TB], func=AF.Sigmoid),
                         reads=[PR(bk)], writes=[R("gT", oc, oc + 1)])


            def ssd_acd(ci):
                tsl = slice(ci * 128, (ci + 1) * 128)
                dAc = dA_bf[:, ci, :]
                P.op("pe", lambda e: e.matmul(pb[7][:, 0:32], lhsT=triub[:], rhs=dAc, start=True, stop=True), reads=[R("triub"), R("dA_bf")], writes=[PR(7)])
                P.op("pe", lambda e: e.matmul(pb[7][:, 32:64], lhsT=onesb[:], rhs=dAc, start=True, stop=True), reads=[R("onesb"), R("dA_bf")], writes=[PR(7)])
                P.op("dve", lambda e: e.tensor_scalar(out=negAcs[:], in0=pb[7][:, 0:32], scalar1=-1.0, scalar2=None, op0=ALU.mult), reads=[PR(7)], writes=[R("negAcs")])
                P.op("act", lambda e: e.activation(out=Eacs[:], in_=pb[7][:, 0:32], func=AF.Exp), reads=[PR(7)], writes=[R("Eacs")])
                P.op("dve", lambda e: e.tensor_tensor(out=dte[:], in0=pb[7][:, 32:64], in1=negAcs[:], op=ALU.add), reads=[PR(7), R("negAcs")], writes=[R("dte")])
                P.op("act", lambda e: e.activation(out=dte[:], in_=dte[:], func=AF.Exp), reads=[R("dte")], writes=[R("dte")])
                P.op("act", lambda e: e.activation(out=cdB[:], in_=pb[7][:, 32:64], func=AF.Exp), reads=[PR(7)], writes=[R("cdB")])
                for g in range(NG):
                    P.op("pe", lambda e, g=g: e.transpose(out=PBF(2)[:, g * 128:(g + 1) * 128], in_=xbc[:, 16 + g, tsl], identity=identb[:]),
                         reads=[R("xbc", 16 + g, 17 + g), R("identb")], writes=[PR(2)])
                P.op("act", lambda e: e.activation(out=B_tok[:], in_=PBF(2)[:, 0:1024], func=AF.Copy), reads=[PR(2)], writes=[R("B_tok")])
                for g in range(NG):
                    bk = 3 + g // 4
                    c0 = (g % 4) * 128
                    P.op("pe", lambda e, g=g, bk=bk, c0=c0: e.matmul(pb[bk][:, c0:c0 + 128], lhsT=xbc[:, 16 + g, tsl], rhs=xbc[:, 24 + g, tsl], start=True, stop=True),
                         reads=[R("xbc", 16 + g, 17 + g), R("xbc", 24 + g, 25 + g)], writes=[PR(bk)])
                for q in range(2):
                    P.op("act", lambda e, q=q: e.activation(out=CBs[:, q * 4:(q + 1) * 4, :], in_=pb[3 + q][:, 0:512].rearrange("p (a b) -> p a b", a=4), func=AF.Copy),
                         reads=[PR(3 + q)], writes=[R("CBs", q * 4, q * 4 + 4)])

            def ssd_b(ci):
                tsl = slice(ci * 128, (ci + 1) * 128)
                xd = xdtb[ci % 2]
                xr = R("xdt%d" % (ci % 2))
                for fc in range(16):
                    bk = fc // 8
                    c0 = (fc % 8) * 128
                    P.op("pe", lambda e, bk=bk, c0=c0, fc=fc: e.transpose(out=PBF(bk)[:, c0:c0 + 128], in_=xbc[:, fc, tsl], identity=identb[:]),
                         reads=[R("xbc", fc, fc + 1), R("identb")], writes=[PR(bk)])
                for bk in range(2):
                    P.op("dve", lambda e, bk=bk: e.tensor_tensor(out=xd[:, bk * 1024:(bk + 1) * 1024].rearrange("p (h q) -> p h q", h=16),
                                                                 in0=PBF(bk)[:, 0:1024].rearrange("p (h q) -> p h q", h=16),
                                                                 in1=dt_tok[:, ci, bk * 16:(bk + 1) * 16].unsqueeze(2).to_broadcast([128, 16, HP]), op=ALU.mult),
                         reads=[PR(bk), R("dt_tok")], writes=[xr])
                P.op("dve", lambda e: e.tensor_tensor(out=xdte[:].rearrange("p (h q) -> p h q", h=NH), in0=xd[:].rearrange("p (h q) -> p h q", h=NH),
                                                      in1=dte[:].unsqueeze(2).to_broadcast([128, NH, HP]), op=ALU.mult),
                     reads=[xr, R("dte")], writes=[R("xdte", 0, 8)])

            def ssd_decay(ci, hq):
                sl = hq % 2
                dbanks = (5, 6) if hq % 2 == 0 else (3, 4)
                for half in range(2):
                    bk = dbanks[half]
                    for j in range(4):
                        h = hq * 8 + half * 4 + j
                        P.op("pe", lambda e, bk=bk, j=j: e.matmul(pb[bk][:, j * 128:(j + 1) * 128], lhsT=identb[:], rhs=nmb[:, 0, :], start=True, stop=False),
                             reads=[R("identb"), R("nmb")], writes=[PR(bk)])
                        P.op("pe", lambda e, bk=bk, j=j, h=h: e.matmul(pb[bk][:, j * 128:(j + 1) * 128], lhsT=dA_bf[:, ci, h:h + 1].to_broadcast([128, 128]),
                                                                        rhs=triub[:], start=False, stop=True),
                             reads=[R("dA_bf"), R("triub")], writes=[PR(bk)])
                    for j in range(4):
                        h = hq * 8 + half * 4 + j
                        P.op("act", lambda e, bk=bk, j=j, h=h, half=half: e.activation(out=dec[sl][:, half * 4 + j, :], in_=pb[bk][:, j * 128:(j + 1) * 128],
                                                                                       func=AF.Exp, bias=negAcs[:, h:h + 1], scale=1.0),
                             reads=[PR(bk), R("negAcs")], writes=[R("dec%d" % sl, half * 4 + j, half * 4 + j + 1)])

            def ssd_y(ci, hq):
                tsl = slice(ci * 128, (ci + 1) * 128)
                sl = hq % 2
                P.op("dve", lambda e: e.tensor_tensor(out=scr[sl][:].rearrange("p (g j) l -> p g j l", g=2), in0=dec[sl][:].rearrange("p (g j) l -> p g j l", g=2),
                                                      in1=CBs[:, 2 * hq:2 * hq + 2, :].unsqueeze(2).to_broadcast([128, 2, 4, 128]), op=ALU.mult),
                     reads=[R("dec%d" % sl, 0, 8), R("CBs", 2 * hq, 2 * hq + 2)], writes=[R("dec%d" % sl, 0, 8)])
                bA = (0, 2)[hq % 2]
                bB = (1, 7)[hq % 2]
                for jj in range(8):
                    h = 8 * hq + jj
                    P.op("pe", lambda e, jj=jj, h=h: e.matmul(pb[bA][:, jj * 64:(jj + 1) * 64], lhsT=xbc[:, h // 2, tsl], rhs=diagD[:, h // 2, (h % 2) * 64:(h % 2 + 1) * 64], start=True, stop=False),
                         reads=[R("xbc", h // 2, h // 2 + 1), R("diagD")], writes=[PR(bA)])
                    P.op("pe", lambda e, jj=jj, h=h: e.matmul(pb[bA][:, jj * 64:(jj + 1) * 64], lhsT=scr[sl][:, jj, :], rhs=xdtb[ci % 2][:, h * 64:(h + 1) * 64], start=False, stop=True),
                         reads=[R("dec%d" % sl, jj, jj + 1), R("xdt%d" % (ci % 2))], writes=[PR(bA)])
                for gg in range(2):
                    g = 2 * hq + gg
                    P.op("pe", lambda e, g=g, gg=gg: e.matmul(pb[bB][:, gg * 256:(gg + 1) * 256], lhsT=xbc[:, 24 + g, tsl], rhs=hstb[:, g * 256:(g + 1) * 256], start=True, stop=True),
                         reads=[R("xbc", 24 + g, 25 + g), R("hstb", g, g + 1)], writes=[PR(bB)])
                ysl = slice(hq * 512, (hq + 1) * 512)
                yr = R("ybuf", 2 * hq, 2 * hq + 2)
                P.op("dve", lambda e: e.tensor_tensor(out=ybuf[:, ysl].rearrange("p (h q) -> p h q", h=8), in0=pb[bB][:, 0:512].rearrange("p (h q) -> p h q", h=8),
                                                      in1=Eacs[:, 8 * hq:8 * hq + 8].unsqueeze(2).to_broadcast([128, 8, HP]), op=ALU.mult),
                     reads=[PR(bB), R("Eacs")], writes=[yr])
                P.op("dve", lambda e: e.tensor_tensor(out=ybuf[:, ysl], in0=pb[bA][:, 0:512], in1=ybuf[:, ysl], op=ALU.add), reads=[PR(bA), yr], writes=[yr])
                P.op("dve", lambda e: e.tensor_tensor(out=ybuf[:, ysl], in0=ybuf[:, ysl], in1=sz[:, ci, ysl], op=ALU.mult), reads=[yr, R(SN, ci, ci + 1)], writes=[yr])

            def ssd_g(ci):
                for q in range(4):
                    sbk = 3 + q
                    hsl = slice(q * 512, (q + 1) * 512)
                    hr = R("hst", 2 * q, 2 * q + 2)
                    for gg in range(2):
                        g = 2 * q + gg
                        P.op("pe", lambda e, g=g, gg=gg, sbk=sbk: e.matmul(pb[sbk][:, gg * 256:(gg + 1) * 256], lhsT=B_tok[:, g * 128:(g + 1) * 128], rhs=xdte[:, g * 256:(g + 1) * 256], start=True, stop=True),
                             reads=[R("B_tok"), R("xdte", g, g + 1)], writes=[PR(sbk)])
                    P.op("pool", lambda e, q=q, hsl=hsl: e.tensor_tensor(out=hst[:, hsl].rearrange("p (h q) -> p h q", h=8), in0=hst[:, hsl].rearrange("p (h q) -> p h q", h=8),
                                                                         in1=cdB[:, 8 * q:8 * q + 8].unsqueeze(2).to_broadcast([128, 8, HP]), op=ALU.mult),
                         reads=[hr, R("cdB")], writes=[hr])
                    P.op("dve", lambda e, sbk=sbk, hsl=hsl: e.tensor_tensor(out=hst[:, hsl], in0=pb[sbk][:, 0:512], in1=hst[:, hsl], op=ALU.add), reads=[PR(sbk), hr], writes=[hr])
                    P.op("act", lambda e, hsl=hsl: e.activation(out=hstb[:, hsl], in_=hst[:, hsl], func=AF.Copy), reads=[hr], writes=[R("hstb", 2 * q, 2 * q + 2)])

            def ssd_h(ci):
                tsl = slice(ci * 128, (ci + 1) * 128)
                ynb2 = xdtb[ci % 2]
                xr = R("xdt%d" % (ci % 2))
                P.op("act", lambda e: e.activation(out=ynb2[:], in_=ybuf[:], func=AF.Square, accum_out=ssq[:, 4:5]), reads=[R("ybuf", 0, 8)], writes=[xr, R("ssq", 4, 5)])
                P.op("act", lambda e: e.activation(out=ssq[:, 5:6], in_=ssq[:, 4:5], func=AF.Ln, scale=1.0 / DI, bias=epsc[:, 0:1]), reads=[R("ssq", 4, 5), R("epsc")], writes=[R("ssq", 5, 6)])
                P.op("act", lambda e: e.activation(out=ssq[:, 6:7], in_=ssq[:, 5:6], func=AF.Exp, scale=-0.5), reads=[R("ssq", 5, 6)], writes=[R("ssq", 6, 7)])
                P.op("dve", lambda e: e.scalar_tensor_tensor(out=ynb2[:], in0=ybuf[:], scalar=ssq[:, 6:7], in1=snwb[:], op0=ALU.mult, op1=ALU.mult),
                     reads=[R("ybuf", 0, 8), R("ssq", 6, 7), R("snwb")], writes=[xr])
                for fc in range(16):
                    bk = fc // 8
                    c0 = (fc % 8) * 128
                    P.op("pe", lambda e, bk=bk, c0=c0, fc=fc: e.transpose(out=PBF(bk)[:, c0:c0 + 128], in_=ynb2[:, fc * 128:(fc + 1) * 128], identity=identb[:]),
                         reads=[xr, R("identb")], writes=[PR(bk)])
                for q in range(2):
                    P.op("act", lambda e, q=q: e.activation(out=ynT[:, q * 8:(q + 1) * 8, tsl], in_=PBF(q)[:, 0:1024].rearrange("p (a b) -> p a b", a=8), func=AF.Copy),
                         reads=[PR(q)], writes=[R("ubuf", 0, 8)])

            ssd_acd(0)
            ssd_b(0)
            for ci in range(4):
                if ci == 0:
                    ssd_decay(ci, 0)
                for hq in range(4):
                    if hq + 1 < 4:
                        ssd_decay(ci, hq + 1)
                    ssd_y(ci, hq)
                ssd_g(ci)
                if ci + 1 < 4:
                    ssd_acd(ci + 1)
                    ssd_b(ci + 1)
                    ssd_decay(ci + 1, 0)
                ssd_h(ci)
            if tb == nblk - 1:
                DMA("sp", o_sp, hst[:], "st_sp", reads=[R("hst", 0, 8)], writes=[R("o_sp")])

            if tb + 1 < nblk:
                DMA("sp", bufX[(tb + 1) % 2][:], xp[(tb + 1) * TB:(tb + 2) * TB, :].rearrange("(t p) d -> p t d", p=128), "ld_x", writes=[R(SN, 0, 4)])
            for t in range(4):
                wt, wr = load_w(w_ssd[:, :, t * 256:(t + 1) * 256], 16, 256, tid=('ssd', t))
                for j in range(2):
                    oc = 2 * t + j
                    bk = next_bank()
                    proj_ws(wt, wr, j, 16, lambda kc: ynT[:, kc, :], lambda kc: [R("ubuf", 0, 8)], bk)
                    P.op("dve", lambda e, bk=bk, oc=oc: e.tensor_tensor(out=sA[:, 0:TB], in0=pb[bk][:, 0:TB], in1=gbT[:, oc, :], op=ALU.mult),
                         reads=[PR(bk), R("gT", oc, oc + 1)], writes=[R("sA")])
                    P.op("dve", lambda e, oc=oc: e.tensor_tensor(out=hT[:, oc, :], in0=sA[:, 0:TB], in1=amT[:, oc, :], op=ALU.add),
                         reads=[R("sA"), R("amT", oc, oc + 1)], writes=[R("hT", oc, oc + 1)])
            for t in range(2):
                wt, wr = load_w(w_out[:, :, t * 512:(t + 1) * 512], 8, 512, tid=('out', t))
                for ti in range(4):
                    bk = next_bank()
                    proj_as(wt, wr, ti, 8, hT_lhs, hT_regs, bk)
                    P.op("dve", lambda e, bk=bk, t=t: e.tensor_tensor(out=sB[:, 0:512], in0=pb[bk][:, 0:512], in1=g1B[:, t * 512:(t + 1) * 512], op=ALU.mult),
                         reads=[PR(bk), R("g1B", t, t + 1)], writes=[R("sB")])
                    P.op("dve", lambda e, ti=ti, t=t: e.tensor_tensor(out=xtok[:, ti, t * 512:(t + 1) * 512], in0=xtok[:, ti, t * 512:(t + 1) * 512], in1=sB[:, 0:512], op=ALU.add),
                         reads=[R("sB"), R(XN, ti, ti + 1)], writes=[R(XN, ti, ti + 1)])
            rmsnorm_to_T("a2T", a2T, 24, xtok, XN)
            for t in range(11):
                wt, wr = load_w(w_ffi[:, :, t * 512:(t + 1) * 512], 8, 512, tid=('ffi', t))
                for j in range(2):
                    fc = 2 * t + j
                    bg = next_bank()
                    proj_ws(wt, wr, j, 8, hT_rhs, hT_regs, bg)
                    bu = next_bank()
                    proj_ws(wt, wr, j, 8, hT_rhs, hT_regs, bu, col0=256)
                    P.op("act", lambda e, bg=bg: e.activation(out=sA[:, 0:TB], in_=pb[bg][:, 0:TB], func=AF.Silu), reads=[PR(bg)], writes=[R("sA")])
                    P.op("dve", lambda e, bu=bu, fc=fc: e.tensor_tensor(out=fT[:, fc, :], in0=pb[bu][:, 0:TB], in1=sA[:, 0:TB], op=ALU.mult),
                         reads=[PR(bu), R("sA")], writes=[R("xbc", fc, fc + 1)])
            fT_lhs = lambda kc, ti: fT[:, kc, ti * 128:(ti + 1) * 128]
            fT_regs = lambda kc: [R("xbc", kc, kc + 1)]
            for half in range(2):
                for kg in range(3):
                    nk = 8 if kg < 2 else 6
                    wt, wr = load_w(w_ffo[:, kg * 8:kg * 8 + nk, half * 512:(half + 1) * 512], nk, 512, tid=('ffo', kg, half))
                    for ti in range(4):
                        proj_as(wt, wr, ti, nk, fT_lhs, fT_regs, 4 + ti, kc0=kg * 8, first=(kg == 0), last=(kg == 2))
                for ti in range(4):
                    P.op("dve", lambda e, ti=ti, half=half: e.tensor_tensor(out=sB[:, 0:512], in0=pb[4 + ti][:, 0:512], in1=g2B[:, half * 512:(half + 1) * 512], op=ALU.mult),
                         reads=[PR(4 + ti), R("g2B", half, half + 1)], writes=[R("sB")])
                    P.op("dve", lambda e, ti=ti, half=half: e.tensor_tensor(out=xtok[:, ti, half * 512:(half + 1) * 512], in0=xtok[:, ti, half * 512:(half + 1) * 512], in1=sB[:, 0:512], op=ALU.add),
                         reads=[R("sB"), R(XN, ti, ti + 1)], writes=[R(XN, ti, ti + 1)])
            for ti in range(4):
                ys = 0
                P.op("act", lambda e, ti=ti: e.activation(out=junk[:, 0:D], in_=xtok[:, ti, :], func=AF.Square, accum_out=ssq[:, ti:ti + 1]),
                     reads=[R(XN, ti, ti + 1)], writes=[R("xdte", 0, 8), R("ssq", ti, ti + 1)])
                P.op("act", lambda e, ti=ti: e.activation(out=rsq[:, 4 + ti:5 + ti], in_=ssq[:, ti:ti + 1], func=AF.Ln, scale=1.0 / D, bias=epsc[:, 0:1]),
                     reads=[R("ssq", ti, ti + 1), R("epsc")], writes=[R("rsq", 4 + ti, 5 + ti)])
                P.op("act", lambda e, ti=ti: e.activation(out=rsq[:, ti:ti + 1], in_=rsq[:, 4 + ti:5 + ti], func=AF.Exp, scale=-0.5), reads=[R("rsq", 4 + ti, 5 + ti)], writes=[R("rsq", ti, ti + 1)])
                P.op("dve", lambda e, ti=ti, ys=ys: e.scalar_tensor_tensor(out=yst[ys][:], in0=xtok[:, ti, :], scalar=rsq[:, ti:ti + 1], in1=rowb[:, RB_FNW:RB_FNW + D], op0=ALU.mult, op1=ALU.mult),
                     reads=[R(XN, ti, ti + 1), R("rsq", ti, ti + 1), R("rowb")], writes=[R("yst%d" % ys)])
                r0 = tb * TB + ti * 128
                DMA("sp", o_y[r0:r0 + 128, :], yst[ys][:], "st_y%d" % ys, reads=[R("yst%d" % ys)], writes=[R("o_y", tb * 4 + ti, tb * 4 + ti + 1)])


        for tb in range(nblk):
            emit_block(tb, *role_views(tb % 2))
        xtok, XN, sz, SN, _pl = role_views((nblk - 1) % 2)

        if do_sample:
            Ssl = slice(1, 17)
            xbcF = xbc[:].rearrange("p a b -> p (a b)")
            stbuf = [xtok[:, 0:2, :].rearrange("p a b -> p (a b)"), xtok[:, 2:4, :].rearrange("p a b -> p (a b)")]
            streg = [R(XN, 0, 2), R(XN, 2, 4)]
            ubF = ubuf[:].rearrange("p a b -> p (a b)")
            stp = ubF[:, 0:1920].rearrange("p (c s r) -> p c s r", c=8, s=NS)
            newst = ubF[:, 1920:3840].rearrange("p (c s r) -> p c s r", c=8, s=NS)
            stc = ybuf[:, 0:1536].rearrange("p (c s r) -> p c s r", c=32, s=NS)
            newcst = hst[:, 0:1536].rearrange("p (c s r) -> p c s r", c=32, s=NS)
            gTf = gT[:].rearrange("p a b -> p (a b)").bitcast(F32)
            projS = gTf[:, 0:56 * NS].rearrange("p (c s) -> p c s", c=56)
            amF = amT[:].rearrange("p a b -> p (a b)").bitcast(F32)
            acc1 = amF[:, 0:512].rearrange("p (c s) -> p c s", c=32)
            acc2 = amF[:, 512:1024].rearrange("p (c s) -> p c s", c=32)
            amS = amF[:, 1024:1152].rearrange("p (c s) -> p c s", c=8)
            gaS = amF[:, 1152:1280].rearrange("p (c s) -> p c s", c=8)
            gbS = amF[:, 1280:1408].rearrange("p (c s) -> p c s", c=8)
            ptmp = amF[:, 1408:1536].rearrange("p (c s) -> p c s", c=8)
            sgS = amF[:, 1536:1568]
            xbcS = B_tok[:, 0:512].rearrange("p (c s) -> p c s", c=32)
            CBf = CBs[:].rearrange("p a b -> p (a b)")
            hTs = CBf[:, 0:128].rearrange("p (c s) -> p c s", c=8)
            mixTs = CBf[:, 128:256].rearrange("p (c s) -> p c s", c=8)
            pooledS = CBf[:, 256:384].rearrange("p (c s) -> p c s", c=8)
            ynTs = CBf[:, 384:640].rearrange("p (c s) -> p c s", c=16)
            fTs = CBf[:, 640:992].rearrange("p (c s) -> p c s", c=22)
            x_tokS = x_tok[0:NS, :]
            xdt_tokS = xdt[0:NS, :]
            szS = xdte[0:NS, :]
            xs_tok = yst[0][0:NS, :]
            szf = sz[:].rearrange("p a b -> p (a b)").bitcast(F32)
            ysS = szf[0:NS, 0:2048]
            gs1 = szf[0:NS, 2048:3072]
            gs2 = szf[0:NS, 3072:4096]
            xnS = dec[0][:].rearrange("p a b -> p (a b)")[0:NS, :]
            hTF = hT[:].rearrange("p a b -> p (a b)")
            ynS = hTF[0:NS, 0:2048]
            junkS = hTF[0:NS, 2048:4096]
            tmpDx = hTF[0:NS, :].bitcast(F32)
            decBs = sA[:, 0:512].rearrange("p (s h) -> p s h", s=NS)
            mask16 = sB[:, 0:256].rearrange("p (a b) -> p a b", a=NS)
            identfS = sB[:, 256:384]
            CmaskS = xbcF[:, 0:2048].rearrange("p (g s m) -> p g s m", g=NG, s=NS)
            ssS, rsS = ssq[0:NS, :], rsq[0:NS, :]
            RCB = R("CBs", 0, 8)
            RAM = R("amT", 0, 8)

            DMA("sp", xs_tok, xsm, "ld_xs", writes=[R("yst0")])
            DMA("sp", stp, st_pool, "ld_stp", writes=[R("ubuf", 0, 8)])
            DMA("sp", stc, st_conv, "ld_stc", writes=[R("ybuf", 0, 8)])
            P.op("pool", lambda e: e.memset(sB[:, 0:384], 0.0), writes=[R("sB")])
            P.op("pool", lambda e: e.affine_select(out=mask16, in_=mask16, pattern=[[1, NS], [-1, NS]], compare_op=ALU.not_equal, fill=1.0, base=0, channel_multiplier=0),
                 reads=[R("sB")], writes=[R("sB")])
            P.op("pool", lambda e: e.affine_select(out=identfS, in_=identfS, pattern=[[-1, 128]], compare_op=ALU.not_equal, fill=1.0, base=0, channel_multiplier=1),
                 reads=[R("sB")], writes=[R("sB")])
            for (gsv, c0, b0) in ((gs1, 16, 0), (gs2, 40, 2)):
                for c in range(8):
                    bk = b0 + c // 4
                    P.op("pe", lambda e, bk=bk, c=c, c0=c0: e.matmul(pb[bk][0:NS, (c % 4) * 128:(c % 4 + 1) * 128], lhsT=modT[:, c0 + c, Ssl], rhs=identfS, start=True, stop=True),
                         reads=[R("modT", c0 + c, c0 + c + 1), R("sB")], writes=[PR(bk)])
                for q in range(2):
                    P.op("act", lambda e, gsv=gsv, q=q, b0=b0: e.activation(out=gsv[:, q * 512:(q + 1) * 512], in_=pb[b0 + q][0:NS, 0:512], func=AF.Copy),
                         reads=[PR(b0 + q)], writes=[R(SN, 0, 4)])

            def s_norm_T(aT, bcol0):
                P.op("act", lambda e: e.activation(out=junkS[:, 0:D], in_=xs_tok, func=AF.Square, accum_out=ssS[:, 0:1]), reads=[R("yst0")], writes=[R("hT", 0, 8), R("ssq", 0, 8)])
                P.op("act", lambda e: e.activation(out=rsS[:, 4:5], in_=ssS[:, 0:1], func=AF.Ln, scale=1.0 / D, bias=epsc[0:NS, 0:1]), reads=[R("ssq", 0, 8), R("epsc")], writes=[R("rsq", 0, 8)])
                P.op("act", lambda e: e.activation(out=rsS[:, 0:1], in_=rsS[:, 4:5], func=AF.Exp, scale=-0.5), reads=[R("rsq", 0, 8)], writes=[R("rsq", 0, 8)])
                P.op("dve", lambda e: e.tensor_scalar(out=xnS, in0=xs_tok, scalar1=rsS[:, 0:1], scalar2=None, op0=ALU.mult), reads=[R("yst0"), R("rsq", 0, 8)], writes=[R("dec0", 0, 8)])
                for kc in range(8):
                    P.op("pe", lambda e, kc=kc: e.transpose(out=PBF(0)[:, kc * NS:(kc + 1) * NS], in_=xnS[:, kc * 128:(kc + 1) * 128], identity=identb[0:NS, 0:NS]),
                         reads=[R("dec0", 0, 8), R("identb")], writes=[PR(0)])
                P.op("dve", lambda e, aT=aT: e.tensor_tensor(out=ptmp, in0=PBF(0)[:, 0:128].rearrange("p (c s) -> p c s", c=8), in1=aT[:, :, Ssl], op=ALU.mult),
                     reads=[PR(0), R("a1T"), R("a2T")], writes=[RAM])
                P.op("dve", lambda e, bcol0=bcol0: e.tensor_tensor(out=hTs, in0=ptmp, in1=modT[:, bcol0:bcol0 + 8, Ssl], op=ALU.add),
                     reads=[RAM, R("modT", bcol0, bcol0 + 8)], writes=[RCB])

            s_norm_T(a1T, 0)
            ws_tiles = [(0, 0), (512, 4)] + [(3072 + 512 * t, 8 + 4 * t) for t in range(8)] + [(7200 + 512 * t, 40 + 4 * t) for t in range(4)]
            for (col0, cb0) in ws_tiles:
                wt, wr = load_w(w_in[:, :, col0:col0 + 512], 8, 512, tid=('in', col0))
                for j in range(4):
                    for kc in range(8):
                        P.op("pe", lambda e, j=j, kc=kc, wt=wt: e.matmul(pb[1][:, j * NS:(j + 1) * NS], lhsT=wt[:, kc, j * 128:(j + 1) * 128], rhs=hTs[:, kc, :], start=(kc == 0), stop=(kc == 7)),
                             reads=[wr, RCB], writes=[PR(1)])
                P.op("act", lambda e, cb0=cb0: e.activation(out=projS[:, cb0:cb0 + 4, :], in_=pb[1][:, 0:4 * NS].rearrange("p (c s) -> p c s", c=4), func=AF.Copy),
                     reads=[PR(1)], writes=[R("gT", 0, 8)])
            for t in range(4):
                wt, wr = load_w(w_in[:, :, 1024 + t * 512:1024 + (t + 1) * 512], 8, 512, tid=('in', 1024 + t * 512))
                for kc in range(8):
                    P.op("pe", lambda e, kc=kc, wt=wt: e.matmul(pb[2][0:NS, 0:512], lhsT=hTs[:, kc, :], rhs=wt[:, kc, :], start=(kc == 0), stop=(kc == 7)),
                         reads=[wr, RCB], writes=[PR(2)])
                P.op("act", lambda e, t=t: e.activation(out=szS[:, t * 512:(t + 1) * 512], in_=pb[2][0:NS, 0:512], func=AF.Silu), reads=[PR(2)], writes=[R("xdte", 0, 8)])
            for kc in range(8):
                P.op("pe", lambda e, kc=kc: e.matmul(pb[3][0:NS, 0:32], lhsT=hTs[:, kc, :], rhs=wdt[:, kc, :], start=(kc == 0), stop=(kc == 7)), reads=[RCB, R("wdt")], writes=[PR(3)])
            d_x, d_t, d_u, d_dt, d_dec = dtx[0:NS, 0, :], dtt[0:NS, 0, :], dtu[0:NS, 0, :], dt_tok[0:NS, 0, :], dtx[0:NS, 1, :]
            P.op("dve", lambda e: e.tensor_tensor(out=d_x, in0=pb[3][0:NS, 0:32], in1=rowb[0:NS, RB_DTB:RB_DTB + 32], op=ALU.add), reads=[PR(3), R("rowb")], writes=[R("dtx")])
            P.op("act", lambda e: e.activation(out=d_t, in_=d_x, func=AF.Abs), reads=[R("dtx")], writes=[R("dtt")])
            P.op("act", lambda e: e.activation(out=d_u, in_=d_t, func=AF.Exp, scale=-1.0), reads=[R("dtt")], writes=[R("dtu")])
            P.op("act", lambda e: e.activation(out=d_t, in_=d_u, func=AF.Ln, bias=1.0, scale=1.0), reads=[R("dtu")], writes=[R("dtt")])
            P.op("dve", lambda e: e.scalar_tensor_tensor(out=d_dt, in0=d_x, scalar=0.0, in1=d_t, op0=ALU.max, op1=ALU.add), reads=[R("dtx"), R("dtt")], writes=[R("dt_tok")])
            P.op("dve", lambda e: e.tensor_tensor(out=d_u, in0=d_dt, in1=anegb[0:NS, :], op=ALU.mult), reads=[R("dt_tok"), R("anegb")], writes=[R("dtu")])
            P.op("act", lambda e: e.activation(out=d_dec, in_=d_u, func=AF.Exp), reads=[R("dtu")], writes=[R("dtx")])
            for s_ in range(NS):
                P.op("pe", lambda e, s_=s_: e.matmul(pb[0][:, s_ * 32:(s_ + 1) * 32], lhsT=identfS[0:NS, s_:s_ + 1].to_broadcast([NS, 128]), rhs=d_dec, start=True, stop=True),
                     reads=[R("sB"), R("dtx")], writes=[PR(0)])
            P.op("act", lambda e: e.activation(out=sA[:, 0:512], in_=pb[0][:, 0:512], func=AF.Copy), reads=[PR(0)], writes=[R("sA")])
            for g in range(4):
                w = 2 ** (g + 1)
                ug = projS[:, 2 * g:2 * g + 2, :]
                P.op("dve", lambda e, g=g, w=w: e.reduce_sum(out=ptmp[:, 0:2, :], in_=stp[:, 2 * g:2 * g + 2, :, 15 - (w - 1):15], axis=mybir.AxisListType.X),
                     reads=[R("ubuf", 0, 8)], writes=[RAM])
                P.op("dve", lambda e, ug=ug: e.tensor_tensor(out=ptmp[:, 0:2, :], in0=ptmp[:, 0:2, :], in1=ug, op=ALU.add), reads=[RAM, R("gT", 0, 8)], writes=[RAM])
                P.op("dve", lambda e, ug=ug, g=g, w=w: e.scalar_tensor_tensor(out=pooledS[:, 2 * g:2 * g + 2, :], in0=ptmp[:, 0:2, :], scalar=1.0 / w, in1=ug, op0=ALU.mult, op1=ALU.subtract),
                     reads=[RAM, R("gT", 0, 8)], writes=[RCB])
            P.op("act", lambda e: e.activation(out=gaS, in_=projS[:, 40:48, :], func=AF.Sigmoid), reads=[R("gT", 0, 8)], writes=[RAM])
            P.op("act", lambda e: e.activation(out=gbS, in_=projS[:, 48:56, :], func=AF.Sigmoid), reads=[R("gT", 0, 8)], writes=[RAM])
            wplt, wplr = load_w(w_pool.rearrange("p g k c -> p (g k) c"), 8, 256, tid=('pool',))
            for g in range(4):
                for j in range(2):
                    oc = 2 * g + j
                    for k2 in range(2):
                        P.op("pe", lambda e, g=g, j=j, k2=k2, oc=oc: e.matmul(pb[2][:, oc * NS:(oc + 1) * NS], lhsT=wplt[:, 2 * g + k2, j * 128:(j + 1) * 128], rhs=pooledS[:, 2 * g + k2, :], start=(k2 == 0), stop=(k2 == 1)),
                             reads=[wplr, RCB], writes=[PR(2)])
            for oc in range(8):
                P.op("dve", lambda e, oc=oc: e.scalar_tensor_tensor(out=amS[:, oc, :], in0=pb[2][:, oc * NS:(oc + 1) * NS], scalar=pv[:, PV_PSC + oc:PV_PSC + oc + 1], in1=gaS[:, oc, :], op0=ALU.mult, op1=ALU.mult),
                     reads=[PR(2), R("pv"), RAM], writes=[RAM])
            P.op("pool", lambda e: e.tensor_copy(out=newst[:, :, :, 0:14], in_=stp[:, :, :, 1:15]), reads=[R("ubuf", 0, 8)], writes=[R("ubuf", 0, 8)])
            P.op("pool", lambda e: e.tensor_copy(out=newst[:, :, :, 14], in_=projS[:, 0:8, :]), reads=[R("gT", 0, 8)], writes=[R("ubuf", 0, 8)])
            DMA("sp", o_ps, newst, "st_ps", reads=[R("ubuf", 0, 8)], writes=[R("o_ps")])
            xnew = projS[:, 8:40, :]
            cwb = lambda k: pv[:, PV_CW + 32 * k:PV_CW + 32 * k + 32].unsqueeze(2).to_broadcast([128, 32, NS])
            P.op("dve", lambda e: e.tensor_tensor(out=acc1, in0=xnew, in1=cwb(3), op=ALU.mult), reads=[R("gT", 0, 8), R("pv")], writes=[RAM])
            for k in range(3):
                P.op("dve", lambda e, k=k: e.tensor_tensor(out=acc2, in0=stc[:, :, :, k], in1=cwb(k), op=ALU.mult), reads=[R("ybuf", 0, 8), R("pv")], writes=[RAM])
                P.op("dve", lambda e: e.tensor_tensor(out=acc1, in0=acc1, in1=acc2, op=ALU.add), reads=[RAM], writes=[RAM])
            P.op("dve", lambda e: e.tensor_tensor(out=acc1, in0=acc1, in1=pv[:, PV_CB:PV_CB + 32].unsqueeze(2).to_broadcast([128, 32, NS]), op=ALU.add), reads=[RAM, R("pv")], writes=[RAM])
            P.op("act", lambda e: e.activation(out=xbcS, in_=acc1, func=AF.Silu), reads=[RAM], writes=[R("B_tok")])
            P.op("pool", lambda e: e.tensor_copy(out=newcst[:, :, :, 0:2], in_=stc[:, :, :, 1:3]), reads=[R("ybuf", 0, 8)], writes=[R("hst", 0, 8)])
            P.op("pool", lambda e: e.tensor_copy(out=newcst[:, :, :, 2], in_=xnew), reads=[R("gT", 0, 8)], writes=[R("hst", 0, 8)])
            DMA("sp", o_cs, newcst, "st_cs", reads=[R("hst", 0, 8)], writes=[R("o_cs")])
            for fc in range(16):
                bk = 1 + fc // 8
                P.op("pe", lambda e, fc=fc, bk=bk: e.transpose(out=PBF(bk)[0:NS, (fc % 8) * 128:(fc % 8 + 1) * 128], in_=xbcS[:, fc, :], identity=identb[:]),
                     reads=[R("B_tok"), R("identb")], writes=[PR(bk)])
            for q in range(2):
                P.op("act", lambda e, q=q: e.activation(out=x_tokS[:, q * 1024:(q + 1) * 1024], in_=PBF(1 + q)[0:NS, 0:1024], func=AF.Copy), reads=[PR(1 + q)], writes=[R("xdt1")])
            P.op("dve", lambda e: e.tensor_tensor(out=xdt_tokS.rearrange("p (h q) -> p h q", h=NH), in0=x_tokS.rearrange("p (h q) -> p h q", h=NH),
                                                  in1=d_dt.unsqueeze(2).to_broadcast([NS, NH, HP]), op=ALU.mult), reads=[R("xdt1"), R("dt_tok")], writes=[R("xdt0")])
            P.op("dve", lambda e: e.tensor_tensor(out=CmaskS, in0=xbcS[:, 24:32, :].unsqueeze(3).to_broadcast([128, NG, NS, NS]),
                                                  in1=mask16.unsqueeze(1).to_broadcast([128, NG, NS, NS]), op=ALU.mult), reads=[R("B_tok"), R("sB")], writes=[R("xbc", 0, 32)])
            P.op("pool", lambda e: e.memset(ysS, 0.0), writes=[R(SN, 0, 4)])
            def samp_L(s_):
                sl = s_ % 2
                DMA("sp", stbuf[sl], st_ssm[s_], "ld_st%d" % sl, writes=[streg[sl]])

            def samp_A(s_):
                sl = s_ % 2
                buf = stbuf[sl]
                for q in range(4):
                    P.op("pe", lambda e, q=q: e.matmul(pb[q][:, 0:512], lhsT=identb[0:NS, s_:s_ + 1].to_broadcast([NS, 128]), rhs=xdt_tokS[:, q * 512:(q + 1) * 512], start=True, stop=True),
                         reads=[R("identb"), R("xdt0")], writes=[PR(q)])
                P.op("pool", lambda e: e.tensor_tensor(out=buf.rearrange("p (h q) -> p h q", h=NH), in0=buf.rearrange("p (h q) -> p h q", h=NH),
                                                       in1=decBs[:, s_, :].unsqueeze(2).to_broadcast([128, NH, HP]), op=ALU.mult), reads=[streg[sl], R("sA")], writes=[streg[sl]])
                for g in range(NG):
                    P.op("dve", lambda e, g=g: e.scalar_tensor_tensor(out=buf[:, g * 256:(g + 1) * 256], in0=pb[g // 2][:, (g % 2) * 256:(g % 2 + 1) * 256],
                                                                       scalar=xbcS[:, 16 + g, s_:s_ + 1], in1=buf[:, g * 256:(g + 1) * 256], op0=ALU.mult, op1=ALU.add),
                         reads=[PR(g // 2), R("B_tok"), streg[sl]], writes=[streg[sl]])
                P.op("act", lambda e: e.activation(out=hstb[:], in_=buf, func=AF.Copy), reads=[streg[sl]], writes=[R("hstb", 0, 8)])
                DMA("sp", o_ss[s_], buf, "st_ss%d" % sl, reads=[streg[sl]], writes=[R("o_ss", s_, s_ + 1)])

            def samp_A2(s_):
                for g in range(NG):
                    P.op("pe", lambda e, g=g: e.matmul(pb[4 + g // 2][0:NS, (g % 2) * 256:(g % 2 + 1) * 256], lhsT=CmaskS[:, g, s_, :], rhs=hstb[:, g * 256:(g + 1) * 256], start=True, stop=True),
                         reads=[R("xbc", 0, 32), R("hstb", 0, 8)], writes=[PR(4 + g // 2)])

            def samp_B(s_):
                for q in range(4):
                    P.op("dve", lambda e, q=q: e.tensor_tensor(out=ysS[:, q * 512:(q + 1) * 512], in0=pb[4 + q][0:NS, 0:512], in1=ysS[:, q * 512:(q + 1) * 512], op=ALU.add),
                         reads=[PR(4 + q), R(SN, 0, 4)], writes=[R(SN, 0, 4)])

            samp_L(0)
            samp_L(1)
            samp_A(0)
            samp_A2(0)
            for s_ in range(NS):
                if s_ + 2 < NS:
                    samp_L(s_ + 2)
                if s_ + 1 < NS:
                    samp_A(s_ + 1)
                samp_B(s_)
                if s_ + 1 < NS:
                    samp_A2(s_ + 1)
            P.op("dve", lambda e: e.tensor_tensor(out=tmpDx.rearrange("p (h q) -> p h q", h=NH), in0=x_tokS.rearrange("p (h q) -> p h q", h=NH),
                                                  in1=rowb[0:NS, RB_DSK:RB_DSK + 32].unsqueeze(2).to_broadcast([NS, NH, HP]), op=ALU.mult), reads=[R("xdt1"), R("rowb")], writes=[R("hT", 0, 8)])
            P.op("dve", lambda e: e.tensor_tensor(out=ysS, in0=ysS, in1=tmpDx, op=ALU.add), reads=[R(SN, 0, 4), R("hT", 0, 8)], writes=[R(SN, 0, 4)])
            P.op("dve", lambda e: e.tensor_tensor(out=ysS, in0=ysS, in1=szS, op=ALU.mult), reads=[R(SN, 0, 4), R("xdte", 0, 8)], writes=[R(SN, 0, 4)])
            P.op("act", lambda e: e.activation(out=junkS, in_=ysS, func=AF.Square, accum_out=ssS[:, 1:2]), reads=[R(SN, 0, 4)], writes=[R("hT", 0, 8), R("ssq", 0, 8)])
            P.op("act", lambda e: e.activation(out=rsS[:, 5:6], in_=ssS[:, 1:2], func=AF.Ln, scale=1.0 / DI, bias=epsc[0:NS, 0:1]), reads=[R("ssq", 0, 8), R("epsc")], writes=[R("rsq", 0, 8)])
            P.op("act", lambda e: e.activation(out=rsS[:, 1:2], in_=rsS[:, 5:6], func=AF.Exp, scale=-0.5), reads=[R("rsq", 0, 8)], writes=[R("rsq", 0, 8)])
            P.op("dve", lambda e: e.scalar_tensor_tensor(out=ynS, in0=ysS, scalar=rsS[:, 1:2], in1=snwb[0:NS, :], op0=ALU.mult, op1=ALU.mult),
                 reads=[R(SN, 0, 4), R("rsq", 0, 8), R("snwb")], writes=[R("hT", 0, 8)])
            for fc in range(16):
                P.op("pe", lambda e, fc=fc: e.transpose(out=PBF(0)[:, fc * NS:(fc + 1) * NS], in_=ynS[:, fc * 128:(fc + 1) * 128], identity=identb[0:NS, 0:NS]),
                     reads=[R("hT", 0, 8), R("identb")], writes=[PR(0)])
            P.op("act", lambda e: e.activation(out=ynTs, in_=PBF(0)[:, 0:256].rearrange("p (c s) -> p c s", c=16), func=AF.Copy), reads=[PR(0)], writes=[RCB])
            for t in range(4):
                wt, wr = load_w(w_ssd[:, :, t * 256:(t + 1) * 256], 16, 256, tid=('ssd', t))
                for j in range(2):
                    oc = 2 * t + j
                    for kc in range(16):
                        P.op("pe", lambda e, j=j, kc=kc, oc=oc, wt=wt: e.matmul(pb[1][:, oc * NS:(oc + 1) * NS], lhsT=wt[:, kc, j * 128:(j + 1) * 128], rhs=ynTs[:, kc, :], start=(kc == 0), stop=(kc == 15)),
                             reads=[wr, RCB], writes=[PR(1)])
            P.op("dve", lambda e: e.tensor_tensor(out=ptmp, in0=pb[1][:, 0:128].rearrange("p (c s) -> p c s", c=8), in1=gbS, op=ALU.mult), reads=[PR(1), RAM], writes=[RAM])
            P.op("dve", lambda e: e.tensor_tensor(out=mixTs, in0=ptmp, in1=amS, op=ALU.add), reads=[RAM], writes=[RCB])
            tmpR = ysS[:, 0:512]
            for t in range(2):
                wt, wr = load_w(w_out[:, :, t * 512:(t + 1) * 512], 8, 512, tid=('out', t))
                for kc in range(8):
                    P.op("pe", lambda e, kc=kc, wt=wt, t=t: e.matmul(pb[2 + t][0:NS, 0:512], lhsT=mixTs[:, kc, :], rhs=wt[:, kc, :], start=(kc == 0), stop=(kc == 7)), reads=[wr, RCB], writes=[PR(2 + t)])
                P.op("dve", lambda e, t=t: e.tensor_tensor(out=tmpR, in0=pb[2 + t][0:NS, 0:512], in1=gs1[:, t * 512:(t + 1) * 512], op=ALU.mult), reads=[PR(2 + t), R(SN, 0, 4)], writes=[R(SN, 0, 4)])
                P.op("dve", lambda e, t=t: e.tensor_tensor(out=xs_tok[:, t * 512:(t + 1) * 512], in0=xs_tok[:, t * 512:(t + 1) * 512], in1=tmpR, op=ALU.add), reads=[R(SN, 0, 4), R("yst0")], writes=[R("yst0")])
            s_norm_T(a2T, 24)
            for t in range(11):
                wt, wr = load_w(w_ffi[:, :, t * 512:(t + 1) * 512], 8, 512, tid=('ffi', t))
                for q in range(4):
                    for kc in range(8):
                        P.op("pe", lambda e, q=q, kc=kc, wt=wt: e.matmul(pb[1][:, q * NS:(q + 1) * NS], lhsT=wt[:, kc, q * 128:(q + 1) * 128], rhs=hTs[:, kc, :], start=(kc == 0), stop=(kc == 7)),
                             reads=[wr, RCB], writes=[PR(1)])
                P.op("act", lambda e: e.activation(out=sgS, in_=pb[1][:, 0:2 * NS], func=AF.Silu), reads=[PR(1)], writes=[RAM])
                P.op("dve", lambda e, t=t: e.tensor_tensor(out=fTs[:, 2 * t:2 * t + 2, :], in0=pb[1][:, 2 * NS:4 * NS].rearrange("p (c s) -> p c s", c=2), in1=sgS.rearrange("p (c s) -> p c s", c=2), op=ALU.mult),
                     reads=[PR(1), RAM], writes=[RCB])
            for half in range(2):
                for kg in range(3):
                    nk = 8 if kg < 2 else 6
                    wt, wr = load_w(w_ffo[:, kg * 8:kg * 8 + nk, half * 512:(half + 1) * 512], nk, 512, tid=('ffo', kg, half))
                    for kc in range(nk):
                        P.op("pe", lambda e, kc=kc, kg=kg, nk=nk, wt=wt, half=half: e.matmul(pb[2 + half][0:NS, 0:512], lhsT=fTs[:, kg * 8 + kc, :], rhs=wt[:, kc, :], start=(kg == 0 and kc == 0), stop=(kg == 2 and kc == nk - 1)),
                             reads=[wr, RCB], writes=[PR(2 + half)])
                P.op("dve", lambda e, half=half: e.tensor_tensor(out=tmpR, in0=pb[2 + half][0:NS, 0:512], in1=gs2[:, half * 512:(half + 1) * 512], op=ALU.mult), reads=[PR(2 + half), R(SN, 0, 4)], writes=[R(SN, 0, 4)])
                P.op("dve", lambda e, half=half: e.tensor_tensor(out=xs_tok[:, half * 512:(half + 1) * 512], in0=xs_tok[:, half * 512:(half + 1) * 512], in1=tmpR, op=ALU.add), reads=[R(SN, 0, 4), R("yst0")], writes=[R("yst0")])
            P.op("act", lambda e: e.activation(out=junkS[:, 0:D], in_=xs_tok, func=AF.Square, accum_out=ssS[:, 2:3]), reads=[R("yst0")], writes=[R("hT", 0, 8), R("ssq", 0, 8)])
            P.op("act", lambda e: e.activation(out=rsS[:, 6:7], in_=ssS[:, 2:3], func=AF.Ln, scale=1.0 / D, bias=epsc[0:NS, 0:1]), reads=[R("ssq", 0, 8), R("epsc")], writes=[R("rsq", 0, 8)])
            P.op("act", lambda e: e.activation(out=rsS[:, 2:3], in_=rsS[:, 6:7], func=AF.Exp, scale=-0.5), reads=[R("rsq", 0, 8)], writes=[R("rsq", 0, 8)])
            P.op("dve", lambda e: e.scalar_tensor_tensor(out=xs_tok, in0=xs_tok, scalar=rsS[:, 2:3], in1=rowb[0:NS, RB_FNW:RB_FNW + D], op0=ALU.mult, op1=ALU.mult),
                 reads=[R("yst0"), R("rsq", 0, 8), R("rowb")], writes=[R("yst0")])
            DMA("sp", o_ys, xs_tok, "st_ys", reads=[R("yst0")], writes=[R("o_ys")])

        P.op("sp", None, reads=[R("o_y", 0, 4 * NBLK), R("o_pp"), R("o_cp"), R("o_sp"), R("o_ys"), R("o_ps"), R("o_cs"), R("o_ss", 0, NS)])

        P.analyze()
        sems_e = {e: es.enter_context(nc.semaphore("se_" + e)) for e in ENGS}
        sems_d = {k: es.enter_context(nc.semaphore("sd_" + k)) for k in sorted(dma_keys)}
        P.emit(sems_e, sems_d)
    return nc


def _tile_k(w):
    K, N = w.shape
    return np.ascontiguousarray(w.reshape(K // 128, 128, N).transpose(1, 0, 2))


def _fm(v):
    return np.ascontiguousarray(v.reshape(-1, 128).T)


_NC_CACHE = {}


def kernel(x_prompt, x_sample, c_prompt, c_sample, state_pool, state_conv, state_ssm, w_ada, b_ada, norm1_w,
           w_in, w_pool, pool_scale, conv_w, conv_b, dt_bias, A_log, D_skip, ssd_norm_w, w_ssd_proj, w_out,
           norm2_w, w_ffn_in, w_ffn_out, final_norm_w):
    f = np.float32
    n = 8
    x_prompt = np.asarray(x_prompt, f)
    pvec = np.zeros((128, PV_N), f)
    pvec[:, PV_N1W:PV_N1W + 8] = _fm(np.asarray(norm1_w[0], f))
    pvec[:, PV_PSC:PV_PSC + 8] = _fm(np.asarray(pool_scale[0], f))
    cw = np.asarray(conv_w[0], f)
    for k in range(4):
        pvec[:, PV_CW + 32 * k:PV_CW + 32 * k + 32] = _fm(cw[k])
    pvec[:, PV_CB:PV_CB + 32] = _fm(np.asarray(conv_b[0], f))
    pvec[:, PV_N2W:PV_N2W + 8] = _fm(np.asarray(norm2_w[0], f))
    pvec[:, PV_BADA:PV_BADA + 48] = _fm(np.asarray(b_ada[0], f))
    pvec[:, PV_DF:PV_DF + 16] = _fm(np.repeat(np.asarray(D_skip[0], f), HP))
    rowb = np.zeros((128, RB_N), f)
    rowb[:, RB_FNW:RB_FNW + D] = np.asarray(final_norm_w, f)[None, :]
    bg = np.zeros((128, 2 * D), f)
    bg[:, 0:D] = np.asarray(b_ada[0], f)[None, 2 * D:3 * D]
    bg[:, D:2 * D] = np.asarray(b_ada[0], f)[None, 5 * D:6 * D]
    rowb[:, RB_DSK:RB_DSK + 32] = np.asarray(D_skip[0], f)[None, :]
    rowb[:, RB_ALOG:RB_ALOG + 32] = np.asarray(A_log[0], f)[None, :]
    rowb[:, RB_DTB:RB_DTB + 32] = np.asarray(dt_bias[0], f)[None, :]
    snw = np.ascontiguousarray(np.broadcast_to(np.asarray(ssd_norm_w[0], f)[None, :], (128, DI)))
    w_ada_t = _tile_k(np.asarray(w_ada[0], f))
    w_in_t = _tile_k(np.asarray(w_in[0], f))
    wp = np.asarray(w_pool[0], f)
    w_pool_t = np.ascontiguousarray(np.stack([_tile_k(wp[g]) for g in range(4)], axis=1))
    w_ssd_t = _tile_k(np.asarray(w_ssd_proj[0], f))
    w_out_t = _tile_k(np.asarray(w_out[0], f))
    wfi = np.asarray(w_ffn_in[0], f)
    perm = np.concatenate([np.concatenate([np.arange(256 * t, 256 * t + 256), DFF + np.arange(256 * t, 256 * t + 256)]) for t in range(11)])
    w_ffi_t = _tile_k(np.ascontiguousarray(wfi[:, perm]))
    w_ffo_t = _tile_k(np.asarray(w_ffn_out[0], f))

    in_maps = []
    for b in range(n):
        s0, s1 = NS * b, NS * (b + 1)
        c17 = np.concatenate([np.asarray(c_prompt[b:b + 1], f), np.asarray(c_sample[s0:s1], f)], axis=0)
        cT = np.ascontiguousarray(c17.T.reshape(8, 128, 17).transpose(1, 0, 2))
        sp = np.asarray(state_pool[0, s0:s1], f)
        sp_t = np.ascontiguousarray(sp.reshape(NS, 15, 8, 128).transpose(3, 2, 0, 1))
        sc = np.asarray(state_conv[0, s0:s1], f)
        sc_t = np.ascontiguousarray(sc.reshape(NS, 3, 32, 128).transpose(3, 2, 0, 1))
        ss = np.asarray(state_ssm[0, s0:s1], f)
        ss_t = np.ascontiguousarray(ss.reshape(NS, DI, DST).transpose(0, 2, 1))
        in_maps.append({
            "xp": np.ascontiguousarray(x_prompt[b]),
            "xsm": np.ascontiguousarray(np.asarray(x_sample[s0:s1, 0], f)),
            "cT": cT, "pvec": pvec, "rowb": rowb, "snw": snw, "bg": bg,
            "w_ada": w_ada_t, "w_in": w_in_t, "w_pool": w_pool_t, "w_ssd": w_ssd_t, "w_out": w_out_t,
            "w_ffi": w_ffi_t, "w_ffo": w_ffo_t,
            "st_pool": sp_t, "st_conv": sc_t, "st_ssm": ss_t,
        })
    nblk = int(os.environ.get("K_NBLK", NBLK))
    ncores = int(os.environ.get("K_CORES", n))
    if "nc" not in _NC_CACHE:
        _NC_CACHE["nc"] = build_nc(nblk=nblk)
    nc = _NC_CACHE["nc"]
    res = run_bass_kernel_spmd(nc, in_maps[:ncores], core_ids=list(range(ncores)))
    rs = list(res.results)
    while len(rs) < n:
        rs.append({k: np.zeros_like(v) for k, v in rs[0].items()})
    y_prompt = np.stack([rs[b]["o_y"] for b in range(n)], axis=0)
    y_sample = np.concatenate([rs[b]["o_ys"] for b in range(n)], axis=0)[:, None, :]
    pool_p = np.stack([rs[b]["o_pp"].transpose(2, 1, 0).reshape(15, D) for b in range(n)], axis=0)[None]
    conv_p = np.stack([rs[b]["o_cp"].transpose(2, 1, 0).reshape(3, CONV) for b in range(n)], axis=0)[None]
    ssm_p = np.stack([rs[b]["o_sp"].T.reshape(NH, HP, DST) for b in range(n)], axis=0)[None]
    pool_s = np.concatenate([rs[b]["o_ps"].transpose(2, 3, 1, 0).reshape(NS, 15, D) for b in range(n)], axis=0)[None]
    conv_s = np.concatenate([rs[b]["o_cs"].transpose(2, 3, 1, 0).reshape(NS, 3, CONV) for b in range(n)], axis=0)[None]
    ssm_s = np.concatenate([rs[b]["o_ss"].transpose(0, 2, 1).reshape(NS, NH, HP, DST) for b in range(n)], axis=0)[None]
    return (np.ascontiguousarray(y_prompt, dtype=f), np.ascontiguousarray(y_sample, dtype=f),
            np.ascontiguousarray(pool_p, dtype=f), np.ascontiguousarray(conv_p, dtype=f),
            np.ascontiguousarray(ssm_p, dtype=f), np.ascontiguousarray(pool_s, dtype=f),
            np.ascontiguousarray(conv_s, dtype=f), np.ascontiguousarray(ssm_s, dtype=f))
```

```python
import os
from contextlib import ExitStack

import numpy as np
import concourse.bass as bass
import concourse.mybir as mybir
from concourse.bass_utils import run_bass_kernel_spmd

F32 = mybir.dt.float32
BF16 = mybir.dt.bfloat16
AF = mybir.ActivationFunctionType
ALU = mybir.AluOpType

ENGS = ("pe", "act", "dve", "pool", "sp")

D = 1024
SEQ = 2048
TB = 512
NBLK = SEQ // TB
NS = 16
DI = 2048
NH = 32
HP = 64
NG = 8
DST = 128
CONV = 4096
DFF = 2816
IN_COLS = 9248
EPS = 1e-6

PV_N1W, PV_PSC, PV_CW, PV_CB, PV_N2W, PV_BADA, PV_DF, PV_N = 0, 8, 16, 144, 176, 184, 232, 248
RB_FNW, RB_DSK, RB_ALOG, RB_DTB, RB_N = 0, 1024, 1056, 1088, 1120


class Op:
    __slots__ = ("eng", "fn", "reads", "writes", "dkey", "dgroup", "idx", "waits",
                 "sig", "cnt", "eidx")

    def __init__(self, eng, fn, reads, writes, dkey=None, dgroup=None):
        self.eng = eng
        self.fn = fn
        self.reads = reads
        self.writes = writes
        self.dkey = dkey
        self.dgroup = dgroup
        self.waits = {}
        self.sig = False
        self.cnt = 0


class Prog:
    def __init__(self, nc):
        self.nc = nc
        self.ops = []

    def op(self, eng, fn, reads=(), writes=()):
        o = Op(eng, fn, list(reads), list(writes))
        o.idx = len(self.ops)
        self.ops.append(o)
        return o

    def dma(self, eng, fn, reads=(), writes=(), key=None, group=None):
        o = Op(eng, fn, list(reads), list(writes), dkey=key, dgroup=group)
        o.idx = len(self.ops)
        self.ops.append(o)
        return o

    def analyze(self):
        ops = self.ops
        ecount = {e: 0 for e in ENGS}
        for o in ops:
            o.eidx = ecount[o.eng]
            ecount[o.eng] += 1
        key_ops = {}
        for o in ops:
            if o.dkey is not None:
                key_ops.setdefault(o.dkey, []).append(o)
        dma_cum, dma_prev = {}, {}
        for k, lst in key_ops.items():
            groups = []
            for o in lst:
                if groups and o.dgroup is not None and groups[-1][0] == o.dgroup:
                    groups[-1][1].append(o)
                else:
                    groups.append((o.dgroup, [o]))
            cum = 0
            for g, gl in groups:
                prev = cum
                cum += len(gl)
                for o in gl:
                    dma_cum[o.idx] = cum
                    dma_prev[o.idx] = prev
        self.keys = sorted(key_ops.keys())
        recs = {}
        waited = {}
        pend = []
        for o in ops:
            d = set()
            for (buf, lo, hi) in o.reads:
                for r in recs.get(buf, ()):
                    if r[3] and r[0] < hi and lo < r[1]:
                        d.add(r[2])
            for (buf, lo, hi) in o.writes:
                for r in recs.get(buf, ()):
                    if r[0] < hi and lo < r[1]:
                        d.add(r[2])
            d.discard(o.idx)
            for (buf, lo, hi) in o.writes:
                lst = recs.setdefault(buf, [])
                lst[:] = [r for r in lst if not (lo <= r[0] and r[1] <= hi)]
                lst.append([lo, hi, o.idx, True])
            for (buf, lo, hi) in o.reads:
                lst = recs.setdefault(buf, [])
                lst[:] = [r for r in lst if not ((not r[3]) and ops[r[2]].eng == o.eng
                                                 and ops[r[2]].dkey is None and o.dkey is None
                                                 and lo <= r[0] and r[1] <= hi)]
                lst.append([lo, hi, o.idx, False])
            need = {}
            for di in d:
                p = ops[di]
                if p.dkey is not None:
                    sk = ("d", p.dkey)
                    val = 16 * dma_cum[p.idx]
                    if need.get(sk, 0) < val:
                        need[sk] = val
                    continue
                if p.eng == o.eng and o.dkey is None:
                    if o.eng in ("pe", "sp"):
                        continue
                sk = ("e", p.eng)
                cur = need.get(sk)
                if cur is None or cur.eidx < p.eidx:
                    need[sk] = p
            if o.dkey is not None and dma_prev[o.idx] > 0:
                sk = ("d", o.dkey)
                val = 16 * dma_prev[o.idx]
                if need.get(sk, 0) < val:
                    need[sk] = val
            o.waits = need
            for sk, v in need.items():
                if sk[0] == "e":
                    v.sig = True
        cnt = {e: 0 for e in ENGS}
        for o in ops:
            if o.dkey is None and o.sig:
                cnt[o.eng] += 1
            o.cnt = cnt[o.eng]
        for o in ops:
            final = {}
            for sk, v in o.waits.items():
                val = v.cnt if sk[0] == "e" else v
                wk = (o.eng, sk)
                if waited.get(wk, 0) >= val:
                    continue
                waited[wk] = val
                final[sk] = val
            o.waits = final

    def emit(self, sems_e, sems_d):
        nc = self.nc
        per = {e: [o for o in self.ops if o.eng == e] for e in ENGS}

        def run(engname, eng):
            for o in per[engname]:
                for sk, val in o.waits.items():
                    sem = sems_e[sk[1]] if sk[0] == "e" else sems_d[sk[1]]
                    eng.wait_ge(sem, val)
                if o.fn is None:
                    continue
                ins = o.fn(eng)
                if o.dkey is not None:
                    ins.then_inc(sems_d[o.dkey], 16)
                elif o.sig:
                    ins.then_inc(sems_e[o.eng], 1)

        with nc.Block() as block:
            @block.tensor
            def _(e):
                run("pe", e)

            @block.scalar
            def _(e):
                run("act", e)

            @block.vector
            def _(e):
                run("dve", e)

            @block.gpsimd
            def _(e):
                run("pool", e)

            @block.sync
            def _(e):
                run("sp", e)


def R(name, lo=0, hi=1):
    return (name, lo, hi)


def build_nc(nblk=NBLK, do_sample=True):
    nc = bass.Bass("TRN2", target_bir_lowering=False)

    def din(name, shape):
        return nc.dram_tensor(name, list(shape), F32, kind="ExternalInput").ap()

    def dout(name, shape):
        return nc.dram_tensor(name, list(shape), F32, kind="ExternalOutput").ap()

    xp = din("xp", [SEQ, D])
    xsm = din("xsm", [NS, D])
    cT = din("cT", [128, 8, 17])
    pvec = din("pvec", [128, PV_N])
    rowb_in = din("rowb", [128, RB_N])
    snw_in = din("snw", [128, DI])
    bg_in = din("bg", [128, 2 * D])
    w_ada = din("w_ada", [128, 8, 6 * D])
    w_in = din("w_in", [128, 8, IN_COLS])
    w_pool = din("w_pool", [128, 4, 2, 256])
    w_ssd = din("w_ssd", [128, 16, D])
    w_out = din("w_out", [128, 8, D])
    w_ffi = din("w_ffi", [128, 8, 2 * DFF])
    w_ffo = din("w_ffo", [128, 22, D])
    st_pool = din("st_pool", [128, 8, NS, 15])
    st_conv = din("st_conv", [128, 32, NS, 3])
    st_ssm = din("st_ssm", [NS, 128, DI])
    o_y = dout("o_y", [SEQ, D])
    o_ys = dout("o_ys", [NS, D])
    o_pp = dout("o_pp", [128, 8, 15])
    o_cp = dout("o_cp", [128, 32, 3])
    o_sp = dout("o_sp", [128, DI])
    o_ps = dout("o_ps", [128, 8, NS, 15])
    o_cs = dout("o_cs", [128, 32, NS, 3])
    o_ss = dout("o_ss", [NS, 128, DI])
    dbg_out = {}

    es = ExitStack()
    with es:
        def sb(name, shape, dt=F32):
            return es.enter_context(nc.sbuf_tensor("s_" + name, list(shape), dt))

        P = Prog(nc)
        dma_keys = set()

        def DMA(eng, out, in_, key, reads=(), writes=(), group=None):
            dma_keys.add(key)
            P.dma(eng, lambda e: e.dma_start(out=out, in_=in_), reads=reads, writes=writes, key=key, group=group)

        pb = [es.enter_context(nc.psum_tensor("pb%d" % i, [128, 512], F32)) for i in range(8)]

        def PBF(i):
            return pb[i][:].bitcast(BF16)

        def PR(i, lo=0, hi=512):
            return ("pb%d" % i, 0, 512)

        sA = sb("sA", [128, 16 + TB])
        sB = sb("sB", [128, 16 + TB])
        identf = sB[:, 0:128]
        triuf = sB[:, 128:256]
        nmf = sA[:, 0:512].rearrange("p (a b) -> p a b", a=4)
        identb = sb("identb", [128, 128], BF16)
        onesb = sb("onesb", [128, 128], BF16)
        triub = sb("triub", [128, 128], BF16)
        nmb = sb("nmb", [128, 4, 128], BF16)
        epsc = sb("epsc", [128, 1])
        invc = sb("invc", [128, 4, 16])
        pv = sb("pv", [128, PV_N])
        rowb = sb("rowb", [128, RB_N])
        snwb = sb("snwb", [128, DI], BF16)
        anegb = sb("anegb", [128, 32])
        cTs = sb("cTs", [128, 8, 17])
        silucT = sb("silucT", [128, 8, 17], BF16)
        modT = sb("modT", [128, 48, 17])
        a1T = sb("a1T", [128, 8, 17])
        a2T = sb("a2T", [128, 8, 17])
        g1B = sb("g1B", [128, D])
        g2B = sb("g2B", [128, D])
        wdt = sb("wdt", [128, 8, 32], BF16)
        diagD = sb("diagD", [128, 16, 128], BF16)
        NWB = 2
        wbuf = [sb("wbuf%d" % i, [128, 4096], BF16) for i in range(NWB)]

        P.op("pool", lambda e: e.memset(identf, 0.0), writes=[R("sB")])
        P.op("pool", lambda e: e.affine_select(out=identf, in_=identf, pattern=[[-1, 128]], compare_op=ALU.not_equal,
                                               fill=1.0, base=0, channel_multiplier=1), reads=[R("sB")], writes=[R("sB")])
        P.op("dve", lambda e: e.tensor_copy(out=identb[:], in_=identf), reads=[R("sB")], writes=[R("identb")])
        P.op("pool", lambda e: e.memset(onesb[:], 1.0), writes=[R("onesb")])
        P.op("pool", lambda e: e.memset(triuf, 1.0), writes=[R("sB")])
        P.op("pool", lambda e: e.affine_select(out=triuf, in_=triuf, pattern=[[1, 128]], compare_op=ALU.is_ge,
                                               fill=0.0, base=0, channel_multiplier=-1), reads=[R("sB")], writes=[R("sB")])
        P.op("dve", lambda e: e.tensor_copy(out=triub[:], in_=triuf), reads=[R("sB")], writes=[R("triub")])
        P.op("pool", lambda e: e.memset(nmf, 0.0), writes=[R("sA")])
        P.op("pool", lambda e: e.affine_select(out=nmf, in_=nmf, pattern=[[0, 4], [1, 128]], compare_op=ALU.is_ge,
                                               fill=-30000.0, base=0, channel_multiplier=-1), reads=[R("sA")], writes=[R("sA")])
        P.op("dve", lambda e: e.tensor_copy(out=nmb[:], in_=nmf), reads=[R("sA")], writes=[R("nmb")])
        P.op("pool", lambda e: e.memset(epsc[:], EPS), writes=[R("epsc")])
        for g in range(4):
            w = 2 ** (g + 1)
            P.op("pool", lambda e, g=g, w=w: e.memset(invc[:, g, :], 1.0 / w), writes=[R("invc")])
            for t in range(w - 1):
                P.op("pool", lambda e, g=g, t=t: e.memset(invc[:, g, t:t + 1], 1.0 / (t + 1)), writes=[R("invc")])

        DMA("sp", pv[:], pvec, "ld_pv", writes=[R("pv")])
        DMA("sp", rowb[:], rowb_in, "ld_rowb", writes=[R("rowb")])
        DMA("sp", cTs[:], cT, "ld_c", writes=[R("cTs")])
        DMA("sp", g1B[:], bg_in[:, 0:D], "ld_g1", writes=[R("g1B", 0, 2)])
        DMA("sp", g2B[:], bg_in[:, D:2 * D], "ld_g2", writes=[R("g2B", 0, 2)])
        DMA("pool", snwb[:], snw_in, "ld_snw", writes=[R("snwb")])
        DMA("pool", wdt[:], w_in[:, :, 7168:7200], "ld_wdt", writes=[R("wdt")])
        P.op("dve", lambda e: e.tensor_tensor(out=diagD[:], in0=identb[:].unsqueeze(1).to_broadcast([128, 16, 128]),
                                              in1=pv[:, PV_DF:PV_DF + 16].unsqueeze(2).to_broadcast([128, 16, 128]), op=ALU.mult),
             reads=[R("identb"), R("pv")], writes=[R("diagD")])

        P.op("act", lambda e: e.activation(out=anegb[:], in_=rowb[:, RB_ALOG:RB_ALOG + 32], func=AF.Exp), reads=[R("rowb")], writes=[R("anegb")])
        P.op("dve", lambda e: e.tensor_scalar(out=anegb[:], in0=anegb[:], scalar1=-1.0, scalar2=None, op0=ALU.mult), reads=[R("anegb")], writes=[R("anegb")])

        wstate = {"i": 0}

        NSCR = 42
        wsc = nc.dram_tensor("wsc", [NSCR, 128, 4096], BF16).ap()
        scr_idx = {}

        def load_w(src_ap, nk, ncol, tid=None):
            i = wstate["i"] % NWB
            wstate["i"] += 1
            wt = wbuf[i]
            view = wt[:, 0:nk * ncol].rearrange("p (k c) -> p k c", k=nk)
            if tid is None:
                DMA("pool", view, src_ap, "w%d" % i, writes=[R("wbuf%d" % i)])
            elif tid not in scr_idx:
                k = len(scr_idx)
                scr_idx[tid] = k
                DMA("pool", view, src_ap, "w%d" % i, writes=[R("wbuf%d" % i)])
                DMA("sp", wsc[k, :, 0:nk * ncol], wt[:, 0:nk * ncol], "ws%d" % i, reads=[R("wbuf%d" % i)], writes=[R("wsc", k, k + 1)])
            else:
                k = scr_idx[tid]
                DMA("sp", wt[:, 0:nk * ncol], wsc[k, :, 0:nk * ncol], "wh%d" % i, reads=[R("wsc", k, k + 1)], writes=[R("wbuf%d" % i)])
            return view, R("wbuf%d" % i)

        rot = {"i": 0}

        def next_bank(banks=(0, 1, 2, 3)):
            b = banks[rot["i"] % len(banks)]
            rot["i"] += 1
            return b

        P.op("act", lambda e: e.activation(out=silucT[:], in_=cTs[:], func=AF.Silu), reads=[R("cTs")], writes=[R("silucT")])
        for t in range(12):
            wt, wr = load_w(w_ada[:, :, t * 512:(t + 1) * 512], 8, 512)
            bk = 4 + (t % 2)
            for j in range(4):
                for kc in range(8):
                    P.op("pe", lambda e, bk=bk, j=j, kc=kc, wt=wt: e.matmul(pb[bk][:, j * 17:(j + 1) * 17], lhsT=wt[:, kc, j * 128:(j + 1) * 128],
                                                                             rhs=silucT[:, kc, :], start=(kc == 0), stop=(kc == 7)),
                         reads=[wr, R("silucT")], writes=[PR(bk, j * 17, j * 17 + 17)])
            P.op("dve", lambda e, bk=bk, t=t: e.tensor_tensor(out=modT[:, 4 * t:4 * t + 4, :], in0=pb[bk][:, 0:68].rearrange("p (a b) -> p a b", a=4),
                                                               in1=pv[:, PV_BADA + 4 * t:PV_BADA + 4 * t + 4].unsqueeze(2).to_broadcast([128, 4, 17]), op=ALU.add),
                 reads=[PR(bk, 0, 68), R("pv")], writes=[R("modT", 4 * t, 4 * t + 4)])
            if t in (4, 5, 10, 11):
                gB = g1B if t < 6 else g2B
                gname = "g1B" if t < 6 else "g2B"
                half = t % 2
                for kc in range(8):
                    P.op("pe", lambda e, kc=kc, wt=wt: e.matmul(pb[6][:, 0:512], lhsT=silucT[:, kc, 0:1].to_broadcast([128, 128]), rhs=wt[:, kc, :],
                                                                  start=(kc == 0), stop=(kc == 7)),
                         reads=[wr, R("silucT")], writes=[PR(6)])
                P.op("dve", lambda e, gB=gB, half=half: e.tensor_tensor(out=gB[:, half * 512:(half + 1) * 512], in0=pb[6][:, 0:512],
                                                                         in1=gB[:, half * 512:(half + 1) * 512], op=ALU.add),
                     reads=[PR(6), R(gname, half, half + 1)], writes=[R(gname, half, half + 1)])
        for (aT, an, sc0, nw0) in ((a1T, "a1T", 8, PV_N1W), (a2T, "a2T", 32, PV_N2W)):
            P.op("dve", lambda e, aT=aT, sc0=sc0: e.tensor_scalar(out=aT[:], in0=modT[:, sc0:sc0 + 8, :], scalar1=1.0, scalar2=None, op0=ALU.add),
                 reads=[R("modT", sc0, sc0 + 8)], writes=[R(an)])
            P.op("dve", lambda e, aT=aT, nw0=nw0: e.tensor_tensor(out=aT[:], in0=aT[:], in1=pv[:, nw0:nw0 + 8].unsqueeze(2).to_broadcast([128, 8, 17]), op=ALU.mult),
                 reads=[R(an), R("pv")], writes=[R(an)])

        xtok = sb("xtok", [128, 4, D])
        ssq = sb("ssq", [128, 8])
        rsq = sb("rsq", [128, 8])
        hT = sb("hT", [128, 8, TB], BF16)
        ubuf = sb("ubuf", [128, 8, 16 + TB])
        gT = sb("gT", [128, 8, TB], BF16)
        gaT = gT
        gbT = gT
        uhist = sb("uhist", [128, 8, 15])
        amT = sb("amT", [128, 8, TB], BF16)
        sz = sb("sz", [128, 4, DI], BF16)
        xpre = [sb("xpre%d" % i, [128, 3 + TB], BF16) for i in range(2)]
        dg = [sb("dg%d" % i, [128, 4, 128], BF16) for i in range(2)]
        chist = sb("chist", [128, 32, 3], BF16)
        cst = sb("cst", [128, 32, 3])
        xbc = sb("xbc", [128, 32, TB], BF16)
        dtx = sb("dtx", [128, 4, 32])
        dtt = sb("dtt", [128, 4, 32])
        dtu = sb("dtu", [128, 4, 32])
        dt_tok = sb("dt_tok", [128, 4, 32])
        dA_bf = sb("dA_bf", [128, 4, 32], BF16)
        negAcs = sb("negAcs", [128, 32])
        Eacs = sb("Eacs", [128, 32])
        dte = sb("dte", [128, 32])
        cdB = sb("cdB", [128, 32])
        xdtb = [sb("xdt%d" % i, [128, DI], BF16) for i in range(2)]
        x_tok = xdtb[1]
        xdt = xdtb[0]
        xdte = sb("xdte", [128, DI], BF16)
        B_tok = sb("B_tok", [128, 1024], BF16)
        CBs = sb("CBs", [128, 8, 128], BF16)
        dec = [sb("dec%d" % i, [128, 8, 128], BF16) for i in range(2)]
        scr = dec
        ybuf = sb("ybuf", [128, DI])
        ytmp = hT[:].rearrange("p a b -> p (a b)").bitcast(F32)
        xn = ybuf[:].bitcast(BF16).rearrange("p (a b) -> p a b", a=4)
        junk = xdte
        pooled = sz[:].rearrange("p a b -> p (a b)")[:, 0:8 * TB].rearrange("p (a b) -> p a b", a=8)
        hst = sb("hst", [128, DI])
        hstb = sb("hstb", [128, DI], BF16)
        yst = [sb("yst%d" % i, [128, D]) for i in range(1)]
        ynT = ubuf[:].rearrange("p a b -> p (a b)").bitcast(BF16)[:, 0:16 * TB].rearrange("p (a b) -> p a b", a=16)
        fT = xbc[:].rearrange("p a b -> p (a b)")[:, 0:22 * TB].rearrange("p (a b) -> p a b", a=22)

        P.op("pool", lambda e: e.memset(chist[:], 0.0), writes=[R("chist", 0, 32)])
        P.op("pool", lambda e: e.memset(hst[:], 0.0), writes=[R("hst", 0, 8)])
        P.op("pool", lambda e: e.memset(hstb[:], 0.0), writes=[R("hstb", 0, 8)])

        def rmsnorm_to_T(blk_name, aT, bcol0, ntok=128, ntile=4):
            for ti in range(ntile):
                P.op("act", lambda e, ti=ti: e.activation(out=junk[:, 0:D], in_=xtok[:, ti, :], func=AF.Square, accum_out=ssq[:, ti:ti + 1]),
                     reads=[R("xtok", ti, ti + 1)], writes=[R("xdte", 0, 8), R("ssq", ti, ti + 1)])
                P.op("act", lambda e, ti=ti: e.activation(out=rsq[:, 4 + ti:5 + ti], in_=ssq[:, ti:ti + 1], func=AF.Ln, scale=1.0 / D, bias=epsc[:, 0:1]),
                     reads=[R("ssq", ti, ti + 1), R("epsc")], writes=[R("rsq", 4 + ti, 5 + ti)])
                P.op("act", lambda e, ti=ti: e.activation(out=rsq[:, ti:ti + 1], in_=rsq[:, 4 + ti:5 + ti], func=AF.Exp, scale=-0.5),
                     reads=[R("rsq", 4 + ti, 5 + ti)], writes=[R("rsq", ti, ti + 1)])
                P.op("dve", lambda e, ti=ti: e.tensor_scalar(out=xn[:, ti, :], in0=xtok[:, ti, :], scalar1=rsq[:, ti:ti + 1], scalar2=None, op0=ALU.mult),
                     reads=[R("xtok", ti, ti + 1), R("rsq", ti, ti + 1)], writes=[R("ybuf", 2 * ti, 2 * ti + 2)])
            for kc in range(8):
                bk = kc // 2
                c0 = (kc % 2) * 512
                for ti in range(ntile):
                    P.op("pe", lambda e, bk=bk, c0=c0, ti=ti, kc=kc: e.transpose(out=PBF(bk)[:, c0 + ti * 128:c0 + (ti + 1) * 128],
                                                                                  in_=xn[:, ti, kc * 128:(kc + 1) * 128], identity=identb[:]),
                         reads=[R("ybuf", 2 * ti, 2 * ti + 2), R("identb")], writes=[PR(bk, c0 // 2 + ti * 64, c0 // 2 + (ti + 1) * 64)])
                P.op("dve", lambda e, bk=bk, c0=c0, kc=kc, aT=aT, bcol0=bcol0: e.tensor_scalar(
                    out=hT[:, kc, :], in0=PBF(bk)[:, c0:c0 + 512], scalar1=aT[:, kc, 0:1], scalar2=modT[:, bcol0 + kc, 0:1], op0=ALU.mult, op1=ALU.add),
                    reads=[PR(bk, c0 // 2, c0 // 2 + 256), R(blk_name), R("modT", bcol0 + kc, bcol0 + kc + 1)], writes=[R("hT", kc, kc + 1)])

        def proj_ws(wt, wr, j, nk, rhs_fn, rhs_regs, bank, ncols=TB, col0=0):
            for kc in range(nk):
                P.op("pe", lambda e, kc=kc: e.matmul(pb[bank][:, 0:ncols], lhsT=wt[:, kc, col0 + j * 128:col0 + (j + 1) * 128], rhs=rhs_fn(kc),
                                                      start=(kc == 0), stop=(kc == nk - 1)),
                     reads=[wr] + rhs_regs(kc), writes=[PR(bank, 0, ncols)])

        def proj_as(wt, wr, ti, nk, lhs_fn, lhs_regs, bank, kc0=0, first=True, last=True, nktot=None):
            for kc in range(nk):
                P.op("pe", lambda e, kc=kc: e.matmul(pb[bank][:, 0:512], lhsT=lhs_fn(kc0 + kc, ti), rhs=wt[:, kc, :],
                                                      start=(first and kc == 0), stop=(last and kc == nk - 1)),
                     reads=[wr] + lhs_regs(kc0 + kc), writes=[PR(bank)])

        hT_rhs = lambda kc: hT[:, kc, :]
        hT_regs = lambda kc: [R("hT", kc, kc + 1)]
        hT_lhs = lambda kc, ti: hT[:, kc, ti * 128:(ti + 1) * 128]

        for tb in range(nblk):
            DMA("sp", xtok[:], xp[tb * TB:(tb + 1) * TB, :].rearrange("(t p) d -> p t d", p=128), "ld_x", writes=[R("xtok", 0, 4)])
            rmsnorm_to_T("a1T", a1T, 0)

            for t in range(2):
                wt, wr = load_w(w_in[:, :, 7200 + t * 512:7200 + (t + 1) * 512], 8, 512, tid=('in', 7200 + t * 512))
                for j in range(4):
                    oc = 4 * t + j
                    bk = next_bank()
                    proj_ws(wt, wr, j, 8, hT_rhs, hT_regs, bk)
                    P.op("act", lambda e, bk=bk, oc=oc: e.activation(out=gaT[:, oc, :], in_=pb[bk][:, 0:TB], func=AF.Sigmoid),
                         reads=[PR(bk)], writes=[R("gT", oc, oc + 1)])
            if tb == 0:
                P.op("pool", lambda e: e.memset(ubuf[:, :, 0:16], 0.0), writes=[R("ubuf", 0, 8)])
            else:
                P.op("pool", lambda e: e.tensor_copy(out=ubuf[:, :, 1:16], in_=uhist[:]), reads=[R("uhist")], writes=[R("ubuf", 0, 8)])
            for t in range(2):
                wt, wr = load_w(w_in[:, :, t * 512:(t + 1) * 512], 8, 512, tid=('in', t * 512))
                for j in range(4):
                    oc = 4 * t + j
                    bk = next_bank()
                    proj_ws(wt, wr, j, 8, hT_rhs, hT_regs, bk)
                    P.op("act", lambda e, bk=bk, oc=oc: e.activation(out=ubuf[:, oc, 16:16 + TB], in_=pb[bk][:, 0:TB], func=AF.Copy),
                         reads=[PR(bk)], writes=[R("ubuf", oc, oc + 1)])
            for oc in range(8):
                g = oc // 2
                w = 2 ** (g + 1)
                U = ubuf[:, oc, :]
                ur = R("ubuf", oc, oc + 1)
                P.op("dve", lambda e, U=U: e.tensor_tensor(out=sA[:, 2:528], in0=U[:, 2:528], in1=U[:, 1:527], op=ALU.add), reads=[ur], writes=[R("sA")])
                cur, curname = sA, "sA"
                if g >= 1:
                    P.op("dve", lambda e: e.tensor_tensor(out=sB[:, 4:528], in0=sA[:, 4:528], in1=sA[:, 2:526], op=ALU.add), reads=[R("sA")], writes=[R("sB")])
                    cur, curname = sB, "sB"
                if g >= 2:
                    P.op("dve", lambda e: e.tensor_tensor(out=sA[:, 8:528], in0=sB[:, 8:528], in1=sB[:, 4:524], op=ALU.add), reads=[R("sB")], writes=[R("sA")])
                    cur, curname = sA, "sA"
                if g >= 3:
                    P.op("dve", lambda e: e.tensor_tensor(out=sB[:, 16:528], in0=sA[:, 16:528], in1=sA[:, 8:520], op=ALU.add), reads=[R("sA")], writes=[R("sB")])
                    cur, curname = sB, "sB"
                P.op("dve", lambda e, cur=cur, U=U, w=w, oc=oc: e.scalar_tensor_tensor(out=pooled[:, oc, :], in0=cur[:, 16:528], scalar=1.0 / w, in1=U[:, 16:528],
                                                                                        op0=ALU.mult, op1=ALU.subtract),
                     reads=[R(curname), ur], writes=[R("sz", 0, 2)])
                if tb == 0:
                    P.op("dve", lambda e, cur=cur, g=g: e.tensor_tensor(out=cur[:, 0:16], in0=cur[:, 16:32], in1=invc[:, g, :], op=ALU.mult),
                         reads=[R(curname), R("invc")], writes=[R(curname)])
                    P.op("dve", lambda e, cur=cur, U=U, oc=oc: e.tensor_tensor(out=pooled[:, oc, 0:16], in0=cur[:, 0:16], in1=U[:, 16:32], op=ALU.subtract),
                         reads=[R(curname), ur], writes=[R("sz", 0, 2)])
            wplt, wplr = load_w(w_pool.rearrange("p g k c -> p (g k) c"), 8, 256, tid=('pool',))
            for g in range(4):
                for j in range(2):
                    oc = 2 * g + j
                    bk = next_bank()
                    for k2 in range(2):
                        P.op("pe", lambda e, bk=bk, g=g, j=j, k2=k2: e.matmul(pb[bk][:, 0:TB], lhsT=wplt[:, 2 * g + k2, j * 128:(j + 1) * 128], rhs=pooled[:, 2 * g + k2, :],
                                                                                start=(k2 == 0), stop=(k2 == 1)),
                             reads=[wplr, R("sz", 0, 2)], writes=[PR(bk)])
                    P.op("dve", lambda e, bk=bk, oc=oc: e.scalar_tensor_tensor(out=amT[:, oc, :], in0=pb[bk][:, 0:TB], scalar=pv[:, PV_PSC + oc:PV_PSC + oc + 1],
                                                                                in1=gaT[:, oc, :], op0=ALU.mult, op1=ALU.mult),
                         reads=[PR(bk), R("pv"), R("gT", oc, oc + 1)], writes=[R("amT", oc, oc + 1)])
            if tb == nblk - 1:
                DMA("sp", o_pp, ubuf[:, :, 513:528], "st_pp", reads=[R("ubuf", 0, 8)], writes=[R("o_pp")])
            else:
                P.op("pool", lambda e: e.tensor_copy(out=uhist[:], in_=ubuf[:, :, 513:528]), reads=[R("ubuf", 0, 8)], writes=[R("uhist")])

            for t in range(4):
                wt, wr = load_w(w_in[:, :, 1024 + t * 512:1024 + (t + 1) * 512], 8, 512, tid=('in', 1024 + t * 512))
                for ti in range(4):
                    bk = next_bank()
                    proj_as(wt, wr, ti, 8, hT_lhs, hT_regs, bk)
                    P.op("act", lambda e, bk=bk, ti=ti, t=t: e.activation(out=sz[:, ti, t * 512:(t + 1) * 512], in_=pb[bk][:, 0:512], func=AF.Silu),
                         reads=[PR(bk)], writes=[R("sz", ti, ti + 1)])
            for t in range(8):
                wt, wr = load_w(w_in[:, :, 3072 + t * 512:3072 + (t + 1) * 512], 8, 512, tid=('in', 3072 + t * 512))
                for j in range(4):
                    oc = 4 * t + j
                    sl = oc % 2
                    bk = next_bank()
                    proj_ws(wt, wr, j, 8, hT_rhs, hT_regs, bk)
                    xpr = R("xpre%d" % sl)
                    P.op("dve", lambda e, sl=sl, oc=oc: e.tensor_copy(out=xpre[sl][:, 0:3], in_=chist[:, oc, :]), reads=[R("chist", oc, oc + 1)], writes=[xpr])
                    P.op("act", lambda e, sl=sl, bk=bk: e.activation(out=xpre[sl][:, 3:3 + TB], in_=pb[bk][:, 0:TB], func=AF.Copy), reads=[PR(bk)], writes=[xpr])
                    if tb == nblk - 1:
                        P.op("dve", lambda e, bk=bk, oc=oc: e.tensor_copy(out=cst[:, oc, :], in_=pb[bk][:, TB - 3:TB]), reads=[PR(bk)], writes=[R("cst", oc, oc + 1)])
                    else:
                        P.op("dve", lambda e, sl=sl, oc=oc: e.tensor_copy(out=chist[:, oc, :], in_=xpre[sl][:, TB:TB + 3]), reads=[xpr], writes=[R("chist", oc, oc + 1)])
                    P.op("pool", lambda e, sl=sl, oc=oc: e.tensor_tensor(out=dg[sl][:], in0=identb[:].unsqueeze(1).to_broadcast([128, 4, 128]),
                                                                          in1=pv[:, PV_CW + oc:PV_CW + oc + 97:32].unsqueeze(2).to_broadcast([128, 4, 128]), op=ALU.mult),
                         reads=[R("identb"), R("pv")], writes=[R("dg%d" % sl, 0, 4)])
                    cbk = 4 + sl
                    for k in range(4):
                        P.op("pe", lambda e, sl=sl, k=k, cbk=cbk: e.matmul(pb[cbk][:, 0:TB], lhsT=dg[sl][:, k, :], rhs=xpre[sl][:, k:k + TB], start=(k == 0), stop=(k == 3)),
                             reads=[R("dg%d" % sl, k, k + 1), xpr], writes=[PR(cbk)])
                    P.op("act", lambda e, cbk=cbk, oc=oc: e.activation(out=xbc[:, oc, :], in_=pb[cbk][:, 0:TB], func=AF.Silu, bias=pv[:, PV_CB + oc:PV_CB + oc + 1], scale=1.0),
                         reads=[PR(cbk), R("pv")], writes=[R("xbc", oc, oc + 1)])
            if tb == nblk - 1:
                DMA("sp", o_cp, cst[:], "st_cp", reads=[R("cst", 0, 32)], writes=[R("o_cp")])
            for ti in range(4):
                for kc in range(8):
                    P.op("pe", lambda e, ti=ti, kc=kc: e.matmul(pb[6][:, ti * 32:(ti + 1) * 32], lhsT=hT[:, kc, ti * 128:(ti + 1) * 128], rhs=wdt[:, kc, :],
                                                                  start=(kc == 0), stop=(kc == 7)),
                         reads=[R("hT", kc, kc + 1), R("wdt")], writes=[PR(6, ti * 32, ti * 32 + 32)])
            P.op("dve", lambda e: e.tensor_tensor(out=dtx[:], in0=pb[6][:, 0:128].rearrange("p (a b) -> p a b", a=4),
                                                  in1=rowb[:, RB_DTB:RB_DTB + 32].unsqueeze(1).to_broadcast([128, 4, 32]), op=ALU.add),
                 reads=[PR(6, 0, 128), R("rowb")], writes=[R("dtx")])
            P.op("act", lambda e: e.activation(out=dtt[:], in_=dtx[:], func=AF.Abs), reads=[R("dtx")], writes=[R("dtt")])
            P.op("act", lambda e: e.activation(out=dtu[:], in_=dtt[:], func=AF.Exp, scale=-1.0), reads=[R("dtt")], writes=[R("dtu")])
            P.op("act", lambda e: e.activation(out=dtt[:], in_=dtu[:], func=AF.Ln, bias=1.0, scale=1.0), reads=[R("dtu")], writes=[R("dtt")])
            P.op("dve", lambda e: e.scalar_tensor_tensor(out=dt_tok[:], in0=dtx[:], scalar=0.0, in1=dtt[:], op0=ALU.max, op1=ALU.add),
                 reads=[R("dtx"), R("dtt")], writes=[R("dt_tok")])
            P.op("dve", lambda e: e.tensor_tensor(out=dA_bf[:], in0=dt_tok[:], in1=anegb[:].unsqueeze(1).to_broadcast([128, 4, 32]), op=ALU.mult),
                 reads=[R("dt_tok"), R("anegb")], writes=[R("dA_bf")])
            for t in range(2):
                wt, wr = load_w(w_in[:, :, 8224 + t * 512:8224 + (t + 1) * 512], 8, 512, tid=('in', 8224 + t * 512))
                for j in range(4):
                    oc = 4 * t + j
                    bk = next_bank()
                    proj_ws(wt, wr, j, 8, hT_rhs, hT_regs, bk)
                    P.op("act", lambda e, bk=bk, oc=oc: e.activation(out=gbT[:, oc, :], in_=pb[bk][:, 0:TB], func=AF.Sigmoid),
                         reads=[PR(bk)], writes=[R("gT", oc, oc + 1)])


            def ssd_acd(ci):
                tsl = slice(ci * 128, (ci + 1) * 128)
                dAc = dA_bf[:, ci, :]
                P.op("pe", lambda e: e.matmul(pb[7][:, 0:32], lhsT=triub[:], rhs=dAc, start=True, stop=True), reads=[R("triub"), R("dA_bf")], writes=[PR(7)])
                P.op("pe", lambda e: e.matmul(pb[7][:, 32:64], lhsT=onesb[:], rhs=dAc, start=True, stop=True), reads=[R("onesb"), R("dA_bf")], writes=[PR(7)])
                P.op("dve", lambda e: e.tensor_scalar(out=negAcs[:], in0=pb[7][:, 0:32], scalar1=-1.0, scalar2=None, op0=ALU.mult), reads=[PR(7)], writes=[R("negAcs")])
                P.op("act", lambda e: e.activation(out=Eacs[:], in_=pb[7][:, 0:32], func=AF.Exp), reads=[PR(7)], writes=[R("Eacs")])
                P.op("dve", lambda e: e.tensor_tensor(out=dte[:], in0=pb[7][:, 32:64], in1=negAcs[:], op=ALU.add), reads=[PR(7), R("negAcs")], writes=[R("dte")])
                P.op("act", lambda e: e.activation(out=dte[:], in_=dte[:], func=AF.Exp), reads=[R("dte")], writes=[R("dte")])
                P.op("act", lambda e: e.activation(out=cdB[:], in_=pb[7][:, 32:64], func=AF.Exp), reads=[PR(7)], writes=[R("cdB")])
                for g in range(NG):
                    P.op("pe", lambda e, g=g: e.transpose(out=PBF(2)[:, g * 128:(g + 1) * 128], in_=xbc[:, 16 + g, tsl], identity=identb[:]),
                         reads=[R("xbc", 16 + g, 17 + g), R("identb")], writes=[PR(2)])
                P.op("act", lambda e: e.activation(out=B_tok[:], in_=PBF(2)[:, 0:1024], func=AF.Copy), reads=[PR(2)], writes=[R("B_tok")])
                for g in range(NG):
                    bk = 3 + g // 4
                    c0 = (g % 4) * 128
                    P.op("pe", lambda e, g=g, bk=bk, c0=c0: e.matmul(pb[bk][:, c0:c0 + 128], lhsT=xbc[:, 16 + g, tsl], rhs=xbc[:, 24 + g, tsl], start=True, stop=True),
                         reads=[R("xbc", 16 + g, 17 + g), R("xbc", 24 + g, 25 + g)], writes=[PR(bk)])
                for q in range(2):
                    P.op("act", lambda e, q=q: e.activation(out=CBs[:, q * 4:(q + 1) * 4, :], in_=pb[3 + q][:, 0:512].rearrange("p (a b) -> p a b", a=4), func=AF.Copy),
                         reads=[PR(3 + q)], writes=[R("CBs", q * 4, q * 4 + 4)])

            def ssd_b(ci):
                tsl = slice(ci * 128, (ci + 1) * 128)
                xd = xdtb[ci % 2]
                xr = R("xdt%d" % (ci % 2))
                for fc in range(16):
                    bk = fc // 8
                    c0 = (fc % 8) * 128
                    P.op("pe", lambda e, bk=bk, c0=c0, fc=fc: e.transpose(out=PBF(bk)[:, c0:c0 + 128], in_=xbc[:, fc, tsl], identity=identb[:]),
                         reads=[R("xbc", fc, fc + 1), R("identb")], writes=[PR(bk)])
                for bk in range(2):
                    P.op("dve", lambda e, bk=bk: e.tensor_tensor(out=xd[:, bk * 1024:(bk + 1) * 1024].rearrange("p (h q) -> p h q", h=16),
                                                                 in0=PBF(bk)[:, 0:1024].rearrange("p (h q) -> p h q", h=16),
                                                                 in1=dt_tok[:, ci, bk * 16:(bk + 1) * 16].unsqueeze(2).to_broadcast([128, 16, HP]), op=ALU.mult),
                         reads=[PR(bk), R("dt_tok")], writes=[xr])
                P.op("dve", lambda e: e.tensor_tensor(out=xdte[:].rearrange("p (h q) -> p h q", h=NH), in0=xd[:].rearrange("p (h q) -> p h q", h=NH),
                                                      in1=dte[:].unsqueeze(2).to_broadcast([128, NH, HP]), op=ALU.mult),
                     reads=[xr, R("dte")], writes=[R("xdte", 0, 8)])

            def ssd_decay(ci, hq):
                sl = hq % 2
                dbanks = (5, 6) if hq % 2 == 0 else (3, 4)
                for half in range(2):
                    bk = dbanks[half]
                    for j in range(4):
                        h = hq * 8 + half * 4 + j
                        P.op("pe", lambda e, bk=bk, j=j: e.matmul(pb[bk][:, j * 128:(j + 1) * 128], lhsT=identb[:], rhs=nmb[:, 0, :], start=True, stop=False),
                             reads=[R("identb"), R("nmb")], writes=[PR(bk)])
                        P.op("pe", lambda e, bk=bk, j=j, h=h: e.matmul(pb[bk][:, j * 128:(j + 1) * 128], lhsT=dA_bf[:, ci, h:h + 1].to_broadcast([128, 128]),
                                                                        rhs=triub[:], start=False, stop=True),
                             reads=[R("dA_bf"), R("triub")], writes=[PR(bk)])
                    for j in range(4):
                        h = hq * 8 + half * 4 + j
                        P.op("act", lambda e, bk=bk, j=j, h=h, half=half: e.activation(out=dec[sl][:, half * 4 + j, :], in_=pb[bk][:, j * 128:(j + 1) * 128],
                                                                                       func=AF.Exp, bias=negAcs[:, h:h + 1], scale=1.0),
                             reads=[PR(bk), R("negAcs")], writes=[R("dec%d" % sl, half * 4 + j, half * 4 + j + 1)])

            def ssd_y(ci, hq):
                tsl = slice(ci * 128, (ci + 1) * 128)
                sl = hq % 2
                P.op("dve", lambda e: e.tensor_tensor(out=scr[sl][:].rearrange("p (g j) l -> p g j l", g=2), in0=dec[sl][:].rearrange("p (g j) l -> p g j l", g=2),
                                                      in1=CBs[:, 2 * hq:2 * hq + 2, :].unsqueeze(2).to_broadcast([128, 2, 4, 128]), op=ALU.mult),
                     reads=[R("dec%d" % sl, 0, 8), R("CBs", 2 * hq, 2 * hq + 2)], writes=[R("dec%d" % sl, 0, 8)])
                bA = (0, 2)[hq % 2]
                bB = (1, 7)[hq % 2]
                for jj in range(8):
                    h = 8 * hq + jj
                    P.op("pe", lambda e, jj=jj, h=h: e.matmul(pb[bA][:, jj * 64:(jj + 1) * 64], lhsT=xbc[:, h // 2, tsl], rhs=diagD[:, h // 2, (h % 2) * 64:(h % 2 + 1) * 64], start=True, stop=False),
                         reads=[R("xbc", h // 2, h // 2 + 1), R("diagD")], writes=[PR(bA)])
                    P.op("pe", lambda e, jj=jj, h=h: e.matmul(pb[bA][:, jj * 64:(jj + 1) * 64], lhsT=scr[sl][:, jj, :], rhs=xdtb[ci % 2][:, h * 64:(h + 1) * 64], start=False, stop=True),
                         reads=[R("dec%d" % sl, jj, jj + 1), R("xdt%d" % (ci % 2))], writes=[PR(bA)])
                for gg in range(2):
                    g = 2 * hq + gg
                    P.op("pe", lambda e, g=g, gg=gg: e.matmul(pb[bB][:, gg * 256:(gg + 1) * 256], lhsT=xbc[:, 24 + g, tsl], rhs=hstb[:, g * 256:(g + 1) * 256], start=True, stop=True),
                         reads=[R("xbc", 24 + g, 25 + g), R("hstb", g, g + 1)], writes=[PR(bB)])

            def ssd_ye(ci, hq):
                bA = (0, 2)[hq % 2]
                bB = (1, 7)[hq % 2]
                ysl = slice(hq * 512, (hq + 1) * 512)
                yr = R("ybuf", 2 * hq, 2 * hq + 2)
                P.op("dve", lambda e: e.tensor_tensor(out=ybuf[:, ysl].rearrange("p (h q) -> p h q", h=8), in0=pb[bB][:, 0:512].rearrange("p (h q) -> p h q", h=8),
                                                      in1=Eacs[:, 8 * hq:8 * hq + 8].unsqueeze(2).to_broadcast([128, 8, HP]), op=ALU.mult),
                     reads=[PR(bB), R("Eacs")], writes=[yr])
                P.op("dve", lambda e: e.tensor_tensor(out=ybuf[:, ysl], in0=pb[bA][:, 0:512], in1=ybuf[:, ysl], op=ALU.add), reads=[PR(bA), yr], writes=[yr])
                P.op("dve", lambda e: e.tensor_tensor(out=ybuf[:, ysl], in0=ybuf[:, ysl], in1=sz[:, ci, ysl], op=ALU.mult), reads=[yr, R("sz", ci, ci + 1)], writes=[yr])

            def ssd_g(ci):
                for q in range(4):
                    sbk = 3 + q
                    hsl = slice(q * 512, (q + 1) * 512)
                    hr = R("hst", 2 * q, 2 * q + 2)
                    for gg in range(2):
                        g = 2 * q + gg
                        P.op("pe", lambda e, g=g, gg=gg, sbk=sbk: e.matmul(pb[sbk][:, gg * 256:(gg + 1) * 256], lhsT=B_tok[:, g * 128:(g + 1) * 128], rhs=xdte[:, g * 256:(g + 1) * 256], start=True, stop=True),
                             reads=[R("B_tok"), R("xdte", g, g + 1)], writes=[PR(sbk)])
                    P.op("pool", lambda e, q=q, hsl=hsl: e.tensor_tensor(out=hst[:, hsl].rearrange("p (h q) -> p h q", h=8), in0=hst[:, hsl].rearrange("p (h q) -> p h q", h=8),
                                                                         in1=cdB[:, 8 * q:8 * q + 8].unsqueeze(2).to_broadcast([128, 8, HP]), op=ALU.mult),
                         reads=[hr, R("cdB")], writes=[hr])
                    P.op("dve", lambda e, sbk=sbk, hsl=hsl: e.tensor_tensor(out=hst[:, hsl], in0=pb[sbk][:, 0:512], in1=hst[:, hsl], op=ALU.add), reads=[PR(sbk), hr], writes=[hr])
                    P.op("act", lambda e, hsl=hsl: e.activation(out=hstb[:, hsl], in_=hst[:, hsl], func=AF.Copy), reads=[hr], writes=[R("hstb", 2 * q, 2 * q + 2)])

            def ssd_h(ci):
                tsl = slice(ci * 128, (ci + 1) * 128)
                ynb2 = xdtb[ci % 2]
                xr = R("xdt%d" % (ci % 2))
                P.op("act", lambda e: e.activation(out=ynb2[:], in_=ybuf[:], func=AF.Square, accum_out=ssq[:, 4:5]), reads=[R("ybuf", 0, 8)], writes=[xr, R("ssq", 4, 5)])
                P.op("act", lambda e: e.activation(out=ssq[:, 5:6], in_=ssq[:, 4:5], func=AF.Ln, scale=1.0 / DI, bias=epsc[:, 0:1]), reads=[R("ssq", 4, 5), R("epsc")], writes=[R("ssq", 5, 6)])
                P.op("act", lambda e: e.activation(out=ssq[:, 6:7], in_=ssq[:, 5:6], func=AF.Exp, scale=-0.5), reads=[R("ssq", 5, 6)], writes=[R("ssq", 6, 7)])
                P.op("dve", lambda e: e.scalar_tensor_tensor(out=ynb2[:], in0=ybuf[:], scalar=ssq[:, 6:7], in1=snwb[:], op0=ALU.mult, op1=ALU.mult),
                     reads=[R("ybuf", 0, 8), R("ssq", 6, 7), R("snwb")], writes=[xr])
                for fc in range(16):
                    bk = fc // 8
                    c0 = (fc % 8) * 128
                    P.op("pe", lambda e, bk=bk, c0=c0, fc=fc: e.transpose(out=PBF(bk)[:, c0:c0 + 128], in_=ynb2[:, fc * 128:(fc + 1) * 128], identity=identb[:]),
                         reads=[xr, R("identb")], writes=[PR(bk)])
                for q in range(2):
                    P.op("act", lambda e, q=q: e.activation(out=ynT[:, q * 8:(q + 1) * 8, tsl], in_=PBF(q)[:, 0:1024].rearrange("p (a b) -> p a b", a=8), func=AF.Copy),
                         reads=[PR(q)], writes=[R("ubuf", 0, 8)])

            ssd_acd(0)
            ssd_b(0)
            for ci in range(4):
                if ci == 0:
                    ssd_decay(ci, 0)
                ssd_decay(ci, 1)
                ssd_y(ci, 0)
                for hq in range(4):
                    if hq + 2 < 4:
                        ssd_decay(ci, hq + 2)
                    if hq + 1 < 4:
                        ssd_y(ci, hq + 1)
                    ssd_ye(ci, hq)
                ssd_g(ci)
                if ci + 1 < 4:
                    ssd_acd(ci + 1)
                    ssd_b(ci + 1)
                    ssd_decay(ci + 1, 0)
                ssd_h(ci)
            if tb == nblk - 1:
                DMA("sp", o_sp, hst[:], "st_sp", reads=[R("hst", 0, 8)], writes=[R("o_sp")])

            for t in range(4):
                wt, wr = load_w(w_ssd[:, :, t * 256:(t + 1) * 256], 16, 256, tid=('ssd', t))
                for j in range(2):
                    oc = 2 * t + j
                    bk = next_bank()
                    proj_ws(wt, wr, j, 16, lambda kc: ynT[:, kc, :], lambda kc: [R("ubuf", 0, 8)], bk)
                    P.op("dve", lambda e, bk=bk, oc=oc: e.tensor_tensor(out=sA[:, 0:TB], in0=pb[bk][:, 0:TB], in1=gbT[:, oc, :], op=ALU.mult),
                         reads=[PR(bk), R("gT", oc, oc + 1)], writes=[R("sA")])
                    P.op("dve", lambda e, oc=oc: e.tensor_tensor(out=hT[:, oc, :], in0=sA[:, 0:TB], in1=amT[:, oc, :], op=ALU.add),
                         reads=[R("sA"), R("amT", oc, oc + 1)], writes=[R("hT", oc, oc + 1)])
            for t in range(2):
                wt, wr = load_w(w_out[:, :, t * 512:(t + 1) * 512], 8, 512, tid=('out', t))
                for ti in range(4):
                    bk = next_bank()
                    proj_as(wt, wr, ti, 8, hT_lhs, hT_regs, bk)
                    P.op("dve", lambda e, bk=bk, t=t: e.tensor_tensor(out=sB[:, 0:512], in0=pb[bk][:, 0:512], in1=g1B[:, t * 512:(t + 1) * 512], op=ALU.mult),
                         reads=[PR(bk), R("g1B", t, t + 1)], writes=[R("sB")])
                    P.op("dve", lambda e, ti=ti, t=t: e.tensor_tensor(out=xtok[:, ti, t * 512:(t + 1) * 512], in0=xtok[:, ti, t * 512:(t + 1) * 512], in1=sB[:, 0:512], op=ALU.add),
                         reads=[R("sB"), R("xtok", ti, ti + 1)], writes=[R("xtok", ti, ti + 1)])
            rmsnorm_to_T("a2T", a2T, 24)
            for t in range(11):
                wt, wr = load_w(w_ffi[:, :, t * 512:(t + 1) * 512], 8, 512, tid=('ffi', t))
                for j in range(2):
                    fc = 2 * t + j
                    bg = next_bank()
                    proj_ws(wt, wr, j, 8, hT_rhs, hT_regs, bg)
                    bu = next_bank()
                    proj_ws(wt, wr, j, 8, hT_rhs, hT_regs, bu, col0=256)
                    P.op("act", lambda e, bg=bg: e.activation(out=sA[:, 0:TB], in_=pb[bg][:, 0:TB], func=AF.Silu), reads=[PR(bg)], writes=[R("sA")])
                    P.op("dve", lambda e, bu=bu, fc=fc: e.tensor_tensor(out=fT[:, fc, :], in0=pb[bu][:, 0:TB], in1=sA[:, 0:TB], op=ALU.mult),
                         reads=[PR(bu), R("sA")], writes=[R("xbc", fc, fc + 1)])
            fT_lhs = lambda kc, ti: fT[:, kc, ti * 128:(ti + 1) * 128]
            fT_regs = lambda kc: [R("xbc", kc, kc + 1)]
            for half in range(2):
                for kg in range(3):
                    nk = 8 if kg < 2 else 6
                    wt, wr = load_w(w_ffo[:, kg * 8:kg * 8 + nk, half * 512:(half + 1) * 512], nk, 512, tid=('ffo', kg, half))
                    for ti in range(4):
                        proj_as(wt, wr, ti, nk, fT_lhs, fT_regs, 4 + ti, kc0=kg * 8, first=(kg == 0), last=(kg == 2))
                for ti in range(4):
                    P.op("dve", lambda e, ti=ti, half=half: e.tensor_tensor(out=sB[:, 0:512], in0=pb[4 + ti][:, 0:512], in1=g2B[:, half * 512:(half + 1) * 512], op=ALU.mult),
                         reads=[PR(4 + ti), R("g2B", half, half + 1)], writes=[R("sB")])
                    P.op("dve", lambda e, ti=ti, half=half: e.tensor_tensor(out=xtok[:, ti, half * 512:(half + 1) * 512], in0=xtok[:, ti, half * 512:(half + 1) * 512], in1=sB[:, 0:512], op=ALU.add),
                         reads=[R("sB"), R("xtok", ti, ti + 1)], writes=[R("xtok", ti, ti + 1)])
            for ti in range(4):
                ys = 0
                P.op("act", lambda e, ti=ti: e.activation(out=junk[:, 0:D], in_=xtok[:, ti, :], func=AF.Square, accum_out=ssq[:, ti:ti + 1]),
                     reads=[R("xtok", ti, ti + 1)], writes=[R("xdte", 0, 8), R("ssq", ti, ti + 1)])
                P.op("act", lambda e, ti=ti: e.activation(out=rsq[:, 4 + ti:5 + ti], in_=ssq[:, ti:ti + 1], func=AF.Ln, scale=1.0 / D, bias=epsc[:, 0:1]),
                     reads=[R("ssq", ti, ti + 1), R("epsc")], writes=[R("rsq", 4 + ti, 5 + ti)])
                P.op("act", lambda e, ti=ti: e.activation(out=rsq[:, ti:ti + 1], in_=rsq[:, 4 + ti:5 + ti], func=AF.Exp, scale=-0.5), reads=[R("rsq", 4 + ti, 5 + ti)], writes=[R("rsq", ti, ti + 1)])
                P.op("dve", lambda e, ti=ti, ys=ys: e.scalar_tensor_tensor(out=yst[ys][:], in0=xtok[:, ti, :], scalar=rsq[:, ti:ti + 1], in1=rowb[:, RB_FNW:RB_FNW + D], op0=ALU.mult, op1=ALU.mult),
                     reads=[R("xtok", ti, ti + 1), R("rsq", ti, ti + 1), R("rowb")], writes=[R("yst%d" % ys)])
                r0 = tb * TB + ti * 128
                DMA("sp", o_y[r0:r0 + 128, :], yst[ys][:], "st_y%d" % ys, reads=[R("yst%d" % ys)], writes=[R("o_y", tb * 4 + ti, tb * 4 + ti + 1)])


        if do_sample:
            Ssl = slice(1, 17)
            xbcF = xbc[:].rearrange("p a b -> p (a b)")
            stbuf = [xtok[:, 0:2, :].rearrange("p a b -> p (a b)"), xtok[:, 2:4, :].rearrange("p a b -> p (a b)")]
            streg = [R("xtok", 0, 2), R("xtok", 2, 4)]
            ubF = ubuf[:].rearrange("p a b -> p (a b)")
            stp = ubF[:, 0:1920].rearrange("p (c s r) -> p c s r", c=8, s=NS)
            newst = ubF[:, 1920:3840].rearrange("p (c s r) -> p c s r", c=8, s=NS)
            stc = ybuf[:, 0:1536].rearrange("p (c s r) -> p c s r", c=32, s=NS)
            newcst = hst[:, 0:1536].rearrange("p (c s r) -> p c s r", c=32, s=NS)
            gTf = gT[:].rearrange("p a b -> p (a b)").bitcast(F32)
            projS = gTf[:, 0:56 * NS].rearrange("p (c s) -> p c s", c=56)
            amF = amT[:].rearrange("p a b -> p (a b)").bitcast(F32)
            acc1 = amF[:, 0:512].rearrange("p (c s) -> p c s", c=32)
            acc2 = amF[:, 512:1024].rearrange("p (c s) -> p c s", c=32)
            amS = amF[:, 1024:1152].rearrange("p (c s) -> p c s", c=8)
            gaS = amF[:, 1152:1280].rearrange("p (c s) -> p c s", c=8)
            gbS = amF[:, 1280:1408].rearrange("p (c s) -> p c s", c=8)
            ptmp = amF[:, 1408:1536].rearrange("p (c s) -> p c s", c=8)
            sgS = amF[:, 1536:1568]
            xbcS = B_tok[:, 0:512].rearrange("p (c s) -> p c s", c=32)
            CBf = CBs[:].rearrange("p a b -> p (a b)")
            hTs = CBf[:, 0:128].rearrange("p (c s) -> p c s", c=8)
            mixTs = CBf[:, 128:256].rearrange("p (c s) -> p c s", c=8)
            pooledS = CBf[:, 256:384].rearrange("p (c s) -> p c s", c=8)
            ynTs = CBf[:, 384:640].rearrange("p (c s) -> p c s", c=16)
            fTs = CBf[:, 640:992].rearrange("p (c s) -> p c s", c=22)
            x_tokS = x_tok[0:NS, :]
            xdt_tokS = xdt[0:NS, :]
            szS = xdte[0:NS, :]
            xs_tok = yst[0][0:NS, :]
            szf = sz[:].rearrange("p a b -> p (a b)").bitcast(F32)
            ysS = szf[0:NS, 0:2048]
            gs1 = szf[0:NS, 2048:3072]
            gs2 = szf[0:NS, 3072:4096]
            xnS = dec[0][:].rearrange("p a b -> p (a b)")[0:NS, :]
            hTF = hT[:].rearrange("p a b -> p (a b)")
            ynS = hTF[0:NS, 0:2048]
            junkS = hTF[0:NS, 2048:4096]
            tmpDx = hTF[0:NS, :].bitcast(F32)
            decBs = sA[:, 0:512].rearrange("p (s h) -> p s h", s=NS)
            mask16 = sB[:, 0:256].rearrange("p (a b) -> p a b", a=NS)
            identfS = sB[:, 256:384]
            CmaskS = xbcF[:, 0:2048].rearrange("p (g s m) -> p g s m", g=NG, s=NS)
            ssS, rsS = ssq[0:NS, :], rsq[0:NS, :]
            RCB = R("CBs", 0, 8)
            RAM = R("amT", 0, 8)

            DMA("sp", xs_tok, xsm, "ld_xs", writes=[R("yst0")])
            DMA("sp", stp, st_pool, "ld_stp", writes=[R("ubuf", 0, 8)])
            DMA("sp", stc, st_conv, "ld_stc", writes=[R("ybuf", 0, 8)])
            P.op("pool", lambda e: e.memset(sB[:, 0:384], 0.0), writes=[R("sB")])
            P.op("pool", lambda e: e.affine_select(out=mask16, in_=mask16, pattern=[[1, NS], [-1, NS]], compare_op=ALU.not_equal, fill=1.0, base=0, channel_multiplier=0),
                 reads=[R("sB")], writes=[R("sB")])
            P.op("pool", lambda e: e.affine_select(out=identfS, in_=identfS, pattern=[[-1, 128]], compare_op=ALU.not_equal, fill=1.0, base=0, channel_multiplier=1),
                 reads=[R("sB")], writes=[R("sB")])
            for (gsv, c0, b0) in ((gs1, 16, 0), (gs2, 40, 2)):
                for c in range(8):
                    bk = b0 + c // 4
                    P.op("pe", lambda e, bk=bk, c=c, c0=c0: e.matmul(pb[bk][0:NS, (c % 4) * 128:(c % 4 + 1) * 128], lhsT=modT[:, c0 + c, Ssl], rhs=identfS, start=True, stop=True),
                         reads=[R("modT", c0 + c, c0 + c + 1), R("sB")], writes=[PR(bk)])
                for q in range(2):
                    P.op("act", lambda e, gsv=gsv, q=q, b0=b0: e.activation(out=gsv[:, q * 512:(q + 1) * 512], in_=pb[b0 + q][0:NS, 0:512], func=AF.Copy),
                         reads=[PR(b0 + q)], writes=[R("sz", 0, 4)])

            def s_norm_T(aT, bcol0):
                P.op("act", lambda e: e.activation(out=junkS[:, 0:D], in_=xs_tok, func=AF.Square, accum_out=ssS[:, 0:1]), reads=[R("yst0")], writes=[R("hT", 0, 8), R("ssq", 0, 8)])
                P.op("act", lambda e: e.activation(out=rsS[:, 4:5], in_=ssS[:, 0:1], func=AF.Ln, scale=1.0 / D, bias=epsc[0:NS, 0:1]), reads=[R("ssq", 0, 8), R("epsc")], writes=[R("rsq", 0, 8)])
                P.op("act", lambda e: e.activation(out=rsS[:, 0:1], in_=rsS[:, 4:5], func=AF.Exp, scale=-0.5), reads=[R("rsq", 0, 8)], writes=[R("rsq", 0, 8)])
                P.op("dve", lambda e: e.tensor_scalar(out=xnS, in0=xs_tok, scalar1=rsS[:, 0:1], scalar2=None, op0=ALU.mult), reads=[R("yst0"), R("rsq", 0, 8)], writes=[R("dec0", 0, 8)])
                for kc in range(8):
                    P.op("pe", lambda e, kc=kc: e.transpose(out=PBF(0)[:, kc * NS:(kc + 1) * NS], in_=xnS[:, kc * 128:(kc + 1) * 128], identity=identb[0:NS, 0:NS]),
                         reads=[R("dec0", 0, 8), R("identb")], writes=[PR(0)])
                P.op("dve", lambda e, aT=aT: e.tensor_tensor(out=ptmp, in0=PBF(0)[:, 0:128].rearrange("p (c s) -> p c s", c=8), in1=aT[:, :, Ssl], op=ALU.mult),
                     reads=[PR(0), R("a1T"), R("a2T")], writes=[RAM])
                P.op("dve", lambda e, bcol0=bcol0: e.tensor_tensor(out=hTs, in0=ptmp, in1=modT[:, bcol0:bcol0 + 8, Ssl], op=ALU.add),
                     reads=[RAM, R("modT", bcol0, bcol0 + 8)], writes=[RCB])

            s_norm_T(a1T, 0)
            ws_tiles = [(0, 0), (512, 4)] + [(3072 + 512 * t, 8 + 4 * t) for t in range(8)] + [(7200 + 512 * t, 40 + 4 * t) for t in range(4)]
            for (col0, cb0) in ws_tiles:
                wt, wr = load_w(w_in[:, :, col0:col0 + 512], 8, 512, tid=('in', col0))
                for j in range(4):
                    for kc in range(8):
                        P.op("pe", lambda e, j=j, kc=kc, wt=wt: e.matmul(pb[1][:, j * NS:(j + 1) * NS], lhsT=wt[:, kc, j * 128:(j + 1) * 128], rhs=hTs[:, kc, :], start=(kc == 0), stop=(kc == 7)),
                             reads=[wr, RCB], writes=[PR(1)])
                P.op("act", lambda e, cb0=cb0: e.activation(out=projS[:, cb0:cb0 + 4, :], in_=pb[1][:, 0:4 * NS].rearrange("p (c s) -> p c s", c=4), func=AF.Copy),
                     reads=[PR(1)], writes=[R("gT", 0, 8)])
            for t in range(4):
                wt, wr = load_w(w_in[:, :, 1024 + t * 512:1024 + (t + 1) * 512], 8, 512, tid=('in', 1024 + t * 512))
                for kc in range(8):
                    P.op("pe", lambda e, kc=kc, wt=wt: e.matmul(pb[2][0:NS, 0:512], lhsT=hTs[:, kc, :], rhs=wt[:, kc, :], start=(kc == 0), stop=(kc == 7)),
                         reads=[wr, RCB], writes=[PR(2)])
                P.op("act", lambda e, t=t: e.activation(out=szS[:, t * 512:(t + 1) * 512], in_=pb[2][0:NS, 0:512], func=AF.Silu), reads=[PR(2)], writes=[R("xdte", 0, 8)])
            for kc in range(8):
                P.op("pe", lambda e, kc=kc: e.matmul(pb[3][0:NS, 0:32], lhsT=hTs[:, kc, :], rhs=wdt[:, kc, :], start=(kc == 0), stop=(kc == 7)), reads=[RCB, R("wdt")], writes=[PR(3)])
            d_x, d_t, d_u, d_dt, d_dec = dtx[0:NS, 0, :], dtt[0:NS, 0, :], dtu[0:NS, 0, :], dt_tok[0:NS, 0, :], dtx[0:NS, 1, :]
            P.op("dve", lambda e: e.tensor_tensor(out=d_x, in0=pb[3][0:NS, 0:32], in1=rowb[0:NS, RB_DTB:RB_DTB + 32], op=ALU.add), reads=[PR(3), R("rowb")], writes=[R("dtx")])
            P.op("act", lambda e: e.activation(out=d_t, in_=d_x, func=AF.Abs), reads=[R("dtx")], writes=[R("dtt")])
            P.op("act", lambda e: e.activation(out=d_u, in_=d_t, func=AF.Exp, scale=-1.0), reads=[R("dtt")], writes=[R("dtu")])
            P.op("act", lambda e: e.activation(out=d_t, in_=d_u, func=AF.Ln, bias=1.0, scale=1.0), reads=[R("dtu")], writes=[R("dtt")])
            P.op("dve", lambda e: e.scalar_tensor_tensor(out=d_dt, in0=d_x, scalar=0.0, in1=d_t, op0=ALU.max, op1=ALU.add), reads=[R("dtx"), R("dtt")], writes=[R("dt_tok")])
            P.op("dve", lambda e: e.tensor_tensor(out=d_u, in0=d_dt, in1=anegb[0:NS, :], op=ALU.mult), reads=[R("dt_tok"), R("anegb")], writes=[R("dtu")])
            P.op("act", lambda e: e.activation(out=d_dec, in_=d_u, func=AF.Exp), reads=[R("dtu")], writes=[R("dtx")])
            for s_ in range(NS):
                P.op("pe", lambda e, s_=s_: e.matmul(pb[0][:, s_ * 32:(s_ + 1) * 32], lhsT=identfS[0:NS, s_:s_ + 1].to_broadcast([NS, 128]), rhs=d_dec, start=True, stop=True),
                     reads=[R("sB"), R("dtx")], writes=[PR(0)])
            P.op("act", lambda e: e.activation(out=sA[:, 0:512], in_=pb[0][:, 0:512], func=AF.Copy), reads=[PR(0)], writes=[R("sA")])
            for g in range(4):
                w = 2 ** (g + 1)
                ug = projS[:, 2 * g:2 * g + 2, :]
                P.op("dve", lambda e, g=g, w=w: e.reduce_sum(out=ptmp[:, 0:2, :], in_=stp[:, 2 * g:2 * g + 2, :, 15 - (w - 1):15], axis=mybir.AxisListType.X),
                     reads=[R("ubuf", 0, 8)], writes=[RAM])
                P.op("dve", lambda e, ug=ug: e.tensor_tensor(out=ptmp[:, 0:2, :], in0=ptmp[:, 0:2, :], in1=ug, op=ALU.add), reads=[RAM, R("gT", 0, 8)], writes=[RAM])
                P.op("dve", lambda e, ug=ug, g=g, w=w: e.scalar_tensor_tensor(out=pooledS[:, 2 * g:2 * g + 2, :], in0=ptmp[:, 0:2, :], scalar=1.0 / w, in1=ug, op0=ALU.mult, op1=ALU.subtract),
                     reads=[RAM, R("gT", 0, 8)], writes=[RCB])
            P.op("act", lambda e: e.activation(out=gaS, in_=projS[:, 40:48, :], func=AF.Sigmoid), reads=[R("gT", 0, 8)], writes=[RAM])
            P.op("act", lambda e: e.activation(out=gbS, in_=projS[:, 48:56, :], func=AF.Sigmoid), reads=[R("gT", 0, 8)], writes=[RAM])
            wplt, wplr = load_w(w_pool.rearrange("p g k c -> p (g k) c"), 8, 256, tid=('pool',))
            for g in range(4):
                for j in range(2):
                    oc = 2 * g + j
                    for k2 in range(2):
                        P.op("pe", lambda e, g=g, j=j, k2=k2, oc=oc: e.matmul(pb[2][:, oc * NS:(oc + 1) * NS], lhsT=wplt[:, 2 * g + k2, j * 128:(j + 1) * 128], rhs=pooledS[:, 2 * g + k2, :], start=(k2 == 0), stop=(k2 == 1)),
                             reads=[wplr, RCB], writes=[PR(2)])
            for oc in range(8):
                P.op("dve", lambda e, oc=oc: e.scalar_tensor_tensor(out=amS[:, oc, :], in0=pb[2][:, oc * NS:(oc + 1) * NS], scalar=pv[:, PV_PSC + oc:PV_PSC + oc + 1], in1=gaS[:, oc, :], op0=ALU.mult, op1=ALU.mult),
                     reads=[PR(2), R("pv"), RAM], writes=[RAM])
            P.op("pool", lambda e: e.tensor_copy(out=newst[:, :, :, 0:14], in_=stp[:, :, :, 1:15]), reads=[R("ubuf", 0, 8)], writes=[R("ubuf", 0, 8)])
            P.op("pool", lambda e: e.tensor_copy(out=newst[:, :, :, 14], in_=projS[:, 0:8, :]), reads=[R("gT", 0, 8)], writes=[R("ubuf", 0, 8)])
            DMA("sp", o_ps, newst, "st_ps", reads=[R("ubuf", 0, 8)], writes=[R("o_ps")])
            xnew = projS[:, 8:40, :]
            cwb = lambda k: pv[:, PV_CW + 32 * k:PV_CW + 32 * k + 32].unsqueeze(2).to_broadcast([128, 32, NS])
            P.op("dve", lambda e: e.tensor_tensor(out=acc1, in0=xnew, in1=cwb(3), op=ALU.mult), reads=[R("gT", 0, 8), R("pv")], writes=[RAM])
            for k in range(3):
                P.op("dve", lambda e, k=k: e.tensor_tensor(out=acc2, in0=stc[:, :, :, k], in1=cwb(k), op=ALU.mult), reads=[R("ybuf", 0, 8), R("pv")], writes=[RAM])
                P.op("dve", lambda e: e.tensor_tensor(out=acc1, in0=acc1, in1=acc2, op=ALU.add), reads=[RAM], writes=[RAM])
            P.op("dve", lambda e: e.tensor_tensor(out=acc1, in0=acc1, in1=pv[:, PV_CB:PV_CB + 32].unsqueeze(2).to_broadcast([128, 32, NS]), op=ALU.add), reads=[RAM, R("pv")], writes=[RAM])
            P.op("act", lambda e: e.activation(out=xbcS, in_=acc1, func=AF.Silu), reads=[RAM], writes=[R("B_tok")])
            P.op("pool", lambda e: e.tensor_copy(out=newcst[:, :, :, 0:2], in_=stc[:, :, :, 1:3]), reads=[R("ybuf", 0, 8)], writes=[R("hst", 0, 8)])
            P.op("pool", lambda e: e.tensor_copy(out=newcst[:, :, :, 2], in_=xnew), reads=[R("gT", 0, 8)], writes=[R("hst", 0, 8)])
            DMA("sp", o_cs, newcst, "st_cs", reads=[R("hst", 0, 8)], writes=[R("o_cs")])
            for fc in range(16):
                bk = 1 + fc // 8
                P.op("pe", lambda e, fc=fc, bk=bk: e.transpose(out=PBF(bk)[0:NS, (fc % 8) * 128:(fc % 8 + 1) * 128], in_=xbcS[:, fc, :], identity=identb[:]),
                     reads=[R("B_tok"), R("identb")], writes=[PR(bk)])
            for q in range(2):
                P.op("act", lambda e, q=q: e.activation(out=x_tokS[:, q * 1024:(q + 1) * 1024], in_=PBF(1 + q)[0:NS, 0:1024], func=AF.Copy), reads=[PR(1 + q)], writes=[R("xdt1")])
            P.op("dve", lambda e: e.tensor_tensor(out=xdt_tokS.rearrange("p (h q) -> p h q", h=NH), in0=x_tokS.rearrange("p (h q) -> p h q", h=NH),
                                                  in1=d_dt.unsqueeze(2).to_broadcast([NS, NH, HP]), op=ALU.mult), reads=[R("xdt1"), R("dt_tok")], writes=[R("xdt0")])
            P.op("dve", lambda e: e.tensor_tensor(out=CmaskS, in0=xbcS[:, 24:32, :].unsqueeze(3).to_broadcast([128, NG, NS, NS]),
                                                  in1=mask16.unsqueeze(1).to_broadcast([128, NG, NS, NS]), op=ALU.mult), reads=[R("B_tok"), R("sB")], writes=[R("xbc", 0, 32)])
            P.op("pool", lambda e: e.memset(ysS, 0.0), writes=[R("sz", 0, 4)])
            def samp_L(s_):
                sl = s_ % 2
                DMA("sp", stbuf[sl], st_ssm[s_], "ld_st%d" % sl, writes=[streg[sl]])

            def samp_A(s_):
                sl = s_ % 2
                buf = stbuf[sl]
                for q in range(4):
                    P.op("pe", lambda e, q=q: e.matmul(pb[q][:, 0:512], lhsT=identb[0:NS, s_:s_ + 1].to_broadcast([NS, 128]), rhs=xdt_tokS[:, q * 512:(q + 1) * 512], start=True, stop=True),
                         reads=[R("identb"), R("xdt0")], writes=[PR(q)])
                P.op("pool", lambda e: e.tensor_tensor(out=buf.rearrange("p (h q) -> p h q", h=NH), in0=buf.rearrange("p (h q) -> p h q", h=NH),
                                                       in1=decBs[:, s_, :].unsqueeze(2).to_broadcast([128, NH, HP]), op=ALU.mult), reads=[streg[sl], R("sA")], writes=[streg[sl]])
                for g in range(NG):
                    P.op("dve", lambda e, g=g: e.scalar_tensor_tensor(out=buf[:, g * 256:(g + 1) * 256], in0=pb[g // 2][:, (g % 2) * 256:(g % 2 + 1) * 256],
                                                                       scalar=xbcS[:, 16 + g, s_:s_ + 1], in1=buf[:, g * 256:(g + 1) * 256], op0=ALU.mult, op1=ALU.add),
                         reads=[PR(g // 2), R("B_tok"), streg[sl]], writes=[streg[sl]])
                P.op("act", lambda e: e.activation(out=hstb[:], in_=buf, func=AF.Copy), reads=[streg[sl]], writes=[R("hstb", 0, 8)])
                DMA("sp", o_ss[s_], buf, "st_ss%d" % sl, reads=[streg[sl]], writes=[R("o_ss", s_, s_ + 1)])

            def samp_A2(s_):
                for g in range(NG):
                    P.op("pe", lambda e, g=g: e.matmul(pb[4 + g // 2][0:NS, (g % 2) * 256:(g % 2 + 1) * 256], lhsT=CmaskS[:, g, s_, :], rhs=hstb[:, g * 256:(g + 1) * 256], start=True, stop=True),
                         reads=[R("xbc", 0, 32), R("hstb", 0, 8)], writes=[PR(4 + g // 2)])

            def samp_B(s_):
                for q in range(4):
                    P.op("dve", lambda e, q=q: e.tensor_tensor(out=ysS[:, q * 512:(q + 1) * 512], in0=pb[4 + q][0:NS, 0:512], in1=ysS[:, q * 512:(q + 1) * 512], op=ALU.add),
                         reads=[PR(4 + q), R("sz", 0, 4)], writes=[R("sz", 0, 4)])

            samp_L(0)
            samp_L(1)
            samp_A(0)
            samp_A2(0)
            for s_ in range(NS):
                if s_ + 2 < NS:
                    samp_L(s_ + 2)
                if s_ + 1 < NS:
                    samp_A(s_ + 1)
                samp_B(s_)
                if s_ + 1 < NS:
                    samp_A2(s_ + 1)
            P.op("dve", lambda e: e.tensor_tensor(out=tmpDx.rearrange("p (h q) -> p h q", h=NH), in0=x_tokS.rearrange("p (h q) -> p h q", h=NH),
                                                  in1=rowb[0:NS, RB_DSK:RB_DSK + 32].unsqueeze(2).to_broadcast([NS, NH, HP]), op=ALU.mult), reads=[R("xdt1"), R("rowb")], writes=[R("hT", 0, 8)])
            P.op("dve", lambda e: e.tensor_tensor(out=ysS, in0=ysS, in1=tmpDx, op=ALU.add), reads=[R("sz", 0, 4), R("hT", 0, 8)], writes=[R("sz", 0, 4)])
            P.op("dve", lambda e: e.tensor_tensor(out=ysS, in0=ysS, in1=szS, op=ALU.mult), reads=[R("sz", 0, 4), R("xdte", 0, 8)], writes=[R("sz", 0, 4)])
            P.op("act", lambda e: e.activation(out=junkS, in_=ysS, func=AF.Square, accum_out=ssS[:, 1:2]), reads=[R("sz", 0, 4)], writes=[R("hT", 0, 8), R("ssq", 0, 8)])
            P.op("act", lambda e: e.activation(out=rsS[:, 5:6], in_=ssS[:, 1:2], func=AF.Ln, scale=1.0 / DI, bias=epsc[0:NS, 0:1]), reads=[R("ssq", 0, 8), R("epsc")], writes=[R("rsq", 0, 8)])
            P.op("act", lambda e: e.activation(out=rsS[:, 1:2], in_=rsS[:, 5:6], func=AF.Exp, scale=-0.5), reads=[R("rsq", 0, 8)], writes=[R("rsq", 0, 8)])
            P.op("dve", lambda e: e.scalar_tensor_tensor(out=ynS, in0=ysS, scalar=rsS[:, 1:2], in1=snwb[0:NS, :], op0=ALU.mult, op1=ALU.mult),
                 reads=[R("sz", 0, 4), R("rsq", 0, 8), R("snwb")], writes=[R("hT", 0, 8)])
            for fc in range(16):
                P.op("pe", lambda e, fc=fc: e.transpose(out=PBF(0)[:, fc * NS:(fc + 1) * NS], in_=ynS[:, fc * 128:(fc + 1) * 128], identity=identb[0:NS, 0:NS]),
                     reads=[R("hT", 0, 8), R("identb")], writes=[PR(0)])
            P.op("act", lambda e: e.activation(out=ynTs, in_=PBF(0)[:, 0:256].rearrange("p (c s) -> p c s", c=16), func=AF.Copy), reads=[PR(0)], writes=[RCB])
            for t in range(4):
                wt, wr = load_w(w_ssd[:, :, t * 256:(t + 1) * 256], 16, 256, tid=('ssd', t))
                for j in range(2):
                    oc = 2 * t + j
                    for kc in range(16):
                        P.op("pe", lambda e, j=j, kc=kc, oc=oc, wt=wt: e.matmul(pb[1][:, oc * NS:(oc + 1) * NS], lhsT=wt[:, kc, j * 128:(j + 1) * 128], rhs=ynTs[:, kc, :], start=(kc == 0), stop=(kc == 15)),
                             reads=[wr, RCB], writes=[PR(1)])
            P.op("dve", lambda e: e.tensor_tensor(out=ptmp, in0=pb[1][:, 0:128].rearrange("p (c s) -> p c s", c=8), in1=gbS, op=ALU.mult), reads=[PR(1), RAM], writes=[RAM])
            P.op("dve", lambda e: e.tensor_tensor(out=mixTs, in0=ptmp, in1=amS, op=ALU.add), reads=[RAM], writes=[RCB])
            tmpR = ysS[:, 0:512]
            for t in range(2):
                wt, wr = load_w(w_out[:, :, t * 512:(t + 1) * 512], 8, 512, tid=('out', t))
                for kc in range(8):
                    P.op("pe", lambda e, kc=kc, wt=wt, t=t: e.matmul(pb[2 + t][0:NS, 0:512], lhsT=mixTs[:, kc, :], rhs=wt[:, kc, :], start=(kc == 0), stop=(kc == 7)), reads=[wr, RCB], writes=[PR(2 + t)])
                P.op("dve", lambda e, t=t: e.tensor_tensor(out=tmpR, in0=pb[2 + t][0:NS, 0:512], in1=gs1[:, t * 512:(t + 1) * 512], op=ALU.mult), reads=[PR(2 + t), R("sz", 0, 4)], writes=[R("sz", 0, 4)])
                P.op("dve", lambda e, t=t: e.tensor_tensor(out=xs_tok[:, t * 512:(t + 1) * 512], in0=xs_tok[:, t * 512:(t + 1) * 512], in1=tmpR, op=ALU.add), reads=[R("sz", 0, 4), R("yst0")], writes=[R("yst0")])
            s_norm_T(a2T, 24)
            for t in range(11):
                wt, wr = load_w(w_ffi[:, :, t * 512:(t + 1) * 512], 8, 512, tid=('ffi', t))
                for q in range(4):
                    for kc in range(8):
                        P.op("pe", lambda e, q=q, kc=kc, wt=wt: e.matmul(pb[1][:, q * NS:(q + 1) * NS], lhsT=wt[:, kc, q * 128:(q + 1) * 128], rhs=hTs[:, kc, :], start=(kc == 0), stop=(kc == 7)),
                             reads=[wr, RCB], writes=[PR(1)])
                P.op("act", lambda e: e.activation(out=sgS, in_=pb[1][:, 0:2 * NS], func=AF.Silu), reads=[PR(1)], writes=[RAM])
                P.op("dve", lambda e, t=t: e.tensor_tensor(out=fTs[:, 2 * t:2 * t + 2, :], in0=pb[1][:, 2 * NS:4 * NS].rearrange("p (c s) -> p c s", c=2), in1=sgS.rearrange("p (c s) -> p c s", c=2), op=ALU.mult),
                     reads=[PR(1), RAM], writes=[RCB])
            for half in range(2):
                for kg in range(3):
                    nk = 8 if kg < 2 else 6
                    wt, wr = load_w(w_ffo[:, kg * 8:kg * 8 + nk, half * 512:(half + 1) * 512], nk, 512, tid=('ffo', kg, half))
                    for kc in range(nk):
                        P.op("pe", lambda e, kc=kc, kg=kg, nk=nk, wt=wt, half=half: e.matmul(pb[2 + half][0:NS, 0:512], lhsT=fTs[:, kg * 8 + kc, :], rhs=wt[:, kc, :], start=(kg == 0 and kc == 0), stop=(kg == 2 and kc == nk - 1)),
                             reads=[wr, RCB], writes=[PR(2 + half)])
                P.op("dve", lambda e, half=half: e.tensor_tensor(out=tmpR, in0=pb[2 + half][0:NS, 0:512], in1=gs2[:, half * 512:(half + 1) * 512], op=ALU.mult), reads=[PR(2 + half), R("sz", 0, 4)], writes=[R("sz", 0, 4)])
                P.op("dve", lambda e, half=half: e.tensor_tensor(out=xs_tok[:, half * 512:(half + 1) * 512], in0=xs_tok[:, half * 512:(half + 1) * 512], in1=tmpR, op=ALU.add), reads=[R("sz", 0, 4), R("yst0")], writes=[R("yst0")])
            P.op("act", lambda e: e.activation(out=junkS[:, 0:D], in_=xs_tok, func=AF.Square, accum_out=ssS[:, 2:3]), reads=[R("yst0")], writes=[R("hT", 0, 8), R("ssq", 0, 8)])
            P.op("act", lambda e: e.activation(out=rsS[:, 6:7], in_=ssS[:, 2:3], func=AF.Ln, scale=1.0 / D, bias=epsc[0:NS, 0:1]), reads=[R("ssq", 0, 8), R("epsc")], writes=[R("rsq", 0, 8)])
            P.op("act", lambda e: e.activation(out=rsS[:, 2:3], in_=rsS[:, 6:7], func=AF.Exp, scale=-0.5), reads=[R("rsq", 0, 8)], writes=[R("rsq", 0, 8)])
            P.op("dve", lambda e: e.scalar_tensor_tensor(out=xs_tok, in0=xs_tok, scalar=rsS[:, 2:3], in1=rowb[0:NS, RB_FNW:RB_FNW + D], op0=ALU.mult, op1=ALU.mult),
                 reads=[R("yst0"), R("rsq", 0, 8), R("rowb")], writes=[R("yst0")])
            DMA("sp", o_ys, xs_tok, "st_ys", reads=[R("yst0")], writes=[R("o_ys")])

        P.op("sp", None, reads=[R("o_y", 0, 4 * NBLK), R("o_pp"), R("o_cp"), R("o_sp"), R("o_ys"), R("o_ps"), R("o_cs"), R("o_ss", 0, NS)])

        P.analyze()
        sems_e = {e: es.enter_context(nc.semaphore("se_" + e)) for e in ENGS}
        sems_d = {k: es.enter_context(nc.semaphore("sd_" + k)) for k in sorted(dma_keys)}
        P.emit(sems_e, sems_d)
    return nc


def _tile_k(w):
    K, N = w.shape
    return np.ascontiguousarray(w.reshape(K // 128, 128, N).transpose(1, 0, 2))


def _fm(v):
    return np.ascontiguousarray(v.reshape(-1, 128).T)


_NC_CACHE = {}


def kernel(x_prompt, x_sample, c_prompt, c_sample, state_pool, state_conv, state_ssm, w_ada, b_ada, norm1_w,
           w_in, w_pool, pool_scale, conv_w, conv_b, dt_bias, A_log, D_skip, ssd_norm_w, w_ssd_proj, w_out,
           norm2_w, w_ffn_in, w_ffn_out, final_norm_w):
    f = np.float32
    n = 8
    x_prompt = np.asarray(x_prompt, f)
    pvec = np.zeros((128, PV_N), f)
    pvec[:, PV_N1W:PV_N1W + 8] = _fm(np.asarray(norm1_w[0], f))
    pvec[:, PV_PSC:PV_PSC + 8] = _fm(np.asarray(pool_scale[0], f))
    cw = np.asarray(conv_w[0], f)
    for k in range(4):
        pvec[:, PV_CW + 32 * k:PV_CW + 32 * k + 32] = _fm(cw[k])
    pvec[:, PV_CB:PV_CB + 32] = _fm(np.asarray(conv_b[0], f))
    pvec[:, PV_N2W:PV_N2W + 8] = _fm(np.asarray(norm2_w[0], f))
    pvec[:, PV_BADA:PV_BADA + 48] = _fm(np.asarray(b_ada[0], f))
    pvec[:, PV_DF:PV_DF + 16] = _fm(np.repeat(np.asarray(D_skip[0], f), HP))
    rowb = np.zeros((128, RB_N), f)
    rowb[:, RB_FNW:RB_FNW + D] = np.asarray(final_norm_w, f)[None, :]
    bg = np.zeros((128, 2 * D), f)
    bg[:, 0:D] = np.asarray(b_ada[0], f)[None, 2 * D:3 * D]
    bg[:, D:2 * D] = np.asarray(b_ada[0], f)[None, 5 * D:6 * D]
    rowb[:, RB_DSK:RB_DSK + 32] = np.asarray(D_skip[0], f)[None, :]
    rowb[:, RB_ALOG:RB_ALOG + 32] = np.asarray(A_log[0], f)[None, :]
    rowb[:, RB_DTB:RB_DTB + 32] = np.asarray(dt_bias[0], f)[None, :]
    snw = np.ascontiguousarray(np.broadcast_to(np.asarray(ssd_norm_w[0], f)[None, :], (128, DI)))
    w_ada_t = _tile_k(np.asarray(w_ada[0], f))
    w_in_t = _tile_k(np.asarray(w_in[0], f))
    wp = np.asarray(w_pool[0], f)
    w_pool_t = np.ascontiguousarray(np.stack([_tile_k(wp[g]) for g in range(4)], axis=1))
    w_ssd_t = _tile_k(np.asarray(w_ssd_proj[0], f))
    w_out_t = _tile_k(np.asarray(w_out[0], f))
    wfi = np.asarray(w_ffn_in[0], f)
    perm = np.concatenate([np.concatenate([np.arange(256 * t, 256 * t + 256), DFF + np.arange(256 * t, 256 * t + 256)]) for t in range(11)])
    w_ffi_t = _tile_k(np.ascontiguousarray(wfi[:, perm]))
    w_ffo_t = _tile_k(np.asarray(w_ffn_out[0], f))

    in_maps = []
    for b in range(n):
        s0, s1 = NS * b, NS * (b + 1)
        c17 = np.concatenate([np.asarray(c_prompt[b:b + 1], f), np.asarray(c_sample[s0:s1], f)], axis=0)
        cT = np.ascontiguousarray(c17.T.reshape(8, 128, 17).transpose(1, 0, 2))
        sp = np.asarray(state_pool[0, s0:s1], f)
        sp_t = np.ascontiguousarray(sp.reshape(NS, 15, 8, 128).transpose(3, 2, 0, 1))
        sc = np.asarray(state_conv[0, s0:s1], f)
        sc_t = np.ascontiguousarray(sc.reshape(NS, 3, 32, 128).transpose(3, 2, 0, 1))
        ss = np.asarray(state_ssm[0, s0:s1], f)
        ss_t = np.ascontiguousarray(ss.reshape(NS, DI, DST).transpose(0, 2, 1))
        in_maps.append({
            "xp": np.ascontiguousarray(x_prompt[b]),
            "xsm": np.ascontiguousarray(np.asarray(x_sample[s0:s1, 0], f)),
            "cT": cT, "pvec": pvec, "rowb": rowb, "snw": snw, "bg": bg,
            "w_ada": w_ada_t, "w_in": w_in_t, "w_pool": w_pool_t, "w_ssd": w_ssd_t, "w_out": w_out_t,
            "w_ffi": w_ffi_t, "w_ffo": w_ffo_t,
            "st_pool": sp_t, "st_conv": sc_t, "st_ssm": ss_t,
        })
    nblk = int(os.environ.get("K_NBLK", NBLK))
    ncores = int(os.environ.get("K_CORES", n))
    if "nc" not in _NC_CACHE:
        _NC_CACHE["nc"] = build_nc(nblk=nblk)
    nc = _NC_CACHE["nc"]
    res = run_bass_kernel_spmd(nc, in_maps[:ncores], core_ids=list(range(ncores)))
    rs = list(res.results)
    while len(rs) < n:
        rs.append({k: np.zeros_like(v) for k, v in rs[0].items()})
    y_prompt = np.stack([rs[b]["o_y"] for b in range(n)], axis=0)
    y_sample = np.concatenate([rs[b]["o_ys"] for b in range(n)], axis=0)[:, None, :]
    pool_p = np.stack([rs[b]["o_pp"].transpose(2, 1, 0).reshape(15, D) for b in range(n)], axis=0)[None]
    conv_p = np.stack([rs[b]["o_cp"].transpose(2, 1, 0).reshape(3, CONV) for b in range(n)], axis=0)[None]
    ssm_p = np.stack([rs[b]["o_sp"].T.reshape(NH, HP, DST) for b in range(n)], axis=0)[None]
    pool_s = np.concatenate([rs[b]["o_ps"].transpose(2, 3, 1, 0).reshape(NS, 15, D) for b in range(n)], axis=0)[None]
    conv_s = np.concatenate([rs[b]["o_cs"].transpose(2, 3, 1, 0).reshape(NS, 3, CONV) for b in range(n)], axis=0)[None]
    ssm_s = np.concatenate([rs[b]["o_ss"].transpose(0, 2, 1).reshape(NS, NH, HP, DST) for b in range(n)], axis=0)[None]
    return (np.ascontiguousarray(y_prompt, dtype=f), np.ascontiguousarray(y_sample, dtype=f),
            np.ascontiguousarray(pool_p, dtype=f), np.ascontiguousarray(conv_p, dtype=f),
            np.ascontiguousarray(ssm_p, dtype=f), np.ascontiguousarray(pool_s, dtype=f),
            np.ascontiguousarray(conv_s, dtype=f), np.ascontiguousarray(ssm_s, dtype=f))
```

```python
import os
from contextlib import ExitStack

import numpy as np
import concourse.bass as bass
import concourse.mybir as mybir
from concourse.bass_utils import run_bass_kernel_spmd

F32 = mybir.dt.float32
BF16 = mybir.dt.bfloat16
AF = mybir.ActivationFunctionType
ALU = mybir.AluOpType

ENGS = ("pe", "act", "dve", "pool", "sp")

D = 1024
SEQ = 2048
TB = 512
NBLK = SEQ // TB
NS = 16
DI = 2048
NH = 32
HP = 64
NG = 8
DST = 128
CONV = 4096
DFF = 2816
IN_COLS = 9248
EPS = 1e-6

PV_N1W, PV_PSC, PV_CW, PV_CB, PV_N2W, PV_BADA, PV_DF, PV_N = 0, 8, 16, 144, 176, 184, 232, 248
RB_FNW, RB_DSK, RB_ALOG, RB_DTB, RB_N = 0, 1024, 1056, 1088, 1120


class Op:
    __slots__ = ("eng", "fn", "reads", "writes", "dkey", "dgroup", "idx", "waits",
                 "sig", "cnt", "eidx")

    def __init__(self, eng, fn, reads, writes, dkey=None, dgroup=None):
        self.eng = eng
        self.fn = fn
        self.reads = reads
        self.writes = writes
        self.dkey = dkey
        self.dgroup = dgroup
        self.waits = {}
        self.sig = False
        self.cnt = 0


class Prog:
    def __init__(self, nc):
        self.nc = nc
        self.ops = []

    def op(self, eng, fn, reads=(), writes=()):
        o = Op(eng, fn, list(reads), list(writes))
        o.idx = len(self.ops)
        self.ops.append(o)
        return o

    def dma(self, eng, fn, reads=(), writes=(), key=None, group=None):
        o = Op(eng, fn, list(reads), list(writes), dkey=key, dgroup=group)
        o.idx = len(self.ops)
        self.ops.append(o)
        return o

    def analyze(self):
        ops = self.ops
        ecount = {e: 0 for e in ENGS}
        for o in ops:
            o.eidx = ecount[o.eng]
            ecount[o.eng] += 1
        key_ops = {}
        for o in ops:
            if o.dkey is not None:
                key_ops.setdefault(o.dkey, []).append(o)
        dma_cum, dma_prev = {}, {}
        for k, lst in key_ops.items():
            groups = []
            for o in lst:
                if groups and o.dgroup is not None and groups[-1][0] == o.dgroup:
                    groups[-1][1].append(o)
                else:
                    groups.append((o.dgroup, [o]))
            cum = 0
            for g, gl in groups:
                prev = cum
                cum += len(gl)
                for o in gl:
                    dma_cum[o.idx] = cum
                    dma_prev[o.idx] = prev
        self.keys = sorted(key_ops.keys())
        recs = {}
        waited = {}
        pend = []
        for o in ops:
            d = set()
            for (buf, lo, hi) in o.reads:
                for r in recs.get(buf, ()):
                    if r[3] and r[0] < hi and lo < r[1]:
                        d.add(r[2])
            for (buf, lo, hi) in o.writes:
                for r in recs.get(buf, ()):
                    if r[0] < hi and lo < r[1]:
                        d.add(r[2])
            d.discard(o.idx)
            for (buf, lo, hi) in o.writes:
                lst = recs.setdefault(buf, [])
                lst[:] = [r for r in lst if not (lo <= r[0] and r[1] <= hi)]
                lst.append([lo, hi, o.idx, True])
            for (buf, lo, hi) in o.reads:
                lst = recs.setdefault(buf, [])
                lst[:] = [r for r in lst if not ((not r[3]) and ops[r[2]].eng == o.eng
                                                 and ops[r[2]].dkey is None and o.dkey is None
                                                 and lo <= r[0] and r[1] <= hi)]
                lst.append([lo, hi, o.idx, False])
            need = {}
            for di in d:
                p = ops[di]
                if p.dkey is not None:
                    sk = ("d", p.dkey)
                    val = 16 * dma_cum[p.idx]
                    if need.get(sk, 0) < val:
                        need[sk] = val
                    continue
                if p.eng == o.eng and o.dkey is None:
                    if o.eng in ("pe", "sp"):
                        continue
                sk = ("e", p.eng)
                cur = need.get(sk)
                if cur is None or cur.eidx < p.eidx:
                    need[sk] = p
            if o.dkey is not None and dma_prev[o.idx] > 0:
                sk = ("d", o.dkey)
                val = 16 * dma_prev[o.idx]
                if need.get(sk, 0) < val:
                    need[sk] = val
            o.waits = need
            for sk, v in need.items():
                if sk[0] == "e":
                    v.sig = True
        cnt = {e: 0 for e in ENGS}
        for o in ops:
            if o.dkey is None and o.sig:
                cnt[o.eng] += 1
            o.cnt = cnt[o.eng]
        for o in ops:
            final = {}
            for sk, v in o.waits.items():
                val = v.cnt if sk[0] == "e" else v
                wk = (o.eng, sk)
                if waited.get(wk, 0) >= val:
                    continue
                waited[wk] = val
                final[sk] = val
            o.waits = final

    def emit(self, sems_e, sems_d):
        nc = self.nc
        per = {e: [o for o in self.ops if o.eng == e] for e in ENGS}

        def run(engname, eng):
            for o in per[engname]:
                for sk, val in o.waits.items():
                    sem = sems_e[sk[1]] if sk[0] == "e" else sems_d[sk[1]]
                    eng.wait_ge(sem, val)
                if o.fn is None:
                    continue
                ins = o.fn(eng)
                if o.dkey is not None:
                    ins.then_inc(sems_d[o.dkey], 16)
                elif o.sig:
                    ins.then_inc(sems_e[o.eng], 1)

        with nc.Block() as block:
            @block.tensor
            def _(e):
                run("pe", e)

            @block.scalar
            def _(e):
                run("act", e)

            @block.vector
            def _(e):
                run("dve", e)

            @block.gpsimd
            def _(e):
                run("pool", e)

            @block.sync
            def _(e):
                run("sp", e)


def R(name, lo=0, hi=1):
    return (name, lo, hi)


def build_nc(nblk=NBLK, do_sample=True):
    nc = bass.Bass("TRN2", target_bir_lowering=False)

    def din(name, shape):
        return nc.dram_tensor(name, list(shape), F32, kind="ExternalInput").ap()

    def dout(name, shape):
        return nc.dram_tensor(name, list(shape), F32, kind="ExternalOutput").ap()

    xp = din("xp", [SEQ, D])
    xsm = din("xsm", [NS, D])
    cT = din("cT", [128, 8, 17])
    pvec = din("pvec", [128, PV_N])
    rowb_in = din("rowb", [128, RB_N])
    snw_in = din("snw", [128, DI])
    bg_in = din("bg", [128, 2 * D])
    w_ada = din("w_ada", [128, 8, 6 * D])
    w_in = din("w_in", [128, 8, IN_COLS])
    w_pool = din("w_pool", [128, 4, 2, 256])
    w_ssd = din("w_ssd", [128, 16, D])
    w_out = din("w_out", [128, 8, D])
    w_ffi = din("w_ffi", [128, 8, 2 * DFF])
    w_ffo = din("w_ffo", [128, 22, D])
    st_pool = din("st_pool", [128, 8, NS, 15])
    st_conv = din("st_conv", [128, 32, NS, 3])
    st_ssm = din("st_ssm", [NS, 128, DI])
    o_y = dout("o_y", [SEQ, D])
    o_ys = dout("o_ys", [NS, D])
    o_pp = dout("o_pp", [128, 8, 15])
    o_cp = dout("o_cp", [128, 32, 3])
    o_sp = dout("o_sp", [128, DI])
    o_ps = dout("o_ps", [128, 8, NS, 15])
    o_cs = dout("o_cs", [128, 32, NS, 3])
    o_ss = dout("o_ss", [NS, 128, DI])
    dbg_out = {}

    es = ExitStack()
    with es:
        def sb(name, shape, dt=F32):
            return es.enter_context(nc.sbuf_tensor("s_" + name, list(shape), dt))

        P = Prog(nc)
        dma_keys = set()

        def DMA(eng, out, in_, key, reads=(), writes=(), group=None):
            dma_keys.add(key)
            P.dma(eng, lambda e: e.dma_start(out=out, in_=in_), reads=reads, writes=writes, key=key, group=group)

        pb = [es.enter_context(nc.psum_tensor("pb%d" % i, [128, 512], F32)) for i in range(8)]

        def PBF(i):
            return pb[i][:].bitcast(BF16)

        def PR(i, lo=0, hi=512):
            return ("pb%d" % i, 0, 512)

        sA = sb("sA", [128, 16 + TB])
        sB = sb("sB", [128, 16 + TB])
        identf = sB[:, 0:128]
        triuf = sB[:, 128:256]
        nmf = sA[:, 0:512].rearrange("p (a b) -> p a b", a=4)
        identb = sb("identb", [128, 128], BF16)
        onesb = sb("onesb", [128, 128], BF16)
        triub = sb("triub", [128, 128], BF16)
        nmb = sb("nmb", [128, 4, 128], BF16)
        epsc = sb("epsc", [128, 1])
        invc = sb("invc", [128, 4, 16])
        pv = sb("pv", [128, PV_N])
        rowb = sb("rowb", [128, RB_N])
        snwb = sb("snwb", [128, DI], BF16)
        anegb = sb("anegb", [128, 32])
        cTs = sb("cTs", [128, 8, 17])
        silucT = sb("silucT", [128, 8, 17], BF16)
        modT = sb("modT", [128, 48, 17])
        a1T = sb("a1T", [128, 8, 17])
        a2T = sb("a2T", [128, 8, 17])
        g1B = sb("g1B", [128, D])
        g2B = sb("g2B", [128, D])
        wdt = sb("wdt", [128, 8, 32], BF16)
        diagD = sb("diagD", [128, 16, 128], BF16)
        NWB = 2
        wbuf = [sb("wbuf%d" % i, [128, 4096], BF16) for i in range(NWB)]

        P.op("pool", lambda e: e.memset(identf, 0.0), writes=[R("sB")])
        P.op("pool", lambda e: e.affine_select(out=identf, in_=identf, pattern=[[-1, 128]], compare_op=ALU.not_equal,
                                               fill=1.0, base=0, channel_multiplier=1), reads=[R("sB")], writes=[R("sB")])
        P.op("dve", lambda e: e.tensor_copy(out=identb[:], in_=identf), reads=[R("sB")], writes=[R("identb")])
        P.op("pool", lambda e: e.memset(onesb[:], 1.0), writes=[R("onesb")])
        P.op("pool", lambda e: e.memset(triuf, 1.0), writes=[R("sB")])
        P.op("pool", lambda e: e.affine_select(out=triuf, in_=triuf, pattern=[[1, 128]], compare_op=ALU.is_ge,
                                               fill=0.0, base=0, channel_multiplier=-1), reads=[R("sB")], writes=[R("sB")])
        P.op("dve", lambda e: e.tensor_copy(out=triub[:], in_=triuf), reads=[R("sB")], writes=[R("triub")])
        P.op("pool", lambda e: e.memset(nmf, 0.0), writes=[R("sA")])
        P.op("pool", lambda e: e.affine_select(out=nmf, in_=nmf, pattern=[[0, 4], [1, 128]], compare_op=ALU.is_ge,
                                               fill=-30000.0, base=0, channel_multiplier=-1), reads=[R("sA")], writes=[R("sA")])
        P.op("dve", lambda e: e.tensor_copy(out=nmb[:], in_=nmf), reads=[R("sA")], writes=[R("nmb")])
        P.op("pool", lambda e: e.memset(epsc[:], EPS), writes=[R("epsc")])
        for g in range(4):
            w = 2 ** (g + 1)
            P.op("pool", lambda e, g=g, w=w: e.memset(invc[:, g, :], 1.0 / w), writes=[R("invc")])
            for t in range(w - 1):
                P.op("pool", lambda e, g=g, t=t: e.memset(invc[:, g, t:t + 1], 1.0 / (t + 1)), writes=[R("invc")])

        DMA("sp", pv[:], pvec, "ld_pv", writes=[R("pv")])
        DMA("sp", rowb[:], rowb_in, "ld_rowb", writes=[R("rowb")])
        DMA("sp", cTs[:], cT, "ld_c", writes=[R("cTs")])
        DMA("sp", g1B[:], bg_in[:, 0:D], "ld_g1", writes=[R("g1B", 0, 2)])
        DMA("sp", g2B[:], bg_in[:, D:2 * D], "ld_g2", writes=[R("g2B", 0, 2)])
        DMA("pool", snwb[:], snw_in, "ld_snw", writes=[R("snwb")])
        DMA("pool", wdt[:], w_in[:, :, 7168:7200], "ld_wdt", writes=[R("wdt")])
        P.op("dve", lambda e: e.tensor_tensor(out=diagD[:], in0=identb[:].unsqueeze(1).to_broadcast([128, 16, 128]),
                                              in1=pv[:, PV_DF:PV_DF + 16].unsqueeze(2).to_broadcast([128, 16, 128]), op=ALU.mult),
             reads=[R("identb"), R("pv")], writes=[R("diagD")])

        P.op("act", lambda e: e.activation(out=anegb[:], in_=rowb[:, RB_ALOG:RB_ALOG + 32], func=AF.Exp), reads=[R("rowb")], writes=[R("anegb")])
        P.op("dve", lambda e: e.tensor_scalar(out=anegb[:], in0=anegb[:], scalar1=-1.0, scalar2=None, op0=ALU.mult), reads=[R("anegb")], writes=[R("anegb")])

        wstate = {"i": 0}

        NSCR = 42
        wsc = nc.dram_tensor("wsc", [NSCR, 128, 4096], BF16).ap()
        scr_idx = {}

        def load_w(src_ap, nk, ncol, tid=None):
            i = wstate["i"] % NWB
            wstate["i"] += 1
            wt = wbuf[i]
            view = wt[:, 0:nk * ncol].rearrange("p (k c) -> p k c", k=nk)
            if tid is None:
                DMA("pool", view, src_ap, "w%d" % i, writes=[R("wbuf%d" % i)])
            elif tid not in scr_idx:
                k = len(scr_idx)
                scr_idx[tid] = k
                DMA("pool", view, src_ap, "w%d" % i, writes=[R("wbuf%d" % i)])
                DMA("sp", wsc[k, :, 0:nk * ncol], wt[:, 0:nk * ncol], "ws%d" % i, reads=[R("wbuf%d" % i)], writes=[R("wsc", k, k + 1)])
            else:
                k = scr_idx[tid]
                DMA("sp", wt[:, 0:nk * ncol], wsc[k, :, 0:nk * ncol], "wh%d" % i, reads=[R("wsc", k, k + 1)], writes=[R("wbuf%d" % i)])
            return view, R("wbuf%d" % i)

        rot = {"i": 0}

        def next_bank(banks=(0, 1, 2, 3)):
            b = banks[rot["i"] % len(banks)]
            rot["i"] += 1
            return b

        P.op("act", lambda e: e.activation(out=silucT[:], in_=cTs[:], func=AF.Silu), reads=[R("cTs")], writes=[R("silucT")])
        for t in range(12):
            wt, wr = load_w(w_ada[:, :, t * 512:(t + 1) * 512], 8, 512)
            bk = 4 + (t % 2)
            for j in range(4):
                for kc in range(8):
                    P.op("pe", lambda e, bk=bk, j=j, kc=kc, wt=wt: e.matmul(pb[bk][:, j * 17:(j + 1) * 17], lhsT=wt[:, kc, j * 128:(j + 1) * 128],
                                                                             rhs=silucT[:, kc, :], start=(kc == 0), stop=(kc == 7)),
                         reads=[wr, R("silucT")], writes=[PR(bk, j * 17, j * 17 + 17)])
            P.op("dve", lambda e, bk=bk, t=t: e.tensor_tensor(out=modT[:, 4 * t:4 * t + 4, :], in0=pb[bk][:, 0:68].rearrange("p (a b) -> p a b", a=4),
                                                               in1=pv[:, PV_BADA + 4 * t:PV_BADA + 4 * t + 4].unsqueeze(2).to_broadcast([128, 4, 17]), op=ALU.add),
                 reads=[PR(bk, 0, 68), R("pv")], writes=[R("modT", 4 * t, 4 * t + 4)])
            if t in (4, 5, 10, 11):
                gB = g1B if t < 6 else g2B
                gname = "g1B" if t < 6 else "g2B"
                half = t % 2
                for kc in range(8):
                    P.op("pe", lambda e, kc=kc, wt=wt: e.matmul(pb[6][:, 0:512], lhsT=silucT[:, kc, 0:1].to_broadcast([128, 128]), rhs=wt[:, kc, :],
                                                                  start=(kc == 0), stop=(kc == 7)),
                         reads=[wr, R("silucT")], writes=[PR(6)])
                P.op("dve", lambda e, gB=gB, half=half: e.tensor_tensor(out=gB[:, half * 512:(half + 1) * 512], in0=pb[6][:, 0:512],
                                                                         in1=gB[:, half * 512:(half + 1) * 512], op=ALU.add),
                     reads=[PR(6), R(gname, half, half + 1)], writes=[R(gname, half, half + 1)])
        for (aT, an, sc0, nw0) in ((a1T, "a1T", 8, PV_N1W), (a2T, "a2T", 32, PV_N2W)):
            P.op("dve", lambda e, aT=aT, sc0=sc0: e.tensor_scalar(out=aT[:], in0=modT[:, sc0:sc0 + 8, :], scalar1=1.0, scalar2=None, op0=ALU.add),
                 reads=[R("modT", sc0, sc0 + 8)], writes=[R(an)])
            P.op("dve", lambda e, aT=aT, nw0=nw0: e.tensor_tensor(out=aT[:], in0=aT[:], in1=pv[:, nw0:nw0 + 8].unsqueeze(2).to_broadcast([128, 8, 17]), op=ALU.mult),
                 reads=[R(an), R("pv")], writes=[R(an)])

        xtok = sb("xtok", [128, 4, D])
        ssq = sb("ssq", [128, 8])
        rsq = sb("rsq", [128, 8])
        hT = sb("hT", [128, 8, TB], BF16)
        ubuf = sb("ubuf", [128, 8, 16 + TB])
        gT = sb("gT", [128, 8, TB], BF16)
        gaT = gT
        gbT = gT
        uhist = sb("uhist", [128, 8, 15])
        amT = sb("amT", [128, 8, TB], BF16)
        sz = sb("sz", [128, 4, DI], BF16)
        xpre = [sb("xpre%d" % i, [128, 3 + TB], BF16) for i in range(2)]
        dg = [sb("dg%d" % i, [128, 4, 128], BF16) for i in range(2)]
        chist = sb("chist", [128, 32, 3], BF16)
        cst = sb("cst", [128, 32, 3])
        xbc = sb("xbc", [128, 32, TB], BF16)
        dtx = sb("dtx", [128, 4, 32])
        dtt = sb("dtt", [128, 4, 32])
        dtu = sb("dtu", [128, 4, 32])
        dt_tok = sb("dt_tok", [128, 4, 32])
        dA_bf = sb("dA_bf", [128, 4, 32], BF16)
        negAcs = sb("negAcs", [128, 32])
        Eacs = sb("Eacs", [128, 32])
        dte = sb("dte", [128, 32])
        cdB = sb("cdB", [128, 32])
        xdtb = [sb("xdt%d" % i, [128, DI], BF16) for i in range(2)]
        x_tok = xdtb[1]
        xdt = xdtb[0]
        xdte = sb("xdte", [128, DI], BF16)
        B_tok = sb("B_tok", [128, 1024], BF16)
        CBs = sb("CBs", [128, 8, 128], BF16)
        dec = [sb("dec%d" % i, [128, 8, 128], BF16) for i in range(2)]
        scr = dec
        ybuf = sb("ybuf", [128, DI])
        ytmp = hT[:].rearrange("p a b -> p (a b)").bitcast(F32)
        xn = ybuf[:].bitcast(BF16).rearrange("p (a b) -> p a b", a=4)
        junk = xdte
        pooled = sz[:].rearrange("p a b -> p (a b)")[:, 0:8 * TB].rearrange("p (a b) -> p a b", a=8)
        hst = sb("hst", [128, DI])
        hstb = sb("hstb", [128, DI], BF16)
        yst = [sb("yst%d" % i, [128, D]) for i in range(1)]
        ynT = ubuf[:].rearrange("p a b -> p (a b)").bitcast(BF16)[:, 0:16 * TB].rearrange("p (a b) -> p a b", a=16)
        fT = xbc[:].rearrange("p a b -> p (a b)")[:, 0:22 * TB].rearrange("p (a b) -> p a b", a=22)

        P.op("pool", lambda e: e.memset(chist[:], 0.0), writes=[R("chist", 0, 32)])
        P.op("pool", lambda e: e.memset(hst[:], 0.0), writes=[R("hst", 0, 8)])
        P.op("pool", lambda e: e.memset(hstb[:], 0.0), writes=[R("hstb", 0, 8)])

        def rmsnorm_to_T(blk_name, aT, bcol0, ntok=128, ntile=4):
            for ti in range(ntile):
                P.op("act", lambda e, ti=ti: e.activation(out=junk[:, 0:D], in_=xtok[:, ti, :], func=AF.Square, accum_out=ssq[:, ti:ti + 1]),
                     reads=[R("xtok", ti, ti + 1)], writes=[R("xdte", 0, 8), R("ssq", ti, ti + 1)])
                P.op("act", lambda e, ti=ti: e.activation(out=rsq[:, 4 + ti:5 + ti], in_=ssq[:, ti:ti + 1], func=AF.Ln, scale=1.0 / D, bias=epsc[:, 0:1]),
                     reads=[R("ssq", ti, ti + 1), R("epsc")], writes=[R("rsq", 4 + ti, 5 + ti)])
                P.op("act", lambda e, ti=ti: e.activation(out=rsq[:, ti:ti + 1], in_=rsq[:, 4 + ti:5 + ti], func=AF.Exp, scale=-0.5),
                     reads=[R("rsq", 4 + ti, 5 + ti)], writes=[R("rsq", ti, ti + 1)])
                P.op("dve", lambda e, ti=ti: e.tensor_scalar(out=xn[:, ti, :], in0=xtok[:, ti, :], scalar1=rsq[:, ti:ti + 1], scalar2=None, op0=ALU.mult),
                     reads=[R("xtok", ti, ti + 1), R("rsq", ti, ti + 1)], writes=[R("ybuf", 2 * ti, 2 * ti + 2)])
            for kc in range(8):
                bk = kc // 2
                c0 = (kc % 2) * 512
                for ti in range(ntile):
                    P.op("pe", lambda e, bk=bk, c0=c0, ti=ti, kc=kc: e.transpose(out=PBF(bk)[:, c0 + ti * 128:c0 + (ti + 1) * 128],
                                                                                  in_=xn[:, ti, kc * 128:(kc + 1) * 128], identity=identb[:]),
                         reads=[R("ybuf", 2 * ti, 2 * ti + 2), R("identb")], writes=[PR(bk, c0 // 2 + ti * 64, c0 // 2 + (ti + 1) * 64)])
                P.op("dve", lambda e, bk=bk, c0=c0, kc=kc, aT=aT, bcol0=bcol0: e.tensor_scalar(
                    out=hT[:, kc, :], in0=PBF(bk)[:, c0:c0 + 512], scalar1=aT[:, kc, 0:1], scalar2=modT[:, bcol0 + kc, 0:1], op0=ALU.mult, op1=ALU.add),
                    reads=[PR(bk, c0 // 2, c0 // 2 + 256), R(blk_name), R("modT", bcol0 + kc, bcol0 + kc + 1)], writes=[R("hT", kc, kc + 1)])

        def proj_ws(wt, wr, j, nk, rhs_fn, rhs_regs, bank, ncols=TB, col0=0):
            for kc in range(nk):
                P.op("pe", lambda e, kc=kc: e.matmul(pb[bank][:, 0:ncols], lhsT=wt[:, kc, col0 + j * 128:col0 + (j + 1) * 128], rhs=rhs_fn(kc),
                                                      start=(kc == 0), stop=(kc == nk - 1)),
                     reads=[wr] + rhs_regs(kc), writes=[PR(bank, 0, ncols)])

        def proj_as(wt, wr, ti, nk, lhs_fn, lhs_regs, bank, kc0=0, first=True, last=True, nktot=None):
            for kc in range(nk):
                P.op("pe", lambda e, kc=kc: e.matmul(pb[bank][:, 0:512], lhsT=lhs_fn(kc0 + kc, ti), rhs=wt[:, kc, :],
                                                      start=(first and kc == 0), stop=(last and kc == nk - 1)),
                     reads=[wr] + lhs_regs(kc0 + kc), writes=[PR(bank)])

        hT_rhs = lambda kc: hT[:, kc, :]
        hT_regs = lambda kc: [R("hT", kc, kc + 1)]
        hT_lhs = lambda kc, ti: hT[:, kc, ti * 128:(ti + 1) * 128]

        for tb in range(nblk):
            DMA("sp", xtok[:], xp[tb * TB:(tb + 1) * TB, :].rearrange("(t p) d -> p t d", p=128), "ld_x", writes=[R("xtok", 0, 4)])
            rmsnorm_to_T("a1T", a1T, 0)

            for t in range(2):
                wt, wr = load_w(w_in[:, :, 7200 + t * 512:7200 + (t + 1) * 512], 8, 512, tid=('in', 7200 + t * 512))
                for j in range(4):
                    oc = 4 * t + j
                    bk = next_bank()
                    proj_ws(wt, wr, j, 8, hT_rhs, hT_regs, bk)
                    P.op("act", lambda e, bk=bk, oc=oc: e.activation(out=gaT[:, oc, :], in_=pb[bk][:, 0:TB], func=AF.Sigmoid),
                         reads=[PR(bk)], writes=[R("gT", oc, oc + 1)])
            if tb == 0:
                P.op("pool", lambda e: e.memset(ubuf[:, :, 0:16], 0.0), writes=[R("ubuf", 0, 8)])
            else:
                P.op("pool", lambda e: e.tensor_copy(out=ubuf[:, :, 1:16], in_=uhist[:]), reads=[R("uhist")], writes=[R("ubuf", 0, 8)])
            for t in range(2):
                wt, wr = load_w(w_in[:, :, t * 512:(t + 1) * 512], 8, 512, tid=('in', t * 512))
                for j in range(4):
                    oc = 4 * t + j
                    bk = next_bank()
                    proj_ws(wt, wr, j, 8, hT_rhs, hT_regs, bk)
                    P.op("act", lambda e, bk=bk, oc=oc: e.activation(out=ubuf[:, oc, 16:16 + TB], in_=pb[bk][:, 0:TB], func=AF.Copy),
                         reads=[PR(bk)], writes=[R("ubuf", oc, oc + 1)])
            for oc in range(8):
                g = oc // 2
                w = 2 ** (g + 1)
                U = ubuf[:, oc, :]
                ur = R("ubuf", oc, oc + 1)
                P.op("dve", lambda e, U=U: e.tensor_tensor(out=sA[:, 2:528], in0=U[:, 2:528], in1=U[:, 1:527], op=ALU.add), reads=[ur], writes=[R("sA")])
                cur, curname = sA, "sA"
                if g >= 1:
                    P.op("dve", lambda e: e.tensor_tensor(out=sB[:, 4:528], in0=sA[:, 4:528], in1=sA[:, 2:526], op=ALU.add), reads=[R("sA")], writes=[R("sB")])
                    cur, curname = sB, "sB"
                if g >= 2:
                    P.op("dve", lambda e: e.tensor_tensor(out=sA[:, 8:528], in0=sB[:, 8:528], in1=sB[:, 4:524], op=ALU.add), reads=[R("sB")], writes=[R("sA")])
                    cur, curname = sA, "sA"
                if g >= 3:
                    P.op("dve", lambda e: e.tensor_tensor(out=sB[:, 16:528], in0=sA[:, 16:528], in1=sA[:, 8:520], op=ALU.add), reads=[R("sA")], writes=[R("sB")])
                    cur, curname = sB, "sB"
                P.op("dve", lambda e, cur=cur, U=U, w=w, oc=oc: e.scalar_tensor_tensor(out=pooled[:, oc, :], in0=cur[:, 16:528], scalar=1.0 / w, in1=U[:, 16:528],
                                                                                        op0=ALU.mult, op1=ALU.subtract),
                     reads=[R(curname), ur], writes=[R("sz", 0, 2)])
                if tb == 0:
                    P.op("dve", lambda e, cur=cur, g=g: e.tensor_tensor(out=cur[:, 0:16], in0=cur[:, 16:32], in1=invc[:, g, :], op=ALU.mult),
                         reads=[R(curname), R("invc")], writes=[R(curname)])
                    P.op("dve", lambda e, cur=cur, U=U, oc=oc: e.tensor_tensor(out=pooled[:, oc, 0:16], in0=cur[:, 0:16], in1=U[:, 16:32], op=ALU.subtract),
                         reads=[R(curname), ur], writes=[R("sz", 0, 2)])
            wplt, wplr = load_w(w_pool.rearrange("p g k c -> p (g k) c"), 8, 256, tid=('pool',))
            for g in range(4):
                for j in range(2):
                    oc = 2 * g + j
                    bk = next_bank()
                    for k2 in range(2):
                        P.op("pe", lambda e, bk=bk, g=g, j=j, k2=k2: e.matmul(pb[bk][:, 0:TB], lhsT=wplt[:, 2 * g + k2, j * 128:(j + 1) * 128], rhs=pooled[:, 2 * g + k2, :],
                                                                                start=(k2 == 0), stop=(k2 == 1)),
                             reads=[wplr, R("sz", 0, 2)], writes=[PR(bk)])
                    P.op("dve", lambda e, bk=bk, oc=oc: e.scalar_tensor_tensor(out=amT[:, oc, :], in0=pb[bk][:, 0:TB], scalar=pv[:, PV_PSC + oc:PV_PSC + oc + 1],
                                                                                in1=gaT[:, oc, :], op0=ALU.mult, op1=ALU.mult),
                         reads=[PR(bk), R("pv"), R("gT", oc, oc + 1)], writes=[R("amT", oc, oc + 1)])
            if tb == nblk - 1:
                DMA("sp", o_pp, ubuf[:, :, 513:528], "st_pp", reads=[R("ubuf", 0, 8)], writes=[R("o_pp")])
            else:
                P.op("pool", lambda e: e.tensor_copy(out=uhist[:], in_=ubuf[:, :, 513:528]), reads=[R("ubuf", 0, 8)], writes=[R("uhist")])

            for t in range(4):
                wt, wr = load_w(w_in[:, :, 1024 + t * 512:1024 + (t + 1) * 512], 8, 512, tid=('in', 1024 + t * 512))
                for ti in range(4):
                    bk = next_bank()
                    proj_as(wt, wr, ti, 8, hT_lhs, hT_regs, bk)
                    P.op("act", lambda e, bk=bk, ti=ti, t=t: e.activation(out=sz[:, ti, t * 512:(t + 1) * 512], in_=pb[bk][:, 0:512], func=AF.Silu),
                         reads=[PR(bk)], writes=[R("sz", ti, ti + 1)])
            for t in range(8):
                wt, wr = load_w(w_in[:, :, 3072 + t * 512:3072 + (t + 1) * 512], 8, 512, tid=('in', 3072 + t * 512))
                for j in range(4):
                    oc = 4 * t + j
                    sl = oc % 2
                    bk = next_bank()
                    proj_ws(wt, wr, j, 8, hT_rhs, hT_regs, bk)
                    xpr = R("xpre%d" % sl)
                    P.op("dve", lambda e, sl=sl, oc=oc: e.tensor_copy(out=xpre[sl][:, 0:3], in_=chist[:, oc, :]), reads=[R("chist", oc, oc + 1)], writes=[xpr])
                    P.op("act", lambda e, sl=sl, bk=bk: e.activation(out=xpre[sl][:, 3:3 + TB], in_=pb[bk][:, 0:TB], func=AF.Copy), reads=[PR(bk)], writes=[xpr])
                    if tb == nblk - 1:
                        P.op("dve", lambda e, bk=bk, oc=oc: e.tensor_copy(out=cst[:, oc, :], in_=pb[bk][:, TB - 3:TB]), reads=[PR(bk)], writes=[R("cst", oc, oc + 1)])
                    else:
                        P.op("dve", lambda e, sl=sl, oc=oc: e.tensor_copy(out=chist[:, oc, :], in_=xpre[sl][:, TB:TB + 3]), reads=[xpr], writes=[R("chist", oc, oc + 1)])
                    P.op("pool", lambda e, sl=sl, oc=oc: e.tensor_tensor(out=dg[sl][:], in0=identb[:].unsqueeze(1).to_broadcast([128, 4, 128]),
                                                                          in1=pv[:, PV_CW + oc:PV_CW + oc + 97:32].unsqueeze(2).to_broadcast([128, 4, 128]), op=ALU.mult),
                         reads=[R("identb"), R("pv")], writes=[R("dg%d" % sl, 0, 4)])
                    cbk = 4 + sl
                    for k in range(4):
                        P.op("pe", lambda e, sl=sl, k=k, cbk=cbk: e.matmul(pb[cbk][:, 0:TB], lhsT=dg[sl][:, k, :], rhs=xpre[sl][:, k:k + TB], start=(k == 0), stop=(k == 3)),
                             reads=[R("dg%d" % sl, k, k + 1), xpr], writes=[PR(cbk)])
                    P.op("act", lambda e, cbk=cbk, oc=oc: e.activation(out=xbc[:, oc, :], in_=pb[cbk][:, 0:TB], func=AF.Silu, bias=pv[:, PV_CB + oc:PV_CB + oc + 1], scale=1.0),
                         reads=[PR(cbk), R("pv")], writes=[R("xbc", oc, oc + 1)])
            if tb == nblk - 1:
                DMA("sp", o_cp, cst[:], "st_cp", reads=[R("cst", 0, 32)], writes=[R("o_cp")])
            for ti in range(4):
                for kc in range(8):
                    P.op("pe", lambda e, ti=ti, kc=kc: e.matmul(pb[6][:, ti * 32:(ti + 1) * 32], lhsT=hT[:, kc, ti * 128:(ti + 1) * 128], rhs=wdt[:, kc, :],
                                                                  start=(kc == 0), stop=(kc == 7)),
                         reads=[R("hT", kc, kc + 1), R("wdt")], writes=[PR(6, ti * 32, ti * 32 + 32)])
            P.op("dve", lambda e: e.tensor_tensor(out=dtx[:], in0=pb[6][:, 0:128].rearrange("p (a b) -> p a b", a=4),
                                                  in1=rowb[:, RB_DTB:RB_DTB + 32].unsqueeze(1).to_broadcast([128, 4, 32]), op=ALU.add),
                 reads=[PR(6, 0, 128), R("rowb")], writes=[R("dtx")])
            P.op("act", lambda e: e.activation(out=dtt[:], in_=dtx[:], func=AF.Abs), reads=[R("dtx")], writes=[R("dtt")])
            P.op("act", lambda e: e.activation(out=dtu[:], in_=dtt[:], func=AF.Exp, scale=-1.0), reads=[R("dtt")], writes=[R("dtu")])
            P.op("act", lambda e: e.activation(out=dtt[:], in_=dtu[:], func=AF.Ln, bias=1.0, scale=1.0), reads=[R("dtu")], writes=[R("dtt")])
            P.op("dve", lambda e: e.scalar_tensor_tensor(out=dt_tok[:], in0=dtx[:], scalar=0.0, in1=dtt[:], op0=ALU.max, op1=ALU.add),
                 reads=[R("dtx"), R("dtt")], writes=[R("dt_tok")])
            P.op("dve", lambda e: e.tensor_tensor(out=dA_bf[:], in0=dt_tok[:], in1=anegb[:].unsqueeze(1).to_broadcast([128, 4, 32]), op=ALU.mult),
                 reads=[R("dt_tok"), R("anegb")], writes=[R("dA_bf")])
            for t in range(2):
                wt, wr = load_w(w_in[:, :, 8224 + t * 512:8224 + (t + 1) * 512], 8, 512, tid=('in', 8224 + t * 512))
                for j in range(4):
                    oc = 4 * t + j
                    bk = next_bank()
                    proj_ws(wt, wr, j, 8, hT_rhs, hT_regs, bk)
                    P.op("act", lambda e, bk=bk, oc=oc: e.activation(out=gbT[:, oc, :], in_=pb[bk][:, 0:TB], func=AF.Sigmoid),
                         reads=[PR(bk)], writes=[R("gT", oc, oc + 1)])


            def ssd_acd(ci):
                tsl = slice(ci * 128, (ci + 1) * 128)
                dAc = dA_bf[:, ci, :]
                P.op("pe", lambda e: e.matmul(pb[7][:, 0:32], lhsT=triub[:], rhs=dAc, start=True, stop=True), reads=[R("triub"), R("dA_bf")], writes=[PR(7)])
                P.op("pe", lambda e: e.matmul(pb[7][:, 32:64], lhsT=onesb[:], rhs=dAc, start=True, stop=True), reads=[R("onesb"), R("dA_bf")], writes=[PR(7)])
                P.op("dve", lambda e: e.tensor_scalar(out=negAcs[:], in0=pb[7][:, 0:32], scalar1=-1.0, scalar2=None, op0=ALU.mult), reads=[PR(7)], writes=[R("negAcs")])
                P.op("act", lambda e: e.activation(out=Eacs[:], in_=pb[7][:, 0:32], func=AF.Exp), reads=[PR(7)], writes=[R("Eacs")])
                P.op("dve", lambda e: e.tensor_tensor(out=dte[:], in0=pb[7][:, 32:64], in1=negAcs[:], op=ALU.add), reads=[PR(7), R("negAcs")], writes=[R("dte")])
                P.op("act", lambda e: e.activation(out=dte[:], in_=dte[:], func=AF.Exp), reads=[R("dte")], writes=[R("dte")])
                P.op("act", lambda e: e.activation(out=cdB[:], in_=pb[7][:, 32:64], func=AF.Exp), reads=[PR(7)], writes=[R("cdB")])
                for g in range(NG):
                    P.op("pe", lambda e, g=g: e.transpose(out=PBF(2)[:, g * 128:(g + 1) * 128], in_=xbc[:, 16 + g, tsl], identity=identb[:]),
                         reads=[R("xbc", 16 + g, 17 + g), R("identb")], writes=[PR(2)])
                P.op("act", lambda e: e.activation(out=B_tok[:], in_=PBF(2)[:, 0:1024], func=AF.Copy), reads=[PR(2)], writes=[R("B_tok")])
                for g in range(NG):
                    bk = 3 + g // 4
                    c0 = (g % 4) * 128
                    P.op("pe", lambda e, g=g, bk=bk, c0=c0: e.matmul(pb[bk][:, c0:c0 + 128], lhsT=xbc[:, 16 + g, tsl], rhs=xbc[:, 24 + g, tsl], start=True, stop=True),
                         reads=[R("xbc", 16 + g, 17 + g), R("xbc", 24 + g, 25 + g)], writes=[PR(bk)])
                for q in range(2):
                    P.op("act", lambda e, q=q: e.activation(out=CBs[:, q * 4:(q + 1) * 4, :], in_=pb[3 + q][:, 0:512].rearrange("p (a b) -> p a b", a=4), func=AF.Copy),
                         reads=[PR(3 + q)], writes=[R("CBs", q * 4, q * 4 + 4)])

            def ssd_b(ci):
                tsl = slice(ci * 128, (ci + 1) * 128)
                xd = xdtb[ci % 2]
                xr = R("xdt%d" % (ci % 2))
                for fc in range(16):
                    bk = fc // 8
                    c0 = (fc % 8) * 128
                    P.op("pe", lambda e, bk=bk, c0=c0, fc=fc: e.transpose(out=PBF(bk)[:, c0:c0 + 128], in_=xbc[:, fc, tsl], identity=identb[:]),
                         reads=[R("xbc", fc, fc + 1), R("identb")], writes=[PR(bk)])
                for bk in range(2):
                    P.op("dve", lambda e, bk=bk: e.tensor_tensor(out=xd[:, bk * 1024:(bk + 1) * 1024].rearrange("p (h q) -> p h q", h=16),
                                                                 in0=PBF(bk)[:, 0:1024].rearrange("p (h q) -> p h q", h=16),
                                                                 in1=dt_tok[:, ci, bk * 16:(bk + 1) * 16].unsqueeze(2).to_broadcast([128, 16, HP]), op=ALU.mult),
                         reads=[PR(bk), R("dt_tok")], writes=[xr])
                P.op("dve", lambda e: e.tensor_tensor(out=xdte[:].rearrange("p (h q) -> p h q", h=NH), in0=xd[:].rearrange("p (h q) -> p h q", h=NH),
                                                      in1=dte[:].unsqueeze(2).to_broadcast([128, NH, HP]), op=ALU.mult),
                     reads=[xr, R("dte")], writes=[R("xdte", 0, 8)])

            def ssd_decay(ci, hq):
                sl = hq % 2
                dbanks = (5, 6) if hq % 2 == 0 else (3, 4)
                for half in range(2):
                    bk = dbanks[half]
                    for j in range(4):
                        h = hq * 8 + half * 4 + j
                        P.op("pe", lambda e, bk=bk, j=j: e.matmul(pb[bk][:, j * 128:(j + 1) * 128], lhsT=identb[:], rhs=nmb[:, 0, :], start=True, stop=False),
                             reads=[R("identb"), R("nmb")], writes=[PR(bk)])
                        P.op("pe", lambda e, bk=bk, j=j, h=h: e.matmul(pb[bk][:, j * 128:(j + 1) * 128], lhsT=dA_bf[:, ci, h:h + 1].to_broadcast([128, 128]),
                                                                        rhs=triub[:], start=False, stop=True),
                             reads=[R("dA_bf"), R("triub")], writes=[PR(bk)])
                    for j in range(4):
                        h = hq * 8 + half * 4 + j
                        P.op("act", lambda e, bk=bk, j=j, h=h, half=half: e.activation(out=dec[sl][:, half * 4 + j, :], in_=pb[bk][:, j * 128:(j + 1) * 128],
                                                                                       func=AF.Exp, bias=negAcs[:, h:h + 1], scale=1.0),
                             reads=[PR(bk), R("negAcs")], writes=[R("dec%d" % sl, half * 4 + j, half * 4 + j + 1)])

            def ssd_y(ci, hq):
                tsl = slice(ci * 128, (ci + 1) * 128)
                sl = hq % 2
                P.op("dve", lambda e: e.tensor_tensor(out=scr[sl][:].rearrange("p (g j) l -> p g j l", g=2), in0=dec[sl][:].rearrange("p (g j) l -> p g j l", g=2),
                                                      in1=CBs[:, 2 * hq:2 * hq + 2, :].unsqueeze(2).to_broadcast([128, 2, 4, 128]), op=ALU.mult),
                     reads=[R("dec%d" % sl, 0, 8), R("CBs", 2 * hq, 2 * hq + 2)], writes=[R("dec%d" % sl, 0, 8)])
                bA = (0, 2)[hq % 2]
                bB = (1, 7)[hq % 2]
                for jj in range(8):
                    h = 8 * hq + jj
                    P.op("pe", lambda e, jj=jj, h=h: e.matmul(pb[bA][:, jj * 64:(jj + 1) * 64], lhsT=xbc[:, h // 2, tsl], rhs=diagD[:, h // 2, (h % 2) * 64:(h % 2 + 1) * 64], start=True, stop=False),
                         reads=[R("xbc", h // 2, h // 2 + 1), R("diagD")], writes=[PR(bA)])
                    P.op("pe", lambda e, jj=jj, h=h: e.matmul(pb[bA][:, jj * 64:(jj + 1) * 64], lhsT=scr[sl][:, jj, :], rhs=xdtb[ci % 2][:, h * 64:(h + 1) * 64], start=False, stop=True),
                         reads=[R("dec%d" % sl, jj, jj + 1), R("xdt%d" % (ci % 2))], writes=[PR(bA)])
                for gg in range(2):
                    g = 2 * hq + gg
                    P.op("pe", lambda e, g=g, gg=gg: e.matmul(pb[bB][:, gg * 256:(gg + 1) * 256], lhsT=xbc[:, 24 + g, tsl], rhs=hstb[:, g * 256:(g + 1) * 256], start=True, stop=True),
                         reads=[R("xbc", 24 + g, 25 + g), R("hstb", g, g + 1)], writes=[PR(bB)])

            def ssd_ye(ci, hq):
                bA = (0, 2)[hq % 2]
                bB = (1, 7)[hq % 2]
                ysl = slice(hq * 512, (hq + 1) * 512)
                yr = R("ybuf", 2 * hq, 2 * hq + 2)
                P.op("dve", lambda e: e.tensor_tensor(out=ybuf[:, ysl].rearrange("p (h q) -> p h q", h=8), in0=pb[bB][:, 0:512].rearrange("p (h q) -> p h q", h=8),
                                                      in1=Eacs[:, 8 * hq:8 * hq + 8].unsqueeze(2).to_broadcast([128, 8, HP]), op=ALU.mult),
                     reads=[PR(bB), R("Eacs")], writes=[yr])
                P.op("dve", lambda e: e.tensor_tensor(out=ybuf[:, ysl], in0=pb[bA][:, 0:512], in1=ybuf[:, ysl], op=ALU.add), reads=[PR(bA), yr], writes=[yr])
                P.op("dve", lambda e: e.tensor_tensor(out=ybuf[:, ysl], in0=ybuf[:, ysl], in1=sz[:, ci, ysl], op=ALU.mult), reads=[yr, R("sz", ci, ci + 1)], writes=[yr])

            def ssd_g(ci):
                for q in range(4):
                    sbk = 3 + q
                    hsl = slice(q * 512, (q + 1) * 512)
                    hr = R("hst", 2 * q, 2 * q + 2)
                    for gg in range(2):
                        g = 2 * q + gg
                        P.op("pe", lambda e, g=g, gg=gg, sbk=sbk: e.matmul(pb[sbk][:, gg * 256:(gg + 1) * 256], lhsT=B_tok[:, g * 128:(g + 1) * 128], rhs=xdte[:, g * 256:(g + 1) * 256], start=True, stop=True),
                             reads=[R("B_tok"), R("xdte", g, g + 1)], writes=[PR(sbk)])
                    P.op("pool", lambda e, q=q, hsl=hsl: e.tensor_tensor(out=hst[:, hsl].rearrange("p (h q) -> p h q", h=8), in0=hst[:, hsl].rearrange("p (h q) -> p h q", h=8),
                                                                         in1=cdB[:, 8 * q:8 * q + 8].unsqueeze(2).to_broadcast([128, 8, HP]), op=ALU.mult),
                         reads=[hr, R("cdB")], writes=[hr])
                    P.op("dve", lambda e, sbk=sbk, hsl=hsl: e.tensor_tensor(out=hst[:, hsl], in0=pb[sbk][:, 0:512], in1=hst[:, hsl], op=ALU.add), reads=[PR(sbk), hr], writes=[hr])
                    P.op("act", lambda e, hsl=hsl: e.activation(out=hstb[:, hsl], in_=hst[:, hsl], func=AF.Copy), reads=[hr], writes=[R("hstb", 2 * q, 2 * q + 2)])

            def ssd_h(ci):
                tsl = slice(ci * 128, (ci + 1) * 128)
                ynb2 = xdtb[ci % 2]
                xr = R("xdt%d" % (ci % 2))
                P.op("act", lambda e: e.activation(out=ynb2[:], in_=ybuf[:], func=AF.Square, accum_out=ssq[:, 4:5]), reads=[R("ybuf", 0, 8)], writes=[xr, R("ssq", 4, 5)])
                P.op("act", lambda e: e.activation(out=ssq[:, 5:6], in_=ssq[:, 4:5], func=AF.Ln, scale=1.0 / DI, bias=epsc[:, 0:1]), reads=[R("ssq", 4, 5), R("epsc")], writes=[R("ssq", 5, 6)])
                P.op("act", lambda e: e.activation(out=ssq[:, 6:7], in_=ssq[:, 5:6], func=AF.Exp, scale=-0.5), reads=[R("ssq", 5, 6)], writes=[R("ssq", 6, 7)])
                P.op("dve", lambda e: e.scalar_tensor_tensor(out=ynb2[:], in0=ybuf[:], scalar=ssq[:, 6:7], in1=snwb[:], op0=ALU.mult, op1=ALU.mult),
                     reads=[R("ybuf", 0, 8), R("ssq", 6, 7), R("snwb")], writes=[xr])
                for fc in range(16):
                    bk = fc // 8
                    c0 = (fc % 8) * 128
                    P.op("pe", lambda e, bk=bk, c0=c0, fc=fc: e.transpose(out=PBF(bk)[:, c0:c0 + 128], in_=ynb2[:, fc * 128:(fc + 1) * 128], identity=identb[:]),
                         reads=[xr, R("identb")], writes=[PR(bk)])
                for q in range(2):
                    P.op("act", lambda e, q=q: e.activation(out=ynT[:, q * 8:(q + 1) * 8, tsl], in_=PBF(q)[:, 0:1024].rearrange("p (a b) -> p a b", a=8), func=AF.Copy),
                         reads=[PR(q)], writes=[R("ubuf", 0, 8)])

            ssd_acd(0)
            ssd_b(0)
            for ci in range(4):
                if ci == 0:
                    ssd_decay(ci, 0)
                if ci == 0:
                    ssd_decay(ci, 1)
                ssd_y(ci, 0)
                for hq in range(4):
                    if hq + 2 < 4:
                        ssd_decay(ci, hq + 2)
                    if hq + 1 < 4:
                        ssd_y(ci, hq + 1)
                    ssd_ye(ci, hq)
                ssd_g(ci)
                if ci + 1 < 4:
                    ssd_acd(ci + 1)
                    ssd_b(ci + 1)
                    ssd_decay(ci + 1, 0)
                    ssd_decay(ci + 1, 1)
                ssd_h(ci)
            if tb == nblk - 1:
                DMA("sp", o_sp, hst[:], "st_sp", reads=[R("hst", 0, 8)], writes=[R("o_sp")])

            for t in range(4):
                wt, wr = load_w(w_ssd[:, :, t * 256:(t + 1) * 256], 16, 256, tid=('ssd', t))
                for j in range(2):
                    oc = 2 * t + j
                    bk = next_bank()
                    proj_ws(wt, wr, j, 16, lambda kc: ynT[:, kc, :], lambda kc: [R("ubuf", 0, 8)], bk)
                    P.op("dve", lambda e, bk=bk, oc=oc: e.tensor_tensor(out=sA[:, 0:TB], in0=pb[bk][:, 0:TB], in1=gbT[:, oc, :], op=ALU.mult),
                         reads=[PR(bk), R("gT", oc, oc + 1)], writes=[R("sA")])
                    P.op("dve", lambda e, oc=oc: e.tensor_tensor(out=hT[:, oc, :], in0=sA[:, 0:TB], in1=amT[:, oc, :], op=ALU.add),
                         reads=[R("sA"), R("amT", oc, oc + 1)], writes=[R("hT", oc, oc + 1)])
            for t in range(2):
                wt, wr = load_w(w_out[:, :, t * 512:(t + 1) * 512], 8, 512, tid=('out', t))
                for ti in range(4):
                    bk = next_bank()
                    proj_as(wt, wr, ti, 8, hT_lhs, hT_regs, bk)
                    P.op("dve", lambda e, bk=bk, t=t: e.tensor_tensor(out=sB[:, 0:512], in0=pb[bk][:, 0:512], in1=g1B[:, t * 512:(t + 1) * 512], op=ALU.mult),
                         reads=[PR(bk), R("g1B", t, t + 1)], writes=[R("sB")])
                    P.op("dve", lambda e, ti=ti, t=t: e.tensor_tensor(out=xtok[:, ti, t * 512:(t + 1) * 512], in0=xtok[:, ti, t * 512:(t + 1) * 512], in1=sB[:, 0:512], op=ALU.add),
                         reads=[R("sB"), R("xtok", ti, ti + 1)], writes=[R("xtok", ti, ti + 1)])
            rmsnorm_to_T("a2T", a2T, 24)
            for t in range(11):
                wt, wr = load_w(w_ffi[:, :, t * 512:(t + 1) * 512], 8, 512, tid=('ffi', t))
                for j in range(2):
                    fc = 2 * t + j
                    bg = next_bank()
                    proj_ws(wt, wr, j, 8, hT_rhs, hT_regs, bg)
                    bu = next_bank()
                    proj_ws(wt, wr, j, 8, hT_rhs, hT_regs, bu, col0=256)
                    P.op("act", lambda e, bg=bg: e.activation(out=sA[:, 0:TB], in_=pb[bg][:, 0:TB], func=AF.Silu), reads=[PR(bg)], writes=[R("sA")])
                    P.op("dve", lambda e, bu=bu, fc=fc: e.tensor_tensor(out=fT[:, fc, :], in0=pb[bu][:, 0:TB], in1=sA[:, 0:TB], op=ALU.mult),
                         reads=[PR(bu), R("sA")], writes=[R("xbc", fc, fc + 1)])
            fT_lhs = lambda kc, ti: fT[:, kc, ti * 128:(ti + 1) * 128]
            fT_regs = lambda kc: [R("xbc", kc, kc + 1)]
            for half in range(2):
                for kg in range(3):
                    nk = 8 if kg < 2 else 6
                    wt, wr = load_w(w_ffo[:, kg * 8:kg * 8 + nk, half * 512:(half + 1) * 512], nk, 512, tid=('ffo', kg, half))
                    for ti in range(4):
                        proj_as(wt, wr, ti, nk, fT_lhs, fT_regs, 4 + ti, kc0=kg * 8, first=(kg == 0), last=(kg == 2))
                for ti in range(4):
                    P.op("dve", lambda e, ti=ti, half=half: e.tensor_tensor(out=sB[:, 0:512], in0=pb[4 + ti][:, 0:512], in1=g2B[:, half * 512:(half + 1) * 512], op=ALU.mult),
                         reads=[PR(4 + ti), R("g2B", half, half + 1)], writes=[R("sB")])
                    P.op("dve", lambda e, ti=ti, half=half: e.tensor_tensor(out=xtok[:, ti, half * 512:(half + 1) * 512], in0=xtok[:, ti, half * 512:(half + 1) * 512], in1=sB[:, 0:512], op=ALU.add),
                         reads=[R("sB"), R("xtok", ti, ti + 1)], writes=[R("xtok", ti, ti + 1)])
            for ti in range(4):
                ys = 0
                P.op("act", lambda e, ti=ti: e.activation(out=junk[:, 0:D], in_=xtok[:, ti, :], func=AF.Square, accum_out=ssq[:, ti:ti + 1]),
                     reads=[R("xtok", ti, ti + 1)], writes=[R("xdte", 0, 8), R("ssq", ti, ti + 1)])
                P.op("act", lambda e, ti=ti: e.activation(out=rsq[:, 4 + ti:5 + ti], in_=ssq[:, ti:ti + 1], func=AF.Ln, scale=1.0 / D, bias=epsc[:, 0:1]),
                     reads=[R("ssq", ti, ti + 1), R("epsc")], writes=[R("rsq", 4 + ti, 5 + ti)])
                P.op("act", lambda e, ti=ti: e.activation(out=rsq[:, ti:ti + 1], in_=rsq[:, 4 + ti:5 + ti], func=AF.Exp, scale=-0.5), reads=[R("rsq", 4 + ti, 5 + ti)], writes=[R("rsq", ti, ti + 1)])
                P.op("dve", lambda e, ti=ti, ys=ys: e.scalar_tensor_tensor(out=yst[ys][:], in0=xtok[:, ti, :], scalar=rsq[:, ti:ti + 1], in1=rowb[:, RB_FNW:RB_FNW + D], op0=ALU.mult, op1=ALU.mult),
                     reads=[R("xtok", ti, ti + 1), R("rsq", ti, ti + 1), R("rowb")], writes=[R("yst%d" % ys)])
                r0 = tb * TB + ti * 128
                DMA("sp", o_y[r0:r0 + 128, :], yst[ys][:], "st_y%d" % ys, reads=[R("yst%d" % ys)], writes=[R("o_y", tb * 4 + ti, tb * 4 + ti + 1)])


        if do_sample:
            Ssl = slice(1, 17)
            xbcF = xbc[:].rearrange("p a b -> p (a b)")
            stbuf = [xtok[:, 0:2, :].rearrange("p a b -> p (a b)"), xtok[:, 2:4, :].rearrange("p a b -> p (a b)")]
            streg = [R("xtok", 0, 2), R("xtok", 2, 4)]
            ubF = ubuf[:].rearrange("p a b -> p (a b)")
            stp = ubF[:, 0:1920].rearrange("p (c s r) -> p c s r", c=8, s=NS)
            newst = ubF[:, 1920:3840].rearrange("p (c s r) -> p c s r", c=8, s=NS)
            stc = ybuf[:, 0:1536].rearrange("p (c s r) -> p c s r", c=32, s=NS)
            newcst = hst[:, 0:1536].rearrange("p (c s r) -> p c s r", c=32, s=NS)
            gTf = gT[:].rearrange("p a b -> p (a b)").bitcast(F32)
            projS = gTf[:, 0:56 * NS].rearrange("p (c s) -> p c s", c=56)
            amF = amT[:].rearrange("p a b -> p (a b)").bitcast(F32)
            acc1 = amF[:, 0:512].rearrange("p (c s) -> p c s", c=32)
            acc2 = amF[:, 512:1024].rearrange("p (c s) -> p c s", c=32)
            amS = amF[:, 1024:1152].rearrange("p (c s) -> p c s", c=8)
            gaS = amF[:, 1152:1280].rearrange("p (c s) -> p c s", c=8)
            gbS = amF[:, 1280:1408].rearrange("p (c s) -> p c s", c=8)
            ptmp = amF[:, 1408:1536].rearrange("p (c s) -> p c s", c=8)
            sgS = amF[:, 1536:1568]
            xbcS = B_tok[:, 0:512].rearrange("p (c s) -> p c s", c=32)
            CBf = CBs[:].rearrange("p a b -> p (a b)")
            hTs = CBf[:, 0:128].rearrange("p (c s) -> p c s", c=8)
            mixTs = CBf[:, 128:256].rearrange("p (c s) -> p c s", c=8)
            pooledS = CBf[:, 256:384].rearrange("p (c s) -> p c s", c=8)
            ynTs = CBf[:, 384:640].rearrange("p (c s) -> p c s", c=16)
            fTs = CBf[:, 640:992].rearrange("p (c s) -> p c s", c=22)
            x_tokS = x_tok[0:NS, :]
            xdt_tokS = xdt[0:NS, :]
            szS = xdte[0:NS, :]
            xs_tok = yst[0][0:NS, :]
            szf = sz[:].rearrange("p a b -> p (a b)").bitcast(F32)
            ysS = szf[0:NS, 0:2048]
            gs1 = szf[0:NS, 2048:3072]
            gs2 = szf[0:NS, 3072:4096]
            xnS = dec[0][:].rearrange("p a b -> p (a b)")[0:NS, :]
            hTF = hT[:].rearrange("p a b -> p (a b)")
            ynS = hTF[0:NS, 0:2048]
            junkS = hTF[0:NS, 2048:4096]
            tmpDx = hTF[0:NS, :].bitcast(F32)
            decBs = sA[:, 0:512].rearrange("p (s h) -> p s h", s=NS)
            mask16 = sB[:, 0:256].rearrange("p (a b) -> p a b", a=NS)
            identfS = sB[:, 256:384]
            CmaskS = xbcF[:, 0:2048].rearrange("p (g s m) -> p g s m", g=NG, s=NS)
            ssS, rsS = ssq[0:NS, :], rsq[0:NS, :]
            RCB = R("CBs", 0, 8)
            RAM = R("amT", 0, 8)

            DMA("sp", xs_tok, xsm, "ld_xs", writes=[R("yst0")])
            DMA("sp", stp, st_pool, "ld_stp", writes=[R("ubuf", 0, 8)])
            DMA("sp", stc, st_conv, "ld_stc", writes=[R("ybuf", 0, 8)])
            P.op("pool", lambda e: e.memset(sB[:, 0:384], 0.0), writes=[R("sB")])
            P.op("pool", lambda e: e.affine_select(out=mask16, in_=mask16, pattern=[[1, NS], [-1, NS]], compare_op=ALU.not_equal, fill=1.0, base=0, channel_multiplier=0),
                 reads=[R("sB")], writes=[R("sB")])
            P.op("pool", lambda e: e.affine_select(out=identfS, in_=identfS, pattern=[[-1, 128]], compare_op=ALU.not_equal, fill=1.0, base=0, channel_multiplier=1),
                 reads=[R("sB")], writes=[R("sB")])
            for (gsv, c0, b0) in ((gs1, 16, 0), (gs2, 40, 2)):
                for c in range(8):
                    bk = b0 + c // 4
                    P.op("pe", lambda e, bk=bk, c=c, c0=c0: e.matmul(pb[bk][0:NS, (c % 4) * 128:(c % 4 + 1) * 128], lhsT=modT[:, c0 + c, Ssl], rhs=identfS, start=True, stop=True),
                         reads=[R("modT", c0 + c, c0 + c + 1), R("sB")], writes=[PR(bk)])
                for q in range(2):
                    P.op("act", lambda e, gsv=gsv, q=q, b0=b0: e.activation(out=gsv[:, q * 512:(q + 1) * 512], in_=pb[b0 + q][0:NS, 0:512], func=AF.Copy),
                         reads=[PR(b0 + q)], writes=[R("sz", 0, 4)])

            def s_norm_T(aT, bcol0):
                P.op("act", lambda e: e.activation(out=junkS[:, 0:D], in_=xs_tok, func=AF.Square, accum_out=ssS[:, 0:1]), reads=[R("yst0")], writes=[R("hT", 0, 8), R("ssq", 0, 8)])
                P.op("act", lambda e: e.activation(out=rsS[:, 4:5], in_=ssS[:, 0:1], func=AF.Ln, scale=1.0 / D, bias=epsc[0:NS, 0:1]), reads=[R("ssq", 0, 8), R("epsc")], writes=[R("rsq", 0, 8)])
                P.op("act", lambda e: e.activation(out=rsS[:, 0:1], in_=rsS[:, 4:5], func=AF.Exp, scale=-0.5), reads=[R("rsq", 0, 8)], writes=[R("rsq", 0, 8)])
                P.op("dve", lambda e: e.tensor_scalar(out=xnS, in0=xs_tok, scalar1=rsS[:, 0:1], scalar2=None, op0=ALU.mult), reads=[R("yst0"), R("rsq", 0, 8)], writes=[R("dec0", 0, 8)])
                for kc in range(8):
                    P.op("pe", lambda e, kc=kc: e.transpose(out=PBF(0)[:, kc * NS:(kc + 1) * NS], in_=xnS[:, kc * 128:(kc + 1) * 128], identity=identb[0:NS, 0:NS]),
                         reads=[R("dec0", 0, 8), R("identb")], writes=[PR(0)])
                P.op("dve", lambda e, aT=aT: e.tensor_tensor(out=ptmp, in0=PBF(0)[:, 0:128].rearrange("p (c s) -> p c s", c=8), in1=aT[:, :, Ssl], op=ALU.mult),
                     reads=[PR(0), R("a1T"), R("a2T")], writes=[RAM])
                P.op("dve", lambda e, bcol0=bcol0: e.tensor_tensor(out=hTs, in0=ptmp, in1=modT[:, bcol0:bcol0 + 8, Ssl], op=ALU.add),
                     reads=[RAM, R("modT", bcol0, bcol0 + 8)], writes=[RCB])

            s_norm_T(a1T, 0)
            ws_tiles = [(0, 0), (512, 4)] + [(3072 + 512 * t, 8 + 4 * t) for t in range(8)] + [(7200 + 512 * t, 40 + 4 * t) for t in range(4)]
            for (col0, cb0) in ws_tiles:
                wt, wr = load_w(w_in[:, :, col0:col0 + 512], 8, 512, tid=('in', col0))
                for j in range(4):
                    for kc in range(8):
                        P.op("pe", lambda e, j=j, kc=kc, wt=wt: e.matmul(pb[1][:, j * NS:(j + 1) * NS], lhsT=wt[:, kc, j * 128:(j + 1) * 128], rhs=hTs[:, kc, :], start=(kc == 0), stop=(kc == 7)),
                             reads=[wr, RCB], writes=[PR(1)])
                P.op("act", lambda e, cb0=cb0: e.activation(out=projS[:, cb0:cb0 + 4, :], in_=pb[1][:, 0:4 * NS].rearrange("p (c s) -> p c s", c=4), func=AF.Copy),
                     reads=[PR(1)], writes=[R("gT", 0, 8)])
            for t in range(4):
                wt, wr = load_w(w_in[:, :, 1024 + t * 512:1024 + (t + 1) * 512], 8, 512, tid=('in', 1024 + t * 512))
                for kc in range(8):
                    P.op("pe", lambda e, kc=kc, wt=wt: e.matmul(pb[2][0:NS, 0:512], lhsT=hTs[:, kc, :], rhs=wt[:, kc, :], start=(kc == 0), stop=(kc == 7)),
                         reads=[wr, RCB], writes=[PR(2)])
                P.op("act", lambda e, t=t: e.activation(out=szS[:, t * 512:(t + 1) * 512], in_=pb[2][0:NS, 0:512], func=AF.Silu), reads=[PR(2)], writes=[R("xdte", 0, 8)])
            for kc in range(8):
                P.op("pe", lambda e, kc=kc: e.matmul(pb[3][0:NS, 0:32], lhsT=hTs[:, kc, :], rhs=wdt[:, kc, :], start=(kc == 0), stop=(kc == 7)), reads=[RCB, R("wdt")], writes=[PR(3)])
            d_x, d_t, d_u, d_dt, d_dec = dtx[0:NS, 0, :], dtt[0:NS, 0, :], dtu[0:NS, 0, :], dt_tok[0:NS, 0, :], dtx[0:NS, 1, :]
            P.op("dve", lambda e: e.tensor_tensor(out=d_x, in0=pb[3][0:NS, 0:32], in1=rowb[0:NS, RB_DTB:RB_DTB + 32], op=ALU.add), reads=[PR(3), R("rowb")], writes=[R("dtx")])
            P.op("act", lambda e: e.activation(out=d_t, in_=d_x, func=AF.Abs), reads=[R("dtx")], writes=[R("dtt")])
            P.op("act", lambda e: e.activation(out=d_u, in_=d_t, func=AF.Exp, scale=-1.0), reads=[R("dtt")], writes=[R("dtu")])
            P.op("act", lambda e: e.activation(out=d_t, in_=d_u, func=AF.Ln, bias=1.0, scale=1.0), reads=[R("dtu")], writes=[R("dtt")])
            P.op("dve", lambda e: e.scalar_tensor_tensor(out=d_dt, in0=d_x, scalar=0.0, in1=d_t, op0=ALU.max, op1=ALU.add), reads=[R("dtx"), R("dtt")], writes=[R("dt_tok")])
            P.op("dve", lambda e: e.tensor_tensor(out=d_u, in0=d_dt, in1=anegb[0:NS, :], op=ALU.mult), reads=[R("dt_tok"), R("anegb")], writes=[R("dtu")])
            P.op("act", lambda e: e.activation(out=d_dec, in_=d_u, func=AF.Exp), reads=[R("dtu")], writes=[R("dtx")])
            for s_ in range(NS):
                P.op("pe", lambda e, s_=s_: e.matmul(pb[0][:, s_ * 32:(s_ + 1) * 32], lhsT=identfS[0:NS, s_:s_ + 1].to_broadcast([NS, 128]), rhs=d_dec, start=True, stop=True),
                     reads=[R("sB"), R("dtx")], writes=[PR(0)])
            P.op("act", lambda e: e.activation(out=sA[:, 0:512], in_=pb[0][:, 0:512], func=AF.Copy), reads=[PR(0)], writes=[R("sA")])
            for g in range(4):
                w = 2 ** (g + 1)
                ug = projS[:, 2 * g:2 * g + 2, :]
                P.op("dve", lambda e, g=g, w=w: e.reduce_sum(out=ptmp[:, 0:2, :], in_=stp[:, 2 * g:2 * g + 2, :, 15 - (w - 1):15], axis=mybir.AxisListType.X),
                     reads=[R("ubuf", 0, 8)], writes=[RAM])
                P.op("dve", lambda e, ug=ug: e.tensor_tensor(out=ptmp[:, 0:2, :], in0=ptmp[:, 0:2, :], in1=ug, op=ALU.add), reads=[RAM, R("gT", 0, 8)], writes=[RAM])
                P.op("dve", lambda e, ug=ug, g=g, w=w: e.scalar_tensor_tensor(out=pooledS[:, 2 * g:2 * g + 2, :], in0=ptmp[:, 0:2, :], scalar=1.0 / w, in1=ug, op0=ALU.mult, op1=ALU.subtract),
                     reads=[RAM, R("gT", 0, 8)], writes=[RCB])
            P.op("act", lambda e: e.activation(out=gaS, in_=projS[:, 40:48, :], func=AF.Sigmoid), reads=[R("gT", 0, 8)], writes=[RAM])
            P.op("act", lambda e: e.activation(out=gbS, in_=projS[:, 48:56, :], func=AF.Sigmoid), reads=[R("gT", 0, 8)], writes=[RAM])
            wplt, wplr = load_w(w_pool.rearrange("p g k c -> p (g k) c"), 8, 256, tid=('pool',))
            for g in range(4):
                for j in range(2):
                    oc = 2 * g + j
                    for k2 in range(2):
                        P.op("pe", lambda e, g=g, j=j, k2=k2, oc=oc: e.matmul(pb[2][:, oc * NS:(oc + 1) * NS], lhsT=wplt[:, 2 * g + k2, j * 128:(j + 1) * 128], rhs=pooledS[:, 2 * g + k2, :], start=(k2 == 0), stop=(k2 == 1)),
                             reads=[wplr, RCB], writes=[PR(2)])
            for oc in range(8):
                P.op("dve", lambda e, oc=oc: e.scalar_tensor_tensor(out=amS[:, oc, :], in0=pb[2][:, oc * NS:(oc + 1) * NS], scalar=pv[:, PV_PSC + oc:PV_PSC + oc + 1], in1=gaS[:, oc, :], op0=ALU.mult, op1=ALU.mult),
                     reads=[PR(2), R("pv"), RAM], writes=[RAM])
            P.op("pool", lambda e: e.tensor_copy(out=newst[:, :, :, 0:14], in_=stp[:, :, :, 1:15]), reads=[R("ubuf", 0, 8)], writes=[R("ubuf", 0, 8)])
            P.op("pool", lambda e: e.tensor_copy(out=newst[:, :, :, 14], in_=projS[:, 0:8, :]), reads=[R("gT", 0, 8)], writes=[R("ubuf", 0, 8)])
            DMA("sp", o_ps, newst, "st_ps", reads=[R("ubuf", 0, 8)], writes=[R("o_ps")])
            xnew = projS[:, 8:40, :]
            cwb = lambda k: pv[:, PV_CW + 32 * k:PV_CW + 32 * k + 32].unsqueeze(2).to_broadcast([128, 32, NS])
            P.op("dve", lambda e: e.tensor_tensor(out=acc1, in0=xnew, in1=cwb(3), op=ALU.mult), reads=[R("gT", 0, 8), R("pv")], writes=[RAM])
            for k in range(3):
                P.op("dve", lambda e, k=k: e.tensor_tensor(out=acc2, in0=stc[:, :, :, k], in1=cwb(k), op=ALU.mult), reads=[R("ybuf", 0, 8), R("pv")], writes=[RAM])
                P.op("dve", lambda e: e.tensor_tensor(out=acc1, in0=acc1, in1=acc2, op=ALU.add), reads=[RAM], writes=[RAM])
            P.op("dve", lambda e: e.tensor_tensor(out=acc1, in0=acc1, in1=pv[:, PV_CB:PV_CB + 32].unsqueeze(2).to_broadcast([128, 32, NS]), op=ALU.add), reads=[RAM, R("pv")], writes=[RAM])
            P.op("act", lambda e: e.activation(out=xbcS, in_=acc1, func=AF.Silu), reads=[RAM], writes=[R("B_tok")])
            P.op("pool", lambda e: e.tensor_copy(out=newcst[:, :, :, 0:2], in_=stc[:, :, :, 1:3]), reads=[R("ybuf", 0, 8)], writes=[R("hst", 0, 8)])
            P.op("pool", lambda e: e.tensor_copy(out=newcst[:, :, :, 2], in_=xnew), reads=[R("gT", 0, 8)], writes=[R("hst", 0, 8)])
            DMA("sp", o_cs, newcst, "st_cs", reads=[R("hst", 0, 8)], writes=[R("o_cs")])
            for fc in range(16):
                bk = 1 + fc // 8
                P.op("pe", lambda e, fc=fc, bk=bk: e.transpose(out=PBF(bk)[0:NS, (fc % 8) * 128:(fc % 8 + 1) * 128], in_=xbcS[:, fc, :], identity=identb[:]),
                     reads=[R("B_tok"), R("identb")], writes=[PR(bk)])
            for q in range(2):
                P.op("act", lambda e, q=q: e.activation(out=x_tokS[:, q * 1024:(q + 1) * 1024], in_=PBF(1 + q)[0:NS, 0:1024], func=AF.Copy), reads=[PR(1 + q)], writes=[R("xdt1")])
            P.op("dve", lambda e: e.tensor_tensor(out=xdt_tokS.rearrange("p (h q) -> p h q", h=NH), in0=x_tokS.rearrange("p (h q) -> p h q", h=NH),
                                                  in1=d_dt.unsqueeze(2).to_broadcast([NS, NH, HP]), op=ALU.mult), reads=[R("xdt1"), R("dt_tok")], writes=[R("xdt0")])
            P.op("dve", lambda e: e.tensor_tensor(out=CmaskS, in0=xbcS[:, 24:32, :].unsqueeze(3).to_broadcast([128, NG, NS, NS]),
                                                  in1=mask16.unsqueeze(1).to_broadcast([128, NG, NS, NS]), op=ALU.mult), reads=[R("B_tok"), R("sB")], writes=[R("xbc", 0, 32)])
            P.op("pool", lambda e: e.memset(ysS, 0.0), writes=[R("sz", 0, 4)])
            def samp_L(s_):
                sl = s_ % 2
                DMA("sp", stbuf[sl], st_ssm[s_], "ld_st%d" % sl, writes=[streg[sl]])

            def samp_A(s_):
                sl = s_ % 2
                buf = stbuf[sl]
                for q in range(4):
                    P.op("pe", lambda e, q=q: e.matmul(pb[q][:, 0:512], lhsT=identb[0:NS, s_:s_ + 1].to_broadcast([NS, 128]), rhs=xdt_tokS[:, q * 512:(q + 1) * 512], start=True, stop=True),
                         reads=[R("identb"), R("xdt0")], writes=[PR(q)])
                P.op("pool", lambda e: e.tensor_tensor(out=buf.rearrange("p (h q) -> p h q", h=NH), in0=buf.rearrange("p (h q) -> p h q", h=NH),
                                                       in1=decBs[:, s_, :].unsqueeze(2).to_broadcast([128, NH, HP]), op=ALU.mult), reads=[streg[sl], R("sA")], writes=[streg[sl]])
                for g in range(NG):
                    P.op("dve", lambda e, g=g: e.scalar_tensor_tensor(out=buf[:, g * 256:(g + 1) * 256], in0=pb[g // 2][:, (g % 2) * 256:(g % 2 + 1) * 256],
                                                                       scalar=xbcS[:, 16 + g, s_:s_ + 1], in1=buf[:, g * 256:(g + 1) * 256], op0=ALU.mult, op1=ALU.add),
                         reads=[PR(g // 2), R("B_tok"), streg[sl]], writes=[streg[sl]])
                P.op("act", lambda e: e.activation(out=hstb[:], in_=buf, func=AF.Copy), reads=[streg[sl]], writes=[R("hstb", 0, 8)])
                DMA("sp", o_ss[s_], buf, "st_ss%d" % sl, reads=[streg[sl]], writes=[R("o_ss", s_, s_ + 1)])

            def samp_A2(s_):
                for g in range(NG):
                    P.op("pe", lambda e, g=g: e.matmul(pb[4 + g // 2][0:NS, (g % 2) * 256:(g % 2 + 1) * 256], lhsT=CmaskS[:, g, s_, :], rhs=hstb[:, g * 256:(g + 1) * 256], start=True, stop=True),
                         reads=[R("xbc", 0, 32), R("hstb", 0, 8)], writes=[PR(4 + g // 2)])

            def samp_B(s_):
                for q in range(4):
                    P.op("dve", lambda e, q=q: e.tensor_tensor(out=ysS[:, q * 512:(q + 1) * 512], in0=pb[4 + q][0:NS, 0:512], in1=ysS[:, q * 512:(q + 1) * 512], op=ALU.add),
                         reads=[PR(4 + q), R("sz", 0, 4)], writes=[R("sz", 0, 4)])

            samp_L(0)
            samp_L(1)
            samp_A(0)
            samp_A2(0)
            for s_ in range(NS):
                if s_ + 2 < NS:
                    samp_L(s_ + 2)
                if s_ + 1 < NS:
                    samp_A(s_ + 1)
                samp_B(s_)
                if s_ + 1 < NS:
                    samp_A2(s_ + 1)
            P.op("dve", lambda e: e.tensor_tensor(out=tmpDx.rearrange("p (h q) -> p h q", h=NH), in0=x_tokS.rearrange("p (h q) -> p h q", h=NH),
                                                  in1=rowb[0:NS, RB_DSK:RB_DSK + 32].unsqueeze(2).to_broadcast([NS, NH, HP]), op=ALU.mult), reads=[R("xdt1"), R("rowb")], writes=[R("hT", 0, 8)])
            P.op("dve", lambda e: e.tensor_tensor(out=ysS, in0=ysS, in1=tmpDx, op=ALU.add), reads=[R("sz", 0, 4), R("hT", 0, 8)], writes=[R("sz", 0, 4)])
            P.op("dve", lambda e: e.tensor_tensor(out=ysS, in0=ysS, in1=szS, op=ALU.mult), reads=[R("sz", 0, 4), R("xdte", 0, 8)], writes=[R("sz", 0, 4)])
            P.op("act", lambda e: e.activation(out=junkS, in_=ysS, func=AF.Square, accum_out=ssS[:, 1:2]), reads=[R("sz", 0, 4)], writes=[R("hT", 0, 8), R("ssq", 0, 8)])
            P.op("act", lambda e: e.activation(out=rsS[:, 5:6], in_=ssS[:, 1:2], func=AF.Ln, scale=1.0 / DI, bias=epsc[0:NS, 0:1]), reads=[R("ssq", 0, 8), R("epsc")], writes=[R("rsq", 0, 8)])
            P.op("act", lambda e: e.activation(out=rsS[:, 1:2], in_=rsS[:, 5:6], func=AF.Exp, scale=-0.5), reads=[R("rsq", 0, 8)], writes=[R("rsq", 0, 8)])
            P.op("dve", lambda e: e.scalar_tensor_tensor(out=ynS, in0=ysS, scalar=rsS[:, 1:2], in1=snwb[0:NS, :], op0=ALU.mult, op1=ALU.mult),
                 reads=[R("sz", 0, 4), R("rsq", 0, 8), R("snwb")], writes=[R("hT", 0, 8)])
            for fc in range(16):
                P.op("pe", lambda e, fc=fc: e.transpose(out=PBF(0)[:, fc * NS:(fc + 1) * NS], in_=ynS[:, fc * 128:(fc + 1) * 128], identity=identb[0:NS, 0:NS]),
                     reads=[R("hT", 0, 8), R("identb")], writes=[PR(0)])
            P.op("act", lambda e: e.activation(out=ynTs, in_=PBF(0)[:, 0:256].rearrange("p (c s) -> p c s", c=16), func=AF.Copy), reads=[PR(0)], writes=[RCB])
            for t in range(4):
                wt, wr = load_w(w_ssd[:, :, t * 256:(t + 1) * 256], 16, 256, tid=('ssd', t))
                for j in range(2):
                    oc = 2 * t + j
                    for kc in range(16):
                        P.op("pe", lambda e, j=j, kc=kc, oc=oc, wt=wt: e.matmul(pb[1][:, oc * NS:(oc + 1) * NS], lhsT=wt[:, kc, j * 128:(j + 1) * 128], rhs=ynTs[:, kc, :], start=(kc == 0), stop=(kc == 15)),
                             reads=[wr, RCB], writes=[PR(1)])
            P.op("dve", lambda e: e.tensor_tensor(out=ptmp, in0=pb[1][:, 0:128].rearrange("p (c s) -> p c s", c=8), in1=gbS, op=ALU.mult), reads=[PR(1), RAM], writes=[RAM])
            P.op("dve", lambda e: e.tensor_tensor(out=mixTs, in0=ptmp, in1=amS, op=ALU.add), reads=[RAM], writes=[RCB])
            tmpR = ysS[:, 0:512]
            for t in range(2):
                wt, wr = load_w(w_out[:, :, t * 512:(t + 1) * 512], 8, 512, tid=('out', t))
                for kc in range(8):
                    P.op("pe", lambda e, kc=kc, wt=wt, t=t: e.matmul(pb[2 + t][0:NS, 0:512], lhsT=mixTs[:, kc, :], rhs=wt[:, kc, :], start=(kc == 0), stop=(kc == 7)), reads=[wr, RCB], writes=[PR(2 + t)])
                P.op("dve", lambda e, t=t: e.tensor_tensor(out=tmpR, in0=pb[2 + t][0:NS, 0:512], in1=gs1[:, t * 512:(t + 1) * 512], op=ALU.mult), reads=[PR(2 + t), R("sz", 0, 4)], writes=[R("sz", 0, 4)])
                P.op("dve", lambda e, t=t: e.tensor_tensor(out=xs_tok[:, t * 512:(t + 1) * 512], in0=xs_tok[:, t * 512:(t + 1) * 512], in1=tmpR, op=ALU.add), reads=[R("sz", 0, 4), R("yst0")], writes=[R("yst0")])
            s_norm_T(a2T, 24)
            for t in range(11):
                wt, wr = load_w(w_ffi[:, :, t * 512:(t + 1) * 512], 8, 512, tid=('ffi', t))
                for q in range(4):
                    for kc in range(8):
                        P.op("pe", lambda e, q=q, kc=kc, wt=wt: e.matmul(pb[1][:, q * NS:(q + 1) * NS], lhsT=wt[:, kc, q * 128:(q + 1) * 128], rhs=hTs[:, kc, :], start=(kc == 0), stop=(kc == 7)),
                             reads=[wr, RCB], writes=[PR(1)])
                P.op("act", lambda e: e.activation(out=sgS, in_=pb[1][:, 0:2 * NS], func=AF.Silu), reads=[PR(1)], writes=[RAM])
                P.op("dve", lambda e, t=t: e.tensor_tensor(out=fTs[:, 2 * t:2 * t + 2, :], in0=pb[1][:, 2 * NS:4 * NS].rearrange("p (c s) -> p c s", c=2), in1=sgS.rearrange("p (c s) -> p c s", c=2), op=ALU.mult),
                     reads=[PR(1), RAM], writes=[RCB])
            for half in range(2):
                for kg in range(3):
                    nk = 8 if kg < 2 else 6
                    wt, wr = load_w(w_ffo[:, kg * 8:kg * 8 + nk, half * 512:(half + 1) * 512], nk, 512, tid=('ffo', kg, half))
                    for kc in range(nk):
                        P.op("pe", lambda e, kc=kc, kg=kg, nk=nk, wt=wt, half=half: e.matmul(pb[2 + half][0:NS, 0:512], lhsT=fTs[:, kg * 8 + kc, :], rhs=wt[:, kc, :], start=(kg == 0 and kc == 0), stop=(kg == 2 and kc == nk - 1)),
                             reads=[wr, RCB], writes=[PR(2 + half)])
                P.op("dve", lambda e, half=half: e.tensor_tensor(out=tmpR, in0=pb[2 + half][0:NS, 0:512], in1=gs2[:, half * 512:(half + 1) * 512], op=ALU.mult), reads=[PR(2 + half), R("sz", 0, 4)], writes=[R("sz", 0, 4)])
                P.op("dve", lambda e, half=half: e.tensor_tensor(out=xs_tok[:, half * 512:(half + 1) * 512], in0=xs_tok[:, half * 512:(half + 1) * 512], in1=tmpR, op=ALU.add), reads=[R("sz", 0, 4), R("yst0")], writes=[R("yst0")])
            P.op("act", lambda e: e.activation(out=junkS[:, 0:D], in_=xs_tok, func=AF.Square, accum_out=ssS[:, 2:3]), reads=[R("yst0")], writes=[R("hT", 0, 8), R("ssq", 0, 8)])
            P.op("act", lambda e: e.activation(out=rsS[:, 6:7], in_=ssS[:, 2:3], func=AF.Ln, scale=1.0 / D, bias=epsc[0:NS, 0:1]), reads=[R("ssq", 0, 8), R("epsc")], writes=[R("rsq", 0, 8)])
            P.op("act", lambda e: e.activation(out=rsS[:, 2:3], in_=rsS[:, 6:7], func=AF.Exp, scale=-0.5), reads=[R("rsq", 0, 8)], writes=[R("rsq", 0, 8)])
            P.op("dve", lambda e: e.scalar_tensor_tensor(out=xs_tok, in0=xs_tok, scalar=rsS[:, 2:3], in1=rowb[0:NS, RB_FNW:RB_FNW + D], op0=ALU.mult, op1=ALU.mult),
                 reads=[R("yst0"), R("rsq", 0, 8), R("rowb")], writes=[R("yst0")])
            DMA("sp", o_ys, xs_tok, "st_ys", reads=[R("yst0")], writes=[R("o_ys")])

        P.op("sp", None, reads=[R("o_y", 0, 4 * NBLK), R("o_pp"), R("o_cp"), R("o_sp"), R("o_ys"), R("o_ps"), R("o_cs"), R("o_ss", 0, NS)])

        P.analyze()
        sems_e = {e: es.enter_context(nc.semaphore("se_" + e)) for e in ENGS}
        sems_d = {k: es.enter_context(nc.semaphore("sd_" + k)) for k in sorted(dma_keys)}
        P.emit(sems_e, sems_d)
    return nc


def _tile_k(w):
    K, N = w.shape
    return np.ascontiguousarray(w.reshape(K // 128, 128, N).transpose(1, 0, 2))


def _fm(v):
    return np.ascontiguousarray(v.reshape(-1, 128).T)


_NC_CACHE = {}


def kernel(x_prompt, x_sample, c_prompt, c_sample, state_pool, state_conv, state_ssm, w_ada, b_ada, norm1_w,
           w_in, w_pool, pool_scale, conv_w, conv_b, dt_bias, A_log, D_skip, ssd_norm_w, w_ssd_proj, w_out,
           norm2_w, w_ffn_in, w_ffn_out, final_norm_w):
    f = np.float32
    n = 8
    x_prompt = np.asarray(x_prompt, f)
    pvec = np.zeros((128, PV_N), f)
    pvec[:, PV_N1W:PV_N1W + 8] = _fm(np.asarray(norm1_w[0], f))
    pvec[:, PV_PSC:PV_PSC + 8] = _fm(np.asarray(pool_scale[0], f))
    cw = np.asarray(conv_w[0], f)
    for k in range(4):
        pvec[:, PV_CW + 32 * k:PV_CW + 32 * k + 32] = _fm(cw[k])
    pvec[:, PV_CB:PV_CB + 32] = _fm(np.asarray(conv_b[0], f))
    pvec[:, PV_N2W:PV_N2W + 8] = _fm(np.asarray(norm2_w[0], f))
    pvec[:, PV_BADA:PV_BADA + 48] = _fm(np.asarray(b_ada[0], f))
    pvec[:, PV_DF:PV_DF + 16] = _fm(np.repeat(np.asarray(D_skip[0], f), HP))
    rowb = np.zeros((128, RB_N), f)
    rowb[:, RB_FNW:RB_FNW + D] = np.asarray(final_norm_w, f)[None, :]
    bg = np.zeros((128, 2 * D), f)
    bg[:, 0:D] = np.asarray(b_ada[0], f)[None, 2 * D:3 * D]
    bg[:, D:2 * D] = np.asarray(b_ada[0], f)[None, 5 * D:6 * D]
    rowb[:, RB_DSK:RB_DSK + 32] = np.asarray(D_skip[0], f)[None, :]
    rowb[:, RB_ALOG:RB_ALOG + 32] = np.asarray(A_log[0], f)[None, :]
    rowb[:, RB_DTB:RB_DTB + 32] = np.asarray(dt_bias[0], f)[None, :]
    snw = np.ascontiguousarray(np.broadcast_to(np.asarray(ssd_norm_w[0], f)[None, :], (128, DI)))
    w_ada_t = _tile_k(np.asarray(w_ada[0], f))
    w_in_t = _tile_k(np.asarray(w_in[0], f))
    wp = np.asarray(w_pool[0], f)
    w_pool_t = np.ascontiguousarray(np.stack([_tile_k(wp[g]) for g in range(4)], axis=1))
    w_ssd_t = _tile_k(np.asarray(w_ssd_proj[0], f))
    w_out_t = _tile_k(np.asarray(w_out[0], f))
    wfi = np.asarray(w_ffn_in[0], f)
    perm = np.concatenate([np.concatenate([np.arange(256 * t, 256 * t + 256), DFF + np.arange(256 * t, 256 * t + 256)]) for t in range(11)])
    w_ffi_t = _tile_k(np.ascontiguousarray(wfi[:, perm]))
    w_ffo_t = _tile_k(np.asarray(w_ffn_out[0], f))

    in_maps = []
    for b in range(n):
        s0, s1 = NS * b, NS * (b + 1)
        c17 = np.concatenate([np.asarray(c_prompt[b:b + 1], f), np.asarray(c_sample[s0:s1], f)], axis=0)
        cT = np.ascontiguousarray(c17.T.reshape(8, 128, 17).transpose(1, 0, 2))
        sp = np.asarray(state_pool[0, s0:s1], f)
        sp_t = np.ascontiguousarray(sp.reshape(NS, 15, 8, 128).transpose(3, 2, 0, 1))
        sc = np.asarray(state_conv[0, s0:s1], f)
        sc_t = np.ascontiguousarray(sc.reshape(NS, 3, 32, 128).transpose(3, 2, 0, 1))
        ss = np.asarray(state_ssm[0, s0:s1], f)
        ss_t = np.ascontiguousarray(ss.reshape(NS, DI, DST).transpose(0, 2, 1))
        in_maps.append({
            "xp": np.ascontiguousarray(x_prompt[b]),
            "xsm": np.ascontiguousarray(np.asarray(x_sample[s0:s1, 0], f)),
            "cT": cT, "pvec": pvec, "rowb": rowb, "snw": snw, "bg": bg,
            "w_ada": w_ada_t, "w_in": w_in_t, "w_pool": w_pool_t, "w_ssd": w_ssd_t, "w_out": w_out_t,
            "w_ffi": w_ffi_t, "w_ffo": w_ffo_t,
            "st_pool": sp_t, "st_conv": sc_t, "st_ssm": ss_t,
        })
    nblk = int(os.environ.get("K_NBLK", NBLK))
    ncores = int(os.environ.get("K_CORES", n))
    if "nc" not in _NC_CACHE:
        _NC_CACHE["nc"] = build_nc(nblk=nblk)
    nc = _NC_CACHE["nc"]
    res = run_bass_kernel_spmd(nc, in_maps[:ncores], core_ids=list(range(ncores)))
    rs = list(res.results)
    while len(rs) < n:
        rs.append({k: np.zeros_like(v) for k, v in rs[0].items()})
    y_prompt = np.stack([rs[b]["o_y"] for b in range(n)], axis=0)
    y_sample = np.concatenate([rs[b]["o_ys"] for b in range(n)], axis=0)[:, None, :]
    pool_p = np.stack([rs[b]["o_pp"].transpose(2, 1, 0).reshape(15, D) for b in range(n)], axis=0)[None]
    conv_p = np.stack([rs[b]["o_cp"].transpose(2, 1, 0).reshape(3, CONV) for b in range(n)], axis=0)[None]
    ssm_p = np.stack([rs[b]["o_sp"].T.reshape(NH, HP, DST) for b in range(n)], axis=0)[None]
    pool_s = np.concatenate([rs[b]["o_ps"].transpose(2, 3, 1, 0).reshape(NS, 15, D) for b in range(n)], axis=0)[None]
    conv_s = np.concatenate([rs[b]["o_cs"].transpose(2, 3, 1, 0).reshape(NS, 3, CONV) for b in range(n)], axis=0)[None]
    ssm_s = np.concatenate([rs[b]["o_ss"].transpose(0, 2, 1).reshape(NS, NH, HP, DST) for b in range(n)], axis=0)[None]
    return (np.ascontiguousarray(y_prompt, dtype=f), np.ascontiguousarray(y_sample, dtype=f),
            np.ascontiguousarray(pool_p, dtype=f), np.ascontiguousarray(conv_p, dtype=f),
            np.ascontiguousarray(ssm_p, dtype=f), np.ascontiguousarray(pool_s, dtype=f),
            np.ascontiguousarray(conv_s, dtype=f), np.ascontiguousarray(ssm_s, dtype=f))
```

```python
import os
from contextlib import ExitStack

import numpy as np
import concourse.bass as bass
import concourse.mybir as mybir
from concourse.bass_utils import run_bass_kernel_spmd

F32 = mybir.dt.float32
BF16 = mybir.dt.bfloat16
AF = mybir.ActivationFunctionType
ALU = mybir.AluOpType

ENGS = ("pe", "act", "dve", "pool", "sp")

D = 1024
SEQ = 2048
TB = 512
NBLK = SEQ // TB
NS = 16
DI = 2048
NH = 32
HP = 64
NG = 8
DST = 128
CONV = 4096
DFF = 2816
IN_COLS = 9248
EPS = 1e-6

PV_N1W, PV_PSC, PV_CW, PV_CB, PV_N2W, PV_BADA, PV_DF, PV_N = 0, 8, 16, 144, 176, 184, 232, 248
RB_FNW, RB_DSK, RB_ALOG, RB_DTB, RB_N = 0, 1024, 1056, 1088, 1120


class Op:
    __slots__ = ("eng", "fn", "reads", "writes", "dkey", "dgroup", "idx", "waits",
                 "sig", "cnt", "eidx")

    def __init__(self, eng, fn, reads, writes, dkey=None, dgroup=None):
        self.eng = eng
        self.fn = fn
        self.reads = reads
        self.writes = writes
        self.dkey = dkey
        self.dgroup = dgroup
        self.waits = {}
        self.sig = False
        self.cnt = 0


class Prog:
    def __init__(self, nc):
        self.nc = nc
        self.ops = []

    def op(self, eng, fn, reads=(), writes=()):
        o = Op(eng, fn, list(reads), list(writes))
        o.idx = len(self.ops)
        self.ops.append(o)
        return o

    def dma(self, eng, fn, reads=(), writes=(), key=None, group=None):
        o = Op(eng, fn, list(reads), list(writes), dkey=key, dgroup=group)
        o.idx = len(self.ops)
        self.ops.append(o)
        return o

    def analyze(self):
        ops = self.ops
        ecount = {e: 0 for e in ENGS}
        for o in ops:
            o.eidx = ecount[o.eng]
            ecount[o.eng] += 1
        key_ops = {}
        for o in ops:
            if o.dkey is not None:
                key_ops.setdefault(o.dkey, []).append(o)
        dma_cum, dma_prev = {}, {}
        for k, lst in key_ops.items():
            groups = []
            for o in lst:
                if groups and o.dgroup is not None and groups[-1][0] == o.dgroup:
                    groups[-1][1].append(o)
                else:
                    groups.append((o.dgroup, [o]))
            cum = 0
            for g, gl in groups:
                prev = cum
                cum += len(gl)
                for o in gl:
                    dma_cum[o.idx] = cum
                    dma_prev[o.idx] = prev
        self.keys = sorted(key_ops.keys())
        recs = {}
        waited = {}
        pend = []
        for o in ops:
            d = set()
            for (buf, lo, hi) in o.reads:
                for r in recs.get(buf, ()):
                    if r[3] and r[0] < hi and lo < r[1]:
                        d.add(r[2])
            for (buf, lo, hi) in o.writes:
                for r in recs.get(buf, ()):
                    if r[0] < hi and lo < r[1]:
                        d.add(r[2])
            d.discard(o.idx)
            for (buf, lo, hi) in o.writes:
                lst = recs.setdefault(buf, [])
                lst[:] = [r for r in lst if not (lo <= r[0] and r[1] <= hi)]
                lst.append([lo, hi, o.idx, True])
            for (buf, lo, hi) in o.reads:
                lst = recs.setdefault(buf, [])
                lst[:] = [r for r in lst if not ((not r[3]) and ops[r[2]].eng == o.eng
                                                 and ops[r[2]].dkey is None and o.dkey is None
                                                 and lo <= r[0] and r[1] <= hi)]
                lst.append([lo, hi, o.idx, False])
            need = {}
            for di in d:
                p = ops[di]
                if p.dkey is not None:
                    sk = ("d", p.dkey)
                    val = 16 * dma_cum[p.idx]
                    if need.get(sk, 0) < val:
                        need[sk] = val
                    continue
                if p.eng == o.eng and o.dkey is None:
                    if o.eng in ("pe", "sp"):
                        continue
                sk = ("e", p.eng)
                cur = need.get(sk)
                if cur is None or cur.eidx < p.eidx:
                    need[sk] = p
            if o.dkey is not None and dma_prev[o.idx] > 0:
                sk = ("d", o.dkey)
                val = 16 * dma_prev[o.idx]
                if need.get(sk, 0) < val:
                    need[sk] = val
            o.waits = need
            for sk, v in need.items():
                if sk[0] == "e":
                    v.sig = True
        cnt = {e: 0 for e in ENGS}
        for o in ops:
            if o.dkey is None and o.sig:
                cnt[o.eng] += 1
            o.cnt = cnt[o.eng]
        for o in ops:
            final = {}
            for sk, v in o.waits.items():
                val = v.cnt if sk[0] == "e" else v
                wk = (o.eng, sk)
                if waited.get(wk, 0) >= val:
                    continue
                waited[wk] = val
                final[sk] = val
            o.waits = final

    def emit(self, sems_e, sems_d):
        nc = self.nc
        per = {e: [o for o in self.ops if o.eng == e] for e in ENGS}

        def run(engname, eng):
            for o in per[engname]:
                for sk, val in o.waits.items():
                    sem = sems_e[sk[1]] if sk[0] == "e" else sems_d[sk[1]]
                    eng.wait_ge(sem, val)
                if o.fn is None:
                    continue
                ins = o.fn(eng)
                if o.dkey is not None:
                    ins.then_inc(sems_d[o.dkey], 16)
                elif o.sig:
                    ins.then_inc(sems_e[o.eng], 1)

        with nc.Block() as block:
            @block.tensor
            def _(e):
                run("pe", e)

            @block.scalar
            def _(e):
                run("act", e)

            @block.vector
            def _(e):
                run("dve", e)

            @block.gpsimd
            def _(e):
                run("pool", e)

            @block.sync
            def _(e):
                run("sp", e)


def R(name, lo=0, hi=1):
    return (name, lo, hi)


def build_nc(nblk=NBLK, do_sample=True):
    nc = bass.Bass("TRN2", target_bir_lowering=False)

    def din(name, shape):
        return nc.dram_tensor(name, list(shape), F32, kind="ExternalInput").ap()

    def dout(name, shape):
        return nc.dram_tensor(name, list(shape), F32, kind="ExternalOutput").ap()

    xp = din("xp", [SEQ, D])
    xsm = din("xsm", [NS, D])
    cT = din("cT", [128, 8, 17])
    pvec = din("pvec", [128, PV_N])
    rowb_in = din("rowb", [128, RB_N])
    snw_in = din("snw", [128, DI])
    bg_in = din("bg", [128, 2 * D])
    w_ada = din("w_ada", [128, 8, 6 * D])
    w_in = din("w_in", [128, 8, IN_COLS])
    w_pool = din("w_pool", [128, 4, 2, 256])
    w_ssd = din("w_ssd", [128, 16, D])
    w_out = din("w_out", [128, 8, D])
    w_ffi = din("w_ffi", [128, 8, 2 * DFF])
    w_ffo = din("w_ffo", [128, 22, D])
    st_pool = din("st_pool", [128, 8, NS, 15])
    st_conv = din("st_conv", [128, 32, NS, 3])
    st_ssm = din("st_ssm", [NS, 128, DI])
    o_y = dout("o_y", [SEQ, D])
    o_ys = dout("o_ys", [NS, D])
    o_pp = dout("o_pp", [128, 8, 15])
    o_cp = dout("o_cp", [128, 32, 3])
    o_sp = dout("o_sp", [128, DI])
    o_ps = dout("o_ps", [128, 8, NS, 15])
    o_cs = dout("o_cs", [128, 32, NS, 3])
    o_ss = dout("o_ss", [NS, 128, DI])
    dbg_out = {}

    es = ExitStack()
    with es:
        def sb(name, shape, dt=F32):
            return es.enter_context(nc.sbuf_tensor("s_" + name, list(shape), dt))

        P = Prog(nc)
        dma_keys = set()

        def DMA(eng, out, in_, key, reads=(), writes=(), group=None):
            dma_keys.add(key)
            P.dma(eng, lambda e: e.dma_start(out=out, in_=in_), reads=reads, writes=writes, key=key, group=group)

        pb = [es.enter_context(nc.psum_tensor("pb%d" % i, [128, 512], F32)) for i in range(8)]

        def PBF(i):
            return pb[i][:].bitcast(BF16)

        def PR(i, lo=0, hi=512):
            return ("pb%d" % i, 0, 512)

        sA = sb("sA", [128, 16 + TB])
        sB = sb("sB", [128, 16 + TB])
        identf = sB[:, 0:128]
        triuf = sB[:, 128:256]
        nmf = sA[:, 0:512].rearrange("p (a b) -> p a b", a=4)
        identb = sb("identb", [128, 128], BF16)
        onesb = sb("onesb", [128, 128], BF16)
        triub = sb("triub", [128, 128], BF16)
        nmb = sb("nmb", [128, 4, 128], BF16)
        epsc = sb("epsc", [128, 1])
        invc = sb("invc", [128, 4, 16])
        pv = sb("pv", [128, PV_N])
        rowb = sb("rowb", [128, RB_N])
        snwb = sb("snwb", [128, DI], BF16)
        anegb = sb("anegb", [128, 32])
        cTs = sb("cTs", [128, 8, 17])
        silucT = sb("silucT", [128, 8, 17], BF16)
        modT = sb("modT", [128, 48, 17])
        a1T = sb("a1T", [128, 8, 17])
        a2T = sb("a2T", [128, 8, 17])
        g1B = sb("g1B", [128, D])
        g2B = sb("g2B", [128, D])
        wdt = sb("wdt", [128, 8, 32], BF16)
        diagD = sb("diagD", [128, 16, 128], BF16)
        NWB = 2
        wbuf = [sb("wbuf%d" % i, [128, 4096], BF16) for i in range(NWB)]

        P.op("pool", lambda e: e.memset(identf, 0.0), writes=[R("sB")])
        P.op("pool", lambda e: e.affine_select(out=identf, in_=identf, pattern=[[-1, 128]], compare_op=ALU.not_equal,
                                               fill=1.0, base=0, channel_multiplier=1), reads=[R("sB")], writes=[R("sB")])
        P.op("dve", lambda e: e.tensor_copy(out=identb[:], in_=identf), reads=[R("sB")], writes=[R("identb")])
        P.op("pool", lambda e: e.memset(onesb[:], 1.0), writes=[R("onesb")])
        P.op("pool", lambda e: e.memset(triuf, 1.0), writes=[R("sB")])
        P.op("pool", lambda e: e.affine_select(out=triuf, in_=triuf, pattern=[[1, 128]], compare_op=ALU.is_ge,
                                               fill=0.0, base=0, channel_multiplier=-1), reads=[R("sB")], writes=[R("sB")])
        P.op("dve", lambda e: e.tensor_copy(out=triub[:], in_=triuf), reads=[R("sB")], writes=[R("triub")])
        P.op("pool", lambda e: e.memset(nmf, 0.0), writes=[R("sA")])
        P.op("pool", lambda e: e.affine_select(out=nmf, in_=nmf, pattern=[[0, 4], [1, 128]], compare_op=ALU.is_ge,
                                               fill=-30000.0, base=0, channel_multiplier=-1), reads=[R("sA")], writes=[R("sA")])
        P.op("dve", lambda e: e.tensor_copy(out=nmb[:], in_=nmf), reads=[R("sA")], writes=[R("nmb")])
        P.op("pool", lambda e: e.memset(epsc[:], EPS), writes=[R("epsc")])
        for g in range(4):
            w = 2 ** (g + 1)
            P.op("pool", lambda e, g=g, w=w: e.memset(invc[:, g, :], 1.0 / w), writes=[R("invc")])
            for t in range(w - 1):
                P.op("pool", lambda e, g=g, t=t: e.memset(invc[:, g, t:t + 1], 1.0 / (t + 1)), writes=[R("invc")])

        DMA("sp", pv[:], pvec, "ld_pv", writes=[R("pv")])
        DMA("sp", rowb[:], rowb_in, "ld_rowb", writes=[R("rowb")])
        DMA("sp", cTs[:], cT, "ld_c", writes=[R("cTs")])
        DMA("sp", g1B[:], bg_in[:, 0:D], "ld_g1", writes=[R("g1B", 0, 2)])
        DMA("sp", g2B[:], bg_in[:, D:2 * D], "ld_g2", writes=[R("g2B", 0, 2)])
        DMA("pool", snwb[:], snw_in, "ld_snw", writes=[R("snwb")])
        DMA("pool", wdt[:], w_in[:, :, 7168:7200], "ld_wdt", writes=[R("wdt")])
        P.op("dve", lambda e: e.tensor_tensor(out=diagD[:], in0=identb[:].unsqueeze(1).to_broadcast([128, 16, 128]),
                                              in1=pv[:, PV_DF:PV_DF + 16].unsqueeze(2).to_broadcast([128, 16, 128]), op=ALU.mult),
             reads=[R("identb"), R("pv")], writes=[R("diagD")])

        P.op("act", lambda e: e.activation(out=anegb[:], in_=rowb[:, RB_ALOG:RB_ALOG + 32], func=AF.Exp), reads=[R("rowb")], writes=[R("anegb")])
        P.op("dve", lambda e: e.tensor_scalar(out=anegb[:], in0=anegb[:], scalar1=-1.0, scalar2=None, op0=ALU.mult), reads=[R("anegb")], writes=[R("anegb")])

        wstate = {"i": 0}

        NSCR = 42
        wsc = nc.dram_tensor("wsc", [NSCR, 128, 4096], BF16).ap()
        scr_idx = {}

        def load_w(src_ap, nk, ncol, tid=None):
            i = wstate["i"] % NWB
            wstate["i"] += 1
            wt = wbuf[i]
            view = wt[:, 0:nk * ncol].rearrange("p (k c) -> p k c", k=nk)
            if tid is None:
                DMA("pool", view, src_ap, "w%d" % i, writes=[R("wbuf%d" % i)])
            elif tid not in scr_idx:
                k = len(scr_idx)
                scr_idx[tid] = k
                DMA("pool", view, src_ap, "w%d" % i, writes=[R("wbuf%d" % i)])
                DMA("sp", wsc[k, :, 0:nk * ncol], wt[:, 0:nk * ncol], "ws%d" % i, reads=[R("wbuf%d" % i)], writes=[R("wsc", k, k + 1)])
            else:
                k = scr_idx[tid]
                DMA("sp", wt[:, 0:nk * ncol], wsc[k, :, 0:nk * ncol], "wh%d" % i, reads=[R("wsc", k, k + 1)], writes=[R("wbuf%d" % i)])
            return view, R("wbuf%d" % i)

        rot = {"i": 0}

        def next_bank(banks=(0, 1, 2, 3)):
            b = banks[rot["i"] % len(banks)]
            rot["i"] += 1
            return b

        P.op("act", lambda e: e.activation(out=silucT[:], in_=cTs[:], func=AF.Silu), reads=[R("cTs")], writes=[R("silucT")])
        for t in range(12):
            wt, wr = load_w(w_ada[:, :, t * 512:(t + 1) * 512], 8, 512)
            bk = 4 + (t % 2)
            for j in range(4):
                for kc in range(8):
                    P.op("pe", lambda e, bk=bk, j=j, kc=kc, wt=wt: e.matmul(pb[bk][:, j * 17:(j + 1) * 17], lhsT=wt[:, kc, j * 128:(j + 1) * 128],
                                                                             rhs=silucT[:, kc, :], start=(kc == 0), stop=(kc == 7)),
                         reads=[wr, R("silucT")], writes=[PR(bk, j * 17, j * 17 + 17)])
            P.op("dve", lambda e, bk=bk, t=t: e.tensor_tensor(out=modT[:, 4 * t:4 * t + 4, :], in0=pb[bk][:, 0:68].rearrange("p (a b) -> p a b", a=4),
                                                               in1=pv[:, PV_BADA + 4 * t:PV_BADA + 4 * t + 4].unsqueeze(2).to_broadcast([128, 4, 17]), op=ALU.add),
                 reads=[PR(bk, 0, 68), R("pv")], writes=[R("modT", 4 * t, 4 * t + 4)])
            if t in (4, 5, 10, 11):
                gB = g1B if t < 6 else g2B
                gname = "g1B" if t < 6 else "g2B"
                half = t % 2
                for kc in range(8):
                    P.op("pe", lambda e, kc=kc, wt=wt: e.matmul(pb[6][:, 0:512], lhsT=silucT[:, kc, 0:1].to_broadcast([128, 128]), rhs=wt[:, kc, :],
                                                                  start=(kc == 0), stop=(kc == 7)),
                         reads=[wr, R("silucT")], writes=[PR(6)])
                P.op("dve", lambda e, gB=gB, half=half: e.tensor_tensor(out=gB[:, half * 512:(half + 1) * 512], in0=pb[6][:, 0:512],
                                                                         in1=gB[:, half * 512:(half + 1) * 512], op=ALU.add),
                     reads=[PR(6), R(gname, half, half + 1)], writes=[R(gname, half, half + 1)])
        for (aT, an, sc0, nw0) in ((a1T, "a1T", 8, PV_N1W), (a2T, "a2T", 32, PV_N2W)):
            P.op("dve", lambda e, aT=aT, sc0=sc0: e.tensor_scalar(out=aT[:], in0=modT[:, sc0:sc0 + 8, :], scalar1=1.0, scalar2=None, op0=ALU.add),
                 reads=[R("modT", sc0, sc0 + 8)], writes=[R(an)])
            P.op("dve", lambda e, aT=aT, nw0=nw0: e.tensor_tensor(out=aT[:], in0=aT[:], in1=pv[:, nw0:nw0 + 8].unsqueeze(2).to_broadcast([128, 8, 17]), op=ALU.mult),
                 reads=[R(an), R("pv")], writes=[R(an)])

        bufX = [sb("bx%d" % i, [128, 4, D]) for i in range(2)]
        ssq = sb("ssq", [128, 8])
        rsq = sb("rsq", [128, 8])
        hT = sb("hT", [128, 8, TB], BF16)
        ubuf = sb("ubuf", [128, 8, 16 + TB])
        gT = sb("gT", [128, 8, TB], BF16)
        gaT = gT
        gbT = gT
        uhist = sb("uhist", [128, 8, 15])
        amT = sb("amT", [128, 8, TB], BF16)
        xpre = [sb("xpre%d" % i, [128, 3 + TB], BF16) for i in range(2)]
        dg = [sb("dg%d" % i, [128, 4, 128], BF16) for i in range(2)]
        chist = sb("chist", [128, 32, 3], BF16)
        cst = sb("cst", [128, 32, 3])
        xbc = sb("xbc", [128, 32, TB], BF16)
        dtx = sb("dtx", [128, 4, 32])
        dtt = sb("dtt", [128, 4, 32])
        dtu = sb("dtu", [128, 4, 32])
        dt_tok = sb("dt_tok", [128, 4, 32])
        dA_bf = sb("dA_bf", [128, 4, 32], BF16)
        negAcs = sb("negAcs", [128, 32])
        Eacs = sb("Eacs", [128, 32])
        dte = sb("dte", [128, 32])
        cdB = sb("cdB", [128, 32])
        xdtb = [sb("xdt%d" % i, [128, DI], BF16) for i in range(2)]
        x_tok = xdtb[1]
        xdt = xdtb[0]
        xdte = sb("xdte", [128, DI], BF16)
        B_tok = sb("B_tok", [128, 1024], BF16)
        CBs = sb("CBs", [128, 8, 128], BF16)
        dec = [sb("dec%d" % i, [128, 8, 128], BF16) for i in range(2)]
        scr = dec
        ybuf = sb("ybuf", [128, DI])
        ytmp = hT[:].rearrange("p a b -> p (a b)").bitcast(F32)
        xn = ybuf[:].bitcast(BF16).rearrange("p (a b) -> p a b", a=4)
        junk = xdte

        def role_views(k):
            xt = bufX[k]
            szv = bufX[1 - k][:].rearrange("p a b -> p (a b)").bitcast(BF16).rearrange("p (a b) -> p a b", a=4)
            pl = bufX[1 - k][:].rearrange("p a b -> p (a b)").bitcast(BF16)[:, 0:8 * TB].rearrange("p (a b) -> p a b", a=8)
            return xt, "bx%d" % k, szv, "bx%d" % (1 - k), pl
        hst = sb("hst", [128, DI])
        hstb = sb("hstb", [128, DI], BF16)
        yst = [sb("yst%d" % i, [128, D]) for i in range(1)]
        ynT = ubuf[:].rearrange("p a b -> p (a b)").bitcast(BF16)[:, 0:16 * TB].rearrange("p (a b) -> p a b", a=16)
        fT = xbc[:].rearrange("p a b -> p (a b)")[:, 0:22 * TB].rearrange("p (a b) -> p a b", a=22)

        P.op("pool", lambda e: e.memset(chist[:], 0.0), writes=[R("chist", 0, 32)])
        P.op("pool", lambda e: e.memset(hst[:], 0.0), writes=[R("hst", 0, 8)])
        P.op("pool", lambda e: e.memset(hstb[:], 0.0), writes=[R("hstb", 0, 8)])

        def rmsnorm_to_T(blk_name, aT, bcol0, xtok, XN, ntok=128, ntile=4):
            for ti in range(ntile):
                P.op("act", lambda e, ti=ti: e.activation(out=junk[:, 0:D], in_=xtok[:, ti, :], func=AF.Square, accum_out=ssq[:, ti:ti + 1]),
                     reads=[R(XN, ti, ti + 1)], writes=[R("xdte", 0, 8), R("ssq", ti, ti + 1)])
                P.op("act", lambda e, ti=ti: e.activation(out=rsq[:, 4 + ti:5 + ti], in_=ssq[:, ti:ti + 1], func=AF.Ln, scale=1.0 / D, bias=epsc[:, 0:1]),
                     reads=[R("ssq", ti, ti + 1), R("epsc")], writes=[R("rsq", 4 + ti, 5 + ti)])
                P.op("act", lambda e, ti=ti: e.activation(out=rsq[:, ti:ti + 1], in_=rsq[:, 4 + ti:5 + ti], func=AF.Exp, scale=-0.5),
                     reads=[R("rsq", 4 + ti, 5 + ti)], writes=[R("rsq", ti, ti + 1)])
                P.op("dve", lambda e, ti=ti: e.tensor_scalar(out=xn[:, ti, :], in0=xtok[:, ti, :], scalar1=rsq[:, ti:ti + 1], scalar2=None, op0=ALU.mult),
                     reads=[R(XN, ti, ti + 1), R("rsq", ti, ti + 1)], writes=[R("ybuf", 2 * ti, 2 * ti + 2)])
            for kc in range(8):
                bk = kc // 2
                c0 = (kc % 2) * 512
                for ti in range(ntile):
                    P.op("pe", lambda e, bk=bk, c0=c0, ti=ti, kc=kc: e.transpose(out=PBF(bk)[:, c0 + ti * 128:c0 + (ti + 1) * 128],
                                                                                  in_=xn[:, ti, kc * 128:(kc + 1) * 128], identity=identb[:]),
                         reads=[R("ybuf", 2 * ti, 2 * ti + 2), R("identb")], writes=[PR(bk, c0 // 2 + ti * 64, c0 // 2 + (ti + 1) * 64)])
                P.op("dve", lambda e, bk=bk, c0=c0, kc=kc, aT=aT, bcol0=bcol0: e.tensor_scalar(
                    out=hT[:, kc, :], in0=PBF(bk)[:, c0:c0 + 512], scalar1=aT[:, kc, 0:1], scalar2=modT[:, bcol0 + kc, 0:1], op0=ALU.mult, op1=ALU.add),
                    reads=[PR(bk, c0 // 2, c0 // 2 + 256), R(blk_name), R("modT", bcol0 + kc, bcol0 + kc + 1)], writes=[R("hT", kc, kc + 1)])

        def proj_ws(wt, wr, j, nk, rhs_fn, rhs_regs, bank, ncols=TB, col0=0):
            for kc in range(nk):
                P.op("pe", lambda e, kc=kc: e.matmul(pb[bank][:, 0:ncols], lhsT=wt[:, kc, col0 + j * 128:col0 + (j + 1) * 128], rhs=rhs_fn(kc),
                                                      start=(kc == 0), stop=(kc == nk - 1)),
                     reads=[wr] + rhs_regs(kc), writes=[PR(bank, 0, ncols)])

        def proj_as(wt, wr, ti, nk, lhs_fn, lhs_regs, bank, kc0=0, first=True, last=True, nktot=None):
            for kc in range(nk):
                P.op("pe", lambda e, kc=kc: e.matmul(pb[bank][:, 0:512], lhsT=lhs_fn(kc0 + kc, ti), rhs=wt[:, kc, :],
                                                      start=(first and kc == 0), stop=(last and kc == nk - 1)),
                     reads=[wr] + lhs_regs(kc0 + kc), writes=[PR(bank)])

        hT_rhs = lambda kc: hT[:, kc, :]
        hT_regs = lambda kc: [R("hT", kc, kc + 1)]
        hT_lhs = lambda kc, ti: hT[:, kc, ti * 128:(ti + 1) * 128]

        def emit_block(tb, xtok, XN, sz, SN, pooled):
            if tb == 0:
                DMA("sp", xtok[:], xp[0:TB, :].rearrange("(t p) d -> p t d", p=128), "ld_x", writes=[R(XN, 0, 4)])
            rmsnorm_to_T("a1T", a1T, 0, xtok, XN)

            for t in range(2):
                wt, wr = load_w(w_in[:, :, 7200 + t * 512:7200 + (t + 1) * 512], 8, 512, tid=('in', 7200 + t * 512))
                for j in range(4):
                    oc = 4 * t + j
                    bk = next_bank()
                    proj_ws(wt, wr, j, 8, hT_rhs, hT_regs, bk)
                    P.op("act", lambda e, bk=bk, oc=oc: e.activation(out=gaT[:, oc, :], in_=pb[bk][:, 0:TB], func=AF.Sigmoid),
                         reads=[PR(bk)], writes=[R("gT", oc, oc + 1)])
            if tb == 0:
                P.op("pool", lambda e: e.memset(ubuf[:, :, 0:16], 0.0), writes=[R("ubuf", 0, 8)])
            else:
                P.op("pool", lambda e: e.tensor_copy(out=ubuf[:, :, 1:16], in_=uhist[:]), reads=[R("uhist")], writes=[R("ubuf", 0, 8)])
            for t in range(2):
                wt, wr = load_w(w_in[:, :, t * 512:(t + 1) * 512], 8, 512, tid=('in', t * 512))
                for j in range(4):
                    oc = 4 * t + j
                    bk = next_bank()
                    proj_ws(wt, wr, j, 8, hT_rhs, hT_regs, bk)
                    P.op("act", lambda e, bk=bk, oc=oc: e.activation(out=ubuf[:, oc, 16:16 + TB], in_=pb[bk][:, 0:TB], func=AF.Copy),
                         reads=[PR(bk)], writes=[R("ubuf", oc, oc + 1)])
            for oc in range(8):
                g = oc // 2
                w = 2 ** (g + 1)
                U = ubuf[:, oc, :]
                ur = R("ubuf", oc, oc + 1)
                P.op("dve", lambda e, U=U: e.tensor_tensor(out=sA[:, 2:528], in0=U[:, 2:528], in1=U[:, 1:527], op=ALU.add), reads=[ur], writes=[R("sA")])
                cur, curname = sA, "sA"
                if g >= 1:
                    P.op("dve", lambda e: e.tensor_tensor(out=sB[:, 4:528], in0=sA[:, 4:528], in1=sA[:, 2:526], op=ALU.add), reads=[R("sA")], writes=[R("sB")])
                    cur, curname = sB, "sB"
                if g >= 2:
                    P.op("dve", lambda e: e.tensor_tensor(out=sA[:, 8:528], in0=sB[:, 8:528], in1=sB[:, 4:524], op=ALU.add), reads=[R("sB")], writes=[R("sA")])
                    cur, curname = sA, "sA"
                if g >= 3:
                    P.op("dve", lambda e: e.tensor_tensor(out=sB[:, 16:528], in0=sA[:, 16:528], in1=sA[:, 8:520], op=ALU.add), reads=[R("sA")], writes=[R("sB")])
                    cur, curname = sB, "sB"
                P.op("dve", lambda e, cur=cur, U=U, w=w, oc=oc: e.scalar_tensor_tensor(out=pooled[:, oc, :], in0=cur[:, 16:528], scalar=1.0 / w, in1=U[:, 16:528],
                                                                                        op0=ALU.mult, op1=ALU.subtract),
                     reads=[R(curname), ur], writes=[R(SN, 0, 2)])
                if tb == 0:
                    P.op("dve", lambda e, cur=cur, g=g: e.tensor_tensor(out=cur[:, 0:16], in0=cur[:, 16:32], in1=invc[:, g, :], op=ALU.mult),
                         reads=[R(curname), R("invc")], writes=[R(curname)])
                    P.op("dve", lambda e, cur=cur, U=U, oc=oc: e.tensor_tensor(out=pooled[:, oc, 0:16], in0=cur[:, 0:16], in1=U[:, 16:32], op=ALU.subtract),
                         reads=[R(curname), ur], writes=[R(SN, 0, 2)])
            wplt, wplr = load_w(w_pool.rearrange("p g k c -> p (g k) c"), 8, 256, tid=('pool',))
            for g in range(4):
                for j in range(2):
                    oc = 2 * g + j
                    bk = next_bank()
                    for k2 in range(2):
                        P.op("pe", lambda e, bk=bk, g=g, j=j, k2=k2: e.matmul(pb[bk][:, 0:TB], lhsT=wplt[:, 2 * g + k2, j * 128:(j + 1) * 128], rhs=pooled[:, 2 * g + k2, :],
                                                                                start=(k2 == 0), stop=(k2 == 1)),
                             reads=[wplr, R(SN, 0, 2)], writes=[PR(bk)])
                    P.op("dve", lambda e, bk=bk, oc=oc: e.scalar_tensor_tensor(out=amT[:, oc, :], in0=pb[bk][:, 0:TB], scalar=pv[:, PV_PSC + oc:PV_PSC + oc + 1],
                                                                                in1=gaT[:, oc, :], op0=ALU.mult, op1=ALU.mult),
                         reads=[PR(bk), R("pv"), R("gT", oc, oc + 1)], writes=[R("amT", oc, oc + 1)])
            if tb == nblk - 1:
                DMA("sp", o_pp, ubuf[:, :, 513:528], "st_pp", reads=[R("ubuf", 0, 8)], writes=[R("o_pp")])
            else:
                P.op("pool", lambda e: e.tensor_copy(out=uhist[:], in_=ubuf[:, :, 513:528]), reads=[R("ubuf", 0, 8)], writes=[R("uhist")])

            for t in range(4):
                wt, wr = load_w(w_in[:, :, 1024 + t * 512:1024 + (t + 1) * 512], 8, 512, tid=('in', 1024 + t * 512))
                for ti in range(4):
                    bk = next_bank()
                    proj_as(wt, wr, ti, 8, hT_lhs, hT_regs, bk)
                    P.op("act", lambda e, bk=bk, ti=ti, t=t: e.activation(out=sz[:, ti, t * 512:(t + 1) * 512], in_=pb[bk][:, 0:512], func=AF.Silu),
                         reads=[PR(bk)], writes=[R(SN, ti, ti + 1)])
            for t in range(8):
                wt, wr = load_w(w_in[:, :, 3072 + t * 512:3072 + (t + 1) * 512], 8, 512, tid=('in', 3072 + t * 512))
                for j in range(4):
                    oc = 4 * t + j
                    sl = oc % 2
                    bk = next_bank()
                    proj_ws(wt, wr, j, 8, hT_rhs, hT_regs, bk)
                    xpr = R("xpre%d" % sl)
                    P.op("dve", lambda e, sl=sl, oc=oc: e.tensor_copy(out=xpre[sl][:, 0:3], in_=chist[:, oc, :]), reads=[R("chist", oc, oc + 1)], writes=[xpr])
                    P.op("act", lambda e, sl=sl, bk=bk: e.activation(out=xpre[sl][:, 3:3 + TB], in_=pb[bk][:, 0:TB], func=AF.Copy), reads=[PR(bk)], writes=[xpr])
                    if tb == nblk - 1:
                        P.op("dve", lambda e, bk=bk, oc=oc: e.tensor_copy(out=cst[:, oc, :], in_=pb[bk][:, TB - 3:TB]), reads=[PR(bk)], writes=[R("cst", oc, oc + 1)])
                    else:
                        P.op("dve", lambda e, sl=sl, oc=oc: e.tensor_copy(out=chist[:, oc, :], in_=xpre[sl][:, TB:TB + 3]), reads=[xpr], writes=[R("chist", oc, oc + 1)])
                    P.op("pool", lambda e, sl=sl, oc=oc: e.tensor_tensor(out=dg[sl][:], in0=identb[:].unsqueeze(1).to_broadcast([128, 4, 128]),
                                                                          in1=pv[:, PV_CW + oc:PV_CW + oc + 97:32].unsqueeze(2).to_broadcast([128, 4, 128]), op=ALU.mult),
                         reads=[R("identb"), R("pv")], writes=[R("dg%d" % sl, 0, 4)])
                    cbk = 4 + sl
                    for k in range(4):
                        P.op("pe", lambda e, sl=sl, k=k, cbk=cbk: e.matmul(pb[cbk][:, 0:TB], lhsT=dg[sl][:, k, :], rhs=xpre[sl][:, k:k + TB], start=(k == 0), stop=(k == 3)),
                             reads=[R("dg%d" % sl, k, k + 1), xpr], writes=[PR(cbk)])
                    P.op("act", lambda e, cbk=cbk, oc=oc: e.activation(out=xbc[:, oc, :], in_=pb[cbk][:, 0:TB], func=AF.Silu, bias=pv[:, PV_CB + oc:PV_CB + oc + 1], scale=1.0),
                         reads=[PR(cbk), R("pv")], writes=[R("xbc", oc, oc + 1)])
            if tb == nblk - 1:
                DMA("sp", o_cp, cst[:], "st_cp", reads=[R("cst", 0, 32)], writes=[R("o_cp")])
            for ti in range(4):
                for kc in range(8):
                    P.op("pe", lambda e, ti=ti, kc=kc: e.matmul(pb[6][:, ti * 32:(ti + 1) * 32], lhsT=hT[:, kc, ti * 128:(ti + 1) * 128], rhs=wdt[:, kc, :],
                                                                  start=(kc == 0), stop=(kc == 7)),
                         reads=[R("hT", kc, kc + 1), R("wdt")], writes=[PR(6, ti * 32, ti * 32 + 32)])
            P.op("dve", lambda e: e.tensor_tensor(out=dtx[:], in0=pb[6][:, 0:128].rearrange("p (a b) -> p a b", a=4),
                                                  in1=rowb[:, RB_DTB:RB_DTB + 32].unsqueeze(1).to_broadcast([128, 4, 32]), op=ALU.add),
                 reads=[PR(6, 0, 128), R("rowb")], writes=[R("dtx")])
            P.op("act", lambda e: e.activation(out=dtt[:], in_=dtx[:], func=AF.Abs), reads=[R("dtx")], writes=[R("dtt")])
            P.op("act", lambda e: e.activation(out=dtu[:], in_=dtt[:], func=AF.Exp, scale=-1.0), reads=[R("dtt")], writes=[R("dtu")])
            P.op("act", lambda e: e.activation(out=dtt[:], in_=dtu[:], func=AF.Ln, bias=1.0, scale=1.0), reads=[R("dtu")], writes=[R("dtt")])
            P.op("dve", lambda e: e.scalar_tensor_tensor(out=dt_tok[:], in0=dtx[:], scalar=0.0, in1=dtt[:], op0=ALU.max, op1=ALU.add),
                 reads=[R("dtx"), R("dtt")], writes=[R("dt_tok")])
            P.op("dve", lambda e: e.tensor_tensor(out=dA_bf[:], in0=dt_tok[:], in1=anegb[:].unsqueeze(1).to_broadcast([128, 4, 32]), op=ALU.mult),
                 reads=[R("dt_tok"), R("anegb")], writes=[R("dA_bf")])
            for t in range(2):
                wt, wr = load_w(w_in[:, :, 8224 + t * 512:8224 + (t + 1) * 512], 8, 512, tid=('in', 8224 + t * 512))
                for j in range(4):
                    oc = 4 * t + j
                    bk = next_bank()
                    proj_ws(wt, wr, j, 8, hT_rhs, hT_regs, bk)
                    P.op("act", lambda e, bk=bk, oc=oc: e.activation(out=gbT[:, oc, :], in_=pb[bk][:, 0:TB], func=AF.Sigmoid),
                         reads=[PR(bk)], writes=[R("gT", oc, oc + 1)])


            def ssd_acd(ci):
                tsl = slice(ci * 128, (ci + 1) * 128)
                dAc = dA_bf[:, ci, :]
                P.op("pe", lambda e: e.matmul(pb[7][:, 0:32], lhsT=triub[:], rhs=dAc, start=True, stop=True), reads=[R("triub"), R("dA_bf")], writes=[PR(7)])
                P.op("pe", lambda e: e.matmul(pb[7][:, 32:64], lhsT=onesb[:], rhs=dAc, start=True, stop=True), reads=[R("onesb"), R("dA_bf")], writes=[PR(7)])
                P.op("dve", lambda e: e.tensor_scalar(out=negAcs[:], in0=pb[7][:, 0:32], scalar1=-1.0, scalar2=None, op0=ALU.mult), reads=[PR(7)], writes=[R("negAcs")])
                P.op("act", lambda e: e.activation(out=Eacs[:], in_=pb[7][:, 0:32], func=AF.Exp), reads=[PR(7)], writes=[R("Eacs")])
                P.op("dve", lambda e: e.tensor_tensor(out=dte[:], in0=pb[7][:, 32:64], in1=negAcs[:], op=ALU.add), reads=[PR(7), R("negAcs")], writes=[R("dte")])
                P.op("act", lambda e: e.activation(out=dte[:], in_=dte[:], func=AF.Exp), reads=[R("dte")], writes=[R("dte")])
                P.op("act", lambda e: e.activation(out=cdB[:], in_=pb[7][:, 32:64], func=AF.Exp), reads=[PR(7)], writes=[R("cdB")])
                for g in range(NG):
                    P.op("pe", lambda e, g=g: e.transpose(out=PBF(2)[:, g * 128:(g + 1) * 128], in_=xbc[:, 16 + g, tsl], identity=identb[:]),
                         reads=[R("xbc", 16 + g, 17 + g), R("identb")], writes=[PR(2)])
                P.op("act", lambda e: e.activation(out=B_tok[:], in_=PBF(2)[:, 0:1024], func=AF.Copy), reads=[PR(2)], writes=[R("B_tok")])
                for g in range(NG):
                    bk = 3 + g // 4
                    c0 = (g % 4) * 128
                    P.op("pe", lambda e, g=g, bk=bk, c0=c0: e.matmul(pb[bk][:, c0:c0 + 128], lhsT=xbc[:, 16 + g, tsl], rhs=xbc[:, 24 + g, tsl], start=True, stop=True),
                         reads=[R("xbc", 16 + g, 17 + g), R("xbc", 24 + g, 25 + g)], writes=[PR(bk)])
                for q in range(2):
                    P.op("act", lambda e, q=q: e.activation(out=CBs[:, q * 4:(q + 1) * 4, :], in_=pb[3 + q][:, 0:512].rearrange("p (a b) -> p a b", a=4), func=AF.Copy),
                         reads=[PR(3 + q)], writes=[R("CBs", q * 4, q * 4 + 4)])

            def ssd_b(ci):
                tsl = slice(ci * 128, (ci + 1) * 128)
                xd = xdtb[ci % 2]
                xr = R("xdt%d" % (ci % 2))
                for fc in range(16):
                    bk = fc // 8
                    c0 = (fc % 8) * 128
                    P.op("pe", lambda e, bk=bk, c0=c0, fc=fc: e.transpose(out=PBF(bk)[:, c0:c0 + 128], in_=xbc[:, fc, tsl], identity=identb[:]),
                         reads=[R("xbc", fc, fc + 1), R("identb")], writes=[PR(bk)])
                for bk in range(2):
                    P.op("dve", lambda e, bk=bk: e.tensor_tensor(out=xd[:, bk * 1024:(bk + 1) * 1024].rearrange("p (h q) -> p h q", h=16),
                                                                 in0=PBF(bk)[:, 0:1024].rearrange("p (h q) -> p h q", h=16),
                                                                 in1=dt_tok[:, ci, bk * 16:(bk + 1) * 16].unsqueeze(2).to_broadcast([128, 16, HP]), op=ALU.mult),
                         reads=[PR(bk), R("dt_tok")], writes=[xr])
                P.op("dve", lambda e: e.tensor_tensor(out=xdte[:].rearrange("p (h q) -> p h q", h=NH), in0=xd[:].rearrange("p (h q) -> p h q", h=NH),
                                                      in1=dte[:].unsqueeze(2).to_broadcast([128, NH, HP]), op=ALU.mult),
                     reads=[xr, R("dte")], writes=[R("xdte", 0, 8)])

            def ssd_decay(ci, hq):
                sl = hq % 2
                dbanks = (5, 6) if hq % 2 == 0 else (3, 4)
                for half in range(2):
                    bk = dbanks[half]
                    for j in range(4):
                        h = hq * 8 + half * 4 + j
                        P.op("pe", lambda e, bk=bk, j=j: e.matmul(pb[bk][:, j * 128:(j + 1) * 128], lhsT=identb[:], rhs=nmb[:, 0, :], start=True, stop=False),
                             reads=[R("identb"), R("nmb")], writes=[PR(bk)])
                        P.op("pe", lambda e, bk=bk, j=j, h=h: e.matmul(pb[bk][:, j * 128:(j + 1) * 128], lhsT=dA_bf[:, ci, h:h + 1].to_broadcast([128, 128]),
                                                                        rhs=triub[:], start=False, stop=True),
                             reads=[R("dA_bf"), R("triub")], writes=[PR(bk)])
                    for j in range(4):
                        h = hq * 8 + half * 4 + j
                        P.op("act", lambda e, bk=bk, j=j, h=h, half=half: e.activation(out=dec[sl][:, half * 4 + j, :], in_=pb[bk][:, j * 128:(j + 1) * 128],
                                                                                       func=AF.Exp, bias=negAcs[:, h:h + 1], scale=1.0),
                             reads=[PR(bk), R("negAcs")], writes=[R("dec%d" % sl, half * 4 + j, half * 4 + j + 1)])

            def ssd_y(ci, hq):
                tsl = slice(ci * 128, (ci + 1) * 128)
                sl = hq % 2
                P.op("dve", lambda e: e.tensor_tensor(out=scr[sl][:].rearrange("p (g j) l -> p g j l", g=2), in0=dec[sl][:].rearrange("p (g j) l -> p g j l", g=2),
                                                      in1=CBs[:, 2 * hq:2 * hq + 2, :].unsqueeze(2).to_broadcast([128, 2, 4, 128]), op=ALU.mult),
                     reads=[R("dec%d" % sl, 0, 8), R("CBs", 2 * hq, 2 * hq + 2)], writes=[R("dec%d" % sl, 0, 8)])
                bA = (0, 2)[hq % 2]
                bB = (1, 7)[hq % 2]
                for jj in range(8):
                    h = 8 * hq + jj
                    P.op("pe", lambda e, jj=jj, h=h: e.matmul(pb[bA][:, jj * 64:(jj + 1) * 64], lhsT=xbc[:, h // 2, tsl], rhs=diagD[:, h // 2, (h % 2) * 64:(h % 2 + 1) * 64], start=True, stop=False),
                         reads=[R("xbc", h // 2, h // 2 + 1), R("diagD")], writes=[PR(bA)])
                    P.op("pe", lambda e, jj=jj, h=h: e.matmul(pb[bA][:, jj * 64:(jj + 1) * 64], lhsT=scr[sl][:, jj, :], rhs=xdtb[ci % 2][:, h * 64:(h + 1) * 64], start=False, stop=True),
                         reads=[R("dec%d" % sl, jj, jj + 1), R("xdt%d" % (ci % 2))], writes=[PR(bA)])
                for gg in range(2):
                    g = 2 * hq + gg
                    P.op("pe", lambda e, g=g, gg=gg: e.matmul(pb[bB][:, gg * 256:(gg + 1) * 256], lhsT=xbc[:, 24 + g, tsl], rhs=hstb[:, g * 256:(g + 1) * 256], start=True, stop=True),
                         reads=[R("xbc", 24 + g, 25 + g), R("hstb", g, g + 1)], writes=[PR(bB)])

            def ssd_ye(ci, hq):
                bA = (0, 2)[hq % 2]
                bB = (1, 7)[hq % 2]
                ysl = slice(hq * 512, (hq + 1) * 512)
                yr = R("ybuf", 2 * hq, 2 * hq + 2)
                P.op("dve", lambda e: e.tensor_tensor(out=ybuf[:, ysl].rearrange("p (h q) -> p h q", h=8), in0=pb[bB][:, 0:512].rearrange("p (h q) -> p h q", h=8),
                                                      in1=Eacs[:, 8 * hq:8 * hq + 8].unsqueeze(2).to_broadcast([128, 8, HP]), op=ALU.mult),
                     reads=[PR(bB), R("Eacs")], writes=[yr])
                P.op("dve", lambda e: e.tensor_tensor(out=ybuf[:, ysl], in0=pb[bA][:, 0:512], in1=ybuf[:, ysl], op=ALU.add), reads=[PR(bA), yr], writes=[yr])
                P.op("dve", lambda e: e.tensor_tensor(out=ybuf[:, ysl], in0=ybuf[:, ysl], in1=sz[:, ci, ysl], op=ALU.mult), reads=[yr, R(SN, ci, ci + 1)], writes=[yr])

            def ssd_g(ci):
                for q in range(4):
                    sbk = 3 + q
                    hsl = slice(q * 512, (q + 1) * 512)
                    hr = R("hst", 2 * q, 2 * q + 2)
                    for gg in range(2):
                        g = 2 * q + gg
                        P.op("pe", lambda e, g=g, gg=gg, sbk=sbk: e.matmul(pb[sbk][:, gg * 256:(gg + 1) * 256], lhsT=B_tok[:, g * 128:(g + 1) * 128], rhs=xdte[:, g * 256:(g + 1) * 256], start=True, stop=True),
                             reads=[R("B_tok"), R("xdte", g, g + 1)], writes=[PR(sbk)])
                    P.op("pool", lambda e, q=q, hsl=hsl: e.tensor_tensor(out=hst[:, hsl].rearrange("p (h q) -> p h q", h=8), in0=hst[:, hsl].rearrange("p (h q) -> p h q", h=8),
                                                                         in1=cdB[:, 8 * q:8 * q + 8].unsqueeze(2).to_broadcast([128, 8, HP]), op=ALU.mult),
                         reads=[hr, R("cdB")], writes=[hr])
                    P.op("dve", lambda e, sbk=sbk, hsl=hsl: e.tensor_tensor(out=hst[:, hsl], in0=pb[sbk][:, 0:512], in1=hst[:, hsl], op=ALU.add), reads=[PR(sbk), hr], writes=[hr])
                    P.op("act", lambda e, hsl=hsl: e.activation(out=hstb[:, hsl], in_=hst[:, hsl], func=AF.Copy), reads=[hr], writes=[R("hstb", 2 * q, 2 * q + 2)])

            def ssd_h(ci):
                tsl = slice(ci * 128, (ci + 1) * 128)
                ynb2 = xdtb[ci % 2]
                xr = R("xdt%d" % (ci % 2))
                P.op("act", lambda e: e.activation(out=ynb2[:], in_=ybuf[:], func=AF.Square, accum_out=ssq[:, 4:5]), reads=[R("ybuf", 0, 8)], writes=[xr, R("ssq", 4, 5)])
                P.op("act", lambda e: e.activation(out=ssq[:, 5:6], in_=ssq[:, 4:5], func=AF.Ln, scale=1.0 / DI, bias=epsc[:, 0:1]), reads=[R("ssq", 4, 5), R("epsc")], writes=[R("ssq", 5, 6)])
                P.op("act", lambda e: e.activation(out=ssq[:, 6:7], in_=ssq[:, 5:6], func=AF.Exp, scale=-0.5), reads=[R("ssq", 5, 6)], writes=[R("ssq", 6, 7)])
                P.op("dve", lambda e: e.scalar_tensor_tensor(out=ynb2[:], in0=ybuf[:], scalar=ssq[:, 6:7], in1=snwb[:], op0=ALU.mult, op1=ALU.mult),
                     reads=[R("ybuf", 0, 8), R("ssq", 6, 7), R("snwb")], writes=[xr])
                for fc in range(16):
                    bk = fc // 8
                    c0 = (fc % 8) * 128
                    P.op("pe", lambda e, bk=bk, c0=c0, fc=fc: e.transpose(out=PBF(bk)[:, c0:c0 + 128], in_=ynb2[:, fc * 128:(fc + 1) * 128], identity=identb[:]),
                         reads=[xr, R("identb")], writes=[PR(bk)])
                for q in range(2):
                    P.op("act", lambda e, q=q: e.activation(out=ynT[:, q * 8:(q + 1) * 8, tsl], in_=PBF(q)[:, 0:1024].rearrange("p (a b) -> p a b", a=8), func=AF.Copy),
                         reads=[PR(q)], writes=[R("ubuf", 0, 8)])

            ssd_acd(0)
            ssd_b(0)
            for ci in range(4):
                if ci == 0:
                    ssd_decay(ci, 0)
                ssd_decay(ci, 1)
                ssd_y(ci, 0)
                for hq in range(4):
                    if hq + 2 < 4:
                        ssd_decay(ci, hq + 2)
                    if hq + 1 < 4:
                        ssd_y(ci, hq + 1)
                    ssd_ye(ci, hq)
                ssd_g(ci)
                if ci + 1 < 4:
                    ssd_acd(ci + 1)
                    ssd_b(ci + 1)
                    ssd_decay(ci + 1, 0)
                ssd_h(ci)
            if tb == nblk - 1:
                DMA("sp", o_sp, hst[:], "st_sp", reads=[R("hst", 0, 8)], writes=[R("o_sp")])

            if tb + 1 < nblk:
                DMA("sp", bufX[(tb + 1) % 2][:], xp[(tb + 1) * TB:(tb + 2) * TB, :].rearrange("(t p) d -> p t d", p=128), "ld_x", writes=[R(SN, 0, 4)])
            for t in range(4):
                wt, wr = load_w(w_ssd[:, :, t * 256:(t + 1) * 256], 16, 256, tid=('ssd', t))
                for j in range(2):
                    oc = 2 * t + j
                    bk = next_bank()
                    proj_ws(wt, wr, j, 16, lambda kc: ynT[:, kc, :], lambda kc: [R("ubuf", 0, 8)], bk)
                    P.op("dve", lambda e, bk=bk, oc=oc: e.tensor_tensor(out=sA[:, 0:TB], in0=pb[bk][:, 0:TB], in1=gbT[:, oc, :], op=ALU.mult),
                         reads=[PR(bk), R("gT", oc, oc + 1)], writes=[R("sA")])
                    P.op("dve", lambda e, oc=oc: e.tensor_tensor(out=hT[:, oc, :], in0=sA[:, 0:TB], in1=amT[:, oc, :], op=ALU.add),
                         reads=[R("sA"), R("amT", oc, oc + 1)], writes=[R("hT", oc, oc + 1)])
            for t in range(2):
                wt, wr = load_w(w_out[:, :, t * 512:(t + 1) * 512], 8, 512, tid=('out', t))
                for ti in range(4):
                    bk = next_bank()
                    proj_as(wt, wr, ti, 8, hT_lhs, hT_regs, bk)
                    P.op("dve", lambda e, bk=bk, t=t: e.tensor_tensor(out=sB[:, 0:512], in0=pb[bk][:, 0:512], in1=g1B[:, t * 512:(t + 1) * 512], op=ALU.mult),
                         reads=[PR(bk), R("g1B", t, t + 1)], writes=[R("sB")])
                    P.op("dve", lambda e, ti=ti, t=t: e.tensor_tensor(out=xtok[:, ti, t * 512:(t + 1) * 512], in0=xtok[:, ti, t * 512:(t + 1) * 512], in1=sB[:, 0:512], op=ALU.add),
                         reads=[R("sB"), R(XN, ti, ti + 1)], writes=[R(XN, ti, ti + 1)])
            rmsnorm_to_T("a2T", a2T, 24, xtok, XN)
            for t in range(11):
                wt, wr = load_w(w_ffi[:, :, t * 512:(t + 1) * 512], 8, 512, tid=('ffi', t))
                for j in range(2):
                    fc = 2 * t + j
                    bg = next_bank()
                    proj_ws(wt, wr, j, 8, hT_rhs, hT_regs, bg)
                    bu = next_bank()
                    proj_ws(wt, wr, j, 8, hT_rhs, hT_regs, bu, col0=256)
                    P.op("act", lambda e, bg=bg: e.activation(out=sA[:, 0:TB], in_=pb[bg][:, 0:TB], func=AF.Silu), reads=[PR(bg)], writes=[R("sA")])
                    P.op("dve", lambda e, bu=bu, fc=fc: e.tensor_tensor(out=fT[:, fc, :], in0=pb[bu][:, 0:TB], in1=sA[:, 0:TB], op=ALU.mult),
                         reads=[PR(bu), R("sA")], writes=[R("xbc", fc, fc + 1)])
            fT_lhs = lambda kc, ti: fT[:, kc, ti * 128:(ti + 1) * 128]
            fT_regs = lambda kc: [R("xbc", kc, kc + 1)]
            for half in range(2):
                for kg in range(3):
                    nk = 8 if kg < 2 else 6
                    wt, wr = load_w(w_ffo[:, kg * 8:kg * 8 + nk, half * 512:(half + 1) * 512], nk, 512, tid=('ffo', kg, half))
                    for ti in range(4):
                        proj_as(wt, wr, ti, nk, fT_lhs, fT_regs, 4 + ti, kc0=kg * 8, first=(kg == 0), last=(kg == 2))
                for ti in range(4):
                    P.op("dve", lambda e, ti=ti, half=half: e.tensor_tensor(out=sB[:, 0:512], in0=pb[4 + ti][:, 0:512], in1=g2B[:, half * 512:(half + 1) * 512], op=ALU.mult),
                         reads=[PR(4 + ti), R("g2B", half, half + 1)], writes=[R("sB")])
                    P.op("dve", lambda e, ti=ti, half=half: e.tensor_tensor(out=xtok[:, ti, half * 512:(half + 1) * 512], in0=xtok[:, ti, half * 512:(half + 1) * 512], in1=sB[:, 0:512], op=ALU.add),
                         reads=[R("sB"), R(XN, ti, ti + 1)], writes=[R(XN, ti, ti + 1)])
            for ti in range(4):
                ys = 0
                P.op("act", lambda e, ti=ti: e.activation(out=junk[:, 0:D], in_=xtok[:, ti, :], func=AF.Square, accum_out=ssq[:, ti:ti + 1]),
                     reads=[R(XN, ti, ti + 1)], writes=[R("xdte", 0, 8), R("ssq", ti, ti + 1)])
                P.op("act", lambda e, ti=ti: e.activation(out=rsq[:, 4 + ti:5 + ti], in_=ssq[:, ti:ti + 1], func=AF.Ln, scale=1.0 / D, bias=epsc[:, 0:1]),
                     reads=[R("ssq", ti, ti + 1), R("epsc")], writes=[R("rsq", 4 + ti, 5 + ti)])
                P.op("act", lambda e, ti=ti: e.activation(out=rsq[:, ti:ti + 1], in_=rsq[:, 4 + ti:5 + ti], func=AF.Exp, scale=-0.5), reads=[R("rsq", 4 + ti, 5 + ti)], writes=[R("rsq", ti, ti + 1)])
                P.op("dve", lambda e, ti=ti, ys=ys: e.scalar_tensor_tensor(out=yst[ys][:], in0=xtok[:, ti, :], scalar=rsq[:, ti:ti + 1], in1=rowb[:, RB_FNW:RB_FNW + D], op0=ALU.mult, op1=ALU.mult),
                     reads=[R(XN, ti, ti + 1), R("rsq", ti, ti + 1), R("rowb")], writes=[R("yst%d" % ys)])
                r0 = tb * TB + ti * 128
                DMA("sp", o_y[r0:r0 + 128, :], yst[ys][:], "st_y%d" % ys, reads=[R("yst%d" % ys)], writes=[R("o_y", tb * 4 + ti, tb * 4 + ti + 1)])


        for tb in range(nblk):
            emit_block(tb, *role_views(tb % 2))
        xtok, XN, sz, SN, _pl = role_views((nblk - 1) % 2)

        if do_sample:
            Ssl = slice(1, 17)
            xbcF = xbc[:].rearrange("p a b -> p (a b)")
            stbuf = [xtok[:, 0:2, :].rearrange("p a b -> p (a b)"), xtok[:, 2:4, :].rearrange("p a b -> p (a b)")]
            streg = [R(XN, 0, 2), R(XN, 2, 4)]
            ubF = ubuf[:].rearrange("p a b -> p (a b)")
            stp = ubF[:, 0:1920].rearrange("p (c s r) -> p c s r", c=8, s=NS)
            newst = ubF[:, 1920:3840].rearrange("p (c s r) -> p c s r", c=8, s=NS)
            stc = ybuf[:, 0:1536].rearrange("p (c s r) -> p c s r", c=32, s=NS)
            newcst = hst[:, 0:1536].rearrange("p (c s r) -> p c s r", c=32, s=NS)
            gTf = gT[:].rearrange("p a b -> p (a b)").bitcast(F32)
            projS = gTf[:, 0:56 * NS].rearrange("p (c s) -> p c s", c=56)
            amF = amT[:].rearrange("p a b -> p (a b)").bitcast(F32)
            acc1 = amF[:, 0:512].rearrange("p (c s) -> p c s", c=32)
            acc2 = amF[:, 512:1024].rearrange("p (c s) -> p c s", c=32)
            amS = amF[:, 1024:1152].rearrange("p (c s) -> p c s", c=8)
            gaS = amF[:, 1152:1280].rearrange("p (c s) -> p c s", c=8)
            gbS = amF[:, 1280:1408].rearrange("p (c s) -> p c s", c=8)
            ptmp = amF[:, 1408:1536].rearrange("p (c s) -> p c s", c=8)
            sgS = amF[:, 1536:1568]
            xbcS = B_tok[:, 0:512].rearrange("p (c s) -> p c s", c=32)
            CBf = CBs[:].rearrange("p a b -> p (a b)")
            hTs = CBf[:, 0:128].rearrange("p (c s) -> p c s", c=8)
            mixTs = CBf[:, 128:256].rearrange("p (c s) -> p c s", c=8)
            pooledS = CBf[:, 256:384].rearrange("p (c s) -> p c s", c=8)
            ynTs = CBf[:, 384:640].rearrange("p (c s) -> p c s", c=16)
            fTs = CBf[:, 640:992].rearrange("p (c s) -> p c s", c=22)
            x_tokS = x_tok[0:NS, :]
            xdt_tokS = xdt[0:NS, :]
            szS = xdte[0:NS, :]
            xs_tok = yst[0][0:NS, :]
            szf = sz[:].rearrange("p a b -> p (a b)").bitcast(F32)
            ysS = szf[0:NS, 0:2048]
            gs1 = szf[0:NS, 2048:3072]
            gs2 = szf[0:NS, 3072:4096]
            xnS = dec[0][:].rearrange("p a b -> p (a b)")[0:NS, :]
            hTF = hT[:].rearrange("p a b -> p (a b)")
            ynS = hTF[0:NS, 0:2048]
            junkS = hTF[0:NS, 2048:4096]
            tmpDx = hTF[0:NS, :].bitcast(F32)
            decBs = sA[:, 0:512].rearrange("p (s h) -> p s h", s=NS)
            mask16 = sB[:, 0:256].rearrange("p (a b) -> p a b", a=NS)
            identfS = sB[:, 256:384]
            CmaskS = xbcF[:, 0:2048].rearrange("p (g s m) -> p g s m", g=NG, s=NS)
            ssS, rsS = ssq[0:NS, :], rsq[0:NS, :]
            RCB = R("CBs", 0, 8)
            RAM = R("amT", 0, 8)

            DMA("sp", xs_tok, xsm, "ld_xs", writes=[R("yst0")])
            DMA("sp", stp, st_pool, "ld_stp", writes=[R("ubuf", 0, 8)])
            DMA("sp", stc, st_conv, "ld_stc", writes=[R("ybuf", 0, 8)])
            P.op("pool", lambda e: e.memset(sB[:, 0:384], 0.0), writes=[R("sB")])
            P.op("pool", lambda e: e.affine_select(out=mask16, in_=mask16, pattern=[[1, NS], [-1, NS]], compare_op=ALU.not_equal, fill=1.0, base=0, channel_multiplier=0),
                 reads=[R("sB")], writes=[R("sB")])
            P.op("pool", lambda e: e.affine_select(out=identfS, in_=identfS, pattern=[[-1, 128]], compare_op=ALU.not_equal, fill=1.0, base=0, channel_multiplier=1),
                 reads=[R("sB")], writes=[R("sB")])
            for (gsv, c0, b0) in ((gs1, 16, 0), (gs2, 40, 2)):
                for c in range(8):
                    bk = b0 + c // 4
                    P.op("pe", lambda e, bk=bk, c=c, c0=c0: e.matmul(pb[bk][0:NS, (c % 4) * 128:(c % 4 + 1) * 128], lhsT=modT[:, c0 + c, Ssl], rhs=identfS, start=True, stop=True),
                         reads=[R("modT", c0 + c, c0 + c + 1), R("sB")], writes=[PR(bk)])
                for q in range(2):
                    P.op("act", lambda e, gsv=gsv, q=q, b0=b0: e.activation(out=gsv[:, q * 512:(q + 1) * 512], in_=pb[b0 + q][0:NS, 0:512], func=AF.Copy),
                         reads=[PR(b0 + q)], writes=[R(SN, 0, 4)])

            def s_norm_T(aT, bcol0):
                P.op("act", lambda e: e.activation(out=junkS[:, 0:D], in_=xs_tok, func=AF.Square, accum_out=ssS[:, 0:1]), reads=[R("yst0")], writes=[R("hT", 0, 8), R("ssq", 0, 8)])
                P.op("act", lambda e: e.activation(out=rsS[:, 4:5], in_=ssS[:, 0:1], func=AF.Ln, scale=1.0 / D, bias=epsc[0:NS, 0:1]), reads=[R("ssq", 0, 8), R("epsc")], writes=[R("rsq", 0, 8)])
                P.op("act", lambda e: e.activation(out=rsS[:, 0:1], in_=rsS[:, 4:5], func=AF.Exp, scale=-0.5), reads=[R("rsq", 0, 8)], writes=[R("rsq", 0, 8)])
                P.op("dve", lambda e: e.tensor_scalar(out=xnS, in0=xs_tok, scalar1=rsS[:, 0:1], scalar2=None, op0=ALU.mult), reads=[R("yst0"), R("rsq", 0, 8)], writes=[R("dec0", 0, 8)])
                for kc in range(8):
                    P.op("pe", lambda e, kc=kc: e.transpose(out=PBF(0)[:, kc * NS:(kc + 1) * NS], in_=xnS[:, kc * 128:(kc + 1) * 128], identity=identb[0:NS, 0:NS]),
                         reads=[R("dec0", 0, 8), R("identb")], writes=[PR(0)])
                P.op("dve", lambda e, aT=aT: e.tensor_tensor(out=ptmp, in0=PBF(0)[:, 0:128].rearrange("p (c s) -> p c s", c=8), in1=aT[:, :, Ssl], op=ALU.mult),
                     reads=[PR(0), R("a1T"), R("a2T")], writes=[RAM])
                P.op("dve", lambda e, bcol0=bcol0: e.tensor_tensor(out=hTs, in0=ptmp, in1=modT[:, bcol0:bcol0 + 8, Ssl], op=ALU.add),
                     reads=[RAM, R("modT", bcol0, bcol0 + 8)], writes=[RCB])

            s_norm_T(a1T, 0)
            ws_tiles = [(0, 0), (512, 4)] + [(3072 + 512 * t, 8 + 4 * t) for t in range(8)] + [(7200 + 512 * t, 40 + 4 * t) for t in range(4)]
            for (col0, cb0) in ws_tiles:
                wt, wr = load_w(w_in[:, :, col0:col0 + 512], 8, 512, tid=('in', col0))
                for j in range(4):
                    for kc in range(8):
                        P.op("pe", lambda e, j=j, kc=kc, wt=wt: e.matmul(pb[1][:, j * NS:(j + 1) * NS], lhsT=wt[:, kc, j * 128:(j + 1) * 128], rhs=hTs[:, kc, :], start=(kc == 0), stop=(kc == 7)),
                             reads=[wr, RCB], writes=[PR(1)])
                P.op("act", lambda e, cb0=cb0: e.activation(out=projS[:, cb0:cb0 + 4, :], in_=pb[1][:, 0:4 * NS].rearrange("p (c s) -> p c s", c=4), func=AF.Copy),
                     reads=[PR(1)], writes=[R("gT", 0, 8)])
            for t in range(4):
                wt, wr = load_w(w_in[:, :, 1024 + t * 512:1024 + (t + 1) * 512], 8, 512, tid=('in', 1024 + t * 512))
                for kc in range(8):
                    P.op("pe", lambda e, kc=kc, wt=wt: e.matmul(pb[2][0:NS, 0:512], lhsT=hTs[:, kc, :], rhs=wt[:, kc, :], start=(kc == 0), stop=(kc == 7)),
                         reads=[wr, RCB], writes=[PR(2)])
                P.op("act", lambda e, t=t: e.activation(out=szS[:, t * 512:(t + 1) * 512], in_=pb[2][0:NS, 0:512], func=AF.Silu), reads=[PR(2)], writes=[R("xdte", 0, 8)])
            for kc in range(8):
                P.op("pe", lambda e, kc=kc: e.matmul(pb[3][0:NS, 0:32], lhsT=hTs[:, kc, :], rhs=wdt[:, kc, :], start=(kc == 0), stop=(kc == 7)), reads=[RCB, R("wdt")], writes=[PR(3)])
            d_x, d_t, d_u, d_dt, d_dec = dtx[0:NS, 0, :], dtt[0:NS, 0, :], dtu[0:NS, 0, :], dt_tok[0:NS, 0, :], dtx[0:NS, 1, :]
            P.op("dve", lambda e: e.tensor_tensor(out=d_x, in0=pb[3][0:NS, 0:32], in1=rowb[0:NS, RB_DTB:RB_DTB + 32], op=ALU.add), reads=[PR(3), R("rowb")], writes=[R("dtx")])
            P.op("act", lambda e: e.activation(out=d_t, in_=d_x, func=AF.Abs), reads=[R("dtx")], writes=[R("dtt")])
            P.op("act", lambda e: e.activation(out=d_u, in_=d_t, func=AF.Exp, scale=-1.0), reads=[R("dtt")], writes=[R("dtu")])
            P.op("act", lambda e: e.activation(out=d_t, in_=d_u, func=AF.Ln, bias=1.0, scale=1.0), reads=[R("dtu")], writes=[R("dtt")])
            P.op("dve", lambda e: e.scalar_tensor_tensor(out=d_dt, in0=d_x, scalar=0.0, in1=d_t, op0=ALU.max, op1=ALU.add), reads=[R("dtx"), R("dtt")], writes=[R("dt_tok")])
            P.op("dve", lambda e: e.tensor_tensor(out=d_u, in0=d_dt, in1=anegb[0:NS, :], op=ALU.mult), reads=[R("dt_tok"), R("anegb")], writes=[R("dtu")])
            P.op("act", lambda e: e.activation(out=d_dec, in_=d_u, func=AF.Exp), reads=[R("dtu")], writes=[R("dtx")])
            for s_ in range(NS):
                P.op("pe", lambda e, s_=s_: e.matmul(pb[0][:, s_ * 32:(s_ + 1) * 32], lhsT=identfS[0:NS, s_:s_ + 1].to_broadcast([NS, 128]), rhs=d_dec, start=True, stop=True),
                     reads=[R("sB"), R("dtx")], writes=[PR(0)])
            P.op("act", lambda e: e.activation(out=sA[:, 0:512], in_=pb[0][:, 0:512], func=AF.Copy), reads=[PR(0)], writes=[R("sA")])
            for g in range(4):
                w = 2 ** (g + 1)
                ug = projS[:, 2 * g:2 * g + 2, :]
                P.op("dve", lambda e, g=g, w=w: e.reduce_sum(out=ptmp[:, 0:2, :], in_=stp[:, 2 * g:2 * g + 2, :, 15 - (w - 1):15], axis=mybir.AxisListType.X),
                     reads=[R("ubuf", 0, 8)], writes=[RAM])
                P.op("dve", lambda e, ug=ug: e.tensor_tensor(out=ptmp[:, 0:2, :], in0=ptmp[:, 0:2, :], in1=ug, op=ALU.add), reads=[RAM, R("gT", 0, 8)], writes=[RAM])
                P.op("dve", lambda e, ug=ug, g=g, w=w: e.scalar_tensor_tensor(out=pooledS[:, 2 * g:2 * g + 2, :], in0=ptmp[:, 0:2, :], scalar=1.0 / w, in1=ug, op0=ALU.mult, op1=ALU.subtract),
                     reads=[RAM, R("gT", 0, 8)], writes=[RCB])
            P.op("act", lambda e: e.activation(out=gaS, in_=projS[:, 40:48, :], func=AF.Sigmoid), reads=[R("gT", 0, 8)], writes=[RAM])
            P.op("act", lambda e: e.activation(out=gbS, in_=projS[:, 48:56, :], func=AF.Sigmoid), reads=[R("gT", 0, 8)], writes=[RAM])
            wplt, wplr = load_w(w_pool.rearrange("p g k c -> p (g k) c"), 8, 256, tid=('pool',))
            for g in range(4):
                for j in range(2):
                    oc = 2 * g + j
                    for k2 in range(2):
                        P.op("pe", lambda e, g=g, j=j, k2=k2, oc=oc: e.matmul(pb[2][:, oc * NS:(oc + 1) * NS], lhsT=wplt[:, 2 * g + k2, j * 128:(j + 1) * 128], rhs=pooledS[:, 2 * g + k2, :], start=(k2 == 0), stop=(k2 == 1)),
                             reads=[wplr, RCB], writes=[PR(2)])
            for oc in range(8):
                P.op("dve", lambda e, oc=oc: e.scalar_tensor_tensor(out=amS[:, oc, :], in0=pb[2][:, oc * NS:(oc + 1) * NS], scalar=pv[:, PV_PSC + oc:PV_PSC + oc + 1], in1=gaS[:, oc, :], op0=ALU.mult, op1=ALU.mult),
                     reads=[PR(2), R("pv"), RAM], writes=[RAM])
            P.op("pool", lambda e: e.tensor_copy(out=newst[:, :, :, 0:14], in_=stp[:, :, :, 1:15]), reads=[R("ubuf", 0, 8)], writes=[R("ubuf", 0, 8)])
            P.op("pool", lambda e: e.tensor_copy(out=newst[:, :, :, 14], in_=projS[:, 0:8, :]), reads=[R("gT", 0, 8)], writes=[R("ubuf", 0, 8)])
            DMA("sp", o_ps, newst, "st_ps", reads=[R("ubuf", 0, 8)], writes=[R("o_ps")])
            xnew = projS[:, 8:40, :]
            cwb = lambda k: pv[:, PV_CW + 32 * k:PV_CW + 32 * k + 32].unsqueeze(2).to_broadcast([128, 32, NS])
            P.op("dve", lambda e: e.tensor_tensor(out=acc1, in0=xnew, in1=cwb(3), op=ALU.mult), reads=[R("gT", 0, 8), R("pv")], writes=[RAM])
            for k in range(3):
                P.op("dve", lambda e, k=k: e.tensor_tensor(out=acc2, in0=stc[:, :, :, k], in1=cwb(k), op=ALU.mult), reads=[R("ybuf", 0, 8), R("pv")], writes=[RAM])
                P.op("dve", lambda e: e.tensor_tensor(out=acc1, in0=acc1, in1=acc2, op=ALU.add), reads=[RAM], writes=[RAM])
            P.op("dve", lambda e: e.tensor_tensor(out=acc1, in0=acc1, in1=pv[:, PV_CB:PV_CB + 32].unsqueeze(2).to_broadcast([128, 32, NS]), op=ALU.add), reads=[RAM, R("pv")], writes=[RAM])
            P.op("act", lambda e: e.activation(out=xbcS, in_=acc1, func=AF.Silu), reads=[RAM], writes=[R("B_tok")])
            P.op("pool", lambda e: e.tensor_copy(out=newcst[:, :, :, 0:2], in_=stc[:, :, :, 1:3]), reads=[R("ybuf", 0, 8)], writes=[R("hst", 0, 8)])
            P.op("pool", lambda e: e.tensor_copy(out=newcst[:, :, :, 2], in_=xnew), reads=[R("gT", 0, 8)], writes=[R("hst", 0, 8)])
            DMA("sp", o_cs, newcst, "st_cs", reads=[R("hst", 0, 8)], writes=[R("o_cs")])
            for fc in range(16):
                bk = 1 + fc // 8
                P.op("pe", lambda e, fc=fc, bk=bk: e.transpose(out=PBF(bk)[0:NS, (fc % 8) * 128:(fc % 8 + 1) * 128], in_=xbcS[:, fc, :], identity=identb[:]),
                     reads=[R("B_tok"), R("identb")], writes=[PR(bk)])
            for q in range(2):
                P.op("act", lambda e, q=q: e.activation(out=x_tokS[:, q * 1024:(q + 1) * 1024], in_=PBF(1 + q)[0:NS, 0:1024], func=AF.Copy), reads=[PR(1 + q)], writes=[R("xdt1")])
            P.op("dve", lambda e: e.tensor_tensor(out=xdt_tokS.rearrange("p (h q) -> p h q", h=NH), in0=x_tokS.rearrange("p (h q) -> p h q", h=NH),
                                                  in1=d_dt.unsqueeze(2).to_broadcast([NS, NH, HP]), op=ALU.mult), reads=[R("xdt1"), R("dt_tok")], writes=[R("xdt0")])
            P.op("dve", lambda e: e.tensor_tensor(out=CmaskS, in0=xbcS[:, 24:32, :].unsqueeze(3).to_broadcast([128, NG, NS, NS]),
                                                  in1=mask16.unsqueeze(1).to_broadcast([128, NG, NS, NS]), op=ALU.mult), reads=[R("B_tok"), R("sB")], writes=[R("xbc", 0, 32)])
            P.op("pool", lambda e: e.memset(ysS, 0.0), writes=[R(SN, 0, 4)])
            def samp_L(s_):
                sl = s_ % 2
                DMA("sp", stbuf[sl], st_ssm[s_], "ld_st%d" % sl, writes=[streg[sl]])

            def samp_A(s_):
                sl = s_ % 2
                buf = stbuf[sl]
                for q in range(4):
                    P.op("pe", lambda e, q=q: e.matmul(pb[q][:, 0:512], lhsT=identb[0:NS, s_:s_ + 1].to_broadcast([NS, 128]), rhs=xdt_tokS[:, q * 512:(q + 1) * 512], start=True, stop=True),
                         reads=[R("identb"), R("xdt0")], writes=[PR(q)])
                P.op("pool", lambda e: e.tensor_tensor(out=buf.rearrange("p (h q) -> p h q", h=NH), in0=buf.rearrange("p (h q) -> p h q", h=NH),
                                                       in1=decBs[:, s_, :].unsqueeze(2).to_broadcast([128, NH, HP]), op=ALU.mult), reads=[streg[sl], R("sA")], writes=[streg[sl]])
                for g in range(NG):
                    P.op("dve", lambda e, g=g: e.scalar_tensor_tensor(out=buf[:, g * 256:(g + 1) * 256], in0=pb[g // 2][:, (g % 2) * 256:(g % 2 + 1) * 256],
                                                                       scalar=xbcS[:, 16 + g, s_:s_ + 1], in1=buf[:, g * 256:(g + 1) * 256], op0=ALU.mult, op1=ALU.add),
                         reads=[PR(g // 2), R("B_tok"), streg[sl]], writes=[streg[sl]])
                P.op("act", lambda e: e.activation(out=hstb[:], in_=buf, func=AF.Copy), reads=[streg[sl]], writes=[R("hstb", 0, 8)])
                DMA("sp", o_ss[s_], buf, "st_ss%d" % sl, reads=[streg[sl]], writes=[R("o_ss", s_, s_ + 1)])

            def samp_A2(s_):
                for g in range(NG):
                    P.op("pe", lambda e, g=g: e.matmul(pb[4 + g // 2][0:NS, (g % 2) * 256:(g % 2 + 1) * 256], lhsT=CmaskS[:, g, s_, :], rhs=hstb[:, g * 256:(g + 1) * 256], start=True, stop=True),
                         reads=[R("xbc", 0, 32), R("hstb", 0, 8)], writes=[PR(4 + g // 2)])

            def samp_B(s_):
                for q in range(4):
                    P.op("dve", lambda e, q=q: e.tensor_tensor(out=ysS[:, q * 512:(q + 1) * 512], in0=pb[4 + q][0:NS, 0:512], in1=ysS[:, q * 512:(q + 1) * 512], op=ALU.add),
                         reads=[PR(4 + q), R(SN, 0, 4)], writes=[R(SN, 0, 4)])

            samp_L(0)
            samp_L(1)
            samp_A(0)
            samp_A2(0)
            for s_ in range(NS):
                if s_ + 2 < NS:
                    samp_L(s_ + 2)
                if s_ + 1 < NS:
                    samp_A(s_ + 1)
                samp_B(s_)
                if s_ + 1 < NS:
                    samp_A2(s_ + 1)
            P.op("dve", lambda e: e.tensor_tensor(out=tmpDx.rearrange("p (h q) -> p h q", h=NH), in0=x_tokS.rearrange("p (h q) -> p h q", h=NH),
                                                  in1=rowb[0:NS, RB_DSK:RB_DSK + 32].unsqueeze(2).to_broadcast([NS, NH, HP]), op=ALU.mult), reads=[R("xdt1"), R("rowb")], writes=[R("hT", 0, 8)])
            P.op("dve", lambda e: e.tensor_tensor(out=ysS, in0=ysS, in1=tmpDx, op=ALU.add), reads=[R(SN, 0, 4), R("hT", 0, 8)], writes=[R(SN, 0, 4)])
            P.op("dve", lambda e: e.tensor_tensor(out=ysS, in0=ysS, in1=szS, op=ALU.mult), reads=[R(SN, 0, 4), R("xdte", 0, 8)], writes=[R(SN, 0, 4)])
            P.op("act", lambda e: e.activation(out=junkS, in_=ysS, func=AF.Square, accum_out=ssS[:, 1:2]), reads=[R(SN, 0, 4)], writes=[R("hT", 0, 8), R("ssq", 0, 8)])
            P.op("act", lambda e: e.activation(out=rsS[:, 5:6], in_=ssS[:, 1:2], func=AF.Ln, scale=1.0 / DI, bias=epsc[0:NS, 0:1]), reads=[R("ssq", 0, 8), R("epsc")], writes=[R("rsq", 0, 8)])
            P.op("act", lambda e: e.activation(out=rsS[:, 1:2], in_=rsS[:, 5:6], func=AF.Exp, scale=-0.5), reads=[R("rsq", 0, 8)], writes=[R("rsq", 0, 8)])
            P.op("dve", lambda e: e.scalar_tensor_tensor(out=ynS, in0=ysS, scalar=rsS[:, 1:2], in1=snwb[0:NS, :], op0=ALU.mult, op1=ALU.mult),
                 reads=[R(SN, 0, 4), R("rsq", 0, 8), R("snwb")], writes=[R("hT", 0, 8)])
            for fc in range(16):
                P.op("pe", lambda e, fc=fc: e.transpose(out=PBF(0)[:, fc * NS:(fc + 1) * NS], in_=ynS[:, fc * 128:(fc + 1) * 128], identity=identb[0:NS, 0:NS]),
                     reads=[R("hT", 0, 8), R("identb")], writes=[PR(0)])
            P.op("act", lambda e: e.activation(out=ynTs, in_=PBF(0)[:, 0:256].rearrange("p (c s) -> p c s", c=16), func=AF.Copy), reads=[PR(0)], writes=[RCB])
            for t in range(4):
                wt, wr = load_w(w_ssd[:, :, t * 256:(t + 1) * 256], 16, 256, tid=('ssd', t))
                for j in range(2):
                    oc = 2 * t + j
                    for kc in range(16):
                        P.op("pe", lambda e, j=j, kc=kc, oc=oc, wt=wt: e.matmul(pb[1][:, oc * NS:(oc + 1) * NS], lhsT=wt[:, kc, j * 128:(j + 1) * 128], rhs=ynTs[:, kc, :], start=(kc == 0), stop=(kc == 15)),
                             reads=[wr, RCB], writes=[PR(1)])
            P.op("dve", lambda e: e.tensor_tensor(out=ptmp, in0=pb[1][:, 0:128].rearrange("p (c s) -> p c s", c=8), in1=gbS, op=ALU.mult), reads=[PR(1), RAM], writes=[RAM])
            P.op("dve", lambda e: e.tensor_tensor(out=mixTs, in0=ptmp, in1=amS, op=ALU.add), reads=[RAM], writes=[RCB])
            tmpR = ysS[:, 0:512]
            for t in range(2):
                wt, wr = load_w(w_out[:, :, t * 512:(t + 1) * 512], 8, 512, tid=('out', t))
                for kc in range(8):
                    P.op("pe", lambda e, kc=kc, wt=wt, t=t: e.matmul(pb[2 + t][0:NS, 0:512], lhsT=mixTs[:, kc, :], rhs=wt[:, kc, :], start=(kc == 0), stop=(kc == 7)), reads=[wr, RCB], writes=[PR(2 + t)])
                P.op("dve", lambda e, t=t: e.tensor_tensor(out=tmpR, in0=pb[2 + t][0:NS, 0:512], in1=gs1[:, t * 512:(t + 1) * 512], op=ALU.mult), reads=[PR(2 + t), R(SN, 0, 4)], writes=[R(SN, 0, 4)])
                P.op("dve", lambda e, t=t: e.tensor_tensor(out=xs_tok[:, t * 512:(t + 1) * 512], in0=xs_tok[:, t * 512:(t + 1) * 512], in1=tmpR, op=ALU.add), reads=[R(SN, 0, 4), R("yst0")], writes=[R("yst0")])
            s_norm_T(a2T, 24)
            for t in range(11):
                wt, wr = load_w(w_ffi[:, :, t * 512:(t + 1) * 512], 8, 512, tid=('ffi', t))
                for q in range(4):
                    for kc in range(8):
                        P.op("pe", lambda e, q=q, kc=kc, wt=wt: e.matmul(pb[1][:, q * NS:(q + 1) * NS], lhsT=wt[:, kc, q * 128:(q + 1) * 128], rhs=hTs[:, kc, :], start=(kc == 0), stop=(kc == 7)),
                             reads=[wr, RCB], writes=[PR(1)])
                P.op("act", lambda e: e.activation(out=sgS, in_=pb[1][:, 0:2 * NS], func=AF.Silu), reads=[PR(1)], writes=[RAM])
                P.op("dve", lambda e, t=t: e.tensor_tensor(out=fTs[:, 2 * t:2 * t + 2, :], in0=pb[1][:, 2 * NS:4 * NS].rearrange("p (c s) -> p c s", c=2), in1=sgS.rearrange("p (c s) -> p c s", c=2), op=ALU.mult),
                     reads=[PR(1), RAM], writes=[RCB])
            for half in range(2):
                for kg in range(3):
                    nk = 8 if kg < 2 else 6
                    wt, wr = load_w(w_ffo[:, kg * 8:kg * 8 + nk, half * 512:(half + 1) * 512], nk, 512, tid=('ffo', kg, half))
                    for kc in range(nk):
                        P.op("pe", lambda e, kc=kc, kg=kg, nk=nk, wt=wt, half=half: e.matmul(pb[2 + half][0:NS, 0:512], lhsT=fTs[:, kg * 8 + kc, :], rhs=wt[:, kc, :], start=(kg == 0 and kc == 0), stop=(kg == 2 and kc == nk - 1)),
                             reads=[wr, RCB], writes=[PR(2 + half)])
                P.op("dve", lambda e, half=half: e.tensor_tensor(out=tmpR, in0=pb[2 + half][0:NS, 0:512], in1=gs2[:, half * 512:(half + 1) * 512], op=ALU.mult), reads=[PR(2 + half), R(SN, 0, 4)], writes=[R(SN, 0, 4)])
                P.op("dve", lambda e, half=half: e.tensor_tensor(out=xs_tok[:, half * 512:(half + 1) * 512], in0=xs_tok[:, half * 512:(half + 1) * 512], in1=tmpR, op=ALU.add), reads=[R(SN, 0, 4), R("yst0")], writes=[R("yst0")])
            P.op("act", lambda e: e.activation(out=junkS[:, 0:D], in_=xs_tok, func=AF.Square, accum_out=ssS[:, 2:3]), reads=[R("yst0")], writes=[R("hT", 0, 8), R("ssq", 0, 8)])
            P.op("act", lambda e: e.activation(out=rsS[:, 6:7], in_=ssS[:, 2:3], func=AF.Ln, scale=1.0 / D, bias=epsc[0:NS, 0:1]), reads=[R("ssq", 0, 8), R("epsc")], writes=[R("rsq", 0, 8)])
            P.op("act", lambda e: e.activation(out=rsS[:, 2:3], in_=rsS[:, 6:7], func=AF.Exp, scale=-0.5), reads=[R("rsq", 0, 8)], writes=[R("rsq", 0, 8)])
            P.op("dve", lambda e: e.scalar_tensor_tensor(out=xs_tok, in0=xs_tok, scalar=rsS[:, 2:3], in1=rowb[0:NS, RB_FNW:RB_FNW + D], op0=ALU.mult, op1=ALU.mult),
                 reads=[R("yst0"), R("rsq", 0, 8), R("rowb")], writes=[R("yst0")])
            DMA("sp", o_ys, xs_tok, "st_ys", reads=[R("yst0")], writes=[R("o_ys")])

        P.op("sp", None, reads=[R("o_y", 0, 4 * NBLK), R("o_pp"), R("o_cp"), R("o_sp"), R("o_ys"), R("o_ps"), R("o_cs"), R("o_ss", 0, NS)])

        P.analyze()
        sems_e = {e: es.enter_context(nc.semaphore("se_" + e)) for e in ENGS}
        sems_d = {k: es.enter_context(nc.semaphore("sd_" + k)) for k in sorted(dma_keys)}
        P.emit(sems_e, sems_d)
    return nc


def _tile_k(w):
    K, N = w.shape
    return np.ascontiguousarray(w.reshape(K // 128, 128, N).transpose(1, 0, 2))


def _fm(v):
    return np.ascontiguousarray(v.reshape(-1, 128).T)


_NC_CACHE = {}


def kernel(x_prompt, x_sample, c_prompt, c_sample, state_pool, state_conv, state_ssm, w_ada, b_ada, norm1_w,
           w_in, w_pool, pool_scale, conv_w, conv_b, dt_bias, A_log, D_skip, ssd_norm_w, w_ssd_proj, w_out,
           norm2_w, w_ffn_in, w_ffn_out, final_norm_w):
    f = np.float32
    n = 8
    x_prompt = np.asarray(x_prompt, f)
    pvec = np.zeros((128, PV_N), f)
    pvec[:, PV_N1W:PV_N1W + 8] = _fm(np.asarray(norm1_w[0], f))
    pvec[:, PV_PSC:PV_PSC + 8] = _fm(np.asarray(pool_scale[0], f))
    cw = np.asarray(conv_w[0], f)
    for k in range(4):
        pvec[:, PV_CW + 32 * k:PV_CW + 32 * k + 32] = _fm(cw[k])
    pvec[:, PV_CB:PV_CB + 32] = _fm(np.asarray(conv_b[0], f))
    pvec[:, PV_N2W:PV_N2W + 8] = _fm(np.asarray(norm2_w[0], f))
    pvec[:, PV_BADA:PV_BADA + 48] = _fm(np.asarray(b_ada[0], f))
    pvec[:, PV_DF:PV_DF + 16] = _fm(np.repeat(np.asarray(D_skip[0], f), HP))
    rowb = np.zeros((128, RB_N), f)
    rowb[:, RB_FNW:RB_FNW + D] = np.asarray(final_norm_w, f)[None, :]
    bg = np.zeros((128, 2 * D), f)
    bg[:, 0:D] = np.asarray(b_ada[0], f)[None, 2 * D:3 * D]
    bg[:, D:2 * D] = np.asarray(b_ada[0], f)[None, 5 * D:6 * D]
    rowb[:, RB_DSK:RB_DSK + 32] = np.asarray(D_skip[0], f)[None, :]
    rowb[:, RB_ALOG:RB_ALOG + 32] = np.asarray(A_log[0], f)[None, :]
    rowb[:, RB_DTB:RB_DTB + 32] = np.asarray(dt_bias[0], f)[None, :]
    snw = np.ascontiguousarray(np.broadcast_to(np.asarray(ssd_norm_w[0], f)[None, :], (128, DI)))
    w_ada_t = _tile_k(np.asarray(w_ada[0], f))
    w_in_t = _tile_k(np.asarray(w_in[0], f))
    wp = np.asarray(w_pool[0], f)
    w_pool_t = np.ascontiguousarray(np.stack([_tile_k(wp[g]) for g in range(4)], axis=1))
    w_ssd_t = _tile_k(np.asarray(w_ssd_proj[0], f))
    w_out_t = _tile_k(np.asarray(w_out[0], f))
    wfi = np.asarray(w_ffn_in[0], f)
    perm = np.concatenate([np.concatenate([np.arange(256 * t, 256 * t + 256), DFF + np.arange(256 * t, 256 * t + 256)]) for t in range(11)])
    w_ffi_t = _tile_k(np.ascontiguousarray(wfi[:, perm]))
    w_ffo_t = _tile_k(np.asarray(w_ffn_out[0], f))

    in_maps = []
    for b in range(n):
        s0, s1 = NS * b, NS * (b + 1)
        c17 = np.concatenate([np.asarray(c_prompt[b:b + 1], f), np.asarray(c_sample[s0:s1], f)], axis=0)
        cT = np.ascontiguousarray(c17.T.reshape(8, 128, 17).transpose(1, 0, 2))
        sp = np.asarray(state_pool[0, s0:s1], f)
        sp_t = np.ascontiguousarray(sp.reshape(NS, 15, 8, 128).transpose(3, 2, 0, 1))
        sc = np.asarray(state_conv[0, s0:s1], f)
        sc_t = np.ascontiguousarray(sc.reshape(NS, 3, 32, 128).transpose(3, 2, 0, 1))
        ss = np.asarray(state_ssm[0, s0:s1], f)
        ss_t = np.ascontiguousarray(ss.reshape(NS, DI, DST).transpose(0, 2, 1))
        in_maps.append({
            "xp": np.ascontiguousarray(x_prompt[b]),
            "xsm": np.ascontiguousarray(np.asarray(x_sample[s0:s1, 0], f)),
            "cT": cT, "pvec": pvec, "rowb": rowb, "snw": snw, "bg": bg,
            "w_ada": w_ada_t, "w_in": w_in_t, "w_pool": w_pool_t, "w_ssd": w_ssd_t, "w_out": w_out_t,
            "w_ffi": w_ffi_t, "w_ffo": w_ffo_t,
            "st_pool": sp_t, "st_conv": sc_t, "st_ssm": ss_t,
        })
    nblk = int(os.environ.get("K_NBLK", NBLK))
    ncores = int(os.environ.get("K_CORES", n))
    if "nc" not in _NC_CACHE:
        _NC_CACHE["nc"] = build_nc(nblk=nblk)
    nc = _NC_CACHE["nc"]
    res = run_bass_kernel_spmd(nc, in_maps[:ncores], core_ids=list(range(ncores)))
    rs = list(res.results)
    while len(rs) < n:
        rs.append({k: np.zeros_like(v) for k, v in rs[0].items()})
    y_prompt = np.stack([rs[b]["o_y"] for b in range(n)], axis=0)
    y_sample = np.concatenate([rs[b]["o_ys"] for b in range(n)], axis=0)[:, None, :]
    pool_p = np.stack([rs[b]["o_pp"].transpose(2, 1, 0).reshape(15, D) for b in range(n)], axis=0)[None]
    conv_p = np.stack([rs[b]["o_cp"].transpose(2, 1, 0).reshape(3, CONV) for b in range(n)], axis=0)[None]
    ssm_p = np.stack([rs[b]["o_sp"].T.reshape(NH, HP, DST) for b in range(n)], axis=0)[None]
    pool_s = np.concatenate([rs[b]["o_ps"].transpose(2, 3, 1, 0).reshape(NS, 15, D) for b in range(n)], axis=0)[None]
    conv_s = np.concatenate([rs[b]["o_cs"].transpose(2, 3, 1, 0).reshape(NS, 3, CONV) for b in range(n)], axis=0)[None]
    ssm_s = np.concatenate([rs[b]["o_ss"].transpose(0, 2, 1).reshape(NS, NH, HP, DST) for b in range(n)], axis=0)[None]
    return (np.ascontiguousarray(y_prompt, dtype=f), np.ascontiguousarray(y_sample, dtype=f),
            np.ascontiguousarray(pool_p, dtype=f), np.ascontiguousarray(conv_p, dtype=f),
            np.ascontiguousarray(ssm_p, dtype=f), np.ascontiguousarray(pool_s, dtype=f),
            np.ascontiguousarray(conv_s, dtype=f), np.ascontiguousarray(ssm_s, dtype=f))
```

```python
import os
from contextlib import ExitStack

import numpy as np
import concourse.bass as bass
import concourse.mybir as mybir
from concourse.bass_utils import run_bass_kernel_spmd

F32 = mybir.dt.float32
BF16 = mybir.dt.bfloat16
AF = mybir.ActivationFunctionType
ALU = mybir.AluOpType

ENGS = ("pe", "act", "dve", "pool", "sp")

D = 1024
SEQ = 2048
TB = 512
NBLK = SEQ // TB
NS = 16
DI = 2048
NH = 32
HP = 64
NG = 8
DST = 128
CONV = 4096
DFF = 2816
IN_COLS = 9248
EPS = 1e-6

PV_N1W, PV_PSC, PV_CW, PV_CB, PV_N2W, PV_BADA, PV_DF, PV_N = 0, 8, 16, 144, 176, 184, 232, 248
RB_FNW, RB_DSK, RB_ALOG, RB_DTB, RB_N = 0, 1024, 1056, 1088, 1120


class Op:
    __slots__ = ("eng", "fn", "reads", "writes", "dkey", "dgroup", "idx", "waits",
                 "sig", "cnt", "eidx")

    def __init__(self, eng, fn, reads, writes, dkey=None, dgroup=None):
        self.eng = eng
        self.fn = fn
        self.reads = reads
        self.writes = writes
        self.dkey = dkey
        self.dgroup = dgroup
        self.waits = {}
        self.sig = False
        self.cnt = 0


class Prog:
    def __init__(self, nc):
        self.nc = nc
        self.ops = []

    def op(self, eng, fn, reads=(), writes=()):
        o = Op(eng, fn, list(reads), list(writes))
        o.idx = len(self.ops)
        self.ops.append(o)
        return o

    def dma(self, eng, fn, reads=(), writes=(), key=None, group=None):
        o = Op(eng, fn, list(reads), list(writes), dkey=key, dgroup=group)
        o.idx = len(self.ops)
        self.ops.append(o)
        return o

    def analyze(self):
        ops = self.ops
        ecount = {e: 0 for e in ENGS}
        for o in ops:
            o.eidx = ecount[o.eng]
            ecount[o.eng] += 1
        key_ops = {}
        for o in ops:
            if o.dkey is not None:
                key_ops.setdefault(o.dkey, []).append(o)
        dma_cum, dma_prev = {}, {}
        for k, lst in key_ops.items():
            groups = []
            for o in lst:
                if groups and o.dgroup is not None and groups[-1][0] == o.dgroup:
                    groups[-1][1].append(o)
                else:
                    groups.append((o.dgroup, [o]))
            cum = 0
            for g, gl in groups:
                prev = cum
                cum += len(gl)
                for o in gl:
                    dma_cum[o.idx] = cum
                    dma_prev[o.idx] = prev
        self.keys = sorted(key_ops.keys())
        recs = {}
        waited = {}
        pend = []
        for o in ops:
            d = set()
            for (buf, lo, hi) in o.reads:
                for r in recs.get(buf, ()):
                    if r[3] and r[0] < hi and lo < r[1]:
                        d.add(r[2])
            for (buf, lo, hi) in o.writes:
                for r in recs.get(buf, ()):
                    if r[0] < hi and lo < r[1]:
                        d.add(r[2])
            d.discard(o.idx)
            for (buf, lo, hi) in o.writes:
                lst = recs.setdefault(buf, [])
                lst[:] = [r for r in lst if not (lo <= r[0] and r[1] <= hi)]
                lst.append([lo, hi, o.idx, True])
            for (buf, lo, hi) in o.reads:
                lst = recs.setdefault(buf, [])
                lst[:] = [r for r in lst if not ((not r[3]) and ops[r[2]].eng == o.eng
                                                 and ops[r[2]].dkey is None and o.dkey is None
                                                 and lo <= r[0] and r[1] <= hi)]
                lst.append([lo, hi, o.idx, False])
            need = {}
            for di in d:
                p = ops[di]
                if p.dkey is not None:
                    sk = ("d", p.dkey)
                    val = 16 * dma_cum[p.idx]
                    if need.get(sk, 0) < val:
                        need[sk] = val
                    continue
                if p.eng == o.eng and o.dkey is None:
                    if o.eng in ("pe", "sp"):
                        continue
                sk = ("e", p.eng)
                cur = need.get(sk)
                if cur is None or cur.eidx < p.eidx:
                    need[sk] = p
            if o.dkey is not None and dma_prev[o.idx] > 0:
                sk = ("d", o.dkey)
                val = 16 * dma_prev[o.idx]
                if need.get(sk, 0) < val:
                    need[sk] = val
            o.waits = need
            for sk, v in need.items():
                if sk[0] == "e":
                    v.sig = True
        cnt = {e: 0 for e in ENGS}
        for o in ops:
            if o.dkey is None and o.sig:
                cnt[o.eng] += 1
            o.cnt = cnt[o.eng]
        for o in ops:
            final = {}
            for sk, v in o.waits.items():
                val = v.cnt if sk[0] == "e" else v
                wk = (o.eng, sk)
                if waited.get(wk, 0) >= val:
                    continue
                waited[wk] = val
                final[sk] = val
            o.waits = final

    def emit(self, sems_e, sems_d):
        nc = self.nc
        per = {e: [o for o in self.ops if o.eng == e] for e in ENGS}

        def run(engname, eng):
            for o in per[engname]:
                for sk, val in o.waits.items():
                    sem = sems_e[sk[1]] if sk[0] == "e" else sems_d[sk[1]]
                    eng.wait_ge(sem, val)
                if o.fn is None:
                    continue
                ins = o.fn(eng)
                if o.dkey is not None:
                    ins.then_inc(sems_d[o.dkey], 16)
                elif o.sig:
                    ins.then_inc(sems_e[o.eng], 1)

        with nc.Block() as block:
            @block.tensor
            def _(e):
                run("pe", e)

            @block.scalar
            def _(e):
                run("act", e)

            @block.vector
            def _(e):
                run("dve", e)

            @block.gpsimd
            def _(e):
                run("pool", e)

            @block.sync
            def _(e):
                run("sp", e)


def R(name, lo=0, hi=1):
    return (name, lo, hi)


def build_nc(nblk=NBLK, do_sample=True):
    nc = bass.Bass("TRN2", target_bir_lowering=False)

    def din(name, shape):
        return nc.dram_tensor(name, list(shape), F32, kind="ExternalInput").ap()

    def dout(name, shape):
        return nc.dram_tensor(name, list(shape), F32, kind="ExternalOutput").ap()

    xp = din("xp", [SEQ, D])
    xsm = din("xsm", [NS, D])
    cT = din("cT", [128, 8, 17])
    pvec = din("pvec", [128, PV_N])
    rowb_in = din("rowb", [128, RB_N])
    snw_in = din("snw", [128, DI])
    bg_in = din("bg", [128, 2 * D])
    w_ada = din("w_ada", [128, 8, 6 * D])
    w_in = din("w_in", [128, 8, IN_COLS])
    w_pool = din("w_pool", [128, 4, 2, 256])
    w_ssd = din("w_ssd", [128, 16, D])
    w_out = din("w_out", [128, 8, D])
    w_ffi = din("w_ffi", [128, 8, 2 * DFF])
    w_ffo = din("w_ffo", [128, 22, D])
    st_pool = din("st_pool", [128, 8, NS, 15])
    st_conv = din("st_conv", [128, 32, NS, 3])
    st_ssm = din("st_ssm", [NS, 128, DI])
    o_y = dout("o_y", [SEQ, D])
    o_ys = dout("o_ys", [NS, D])
    o_pp = dout("o_pp", [128, 8, 15])
    o_cp = dout("o_cp", [128, 32, 3])
    o_sp = dout("o_sp", [128, DI])
    o_ps = dout("o_ps", [128, 8, NS, 15])
    o_cs = dout("o_cs", [128, 32, NS, 3])
    o_ss = dout("o_ss", [NS, 128, DI])
    dbg_out = {}

    es = ExitStack()
    with es:
        def sb(name, shape, dt=F32):
            return es.enter_context(nc.sbuf_tensor("s_" + name, list(shape), dt))

        P = Prog(nc)
        dma_keys = set()

        def DMA(eng, out, in_, key, reads=(), writes=(), group=None):
            dma_keys.add(key)
            P.dma(eng, lambda e: e.dma_start(out=out, in_=in_), reads=reads, writes=writes, key=key, group=group)

        pb = [es.enter_context(nc.psum_tensor("pb%d" % i, [128, 512], F32)) for i in range(8)]

        def PBF(i):
            return pb[i][:].bitcast(BF16)

        def PR(i, lo=0, hi=512):
            return ("pb%d" % i, 0, 512)

        sA = sb("sA", [128, 16 + TB])
        sB = sb("sB", [128, 16 + TB])
        identf = sB[:, 0:128]
        triuf = sB[:, 128:256]
        nmf = sA[:, 0:512].rearrange("p (a b) -> p a b", a=4)
        identb = sb("identb", [128, 128], BF16)
        onesb = sb("onesb", [128, 128], BF16)
        triub = sb("triub", [128, 128], BF16)
        nmb = sb("nmb", [128, 4, 128], BF16)
        epsc = sb("epsc", [128, 1])
        invc = sb("invc", [128, 4, 16])
        pv = sb("pv", [128, PV_N])
        rowb = sb("rowb", [128, RB_N])
        snwb = sb("snwb", [128, DI], BF16)
        anegb = sb("anegb", [128, 32])
        cTs = sb("cTs", [128, 8, 17])
        silucT = sb("silucT", [128, 8, 17], BF16)
        modT = sb("modT", [128, 48, 17])
        a1T = sb("a1T", [128, 8, 17])
        a2T = sb("a2T", [128, 8, 17])
        g1B = sb("g1B", [128, D])
        g2B = sb("g2B", [128, D])
        wdt = sb("wdt", [128, 8, 32], BF16)
        diagD = sb("diagD", [128, 16, 128], BF16)
        NWB = 2
        wbuf = [sb("wbuf%d" % i, [128, 4096], BF16) for i in range(NWB)]

        P.op("pool", lambda e: e.memset(identf, 0.0), writes=[R("sB")])
        P.op("pool", lambda e: e.affine_select(out=identf, in_=identf, pattern=[[-1, 128]], compare_op=ALU.not_equal,
                                               fill=1.0, base=0, channel_multiplier=1), reads=[R("sB")], writes=[R("sB")])
        P.op("dve", lambda e: e.tensor_copy(out=identb[:], in_=identf), reads=[R("sB")], writes=[R("identb")])
        P.op("pool", lambda e: e.memset(onesb[:], 1.0), writes=[R("onesb")])
        P.op("pool", lambda e: e.memset(triuf, 1.0), writes=[R("sB")])
        P.op("pool", lambda e: e.affine_select(out=triuf, in_=triuf, pattern=[[1, 128]], compare_op=ALU.is_ge,
                                               fill=0.0, base=0, channel_multiplier=-1), reads=[R("sB")], writes=[R("sB")])
        P.op("dve", lambda e: e.tensor_copy(out=triub[:], in_=triuf), reads=[R("sB")], writes=[R("triub")])
        P.op("pool", lambda e: e.memset(nmf, 0.0), writes=[R("sA")])
        P.op("pool", lambda e: e.affine_select(out=nmf, in_=nmf, pattern=[[0, 4], [1, 128]], compare_op=ALU.is_ge,
                                               fill=-30000.0, base=0, channel_multiplier=-1), reads=[R("sA")], writes=[R("sA")])
        P.op("dve", lambda e: e.tensor_copy(out=nmb[:], in_=nmf), reads=[R("sA")], writes=[R("nmb")])
        P.op("pool", lambda e: e.memset(epsc[:], EPS), writes=[R("epsc")])
        for g in range(4):
            w = 2 ** (g + 1)
            P.op("pool", lambda e, g=g, w=w: e.memset(invc[:, g, :], 1.0 / w), writes=[R("invc")])
            for t in range(w - 1):
                P.op("pool", lambda e, g=g, t=t: e.memset(invc[:, g, t:t + 1], 1.0 / (t + 1)), writes=[R("invc")])

        DMA("sp", pv[:], pvec, "ld_pv", writes=[R("pv")])
        DMA("sp", rowb[:], rowb_in, "ld_rowb", writes=[R("rowb")])
        DMA("sp", cTs[:], cT, "ld_c", writes=[R("cTs")])
        DMA("sp", g1B[:], bg_in[:, 0:D], "ld_g1", writes=[R("g1B", 0, 2)])
        DMA("sp", g2B[:], bg_in[:, D:2 * D], "ld_g2", writes=[R("g2B", 0, 2)])
        DMA("pool", snwb[:], snw_in, "ld_snw", writes=[R("snwb")])
        DMA("pool", wdt[:], w_in[:, :, 7168:7200], "ld_wdt", writes=[R("wdt")])
        P.op("dve", lambda e: e.tensor_tensor(out=diagD[:], in0=identb[:].unsqueeze(1).to_broadcast([128, 16, 128]),
                                              in1=pv[:, PV_DF:PV_DF + 16].unsqueeze(2).to_broadcast([128, 16, 128]), op=ALU.mult),
             reads=[R("identb"), R("pv")], writes=[R("diagD")])

        P.op("act", lambda e: e.activation(out=anegb[:], in_=rowb[:, RB_ALOG:RB_ALOG + 32], func=AF.Exp), reads=[R("rowb")], writes=[R("anegb")])
        P.op("dve", lambda e: e.tensor_scalar(out=anegb[:], in0=anegb[:], scalar1=-1.0, scalar2=None, op0=ALU.mult), reads=[R("anegb")], writes=[R("anegb")])

        wstate = {"i": 0}

        NSCR = 42
        wsc = nc.dram_tensor("wsc", [NSCR, 128, 4096], BF16).ap()
        scr_idx = {}

        prefetched = {}

        def load_w(src_ap, nk, ncol, tid=None, prefetch=False):
            if tid is not None and not prefetch and tid in prefetched:
                return prefetched.pop(tid)
            res = _load_w(src_ap, nk, ncol, tid)
            if prefetch:
                prefetched[tid] = res
            return res

        def _load_w(src_ap, nk, ncol, tid=None):
            i = wstate["i"] % NWB
            wstate["i"] += 1
            wt = wbuf[i]
            view = wt[:, 0:nk * ncol].rearrange("p (k c) -> p k c", k=nk)
            if tid is None:
                DMA("pool", view, src_ap, "w%d" % i, writes=[R("wbuf%d" % i)])
            elif tid not in scr_idx:
                k = len(scr_idx)
                scr_idx[tid] = k
                DMA("pool", view, src_ap, "w%d" % i, writes=[R("wbuf%d" % i)])
                DMA("sp", wsc[k, :, 0:nk * ncol], wt[:, 0:nk * ncol], "ws%d" % i, reads=[R("wbuf%d" % i)], writes=[R("wsc", k, k + 1)])
            else:
                k = scr_idx[tid]
                DMA("sp", wt[:, 0:nk * ncol], wsc[k, :, 0:nk * ncol], "wh%d" % i, reads=[R("wsc", k, k + 1)], writes=[R("wbuf%d" % i)])
            return view, R("wbuf%d" % i)

        rot = {"i": 0}

        def next_bank(banks=(0, 1, 2, 3)):
            b = banks[rot["i"] % len(banks)]
            rot["i"] += 1
            return b

        P.op("act", lambda e: e.activation(out=silucT[:], in_=cTs[:], func=AF.Silu), reads=[R("cTs")], writes=[R("silucT")])
        for t in range(12):
            wt, wr = load_w(w_ada[:, :, t * 512:(t + 1) * 512], 8, 512)
            bk = 4 + (t % 2)
            for j in range(4):
                for kc in range(8):
                    P.op("pe", lambda e, bk=bk, j=j, kc=kc, wt=wt: e.matmul(pb[bk][:, j * 17:(j + 1) * 17], lhsT=wt[:, kc, j * 128:(j + 1) * 128],
                                                                             rhs=silucT[:, kc, :], start=(kc == 0), stop=(kc == 7)),
                         reads=[wr, R("silucT")], writes=[PR(bk, j * 17, j * 17 + 17)])
            P.op("dve", lambda e, bk=bk, t=t: e.tensor_tensor(out=modT[:, 4 * t:4 * t + 4, :], in0=pb[bk][:, 0:68].rearrange("p (a b) -> p a b", a=4),
                                                               in1=pv[:, PV_BADA + 4 * t:PV_BADA + 4 * t + 4].unsqueeze(2).to_broadcast([128, 4, 17]), op=ALU.add),
                 reads=[PR(bk, 0, 68), R("pv")], writes=[R("modT", 4 * t, 4 * t + 4)])
            if t in (4, 5, 10, 11):
                gB = g1B if t < 6 else g2B
                gname = "g1B" if t < 6 else "g2B"
                half = t % 2
                for kc in range(8):
                    P.op("pe", lambda e, kc=kc, wt=wt: e.matmul(pb[6][:, 0:512], lhsT=silucT[:, kc, 0:1].to_broadcast([128, 128]), rhs=wt[:, kc, :],
                                                                  start=(kc == 0), stop=(kc == 7)),
                         reads=[wr, R("silucT")], writes=[PR(6)])
                P.op("dve", lambda e, gB=gB, half=half: e.tensor_tensor(out=gB[:, half * 512:(half + 1) * 512], in0=pb[6][:, 0:512],
                                                                         in1=gB[:, half * 512:(half + 1) * 512], op=ALU.add),
                     reads=[PR(6), R(gname, half, half + 1)], writes=[R(gname, half, half + 1)])
        for (aT, an, sc0, nw0) in ((a1T, "a1T", 8, PV_N1W), (a2T, "a2T", 32, PV_N2W)):
            P.op("dve", lambda e, aT=aT, sc0=sc0: e.tensor_scalar(out=aT[:], in0=modT[:, sc0:sc0 + 8, :], scalar1=1.0, scalar2=None, op0=ALU.add),
                 reads=[R("modT", sc0, sc0 + 8)], writes=[R(an)])
            P.op("dve", lambda e, aT=aT, nw0=nw0: e.tensor_tensor(out=aT[:], in0=aT[:], in1=pv[:, nw0:nw0 + 8].unsqueeze(2).to_broadcast([128, 8, 17]), op=ALU.mult),
                 reads=[R(an), R("pv")], writes=[R(an)])

        bufX = [sb("bx%d" % i, [128, 4, D]) for i in range(2)]
        ssq = sb("ssq", [128, 8])
        rsq = sb("rsq", [128, 8])
        hT = sb("hT", [128, 8, TB], BF16)
        ubuf = sb("ubuf", [128, 8, 16 + TB])
        gT = sb("gT", [128, 8, TB], BF16)
        gaT = gT
        gbT = gT
        uhist = sb("uhist", [128, 8, 15])
        amT = sb("amT", [128, 8, TB], BF16)
        xpre = [sb("xpre%d" % i, [128, 3 + TB], BF16) for i in range(2)]
        dg = [sb("dg%d" % i, [128, 4, 128], BF16) for i in range(2)]
        chist = sb("chist", [128, 32, 3], BF16)
        cst = sb("cst", [128, 32, 3])
        xbc = sb("xbc", [128, 32, TB], BF16)
        dtx = sb("dtx", [128, 4, 32])
        dtt = sb("dtt", [128, 4, 32])
        dtu = sb("dtu", [128, 4, 32])
        dt_tok = sb("dt_tok", [128, 4, 32])
        dA_bf = sb("dA_bf", [128, 4, 32], BF16)
        negAcs = sb("negAcs", [128, 32])
        Eacs = sb("Eacs", [128, 32])
        dte = sb("dte", [128, 32])
        cdB = sb("cdB", [128, 32])
        xdtb = [sb("xdt%d" % i, [128, DI], BF16) for i in range(2)]
        x_tok = xdtb[1]
        xdt = xdtb[0]
        xdte = sb("xdte", [128, DI], BF16)
        B_tok = sb("B_tok", [128, 1024], BF16)
        CBs = sb("CBs", [128, 8, 128], BF16)
        dec = [sb("dec%d" % i, [128, 8, 128], BF16) for i in range(2)]
        scr = dec
        ybuf = sb("ybuf", [128, DI])
        ytmp = hT[:].rearrange("p a b -> p (a b)").bitcast(F32)
        xn = ybuf[:].bitcast(BF16).rearrange("p (a b) -> p a b", a=4)
        junk = xdte

        def role_views(k):
            xt = bufX[k]
            szv = bufX[1 - k][:].rearrange("p a b -> p (a b)").bitcast(BF16).rearrange("p (a b) -> p a b", a=4)
            pl = bufX[1 - k][:].rearrange("p a b -> p (a b)").bitcast(BF16)[:, 0:8 * TB].rearrange("p (a b) -> p a b", a=8)
            return xt, "bx%d" % k, szv, "bx%d" % (1 - k), pl
        hst = sb("hst", [128, DI])
        hstb = sb("hstb", [128, DI], BF16)
        yst = [sb("yst%d" % i, [128, D]) for i in range(1)]
        ynT = ubuf[:].rearrange("p a b -> p (a b)").bitcast(BF16)[:, 0:16 * TB].rearrange("p (a b) -> p a b", a=16)
        fT = xbc[:].rearrange("p a b -> p (a b)")[:, 0:22 * TB].rearrange("p (a b) -> p a b", a=22)

        P.op("pool", lambda e: e.memset(chist[:], 0.0), writes=[R("chist", 0, 32)])
        P.op("pool", lambda e: e.memset(hst[:], 0.0), writes=[R("hst", 0, 8)])
        P.op("pool", lambda e: e.memset(hstb[:], 0.0), writes=[R("hstb", 0, 8)])

        def rmsnorm_to_T(blk_name, aT, bcol0, xtok, XN, ntok=128, ntile=4):
            for ti in range(ntile):
                P.op("act", lambda e, ti=ti: e.activation(out=junk[:, 0:D], in_=xtok[:, ti, :], func=AF.Square, accum_out=ssq[:, ti:ti + 1]),
                     reads=[R(XN, ti, ti + 1)], writes=[R("xdte", 0, 8), R("ssq", ti, ti + 1)])
                P.op("act", lambda e, ti=ti: e.activation(out=rsq[:, 4 + ti:5 + ti], in_=ssq[:, ti:ti + 1], func=AF.Ln, scale=1.0 / D, bias=epsc[:, 0:1]),
                     reads=[R("ssq", ti, ti + 1), R("epsc")], writes=[R("rsq", 4 + ti, 5 + ti)])
                P.op("act", lambda e, ti=ti: e.activation(out=rsq[:, ti:ti + 1], in_=rsq[:, 4 + ti:5 + ti], func=AF.Exp, scale=-0.5),
                     reads=[R("rsq", 4 + ti, 5 + ti)], writes=[R("rsq", ti, ti + 1)])
                P.op("dve", lambda e, ti=ti: e.tensor_scalar(out=xn[:, ti, :], in0=xtok[:, ti, :], scalar1=rsq[:, ti:ti + 1], scalar2=None, op0=ALU.mult),
                     reads=[R(XN, ti, ti + 1), R("rsq", ti, ti + 1)], writes=[R("ybuf", 2 * ti, 2 * ti + 2)])
            for kc in range(8):
                bk = kc // 2
                c0 = (kc % 2) * 512
                for ti in range(ntile):
                    P.op("pe", lambda e, bk=bk, c0=c0, ti=ti, kc=kc: e.transpose(out=PBF(bk)[:, c0 + ti * 128:c0 + (ti + 1) * 128],
                                                                                  in_=xn[:, ti, kc * 128:(kc + 1) * 128], identity=identb[:]),
                         reads=[R("ybuf", 2 * ti, 2 * ti + 2), R("identb")], writes=[PR(bk, c0 // 2 + ti * 64, c0 // 2 + (ti + 1) * 64)])
                P.op("dve", lambda e, bk=bk, c0=c0, kc=kc, aT=aT, bcol0=bcol0: e.tensor_scalar(
                    out=hT[:, kc, :], in0=PBF(bk)[:, c0:c0 + 512], scalar1=aT[:, kc, 0:1], scalar2=modT[:, bcol0 + kc, 0:1], op0=ALU.mult, op1=ALU.add),
                    reads=[PR(bk, c0 // 2, c0 // 2 + 256), R(blk_name), R("modT", bcol0 + kc, bcol0 + kc + 1)], writes=[R("hT", kc, kc + 1)])

        def proj_ws(wt, wr, j, nk, rhs_fn, rhs_regs, bank, ncols=TB, col0=0):
            for kc in range(nk):
                P.op("pe", lambda e, kc=kc: e.matmul(pb[bank][:, 0:ncols], lhsT=wt[:, kc, col0 + j * 128:col0 + (j + 1) * 128], rhs=rhs_fn(kc),
                                                      start=(kc == 0), stop=(kc == nk - 1)),
                     reads=[wr] + rhs_regs(kc), writes=[PR(bank, 0, ncols)])

        def proj_as(wt, wr, ti, nk, lhs_fn, lhs_regs, bank, kc0=0, first=True, last=True, nktot=None):
            for kc in range(nk):
                P.op("pe", lambda e, kc=kc: e.matmul(pb[bank][:, 0:512], lhsT=lhs_fn(kc0 + kc, ti), rhs=wt[:, kc, :],
                                                      start=(first and kc == 0), stop=(last and kc == nk - 1)),
                     reads=[wr] + lhs_regs(kc0 + kc), writes=[PR(bank)])

        hT_rhs = lambda kc: hT[:, kc, :]
        hT_regs = lambda kc: [R("hT", kc, kc + 1)]
        hT_lhs = lambda kc, ti: hT[:, kc, ti * 128:(ti + 1) * 128]

        def emit_block(tb, xtok, XN, sz, SN, pooled):
            if tb == 0:
                DMA("sp", xtok[:], xp[0:TB, :].rearrange("(t p) d -> p t d", p=128), "ld_x", writes=[R(XN, 0, 4)])
            rmsnorm_to_T("a1T", a1T, 0, xtok, XN)

            for t in range(2):
                wt, wr = load_w(w_in[:, :, 7200 + t * 512:7200 + (t + 1) * 512], 8, 512, tid=('in', 7200 + t * 512))
                for j in range(4):
                    oc = 4 * t + j
                    bk = next_bank()
                    proj_ws(wt, wr, j, 8, hT_rhs, hT_regs, bk)
                    P.op("act", lambda e, bk=bk, oc=oc: e.activation(out=gaT[:, oc, :], in_=pb[bk][:, 0:TB], func=AF.Sigmoid),
                         reads=[PR(bk)], writes=[R("gT", oc, oc + 1)])
            if tb == 0:
                P.op("pool", lambda e: e.memset(ubuf[:, :, 0:16], 0.0), writes=[R("ubuf", 0, 8)])
            else:
                P.op("pool", lambda e: e.tensor_copy(out=ubuf[:, :, 1:16], in_=uhist[:]), reads=[R("uhist")], writes=[R("ubuf", 0, 8)])
            for t in range(2):
                wt, wr = load_w(w_in[:, :, t * 512:(t + 1) * 512], 8, 512, tid=('in', t * 512))
                for j in range(4):
                    oc = 4 * t + j
                    bk = next_bank()
                    proj_ws(wt, wr, j, 8, hT_rhs, hT_regs, bk)
                    P.op("act", lambda e, bk=bk, oc=oc: e.activation(out=ubuf[:, oc, 16:16 + TB], in_=pb[bk][:, 0:TB], func=AF.Copy),
                         reads=[PR(bk)], writes=[R("ubuf", oc, oc + 1)])
            for oc in range(8):
                g = oc // 2
                w = 2 ** (g + 1)
                U = ubuf[:, oc, :]
                ur = R("ubuf", oc, oc + 1)
                P.op("dve", lambda e, U=U: e.tensor_tensor(out=sA[:, 2:528], in0=U[:, 2:528], in1=U[:, 1:527], op=ALU.add), reads=[ur], writes=[R("sA")])
                cur, curname = sA, "sA"
                if g >= 1:
                    P.op("dve", lambda e: e.tensor_tensor(out=sB[:, 4:528], in0=sA[:, 4:528], in1=sA[:, 2:526], op=ALU.add), reads=[R("sA")], writes=[R("sB")])
                    cur, curname = sB, "sB"
                if g >= 2:
                    P.op("dve", lambda e: e.tensor_tensor(out=sA[:, 8:528], in0=sB[:, 8:528], in1=sB[:, 4:524], op=ALU.add), reads=[R("sB")], writes=[R("sA")])
                    cur, curname = sA, "sA"
                if g >= 3:
                    P.op("dve", lambda e: e.tensor_tensor(out=sB[:, 16:528], in0=sA[:, 16:528], in1=sA[:, 8:520], op=ALU.add), reads=[R("sA")], writes=[R("sB")])
                    cur, curname = sB, "sB"
                P.op("dve", lambda e, cur=cur, U=U, w=w, oc=oc: e.scalar_tensor_tensor(out=pooled[:, oc, :], in0=cur[:, 16:528], scalar=1.0 / w, in1=U[:, 16:528],
                                                                                        op0=ALU.mult, op1=ALU.subtract),
                     reads=[R(curname), ur], writes=[R(SN, 0, 2)])
                if tb == 0:
                    P.op("dve", lambda e, cur=cur, g=g: e.tensor_tensor(out=cur[:, 0:16], in0=cur[:, 16:32], in1=invc[:, g, :], op=ALU.mult),
                         reads=[R(curname), R("invc")], writes=[R(curname)])
                    P.op("dve", lambda e, cur=cur, U=U, oc=oc: e.tensor_tensor(out=pooled[:, oc, 0:16], in0=cur[:, 0:16], in1=U[:, 16:32], op=ALU.subtract),
                         reads=[R(curname), ur], writes=[R(SN, 0, 2)])
            wplt, wplr = load_w(w_pool.rearrange("p g k c -> p (g k) c"), 8, 256, tid=('pool',))
            for g in range(4):
                for j in range(2):
                    oc = 2 * g + j
                    bk = next_bank()
                    for k2 in range(2):
                        P.op("pe", lambda e, bk=bk, g=g, j=j, k2=k2: e.matmul(pb[bk][:, 0:TB], lhsT=wplt[:, 2 * g + k2, j * 128:(j + 1) * 128], rhs=pooled[:, 2 * g + k2, :],
                                                                                start=(k2 == 0), stop=(k2 == 1)),
                             reads=[wplr, R(SN, 0, 2)], writes=[PR(bk)])
                    P.op("dve", lambda e, bk=bk, oc=oc: e.scalar_tensor_tensor(out=amT[:, oc, :], in0=pb[bk][:, 0:TB], scalar=pv[:, PV_PSC + oc:PV_PSC + oc + 1],
                                                                                in1=gaT[:, oc, :], op0=ALU.mult, op1=ALU.mult),
                         reads=[PR(bk), R("pv"), R("gT", oc, oc + 1)], writes=[R("amT", oc, oc + 1)])
            if tb == nblk - 1:
                DMA("sp", o_pp, ubuf[:, :, 513:528], "st_pp", reads=[R("ubuf", 0, 8)], writes=[R("o_pp")])
            else:
                P.op("pool", lambda e: e.tensor_copy(out=uhist[:], in_=ubuf[:, :, 513:528]), reads=[R("ubuf", 0, 8)], writes=[R("uhist")])

            for t in range(4):
                wt, wr = load_w(w_in[:, :, 1024 + t * 512:1024 + (t + 1) * 512], 8, 512, tid=('in', 1024 + t * 512))
                for ti in range(4):
                    bk = next_bank()
                    proj_as(wt, wr, ti, 8, hT_lhs, hT_regs, bk)
                    P.op("act", lambda e, bk=bk, ti=ti, t=t: e.activation(out=sz[:, ti, t * 512:(t + 1) * 512], in_=pb[bk][:, 0:512], func=AF.Silu),
                         reads=[PR(bk)], writes=[R(SN, ti, ti + 1)])
            for t in range(8):
                wt, wr = load_w(w_in[:, :, 3072 + t * 512:3072 + (t + 1) * 512], 8, 512, tid=('in', 3072 + t * 512))
                for j in range(4):
                    oc = 4 * t + j
                    sl = oc % 2
                    bk = next_bank()
                    proj_ws(wt, wr, j, 8, hT_rhs, hT_regs, bk)
                    xpr = R("xpre%d" % sl)
                    P.op("dve", lambda e, sl=sl, oc=oc: e.tensor_copy(out=xpre[sl][:, 0:3], in_=chist[:, oc, :]), reads=[R("chist", oc, oc + 1)], writes=[xpr])
                    P.op("act", lambda e, sl=sl, bk=bk: e.activation(out=xpre[sl][:, 3:3 + TB], in_=pb[bk][:, 0:TB], func=AF.Copy), reads=[PR(bk)], writes=[xpr])
                    if tb == nblk - 1:
                        P.op("dve", lambda e, bk=bk, oc=oc: e.tensor_copy(out=cst[:, oc, :], in_=pb[bk][:, TB - 3:TB]), reads=[PR(bk)], writes=[R("cst", oc, oc + 1)])
                    else:
                        P.op("dve", lambda e, sl=sl, oc=oc: e.tensor_copy(out=chist[:, oc, :], in_=xpre[sl][:, TB:TB + 3]), reads=[xpr], writes=[R("chist", oc, oc + 1)])
                    P.op("pool", lambda e, sl=sl, oc=oc: e.tensor_tensor(out=dg[sl][:], in0=identb[:].unsqueeze(1).to_broadcast([128, 4, 128]),
                                                                          in1=pv[:, PV_CW + oc:PV_CW + oc + 97:32].unsqueeze(2).to_broadcast([128, 4, 128]), op=ALU.mult),
                         reads=[R("identb"), R("pv")], writes=[R("dg%d" % sl, 0, 4)])
                    cbk = 4 + sl
                    for k in range(4):
                        P.op("pe", lambda e, sl=sl, k=k, cbk=cbk: e.matmul(pb[cbk][:, 0:TB], lhsT=dg[sl][:, k, :], rhs=xpre[sl][:, k:k + TB], start=(k == 0), stop=(k == 3)),
                             reads=[R("dg%d" % sl, k, k + 1), xpr], writes=[PR(cbk)])
                    P.op("act", lambda e, cbk=cbk, oc=oc: e.activation(out=xbc[:, oc, :], in_=pb[cbk][:, 0:TB], func=AF.Silu, bias=pv[:, PV_CB + oc:PV_CB + oc + 1], scale=1.0),
                         reads=[PR(cbk), R("pv")], writes=[R("xbc", oc, oc + 1)])
            if tb == nblk - 1:
                DMA("sp", o_cp, cst[:], "st_cp", reads=[R("cst", 0, 32)], writes=[R("o_cp")])
            for ti in range(4):
                for kc in range(8):
                    P.op("pe", lambda e, ti=ti, kc=kc: e.matmul(pb[6][:, ti * 32:(ti + 1) * 32], lhsT=hT[:, kc, ti * 128:(ti + 1) * 128], rhs=wdt[:, kc, :],
                                                                  start=(kc == 0), stop=(kc == 7)),
                         reads=[R("hT", kc, kc + 1), R("wdt")], writes=[PR(6, ti * 32, ti * 32 + 32)])
            P.op("dve", lambda e: e.tensor_tensor(out=dtx[:], in0=pb[6][:, 0:128].rearrange("p (a b) -> p a b", a=4),
                                                  in1=rowb[:, RB_DTB:RB_DTB + 32].unsqueeze(1).to_broadcast([128, 4, 32]), op=ALU.add),
                 reads=[PR(6, 0, 128), R("rowb")], writes=[R("dtx")])
            P.op("act", lambda e: e.activation(out=dtt[:], in_=dtx[:], func=AF.Abs), reads=[R("dtx")], writes=[R("dtt")])
            P.op("act", lambda e: e.activation(out=dtu[:], in_=dtt[:], func=AF.Exp, scale=-1.0), reads=[R("dtt")], writes=[R("dtu")])
            P.op("act", lambda e: e.activation(out=dtt[:], in_=dtu[:], func=AF.Ln, bias=1.0, scale=1.0), reads=[R("dtu")], writes=[R("dtt")])
            P.op("dve", lambda e: e.scalar_tensor_tensor(out=dt_tok[:], in0=dtx[:], scalar=0.0, in1=dtt[:], op0=ALU.max, op1=ALU.add),
                 reads=[R("dtx"), R("dtt")], writes=[R("dt_tok")])
            P.op("dve", lambda e: e.tensor_tensor(out=dA_bf[:], in0=dt_tok[:], in1=anegb[:].unsqueeze(1).to_broadcast([128, 4, 32]), op=ALU.mult),
                 reads=[R("dt_tok"), R("anegb")], writes=[R("dA_bf")])
            for t in range(2):
                wt, wr = load_w(w_in[:, :, 8224 + t * 512:8224 + (t + 1) * 512], 8, 512, tid=('in', 8224 + t * 512))
                for j in range(4):
                    oc = 4 * t + j
                    bk = next_bank()
                    proj_ws(wt, wr, j, 8, hT_rhs, hT_regs, bk)
                    P.op("act", lambda e, bk=bk, oc=oc: e.activation(out=gbT[:, oc, :], in_=pb[bk][:, 0:TB], func=AF.Sigmoid),
                         reads=[PR(bk)], writes=[R("gT", oc, oc + 1)])


            def ssd_acd(ci):
                tsl = slice(ci * 128, (ci + 1) * 128)
                dAc = dA_bf[:, ci, :]
                P.op("pe", lambda e: e.matmul(pb[7][:, 0:32], lhsT=triub[:], rhs=dAc, start=True, stop=True), reads=[R("triub"), R("dA_bf")], writes=[PR(7)])
                P.op("pe", lambda e: e.matmul(pb[7][:, 32:64], lhsT=onesb[:], rhs=dAc, start=True, stop=True), reads=[R("onesb"), R("dA_bf")], writes=[PR(7)])
                P.op("dve", lambda e: e.tensor_scalar(out=negAcs[:], in0=pb[7][:, 0:32], scalar1=-1.0, scalar2=None, op0=ALU.mult), reads=[PR(7)], writes=[R("negAcs")])
                P.op("act", lambda e: e.activation(out=Eacs[:], in_=pb[7][:, 0:32], func=AF.Exp), reads=[PR(7)], writes=[R("Eacs")])
                P.op("dve", lambda e: e.tensor_tensor(out=dte[:], in0=pb[7][:, 32:64], in1=negAcs[:], op=ALU.add), reads=[PR(7), R("negAcs")], writes=[R("dte")])
                P.op("act", lambda e: e.activation(out=dte[:], in_=dte[:], func=AF.Exp), reads=[R("dte")], writes=[R("dte")])
                P.op("act", lambda e: e.activation(out=cdB[:], in_=pb[7][:, 32:64], func=AF.Exp), reads=[PR(7)], writes=[R("cdB")])
                for g in range(NG):
                    P.op("pe", lambda e, g=g: e.transpose(out=PBF(2)[:, g * 128:(g + 1) * 128], in_=xbc[:, 16 + g, tsl], identity=identb[:]),
                         reads=[R("xbc", 16 + g, 17 + g), R("identb")], writes=[PR(2)])
                P.op("act", lambda e: e.activation(out=B_tok[:], in_=PBF(2)[:, 0:1024], func=AF.Copy), reads=[PR(2)], writes=[R("B_tok")])
                for g in range(NG):
                    bk = 3 + g // 4
                    c0 = (g % 4) * 128
                    P.op("pe", lambda e, g=g, bk=bk, c0=c0: e.matmul(pb[bk][:, c0:c0 + 128], lhsT=xbc[:, 16 + g, tsl], rhs=xbc[:, 24 + g, tsl], start=True, stop=True),
                         reads=[R("xbc", 16 + g, 17 + g), R("xbc", 24 + g, 25 + g)], writes=[PR(bk)])
                for q in range(2):
                    P.op("act", lambda e, q=q: e.activation(out=CBs[:, q * 4:(q + 1) * 4, :], in_=pb[3 + q][:, 0:512].rearrange("p (a b) -> p a b", a=4), func=AF.Copy),
                         reads=[PR(3 + q)], writes=[R("CBs", q * 4, q * 4 + 4)])

            def ssd_b(ci):
                tsl = slice(ci * 128, (ci + 1) * 128)
                xd = xdtb[ci % 2]
                xr = R("xdt%d" % (ci % 2))
                for fc in range(16):
                    bk = fc // 8
                    c0 = (fc % 8) * 128
                    P.op("pe", lambda e, bk=bk, c0=c0, fc=fc: e.transpose(out=PBF(bk)[:, c0:c0 + 128], in_=xbc[:, fc, tsl], identity=identb[:]),
                         reads=[R("xbc", fc, fc + 1), R("identb")], writes=[PR(bk)])
                for bk in range(2):
                    P.op("dve", lambda e, bk=bk: e.tensor_tensor(out=xd[:, bk * 1024:(bk + 1) * 1024].rearrange("p (h q) -> p h q", h=16),
                                                                 in0=PBF(bk)[:, 0:1024].rearrange("p (h q) -> p h q", h=16),
                                                                 in1=dt_tok[:, ci, bk * 16:(bk + 1) * 16].unsqueeze(2).to_broadcast([128, 16, HP]), op=ALU.mult),
                         reads=[PR(bk), R("dt_tok")], writes=[xr])
                P.op("dve", lambda e: e.tensor_tensor(out=xdte[:].rearrange("p (h q) -> p h q", h=NH), in0=xd[:].rearrange("p (h q) -> p h q", h=NH),
                                                      in1=dte[:].unsqueeze(2).to_broadcast([128, NH, HP]), op=ALU.mult),
                     reads=[xr, R("dte")], writes=[R("xdte", 0, 8)])

            def ssd_decay(ci, hq):
                sl = hq % 2
                dbanks = (5, 6) if hq % 2 == 0 else (3, 4)
                for half in range(2):
                    bk = dbanks[half]
                    for j in range(4):
                        h = hq * 8 + half * 4 + j
                        P.op("pe", lambda e, bk=bk, j=j: e.matmul(pb[bk][:, j * 128:(j + 1) * 128], lhsT=identb[:], rhs=nmb[:, 0, :], start=True, stop=False),
                             reads=[R("identb"), R("nmb")], writes=[PR(bk)])
                        P.op("pe", lambda e, bk=bk, j=j, h=h: e.matmul(pb[bk][:, j * 128:(j + 1) * 128], lhsT=dA_bf[:, ci, h:h + 1].to_broadcast([128, 128]),
                                                                        rhs=triub[:], start=False, stop=True),
                             reads=[R("dA_bf"), R("triub")], writes=[PR(bk)])
                    for j in range(4):
                        h = hq * 8 + half * 4 + j
                        P.op("act", lambda e, bk=bk, j=j, h=h, half=half: e.activation(out=dec[sl][:, half * 4 + j, :], in_=pb[bk][:, j * 128:(j + 1) * 128],
                                                                                       func=AF.Exp, bias=negAcs[:, h:h + 1], scale=1.0),
                             reads=[PR(bk), R("negAcs")], writes=[R("dec%d" % sl, half * 4 + j, half * 4 + j + 1)])

            def ssd_y(ci, hq):
                tsl = slice(ci * 128, (ci + 1) * 128)
                sl = hq % 2
                P.op("dve", lambda e: e.tensor_tensor(out=scr[sl][:].rearrange("p (g j) l -> p g j l", g=2), in0=dec[sl][:].rearrange("p (g j) l -> p g j l", g=2),
                                                      in1=CBs[:, 2 * hq:2 * hq + 2, :].unsqueeze(2).to_broadcast([128, 2, 4, 128]), op=ALU.mult),
                     reads=[R("dec%d" % sl, 0, 8), R("CBs", 2 * hq, 2 * hq + 2)], writes=[R("dec%d" % sl, 0, 8)])
                bA = (0, 2)[hq % 2]
                bB = (1, 7)[hq % 2]
                for jj in range(8):
                    h = 8 * hq + jj
                    P.op("pe", lambda e, jj=jj, h=h: e.matmul(pb[bA][:, jj * 64:(jj + 1) * 64], lhsT=xbc[:, h // 2, tsl], rhs=diagD[:, h // 2, (h % 2) * 64:(h % 2 + 1) * 64], start=True, stop=False),
                         reads=[R("xbc", h // 2, h // 2 + 1), R("diagD")], writes=[PR(bA)])
                    P.op("pe", lambda e, jj=jj, h=h: e.matmul(pb[bA][:, jj * 64:(jj + 1) * 64], lhsT=scr[sl][:, jj, :], rhs=xdtb[ci % 2][:, h * 64:(h + 1) * 64], start=False, stop=True),
                         reads=[R("dec%d" % sl, jj, jj + 1), R("xdt%d" % (ci % 2))], writes=[PR(bA)])
                for gg in range(2):
                    g = 2 * hq + gg
                    P.op("pe", lambda e, g=g, gg=gg: e.matmul(pb[bB][:, gg * 256:(gg + 1) * 256], lhsT=xbc[:, 24 + g, tsl], rhs=hstb[:, g * 256:(g + 1) * 256], start=True, stop=True),
                         reads=[R("xbc", 24 + g, 25 + g), R("hstb", g, g + 1)], writes=[PR(bB)])

            def ssd_ye(ci, hq):
                bA = (0, 2)[hq % 2]
                bB = (1, 7)[hq % 2]
                ysl = slice(hq * 512, (hq + 1) * 512)
                yr = R("ybuf", 2 * hq, 2 * hq + 2)
                P.op("dve", lambda e: e.tensor_tensor(out=ybuf[:, ysl].rearrange("p (h q) -> p h q", h=8), in0=pb[bB][:, 0:512].rearrange("p (h q) -> p h q", h=8),
                                                      in1=Eacs[:, 8 * hq:8 * hq + 8].unsqueeze(2).to_broadcast([128, 8, HP]), op=ALU.mult),
                     reads=[PR(bB), R("Eacs")], writes=[yr])
                P.op("dve", lambda e: e.tensor_tensor(out=ybuf[:, ysl], in0=pb[bA][:, 0:512], in1=ybuf[:, ysl], op=ALU.add), reads=[PR(bA), yr], writes=[yr])
                P.op("dve", lambda e: e.tensor_tensor(out=ybuf[:, ysl], in0=ybuf[:, ysl], in1=sz[:, ci, ysl], op=ALU.mult), reads=[yr, R(SN, ci, ci + 1)], writes=[yr])

            def ssd_g(ci):
                for q in range(4):
                    sbk = 3 + q
                    hsl = slice(q * 512, (q + 1) * 512)
                    hr = R("hst", 2 * q, 2 * q + 2)
                    for gg in range(2):
                        g = 2 * q + gg
                        P.op("pe", lambda e, g=g, gg=gg, sbk=sbk: e.matmul(pb[sbk][:, gg * 256:(gg + 1) * 256], lhsT=B_tok[:, g * 128:(g + 1) * 128], rhs=xdte[:, g * 256:(g + 1) * 256], start=True, stop=True),
                             reads=[R("B_tok"), R("xdte", g, g + 1)], writes=[PR(sbk)])
                    P.op("pool", lambda e, q=q, hsl=hsl: e.tensor_tensor(out=hst[:, hsl].rearrange("p (h q) -> p h q", h=8), in0=hst[:, hsl].rearrange("p (h q) -> p h q", h=8),
                                                                         in1=cdB[:, 8 * q:8 * q + 8].unsqueeze(2).to_broadcast([128, 8, HP]), op=ALU.mult),
                         reads=[hr, R("cdB")], writes=[hr])
                    P.op("dve", lambda e, sbk=sbk, hsl=hsl: e.tensor_tensor(out=hst[:, hsl], in0=pb[sbk][:, 0:512], in1=hst[:, hsl], op=ALU.add), reads=[PR(sbk), hr], writes=[hr])
                    P.op("act", lambda e, hsl=hsl: e.activation(out=hstb[:, hsl], in_=hst[:, hsl], func=AF.Copy), reads=[hr], writes=[R("hstb", 2 * q, 2 * q + 2)])

            def ssd_h(ci):
                tsl = slice(ci * 128, (ci + 1) * 128)
                ynb2 = xdtb[ci % 2]
                xr = R("xdt%d" % (ci % 2))
                P.op("act", lambda e: e.activation(out=ynb2[:], in_=ybuf[:], func=AF.Square, accum_out=ssq[:, 4:5]), reads=[R("ybuf", 0, 8)], writes=[xr, R("ssq", 4, 5)])
                P.op("act", lambda e: e.activation(out=ssq[:, 5:6], in_=ssq[:, 4:5], func=AF.Ln, scale=1.0 / DI, bias=epsc[:, 0:1]), reads=[R("ssq", 4, 5), R("epsc")], writes=[R("ssq", 5, 6)])
                P.op("act", lambda e: e.activation(out=ssq[:, 6:7], in_=ssq[:, 5:6], func=AF.Exp, scale=-0.5), reads=[R("ssq", 5, 6)], writes=[R("ssq", 6, 7)])
                P.op("dve", lambda e: e.scalar_tensor_tensor(out=ynb2[:], in0=ybuf[:], scalar=ssq[:, 6:7], in1=snwb[:], op0=ALU.mult, op1=ALU.mult),
                     reads=[R("ybuf", 0, 8), R("ssq", 6, 7), R("snwb")], writes=[xr])
                for fc in range(16):
                    bk = fc // 8
                    c0 = (fc % 8) * 128
                    P.op("pe", lambda e, bk=bk, c0=c0, fc=fc: e.transpose(out=PBF(bk)[:, c0:c0 + 128], in_=ynb2[:, fc * 128:(fc + 1) * 128], identity=identb[:]),
                         reads=[xr, R("identb")], writes=[PR(bk)])
                for q in range(2):
                    P.op("act", lambda e, q=q: e.activation(out=ynT[:, q * 8:(q + 1) * 8, tsl], in_=PBF(q)[:, 0:1024].rearrange("p (a b) -> p a b", a=8), func=AF.Copy),
                         reads=[PR(q)], writes=[R("ubuf", 0, 8)])

            ssd_acd(0)
            ssd_b(0)
            for ci in range(4):
                if ci == 0:
                    ssd_decay(ci, 0)
                ssd_decay(ci, 1)
                ssd_y(ci, 0)
                for hq in range(4):
                    if hq + 2 < 4:
                        ssd_decay(ci, hq + 2)
                    if hq + 1 < 4:
                        ssd_y(ci, hq + 1)
                    ssd_ye(ci, hq)
                ssd_g(ci)
                if ci + 1 < 4:
                    ssd_acd(ci + 1)
                    ssd_b(ci + 1)
                    ssd_decay(ci + 1, 0)
                ssd_h(ci)
            if tb == nblk - 1:
                DMA("sp", o_sp, hst[:], "st_sp", reads=[R("hst", 0, 8)], writes=[R("o_sp")])

            if tb + 1 < nblk:
                DMA("sp", bufX[(tb + 1) % 2][:], xp[(tb + 1) * TB:(tb + 2) * TB, :].rearrange("(t p) d -> p t d", p=128), "ld_x", writes=[R(SN, 0, 4)])
            for t in range(4):
                wt, wr = load_w(w_ssd[:, :, t * 256:(t + 1) * 256], 16, 256, tid=('ssd', t))
                for j in range(2):
                    oc = 2 * t + j
                    bk = next_bank()
                    proj_ws(wt, wr, j, 16, lambda kc: ynT[:, kc, :], lambda kc: [R("ubuf", 0, 8)], bk)
                    P.op("dve", lambda e, bk=bk, oc=oc: e.tensor_tensor(out=sA[:, 0:TB], in0=pb[bk][:, 0:TB], in1=gbT[:, oc, :], op=ALU.mult),
                         reads=[PR(bk), R("gT", oc, oc + 1)], writes=[R("sA")])
                    P.op("dve", lambda e, oc=oc: e.tensor_tensor(out=hT[:, oc, :], in0=sA[:, 0:TB], in1=amT[:, oc, :], op=ALU.add),
                         reads=[R("sA"), R("amT", oc, oc + 1)], writes=[R("hT", oc, oc + 1)])
            for t in range(2):
                wt, wr = load_w(w_out[:, :, t * 512:(t + 1) * 512], 8, 512, tid=('out', t))
                for ti in range(4):
                    bk = next_bank()
                    proj_as(wt, wr, ti, 8, hT_lhs, hT_regs, bk)
                    P.op("dve", lambda e, bk=bk, t=t: e.tensor_tensor(out=sB[:, 0:512], in0=pb[bk][:, 0:512], in1=g1B[:, t * 512:(t + 1) * 512], op=ALU.mult),
                         reads=[PR(bk), R("g1B", t, t + 1)], writes=[R("sB")])
                    P.op("dve", lambda e, ti=ti, t=t: e.tensor_tensor(out=xtok[:, ti, t * 512:(t + 1) * 512], in0=xtok[:, ti, t * 512:(t + 1) * 512], in1=sB[:, 0:512], op=ALU.add),
                         reads=[R("sB"), R(XN, ti, ti + 1)], writes=[R(XN, ti, ti + 1)])
            rmsnorm_to_T("a2T", a2T, 24, xtok, XN)
            for t in range(11):
                wt, wr = load_w(w_ffi[:, :, t * 512:(t + 1) * 512], 8, 512, tid=('ffi', t))
                for j in range(2):
                    fc = 2 * t + j
                    bg = next_bank()
                    proj_ws(wt, wr, j, 8, hT_rhs, hT_regs, bg)
                    bu = next_bank()
                    proj_ws(wt, wr, j, 8, hT_rhs, hT_regs, bu, col0=256)
                    P.op("act", lambda e, bg=bg: e.activation(out=sA[:, 0:TB], in_=pb[bg][:, 0:TB], func=AF.Silu), reads=[PR(bg)], writes=[R("sA")])
                    P.op("dve", lambda e, bu=bu, fc=fc: e.tensor_tensor(out=fT[:, fc, :], in0=pb[bu][:, 0:TB], in1=sA[:, 0:TB], op=ALU.mult),
                         reads=[PR(bu), R("sA")], writes=[R("xbc", fc, fc + 1)])
            fT_lhs = lambda kc, ti: fT[:, kc, ti * 128:(ti + 1) * 128]
            fT_regs = lambda kc: [R("xbc", kc, kc + 1)]
            for half in range(2):
                for kg in range(3):
                    nk = 8 if kg < 2 else 6
                    wt, wr = load_w(w_ffo[:, kg * 8:kg * 8 + nk, half * 512:(half + 1) * 512], nk, 512, tid=('ffo', kg, half))
                    for ti in range(4):
                        proj_as(wt, wr, ti, nk, fT_lhs, fT_regs, 4 + ti, kc0=kg * 8, first=(kg == 0), last=(kg == 2))
                for ti in range(4):
                    P.op("dve", lambda e, ti=ti, half=half: e.tensor_tensor(out=sB[:, 0:512], in0=pb[4 + ti][:, 0:512], in1=g2B[:, half * 512:(half + 1) * 512], op=ALU.mult),
                         reads=[PR(4 + ti), R("g2B", half, half + 1)], writes=[R("sB")])
                    P.op("dve", lambda e, ti=ti, half=half: e.tensor_tensor(out=xtok[:, ti, half * 512:(half + 1) * 512], in0=xtok[:, ti, half * 512:(half + 1) * 512], in1=sB[:, 0:512], op=ALU.add),
                         reads=[R("sB"), R(XN, ti, ti + 1)], writes=[R(XN, ti, ti + 1)])
            if tb + 1 < nblk:
                for t in range(2):
                    load_w(w_in[:, :, 7200 + t * 512:7200 + (t + 1) * 512], 8, 512, tid=('in', 7200 + t * 512), prefetch=True)
            elif do_sample:
                for c0 in (0, 512):
                    load_w(w_in[:, :, c0:c0 + 512], 8, 512, tid=('in', c0), prefetch=True)
            for ti in range(4):
                ys = 0
                P.op("act", lambda e, ti=ti: e.activation(out=junk[:, 0:D], in_=xtok[:, ti, :], func=AF.Square, accum_out=ssq[:, ti:ti + 1]),
                     reads=[R(XN, ti, ti + 1)], writes=[R("xdte", 0, 8), R("ssq", ti, ti + 1)])
                P.op("act", lambda e, ti=ti: e.activation(out=rsq[:, 4 + ti:5 + ti], in_=ssq[:, ti:ti + 1], func=AF.Ln, scale=1.0 / D, bias=epsc[:, 0:1]),
                     reads=[R("ssq", ti, ti + 1), R("epsc")], writes=[R("rsq", 4 + ti, 5 + ti)])
                P.op("act", lambda e, ti=ti: e.activation(out=rsq[:, ti:ti + 1], in_=rsq[:, 4 + ti:5 + ti], func=AF.Exp, scale=-0.5), reads=[R("rsq", 4 + ti, 5 + ti)], writes=[R("rsq", ti, ti + 1)])
                P.op("dve", lambda e, ti=ti, ys=ys: e.scalar_tensor_tensor(out=yst[ys][:], in0=xtok[:, ti, :], scalar=rsq[:, ti:ti + 1], in1=rowb[:, RB_FNW:RB_FNW + D], op0=ALU.mult, op1=ALU.mult),
                     reads=[R(XN, ti, ti + 1), R("rsq", ti, ti + 1), R("rowb")], writes=[R("yst%d" % ys)])
                r0 = tb * TB + ti * 128
                DMA("sp", o_y[r0:r0 + 128, :], yst[ys][:], "st_y%d" % ys, reads=[R("yst%d" % ys)], writes=[R("o_y", tb * 4 + ti, tb * 4 + ti + 1)])


        for tb in range(nblk):
            emit_block(tb, *role_views(tb % 2))
        xtok, XN, sz, SN, _pl = role_views((nblk - 1) % 2)

        if do_sample:
            Ssl = slice(1, 17)
            xbcF = xbc[:].rearrange("p a b -> p (a b)")
            stbuf = [xtok[:, 0:2, :].rearrange("p a b -> p (a b)"), xtok[:, 2:4, :].rearrange("p a b -> p (a b)")]
            streg = [R(XN, 0, 2), R(XN, 2, 4)]
            ubF = ubuf[:].rearrange("p a b -> p (a b)")
            stp = ubF[:, 0:1920].rearrange("p (c s r) -> p c s r", c=8, s=NS)
            newst = ubF[:, 1920:3840].rearrange("p (c s r) -> p c s r", c=8, s=NS)
            stc = ybuf[:, 0:1536].rearrange("p (c s r) -> p c s r", c=32, s=NS)
            newcst = hst[:, 0:1536].rearrange("p (c s r) -> p c s r", c=32, s=NS)
            gTf = gT[:].rearrange("p a b -> p (a b)").bitcast(F32)
            projS = gTf[:, 0:56 * NS].rearrange("p (c s) -> p c s", c=56)
            amF = amT[:].rearrange("p a b -> p (a b)").bitcast(F32)
            acc1 = amF[:, 0:512].rearrange("p (c s) -> p c s", c=32)
            acc2 = amF[:, 512:1024].rearrange("p (c s) -> p c s", c=32)
            amS = amF[:, 1024:1152].rearrange("p (c s) -> p c s", c=8)
            gaS = amF[:, 1152:1280].rearrange("p (c s) -> p c s", c=8)
            gbS = amF[:, 1280:1408].rearrange("p (c s) -> p c s", c=8)
            ptmp = amF[:, 1408:1536].rearrange("p (c s) -> p c s", c=8)
            sgS = amF[:, 1536:1568]
            xbcS = B_tok[:, 0:512].rearrange("p (c s) -> p c s", c=32)
            CBf = CBs[:].rearrange("p a b -> p (a b)")
            hTs = CBf[:, 0:128].rearrange("p (c s) -> p c s", c=8)
            mixTs = CBf[:, 128:256].rearrange("p (c s) -> p c s", c=8)
            pooledS = CBf[:, 256:384].rearrange("p (c s) -> p c s", c=8)
            ynTs = CBf[:, 384:640].rearrange("p (c s) -> p c s", c=16)
            fTs = CBf[:, 640:992].rearrange("p (c s) -> p c s", c=22)
            x_tokS = x_tok[0:NS, :]
            xdt_tokS = xdt[0:NS, :]
            szS = xdte[0:NS, :]
            xs_tok = yst[0][0:NS, :]
            szf = sz[:].rearrange("p a b -> p (a b)").bitcast(F32)
            ysS = szf[0:NS, 0:2048]
            gs1 = szf[0:NS, 2048:3072]
            gs2 = szf[0:NS, 3072:4096]
            xnS = dec[0][:].rearrange("p a b -> p (a b)")[0:NS, :]
            hTF = hT[:].rearrange("p a b -> p (a b)")
            ynS = hTF[0:NS, 0:2048]
            junkS = hTF[0:NS, 2048:4096]
            tmpDx = hTF[0:NS, :].bitcast(F32)
            decBs = sA[:, 0:512].rearrange("p (s h) -> p s h", s=NS)
            mask16 = sB[:, 0:256].rearrange("p (a b) -> p a b", a=NS)
            identfS = sB[:, 256:384]
            CmaskS = xbcF[:, 0:2048].rearrange("p (g s m) -> p g s m", g=NG, s=NS)
            ssS, rsS = ssq[0:NS, :], rsq[0:NS, :]
            RCB = R("CBs", 0, 8)
            RAM = R("amT", 0, 8)

            DMA("sp", xs_tok, xsm, "ld_xs", writes=[R("yst0")])
            DMA("sp", stp, st_pool, "ld_stp", writes=[R("ubuf", 0, 8)])
            DMA("sp", stc, st_conv, "ld_stc", writes=[R("ybuf", 0, 8)])
            P.op("pool", lambda e: e.memset(sB[:, 0:384], 0.0), writes=[R("sB")])
            P.op("pool", lambda e: e.affine_select(out=mask16, in_=mask16, pattern=[[1, NS], [-1, NS]], compare_op=ALU.not_equal, fill=1.0, base=0, channel_multiplier=0),
                 reads=[R("sB")], writes=[R("sB")])
            P.op("pool", lambda e: e.affine_select(out=identfS, in_=identfS, pattern=[[-1, 128]], compare_op=ALU.not_equal, fill=1.0, base=0, channel_multiplier=1),
                 reads=[R("sB")], writes=[R("sB")])
            for (gsv, c0, b0) in ((gs1, 16, 0), (gs2, 40, 2)):
                for c in range(8):
                    bk = b0 + c // 4
                    P.op("pe", lambda e, bk=bk, c=c, c0=c0: e.matmul(pb[bk][0:NS, (c % 4) * 128:(c % 4 + 1) * 128], lhsT=modT[:, c0 + c, Ssl], rhs=identfS, start=True, stop=True),
                         reads=[R("modT", c0 + c, c0 + c + 1), R("sB")], writes=[PR(bk)])
                for q in range(2):
                    P.op("act", lambda e, gsv=gsv, q=q, b0=b0: e.activation(out=gsv[:, q * 512:(q + 1) * 512], in_=pb[b0 + q][0:NS, 0:512], func=AF.Copy),
                         reads=[PR(b0 + q)], writes=[R(SN, 0, 4)])

            def s_norm_T(aT, bcol0):
                P.op("act", lambda e: e.activation(out=junkS[:, 0:D], in_=xs_tok, func=AF.Square, accum_out=ssS[:, 0:1]), reads=[R("yst0")], writes=[R("hT", 0, 8), R("ssq", 0, 8)])
                P.op("act", lambda e: e.activation(out=rsS[:, 4:5], in_=ssS[:, 0:1], func=AF.Ln, scale=1.0 / D, bias=epsc[0:NS, 0:1]), reads=[R("ssq", 0, 8), R("epsc")], writes=[R("rsq", 0, 8)])
                P.op("act", lambda e: e.activation(out=rsS[:, 0:1], in_=rsS[:, 4:5], func=AF.Exp, scale=-0.5), reads=[R("rsq", 0, 8)], writes=[R("rsq", 0, 8)])
                P.op("dve", lambda e: e.tensor_scalar(out=xnS, in0=xs_tok, scalar1=rsS[:, 0:1], scalar2=None, op0=ALU.mult), reads=[R("yst0"), R("rsq", 0, 8)], writes=[R("dec0", 0, 8)])
                for kc in range(8):
                    P.op("pe", lambda e, kc=kc: e.transpose(out=PBF(0)[:, kc * NS:(kc + 1) * NS], in_=xnS[:, kc * 128:(kc + 1) * 128], identity=identb[0:NS, 0:NS]),
                         reads=[R("dec0", 0, 8), R("identb")], writes=[PR(0)])
                P.op("dve", lambda e, aT=aT: e.tensor_tensor(out=ptmp, in0=PBF(0)[:, 0:128].rearrange("p (c s) -> p c s", c=8), in1=aT[:, :, Ssl], op=ALU.mult),
                     reads=[PR(0), R("a1T"), R("a2T")], writes=[RAM])
                P.op("dve", lambda e, bcol0=bcol0: e.tensor_tensor(out=hTs, in0=ptmp, in1=modT[:, bcol0:bcol0 + 8, Ssl], op=ALU.add),
                     reads=[RAM, R("modT", bcol0, bcol0 + 8)], writes=[RCB])

            s_norm_T(a1T, 0)
            ws_tiles = [(0, 0), (512, 4)] + [(3072 + 512 * t, 8 + 4 * t) for t in range(8)] + [(7200 + 512 * t, 40 + 4 * t) for t in range(4)]
            for (col0, cb0) in ws_tiles:
                wt, wr = load_w(w_in[:, :, col0:col0 + 512], 8, 512, tid=('in', col0))
                for j in range(4):
                    for kc in range(8):
                        P.op("pe", lambda e, j=j, kc=kc, wt=wt: e.matmul(pb[1][:, j * NS:(j + 1) * NS], lhsT=wt[:, kc, j * 128:(j + 1) * 128], rhs=hTs[:, kc, :], start=(kc == 0), stop=(kc == 7)),
                             reads=[wr, RCB], writes=[PR(1)])
                P.op("act", lambda e, cb0=cb0: e.activation(out=projS[:, cb0:cb0 + 4, :], in_=pb[1][:, 0:4 * NS].rearrange("p (c s) -> p c s", c=4), func=AF.Copy),
                     reads=[PR(1)], writes=[R("gT", 0, 8)])
            for t in range(4):
                wt, wr = load_w(w_in[:, :, 1024 + t * 512:1024 + (t + 1) * 512], 8, 512, tid=('in', 1024 + t * 512))
                for kc in range(8):
                    P.op("pe", lambda e, kc=kc, wt=wt: e.matmul(pb[2][0:NS, 0:512], lhsT=hTs[:, kc, :], rhs=wt[:, kc, :], start=(kc == 0), stop=(kc == 7)),
                         reads=[wr, RCB], writes=[PR(2)])
                P.op("act", lambda e, t=t: e.activation(out=szS[:, t * 512:(t + 1) * 512], in_=pb[2][0:NS, 0:512], func=AF.Silu), reads=[PR(2)], writes=[R("xdte", 0, 8)])
            for kc in range(8):
                P.op("pe", lambda e, kc=kc: e.matmul(pb[3][0:NS, 0:32], lhsT=hTs[:, kc, :], rhs=wdt[:, kc, :], start=(kc == 0), stop=(kc == 7)), reads=[RCB, R("wdt")], writes=[PR(3)])
            d_x, d_t, d_u, d_dt, d_dec = dtx[0:NS, 0, :], dtt[0:NS, 0, :], dtu[0:NS, 0, :], dt_tok[0:NS, 0, :], dtx[0:NS, 1, :]
            P.op("dve", lambda e: e.tensor_tensor(out=d_x, in0=pb[3][0:NS, 0:32], in1=rowb[0:NS, RB_DTB:RB_DTB + 32], op=ALU.add), reads=[PR(3), R("rowb")], writes=[R("dtx")])
            P.op("act", lambda e: e.activation(out=d_t, in_=d_x, func=AF.Abs), reads=[R("dtx")], writes=[R("dtt")])
            P.op("act", lambda e: e.activation(out=d_u, in_=d_t, func=AF.Exp, scale=-1.0), reads=[R("dtt")], writes=[R("dtu")])
            P.op("act", lambda e: e.activation(out=d_t, in_=d_u, func=AF.Ln, bias=1.0, scale=1.0), reads=[R("dtu")], writes=[R("dtt")])
            P.op("dve", lambda e: e.scalar_tensor_tensor(out=d_dt, in0=d_x, scalar=0.0, in1=d_t, op0=ALU.max, op1=ALU.add), reads=[R("dtx"), R("dtt")], writes=[R("dt_tok")])
            P.op("dve", lambda e: e.tensor_tensor(out=d_u, in0=d_dt, in1=anegb[0:NS, :], op=ALU.mult), reads=[R("dt_tok"), R("anegb")], writes=[R("dtu")])
            P.op("act", lambda e: e.activation(out=d_dec, in_=d_u, func=AF.Exp), reads=[R("dtu")], writes=[R("dtx")])
            for s_ in range(NS):
                P.op("pe", lambda e, s_=s_: e.matmul(pb[0][:, s_ * 32:(s_ + 1) * 32], lhsT=identfS[0:NS, s_:s_ + 1].to_broadcast([NS, 128]), rhs=d_dec, start=True, stop=True),
                     reads=[R("sB"), R("dtx")], writes=[PR(0)])
            P.op("act", lambda e: e.activation(out=sA[:, 0:512], in_=pb[0][:, 0:512], func=AF.Copy), reads=[PR(0)], writes=[R("sA")])
            for g in range(4):
                w = 2 ** (g + 1)
                ug = projS[:, 2 * g:2 * g + 2, :]
                P.op("dve", lambda e, g=g, w=w: e.reduce_sum(out=ptmp[:, 0:2, :], in_=stp[:, 2 * g:2 * g + 2, :, 15 - (w - 1):15], axis=mybir.AxisListType.X),
                     reads=[R("ubuf", 0, 8)], writes=[RAM])
                P.op("dve", lambda e, ug=ug: e.tensor_tensor(out=ptmp[:, 0:2, :], in0=ptmp[:, 0:2, :], in1=ug, op=ALU.add), reads=[RAM, R("gT", 0, 8)], writes=[RAM])
                P.op("dve", lambda e, ug=ug, g=g, w=w: e.scalar_tensor_tensor(out=pooledS[:, 2 * g:2 * g + 2, :], in0=ptmp[:, 0:2, :], scalar=1.0 / w, in1=ug, op0=ALU.mult, op1=ALU.subtract),
                     reads=[RAM, R("gT", 0, 8)], writes=[RCB])
            P.op("act", lambda e: e.activation(out=gaS, in_=projS[:, 40:48, :], func=AF.Sigmoid), reads=[R("gT", 0, 8)], writes=[RAM])
            P.op("act", lambda e: e.activation(out=gbS, in_=projS[:, 48:56, :], func=AF.Sigmoid), reads=[R("gT", 0, 8)], writes=[RAM])
            wplt, wplr = load_w(w_pool.rearrange("p g k c -> p (g k) c"), 8, 256, tid=('pool',))
            for g in range(4):
                for j in range(2):
                    oc = 2 * g + j
                    for k2 in range(2):
                        P.op("pe", lambda e, g=g, j=j, k2=k2, oc=oc: e.matmul(pb[2][:, oc * NS:(oc + 1) * NS], lhsT=wplt[:, 2 * g + k2, j * 128:(j + 1) * 128], rhs=pooledS[:, 2 * g + k2, :], start=(k2 == 0), stop=(k2 == 1)),
                             reads=[wplr, RCB], writes=[PR(2)])
            for oc in range(8):
                P.op("dve", lambda e, oc=oc: e.scalar_tensor_tensor(out=amS[:, oc, :], in0=pb[2][:, oc * NS:(oc + 1) * NS], scalar=pv[:, PV_PSC + oc:PV_PSC + oc + 1], in1=gaS[:, oc, :], op0=ALU.mult, op1=ALU.mult),
                     reads=[PR(2), R("pv"), RAM], writes=[RAM])
            P.op("pool", lambda e: e.tensor_copy(out=newst[:, :, :, 0:14], in_=stp[:, :, :, 1:15]), reads=[R("ubuf", 0, 8)], writes=[R("ubuf", 0, 8)])
            P.op("pool", lambda e: e.tensor_copy(out=newst[:, :, :, 14], in_=projS[:, 0:8, :]), reads=[R("gT", 0, 8)], writes=[R("ubuf", 0, 8)])
            DMA("sp", o_ps, newst, "st_ps", reads=[R("ubuf", 0, 8)], writes=[R("o_ps")])
            xnew = projS[:, 8:40, :]
            cwb = lambda k: pv[:, PV_CW + 32 * k:PV_CW + 32 * k + 32].unsqueeze(2).to_broadcast([128, 32, NS])
            P.op("dve", lambda e: e.tensor_tensor(out=acc1, in0=xnew, in1=cwb(3), op=ALU.mult), reads=[R("gT", 0, 8), R("pv")], writes=[RAM])
            for k in range(3):
                P.op("dve", lambda e, k=k: e.tensor_tensor(out=acc2, in0=stc[:, :, :, k], in1=cwb(k), op=ALU.mult), reads=[R("ybuf", 0, 8), R("pv")], writes=[RAM])
                P.op("dve", lambda e: e.tensor_tensor(out=acc1, in0=acc1, in1=acc2, op=ALU.add), reads=[RAM], writes=[RAM])
            P.op("dve", lambda e: e.tensor_tensor(out=acc1, in0=acc1, in1=pv[:, PV_CB:PV_CB + 32].unsqueeze(2).to_broadcast([128, 32, NS]), op=ALU.add), reads=[RAM, R("pv")], writes=[RAM])
            P.op("act", lambda e: e.activation(out=xbcS, in_=acc1, func=AF.Silu), reads=[RAM], writes=[R("B_tok")])
            P.op("pool", lambda e: e.tensor_copy(out=newcst[:, :, :, 0:2], in_=stc[:, :, :, 1:3]), reads=[R("ybuf", 0, 8)], writes=[R("hst", 0, 8)])
            P.op("pool", lambda e: e.tensor_copy(out=newcst[:, :, :, 2], in_=xnew), reads=[R("gT", 0, 8)], writes=[R("hst", 0, 8)])
            DMA("sp", o_cs, newcst, "st_cs", reads=[R("hst", 0, 8)], writes=[R("o_cs")])
            for fc in range(16):
                bk = 1 + fc // 8
                P.op("pe", lambda e, fc=fc, bk=bk: e.transpose(out=PBF(bk)[0:NS, (fc % 8) * 128:(fc % 8 + 1) * 128], in_=xbcS[:, fc, :], identity=identb[:]),
                     reads=[R("B_tok"), R("identb")], writes=[PR(bk)])
            for q in range(2):
                P.op("act", lambda e, q=q: e.activation(out=x_tokS[:, q * 1024:(q + 1) * 1024], in_=PBF(1 + q)[0:NS, 0:1024], func=AF.Copy), reads=[PR(1 + q)], writes=[R("xdt1")])
            P.op("dve", lambda e: e.tensor_tensor(out=xdt_tokS.rearrange("p (h q) -> p h q", h=NH), in0=x_tokS.rearrange("p (h q) -> p h q", h=NH),
                                                  in1=d_dt.unsqueeze(2).to_broadcast([NS, NH, HP]), op=ALU.mult), reads=[R("xdt1"), R("dt_tok")], writes=[R("xdt0")])
            P.op("dve", lambda e: e.tensor_tensor(out=CmaskS, in0=xbcS[:, 24:32, :].unsqueeze(3).to_broadcast([128, NG, NS, NS]),
                                                  in1=mask16.unsqueeze(1).to_broadcast([128, NG, NS, NS]), op=ALU.mult), reads=[R("B_tok"), R("sB")], writes=[R("xbc", 0, 32)])
            P.op("pool", lambda e: e.memset(ysS, 0.0), writes=[R(SN, 0, 4)])
            def samp_L(s_):
                sl = s_ % 2
                DMA("sp", stbuf[sl], st_ssm[s_], "ld_st%d" % sl, writes=[streg[sl]])

            def samp_A(s_):
                sl = s_ % 2
                buf = stbuf[sl]
                for q in range(4):
                    P.op("pe", lambda e, q=q: e.matmul(pb[q][:, 0:512], lhsT=identb[0:NS, s_:s_ + 1].to_broadcast([NS, 128]), rhs=xdt_tokS[:, q * 512:(q + 1) * 512], start=True, stop=True),
                         reads=[R("identb"), R("xdt0")], writes=[PR(q)])
                P.op("pool", lambda e: e.tensor_tensor(out=buf.rearrange("p (h q) -> p h q", h=NH), in0=buf.rearrange("p (h q) -> p h q", h=NH),
                                                       in1=decBs[:, s_, :].unsqueeze(2).to_broadcast([128, NH, HP]), op=ALU.mult), reads=[streg[sl], R("sA")], writes=[streg[sl]])
                for g in range(NG):
                    P.op("dve", lambda e, g=g: e.scalar_tensor_tensor(out=buf[:, g * 256:(g + 1) * 256], in0=pb[g // 2][:, (g % 2) * 256:(g % 2 + 1) * 256],
                                                                       scalar=xbcS[:, 16 + g, s_:s_ + 1], in1=buf[:, g * 256:(g + 1) * 256], op0=ALU.mult, op1=ALU.add),
                         reads=[PR(g // 2), R("B_tok"), streg[sl]], writes=[streg[sl]])
                P.op("act", lambda e: e.activation(out=hstb[:], in_=buf, func=AF.Copy), reads=[streg[sl]], writes=[R("hstb", 0, 8)])
                DMA("sp", o_ss[s_], buf, "st_ss%d" % sl, reads=[streg[sl]], writes=[R("o_ss", s_, s_ + 1)])

            def samp_A2(s_):
                for g in range(NG):
                    P.op("pe", lambda e, g=g: e.matmul(pb[4 + g // 2][0:NS, (g % 2) * 256:(g % 2 + 1) * 256], lhsT=CmaskS[:, g, s_, :], rhs=hstb[:, g * 256:(g + 1) * 256], start=True, stop=True),
                         reads=[R("xbc", 0, 32), R("hstb", 0, 8)], writes=[PR(4 + g // 2)])

            def samp_B(s_):
                for q in range(4):
                    P.op("dve", lambda e, q=q: e.tensor_tensor(out=ysS[:, q * 512:(q + 1) * 512], in0=pb[4 + q][0:NS, 0:512], in1=ysS[:, q * 512:(q + 1) * 512], op=ALU.add),
                         reads=[PR(4 + q), R(SN, 0, 4)], writes=[R(SN, 0, 4)])

            samp_L(0)
            samp_L(1)
            samp_A(0)
            samp_A2(0)
            for s_ in range(NS):
                if s_ + 2 < NS:
                    samp_L(s_ + 2)
                if s_ + 1 < NS:
                    samp_A(s_ + 1)
                samp_B(s_)
                if s_ + 1 < NS:
                    samp_A2(s_ + 1)
            P.op("dve", lambda e: e.tensor_tensor(out=tmpDx.rearrange("p (h q) -> p h q", h=NH), in0=x_tokS.rearrange("p (h q) -> p h q", h=NH),
                                                  in1=rowb[0:NS, RB_DSK:RB_DSK + 32].unsqueeze(2).to_broadcast([NS, NH, HP]), op=ALU.mult), reads=[R("xdt1"), R("rowb")], writes=[R("hT", 0, 8)])
            P.op("dve", lambda e: e.tensor_tensor(out=ysS, in0=ysS, in1=tmpDx, op=ALU.add), reads=[R(SN, 0, 4), R("hT", 0, 8)], writes=[R(SN, 0, 4)])
            P.op("dve", lambda e: e.tensor_tensor(out=ysS, in0=ysS, in1=szS, op=ALU.mult), reads=[R(SN, 0, 4), R("xdte", 0, 8)], writes=[R(SN, 0, 4)])
            P.op("act", lambda e: e.activation(out=junkS, in_=ysS, func=AF.Square, accum_out=ssS[:, 1:2]), reads=[R(SN, 0, 4)], writes=[R("hT", 0, 8), R("ssq", 0, 8)])
            P.op("act", lambda e: e.activation(out=rsS[:, 5:6], in_=ssS[:, 1:2], func=AF.Ln, scale=1.0 / DI, bias=epsc[0:NS, 0:1]), reads=[R("ssq", 0, 8), R("epsc")], writes=[R("rsq", 0, 8)])
            P.op("act", lambda e: e.activation(out=rsS[:, 1:2], in_=rsS[:, 5:6], func=AF.Exp, scale=-0.5), reads=[R("rsq", 0, 8)], writes=[R("rsq", 0, 8)])
            P.op("dve", lambda e: e.scalar_tensor_tensor(out=ynS, in0=ysS, scalar=rsS[:, 1:2], in1=snwb[0:NS, :], op0=ALU.mult, op1=ALU.mult),
                 reads=[R(SN, 0, 4), R("rsq", 0, 8), R("snwb")], writes=[R("hT", 0, 8)])
            for fc in range(16):
                P.op("pe", lambda e, fc=fc: e.transpose(out=PBF(0)[:, fc * NS:(fc + 1) * NS], in_=ynS[:, fc * 128:(fc + 1) * 128], identity=identb[0:NS, 0:NS]),
                     reads=[R("hT", 0, 8), R("identb")], writes=[PR(0)])
            P.op("act", lambda e: e.activation(out=ynTs, in_=PBF(0)[:, 0:256].rearrange("p (c s) -> p c s", c=16), func=AF.Copy), reads=[PR(0)], writes=[RCB])
            for t in range(4):
                wt, wr = load_w(w_ssd[:, :, t * 256:(t + 1) * 256], 16, 256, tid=('ssd', t))
                for j in range(2):
                    oc = 2 * t + j
                    for kc in range(16):
                        P.op("pe", lambda e, j=j, kc=kc, oc=oc, wt=wt: e.matmul(pb[1][:, oc * NS:(oc + 1) * NS], lhsT=wt[:, kc, j * 128:(j + 1) * 128], rhs=ynTs[:, kc, :], start=(kc == 0), stop=(kc == 15)),
                             reads=[wr, RCB], writes=[PR(1)])
            P.op("dve", lambda e: e.tensor_tensor(out=ptmp, in0=pb[1][:, 0:128].rearrange("p (c s) -> p c s", c=8), in1=gbS, op=ALU.mult), reads=[PR(1), RAM], writes=[RAM])
            P.op("dve", lambda e: e.tensor_tensor(out=mixTs, in0=ptmp, in1=amS, op=ALU.add), reads=[RAM], writes=[RCB])
            tmpR = ysS[:, 0:512]
            for t in range(2):
                wt, wr = load_w(w_out[:, :, t * 512:(t + 1) * 512], 8, 512, tid=('out', t))
                for kc in range(8):
                    P.op("pe", lambda e, kc=kc, wt=wt, t=t: e.matmul(pb[2 + t][0:NS, 0:512], lhsT=mixTs[:, kc, :], rhs=wt[:, kc, :], start=(kc == 0), stop=(kc == 7)), reads=[wr, RCB], writes=[PR(2 + t)])
                P.op("dve", lambda e, t=t: e.tensor_tensor(out=tmpR, in0=pb[2 + t][0:NS, 0:512], in1=gs1[:, t * 512:(t + 1) * 512], op=ALU.mult), reads=[PR(2 + t), R(SN, 0, 4)], writes=[R(SN, 0, 4)])
                P.op("dve", lambda e, t=t: e.tensor_tensor(out=xs_tok[:, t * 512:(t + 1) * 512], in0=xs_tok[:, t * 512:(t + 1) * 512], in1=tmpR, op=ALU.add), reads=[R(SN, 0, 4), R("yst0")], writes=[R("yst0")])
            s_norm_T(a2T, 24)
            for t in range(11):
                wt, wr = load_w(w_ffi[:, :, t * 512:(t + 1) * 512], 8, 512, tid=('ffi', t))
                for q in range(4):
                    for kc in range(8):
                        P.op("pe", lambda e, q=q, kc=kc, wt=wt: e.matmul(pb[1][:, q * NS:(q + 1) * NS], lhsT=wt[:, kc, q * 128:(q + 1) * 128], rhs=hTs[:, kc, :], start=(kc == 0), stop=(kc == 7)),
                             reads=[wr, RCB], writes=[PR(1)])
                P.op("act", lambda e: e.activation(out=sgS, in_=pb[1][:, 0:2 * NS], func=AF.Silu), reads=[PR(1)], writes=[RAM])
                P.op("dve", lambda e, t=t: e.tensor_tensor(out=fTs[:, 2 * t:2 * t + 2, :], in0=pb[1][:, 2 * NS:4 * NS].rearrange("p (c s) -> p c s", c=2), in1=sgS.rearrange("p (c s) -> p c s", c=2), op=ALU.mult),
                     reads=[PR(1), RAM], writes=[RCB])
            for half in range(2):
                for kg in range(3):
                    nk = 8 if kg < 2 else 6
                    wt, wr = load_w(w_ffo[:, kg * 8:kg * 8 + nk, half * 512:(half + 1) * 512], nk, 512, tid=('ffo', kg, half))
                    for kc in range(nk):
                        P.op("pe", lambda e, kc=kc, kg=kg, nk=nk, wt=wt, half=half: e.matmul(pb[2 + half][0:NS, 0:512], lhsT=fTs[:, kg * 8 + kc, :], rhs=wt[:, kc, :], start=(kg == 0 and kc == 0), stop=(kg == 2 and kc == nk - 1)),
                             reads=[wr, RCB], writes=[PR(2 + half)])
                P.op("dve", lambda e, half=half: e.tensor_tensor(out=tmpR, in0=pb[2 + half][0:NS, 0:512], in1=gs2[:, half * 512:(half + 1) * 512], op=ALU.mult), reads=[PR(2 + half), R(SN, 0, 4)], writes=[R(SN, 0, 4)])
                P.op("dve", lambda e, half=half: e.tensor_tensor(out=xs_tok[:, half * 512:(half + 1) * 512], in0=xs_tok[:, half * 512:(half + 1) * 512], in1=tmpR, op=ALU.add), reads=[R(SN, 0, 4), R("yst0")], writes=[R("yst0")])
            P.op("act", lambda e: e.activation(out=junkS[:, 0:D], in_=xs_tok, func=AF.Square, accum_out=ssS[:, 2:3]), reads=[R("yst0")], writes=[R("hT", 0, 8), R("ssq", 0, 8)])
            P.op("act", lambda e: e.activation(out=rsS[:, 6:7], in_=ssS[:, 2:3], func=AF.Ln, scale=1.0 / D, bias=epsc[0:NS, 0:1]), reads=[R("ssq", 0, 8), R("epsc")], writes=[R("rsq", 0, 8)])
            P.op("act", lambda e: e.activation(out=rsS[:, 2:3], in_=rsS[:, 6:7], func=AF.Exp, scale=-0.5), reads=[R("rsq", 0, 8)], writes=[R("rsq", 0, 8)])
            P.op("dve", lambda e: e.scalar_tensor_tensor(out=xs_tok, in0=xs_tok, scalar=rsS[:, 2:3], in1=rowb[0:NS, RB_FNW:RB_FNW + D], op0=ALU.mult, op1=ALU.mult),
                 reads=[R("yst0"), R("rsq", 0, 8), R("rowb")], writes=[R("yst0")])
            DMA("sp", o_ys, xs_tok, "st_ys", reads=[R("yst0")], writes=[R("o_ys")])

        P.op("sp", None, reads=[R("o_y", 0, 4 * NBLK), R("o_pp"), R("o_cp"), R("o_sp"), R("o_ys"), R("o_ps"), R("o_cs"), R("o_ss", 0, NS)])

        P.analyze()
        sems_e = {e: es.enter_context(nc.semaphore("se_" + e)) for e in ENGS}
        sems_d = {k: es.enter_context(nc.semaphore("sd_" + k)) for k in sorted(dma_keys)}
        P.emit(sems_e, sems_d)
    return nc


def _tile_k(w):
    K, N = w.shape
    return np.ascontiguousarray(w.reshape(K // 128, 128, N).transpose(1, 0, 2))


def _fm(v):
    return np.ascontiguousarray(v.reshape(-1, 128).T)


_NC_CACHE = {}


def kernel(x_prompt, x_sample, c_prompt, c_sample, state_pool, state_conv, state_ssm, w_ada, b_ada, norm1_w,
           w_in, w_pool, pool_scale, conv_w, conv_b, dt_bias, A_log, D_skip, ssd_norm_w, w_ssd_proj, w_out,
           norm2_w, w_ffn_in, w_ffn_out, final_norm_w):
    f = np.float32
    n = 8
    x_prompt = np.asarray(x_prompt, f)
    pvec = np.zeros((128, PV_N), f)
    pvec[:, PV_N1W:PV_N1W + 8] = _fm(np.asarray(norm1_w[0], f))
    pvec[:, PV_PSC:PV_PSC + 8] = _fm(np.asarray(pool_scale[0], f))
    cw = np.asarray(conv_w[0], f)
    for k in range(4):
        pvec[:, PV_CW + 32 * k:PV_CW + 32 * k + 32] = _fm(cw[k])
    pvec[:, PV_CB:PV_CB + 32] = _fm(np.asarray(conv_b[0], f))
    pvec[:, PV_N2W:PV_N2W + 8] = _fm(np.asarray(norm2_w[0], f))
    pvec[:, PV_BADA:PV_BADA + 48] = _fm(np.asarray(b_ada[0], f))
    pvec[:, PV_DF:PV_DF + 16] = _fm(np.repeat(np.asarray(D_skip[0], f), HP))
    rowb = np.zeros((128, RB_N), f)
    rowb[:, RB_FNW:RB_FNW + D] = np.asarray(final_norm_w, f)[None, :]
    bg = np.zeros((128, 2 * D), f)
    bg[:, 0:D] = np.asarray(b_ada[0], f)[None, 2 * D:3 * D]
    bg[:, D:2 * D] = np.asarray(b_ada[0], f)[None, 5 * D:6 * D]
    rowb[:, RB_DSK:RB_DSK + 32] = np.asarray(D_skip[0], f)[None, :]
    rowb[:, RB_ALOG:RB_ALOG + 32] = np.asarray(A_log[0], f)[None, :]
    rowb[:, RB_DTB:RB_DTB + 32] = np.asarray(dt_bias[0], f)[None, :]
    snw = np.ascontiguousarray(np.broadcast_to(np.asarray(ssd_norm_w[0], f)[None, :], (128, DI)))
    w_ada_t = _tile_k(np.asarray(w_ada[0], f))
    w_in_t = _tile_k(np.asarray(w_in[0], f))
    wp = np.asarray(w_pool[0], f)
    w_pool_t = np.ascontiguousarray(np.stack([_tile_k(wp[g]) for g in range(4)], axis=1))
    w_ssd_t = _tile_k(np.asarray(w_ssd_proj[0], f))
    w_out_t = _tile_k(np.asarray(w_out[0], f))
    wfi = np.asarray(w_ffn_in[0], f)
    perm = np.concatenate([np.concatenate([np.arange(256 * t, 256 * t + 256), DFF + np.arange(256 * t, 256 * t + 256)]) for t in range(11)])
    w_ffi_t = _tile_k(np.ascontiguousarray(wfi[:, perm]))
    w_ffo_t = _tile_k(np.asarray(w_ffn_out[0], f))

    in_maps = []
    for b in range(n):
        s0, s1 = NS * b, NS * (b + 1)
        c17 = np.concatenate([np.asarray(c_prompt[b:b + 1], f), np.asarray(c_sample[s0:s1], f)], axis=0)
        cT = np.ascontiguousarray(c17.T.reshape(8, 128, 17).transpose(1, 0, 2))
        sp = np.asarray(state_pool[0, s0:s1], f)
        sp_t = np.ascontiguousarray(sp.reshape(NS, 15, 8, 128).transpose(3, 2, 0, 1))
        sc = np.asarray(state_conv[0, s0:s1], f)
        sc_t = np.ascontiguousarray(sc.reshape(NS, 3, 32, 128).transpose(3, 2, 0, 1))
        ss = np.asarray(state_ssm[0, s0:s1], f)
        ss_t = np.ascontiguousarray(ss.reshape(NS, DI, DST).transpose(0, 2, 1))
        in_maps.append({
            "xp": np.ascontiguousarray(x_prompt[b]),
            "xsm": np.ascontiguousarray(np.asarray(x_sample[s0:s1, 0], f)),
            "cT": cT, "pvec": pvec, "rowb": rowb, "snw": snw, "bg": bg,
            "w_ada": w_ada_t, "w_in": w_in_t, "w_pool": w_pool_t, "w_ssd": w_ssd_t, "w_out": w_out_t,
            "w_ffi": w_ffi_t, "w_ffo": w_ffo_t,
            "st_pool": sp_t, "st_conv": sc_t, "st_ssm": ss_t,
        })
    nblk = int(os.environ.get("K_NBLK", NBLK))
    ncores = int(os.environ.get("K_CORES", n))
    if "nc" not in _NC_CACHE:
        _NC_CACHE["nc"] = build_nc(nblk=nblk)
    nc = _NC_CACHE["nc"]
    res = run_bass_kernel_spmd(nc, in_maps[:ncores], core_ids=list(range(ncores)))
    rs = list(res.results)
    while len(rs) < n:
        rs.append({k: np.zeros_like(v) for k, v in rs[0].items()})
    y_prompt = np.stack([rs[b]["o_y"] for b in range(n)], axis=0)
    y_sample = np.concatenate([rs[b]["o_ys"] for b in range(n)], axis=0)[:, None, :]
    pool_p = np.stack([rs[b]["o_pp"].transpose(2, 1, 0).reshape(15, D) for b in range(n)], axis=0)[None]
    conv_p = np.stack([rs[b]["o_cp"].transpose(2, 1, 0).reshape(3, CONV) for b in range(n)], axis=0)[None]
    ssm_p = np.stack([rs[b]["o_sp"].T.reshape(NH, HP, DST) for b in range(n)], axis=0)[None]
    pool_s = np.concatenate([rs[b]["o_ps"].transpose(2, 3, 1, 0).reshape(NS, 15, D) for b in range(n)], axis=0)[None]
    conv_s = np.concatenate([rs[b]["o_cs"].transpose(2, 3, 1, 0).reshape(NS, 3, CONV) for b in range(n)], axis=0)[None]
    ssm_s = np.concatenate([rs[b]["o_ss"].transpose(0, 2, 1).reshape(NS, NH, HP, DST) for b in range(n)], axis=0)[None]
    return (np.ascontiguousarray(y_prompt, dtype=f), np.ascontiguousarray(y_sample, dtype=f),
            np.ascontiguousarray(pool_p, dtype=f), np.ascontiguousarray(conv_p, dtype=f),
            np.ascontiguousarray(ssm_p, dtype=f), np.ascontiguousarray(pool_s, dtype=f),
            np.ascontiguousarray(conv_s, dtype=f), np.ascontiguousarray(ssm_s, dtype=f))
```

```python
import os
from contextlib import ExitStack

import numpy as np
import concourse.bass as bass
import concourse.mybir as mybir
from concourse.bass_utils import run_bass_kernel_spmd

F32 = mybir.dt.float32
BF16 = mybir.dt.bfloat16
AF = mybir.ActivationFunctionType
ALU = mybir.AluOpType

ENGS = ("pe", "act", "dve", "pool", "sp")

D = 1024
SEQ = 2048
TB = 512
NBLK = SEQ // TB
NS = 16
DI = 2048
NH = 32
HP = 64
NG = 8
DST = 128
CONV = 4096
DFF = 2816
IN_COLS = 9248
EPS = 1e-6

PV_N1W, PV_PSC, PV_CW, PV_CB, PV_N2W, PV_BADA, PV_DF, PV_N = 0, 8, 16, 144, 176, 184, 232, 248
RB_FNW, RB_DSK, RB_ALOG, RB_DTB, RB_N = 0, 1024, 1056, 1088, 1120


class Op:
    __slots__ = ("eng", "fn", "reads", "writes", "dkey", "dgroup", "idx", "waits",
                 "sig", "cnt", "eidx")

    def __init__(self, eng, fn, reads, writes, dkey=None, dgroup=None):
        self.eng = eng
        self.fn = fn
        self.reads = reads
        self.writes = writes
        self.dkey = dkey
        self.dgroup = dgroup
        self.waits = {}
        self.sig = False
        self.cnt = 0


class Prog:
    def __init__(self, nc):
        self.nc = nc
        self.ops = []

    def op(self, eng, fn, reads=(), writes=()):
        o = Op(eng, fn, list(reads), list(writes))
        o.idx = len(self.ops)
        self.ops.append(o)
        return o

    def dma(self, eng, fn, reads=(), writes=(), key=None, group=None):
        o = Op(eng, fn, list(reads), list(writes), dkey=key, dgroup=group)
        o.idx = len(self.ops)
        self.ops.append(o)
        return o

    def analyze(self):
        ops = self.ops
        ecount = {e: 0 for e in ENGS}
        for o in ops:
            o.eidx = ecount[o.eng]
            ecount[o.eng] += 1
        key_ops = {}
        for o in ops:
            if o.dkey is not None:
                key_ops.setdefault(o.dkey, []).append(o)
        dma_cum, dma_prev = {}, {}
        for k, lst in key_ops.items():
            groups = []
            for o in lst:
                if groups and o.dgroup is not None and groups[-1][0] == o.dgroup:
                    groups[-1][1].append(o)
                else:
                    groups.append((o.dgroup, [o]))
            cum = 0
            for g, gl in groups:
                prev = cum
                cum += len(gl)
                for o in gl:
                    dma_cum[o.idx] = cum
                    dma_prev[o.idx] = prev
        self.keys = sorted(key_ops.keys())
        recs = {}
        waited = {}
        pend = []
        for o in ops:
            d = set()
            for (buf, lo, hi) in o.reads:
                for r in recs.get(buf, ()):
                    if r[3] and r[0] < hi and lo < r[1]:
                        d.add(r[2])
            for (buf, lo, hi) in o.writes:
                for r in recs.get(buf, ()):
                    if r[0] < hi and lo < r[1]:
                        d.add(r[2])
            d.discard(o.idx)
            for (buf, lo, hi) in o.writes:
                lst = recs.setdefault(buf, [])
                lst[:] = [r for r in lst if not (lo <= r[0] and r[1] <= hi)]
                lst.append([lo, hi, o.idx, True])
            for (buf, lo, hi) in o.reads:
                lst = recs.setdefault(buf, [])
                lst[:] = [r for r in lst if not ((not r[3]) and ops[r[2]].eng == o.eng
                                                 and ops[r[2]].dkey is None and o.dkey is None
                                                 and lo <= r[0] and r[1] <= hi)]
                lst.append([lo, hi, o.idx, False])
            need = {}
            for di in d:
                p = ops[di]
                if p.dkey is not None:
                    sk = ("d", p.dkey)
                    val = 16 * dma_cum[p.idx]
                    if need.get(sk, 0) < val:
                        need[sk] = val
                    continue
                if p.eng == o.eng and o.dkey is None:
                    if o.eng in ("pe", "sp"):
                        continue
                sk = ("e", p.eng)
                cur = need.get(sk)
                if cur is None or cur.eidx < p.eidx:
                    need[sk] = p
            if o.dkey is not None and dma_prev[o.idx] > 0:
                sk = ("d", o.dkey)
                val = 16 * dma_prev[o.idx]
                if need.get(sk, 0) < val:
                    need[sk] = val
            o.waits = need
            for sk, v in need.items():
                if sk[0] == "e":
                    v.sig = True
        cnt = {e: 0 for e in ENGS}
        for o in ops:
            if o.dkey is None and o.sig:
                cnt[o.eng] += 1
            o.cnt = cnt[o.eng]
        for o in ops:
            final = {}
            for sk, v in o.waits.items():
                val = v.cnt if sk[0] == "e" else v
                wk = (o.eng, sk)
                if waited.get(wk, 0) >= val:
                    continue
                waited[wk] = val
                final[sk] = val
            o.waits = final

    def emit(self, sems_e, sems_d):
        nc = self.nc
        per = {e: [o for o in self.ops if o.eng == e] for e in ENGS}

        def run(engname, eng):
            for o in per[engname]:
                for sk, val in o.waits.items():
                    sem = sems_e[sk[1]] if sk[0] == "e" else sems_d[sk[1]]
                    eng.wait_ge(sem, val)
                if o.fn is None:
                    continue
                ins = o.fn(eng)
                if o.dkey is not None:
                    ins.then_inc(sems_d[o.dkey], 16)
                elif o.sig:
                    ins.then_inc(sems_e[o.eng], 1)

        with nc.Block() as block:
            @block.tensor
            def _(e):
                run("pe", e)

            @block.scalar
            def _(e):
                run("act", e)

            @block.vector
            def _(e):
                run("dve", e)

            @block.gpsimd
            def _(e):
                run("pool", e)

            @block.sync
            def _(e):
                run("sp", e)


def R(name, lo=0, hi=1):
    return (name, lo, hi)


def build_nc(nblk=NBLK, do_sample=True):
    nc = bass.Bass("TRN2", target_bir_lowering=False)

    def din(name, shape):
        return nc.dram_tensor(name, list(shape), F32, kind="ExternalInput").ap()

    def dout(name, shape):
        return nc.dram_tensor(name, list(shape), F32, kind="ExternalOutput").ap()

    xp = din("xp", [SEQ, D])
    xsm = din("xsm", [NS, D])
    cT = din("cT", [128, 8, 17])
    pvec = din("pvec", [128, PV_N])
    rowb_in = din("rowb", [128, RB_N])
    snw_in = din("snw", [128, DI])
    bg_in = din("bg", [128, 2 * D])
    w_ada = din("w_ada", [128, 8, 6 * D])
    w_in = din("w_in", [128, 8, IN_COLS])
    w_pool = din("w_pool", [128, 4, 2, 256])
    w_ssd = din("w_ssd", [128, 16, D])
    w_out = din("w_out", [128, 8, D])
    w_ffi = din("w_ffi", [128, 8, 2 * DFF])
    w_ffo = din("w_ffo", [128, 22, D])
    st_pool = din("st_pool", [128, 8, NS, 15])
    st_conv = din("st_conv", [128, 32, NS, 3])
    st_ssm = din("st_ssm", [NS, 128, DI])
    o_y = dout("o_y", [SEQ, D])
    o_ys = dout("o_ys", [NS, D])
    o_pp = dout("o_pp", [128, 8, 15])
    o_cp = dout("o_cp", [128, 32, 3])
    o_sp = dout("o_sp", [128, DI])
    o_ps = dout("o_ps", [128, 8, NS, 15])
    o_cs = dout("o_cs", [128, 32, NS, 3])
    o_ss = dout("o_ss", [NS, 128, DI])
    dbg_out = {}

    es = ExitStack()
    with es:
        def sb(name, shape, dt=F32):
            return es.enter_context(nc.sbuf_tensor("s_" + name, list(shape), dt))

        P = Prog(nc)
        dma_keys = set()

        def DMA(eng, out, in_, key, reads=(), writes=(), group=None):
            dma_keys.add(key)
            P.dma(eng, lambda e: e.dma_start(out=out, in_=in_), reads=reads, writes=writes, key=key, group=group)

        pb = [es.enter_context(nc.psum_tensor("pb%d" % i, [128, 512], F32)) for i in range(8)]

        def PBF(i):
            return pb[i][:].bitcast(BF16)

        def PR(i, lo=0, hi=512):
            return ("pb%d" % i, 0, 512)

        sA = sb("sA", [128, 16 + TB])
        sB = sb("sB", [128, 16 + TB])
        identf = sB[:, 0:128]
        triuf = sB[:, 128:256]
        nmf = sA[:, 0:512].rearrange("p (a b) -> p a b", a=4)
        identb = sb("identb", [128, 128], BF16)
        onesb = sb("onesb", [128, 128], BF16)
        triub = sb("triub", [128, 128], BF16)
        nmb = sb("nmb", [128, 4, 128], BF16)
        epsc = sb("epsc", [128, 1])
        invc = sb("invc", [128, 4, 16])
        pv = sb("pv", [128, PV_N])
        rowb = sb("rowb", [128, RB_N])
        snwb = sb("snwb", [128, DI], BF16)
        anegb = sb("anegb", [128, 32])
        cTs = sb("cTs", [128, 8, 17])
        silucT = sb("silucT", [128, 8, 17], BF16)
        modT = sb("modT", [128, 48, 17])
        a1T = sb("a1T", [128, 8, 17])
        a2T = sb("a2T", [128, 8, 17])
        g1B = sb("g1B", [128, D])
        g2B = sb("g2B", [128, D])
        wdt = sb("wdt", [128, 8, 32], BF16)
        diagD = sb("diagD", [128, 16, 128], BF16)
        NWB = 2
        wbuf = [sb("wbuf%d" % i, [128, 4096], BF16) for i in range(NWB)]

        P.op("pool", lambda e: e.memset(identf, 0.0), writes=[R("sB")])
        P.op("pool", lambda e: e.affine_select(out=identf, in_=identf, pattern=[[-1, 128]], compare_op=ALU.not_equal,
                                               fill=1.0, base=0, channel_multiplier=1), reads=[R("sB")], writes=[R("sB")])
        P.op("dve", lambda e: e.tensor_copy(out=identb[:], in_=identf), reads=[R("sB")], writes=[R("identb")])
        P.op("pool", lambda e: e.memset(onesb[:], 1.0), writes=[R("onesb")])
        P.op("pool", lambda e: e.memset(triuf, 1.0), writes=[R("sB")])
        P.op("pool", lambda e: e.affine_select(out=triuf, in_=triuf, pattern=[[1, 128]], compare_op=ALU.is_ge,
                                               fill=0.0, base=0, channel_multiplier=-1), reads=[R("sB")], writes=[R("sB")])
        P.op("dve", lambda e: e.tensor_copy(out=triub[:], in_=triuf), reads=[R("sB")], writes=[R("triub")])
        P.op("pool", lambda e: e.memset(nmf, 0.0), writes=[R("sA")])
        P.op("pool", lambda e: e.affine_select(out=nmf, in_=nmf, pattern=[[0, 4], [1, 128]], compare_op=ALU.is_ge,
                                               fill=-30000.0, base=0, channel_multiplier=-1), reads=[R("sA")], writes=[R("sA")])
        P.op("dve", lambda e: e.tensor_copy(out=nmb[:], in_=nmf), reads=[R("sA")], writes=[R("nmb")])
        P.op("pool", lambda e: e.memset(epsc[:], EPS), writes=[R("epsc")])
        for g in range(4):
            w = 2 ** (g + 1)
            P.op("pool", lambda e, g=g, w=w: e.memset(invc[:, g, :], 1.0 / w), writes=[R("invc")])
            for t in range(w - 1):
                P.op("pool", lambda e, g=g, t=t: e.memset(invc[:, g, t:t + 1], 1.0 / (t + 1)), writes=[R("invc")])

        DMA("sp", pv[:], pvec, "ld_pv", writes=[R("pv")])
        DMA("sp", rowb[:], rowb_in, "ld_rowb", writes=[R("rowb")])
        DMA("sp", cTs[:], cT, "ld_c", writes=[R("cTs")])
        DMA("sp", g1B[:], bg_in[:, 0:D], "ld_g1", writes=[R("g1B", 0, 2)])
        DMA("sp", g2B[:], bg_in[:, D:2 * D], "ld_g2", writes=[R("g2B", 0, 2)])
        DMA("pool", snwb[:], snw_in, "ld_snw", writes=[R("snwb")])
        DMA("pool", wdt[:], w_in[:, :, 7168:7200], "ld_wdt", writes=[R("wdt")])
        P.op("dve", lambda e: e.tensor_tensor(out=diagD[:], in0=identb[:].unsqueeze(1).to_broadcast([128, 16, 128]),
                                              in1=pv[:, PV_DF:PV_DF + 16].unsqueeze(2).to_broadcast([128, 16, 128]), op=ALU.mult),
             reads=[R("identb"), R("pv")], writes=[R("diagD")])

        P.op("act", lambda e: e.activation(out=anegb[:], in_=rowb[:, RB_ALOG:RB_ALOG + 32], func=AF.Exp), reads=[R("rowb")], writes=[R("anegb")])
        P.op("dve", lambda e: e.tensor_scalar(out=anegb[:], in0=anegb[:], scalar1=-1.0, scalar2=None, op0=ALU.mult), reads=[R("anegb")], writes=[R("anegb")])

        wstate = {"i": 0}

        NSCR = 42
        wsc = nc.dram_tensor("wsc", [NSCR, 128, 4096], BF16).ap()
        scr_idx = {}

        prefetched = {}
        wslots = [(wbuf[i][:], R("wbuf%d" % i)) for i in range(NWB)]

        def load_w(src_ap, nk, ncol, tid=None, prefetch=False):
            if tid is not None and not prefetch and tid in prefetched:
                return prefetched.pop(tid)
            res = _load_w(src_ap, nk, ncol, tid)
            if prefetch:
                prefetched[tid] = res
            return res

        def _load_w(src_ap, nk, ncol, tid=None):
            i = wstate["i"] % len(wslots)
            wstate["i"] += 1
            wt, wreg = wslots[i]
            view = wt[:, 0:nk * ncol].rearrange("p (k c) -> p k c", k=nk)
            if tid is None:
                DMA("pool", view, src_ap, "w%d" % i, writes=[wreg])
            elif tid not in scr_idx:
                k = len(scr_idx)
                scr_idx[tid] = k
                DMA("pool", view, src_ap, "w%d" % i, writes=[wreg])
                DMA("sp", wsc[k, :, 0:nk * ncol], wt[:, 0:nk * ncol], "ws%d" % i, reads=[wreg], writes=[R("wsc", k, k + 1)])
            else:
                k = scr_idx[tid]
                DMA("sp", wt[:, 0:nk * ncol], wsc[k, :, 0:nk * ncol], "wh%d" % i, reads=[R("wsc", k, k + 1)], writes=[wreg])
            return view, wreg

        rot = {"i": 0}

        def next_bank(banks=(0, 1, 2, 3)):
            b = banks[rot["i"] % len(banks)]
            rot["i"] += 1
            return b

        P.op("act", lambda e: e.activation(out=silucT[:], in_=cTs[:], func=AF.Silu), reads=[R("cTs")], writes=[R("silucT")])
        for t in range(12):
            wt, wr = load_w(w_ada[:, :, t * 512:(t + 1) * 512], 8, 512)
            bk = 4 + (t % 2)
            for j in range(4):
                for kc in range(8):
                    P.op("pe", lambda e, bk=bk, j=j, kc=kc, wt=wt: e.matmul(pb[bk][:, j * 17:(j + 1) * 17], lhsT=wt[:, kc, j * 128:(j + 1) * 128],
                                                                             rhs=silucT[:, kc, :], start=(kc == 0), stop=(kc == 7)),
                         reads=[wr, R("silucT")], writes=[PR(bk, j * 17, j * 17 + 17)])
            P.op("dve", lambda e, bk=bk, t=t: e.tensor_tensor(out=modT[:, 4 * t:4 * t + 4, :], in0=pb[bk][:, 0:68].rearrange("p (a b) -> p a b", a=4),
                                                               in1=pv[:, PV_BADA + 4 * t:PV_BADA + 4 * t + 4].unsqueeze(2).to_broadcast([128, 4, 17]), op=ALU.add),
                 reads=[PR(bk, 0, 68), R("pv")], writes=[R("modT", 4 * t, 4 * t + 4)])
            if t in (4, 5, 10, 11):
                gB = g1B if t < 6 else g2B
                gname = "g1B" if t < 6 else "g2B"
                half = t % 2
                for kc in range(8):
                    P.op("pe", lambda e, kc=kc, wt=wt: e.matmul(pb[6][:, 0:512], lhsT=silucT[:, kc, 0:1].to_broadcast([128, 128]), rhs=wt[:, kc, :],
                                                                  start=(kc == 0), stop=(kc == 7)),
                         reads=[wr, R("silucT")], writes=[PR(6)])
                P.op("dve", lambda e, gB=gB, half=half: e.tensor_tensor(out=gB[:, half * 512:(half + 1) * 512], in0=pb[6][:, 0:512],
                                                                         in1=gB[:, half * 512:(half + 1) * 512], op=ALU.add),
                     reads=[PR(6), R(gname, half, half + 1)], writes=[R(gname, half, half + 1)])
        for (aT, an, sc0, nw0) in ((a1T, "a1T", 8, PV_N1W), (a2T, "a2T", 32, PV_N2W)):
            P.op("dve", lambda e, aT=aT, sc0=sc0: e.tensor_scalar(out=aT[:], in0=modT[:, sc0:sc0 + 8, :], scalar1=1.0, scalar2=None, op0=ALU.add),
                 reads=[R("modT", sc0, sc0 + 8)], writes=[R(an)])
            P.op("dve", lambda e, aT=aT, nw0=nw0: e.tensor_tensor(out=aT[:], in0=aT[:], in1=pv[:, nw0:nw0 + 8].unsqueeze(2).to_broadcast([128, 8, 17]), op=ALU.mult),
                 reads=[R(an), R("pv")], writes=[R(an)])

        bufX = [sb("bx%d" % i, [128, 4, D]) for i in range(2)]
        ssq = sb("ssq", [128, 8])
        rsq = sb("rsq", [128, 8])
        hT = sb("hT", [128, 8, TB], BF16)
        ubuf = sb("ubuf", [128, 8, 16 + TB])
        gT = sb("gT", [128, 8, TB], BF16)
        gaT = gT
        gbT = gT
        uhist = sb("uhist", [128, 8, 15])
        amT = sb("amT", [128, 8, TB], BF16)
        xpre = [sb("xpre%d" % i, [128, 3 + TB], BF16) for i in range(2)]
        dg = [sb("dg%d" % i, [128, 4, 128], BF16) for i in range(2)]
        chist = sb("chist", [128, 32, 3], BF16)
        cst = sb("cst", [128, 32, 3])
        xbc = sb("xbc", [128, 32, TB], BF16)
        dtx = sb("dtx", [128, 4, 32])
        dtt = sb("dtt", [128, 4, 32])
        dtu = sb("dtu", [128, 4, 32])
        dt_tok = sb("dt_tok", [128, 4, 32])
        dA_bf = sb("dA_bf", [128, 4, 32], BF16)
        negAcs = sb("negAcs", [128, 32])
        Eacs = sb("Eacs", [128, 32])
        dte = sb("dte", [128, 32])
        cdB = sb("cdB", [128, 32])
        xdtb = [sb("xdt%d" % i, [128, DI], BF16) for i in range(2)]
        x_tok = xdtb[1]
        xdt = xdtb[0]
        xdte = sb("xdte", [128, DI], BF16)
        B_tok = sb("B_tok", [128, 1024], BF16)
        CBs = sb("CBs", [128, 8, 128], BF16)
        dec = [sb("dec%d" % i, [128, 8, 128], BF16) for i in range(2)]
        scr = dec
        ybuf = sb("ybuf", [128, DI])
        ytmp = hT[:].rearrange("p a b -> p (a b)").bitcast(F32)
        xn = ybuf[:].bitcast(BF16).rearrange("p (a b) -> p a b", a=4)
        junk = xdte

        def role_views(k):
            xt = bufX[k]
            szv = bufX[1 - k][:].rearrange("p a b -> p (a b)").bitcast(BF16).rearrange("p (a b) -> p a b", a=4)
            pl = bufX[1 - k][:].rearrange("p a b -> p (a b)").bitcast(BF16)[:, 0:8 * TB].rearrange("p (a b) -> p a b", a=8)
            return xt, "bx%d" % k, szv, "bx%d" % (1 - k), pl
        hst = sb("hst", [128, DI])
        hstb = sb("hstb", [128, DI], BF16)
        yst = [sb("yst%d" % i, [128, D]) for i in range(1)]
        ynT = ubuf[:].rearrange("p a b -> p (a b)").bitcast(BF16)[:, 0:16 * TB].rearrange("p (a b) -> p a b", a=16)
        fT = xbc[:].rearrange("p a b -> p (a b)")[:, 0:22 * TB].rearrange("p (a b) -> p a b", a=22)

        P.op("pool", lambda e: e.memset(chist[:], 0.0), writes=[R("chist", 0, 32)])
        P.op("pool", lambda e: e.memset(hst[:], 0.0), writes=[R("hst", 0, 8)])
        P.op("pool", lambda e: e.memset(hstb[:], 0.0), writes=[R("hstb", 0, 8)])

        def rmsnorm_to_T(blk_name, aT, bcol0, xtok, XN, ntok=128, ntile=4):
            for ti in range(ntile):
                P.op("act", lambda e, ti=ti: e.activation(out=junk[:, 0:D], in_=xtok[:, ti, :], func=AF.Square, accum_out=ssq[:, ti:ti + 1]),
                     reads=[R(XN, ti, ti + 1)], writes=[R("xdte", 0, 8), R("ssq", ti, ti + 1)])
                P.op("act", lambda e, ti=ti: e.activation(out=rsq[:, 4 + ti:5 + ti], in_=ssq[:, ti:ti + 1], func=AF.Ln, scale=1.0 / D, bias=epsc[:, 0:1]),
                     reads=[R("ssq", ti, ti + 1), R("epsc")], writes=[R("rsq", 4 + ti, 5 + ti)])
                P.op("act", lambda e, ti=ti: e.activation(out=rsq[:, ti:ti + 1], in_=rsq[:, 4 + ti:5 + ti], func=AF.Exp, scale=-0.5),
                     reads=[R("rsq", 4 + ti, 5 + ti)], writes=[R("rsq", ti, ti + 1)])
                P.op("dve", lambda e, ti=ti: e.tensor_scalar(out=xn[:, ti, :], in0=xtok[:, ti, :], scalar1=rsq[:, ti:ti + 1], scalar2=None, op0=ALU.mult),
                     reads=[R(XN, ti, ti + 1), R("rsq", ti, ti + 1)], writes=[R("ybuf", 2 * ti, 2 * ti + 2)])
            for kc in range(8):
                bk = kc // 2
                c0 = (kc % 2) * 512
                for ti in range(ntile):
                    P.op("pe", lambda e, bk=bk, c0=c0, ti=ti, kc=kc: e.transpose(out=PBF(bk)[:, c0 + ti * 128:c0 + (ti + 1) * 128],
                                                                                  in_=xn[:, ti, kc * 128:(kc + 1) * 128], identity=identb[:]),
                         reads=[R("ybuf", 2 * ti, 2 * ti + 2), R("identb")], writes=[PR(bk, c0 // 2 + ti * 64, c0 // 2 + (ti + 1) * 64)])
                P.op("dve", lambda e, bk=bk, c0=c0, kc=kc, aT=aT, bcol0=bcol0: e.tensor_scalar(
                    out=hT[:, kc, :], in0=PBF(bk)[:, c0:c0 + 512], scalar1=aT[:, kc, 0:1], scalar2=modT[:, bcol0 + kc, 0:1], op0=ALU.mult, op1=ALU.add),
                    reads=[PR(bk, c0 // 2, c0 // 2 + 256), R(blk_name), R("modT", bcol0 + kc, bcol0 + kc + 1)], writes=[R("hT", kc, kc + 1)])

        def proj_ws(wt, wr, j, nk, rhs_fn, rhs_regs, bank, ncols=TB, col0=0):
            for kc in range(nk):
                P.op("pe", lambda e, kc=kc: e.matmul(pb[bank][:, 0:ncols], lhsT=wt[:, kc, col0 + j * 128:col0 + (j + 1) * 128], rhs=rhs_fn(kc),
                                                      start=(kc == 0), stop=(kc == nk - 1)),
                     reads=[wr] + rhs_regs(kc), writes=[PR(bank, 0, ncols)])

        def proj_as(wt, wr, ti, nk, lhs_fn, lhs_regs, bank, kc0=0, first=True, last=True, nktot=None):
            for kc in range(nk):
                P.op("pe", lambda e, kc=kc: e.matmul(pb[bank][:, 0:512], lhsT=lhs_fn(kc0 + kc, ti), rhs=wt[:, kc, :],
                                                      start=(first and kc == 0), stop=(last and kc == nk - 1)),
                     reads=[wr] + lhs_regs(kc0 + kc), writes=[PR(bank)])

        hT_rhs = lambda kc: hT[:, kc, :]
        hT_regs = lambda kc: [R("hT", kc, kc + 1)]
        hT_lhs = lambda kc, ti: hT[:, kc, ti * 128:(ti + 1) * 128]

        def emit_block(tb, xtok, XN, sz, SN, pooled):
            if tb == 0:
                DMA("sp", xtok[:], xp[0:TB, :].rearrange("(t p) d -> p t d", p=128), "ld_x", writes=[R(XN, 0, 4)])
            rmsnorm_to_T("a1T", a1T, 0, xtok, XN)

            for t in range(2):
                wt, wr = load_w(w_in[:, :, 7200 + t * 512:7200 + (t + 1) * 512], 8, 512, tid=('in', 7200 + t * 512))
                for j in range(4):
                    oc = 4 * t + j
                    bk = next_bank()
                    proj_ws(wt, wr, j, 8, hT_rhs, hT_regs, bk)
                    P.op("act", lambda e, bk=bk, oc=oc: e.activation(out=gaT[:, oc, :], in_=pb[bk][:, 0:TB], func=AF.Sigmoid),
                         reads=[PR(bk)], writes=[R("gT", oc, oc + 1)])
            if tb == 0:
                P.op("pool", lambda e: e.memset(ubuf[:, :, 0:16], 0.0), writes=[R("ubuf", 0, 8)])
            else:
                P.op("pool", lambda e: e.tensor_copy(out=ubuf[:, :, 1:16], in_=uhist[:]), reads=[R("uhist")], writes=[R("ubuf", 0, 8)])
            for t in range(2):
                wt, wr = load_w(w_in[:, :, t * 512:(t + 1) * 512], 8, 512, tid=('in', t * 512))
                for j in range(4):
                    oc = 4 * t + j
                    bk = next_bank()
                    proj_ws(wt, wr, j, 8, hT_rhs, hT_regs, bk)
                    P.op("act", lambda e, bk=bk, oc=oc: e.activation(out=ubuf[:, oc, 16:16 + TB], in_=pb[bk][:, 0:TB], func=AF.Copy),
                         reads=[PR(bk)], writes=[R("ubuf", oc, oc + 1)])
            for oc in range(8):
                g = oc // 2
                w = 2 ** (g + 1)
                U = ubuf[:, oc, :]
                ur = R("ubuf", oc, oc + 1)
                P.op("dve", lambda e, U=U: e.tensor_tensor(out=sA[:, 2:528], in0=U[:, 2:528], in1=U[:, 1:527], op=ALU.add), reads=[ur], writes=[R("sA")])
                cur, curname = sA, "sA"
                if g >= 1:
                    P.op("dve", lambda e: e.tensor_tensor(out=sB[:, 4:528], in0=sA[:, 4:528], in1=sA[:, 2:526], op=ALU.add), reads=[R("sA")], writes=[R("sB")])
                    cur, curname = sB, "sB"
                if g >= 2:
                    P.op("dve", lambda e: e.tensor_tensor(out=sA[:, 8:528], in0=sB[:, 8:528], in1=sB[:, 4:524], op=ALU.add), reads=[R("sB")], writes=[R("sA")])
                    cur, curname = sA, "sA"
                if g >= 3:
                    P.op("dve", lambda e: e.tensor_tensor(out=sB[:, 16:528], in0=sA[:, 16:528], in1=sA[:, 8:520], op=ALU.add), reads=[R("sA")], writes=[R("sB")])
                    cur, curname = sB, "sB"
                P.op("dve", lambda e, cur=cur, U=U, w=w, oc=oc: e.scalar_tensor_tensor(out=pooled[:, oc, :], in0=cur[:, 16:528], scalar=1.0 / w, in1=U[:, 16:528],
                                                                                        op0=ALU.mult, op1=ALU.subtract),
                     reads=[R(curname), ur], writes=[R(SN, 0, 2)])
                if tb == 0:
                    P.op("dve", lambda e, cur=cur, g=g: e.tensor_tensor(out=cur[:, 0:16], in0=cur[:, 16:32], in1=invc[:, g, :], op=ALU.mult),
                         reads=[R(curname), R("invc")], writes=[R(curname)])
                    P.op("dve", lambda e, cur=cur, U=U, oc=oc: e.tensor_tensor(out=pooled[:, oc, 0:16], in0=cur[:, 0:16], in1=U[:, 16:32], op=ALU.subtract),
                         reads=[R(curname), ur], writes=[R(SN, 0, 2)])
            wplt, wplr = load_w(w_pool.rearrange("p g k c -> p (g k) c"), 8, 256, tid=('pool',))
            for g in range(4):
                for j in range(2):
                    oc = 2 * g + j
                    bk = next_bank()
                    for k2 in range(2):
                        P.op("pe", lambda e, bk=bk, g=g, j=j, k2=k2: e.matmul(pb[bk][:, 0:TB], lhsT=wplt[:, 2 * g + k2, j * 128:(j + 1) * 128], rhs=pooled[:, 2 * g + k2, :],
                                                                                start=(k2 == 0), stop=(k2 == 1)),
                             reads=[wplr, R(SN, 0, 2)], writes=[PR(bk)])
                    P.op("dve", lambda e, bk=bk, oc=oc: e.scalar_tensor_tensor(out=amT[:, oc, :], in0=pb[bk][:, 0:TB], scalar=pv[:, PV_PSC + oc:PV_PSC + oc + 1],
                                                                                in1=gaT[:, oc, :], op0=ALU.mult, op1=ALU.mult),
                         reads=[PR(bk), R("pv"), R("gT", oc, oc + 1)], writes=[R("amT", oc, oc + 1)])
            if tb == nblk - 1:
                DMA("sp", o_pp, ubuf[:, :, 513:528], "st_pp", reads=[R("ubuf", 0, 8)], writes=[R("o_pp")])
            else:
                P.op("pool", lambda e: e.tensor_copy(out=uhist[:], in_=ubuf[:, :, 513:528]), reads=[R("ubuf", 0, 8)], writes=[R("uhist")])

            for t in range(4):
                wt, wr = load_w(w_in[:, :, 1024 + t * 512:1024 + (t + 1) * 512], 8, 512, tid=('in', 1024 + t * 512))
                for ti in range(4):
                    bk = next_bank()
                    proj_as(wt, wr, ti, 8, hT_lhs, hT_regs, bk)
                    P.op("act", lambda e, bk=bk, ti=ti, t=t: e.activation(out=sz[:, ti, t * 512:(t + 1) * 512], in_=pb[bk][:, 0:512], func=AF.Silu),
                         reads=[PR(bk)], writes=[R(SN, ti, ti + 1)])
            for t in range(8):
                wt, wr = load_w(w_in[:, :, 3072 + t * 512:3072 + (t + 1) * 512], 8, 512, tid=('in', 3072 + t * 512))
                for j in range(4):
                    oc = 4 * t + j
                    sl = oc % 2
                    bk = next_bank()
                    proj_ws(wt, wr, j, 8, hT_rhs, hT_regs, bk)
                    xpr = R("xpre%d" % sl)
                    P.op("dve", lambda e, sl=sl, oc=oc: e.tensor_copy(out=xpre[sl][:, 0:3], in_=chist[:, oc, :]), reads=[R("chist", oc, oc + 1)], writes=[xpr])
                    P.op("act", lambda e, sl=sl, bk=bk: e.activation(out=xpre[sl][:, 3:3 + TB], in_=pb[bk][:, 0:TB], func=AF.Copy), reads=[PR(bk)], writes=[xpr])
                    if tb == nblk - 1:
                        P.op("dve", lambda e, bk=bk, oc=oc: e.tensor_copy(out=cst[:, oc, :], in_=pb[bk][:, TB - 3:TB]), reads=[PR(bk)], writes=[R("cst", oc, oc + 1)])
                    else:
                        P.op("dve", lambda e, sl=sl, oc=oc: e.tensor_copy(out=chist[:, oc, :], in_=xpre[sl][:, TB:TB + 3]), reads=[xpr], writes=[R("chist", oc, oc + 1)])
                    P.op("pool", lambda e, sl=sl, oc=oc: e.tensor_tensor(out=dg[sl][:], in0=identb[:].unsqueeze(1).to_broadcast([128, 4, 128]),
                                                                          in1=pv[:, PV_CW + oc:PV_CW + oc + 97:32].unsqueeze(2).to_broadcast([128, 4, 128]), op=ALU.mult),
                         reads=[R("identb"), R("pv")], writes=[R("dg%d" % sl, 0, 4)])
                    cbk = 4 + sl
                    for k in range(4):
                        P.op("pe", lambda e, sl=sl, k=k, cbk=cbk: e.matmul(pb[cbk][:, 0:TB], lhsT=dg[sl][:, k, :], rhs=xpre[sl][:, k:k + TB], start=(k == 0), stop=(k == 3)),
                             reads=[R("dg%d" % sl, k, k + 1), xpr], writes=[PR(cbk)])
                    P.op("act", lambda e, cbk=cbk, oc=oc: e.activation(out=xbc[:, oc, :], in_=pb[cbk][:, 0:TB], func=AF.Silu, bias=pv[:, PV_CB + oc:PV_CB + oc + 1], scale=1.0),
                         reads=[PR(cbk), R("pv")], writes=[R("xbc", oc, oc + 1)])
            if tb == nblk - 1:
                DMA("sp", o_cp, cst[:], "st_cp", reads=[R("cst", 0, 32)], writes=[R("o_cp")])
            for ti in range(4):
                for kc in range(8):
                    P.op("pe", lambda e, ti=ti, kc=kc: e.matmul(pb[6][:, ti * 32:(ti + 1) * 32], lhsT=hT[:, kc, ti * 128:(ti + 1) * 128], rhs=wdt[:, kc, :],
                                                                  start=(kc == 0), stop=(kc == 7)),
                         reads=[R("hT", kc, kc + 1), R("wdt")], writes=[PR(6, ti * 32, ti * 32 + 32)])
            P.op("dve", lambda e: e.tensor_tensor(out=dtx[:], in0=pb[6][:, 0:128].rearrange("p (a b) -> p a b", a=4),
                                                  in1=rowb[:, RB_DTB:RB_DTB + 32].unsqueeze(1).to_broadcast([128, 4, 32]), op=ALU.add),
                 reads=[PR(6, 0, 128), R("rowb")], writes=[R("dtx")])
            P.op("act", lambda e: e.activation(out=dtt[:], in_=dtx[:], func=AF.Abs), reads=[R("dtx")], writes=[R("dtt")])
            P.op("act", lambda e: e.activation(out=dtu[:], in_=dtt[:], func=AF.Exp, scale=-1.0), reads=[R("dtt")], writes=[R("dtu")])
            P.op("act", lambda e: e.activation(out=dtt[:], in_=dtu[:], func=AF.Ln, bias=1.0, scale=1.0), reads=[R("dtu")], writes=[R("dtt")])
            P.op("dve", lambda e: e.scalar_tensor_tensor(out=dt_tok[:], in0=dtx[:], scalar=0.0, in1=dtt[:], op0=ALU.max, op1=ALU.add),
                 reads=[R("dtx"), R("dtt")], writes=[R("dt_tok")])
            P.op("dve", lambda e: e.tensor_tensor(out=dA_bf[:], in0=dt_tok[:], in1=anegb[:].unsqueeze(1).to_broadcast([128, 4, 32]), op=ALU.mult),
                 reads=[R("dt_tok"), R("anegb")], writes=[R("dA_bf")])
            for t in range(2):
                wt, wr = load_w(w_in[:, :, 8224 + t * 512:8224 + (t + 1) * 512], 8, 512, tid=('in', 8224 + t * 512))
                for j in range(4):
                    oc = 4 * t + j
                    bk = next_bank()
                    proj_ws(wt, wr, j, 8, hT_rhs, hT_regs, bk)
                    P.op("act", lambda e, bk=bk, oc=oc: e.activation(out=gbT[:, oc, :], in_=pb[bk][:, 0:TB], func=AF.Sigmoid),
                         reads=[PR(bk)], writes=[R("gT", oc, oc + 1)])


            def ssd_acd(ci):
                tsl = slice(ci * 128, (ci + 1) * 128)
                dAc = dA_bf[:, ci, :]
                P.op("pe", lambda e: e.matmul(pb[7][:, 0:32], lhsT=triub[:], rhs=dAc, start=True, stop=True), reads=[R("triub"), R("dA_bf")], writes=[PR(7)])
                P.op("pe", lambda e: e.matmul(pb[7][:, 32:64], lhsT=onesb[:], rhs=dAc, start=True, stop=True), reads=[R("onesb"), R("dA_bf")], writes=[PR(7)])
                P.op("dve", lambda e: e.tensor_scalar(out=negAcs[:], in0=pb[7][:, 0:32], scalar1=-1.0, scalar2=None, op0=ALU.mult), reads=[PR(7)], writes=[R("negAcs")])
                P.op("act", lambda e: e.activation(out=Eacs[:], in_=pb[7][:, 0:32], func=AF.Exp), reads=[PR(7)], writes=[R("Eacs")])
                P.op("dve", lambda e: e.tensor_tensor(out=dte[:], in0=pb[7][:, 32:64], in1=negAcs[:], op=ALU.add), reads=[PR(7), R("negAcs")], writes=[R("dte")])
                P.op("act", lambda e: e.activation(out=dte[:], in_=dte[:], func=AF.Exp), reads=[R("dte")], writes=[R("dte")])
                P.op("act", lambda e: e.activation(out=cdB[:], in_=pb[7][:, 32:64], func=AF.Exp), reads=[PR(7)], writes=[R("cdB")])
                for g in range(NG):
                    P.op("pe", lambda e, g=g: e.transpose(out=PBF(2)[:, g * 128:(g + 1) * 128], in_=xbc[:, 16 + g, tsl], identity=identb[:]),
                         reads=[R("xbc", 16 + g, 17 + g), R("identb")], writes=[PR(2)])
                P.op("act", lambda e: e.activation(out=B_tok[:], in_=PBF(2)[:, 0:1024], func=AF.Copy), reads=[PR(2)], writes=[R("B_tok")])
                for g in range(NG):
                    bk = 3 + g // 4
                    c0 = (g % 4) * 128
                    P.op("pe", lambda e, g=g, bk=bk, c0=c0: e.matmul(pb[bk][:, c0:c0 + 128], lhsT=xbc[:, 16 + g, tsl], rhs=xbc[:, 24 + g, tsl], start=True, stop=True),
                         reads=[R("xbc", 16 + g, 17 + g), R("xbc", 24 + g, 25 + g)], writes=[PR(bk)])
                for q in range(2):
                    P.op("act", lambda e, q=q: e.activation(out=CBs[:, q * 4:(q + 1) * 4, :], in_=pb[3 + q][:, 0:512].rearrange("p (a b) -> p a b", a=4), func=AF.Copy),
                         reads=[PR(3 + q)], writes=[R("CBs", q * 4, q * 4 + 4)])

            def ssd_b(ci):
                tsl = slice(ci * 128, (ci + 1) * 128)
                xd = xdtb[ci % 2]
                xr = R("xdt%d" % (ci % 2))
                for fc in range(16):
                    bk = fc // 8
                    c0 = (fc % 8) * 128
                    P.op("pe", lambda e, bk=bk, c0=c0, fc=fc: e.transpose(out=PBF(bk)[:, c0:c0 + 128], in_=xbc[:, fc, tsl], identity=identb[:]),
                         reads=[R("xbc", fc, fc + 1), R("identb")], writes=[PR(bk)])
                for bk in range(2):
                    P.op("dve", lambda e, bk=bk: e.tensor_tensor(out=xd[:, bk * 1024:(bk + 1) * 1024].rearrange("p (h q) -> p h q", h=16),
                                                                 in0=PBF(bk)[:, 0:1024].rearrange("p (h q) -> p h q", h=16),
                                                                 in1=dt_tok[:, ci, bk * 16:(bk + 1) * 16].unsqueeze(2).to_broadcast([128, 16, HP]), op=ALU.mult),
                         reads=[PR(bk), R("dt_tok")], writes=[xr])
                P.op("dve", lambda e: e.tensor_tensor(out=xdte[:].rearrange("p (h q) -> p h q", h=NH), in0=xd[:].rearrange("p (h q) -> p h q", h=NH),
                                                      in1=dte[:].unsqueeze(2).to_broadcast([128, NH, HP]), op=ALU.mult),
                     reads=[xr, R("dte")], writes=[R("xdte", 0, 8)])

            def ssd_decay(ci, hq):
                sl = hq % 2
                dbanks = (5, 6) if hq % 2 == 0 else (3, 4)
                for half in range(2):
                    bk = dbanks[half]
                    for j in range(4):
                        h = hq * 8 + half * 4 + j
                        P.op("pe", lambda e, bk=bk, j=j: e.matmul(pb[bk][:, j * 128:(j + 1) * 128], lhsT=identb[:], rhs=nmb[:, 0, :], start=True, stop=False),
                             reads=[R("identb"), R("nmb")], writes=[PR(bk)])
                        P.op("pe", lambda e, bk=bk, j=j, h=h: e.matmul(pb[bk][:, j * 128:(j + 1) * 128], lhsT=dA_bf[:, ci, h:h + 1].to_broadcast([128, 128]),
                                                                        rhs=triub[:], start=False, stop=True),
                             reads=[R("dA_bf"), R("triub")], writes=[PR(bk)])
                    for j in range(4):
                        h = hq * 8 + half * 4 + j
                        P.op("act", lambda e, bk=bk, j=j, h=h, half=half: e.activation(out=dec[sl][:, half * 4 + j, :], in_=pb[bk][:, j * 128:(j + 1) * 128],
                                                                                       func=AF.Exp, bias=negAcs[:, h:h + 1], scale=1.0),
                             reads=[PR(bk), R("negAcs")], writes=[R("dec%d" % sl, half * 4 + j, half * 4 + j + 1)])

            def ssd_y(ci, hq):
                tsl = slice(ci * 128, (ci + 1) * 128)
                sl = hq % 2
                P.op("dve", lambda e: e.tensor_tensor(out=scr[sl][:].rearrange("p (g j) l -> p g j l", g=2), in0=dec[sl][:].rearrange("p (g j) l -> p g j l", g=2),
                                                      in1=CBs[:, 2 * hq:2 * hq + 2, :].unsqueeze(2).to_broadcast([128, 2, 4, 128]), op=ALU.mult),
                     reads=[R("dec%d" % sl, 0, 8), R("CBs", 2 * hq, 2 * hq + 2)], writes=[R("dec%d" % sl, 0, 8)])
                bA = (0, 2)[hq % 2]
                bB = (1, 7)[hq % 2]
                for jj in range(8):
                    h = 8 * hq + jj
                    P.op("pe", lambda e, jj=jj, h=h: e.matmul(pb[bA][:, jj * 64:(jj + 1) * 64], lhsT=xbc[:, h // 2, tsl], rhs=diagD[:, h // 2, (h % 2) * 64:(h % 2 + 1) * 64], start=True, stop=False),
                         reads=[R("xbc", h // 2, h // 2 + 1), R("diagD")], writes=[PR(bA)])
                    P.op("pe", lambda e, jj=jj, h=h: e.matmul(pb[bA][:, jj * 64:(jj + 1) * 64], lhsT=scr[sl][:, jj, :], rhs=xdtb[ci % 2][:, h * 64:(h + 1) * 64], start=False, stop=True),
                         reads=[R("dec%d" % sl, jj, jj + 1), R("xdt%d" % (ci % 2))], writes=[PR(bA)])
                for gg in range(2):
                    g = 2 * hq + gg
                    P.op("pe", lambda e, g=g, gg=gg: e.matmul(pb[bB][:, gg * 256:(gg + 1) * 256], lhsT=xbc[:, 24 + g, tsl], rhs=hstb[:, g * 256:(g + 1) * 256], start=True, stop=True),
                         reads=[R("xbc", 24 + g, 25 + g), R("hstb", g, g + 1)], writes=[PR(bB)])

            def ssd_ye(ci, hq):
                bA = (0, 2)[hq % 2]
                bB = (1, 7)[hq % 2]
                ysl = slice(hq * 512, (hq + 1) * 512)
                yr = R("ybuf", 2 * hq, 2 * hq + 2)
                P.op("dve", lambda e: e.tensor_tensor(out=ybuf[:, ysl].rearrange("p (h q) -> p h q", h=8), in0=pb[bB][:, 0:512].rearrange("p (h q) -> p h q", h=8),
                                                      in1=Eacs[:, 8 * hq:8 * hq + 8].unsqueeze(2).to_broadcast([128, 8, HP]), op=ALU.mult),
                     reads=[PR(bB), R("Eacs")], writes=[yr])
                P.op("dve", lambda e: e.tensor_tensor(out=ybuf[:, ysl], in0=pb[bA][:, 0:512], in1=ybuf[:, ysl], op=ALU.add), reads=[PR(bA), yr], writes=[yr])
                P.op("dve", lambda e: e.tensor_tensor(out=ybuf[:, ysl], in0=ybuf[:, ysl], in1=sz[:, ci, ysl], op=ALU.mult), reads=[yr, R(SN, ci, ci + 1)], writes=[yr])

            def ssd_g(ci):
                for q in range(4):
                    sbk = 3 + q
                    hsl = slice(q * 512, (q + 1) * 512)
                    hr = R("hst", 2 * q, 2 * q + 2)
                    for gg in range(2):
                        g = 2 * q + gg
                        P.op("pe", lambda e, g=g, gg=gg, sbk=sbk: e.matmul(pb[sbk][:, gg * 256:(gg + 1) * 256], lhsT=B_tok[:, g * 128:(g + 1) * 128], rhs=xdte[:, g * 256:(g + 1) * 256], start=True, stop=True),
                             reads=[R("B_tok"), R("xdte", g, g + 1)], writes=[PR(sbk)])
                    P.op("pool", lambda e, q=q, hsl=hsl: e.tensor_tensor(out=hst[:, hsl].rearrange("p (h q) -> p h q", h=8), in0=hst[:, hsl].rearrange("p (h q) -> p h q", h=8),
                                                                         in1=cdB[:, 8 * q:8 * q + 8].unsqueeze(2).to_broadcast([128, 8, HP]), op=ALU.mult),
                         reads=[hr, R("cdB")], writes=[hr])
                    P.op("dve", lambda e, sbk=sbk, hsl=hsl: e.tensor_tensor(out=hst[:, hsl], in0=pb[sbk][:, 0:512], in1=hst[:, hsl], op=ALU.add), reads=[PR(sbk), hr], writes=[hr])
                    P.op("act", lambda e, hsl=hsl: e.activation(out=hstb[:, hsl], in_=hst[:, hsl], func=AF.Copy), reads=[hr], writes=[R("hstb", 2 * q, 2 * q + 2)])

            def ssd_h(ci):
                tsl = slice(ci * 128, (ci + 1) * 128)
                ynb2 = xdtb[ci % 2]
                xr = R("xdt%d" % (ci % 2))
                P.op("act", lambda e: e.activation(out=ynb2[:], in_=ybuf[:], func=AF.Square, accum_out=ssq[:, 4:5]), reads=[R("ybuf", 0, 8)], writes=[xr, R("ssq", 4, 5)])
                P.op("act", lambda e: e.activation(out=ssq[:, 5:6], in_=ssq[:, 4:5], func=AF.Ln, scale=1.0 / DI, bias=epsc[:, 0:1]), reads=[R("ssq", 4, 5), R("epsc")], writes=[R("ssq", 5, 6)])
                P.op("act", lambda e: e.activation(out=ssq[:, 6:7], in_=ssq[:, 5:6], func=AF.Exp, scale=-0.5), reads=[R("ssq", 5, 6)], writes=[R("ssq", 6, 7)])
                P.op("dve", lambda e: e.scalar_tensor_tensor(out=ynb2[:], in0=ybuf[:], scalar=ssq[:, 6:7], in1=snwb[:], op0=ALU.mult, op1=ALU.mult),
                     reads=[R("ybuf", 0, 8), R("ssq", 6, 7), R("snwb")], writes=[xr])
                for fc in range(16):
                    bk = fc // 8
                    c0 = (fc % 8) * 128
                    P.op("pe", lambda e, bk=bk, c0=c0, fc=fc: e.transpose(out=PBF(bk)[:, c0:c0 + 128], in_=ynb2[:, fc * 128:(fc + 1) * 128], identity=identb[:]),
                         reads=[xr, R("identb")], writes=[PR(bk)])
                for q in range(2):
                    P.op("act", lambda e, q=q: e.activation(out=ynT[:, q * 8:(q + 1) * 8, tsl], in_=PBF(q)[:, 0:1024].rearrange("p (a b) -> p a b", a=8), func=AF.Copy),
                         reads=[PR(q)], writes=[R("ubuf", 0, 8)])

            ssd_acd(0)
            ssd_b(0)
            for ci in range(4):
                if ci == 0:
                    ssd_decay(ci, 0)
                ssd_decay(ci, 1)
                ssd_y(ci, 0)
                for hq in range(4):
                    if hq + 2 < 4:
                        ssd_decay(ci, hq + 2)
                    if hq + 1 < 4:
                        ssd_y(ci, hq + 1)
                    ssd_ye(ci, hq)
                ssd_g(ci)
                if ci + 1 < 4:
                    ssd_acd(ci + 1)
                    ssd_b(ci + 1)
                    ssd_decay(ci + 1, 0)
                ssd_h(ci)
            if tb == nblk - 1:
                DMA("sp", o_sp, hst[:], "st_sp", reads=[R("hst", 0, 8)], writes=[R("o_sp")])

            if tb + 1 < nblk:
                DMA("sp", bufX[(tb + 1) % 2][:], xp[(tb + 1) * TB:(tb + 2) * TB, :].rearrange("(t p) d -> p t d", p=128), "ld_x", writes=[R(SN, 0, 4)])
            for t in range(4):
                wt, wr = load_w(w_ssd[:, :, t * 256:(t + 1) * 256], 16, 256, tid=('ssd', t))
                for j in range(2):
                    oc = 2 * t + j
                    bk = next_bank()
                    proj_ws(wt, wr, j, 16, lambda kc: ynT[:, kc, :], lambda kc: [R("ubuf", 0, 8)], bk)
                    P.op("dve", lambda e, bk=bk, oc=oc: e.tensor_tensor(out=sA[:, 0:TB], in0=pb[bk][:, 0:TB], in1=gbT[:, oc, :], op=ALU.mult),
                         reads=[PR(bk), R("gT", oc, oc + 1)], writes=[R("sA")])
                    P.op("dve", lambda e, oc=oc: e.tensor_tensor(out=hT[:, oc, :], in0=sA[:, 0:TB], in1=amT[:, oc, :], op=ALU.add),
                         reads=[R("sA"), R("amT", oc, oc + 1)], writes=[R("hT", oc, oc + 1)])
            for t in range(2):
                wt, wr = load_w(w_out[:, :, t * 512:(t + 1) * 512], 8, 512, tid=('out', t))
                for ti in range(4):
                    bk = next_bank()
                    proj_as(wt, wr, ti, 8, hT_lhs, hT_regs, bk)
                    P.op("dve", lambda e, bk=bk, t=t: e.tensor_tensor(out=sB[:, 0:512], in0=pb[bk][:, 0:512], in1=g1B[:, t * 512:(t + 1) * 512], op=ALU.mult),
                         reads=[PR(bk), R("g1B", t, t + 1)], writes=[R("sB")])
                    P.op("dve", lambda e, ti=ti, t=t: e.tensor_tensor(out=xtok[:, ti, t * 512:(t + 1) * 512], in0=xtok[:, ti, t * 512:(t + 1) * 512], in1=sB[:, 0:512], op=ALU.add),
                         reads=[R("sB"), R(XN, ti, ti + 1)], writes=[R(XN, ti, ti + 1)])
            rmsnorm_to_T("a2T", a2T, 24, xtok, XN)
            for t in range(11):
                wt, wr = load_w(w_ffi[:, :, t * 512:(t + 1) * 512], 8, 512, tid=('ffi', t))
                for j in range(2):
                    fc = 2 * t + j
                    bg = next_bank()
                    proj_ws(wt, wr, j, 8, hT_rhs, hT_regs, bg)
                    bu = next_bank()
                    proj_ws(wt, wr, j, 8, hT_rhs, hT_regs, bu, col0=256)
                    P.op("act", lambda e, bg=bg: e.activation(out=sA[:, 0:TB], in_=pb[bg][:, 0:TB], func=AF.Silu), reads=[PR(bg)], writes=[R("sA")])
                    P.op("dve", lambda e, bu=bu, fc=fc: e.tensor_tensor(out=fT[:, fc, :], in0=pb[bu][:, 0:TB], in1=sA[:, 0:TB], op=ALU.mult),
                         reads=[PR(bu), R("sA")], writes=[R("xbc", fc, fc + 1)])
            fT_lhs = lambda kc, ti: fT[:, kc, ti * 128:(ti + 1) * 128]
            fT_regs = lambda kc: [R("xbc", kc, kc + 1)]
            for half in range(2):
                for kg in range(3):
                    nk = 8 if kg < 2 else 6
                    wt, wr = load_w(w_ffo[:, kg * 8:kg * 8 + nk, half * 512:(half + 1) * 512], nk, 512, tid=('ffo', kg, half))
                    for ti in range(4):
                        proj_as(wt, wr, ti, nk, fT_lhs, fT_regs, 4 + ti, kc0=kg * 8, first=(kg == 0), last=(kg == 2))
                for ti in range(4):
                    P.op("dve", lambda e, ti=ti, half=half: e.tensor_tensor(out=sB[:, 0:512], in0=pb[4 + ti][:, 0:512], in1=g2B[:, half * 512:(half + 1) * 512], op=ALU.mult),
                         reads=[PR(4 + ti), R("g2B", half, half + 1)], writes=[R("sB")])
                    P.op("dve", lambda e, ti=ti, half=half: e.tensor_tensor(out=xtok[:, ti, half * 512:(half + 1) * 512], in0=xtok[:, ti, half * 512:(half + 1) * 512], in1=sB[:, 0:512], op=ALU.add),
                         reads=[R("sB"), R(XN, ti, ti + 1)], writes=[R(XN, ti, ti + 1)])
            if tb + 1 < nblk:
                for t in range(2):
                    load_w(w_in[:, :, 7200 + t * 512:7200 + (t + 1) * 512], 8, 512, tid=('in', 7200 + t * 512), prefetch=True)
            elif do_sample:
                for c0 in (0, 512):
                    load_w(w_in[:, :, c0:c0 + 512], 8, 512, tid=('in', c0), prefetch=True)
            for ti in range(4):
                ys = 0
                P.op("act", lambda e, ti=ti: e.activation(out=junk[:, 0:D], in_=xtok[:, ti, :], func=AF.Square, accum_out=ssq[:, ti:ti + 1]),
                     reads=[R(XN, ti, ti + 1)], writes=[R("xdte", 0, 8), R("ssq", ti, ti + 1)])
                P.op("act", lambda e, ti=ti: e.activation(out=rsq[:, 4 + ti:5 + ti], in_=ssq[:, ti:ti + 1], func=AF.Ln, scale=1.0 / D, bias=epsc[:, 0:1]),
                     reads=[R("ssq", ti, ti + 1), R("epsc")], writes=[R("rsq", 4 + ti, 5 + ti)])
                P.op("act", lambda e, ti=ti: e.activation(out=rsq[:, ti:ti + 1], in_=rsq[:, 4 + ti:5 + ti], func=AF.Exp, scale=-0.5), reads=[R("rsq", 4 + ti, 5 + ti)], writes=[R("rsq", ti, ti + 1)])
                P.op("dve", lambda e, ti=ti, ys=ys: e.scalar_tensor_tensor(out=yst[ys][:], in0=xtok[:, ti, :], scalar=rsq[:, ti:ti + 1], in1=rowb[:, RB_FNW:RB_FNW + D], op0=ALU.mult, op1=ALU.mult),
                     reads=[R(XN, ti, ti + 1), R("rsq", ti, ti + 1), R("rowb")], writes=[R("yst%d" % ys)])
                r0 = tb * TB + ti * 128
                DMA("sp", o_y[r0:r0 + 128, :], yst[ys][:], "st_y%d" % ys, reads=[R("yst%d" % ys)], writes=[R("o_y", tb * 4 + ti, tb * 4 + ti + 1)])


        for tb in range(nblk):
            emit_block(tb, *role_views(tb % 2))
        xtok, XN, sz, SN, _pl = role_views((nblk - 1) % 2)

        if do_sample:
            Ssl = slice(1, 17)
            xbcF = xbc[:].rearrange("p a b -> p (a b)")
            wslots.append((xbcF[:, 4096:8192], R("xbc", 8, 16)))
            wslots.append((xbcF[:, 8192:12288], R("xbc", 16, 24)))
            stbuf = [xtok[:, 0:2, :].rearrange("p a b -> p (a b)"), xtok[:, 2:4, :].rearrange("p a b -> p (a b)")]
            streg = [R(XN, 0, 2), R(XN, 2, 4)]
            ubF = ubuf[:].rearrange("p a b -> p (a b)")
            stp = ubF[:, 0:1920].rearrange("p (c s r) -> p c s r", c=8, s=NS)
            newst = ubF[:, 1920:3840].rearrange("p (c s r) -> p c s r", c=8, s=NS)
            stc = ybuf[:, 0:1536].rearrange("p (c s r) -> p c s r", c=32, s=NS)
            newcst = hst[:, 0:1536].rearrange("p (c s r) -> p c s r", c=32, s=NS)
            gTf = gT[:].rearrange("p a b -> p (a b)").bitcast(F32)
            projS = gTf[:, 0:56 * NS].rearrange("p (c s) -> p c s", c=56)
            amF = amT[:].rearrange("p a b -> p (a b)").bitcast(F32)
            acc1 = amF[:, 0:512].rearrange("p (c s) -> p c s", c=32)
            acc2 = amF[:, 512:1024].rearrange("p (c s) -> p c s", c=32)
            amS = amF[:, 1024:1152].rearrange("p (c s) -> p c s", c=8)
            gaS = amF[:, 1152:1280].rearrange("p (c s) -> p c s", c=8)
            gbS = amF[:, 1280:1408].rearrange("p (c s) -> p c s", c=8)
            ptmp = amF[:, 1408:1536].rearrange("p (c s) -> p c s", c=8)
            sgS = amF[:, 1536:1568]
            xbcS = B_tok[:, 0:512].rearrange("p (c s) -> p c s", c=32)
            CBf = CBs[:].rearrange("p a b -> p (a b)")
            hTs = CBf[:, 0:128].rearrange("p (c s) -> p c s", c=8)
            mixTs = CBf[:, 128:256].rearrange("p (c s) -> p c s", c=8)
            pooledS = CBf[:, 256:384].rearrange("p (c s) -> p c s", c=8)
            ynTs = CBf[:, 384:640].rearrange("p (c s) -> p c s", c=16)
            fTs = CBf[:, 640:992].rearrange("p (c s) -> p c s", c=22)
            x_tokS = x_tok[0:NS, :]
            xdt_tokS = xdt[0:NS, :]
            szS = xdte[0:NS, :]
            xs_tok = yst[0][0:NS, :]
            szf = sz[:].rearrange("p a b -> p (a b)").bitcast(F32)
            ysS = szf[0:NS, 0:2048]
            gs1 = szf[0:NS, 2048:3072]
            gs2 = szf[0:NS, 3072:4096]
            xnS = dec[0][:].rearrange("p a b -> p (a b)")[0:NS, :]
            hTF = hT[:].rearrange("p a b -> p (a b)")
            ynS = hTF[0:NS, 0:2048]
            junkS = hTF[0:NS, 2048:4096]
            tmpDx = hTF[0:NS, :].bitcast(F32)
            decBs = sA[:, 0:512].rearrange("p (s h) -> p s h", s=NS)
            mask16 = sB[:, 0:256].rearrange("p (a b) -> p a b", a=NS)
            identfS = sB[:, 256:384]
            CmaskS = xbcF[:, 0:2048].rearrange("p (g s m) -> p g s m", g=NG, s=NS)
            ssS, rsS = ssq[0:NS, :], rsq[0:NS, :]
            RCB = R("CBs", 0, 8)
            RAM = R("amT", 0, 8)

            DMA("sp", xs_tok, xsm, "ld_xs", writes=[R("yst0")])
            DMA("sp", stp, st_pool, "ld_stp", writes=[R("ubuf", 0, 8)])
            DMA("sp", stc, st_conv, "ld_stc", writes=[R("ybuf", 0, 8)])
            P.op("pool", lambda e: e.memset(sB[:, 0:384], 0.0), writes=[R("sB")])
            P.op("pool", lambda e: e.affine_select(out=mask16, in_=mask16, pattern=[[1, NS], [-1, NS]], compare_op=ALU.not_equal, fill=1.0, base=0, channel_multiplier=0),
                 reads=[R("sB")], writes=[R("sB")])
            P.op("pool", lambda e: e.affine_select(out=identfS, in_=identfS, pattern=[[-1, 128]], compare_op=ALU.not_equal, fill=1.0, base=0, channel_multiplier=1),
                 reads=[R("sB")], writes=[R("sB")])
            for (gsv, c0, b0) in ((gs1, 16, 0), (gs2, 40, 2)):
                for c in range(8):
                    bk = b0 + c // 4
                    P.op("pe", lambda e, bk=bk, c=c, c0=c0: e.matmul(pb[bk][0:NS, (c % 4) * 128:(c % 4 + 1) * 128], lhsT=modT[:, c0 + c, Ssl], rhs=identfS, start=True, stop=True),
                         reads=[R("modT", c0 + c, c0 + c + 1), R("sB")], writes=[PR(bk)])
                for q in range(2):
                    P.op("act", lambda e, gsv=gsv, q=q, b0=b0: e.activation(out=gsv[:, q * 512:(q + 1) * 512], in_=pb[b0 + q][0:NS, 0:512], func=AF.Copy),
                         reads=[PR(b0 + q)], writes=[R(SN, 0, 4)])

            def s_norm_T(aT, bcol0):
                P.op("act", lambda e: e.activation(out=junkS[:, 0:D], in_=xs_tok, func=AF.Square, accum_out=ssS[:, 0:1]), reads=[R("yst0")], writes=[R("hT", 0, 8), R("ssq", 0, 8)])
                P.op("act", lambda e: e.activation(out=rsS[:, 4:5], in_=ssS[:, 0:1], func=AF.Ln, scale=1.0 / D, bias=epsc[0:NS, 0:1]), reads=[R("ssq", 0, 8), R("epsc")], writes=[R("rsq", 0, 8)])
                P.op("act", lambda e: e.activation(out=rsS[:, 0:1], in_=rsS[:, 4:5], func=AF.Exp, scale=-0.5), reads=[R("rsq", 0, 8)], writes=[R("rsq", 0, 8)])
                P.op("dve", lambda e: e.tensor_scalar(out=xnS, in0=xs_tok, scalar1=rsS[:, 0:1], scalar2=None, op0=ALU.mult), reads=[R("yst0"), R("rsq", 0, 8)], writes=[R("dec0", 0, 8)])
                for kc in range(8):
                    P.op("pe", lambda e, kc=kc: e.transpose(out=PBF(0)[:, kc * NS:(kc + 1) * NS], in_=xnS[:, kc * 128:(kc + 1) * 128], identity=identb[0:NS, 0:NS]),
                         reads=[R("dec0", 0, 8), R("identb")], writes=[PR(0)])
                P.op("dve", lambda e, aT=aT: e.tensor_tensor(out=ptmp, in0=PBF(0)[:, 0:128].rearrange("p (c s) -> p c s", c=8), in1=aT[:, :, Ssl], op=ALU.mult),
                     reads=[PR(0), R("a1T"), R("a2T")], writes=[RAM])
                P.op("dve", lambda e, bcol0=bcol0: e.tensor_tensor(out=hTs, in0=ptmp, in1=modT[:, bcol0:bcol0 + 8, Ssl], op=ALU.add),
                     reads=[RAM, R("modT", bcol0, bcol0 + 8)], writes=[RCB])

            s_norm_T(a1T, 0)
            ws_tiles = [(0, 0), (512, 4)] + [(3072 + 512 * t, 8 + 4 * t) for t in range(8)] + [(7200 + 512 * t, 40 + 4 * t) for t in range(4)]
            for (col0, cb0) in ws_tiles:
                wt, wr = load_w(w_in[:, :, col0:col0 + 512], 8, 512, tid=('in', col0))
                for j in range(4):
                    for kc in range(8):
                        P.op("pe", lambda e, j=j, kc=kc, wt=wt: e.matmul(pb[1][:, j * NS:(j + 1) * NS], lhsT=wt[:, kc, j * 128:(j + 1) * 128], rhs=hTs[:, kc, :], start=(kc == 0), stop=(kc == 7)),
                             reads=[wr, RCB], writes=[PR(1)])
                P.op("act", lambda e, cb0=cb0: e.activation(out=projS[:, cb0:cb0 + 4, :], in_=pb[1][:, 0:4 * NS].rearrange("p (c s) -> p c s", c=4), func=AF.Copy),
                     reads=[PR(1)], writes=[R("gT", 0, 8)])
            for t in range(4):
                wt, wr = load_w(w_in[:, :, 1024 + t * 512:1024 + (t + 1) * 512], 8, 512, tid=('in', 1024 + t * 512))
                for kc in range(8):
                    P.op("pe", lambda e, kc=kc, wt=wt: e.matmul(pb[2][0:NS, 0:512], lhsT=hTs[:, kc, :], rhs=wt[:, kc, :], start=(kc == 0), stop=(kc == 7)),
                         reads=[wr, RCB], writes=[PR(2)])
                P.op("act", lambda e, t=t: e.activation(out=szS[:, t * 512:(t + 1) * 512], in_=pb[2][0:NS, 0:512], func=AF.Silu), reads=[PR(2)], writes=[R("xdte", 0, 8)])
            for kc in range(8):
                P.op("pe", lambda e, kc=kc: e.matmul(pb[3][0:NS, 0:32], lhsT=hTs[:, kc, :], rhs=wdt[:, kc, :], start=(kc == 0), stop=(kc == 7)), reads=[RCB, R("wdt")], writes=[PR(3)])
            d_x, d_t, d_u, d_dt, d_dec = dtx[0:NS, 0, :], dtt[0:NS, 0, :], dtu[0:NS, 0, :], dt_tok[0:NS, 0, :], dtx[0:NS, 1, :]
            P.op("dve", lambda e: e.tensor_tensor(out=d_x, in0=pb[3][0:NS, 0:32], in1=rowb[0:NS, RB_DTB:RB_DTB + 32], op=ALU.add), reads=[PR(3), R("rowb")], writes=[R("dtx")])
            P.op("act", lambda e: e.activation(out=d_t, in_=d_x, func=AF.Abs), reads=[R("dtx")], writes=[R("dtt")])
            P.op("act", lambda e: e.activation(out=d_u, in_=d_t, func=AF.Exp, scale=-1.0), reads=[R("dtt")], writes=[R("dtu")])
            P.op("act", lambda e: e.activation(out=d_t, in_=d_u, func=AF.Ln, bias=1.0, scale=1.0), reads=[R("dtu")], writes=[R("dtt")])
            P.op("dve", lambda e: e.scalar_tensor_tensor(out=d_dt, in0=d_x, scalar=0.0, in1=d_t, op0=ALU.max, op1=ALU.add), reads=[R("dtx"), R("dtt")], writes=[R("dt_tok")])
            P.op("dve", lambda e: e.tensor_tensor(out=d_u, in0=d_dt, in1=anegb[0:NS, :], op=ALU.mult), reads=[R("dt_tok"), R("anegb")], writes=[R("dtu")])
            P.op("act", lambda e: e.activation(out=d_dec, in_=d_u, func=AF.Exp), reads=[R("dtu")], writes=[R("dtx")])
            for s_ in range(NS):
                P.op("pe", lambda e, s_=s_: e.matmul(pb[0][:, s_ * 32:(s_ + 1) * 32], lhsT=identfS[0:NS, s_:s_ + 1].to_broadcast([NS, 128]), rhs=d_dec, start=True, stop=True),
                     reads=[R("sB"), R("dtx")], writes=[PR(0)])
            P.op("act", lambda e: e.activation(out=sA[:, 0:512], in_=pb[0][:, 0:512], func=AF.Copy), reads=[PR(0)], writes=[R("sA")])
            for g in range(4):
                w = 2 ** (g + 1)
                ug = projS[:, 2 * g:2 * g + 2, :]
                P.op("dve", lambda e, g=g, w=w: e.reduce_sum(out=ptmp[:, 0:2, :], in_=stp[:, 2 * g:2 * g + 2, :, 15 - (w - 1):15], axis=mybir.AxisListType.X),
                     reads=[R("ubuf", 0, 8)], writes=[RAM])
                P.op("dve", lambda e, ug=ug: e.tensor_tensor(out=ptmp[:, 0:2, :], in0=ptmp[:, 0:2, :], in1=ug, op=ALU.add), reads=[RAM, R("gT", 0, 8)], writes=[RAM])
                P.op("dve", lambda e, ug=ug, g=g, w=w: e.scalar_tensor_tensor(out=pooledS[:, 2 * g:2 * g + 2, :], in0=ptmp[:, 0:2, :], scalar=1.0 / w, in1=ug, op0=ALU.mult, op1=ALU.subtract),
                     reads=[RAM, R("gT", 0, 8)], writes=[RCB])
            P.op("act", lambda e: e.activation(out=gaS, in_=projS[:, 40:48, :], func=AF.Sigmoid), reads=[R("gT", 0, 8)], writes=[RAM])
            P.op("act", lambda e: e.activation(out=gbS, in_=projS[:, 48:56, :], func=AF.Sigmoid), reads=[R("gT", 0, 8)], writes=[RAM])
            wplt, wplr = load_w(w_pool.rearrange("p g k c -> p (g k) c"), 8, 256, tid=('pool',))
            for g in range(4):
                for j in range(2):
                    oc = 2 * g + j
                    for k2 in range(2):
                        P.op("pe", lambda e, g=g, j=j, k2=k2, oc=oc: e.matmul(pb[2][:, oc * NS:(oc + 1) * NS], lhsT=wplt[:, 2 * g + k2, j * 128:(j + 1) * 128], rhs=pooledS[:, 2 * g + k2, :], start=(k2 == 0), stop=(k2 == 1)),
                             reads=[wplr, RCB], writes=[PR(2)])
            for oc in range(8):
                P.op("dve", lambda e, oc=oc: e.scalar_tensor_tensor(out=amS[:, oc, :], in0=pb[2][:, oc * NS:(oc + 1) * NS], scalar=pv[:, PV_PSC + oc:PV_PSC + oc + 1], in1=gaS[:, oc, :], op0=ALU.mult, op1=ALU.mult),
                     reads=[PR(2), R("pv"), RAM], writes=[RAM])
            P.op("pool", lambda e: e.tensor_copy(out=newst[:, :, :, 0:14], in_=stp[:, :, :, 1:15]), reads=[R("ubuf", 0, 8)], writes=[R("ubuf", 0, 8)])
            P.op("pool", lambda e: e.tensor_copy(out=newst[:, :, :, 14], in_=projS[:, 0:8, :]), reads=[R("gT", 0, 8)], writes=[R("ubuf", 0, 8)])
            DMA("sp", o_ps, newst, "st_ps", reads=[R("ubuf", 0, 8)], writes=[R("o_ps")])
            xnew = projS[:, 8:40, :]
            cwb = lambda k: pv[:, PV_CW + 32 * k:PV_CW + 32 * k + 32].unsqueeze(2).to_broadcast([128, 32, NS])
            P.op("dve", lambda e: e.tensor_tensor(out=acc1, in0=xnew, in1=cwb(3), op=ALU.mult), reads=[R("gT", 0, 8), R("pv")], writes=[RAM])
            for k in range(3):
                P.op("dve", lambda e, k=k: e.tensor_tensor(out=acc2, in0=stc[:, :, :, k], in1=cwb(k), op=ALU.mult), reads=[R("ybuf", 0, 8), R("pv")], writes=[RAM])
                P.op("dve", lambda e: e.tensor_tensor(out=acc1, in0=acc1, in1=acc2, op=ALU.add), reads=[RAM], writes=[RAM])
            P.op("dve", lambda e: e.tensor_tensor(out=acc1, in0=acc1, in1=pv[:, PV_CB:PV_CB + 32].unsqueeze(2).to_broadcast([128, 32, NS]), op=ALU.add), reads=[RAM, R("pv")], writes=[RAM])
            P.op("act", lambda e: e.activation(out=xbcS, in_=acc1, func=AF.Silu), reads=[RAM], writes=[R("B_tok")])
            P.op("pool", lambda e: e.tensor_copy(out=newcst[:, :, :, 0:2], in_=stc[:, :, :, 1:3]), reads=[R("ybuf", 0, 8)], writes=[R("hst", 0, 8)])
            P.op("pool", lambda e: e.tensor_copy(out=newcst[:, :, :, 2], in_=xnew), reads=[R("gT", 0, 8)], writes=[R("hst", 0, 8)])
            DMA("sp", o_cs, newcst, "st_cs", reads=[R("hst", 0, 8)], writes=[R("o_cs")])
            for fc in range(16):
                bk = 1 + fc // 8
                P.op("pe", lambda e, fc=fc, bk=bk: e.transpose(out=PBF(bk)[0:NS, (fc % 8) * 128:(fc % 8 + 1) * 128], in_=xbcS[:, fc, :], identity=identb[:]),
                     reads=[R("B_tok"), R("identb")], writes=[PR(bk)])
            for q in range(2):
                P.op("act", lambda e, q=q: e.activation(out=x_tokS[:, q * 1024:(q + 1) * 1024], in_=PBF(1 + q)[0:NS, 0:1024], func=AF.Copy), reads=[PR(1 + q)], writes=[R("xdt1")])
            P.op("dve", lambda e: e.tensor_tensor(out=xdt_tokS.rearrange("p (h q) -> p h q", h=NH), in0=x_tokS.rearrange("p (h q) -> p h q", h=NH),
                                                  in1=d_dt.unsqueeze(2).to_broadcast([NS, NH, HP]), op=ALU.mult), reads=[R("xdt1"), R("dt_tok")], writes=[R("xdt0")])
            P.op("dve", lambda e: e.tensor_tensor(out=CmaskS, in0=xbcS[:, 24:32, :].unsqueeze(3).to_broadcast([128, NG, NS, NS]),
                                                  in1=mask16.unsqueeze(1).to_broadcast([128, NG, NS, NS]), op=ALU.mult), reads=[R("B_tok"), R("sB")], writes=[R("xbc", 0, 4)])
            P.op("pool", lambda e: e.memset(ysS, 0.0), writes=[R(SN, 0, 4)])
            def samp_L(s_):
                sl = s_ % 2
                DMA("sp", stbuf[sl], st_ssm[s_], "ld_st%d" % sl, writes=[streg[sl]])

            def samp_A(s_):
                sl = s_ % 2
                buf = stbuf[sl]
                for q in range(4):
                    P.op("pe", lambda e, q=q: e.matmul(pb[q][:, 0:512], lhsT=identb[0:NS, s_:s_ + 1].to_broadcast([NS, 128]), rhs=xdt_tokS[:, q * 512:(q + 1) * 512], start=True, stop=True),
                         reads=[R("identb"), R("xdt0")], writes=[PR(q)])
                P.op("pool", lambda e: e.tensor_tensor(out=buf.rearrange("p (h q) -> p h q", h=NH), in0=buf.rearrange("p (h q) -> p h q", h=NH),
                                                       in1=decBs[:, s_, :].unsqueeze(2).to_broadcast([128, NH, HP]), op=ALU.mult), reads=[streg[sl], R("sA")], writes=[streg[sl]])
                for g in range(NG):
                    P.op("dve", lambda e, g=g: e.scalar_tensor_tensor(out=buf[:, g * 256:(g + 1) * 256], in0=pb[g // 2][:, (g % 2) * 256:(g % 2 + 1) * 256],
                                                                       scalar=xbcS[:, 16 + g, s_:s_ + 1], in1=buf[:, g * 256:(g + 1) * 256], op0=ALU.mult, op1=ALU.add),
                         reads=[PR(g // 2), R("B_tok"), streg[sl]], writes=[streg[sl]])
                P.op("act", lambda e: e.activation(out=hstb[:], in_=buf, func=AF.Copy), reads=[streg[sl]], writes=[R("hstb", 0, 8)])
                DMA("sp", o_ss[s_], buf, "st_ss%d" % sl, reads=[streg[sl]], writes=[R("o_ss", s_, s_ + 1)])

            def samp_A2(s_):
                for g in range(NG):
                    P.op("pe", lambda e, g=g: e.matmul(pb[4 + g // 2][0:NS, (g % 2) * 256:(g % 2 + 1) * 256], lhsT=CmaskS[:, g, s_, :], rhs=hstb[:, g * 256:(g + 1) * 256], start=True, stop=True),
                         reads=[R("xbc", 0, 4), R("hstb", 0, 8)], writes=[PR(4 + g // 2)])

            def samp_B(s_):
                for q in range(4):
                    P.op("dve", lambda e, q=q: e.tensor_tensor(out=ysS[:, q * 512:(q + 1) * 512], in0=pb[4 + q][0:NS, 0:512], in1=ysS[:, q * 512:(q + 1) * 512], op=ALU.add),
                         reads=[PR(4 + q), R(SN, 0, 4)], writes=[R(SN, 0, 4)])

            samp_L(0)
            samp_L(1)
            samp_A(0)
            samp_A2(0)
            for s_ in range(NS):
                if s_ + 2 < NS:
                    samp_L(s_ + 2)
                if s_ + 1 < NS:
                    samp_A(s_ + 1)
                samp_B(s_)
                if s_ + 1 < NS:
                    samp_A2(s_ + 1)
            P.op("dve", lambda e: e.tensor_tensor(out=tmpDx.rearrange("p (h q) -> p h q", h=NH), in0=x_tokS.rearrange("p (h q) -> p h q", h=NH),
                                                  in1=rowb[0:NS, RB_DSK:RB_DSK + 32].unsqueeze(2).to_broadcast([NS, NH, HP]), op=ALU.mult), reads=[R("xdt1"), R("rowb")], writes=[R("hT", 0, 8)])
            P.op("dve", lambda e: e.tensor_tensor(out=ysS, in0=ysS, in1=tmpDx, op=ALU.add), reads=[R(SN, 0, 4), R("hT", 0, 8)], writes=[R(SN, 0, 4)])
            P.op("dve", lambda e: e.tensor_tensor(out=ysS, in0=ysS, in1=szS, op=ALU.mult), reads=[R(SN, 0, 4), R("xdte", 0, 8)], writes=[R(SN, 0, 4)])
            P.op("act", lambda e: e.activation(out=junkS, in_=ysS, func=AF.Square, accum_out=ssS[:, 1:2]), reads=[R(SN, 0, 4)], writes=[R("hT", 0, 8), R("ssq", 0, 8)])
            P.op("act", lambda e: e.activation(out=rsS[:, 5:6], in_=ssS[:, 1:2], func=AF.Ln, scale=1.0 / DI, bias=epsc[0:NS, 0:1]), reads=[R("ssq", 0, 8), R("epsc")], writes=[R("rsq", 0, 8)])
            P.op("act", lambda e: e.activation(out=rsS[:, 1:2], in_=rsS[:, 5:6], func=AF.Exp, scale=-0.5), reads=[R("rsq", 0, 8)], writes=[R("rsq", 0, 8)])
            P.op("dve", lambda e: e.scalar_tensor_tensor(out=ynS, in0=ysS, scalar=rsS[:, 1:2], in1=snwb[0:NS, :], op0=ALU.mult, op1=ALU.mult),
                 reads=[R(SN, 0, 4), R("rsq", 0, 8), R("snwb")], writes=[R("hT", 0, 8)])
            for fc in range(16):
                P.op("pe", lambda e, fc=fc: e.transpose(out=PBF(0)[:, fc * NS:(fc + 1) * NS], in_=ynS[:, fc * 128:(fc + 1) * 128], identity=identb[0:NS, 0:NS]),
                     reads=[R("hT", 0, 8), R("identb")], writes=[PR(0)])
            P.op("act", lambda e: e.activation(out=ynTs, in_=PBF(0)[:, 0:256].rearrange("p (c s) -> p c s", c=16), func=AF.Copy), reads=[PR(0)], writes=[RCB])
            for t in range(4):
                wt, wr = load_w(w_ssd[:, :, t * 256:(t + 1) * 256], 16, 256, tid=('ssd', t))
                for j in range(2):
                    oc = 2 * t + j
                    for kc in range(16):
                        P.op("pe", lambda e, j=j, kc=kc, oc=oc, wt=wt: e.matmul(pb[1][:, oc * NS:(oc + 1) * NS], lhsT=wt[:, kc, j * 128:(j + 1) * 128], rhs=ynTs[:, kc, :], start=(kc == 0), stop=(kc == 15)),
                             reads=[wr, RCB], writes=[PR(1)])
            P.op("dve", lambda e: e.tensor_tensor(out=ptmp, in0=pb[1][:, 0:128].rearrange("p (c s) -> p c s", c=8), in1=gbS, op=ALU.mult), reads=[PR(1), RAM], writes=[RAM])
            P.op("dve", lambda e: e.tensor_tensor(out=mixTs, in0=ptmp, in1=amS, op=ALU.add), reads=[RAM], writes=[RCB])
            tmpR = ysS[:, 0:512]
            for t in range(2):
                wt, wr = load_w(w_out[:, :, t * 512:(t + 1) * 512], 8, 512, tid=('out', t))
                for kc in range(8):
                    P.op("pe", lambda e, kc=kc, wt=wt, t=t: e.matmul(pb[2 + t][0:NS, 0:512], lhsT=mixTs[:, kc, :], rhs=wt[:, kc, :], start=(kc == 0), stop=(kc == 7)), reads=[wr, RCB], writes=[PR(2 + t)])
                P.op("dve", lambda e, t=t: e.tensor_tensor(out=tmpR, in0=pb[2 + t][0:NS, 0:512], in1=gs1[:, t * 512:(t + 1) * 512], op=ALU.mult), reads=[PR(2 + t), R(SN, 0, 4)], writes=[R(SN, 0, 4)])
                P.op("dve", lambda e, t=t: e.tensor_tensor(out=xs_tok[:, t * 512:(t + 1) * 512], in0=xs_tok[:, t * 512:(t + 1) * 512], in1=tmpR, op=ALU.add), reads=[R(SN, 0, 4), R("yst0")], writes=[R("yst0")])
            s_norm_T(a2T, 24)
            for t in range(11):
                wt, wr = load_w(w_ffi[:, :, t * 512:(t + 1) * 512], 8, 512, tid=('ffi', t))
                for q in range(4):
                    for kc in range(8):
                        P.op("pe", lambda e, q=q, kc=kc, wt=wt: e.matmul(pb[1][:, q * NS:(q + 1) * NS], lhsT=wt[:, kc, q * 128:(q + 1) * 128], rhs=hTs[:, kc, :], start=(kc == 0), stop=(kc == 7)),
                             reads=[wr, RCB], writes=[PR(1)])
                P.op("act", lambda e: e.activation(out=sgS, in_=pb[1][:, 0:2 * NS], func=AF.Silu), reads=[PR(1)], writes=[RAM])
                P.op("dve", lambda e, t=t: e.tensor_tensor(out=fTs[:, 2 * t:2 * t + 2, :], in0=pb[1][:, 2 * NS:4 * NS].rearrange("p (c s) -> p c s", c=2), in1=sgS.rearrange("p (c s) -> p c s", c=2), op=ALU.mult),
                     reads=[PR(1), RAM], writes=[RCB])
            for half in range(2):
                for kg in range(3):
                    nk = 8 if kg < 2 else 6
                    wt, wr = load_w(w_ffo[:, kg * 8:kg * 8 + nk, half * 512:(half + 1) * 512], nk, 512, tid=('ffo', kg, half))
                    for kc in range(nk):
                        P.op("pe", lambda e, kc=kc, kg=kg, nk=nk, wt=wt, half=half: e.matmul(pb[2 + half][0:NS, 0:512], lhsT=fTs[:, kg * 8 + kc, :], rhs=wt[:, kc, :], start=(kg == 0 and kc == 0), stop=(kg == 2 and kc == nk - 1)),
                             reads=[wr, RCB], writes=[PR(2 + half)])
                P.op("dve", lambda e, half=half: e.tensor_tensor(out=tmpR, in0=pb[2 + half][0:NS, 0:512], in1=gs2[:, half * 512:(half + 1) * 512], op=ALU.mult), reads=[PR(2 + half), R(SN, 0, 4)], writes=[R(SN, 0, 4)])
                P.op("dve", lambda e, half=half: e.tensor_tensor(out=xs_tok[:, half * 512:(half + 1) * 512], in0=xs_tok[:, half * 512:(half + 1) * 512], in1=tmpR, op=ALU.add), reads=[R(SN, 0, 4), R("yst0")], writes=[R("yst0")])
            P.op("act", lambda e: e.activation(out=junkS[:, 0:D], in_=xs_tok, func=AF.Square, accum_out=ssS[:, 2:3]), reads=[R("yst0")], writes=[R("hT", 0, 8), R("ssq", 0, 8)])
            P.op("act", lambda e: e.activation(out=rsS[:, 6:7], in_=ssS[:, 2:3], func=AF.Ln, scale=1.0 / D, bias=epsc[0:NS, 0:1]), reads=[R("ssq", 0, 8), R("epsc")], writes=[R("rsq", 0, 8)])
            P.op("act", lambda e: e.activation(out=rsS[:, 2:3], in_=rsS[:, 6:7], func=AF.Exp, scale=-0.5), reads=[R("rsq", 0, 8)], writes=[R("rsq", 0, 8)])
            P.op("dve", lambda e: e.scalar_tensor_tensor(out=xs_tok, in0=xs_tok, scalar=rsS[:, 2:3], in1=rowb[0:NS, RB_FNW:RB_FNW + D], op0=ALU.mult, op1=ALU.mult),
                 reads=[R("yst0"), R("rsq", 0, 8), R("rowb")], writes=[R("yst0")])
            DMA("sp", o_ys, xs_tok, "st_ys", reads=[R("yst0")], writes=[R("o_ys")])

        P.op("sp", None, reads=[R("o_y", 0, 4 * NBLK), R("o_pp"), R("o_cp"), R("o_sp"), R("o_ys"), R("o_ps"), R("o_cs"), R("o_ss", 0, NS)])

        P.analyze()
        sems_e = {e: es.enter_context(nc.semaphore("se_" + e)) for e in ENGS}
        sems_d = {k: es.enter_context(nc.semaphore("sd_" + k)) for k in sorted(dma_keys)}
        P.emit(sems_e, sems_d)
    return nc


def _tile_k(w):
    K, N = w.shape
    return np.ascontiguousarray(w.reshape(K // 128, 128, N).transpose(1, 0, 2))


def _fm(v):
    return np.ascontiguousarray(v.reshape(-1, 128).T)


_NC_CACHE = {}


def kernel(x_prompt, x_sample, c_prompt, c_sample, state_pool, state_conv, state_ssm, w_ada, b_ada, norm1_w,
           w_in, w_pool, pool_scale, conv_w, conv_b, dt_bias, A_log, D_skip, ssd_norm_w, w_ssd_proj, w_out,
           norm2_w, w_ffn_in, w_ffn_out, final_norm_w):
    f = np.float32
    n = 8
    x_prompt = np.asarray(x_prompt, f)
    pvec = np.zeros((128, PV_N), f)
    pvec[:, PV_N1W:PV_N1W + 8] = _fm(np.asarray(norm1_w[0], f))
    pvec[:, PV_PSC:PV_PSC + 8] = _fm(np.asarray(pool_scale[0], f))
    cw = np.asarray(conv_w[0], f)
    for k in range(4):
        pvec[:, PV_CW + 32 * k:PV_CW + 32 * k + 32] = _fm(cw[k])
    pvec[:, PV_CB:PV_CB + 32] = _fm(np.asarray(conv_b[0], f))
    pvec[:, PV_N2W:PV_N2W + 8] = _fm(np.asarray(norm2_w[0], f))
    pvec[:, PV_BADA:PV_BADA + 48] = _fm(np.asarray(b_ada[0], f))
    pvec[:, PV_DF:PV_DF + 16] = _fm(np.repeat(np.asarray(D_skip[0], f), HP))
    rowb = np.zeros((128, RB_N), f)
    rowb[:, RB_FNW:RB_FNW + D] = np.asarray(final_norm_w, f)[None, :]
    bg = np.zeros((128, 2 * D), f)
    bg[:, 0:D] = np.asarray(b_ada[0], f)[None, 2 * D:3 * D]
    bg[:, D:2 * D] = np.asarray(b_ada[0], f)[None, 5 * D:6 * D]
    rowb[:, RB_DSK:RB_DSK + 32] = np.asarray(D_skip[0], f)[None, :]
    rowb[:, RB_ALOG:RB_ALOG + 32] = np.asarray(A_log[0], f)[None, :]
    rowb[:, RB_DTB:RB_DTB + 32] = np.asarray(dt_bias[0], f)[None, :]
    snw = np.ascontiguousarray(np.broadcast_to(np.asarray(ssd_norm_w[0], f)[None, :], (128, DI)))
    w_ada_t = _tile_k(np.asarray(w_ada[0], f))
    w_in_t = _tile_k(np.asarray(w_in[0], f))
    wp = np.asarray(w_pool[0], f)
    w_pool_t = np.ascontiguousarray(np.stack([_tile_k(wp[g]) for g in range(4)], axis=1))
    w_ssd_t = _tile_k(np.asarray(w_ssd_proj[0], f))
    w_out_t = _tile_k(np.asarray(w_out[0], f))
    wfi = np.asarray(w_ffn_in[0], f)
    perm = np.concatenate([np.concatenate([np.arange(256 * t, 256 * t + 256), DFF + np.arange(256 * t, 256 * t + 256)]) for t in range(11)])
    w_ffi_t = _tile_k(np.ascontiguousarray(wfi[:, perm]))
    w_ffo_t = _tile_k(np.asarray(w_ffn_out[0], f))

    in_maps = []
    for b in range(n):
        s0, s1 = NS * b, NS * (b + 1)
        c17 = np.concatenate([np.asarray(c_prompt[b:b + 1], f), np.asarray(c_sample[s0:s1], f)], axis=0)
        cT = np.ascontiguousarray(c17.T.reshape(8, 128, 17).transpose(1, 0, 2))
        sp = np.asarray(state_pool[0, s0:s1], f)
        sp_t = np.ascontiguousarray(sp.reshape(NS, 15, 8, 128).transpose(3, 2, 0, 1))
        sc = np.asarray(state_conv[0, s0:s1], f)
        sc_t = np.ascontiguousarray(sc.reshape(NS, 3, 32, 128).transpose(3, 2, 0, 1))
        ss = np.asarray(state_ssm[0, s0:s1], f)
        ss_t = np.ascontiguousarray(ss.reshape(NS, DI, DST).transpose(0, 2, 1))
        in_maps.append({
            "xp": np.ascontiguousarray(x_prompt[b]),
            "xsm": np.ascontiguousarray(np.asarray(x_sample[s0:s1, 0], f)),
            "cT": cT, "pvec": pvec, "rowb": rowb, "snw": snw, "bg": bg,
            "w_ada": w_ada_t, "w_in": w_in_t, "w_pool": w_pool_t, "w_ssd": w_ssd_t, "w_out": w_out_t,
            "w_ffi": w_ffi_t, "w_ffo": w_ffo_t,
            "st_pool": sp_t, "st_conv": sc_t, "st_ssm": ss_t,
        })
    nblk = int(os.environ.get("K_NBLK", NBLK))
    ncores = int(os.environ.get("K_CORES", n))
    if "nc" not in _NC_CACHE:
        _NC_CACHE["nc"] = build_nc(nblk=nblk)
    nc = _NC_CACHE["nc"]
    res = run_bass_kernel_spmd(nc, in_maps[:ncores], core_ids=list(range(ncores)))
    rs = list(res.results)
    while len(rs) < n:
        rs.append({k: np.zeros_like(v) for k, v in rs[0].items()})
    y_prompt = np.stack([rs[b]["o_y"] for b in range(n)], axis=0)
    y_sample = np.concatenate([rs[b]["o_ys"] for b in range(n)], axis=0)[:, None, :]
    pool_p = np.stack([rs[b]["o_pp"].transpose(2, 1, 0).reshape(15, D) for b in range(n)], axis=0)[None]
    conv_p = np.stack([rs[b]["o_cp"].transpose(2, 1, 0).reshape(3, CONV) for b in range(n)], axis=0)[None]
    ssm_p = np.stack([rs[b]["o_sp"].T.reshape(NH, HP, DST) for b in range(n)], axis=0)[None]
    pool_s = np.concatenate([rs[b]["o_ps"].transpose(2, 3, 1, 0).reshape(NS, 15, D) for b in range(n)], axis=0)[None]
    conv_s = np.concatenate([rs[b]["o_cs"].transpose(2, 3, 1, 0).reshape(NS, 3, CONV) for b in range(n)], axis=0)[None]
    ssm_s = np.concatenate([rs[b]["o_ss"].transpose(0, 2, 1).reshape(NS, NH, HP, DST) for b in range(n)], axis=0)[None]
    return (np.ascontiguousarray(y_prompt, dtype=f), np.ascontiguousarray(y_sample, dtype=f),
            np.ascontiguousarray(pool_p, dtype=f), np.ascontiguousarray(conv_p, dtype=f),
            np.ascontiguousarray(ssm_p, dtype=f), np.ascontiguousarray(pool_s, dtype=f),
            np.ascontiguousarray(conv_s, dtype=f), np.ascontiguousarray(ssm_s, dtype=f))
```

```python
import os
from contextlib import ExitStack

import numpy as np
import concourse.bass as bass
import concourse.mybir as mybir
from concourse.bass_utils import run_bass_kernel_spmd

F32 = mybir.dt.float32
BF16 = mybir.dt.bfloat16
AF = mybir.ActivationFunctionType
ALU = mybir.AluOpType

ENGS = ("pe", "act", "dve", "pool", "sp")

D = 1024
SEQ = 2048
TB = 512
NBLK = SEQ // TB
NS = 16
DI = 2048
NH = 32
HP = 64
NG = 8
DST = 128
CONV = 4096
DFF = 2816
IN_COLS = 9248
EPS = 1e-6

PV_N1W, PV_PSC, PV_CW, PV_CB, PV_N2W, PV_BADA, PV_DF, PV_N = 0, 8, 16, 144, 176, 184, 232, 248
RB_FNW, RB_DSK, RB_ALOG, RB_DTB, RB_N = 0, 1024, 1056, 1088, 1120


class Op:
    __slots__ = ("eng", "fn", "reads", "writes", "dkey", "dgroup", "idx", "waits",
                 "sig", "cnt", "eidx")

    def __init__(self, eng, fn, reads, writes, dkey=None, dgroup=None):
        self.eng = eng
        self.fn = fn
        self.reads = reads
        self.writes = writes
        self.dkey = dkey
        self.dgroup = dgroup
        self.waits = {}
        self.sig = False
        self.cnt = 0


class Prog:
    def __init__(self, nc):
        self.nc = nc
        self.ops = []

    def op(self, eng, fn, reads=(), writes=()):
        o = Op(eng, fn, list(reads), list(writes))
        o.idx = len(self.ops)
        self.ops.append(o)
        return o

    def dma(self, eng, fn, reads=(), writes=(), key=None, group=None):
        o = Op(eng, fn, list(reads), list(writes), dkey=key, dgroup=group)
        o.idx = len(self.ops)
        self.ops.append(o)
        return o

    def analyze(self):
        ops = self.ops
        ecount = {e: 0 for e in ENGS}
        for o in ops:
            o.eidx = ecount[o.eng]
            ecount[o.eng] += 1
        key_ops = {}
        for o in ops:
            if o.dkey is not None:
                key_ops.setdefault(o.dkey, []).append(o)
        dma_cum, dma_prev = {}, {}
        for k, lst in key_ops.items():
            groups = []
            for o in lst:
                if groups and o.dgroup is not None and groups[-1][0] == o.dgroup:
                    groups[-1][1].append(o)
                else:
                    groups.append((o.dgroup, [o]))
            cum = 0
            for g, gl in groups:
                prev = cum
                cum += len(gl)
                for o in gl:
                    dma_cum[o.idx] = cum
                    dma_prev[o.idx] = prev
        self.keys = sorted(key_ops.keys())
        recs = {}
        waited = {}
        pend = []
        for o in ops:
            d = set()
            for (buf, lo, hi) in o.reads:
                for r in recs.get(buf, ()):
                    if r[3] and r[0] < hi and lo < r[1]:
                        d.add(r[2])
            for (buf, lo, hi) in o.writes:
                for r in recs.get(buf, ()):
                    if r[0] < hi and lo < r[1]:
                        d.add(r[2])
            d.discard(o.idx)
            for (buf, lo, hi) in o.writes:
                lst = recs.setdefault(buf, [])
                lst[:] = [r for r in lst if not (lo <= r[0] and r[1] <= hi)]
                lst.append([lo, hi, o.idx, True])
            for (buf, lo, hi) in o.reads:
                lst = recs.setdefault(buf, [])
                lst[:] = [r for r in lst if not ((not r[3]) and ops[r[2]].eng == o.eng
                                                 and ops[r[2]].dkey is None and o.dkey is None
                                                 and lo <= r[0] and r[1] <= hi)]
                lst.append([lo, hi, o.idx, False])
            need = {}
            for di in d:
                p = ops[di]
                if p.dkey is not None:
                    sk = ("d", p.dkey)
                    val = 16 * dma_cum[p.idx]
                    if need.get(sk, 0) < val:
                        need[sk] = val
                    continue
                if p.eng == o.eng and o.dkey is None:
                    if o.eng in ("pe", "sp"):
                        continue
                sk = ("e", p.eng)
                cur = need.get(sk)
                if cur is None or cur.eidx < p.eidx:
                    need[sk] = p
            if o.dkey is not None and dma_prev[o.idx] > 0:
                sk = ("d", o.dkey)
                val = 16 * dma_prev[o.idx]
                if need.get(sk, 0) < val:
                    need[sk] = val
            o.waits = need
            for sk, v in need.items():
                if sk[0] == "e":
                    v.sig = True
        cnt = {e: 0 for e in ENGS}
        for o in ops:
            if o.dkey is None and o.sig:
                cnt[o.eng] += 1
            o.cnt = cnt[o.eng]
        for o in ops:
            final = {}
            for sk, v in o.waits.items():
                val = v.cnt if sk[0] == "e" else v
                wk = (o.eng, sk)
                if waited.get(wk, 0) >= val:
                    continue
                waited[wk] = val
                final[sk] = val
            o.waits = final

    def emit(self, sems_e, sems_d):
        nc = self.nc
        per = {e: [o for o in self.ops if o.eng == e] for e in ENGS}

        def run(engname, eng):
            for o in per[engname]:
                for sk, val in o.waits.items():
                    sem = sems_e[sk[1]] if sk[0] == "e" else sems_d[sk[1]]
                    eng.wait_ge(sem, val)
                if o.fn is None:
                    continue
                ins = o.fn(eng)
                if o.dkey is not None:
                    ins.then_inc(sems_d[o.dkey], 16)
                elif o.sig:
                    ins.then_inc(sems_e[o.eng], 1)

        with nc.Block() as block:
            @block.tensor
            def _(e):
                run("pe", e)

            @block.scalar
            def _(e):
                run("act", e)

            @block.vector
            def _(e):
                run("dve", e)

            @block.gpsimd
            def _(e):
                run("pool", e)

            @block.sync
            def _(e):
                run("sp", e)


def R(name, lo=0, hi=1):
    return (name, lo, hi)


def build_nc(nblk=NBLK, do_sample=True):
    nc = bass.Bass("TRN2", target_bir_lowering=False)

    def din(name, shape):
        return nc.dram_tensor(name, list(shape), F32, kind="ExternalInput").ap()

    def dout(name, shape):
        return nc.dram_tensor(name, list(shape), F32, kind="ExternalOutput").ap()

    xp = din("xp", [SEQ, D])
    xsm = din("xsm", [NS, D])
    cT = din("cT", [128, 8, 17])
    pvec = din("pvec", [128, PV_N])
    rowb_in = din("rowb", [128, RB_N])
    snw_in = din("snw", [128, DI])
    bg_in = din("bg", [128, 2 * D])
    w_ada = din("w_ada", [128, 8, 6 * D])
    w_in = din("w_in", [128, 8, IN_COLS])
    w_pool = din("w_pool", [128, 4, 2, 256])
    w_ssd = din("w_ssd", [128, 16, D])
    w_out = din("w_out", [128, 8, D])
    w_ffi = din("w_ffi", [128, 8, 2 * DFF])
    w_ffo = din("w_ffo", [128, 22, D])
    st_pool = din("st_pool", [128, 8, NS, 15])
    st_conv = din("st_conv", [128, 32, NS, 3])
    st_ssm = din("st_ssm", [NS, 128, DI])
    o_y = dout("o_y", [SEQ, D])
    o_ys = dout("o_ys", [NS, D])
    o_pp = dout("o_pp", [128, 8, 15])
    o_cp = dout("o_cp", [128, 32, 3])
    o_sp = dout("o_sp", [128, DI])
    o_ps = dout("o_ps", [128, 8, NS, 15])
    o_cs = dout("o_cs", [128, 32, NS, 3])
    o_ss = dout("o_ss", [NS, 128, DI])
    dbg_out = {}

    es = ExitStack()
    with es:
        def sb(name, shape, dt=F32):
            return es.enter_context(nc.sbuf_tensor("s_" + name, list(shape), dt))

        P = Prog(nc)
        dma_keys = set()

        def DMA(eng, out, in_, key, reads=(), writes=(), group=None):
            dma_keys.add(key)
            P.dma(eng, lambda e: e.dma_start(out=out, in_=in_), reads=reads, writes=writes, key=key, group=group)

        pb = [es.enter_context(nc.psum_tensor("pb%d" % i, [128, 512], F32)) for i in range(8)]

        def PBF(i):
            return pb[i][:].bitcast(BF16)

        def PR(i, lo=0, hi=512):
            return ("pb%d" % i, 0, 512)

        sA = sb("sA", [128, 16 + TB])
        sB = sb("sB", [128, 16 + TB])
        identf = sB[:, 0:128]
        triuf = sB[:, 128:256]
        nmf = sA[:, 0:512].rearrange("p (a b) -> p a b", a=4)
        identb = sb("identb", [128, 128], BF16)
        onesb = sb("onesb", [128, 128], BF16)
        triub = sb("triub", [128, 128], BF16)
        nmb = sb("nmb", [128, 4, 128], BF16)
        epsc = sb("epsc", [128, 1])
        invc = sb("invc", [128, 4, 16])
        pv = sb("pv", [128, PV_N])
        rowb = sb("rowb", [128, RB_N])
        snwb = sb("snwb", [128, DI], BF16)
        anegb = sb("anegb", [128, 32])
        cTs = sb("cTs", [128, 8, 17])
        silucT = sb("silucT", [128, 8, 17], BF16)
        modT = sb("modT", [128, 48, 17])
        a1T = sb("a1T", [128, 8, 17])
        a2T = sb("a2T", [128, 8, 17])
        g1B = sb("g1B", [128, D])
        g2B = sb("g2B", [128, D])
        wdt = sb("wdt", [128, 8, 32], BF16)
        diagD = sb("diagD", [128, 16, 128], BF16)
        NWB = 2
        wbuf = [sb("wbuf%d" % i, [128, 4096], BF16) for i in range(NWB)]

        P.op("pool", lambda e: e.memset(identf, 0.0), writes=[R("sB")])
        P.op("pool", lambda e: e.affine_select(out=identf, in_=identf, pattern=[[-1, 128]], compare_op=ALU.not_equal,
                                               fill=1.0, base=0, channel_multiplier=1), reads=[R("sB")], writes=[R("sB")])
        P.op("dve", lambda e: e.tensor_copy(out=identb[:], in_=identf), reads=[R("sB")], writes=[R("identb")])
        P.op("pool", lambda e: e.memset(onesb[:], 1.0), writes=[R("onesb")])
        P.op("pool", lambda e: e.memset(triuf, 1.0), writes=[R("sB")])
        P.op("pool", lambda e: e.affine_select(out=triuf, in_=triuf, pattern=[[1, 128]], compare_op=ALU.is_ge,
                                               fill=0.0, base=0, channel_multiplier=-1), reads=[R("sB")], writes=[R("sB")])
        P.op("dve", lambda e: e.tensor_copy(out=triub[:], in_=triuf), reads=[R("sB")], writes=[R("triub")])
        P.op("pool", lambda e: e.memset(nmf, 0.0), writes=[R("sA")])
        P.op("pool", lambda e: e.affine_select(out=nmf, in_=nmf, pattern=[[0, 4], [1, 128]], compare_op=ALU.is_ge,
                                               fill=-30000.0, base=0, channel_multiplier=-1), reads=[R("sA")], writes=[R("sA")])
        P.op("dve", lambda e: e.tensor_copy(out=nmb[:], in_=nmf), reads=[R("sA")], writes=[R("nmb")])
        P.op("pool", lambda e: e.memset(epsc[:], EPS), writes=[R("epsc")])
        for g in range(4):
            w = 2 ** (g + 1)
            P.op("pool", lambda e, g=g, w=w: e.memset(invc[:, g, :], 1.0 / w), writes=[R("invc")])
            for t in range(w - 1):
                P.op("pool", lambda e, g=g, t=t: e.memset(invc[:, g, t:t + 1], 1.0 / (t + 1)), writes=[R("invc")])

        DMA("sp", pv[:], pvec, "ld_pv", writes=[R("pv")])
        DMA("sp", rowb[:], rowb_in, "ld_rowb", writes=[R("rowb")])
        DMA("sp", cTs[:], cT, "ld_c", writes=[R("cTs")])
        DMA("sp", g1B[:], bg_in[:, 0:D], "ld_g1", writes=[R("g1B", 0, 2)])
        DMA("sp", g2B[:], bg_in[:, D:2 * D], "ld_g2", writes=[R("g2B", 0, 2)])
        DMA("pool", snwb[:], snw_in, "ld_snw", writes=[R("snwb")])
        DMA("pool", wdt[:], w_in[:, :, 7168:7200], "ld_wdt", writes=[R("wdt")])
        P.op("dve", lambda e: e.tensor_tensor(out=diagD[:], in0=identb[:].unsqueeze(1).to_broadcast([128, 16, 128]),
                                              in1=pv[:, PV_DF:PV_DF + 16].unsqueeze(2).to_broadcast([128, 16, 128]), op=ALU.mult),
             reads=[R("identb"), R("pv")], writes=[R("diagD")])

        P.op("act", lambda e: e.activation(out=anegb[:], in_=rowb[:, RB_ALOG:RB_ALOG + 32], func=AF.Exp), reads=[R("rowb")], writes=[R("anegb")])
        P.op("dve", lambda e: e.tensor_scalar(out=anegb[:], in0=anegb[:], scalar1=-1.0, scalar2=None, op0=ALU.mult), reads=[R("anegb")], writes=[R("anegb")])

        wstate = {"i": 0}

        NSCR = 42
        wsc = nc.dram_tensor("wsc", [NSCR, 128, 4096], BF16).ap()
        scr_idx = {}

        prefetched = {}
        wslots = [(wbuf[i][:], R("wbuf%d" % i)) for i in range(NWB)]

        def load_w(src_ap, nk, ncol, tid=None, prefetch=False):
            if tid is not None and not prefetch and tid in prefetched:
                return prefetched.pop(tid)
            res = _load_w(src_ap, nk, ncol, tid)
            if prefetch:
                prefetched[tid] = res
            return res

        def _load_w(src_ap, nk, ncol, tid=None):
            i = wstate["i"] % len(wslots)
            wstate["i"] += 1
            wt, wreg = wslots[i]
            view = wt[:, 0:nk * ncol].rearrange("p (k c) -> p k c", k=nk)
            if tid is None:
                DMA("pool", view, src_ap, "w%d" % i, writes=[wreg])
            elif tid not in scr_idx:
                k = len(scr_idx)
                scr_idx[tid] = k
                DMA("pool", view, src_ap, "w%d" % i, writes=[wreg])
                DMA("sp", wsc[k, :, 0:nk * ncol], wt[:, 0:nk * ncol], "ws%d" % i, reads=[wreg], writes=[R("wsc", k, k + 1)])
            else:
                k = scr_idx[tid]
                DMA("sp", wt[:, 0:nk * ncol], wsc[k, :, 0:nk * ncol], "wh%d" % i, reads=[R("wsc", k, k + 1)], writes=[wreg])
            return view, wreg

        rot = {"i": 0}

        def next_bank(banks=(0, 1, 2, 3)):
            b = banks[rot["i"] % len(banks)]
            rot["i"] += 1
            return b

        P.op("act", lambda e: e.activation(out=silucT[:], in_=cTs[:], func=AF.Silu), reads=[R("cTs")], writes=[R("silucT")])
        for t in range(12):
            wt, wr = load_w(w_ada[:, :, t * 512:(t + 1) * 512], 8, 512)
            bk = 4 + (t % 2)
            for j in range(4):
                for kc in range(8):
                    P.op("pe", lambda e, bk=bk, j=j, kc=kc, wt=wt: e.matmul(pb[bk][:, j * 17:(j + 1) * 17], lhsT=wt[:, kc, j * 128:(j + 1) * 128],
                                                                             rhs=silucT[:, kc, :], start=(kc == 0), stop=(kc == 7)),
                         reads=[wr, R("silucT")], writes=[PR(bk, j * 17, j * 17 + 17)])
            P.op("dve", lambda e, bk=bk, t=t: e.tensor_tensor(out=modT[:, 4 * t:4 * t + 4, :], in0=pb[bk][:, 0:68].rearrange("p (a b) -> p a b", a=4),
                                                               in1=pv[:, PV_BADA + 4 * t:PV_BADA + 4 * t + 4].unsqueeze(2).to_broadcast([128, 4, 17]), op=ALU.add),
                 reads=[PR(bk, 0, 68), R("pv")], writes=[R("modT", 4 * t, 4 * t + 4)])
            if t in (4, 5, 10, 11):
                gB = g1B if t < 6 else g2B
                gname = "g1B" if t < 6 else "g2B"
                half = t % 2
                for kc in range(8):
                    P.op("pe", lambda e, kc=kc, wt=wt: e.matmul(pb[6][:, 0:512], lhsT=silucT[:, kc, 0:1].to_broadcast([128, 128]), rhs=wt[:, kc, :],
                                                                  start=(kc == 0), stop=(kc == 7)),
                         reads=[wr, R("silucT")], writes=[PR(6)])
                P.op("dve", lambda e, gB=gB, half=half: e.tensor_tensor(out=gB[:, half * 512:(half + 1) * 512], in0=pb[6][:, 0:512],
                                                                         in1=gB[:, half * 512:(half + 1) * 512], op=ALU.add),
                     reads=[PR(6), R(gname, half, half + 1)], writes=[R(gname, half, half + 1)])
        for (aT, an, sc0, nw0) in ((a1T, "a1T", 8, PV_N1W), (a2T, "a2T", 32, PV_N2W)):
            P.op("dve", lambda e, aT=aT, sc0=sc0: e.tensor_scalar(out=aT[:], in0=modT[:, sc0:sc0 + 8, :], scalar1=1.0, scalar2=None, op0=ALU.add),
                 reads=[R("modT", sc0, sc0 + 8)], writes=[R(an)])
            P.op("dve", lambda e, aT=aT, nw0=nw0: e.tensor_tensor(out=aT[:], in0=aT[:], in1=pv[:, nw0:nw0 + 8].unsqueeze(2).to_broadcast([128, 8, 17]), op=ALU.mult),
                 reads=[R(an), R("pv")], writes=[R(an)])

        bufX = [sb("bx%d" % i, [128, 4, D]) for i in range(2)]
        ssq = sb("ssq", [128, 8])
        rsq = sb("rsq", [128, 8])
        hT = sb("hT", [128, 8, TB], BF16)
        ubuf = sb("ubuf", [128, 8, 16 + TB])
        gT = sb("gT", [128, 8, TB], BF16)
        gaT = gT
        gbT = gT
        uhist = sb("uhist", [128, 8, 15])
        amT = sb("amT", [128, 8, TB], BF16)
        xpre = [sb("xpre%d" % i, [128, 3 + TB], BF16) for i in range(2)]
        dg = [sb("dg%d" % i, [128, 4, 128], BF16) for i in range(2)]
        chist = sb("chist", [128, 32, 3], BF16)
        cst = sb("cst", [128, 32, 3])
        xbc = sb("xbc", [128, 32, TB], BF16)
        dtx = sb("dtx", [128, 4, 32])
        dtt = sb("dtt", [128, 4, 32])
        dtu = sb("dtu", [128, 4, 32])
        dt_tok = sb("dt_tok", [128, 4, 32])
        dA_bf = sb("dA_bf", [128, 4, 32], BF16)
        negAcs = sb("negAcs", [128, 32])
        Eacs = sb("Eacs", [128, 32])
        dte = sb("dte", [128, 32])
        cdB = sb("cdB", [128, 32])
        xdtb = [sb("xdt%d" % i, [128, DI], BF16) for i in range(2)]
        x_tok = xdtb[1]
        xdt = xdtb[0]
        xdte = sb("xdte", [128, DI], BF16)
        B_tok = sb("B_tok", [128, 1024], BF16)
        CBs = sb("CBs", [128, 8, 128], BF16)
        dec = [sb("dec%d" % i, [128, 8, 128], BF16) for i in range(2)]
        scr = dec
        ybuf = sb("ybuf", [128, DI])
        ytmp = hT[:].rearrange("p a b -> p (a b)").bitcast(F32)
        xn = ybuf[:].bitcast(BF16).rearrange("p (a b) -> p a b", a=4)
        junk = xdte

        def role_views(k):
            xt = bufX[k]
            szv = bufX[1 - k][:].rearrange("p a b -> p (a b)").bitcast(BF16).rearrange("p (a b) -> p a b", a=4)
            pl = bufX[1 - k][:].rearrange("p a b -> p (a b)").bitcast(BF16)[:, 0:8 * TB].rearrange("p (a b) -> p a b", a=8)
            return xt, "bx%d" % k, szv, "bx%d" % (1 - k), pl
        hst = sb("hst", [128, DI])
        hstb = sb("hstb", [128, DI], BF16)
        yst = [sb("yst%d" % i, [128, D]) for i in range(1)]
        ynT = ubuf[:].rearrange("p a b -> p (a b)").bitcast(BF16)[:, 0:16 * TB].rearrange("p (a b) -> p a b", a=16)
        fT = xbc[:].rearrange("p a b -> p (a b)")[:, 0:22 * TB].rearrange("p (a b) -> p a b", a=22)

        P.op("pool", lambda e: e.memset(chist[:], 0.0), writes=[R("chist", 0, 32)])
        P.op("pool", lambda e: e.memset(hst[:], 0.0), writes=[R("hst", 0, 8)])
        P.op("pool", lambda e: e.memset(hstb[:], 0.0), writes=[R("hstb", 0, 8)])

        def rmsnorm_to_T(blk_name, aT, bcol0, xtok, XN, ntok=128, ntile=4):
            for ti in range(ntile):
                P.op("act", lambda e, ti=ti: e.activation(out=junk[:, 0:D], in_=xtok[:, ti, :], func=AF.Square, accum_out=ssq[:, ti:ti + 1]),
                     reads=[R(XN, ti, ti + 1)], writes=[R("xdte", 0, 8), R("ssq", ti, ti + 1)])
                P.op("act", lambda e, ti=ti: e.activation(out=rsq[:, 4 + ti:5 + ti], in_=ssq[:, ti:ti + 1], func=AF.Ln, scale=1.0 / D, bias=epsc[:, 0:1]),
                     reads=[R("ssq", ti, ti + 1), R("epsc")], writes=[R("rsq", 4 + ti, 5 + ti)])
                P.op("act", lambda e, ti=ti: e.activation(out=rsq[:, ti:ti + 1], in_=rsq[:, 4 + ti:5 + ti], func=AF.Exp, scale=-0.5),
                     reads=[R("rsq", 4 + ti, 5 + ti)], writes=[R("rsq", ti, ti + 1)])
                P.op("dve", lambda e, ti=ti: e.tensor_scalar(out=xn[:, ti, :], in0=xtok[:, ti, :], scalar1=rsq[:, ti:ti + 1], scalar2=None, op0=ALU.mult),
                     reads=[R(XN, ti, ti + 1), R("rsq", ti, ti + 1)], writes=[R("ybuf", 2 * ti, 2 * ti + 2)])
            for kc in range(8):
                bk = kc // 2
                c0 = (kc % 2) * 512
                for ti in range(ntile):
                    P.op("pe", lambda e, bk=bk, c0=c0, ti=ti, kc=kc: e.transpose(out=PBF(bk)[:, c0 + ti * 128:c0 + (ti + 1) * 128],
                                                                                  in_=xn[:, ti, kc * 128:(kc + 1) * 128], identity=identb[:]),
                         reads=[R("ybuf", 2 * ti, 2 * ti + 2), R("identb")], writes=[PR(bk, c0 // 2 + ti * 64, c0 // 2 + (ti + 1) * 64)])
                P.op("dve", lambda e, bk=bk, c0=c0, kc=kc, aT=aT, bcol0=bcol0: e.tensor_scalar(
                    out=hT[:, kc, :], in0=PBF(bk)[:, c0:c0 + 512], scalar1=aT[:, kc, 0:1], scalar2=modT[:, bcol0 + kc, 0:1], op0=ALU.mult, op1=ALU.add),
                    reads=[PR(bk, c0 // 2, c0 // 2 + 256), R(blk_name), R("modT", bcol0 + kc, bcol0 + kc + 1)], writes=[R("hT", kc, kc + 1)])

        def proj_ws(wt, wr, j, nk, rhs_fn, rhs_regs, bank, ncols=TB, col0=0):
            for kc in range(nk):
                P.op("pe", lambda e, kc=kc: e.matmul(pb[bank][:, 0:ncols], lhsT=wt[:, kc, col0 + j * 128:col0 + (j + 1) * 128], rhs=rhs_fn(kc),
                                                      start=(kc == 0), stop=(kc == nk - 1)),
                     reads=[wr] + rhs_regs(kc), writes=[PR(bank, 0, ncols)])

        def proj_as(wt, wr, ti, nk, lhs_fn, lhs_regs, bank, kc0=0, first=True, last=True, nktot=None):
            for kc in range(nk):
                P.op("pe", lambda e, kc=kc: e.matmul(pb[bank][:, 0:512], lhsT=lhs_fn(kc0 + kc, ti), rhs=wt[:, kc, :],
                                                      start=(first and kc == 0), stop=(last and kc == nk - 1)),
                     reads=[wr] + lhs_regs(kc0 + kc), writes=[PR(bank)])

        hT_rhs = lambda kc: hT[:, kc, :]
        hT_regs = lambda kc: [R("hT", kc, kc + 1)]
        hT_lhs = lambda kc, ti: hT[:, kc, ti * 128:(ti + 1) * 128]

        def emit_block(tb, xtok, XN, sz, SN, pooled):
            if tb == 0:
                DMA("sp", xtok[:], xp[0:TB, :].rearrange("(t p) d -> p t d", p=128), "ld_x", writes=[R(XN, 0, 4)])
            rmsnorm_to_T("a1T", a1T, 0, xtok, XN)

            for t in range(2):
                wt, wr = load_w(w_in[:, :, 7200 + t * 512:7200 + (t + 1) * 512], 8, 512, tid=('in', 7200 + t * 512))
                for j in range(4):
                    oc = 4 * t + j
                    bk = next_bank()
                    proj_ws(wt, wr, j, 8, hT_rhs, hT_regs, bk)
                    P.op("act", lambda e, bk=bk, oc=oc: e.activation(out=gaT[:, oc, :], in_=pb[bk][:, 0:TB], func=AF.Sigmoid),
                         reads=[PR(bk)], writes=[R("gT", oc, oc + 1)])
            if tb == 0:
                P.op("pool", lambda e: e.memset(ubuf[:, :, 0:16], 0.0), writes=[R("ubuf", 0, 8)])
            else:
                P.op("pool", lambda e: e.tensor_copy(out=ubuf[:, :, 1:16], in_=uhist[:]), reads=[R("uhist")], writes=[R("ubuf", 0, 8)])
            for t in range(2):
                wt, wr = load_w(w_in[:, :, t * 512:(t + 1) * 512], 8, 512, tid=('in', t * 512))
                for j in range(4):
                    oc = 4 * t + j
                    bk = next_bank()
                    proj_ws(wt, wr, j, 8, hT_rhs, hT_regs, bk)
                    P.op("act", lambda e, bk=bk, oc=oc: e.activation(out=ubuf[:, oc, 16:16 + TB], in_=pb[bk][:, 0:TB], func=AF.Copy),
                         reads=[PR(bk)], writes=[R("ubuf", oc, oc + 1)])
            for oc in range(8):
                g = oc // 2
                w = 2 ** (g + 1)
                U = ubuf[:, oc, :]
                ur = R("ubuf", oc, oc + 1)
                P.op("dve", lambda e, U=U: e.tensor_tensor(out=sA[:, 2:528], in0=U[:, 2:528], in1=U[:, 1:527], op=ALU.add), reads=[ur], writes=[R("sA")])
                cur, curname = sA, "sA"
                if g >= 1:
                    P.op("dve", lambda e: e.tensor_tensor(out=sB[:, 4:528], in0=sA[:, 4:528], in1=sA[:, 2:526], op=ALU.add), reads=[R("sA")], writes=[R("sB")])
                    cur, curname = sB, "sB"
                if g >= 2:
                    P.op("dve", lambda e: e.tensor_tensor(out=sA[:, 8:528], in0=sB[:, 8:528], in1=sB[:, 4:524], op=ALU.add), reads=[R("sB")], writes=[R("sA")])
                    cur, curname = sA, "sA"
                if g >= 3:
                    P.op("dve", lambda e: e.tensor_tensor(out=sB[:, 16:528], in0=sA[:, 16:528], in1=sA[:, 8:520], op=ALU.add), reads=[R("sA")], writes=[R("sB")])
                    cur, curname = sB, "sB"
                P.op("dve", lambda e, cur=cur, U=U, w=w, oc=oc: e.scalar_tensor_tensor(out=pooled[:, oc, :], in0=cur[:, 16:528], scalar=1.0 / w, in1=U[:, 16:528],
                                                                                        op0=ALU.mult, op1=ALU.subtract),
                     reads=[R(curname), ur], writes=[R(SN, 0, 2)])
                if tb == 0:
                    P.op("dve", lambda e, cur=cur, g=g: e.tensor_tensor(out=cur[:, 0:16], in0=cur[:, 16:32], in1=invc[:, g, :], op=ALU.mult),
                         reads=[R(curname), R("invc")], writes=[R(curname)])
                    P.op("dve", lambda e, cur=cur, U=U, oc=oc: e.tensor_tensor(out=pooled[:, oc, 0:16], in0=cur[:, 0:16], in1=U[:, 16:32], op=ALU.subtract),
                         reads=[R(curname), ur], writes=[R(SN, 0, 2)])
            wplt, wplr = load_w(w_pool.rearrange("p g k c -> p (g k) c"), 8, 256, tid=('pool',))
            for g in range(4):
                for j in range(2):
                    oc = 2 * g + j
                    bk = next_bank()
                    for k2 in range(2):
                        P.op("pe", lambda e, bk=bk, g=g, j=j, k2=k2: e.matmul(pb[bk][:, 0:TB], lhsT=wplt[:, 2 * g + k2, j * 128:(j + 1) * 128], rhs=pooled[:, 2 * g + k2, :],
                                                                                start=(k2 == 0), stop=(k2 == 1)),
                             reads=[wplr, R(SN, 0, 2)], writes=[PR(bk)])
                    P.op("dve", lambda e, bk=bk, oc=oc: e.scalar_tensor_tensor(out=amT[:, oc, :], in0=pb[bk][:, 0:TB], scalar=pv[:, PV_PSC + oc:PV_PSC + oc + 1],
                                                                                in1=gaT[:, oc, :], op0=ALU.mult, op1=ALU.mult),
                         reads=[PR(bk), R("pv"), R("gT", oc, oc + 1)], writes=[R("amT", oc, oc + 1)])
            if tb == nblk - 1:
                DMA("sp", o_pp, ubuf[:, :, 513:528], "st_pp", reads=[R("ubuf", 0, 8)], writes=[R("o_pp")])
            else:
                P.op("pool", lambda e: e.tensor_copy(out=uhist[:], in_=ubuf[:, :, 513:528]), reads=[R("ubuf", 0, 8)], writes=[R("uhist")])

            for t in range(4):
                wt, wr = load_w(w_in[:, :, 1024 + t * 512:1024 + (t + 1) * 512], 8, 512, tid=('in', 1024 + t * 512))
                for ti in range(4):
                    bk = next_bank()
                    proj_as(wt, wr, ti, 8, hT_lhs, hT_regs, bk)
                    P.op("act", lambda e, bk=bk, ti=ti, t=t: e.activation(out=sz[:, ti, t * 512:(t + 1) * 512], in_=pb[bk][:, 0:512], func=AF.Silu),
                         reads=[PR(bk)], writes=[R(SN, ti, ti + 1)])
            for t in range(8):
                wt, wr = load_w(w_in[:, :, 3072 + t * 512:3072 + (t + 1) * 512], 8, 512, tid=('in', 3072 + t * 512))
                for j in range(4):
                    oc = 4 * t + j
                    sl = oc % 2
                    bk = next_bank()
                    proj_ws(wt, wr, j, 8, hT_rhs, hT_regs, bk)
                    xpr = R("xpre%d" % sl)
                    P.op("dve", lambda e, sl=sl, oc=oc: e.tensor_copy(out=xpre[sl][:, 0:3], in_=chist[:, oc, :]), reads=[R("chist", oc, oc + 1)], writes=[xpr])
                    P.op("act", lambda e, sl=sl, bk=bk: e.activation(out=xpre[sl][:, 3:3 + TB], in_=pb[bk][:, 0:TB], func=AF.Copy), reads=[PR(bk)], writes=[xpr])
                    if tb == nblk - 1:
                        P.op("dve", lambda e, bk=bk, oc=oc: e.tensor_copy(out=cst[:, oc, :], in_=pb[bk][:, TB - 3:TB]), reads=[PR(bk)], writes=[R("cst", oc, oc + 1)])
                    else:
                        P.op("dve", lambda e, sl=sl, oc=oc: e.tensor_copy(out=chist[:, oc, :], in_=xpre[sl][:, TB:TB + 3]), reads=[xpr], writes=[R("chist", oc, oc + 1)])
                    P.op("pool", lambda e, sl=sl, oc=oc: e.tensor_tensor(out=dg[sl][:], in0=identb[:].unsqueeze(1).to_broadcast([128, 4, 128]),
                                                                          in1=pv[:, PV_CW + oc:PV_CW + oc + 97:32].unsqueeze(2).to_broadcast([128, 4, 128]), op=ALU.mult),
                         reads=[R("identb"), R("pv")], writes=[R("dg%d" % sl, 0, 4)])
                    cbk = 4 + sl
                    for k in range(4):
                        P.op("pe", lambda e, sl=sl, k=k, cbk=cbk: e.matmul(pb[cbk][:, 0:TB], lhsT=dg[sl][:, k, :], rhs=xpre[sl][:, k:k + TB], start=(k == 0), stop=(k == 3)),
                             reads=[R("dg%d" % sl, k, k + 1), xpr], writes=[PR(cbk)])
                    P.op("act", lambda e, cbk=cbk, oc=oc: e.activation(out=xbc[:, oc, :], in_=pb[cbk][:, 0:TB], func=AF.Silu, bias=pv[:, PV_CB + oc:PV_CB + oc + 1], scale=1.0),
                         reads=[PR(cbk), R("pv")], writes=[R("xbc", oc, oc + 1)])
            if tb == nblk - 1:
                DMA("sp", o_cp, cst[:], "st_cp", reads=[R("cst", 0, 32)], writes=[R("o_cp")])
            for ti in range(4):
                for kc in range(8):
                    P.op("pe", lambda e, ti=ti, kc=kc: e.matmul(pb[6][:, ti * 32:(ti + 1) * 32], lhsT=hT[:, kc, ti * 128:(ti + 1) * 128], rhs=wdt[:, kc, :],
                                                                  start=(kc == 0), stop=(kc == 7)),
                         reads=[R("hT", kc, kc + 1), R("wdt")], writes=[PR(6, ti * 32, ti * 32 + 32)])
            P.op("dve", lambda e: e.tensor_tensor(out=dtx[:], in0=pb[6][:, 0:128].rearrange("p (a b) -> p a b", a=4),
                                                  in1=rowb[:, RB_DTB:RB_DTB + 32].unsqueeze(1).to_broadcast([128, 4, 32]), op=ALU.add),
                 reads=[PR(6, 0, 128), R("rowb")], writes=[R("dtx")])
            P.op("act", lambda e: e.activation(out=dtt[:], in_=dtx[:], func=AF.Abs), reads=[R("dtx")], writes=[R("dtt")])
            P.op("act", lambda e: e.activation(out=dtu[:], in_=dtt[:], func=AF.Exp, scale=-1.0), reads=[R("dtt")], writes=[R("dtu")])
            P.op("act", lambda e: e.activation(out=dtt[:], in_=dtu[:], func=AF.Ln, bias=1.0, scale=1.0), reads=[R("dtu")], writes=[R("dtt")])
            P.op("dve", lambda e: e.scalar_tensor_tensor(out=dt_tok[:], in0=dtx[:], scalar=0.0, in1=dtt[:], op0=ALU.max, op1=ALU.add),
                 reads=[R("dtx"), R("dtt")], writes=[R("dt_tok")])
            P.op("dve", lambda e: e.tensor_tensor(out=dA_bf[:], in0=dt_tok[:], in1=anegb[:].unsqueeze(1).to_broadcast([128, 4, 32]), op=ALU.mult),
                 reads=[R("dt_tok"), R("anegb")], writes=[R("dA_bf")])
            for t in range(2):
                wt, wr = load_w(w_in[:, :, 8224 + t * 512:8224 + (t + 1) * 512], 8, 512, tid=('in', 8224 + t * 512))
                for j in range(4):
                    oc = 4 * t + j
                    bk = next_bank()
                    proj_ws(wt, wr, j, 8, hT_rhs, hT_regs, bk)
                    P.op("act", lambda e, bk=bk, oc=oc: e.activation(out=gbT[:, oc, :], in_=pb[bk][:, 0:TB], func=AF.Sigmoid),
                         reads=[PR(bk)], writes=[R("gT", oc, oc + 1)])


            def ssd_acd(ci):
                tsl = slice(ci * 128, (ci + 1) * 128)
                dAc = dA_bf[:, ci, :]
                P.op("pe", lambda e: e.matmul(pb[7][:, 0:32], lhsT=triub[:], rhs=dAc, start=True, stop=True), reads=[R("triub"), R("dA_bf")], writes=[PR(7)])
                P.op("pe", lambda e: e.matmul(pb[7][:, 32:64], lhsT=onesb[:], rhs=dAc, start=True, stop=True), reads=[R("onesb"), R("dA_bf")], writes=[PR(7)])
                P.op("dve", lambda e: e.tensor_scalar(out=negAcs[:], in0=pb[7][:, 0:32], scalar1=-1.0, scalar2=None, op0=ALU.mult), reads=[PR(7)], writes=[R("negAcs")])
                P.op("act", lambda e: e.activation(out=Eacs[:], in_=pb[7][:, 0:32], func=AF.Exp), reads=[PR(7)], writes=[R("Eacs")])
                P.op("dve", lambda e: e.tensor_tensor(out=dte[:], in0=pb[7][:, 32:64], in1=negAcs[:], op=ALU.add), reads=[PR(7), R("negAcs")], writes=[R("dte")])
                P.op("act", lambda e: e.activation(out=dte[:], in_=dte[:], func=AF.Exp), reads=[R("dte")], writes=[R("dte")])
                P.op("act", lambda e: e.activation(out=cdB[:], in_=pb[7][:, 32:64], func=AF.Exp), reads=[PR(7)], writes=[R("cdB")])
                for g in range(NG):
                    P.op("pe", lambda e, g=g: e.transpose(out=PBF(2)[:, g * 128:(g + 1) * 128], in_=xbc[:, 16 + g, tsl], identity=identb[:]),
                         reads=[R("xbc", 16 + g, 17 + g), R("identb")], writes=[PR(2)])
                P.op("act", lambda e: e.activation(out=B_tok[:], in_=PBF(2)[:, 0:1024], func=AF.Copy), reads=[PR(2)], writes=[R("B_tok")])
                for g in range(NG):
                    bk = 3 + g // 4
                    c0 = (g % 4) * 128
                    P.op("pe", lambda e, g=g, bk=bk, c0=c0: e.matmul(pb[bk][:, c0:c0 + 128], lhsT=xbc[:, 16 + g, tsl], rhs=xbc[:, 24 + g, tsl], start=True, stop=True),
                         reads=[R("xbc", 16 + g, 17 + g), R("xbc", 24 + g, 25 + g)], writes=[PR(bk)])
                for q in range(2):
                    P.op("act", lambda e, q=q: e.activation(out=CBs[:, q * 4:(q + 1) * 4, :], in_=pb[3 + q][:, 0:512].rearrange("p (a b) -> p a b", a=4), func=AF.Copy),
                         reads=[PR(3 + q)], writes=[R("CBs", q * 4, q * 4 + 4)])

            def ssd_b(ci):
                tsl = slice(ci * 128, (ci + 1) * 128)
                xd = xdtb[ci % 2]
                xr = R("xdt%d" % (ci % 2))
                for fc in range(16):
                    bk = fc // 8
                    c0 = (fc % 8) * 128
                    P.op("pe", lambda e, bk=bk, c0=c0, fc=fc: e.transpose(out=PBF(bk)[:, c0:c0 + 128], in_=xbc[:, fc, tsl], identity=identb[:]),
                         reads=[R("xbc", fc, fc + 1), R("identb")], writes=[PR(bk)])
                for bk in range(2):
                    P.op("dve", lambda e, bk=bk: e.tensor_tensor(out=xd[:, bk * 1024:(bk + 1) * 1024].rearrange("p (h q) -> p h q", h=16),
                                                                 in0=PBF(bk)[:, 0:1024].rearrange("p (h q) -> p h q", h=16),
                                                                 in1=dt_tok[:, ci, bk * 16:(bk + 1) * 16].unsqueeze(2).to_broadcast([128, 16, HP]), op=ALU.mult),
                         reads=[PR(bk), R("dt_tok")], writes=[xr])
                P.op("dve", lambda e: e.tensor_tensor(out=xdte[:].rearrange("p (h q) -> p h q", h=NH), in0=xd[:].rearrange("p (h q) -> p h q", h=NH),
                                                      in1=dte[:].unsqueeze(2).to_broadcast([128, NH, HP]), op=ALU.mult),
                     reads=[xr, R("dte")], writes=[R("xdte", 0, 8)])

            def ssd_decay(ci, hq):
                sl = hq % 2
                dbanks = (5, 6) if hq % 2 == 0 else (3, 4)
                for half in range(2):
                    bk = dbanks[half]
                    for j in range(4):
                        h = hq * 8 + half * 4 + j
                        P.op("pe", lambda e, bk=bk, j=j: e.matmul(pb[bk][:, j * 128:(j + 1) * 128], lhsT=identb[:], rhs=nmb[:, 0, :], start=True, stop=False),
                             reads=[R("identb"), R("nmb")], writes=[PR(bk)])
                        P.op("pe", lambda e, bk=bk, j=j, h=h: e.matmul(pb[bk][:, j * 128:(j + 1) * 128], lhsT=dA_bf[:, ci, h:h + 1].to_broadcast([128, 128]),
                                                                        rhs=triub[:], start=False, stop=True),
                             reads=[R("dA_bf"), R("triub")], writes=[PR(bk)])
                    for j in range(4):
                        h = hq * 8 + half * 4 + j
                        P.op("act", lambda e, bk=bk, j=j, h=h, half=half: e.activation(out=dec[sl][:, half * 4 + j, :], in_=pb[bk][:, j * 128:(j + 1) * 128],
                                                                                       func=AF.Exp, bias=negAcs[:, h:h + 1], scale=1.0),
                             reads=[PR(bk), R("negAcs")], writes=[R("dec%d" % sl, half * 4 + j, half * 4 + j + 1)])

            def ssd_y(ci, hq):
                tsl = slice(ci * 128, (ci + 1) * 128)
                sl = hq % 2
                P.op("dve", lambda e: e.tensor_tensor(out=scr[sl][:].rearrange("p (g j) l -> p g j l", g=2), in0=dec[sl][:].rearrange("p (g j) l -> p g j l", g=2),
                                                      in1=CBs[:, 2 * hq:2 * hq + 2, :].unsqueeze(2).to_broadcast([128, 2, 4, 128]), op=ALU.mult),
                     reads=[R("dec%d" % sl, 0, 8), R("CBs", 2 * hq, 2 * hq + 2)], writes=[R("dec%d" % sl, 0, 8)])
                bA = (0, 2)[hq % 2]
                bB = (1, 7)[hq % 2]
                for jj in range(8):
                    h = 8 * hq + jj
                    P.op("pe", lambda e, jj=jj, h=h: e.matmul(pb[bA][:, jj * 64:(jj + 1) * 64], lhsT=xbc[:, h // 2, tsl], rhs=diagD[:, h // 2, (h % 2) * 64:(h % 2 + 1) * 64], start=True, stop=False),
                         reads=[R("xbc", h // 2, h // 2 + 1), R("diagD")], writes=[PR(bA)])
                    P.op("pe", lambda e, jj=jj, h=h: e.matmul(pb[bA][:, jj * 64:(jj + 1) * 64], lhsT=scr[sl][:, jj, :], rhs=xdtb[ci % 2][:, h * 64:(h + 1) * 64], start=False, stop=True),
                         reads=[R("dec%d" % sl, jj, jj + 1), R("xdt%d" % (ci % 2))], writes=[PR(bA)])
                for gg in range(2):
                    g = 2 * hq + gg
                    P.op("pe", lambda e, g=g, gg=gg: e.matmul(pb[bB][:, gg * 256:(gg + 1) * 256], lhsT=xbc[:, 24 + g, tsl], rhs=hstb[:, g * 256:(g + 1) * 256], start=True, stop=True),
                         reads=[R("xbc", 24 + g, 25 + g), R("hstb", g, g + 1)], writes=[PR(bB)])

            def ssd_ye(ci, hq):
                bA = (0, 2)[hq % 2]
                bB = (1, 7)[hq % 2]
                ysl = slice(hq * 512, (hq + 1) * 512)
                yr = R("ybuf", 2 * hq, 2 * hq + 2)
                P.op("dve", lambda e: e.tensor_tensor(out=ybuf[:, ysl].rearrange("p (h q) -> p h q", h=8), in0=pb[bB][:, 0:512].rearrange("p (h q) -> p h q", h=8),
                                                      in1=Eacs[:, 8 * hq:8 * hq + 8].unsqueeze(2).to_broadcast([128, 8, HP]), op=ALU.mult),
                     reads=[PR(bB), R("Eacs")], writes=[yr])
                P.op("dve", lambda e: e.tensor_tensor(out=ybuf[:, ysl], in0=pb[bA][:, 0:512], in1=ybuf[:, ysl], op=ALU.add), reads=[PR(bA), yr], writes=[yr])
                P.op("dve", lambda e: e.tensor_tensor(out=ybuf[:, ysl], in0=ybuf[:, ysl], in1=sz[:, ci, ysl], op=ALU.mult), reads=[yr, R(SN, ci, ci + 1)], writes=[yr])

            def ssd_g(ci):
                for q in range(4):
                    sbk = 3 + q
                    hsl = slice(q * 512, (q + 1) * 512)
                    hr = R("hst", 2 * q, 2 * q + 2)
                    for gg in range(2):
                        g = 2 * q + gg
                        P.op("pe", lambda e, g=g, gg=gg, sbk=sbk: e.matmul(pb[sbk][:, gg * 256:(gg + 1) * 256], lhsT=B_tok[:, g * 128:(g + 1) * 128], rhs=xdte[:, g * 256:(g + 1) * 256], start=True, stop=True),
                             reads=[R("B_tok"), R("xdte", g, g + 1)], writes=[PR(sbk)])
                    P.op("pool", lambda e, q=q, hsl=hsl: e.tensor_tensor(out=hst[:, hsl].rearrange("p (h q) -> p h q", h=8), in0=hst[:, hsl].rearrange("p (h q) -> p h q", h=8),
                                                                         in1=cdB[:, 8 * q:8 * q + 8].unsqueeze(2).to_broadcast([128, 8, HP]), op=ALU.mult),
                         reads=[hr, R("cdB")], writes=[hr])
                    P.op("dve", lambda e, sbk=sbk, hsl=hsl: e.tensor_tensor(out=hst[:, hsl], in0=pb[sbk][:, 0:512], in1=hst[:, hsl], op=ALU.add), reads=[PR(sbk), hr], writes=[hr])
                    P.op("act", lambda e, hsl=hsl: e.activation(out=hstb[:, hsl], in_=hst[:, hsl], func=AF.Copy), reads=[hr], writes=[R("hstb", 2 * q, 2 * q + 2)])

            def ssd_h(ci):
                tsl = slice(ci * 128, (ci + 1) * 128)
                ynb2 = xdtb[ci % 2]
                xr = R("xdt%d" % (ci % 2))
                P.op("act", lambda e: e.activation(out=ynb2[:], in_=ybuf[:], func=AF.Square, accum_out=ssq[:, 4:5]), reads=[R("ybuf", 0, 8)], writes=[xr, R("ssq", 4, 5)])
                P.op("act", lambda e: e.activation(out=ssq[:, 5:6], in_=ssq[:, 4:5], func=AF.Ln, scale=1.0 / DI, bias=epsc[:, 0:1]), reads=[R("ssq", 4, 5), R("epsc")], writes=[R("ssq", 5, 6)])
                P.op("act", lambda e: e.activation(out=ssq[:, 6:7], in_=ssq[:, 5:6], func=AF.Exp, scale=-0.5), reads=[R("ssq", 5, 6)], writes=[R("ssq", 6, 7)])
                P.op("dve", lambda e: e.scalar_tensor_tensor(out=ynb2[:], in0=ybuf[:], scalar=ssq[:, 6:7], in1=snwb[:], op0=ALU.mult, op1=ALU.mult),
                     reads=[R("ybuf", 0, 8), R("ssq", 6, 7), R("snwb")], writes=[xr])
                for fc in range(16):
                    bk = fc // 8
                    c0 = (fc % 8) * 128
                    P.op("pe", lambda e, bk=bk, c0=c0, fc=fc: e.transpose(out=PBF(bk)[:, c0:c0 + 128], in_=ynb2[:, fc * 128:(fc + 1) * 128], identity=identb[:]),
                         reads=[xr, R("identb")], writes=[PR(bk)])
                for q in range(2):
                    P.op("act", lambda e, q=q: e.activation(out=ynT[:, q * 8:(q + 1) * 8, tsl], in_=PBF(q)[:, 0:1024].rearrange("p (a b) -> p a b", a=8), func=AF.Copy),
                         reads=[PR(q)], writes=[R("ubuf", 0, 8)])

            ssd_acd(0)
            ssd_b(0)
            for ci in range(4):
                if ci == 0:
                    ssd_decay(ci, 0)
                ssd_decay(ci, 1)
                ssd_y(ci, 0)
                for hq in range(4):
                    if hq + 2 < 4:
                        ssd_decay(ci, hq + 2)
                    if hq + 1 < 4:
                        ssd_y(ci, hq + 1)
                    ssd_ye(ci, hq)
                ssd_g(ci)
                if ci + 1 < 4:
                    ssd_acd(ci + 1)
                    ssd_b(ci + 1)
                    ssd_decay(ci + 1, 0)
                ssd_h(ci)
            if tb == nblk - 1:
                DMA("sp", o_sp, hst[:], "st_sp", reads=[R("hst", 0, 8)], writes=[R("o_sp")])

            if tb + 1 < nblk:
                DMA("sp", bufX[(tb + 1) % 2][:], xp[(tb + 1) * TB:(tb + 2) * TB, :].rearrange("(t p) d -> p t d", p=128), "ld_x", writes=[R(SN, 0, 4)])
            for t in range(4):
                wt, wr = load_w(w_ssd[:, :, t * 256:(t + 1) * 256], 16, 256, tid=('ssd', t))
                for j in range(2):
                    oc = 2 * t + j
                    bk = next_bank()
                    proj_ws(wt, wr, j, 16, lambda kc: ynT[:, kc, :], lambda kc: [R("ubuf", 0, 8)], bk)
                    P.op("dve", lambda e, bk=bk, oc=oc: e.tensor_tensor(out=sA[:, 0:TB], in0=pb[bk][:, 0:TB], in1=gbT[:, oc, :], op=ALU.mult),
                         reads=[PR(bk), R("gT", oc, oc + 1)], writes=[R("sA")])
                    P.op("dve", lambda e, oc=oc: e.tensor_tensor(out=hT[:, oc, :], in0=sA[:, 0:TB], in1=amT[:, oc, :], op=ALU.add),
                         reads=[R("sA"), R("amT", oc, oc + 1)], writes=[R("hT", oc, oc + 1)])
            for t in range(2):
                wt, wr = load_w(w_out[:, :, t * 512:(t + 1) * 512], 8, 512, tid=('out', t))
                for ti in range(4):
                    bk = next_bank()
                    proj_as(wt, wr, ti, 8, hT_lhs, hT_regs, bk)
                    P.op("dve", lambda e, bk=bk, t=t: e.tensor_tensor(out=sB[:, 0:512], in0=pb[bk][:, 0:512], in1=g1B[:, t * 512:(t + 1) * 512], op=ALU.mult),
                         reads=[PR(bk), R("g1B", t, t + 1)], writes=[R("sB")])
                    P.op("dve", lambda e, ti=ti, t=t: e.tensor_tensor(out=xtok[:, ti, t * 512:(t + 1) * 512], in0=xtok[:, ti, t * 512:(t + 1) * 512], in1=sB[:, 0:512], op=ALU.add),
                         reads=[R("sB"), R(XN, ti, ti + 1)], writes=[R(XN, ti, ti + 1)])
            rmsnorm_to_T("a2T", a2T, 24, xtok, XN)
            for t in range(11):
                wt, wr = load_w(w_ffi[:, :, t * 512:(t + 1) * 512], 8, 512, tid=('ffi', t))
                for j in range(2):
                    fc = 2 * t + j
                    bg = next_bank()
                    proj_ws(wt, wr, j, 8, hT_rhs, hT_regs, bg)
                    bu = next_bank()
                    proj_ws(wt, wr, j, 8, hT_rhs, hT_regs, bu, col0=256)
                    P.op("act", lambda e, bg=bg: e.activation(out=sA[:, 0:TB], in_=pb[bg][:, 0:TB], func=AF.Silu), reads=[PR(bg)], writes=[R("sA")])
                    P.op("dve", lambda e, bu=bu, fc=fc: e.tensor_tensor(out=fT[:, fc, :], in0=pb[bu][:, 0:TB], in1=sA[:, 0:TB], op=ALU.mult),
                         reads=[PR(bu), R("sA")], writes=[R("xbc", fc, fc + 1)])
            fT_lhs = lambda kc, ti: fT[:, kc, ti * 128:(ti + 1) * 128]
            fT_regs = lambda kc: [R("xbc", kc, kc + 1)]
            for half in range(2):
                for kg in range(3):
                    nk = 8 if kg < 2 else 6
                    wt, wr = load_w(w_ffo[:, kg * 8:kg * 8 + nk, half * 512:(half + 1) * 512], nk, 512, tid=('ffo', kg, half))
                    for ti in range(4):
                        proj_as(wt, wr, ti, nk, fT_lhs, fT_regs, 4 + ti, kc0=kg * 8, first=(kg == 0), last=(kg == 2))
                for ti in range(4):
                    P.op("dve", lambda e, ti=ti, half=half: e.tensor_tensor(out=sB[:, 0:512], in0=pb[4 + ti][:, 0:512], in1=g2B[:, half * 512:(half + 1) * 512], op=ALU.mult),
                         reads=[PR(4 + ti), R("g2B", half, half + 1)], writes=[R("sB")])
                    P.op("dve", lambda e, ti=ti, half=half: e.tensor_tensor(out=xtok[:, ti, half * 512:(half + 1) * 512], in0=xtok[:, ti, half * 512:(half + 1) * 512], in1=sB[:, 0:512], op=ALU.add),
                         reads=[R("sB"), R(XN, ti, ti + 1)], writes=[R(XN, ti, ti + 1)])
            if tb + 1 < nblk:
                for t in range(2):
                    load_w(w_in[:, :, 7200 + t * 512:7200 + (t + 1) * 512], 8, 512, tid=('in', 7200 + t * 512), prefetch=True)
            elif do_sample:
                for c0 in (0, 512):
                    load_w(w_in[:, :, c0:c0 + 512], 8, 512, tid=('in', c0), prefetch=True)
            for ti in range(4):
                ys = 0
                P.op("act", lambda e, ti=ti: e.activation(out=junk[:, 0:D], in_=xtok[:, ti, :], func=AF.Square, accum_out=ssq[:, ti:ti + 1]),
                     reads=[R(XN, ti, ti + 1)], writes=[R("xdte", 0, 8), R("ssq", ti, ti + 1)])
                P.op("act", lambda e, ti=ti: e.activation(out=rsq[:, 4 + ti:5 + ti], in_=ssq[:, ti:ti + 1], func=AF.Ln, scale=1.0 / D, bias=epsc[:, 0:1]),
                     reads=[R("ssq", ti, ti + 1), R("epsc")], writes=[R("rsq", 4 + ti, 5 + ti)])
                P.op("act", lambda e, ti=ti: e.activation(out=rsq[:, ti:ti + 1], in_=rsq[:, 4 + ti:5 + ti], func=AF.Exp, scale=-0.5), reads=[R("rsq", 4 + ti, 5 + ti)], writes=[R("rsq", ti, ti + 1)])
                P.op("dve", lambda e, ti=ti, ys=ys: e.scalar_tensor_tensor(out=yst[ys][:], in0=xtok[:, ti, :], scalar=rsq[:, ti:ti + 1], in1=rowb[:, RB_FNW:RB_FNW + D], op0=ALU.mult, op1=ALU.mult),
                     reads=[R(XN, ti, ti + 1), R("rsq", ti, ti + 1), R("rowb")], writes=[R("yst%d" % ys)])
                r0 = tb * TB + ti * 128
                DMA("sp", o_y[r0:r0 + 128, :], yst[ys][:], "st_y%d" % ys, reads=[R("yst%d" % ys)], writes=[R("o_y", tb * 4 + ti, tb * 4 + ti + 1)])


        for tb in range(nblk):
            emit_block(tb, *role_views(tb % 2))
        xtok, XN, sz, SN, _pl = role_views((nblk - 1) % 2)

        if do_sample:
            Ssl = slice(1, 17)
            xbcF = xbc[:].rearrange("p a b -> p (a b)")
            wslots.append((xbcF[:, 4096:8192], R("xbc", 8, 16)))
            wslots.append((xbcF[:, 8192:12288], R("xbc", 16, 24)))
            stbuf = [xtok[:, 0:2, :].rearrange("p a b -> p (a b)"), xtok[:, 2:4, :].rearrange("p a b -> p (a b)")]
            streg = [R(XN, 0, 2), R(XN, 2, 4)]
            ubF = ubuf[:].rearrange("p a b -> p (a b)")
            stp = ubF[:, 0:1920].rearrange("p (c s r) -> p c s r", c=8, s=NS)
            newst = ubF[:, 1920:3840].rearrange("p (c s r) -> p c s r", c=8, s=NS)
            stc = ybuf[:, 0:1536].rearrange("p (c s r) -> p c s r", c=32, s=NS)
            newcst = hst[:, 0:1536].rearrange("p (c s r) -> p c s r", c=32, s=NS)
            gTf = gT[:].rearrange("p a b -> p (a b)").bitcast(F32)
            projS = gTf[:, 0:56 * NS].rearrange("p (c s) -> p c s", c=56)
            amF = amT[:].rearrange("p a b -> p (a b)").bitcast(F32)
            acc1 = amF[:, 0:512].rearrange("p (c s) -> p c s", c=32)
            acc2 = amF[:, 512:1024].rearrange("p (c s) -> p c s", c=32)
            amS = amF[:, 1024:1152].rearrange("p (c s) -> p c s", c=8)
            gaS = amF[:, 1152:1280].rearrange("p (c s) -> p c s", c=8)
            gbS = amF[:, 1280:1408].rearrange("p (c s) -> p c s", c=8)
            ptmp = amF[:, 1408:1536].rearrange("p (c s) -> p c s", c=8)
            sgS = amF[:, 1536:1568]
            xbcS = B_tok[:, 0:512].rearrange("p (c s) -> p c s", c=32)
            CBf = CBs[:].rearrange("p a b -> p (a b)")
            hTs = CBf[:, 0:128].rearrange("p (c s) -> p c s", c=8)
            mixTs = CBf[:, 128:256].rearrange("p (c s) -> p c s", c=8)
            pooledS = CBf[:, 256:384].rearrange("p (c s) -> p c s", c=8)
            ynTs = CBf[:, 384:640].rearrange("p (c s) -> p c s", c=16)
            fTs = CBf[:, 640:992].rearrange("p (c s) -> p c s", c=22)
            x_tokS = x_tok[0:NS, :]
            xdt_tokS = xdt[0:NS, :]
            szS = xdte[0:NS, :]
            xs_tok = yst[0][0:NS, :]
            szf = sz[:].rearrange("p a b -> p (a b)").bitcast(F32)
            ysS = szf[0:NS, 0:2048]
            gs1 = szf[0:NS, 2048:3072]
            gs2 = szf[0:NS, 3072:4096]
            xnS = dec[0][:].rearrange("p a b -> p (a b)")[0:NS, :]
            hTF = hT[:].rearrange("p a b -> p (a b)")
            ynS = hTF[0:NS, 0:2048]
            junkS = hTF[0:NS, 2048:4096]
            tmpDx = hTF[0:NS, :].bitcast(F32)
            decBs = sA[:, 0:512].rearrange("p (s h) -> p s h", s=NS)
            mask16 = sB[:, 0:256].rearrange("p (a b) -> p a b", a=NS)
            identfS = sB[:, 256:384]
            CmaskS = xbcF[:, 0:2048].rearrange("p (g s m) -> p g s m", g=NG, s=NS)
            ssS, rsS = ssq[0:NS, :], rsq[0:NS, :]
            RCB = R("CBs", 0, 8)
            RAM = R("amT", 0, 8)

            DMA("sp", xs_tok, xsm, "ld_xs", writes=[R("yst0")])
            DMA("sp", stp, st_pool, "ld_stp", writes=[R("ubuf", 0, 8)])
            DMA("sp", stc, st_conv, "ld_stc", writes=[R("ybuf", 0, 8)])
            P.op("pool", lambda e: e.memset(sB[:, 0:384], 0.0), writes=[R("sB")])
            P.op("pool", lambda e: e.affine_select(out=mask16, in_=mask16, pattern=[[1, NS], [-1, NS]], compare_op=ALU.not_equal, fill=1.0, base=0, channel_multiplier=0),
                 reads=[R("sB")], writes=[R("sB")])
            P.op("pool", lambda e: e.affine_select(out=identfS, in_=identfS, pattern=[[-1, 128]], compare_op=ALU.not_equal, fill=1.0, base=0, channel_multiplier=1),
                 reads=[R("sB")], writes=[R("sB")])
            for (gsv, c0, b0) in ((gs1, 16, 0), (gs2, 40, 2)):
                for c in range(8):
                    bk = b0 + c // 4
                    P.op("pe", lambda e, bk=bk, c=c, c0=c0: e.matmul(pb[bk][0:NS, (c % 4) * 128:(c % 4 + 1) * 128], lhsT=modT[:, c0 + c, Ssl], rhs=identfS, start=True, stop=True),
                         reads=[R("modT", c0 + c, c0 + c + 1), R("sB")], writes=[PR(bk)])
                for q in range(2):
                    P.op("act", lambda e, gsv=gsv, q=q, b0=b0: e.activation(out=gsv[:, q * 512:(q + 1) * 512], in_=pb[b0 + q][0:NS, 0:512], func=AF.Copy),
                         reads=[PR(b0 + q)], writes=[R(SN, 0, 4)])

            def s_norm_T(aT, bcol0):
                P.op("act", lambda e: e.activation(out=junkS[:, 0:D], in_=xs_tok, func=AF.Square, accum_out=ssS[:, 0:1]), reads=[R("yst0")], writes=[R("hT", 0, 8), R("ssq", 0, 8)])
                P.op("act", lambda e: e.activation(out=rsS[:, 4:5], in_=ssS[:, 0:1], func=AF.Ln, scale=1.0 / D, bias=epsc[0:NS, 0:1]), reads=[R("ssq", 0, 8), R("epsc")], writes=[R("rsq", 0, 8)])
                P.op("act", lambda e: e.activation(out=rsS[:, 0:1], in_=rsS[:, 4:5], func=AF.Exp, scale=-0.5), reads=[R("rsq", 0, 8)], writes=[R("rsq", 0, 8)])
                P.op("dve", lambda e: e.tensor_scalar(out=xnS, in0=xs_tok, scalar1=rsS[:, 0:1], scalar2=None, op0=ALU.mult), reads=[R("yst0"), R("rsq", 0, 8)], writes=[R("dec0", 0, 8)])
                for kc in range(8):
                    P.op("pe", lambda e, kc=kc: e.transpose(out=PBF(0)[:, kc * NS:(kc + 1) * NS], in_=xnS[:, kc * 128:(kc + 1) * 128], identity=identb[0:NS, 0:NS]),
                         reads=[R("dec0", 0, 8), R("identb")], writes=[PR(0)])
                P.op("dve", lambda e, aT=aT: e.tensor_tensor(out=ptmp, in0=PBF(0)[:, 0:128].rearrange("p (c s) -> p c s", c=8), in1=aT[:, :, Ssl], op=ALU.mult),
                     reads=[PR(0), R("a1T"), R("a2T")], writes=[RAM])
                P.op("dve", lambda e, bcol0=bcol0: e.tensor_tensor(out=hTs, in0=ptmp, in1=modT[:, bcol0:bcol0 + 8, Ssl], op=ALU.add),
                     reads=[RAM, R("modT", bcol0, bcol0 + 8)], writes=[RCB])

            s_norm_T(a1T, 0)
            ws_tiles = [(0, 0), (512, 4)] + [(3072 + 512 * t, 8 + 4 * t) for t in range(8)] + [(7200 + 512 * t, 40 + 4 * t) for t in range(4)]
            for wi, (col0, cb0) in enumerate(ws_tiles):
                wt, wr = load_w(w_in[:, :, col0:col0 + 512], 8, 512, tid=('in', col0))
                sbk = (1, 4, 5, 6)[wi % 4]
                for j in range(4):
                    for kc in range(8):
                        P.op("pe", lambda e, j=j, kc=kc, wt=wt, sbk=sbk: e.matmul(pb[sbk][:, j * NS:(j + 1) * NS], lhsT=wt[:, kc, j * 128:(j + 1) * 128], rhs=hTs[:, kc, :], start=(kc == 0), stop=(kc == 7)),
                             reads=[wr, RCB], writes=[PR(sbk)])
                P.op("act", lambda e, cb0=cb0, sbk=sbk: e.activation(out=projS[:, cb0:cb0 + 4, :], in_=pb[sbk][:, 0:4 * NS].rearrange("p (c s) -> p c s", c=4), func=AF.Copy),
                     reads=[PR(sbk)], writes=[R("gT", 0, 8)])
            for t in range(4):
                wt, wr = load_w(w_in[:, :, 1024 + t * 512:1024 + (t + 1) * 512], 8, 512, tid=('in', 1024 + t * 512))
                zbk = (2, 7)[t % 2]
                for kc in range(8):
                    P.op("pe", lambda e, kc=kc, wt=wt, zbk=zbk: e.matmul(pb[zbk][0:NS, 0:512], lhsT=hTs[:, kc, :], rhs=wt[:, kc, :], start=(kc == 0), stop=(kc == 7)),
                         reads=[wr, RCB], writes=[PR(zbk)])
                P.op("act", lambda e, t=t, zbk=zbk: e.activation(out=szS[:, t * 512:(t + 1) * 512], in_=pb[zbk][0:NS, 0:512], func=AF.Silu), reads=[PR(zbk)], writes=[R("xdte", 0, 8)])
            for kc in range(8):
                P.op("pe", lambda e, kc=kc: e.matmul(pb[3][0:NS, 0:32], lhsT=hTs[:, kc, :], rhs=wdt[:, kc, :], start=(kc == 0), stop=(kc == 7)), reads=[RCB, R("wdt")], writes=[PR(3)])
            d_x, d_t, d_u, d_dt, d_dec = dtx[0:NS, 0, :], dtt[0:NS, 0, :], dtu[0:NS, 0, :], dt_tok[0:NS, 0, :], dtx[0:NS, 1, :]
            P.op("dve", lambda e: e.tensor_tensor(out=d_x, in0=pb[3][0:NS, 0:32], in1=rowb[0:NS, RB_DTB:RB_DTB + 32], op=ALU.add), reads=[PR(3), R("rowb")], writes=[R("dtx")])
            P.op("act", lambda e: e.activation(out=d_t, in_=d_x, func=AF.Abs), reads=[R("dtx")], writes=[R("dtt")])
            P.op("act", lambda e: e.activation(out=d_u, in_=d_t, func=AF.Exp, scale=-1.0), reads=[R("dtt")], writes=[R("dtu")])
            P.op("act", lambda e: e.activation(out=d_t, in_=d_u, func=AF.Ln, bias=1.0, scale=1.0), reads=[R("dtu")], writes=[R("dtt")])
            P.op("dve", lambda e: e.scalar_tensor_tensor(out=d_dt, in0=d_x, scalar=0.0, in1=d_t, op0=ALU.max, op1=ALU.add), reads=[R("dtx"), R("dtt")], writes=[R("dt_tok")])
            P.op("dve", lambda e: e.tensor_tensor(out=d_u, in0=d_dt, in1=anegb[0:NS, :], op=ALU.mult), reads=[R("dt_tok"), R("anegb")], writes=[R("dtu")])
            P.op("act", lambda e: e.activation(out=d_dec, in_=d_u, func=AF.Exp), reads=[R("dtu")], writes=[R("dtx")])
            for s_ in range(NS):
                P.op("pe", lambda e, s_=s_: e.matmul(pb[0][:, s_ * 32:(s_ + 1) * 32], lhsT=identfS[0:NS, s_:s_ + 1].to_broadcast([NS, 128]), rhs=d_dec, start=True, stop=True),
                     reads=[R("sB"), R("dtx")], writes=[PR(0)])
            P.op("act", lambda e: e.activation(out=sA[:, 0:512], in_=pb[0][:, 0:512], func=AF.Copy), reads=[PR(0)], writes=[R("sA")])
            for g in range(4):
                w = 2 ** (g + 1)
                ug = projS[:, 2 * g:2 * g + 2, :]
                P.op("dve", lambda e, g=g, w=w: e.reduce_sum(out=ptmp[:, 0:2, :], in_=stp[:, 2 * g:2 * g + 2, :, 15 - (w - 1):15], axis=mybir.AxisListType.X),
                     reads=[R("ubuf", 0, 8)], writes=[RAM])
                P.op("dve", lambda e, ug=ug: e.tensor_tensor(out=ptmp[:, 0:2, :], in0=ptmp[:, 0:2, :], in1=ug, op=ALU.add), reads=[RAM, R("gT", 0, 8)], writes=[RAM])
                P.op("dve", lambda e, ug=ug, g=g, w=w: e.scalar_tensor_tensor(out=pooledS[:, 2 * g:2 * g + 2, :], in0=ptmp[:, 0:2, :], scalar=1.0 / w, in1=ug, op0=ALU.mult, op1=ALU.subtract),
                     reads=[RAM, R("gT", 0, 8)], writes=[RCB])
            P.op("act", lambda e: e.activation(out=gaS, in_=projS[:, 40:48, :], func=AF.Sigmoid), reads=[R("gT", 0, 8)], writes=[RAM])
            P.op("act", lambda e: e.activation(out=gbS, in_=projS[:, 48:56, :], func=AF.Sigmoid), reads=[R("gT", 0, 8)], writes=[RAM])
            wplt, wplr = load_w(w_pool.rearrange("p g k c -> p (g k) c"), 8, 256, tid=('pool',))
            for g in range(4):
                for j in range(2):
                    oc = 2 * g + j
                    for k2 in range(2):
                        P.op("pe", lambda e, g=g, j=j, k2=k2, oc=oc: e.matmul(pb[2][:, oc * NS:(oc + 1) * NS], lhsT=wplt[:, 2 * g + k2, j * 128:(j + 1) * 128], rhs=pooledS[:, 2 * g + k2, :], start=(k2 == 0), stop=(k2 == 1)),
                             reads=[wplr, RCB], writes=[PR(2)])
            for oc in range(8):
                P.op("dve", lambda e, oc=oc: e.scalar_tensor_tensor(out=amS[:, oc, :], in0=pb[2][:, oc * NS:(oc + 1) * NS], scalar=pv[:, PV_PSC + oc:PV_PSC + oc + 1], in1=gaS[:, oc, :], op0=ALU.mult, op1=ALU.mult),
                     reads=[PR(2), R("pv"), RAM], writes=[RAM])
            P.op("pool", lambda e: e.tensor_copy(out=newst[:, :, :, 0:14], in_=stp[:, :, :, 1:15]), reads=[R("ubuf", 0, 8)], writes=[R("ubuf", 0, 8)])
            P.op("pool", lambda e: e.tensor_copy(out=newst[:, :, :, 14], in_=projS[:, 0:8, :]), reads=[R("gT", 0, 8)], writes=[R("ubuf", 0, 8)])
            DMA("sp", o_ps, newst, "st_ps", reads=[R("ubuf", 0, 8)], writes=[R("o_ps")])
            xnew = projS[:, 8:40, :]
            cwb = lambda k: pv[:, PV_CW + 32 * k:PV_CW + 32 * k + 32].unsqueeze(2).to_broadcast([128, 32, NS])
            P.op("dve", lambda e: e.tensor_tensor(out=acc1, in0=xnew, in1=cwb(3), op=ALU.mult), reads=[R("gT", 0, 8), R("pv")], writes=[RAM])
            for k in range(3):
                P.op("dve", lambda e, k=k: e.tensor_tensor(out=acc2, in0=stc[:, :, :, k], in1=cwb(k), op=ALU.mult), reads=[R("ybuf", 0, 8), R("pv")], writes=[RAM])
                P.op("dve", lambda e: e.tensor_tensor(out=acc1, in0=acc1, in1=acc2, op=ALU.add), reads=[RAM], writes=[RAM])
            P.op("dve", lambda e: e.tensor_tensor(out=acc1, in0=acc1, in1=pv[:, PV_CB:PV_CB + 32].unsqueeze(2).to_broadcast([128, 32, NS]), op=ALU.add), reads=[RAM, R("pv")], writes=[RAM])
            P.op("act", lambda e: e.activation(out=xbcS, in_=acc1, func=AF.Silu), reads=[RAM], writes=[R("B_tok")])
            P.op("pool", lambda e: e.tensor_copy(out=newcst[:, :, :, 0:2], in_=stc[:, :, :, 1:3]), reads=[R("ybuf", 0, 8)], writes=[R("hst", 0, 8)])
            P.op("pool", lambda e: e.tensor_copy(out=newcst[:, :, :, 2], in_=xnew), reads=[R("gT", 0, 8)], writes=[R("hst", 0, 8)])
            DMA("sp", o_cs, newcst, "st_cs", reads=[R("hst", 0, 8)], writes=[R("o_cs")])
            for fc in range(16):
                bk = 1 + fc // 8
                P.op("pe", lambda e, fc=fc, bk=bk: e.transpose(out=PBF(bk)[0:NS, (fc % 8) * 128:(fc % 8 + 1) * 128], in_=xbcS[:, fc, :], identity=identb[:]),
                     reads=[R("B_tok"), R("identb")], writes=[PR(bk)])
            for q in range(2):
                P.op("act", lambda e, q=q: e.activation(out=x_tokS[:, q * 1024:(q + 1) * 1024], in_=PBF(1 + q)[0:NS, 0:1024], func=AF.Copy), reads=[PR(1 + q)], writes=[R("xdt1")])
            P.op("dve", lambda e: e.tensor_tensor(out=xdt_tokS.rearrange("p (h q) -> p h q", h=NH), in0=x_tokS.rearrange("p (h q) -> p h q", h=NH),
                                                  in1=d_dt.unsqueeze(2).to_broadcast([NS, NH, HP]), op=ALU.mult), reads=[R("xdt1"), R("dt_tok")], writes=[R("xdt0")])
            P.op("dve", lambda e: e.tensor_tensor(out=CmaskS, in0=xbcS[:, 24:32, :].unsqueeze(3).to_broadcast([128, NG, NS, NS]),
                                                  in1=mask16.unsqueeze(1).to_broadcast([128, NG, NS, NS]), op=ALU.mult), reads=[R("B_tok"), R("sB")], writes=[R("xbc", 0, 4)])
            P.op("pool", lambda e: e.memset(ysS, 0.0), writes=[R(SN, 0, 4)])
            def samp_L(s_):
                sl = s_ % 2
                DMA("sp", stbuf[sl], st_ssm[s_], "ld_st%d" % sl, writes=[streg[sl]])

            def samp_A(s_):
                sl = s_ % 2
                buf = stbuf[sl]
                for q in range(4):
                    P.op("pe", lambda e, q=q: e.matmul(pb[q][:, 0:512], lhsT=identb[0:NS, s_:s_ + 1].to_broadcast([NS, 128]), rhs=xdt_tokS[:, q * 512:(q + 1) * 512], start=True, stop=True),
                         reads=[R("identb"), R("xdt0")], writes=[PR(q)])
                P.op("pool", lambda e: e.tensor_tensor(out=buf.rearrange("p (h q) -> p h q", h=NH), in0=buf.rearrange("p (h q) -> p h q", h=NH),
                                                       in1=decBs[:, s_, :].unsqueeze(2).to_broadcast([128, NH, HP]), op=ALU.mult), reads=[streg[sl], R("sA")], writes=[streg[sl]])
                for g in range(NG):
                    P.op("dve", lambda e, g=g: e.scalar_tensor_tensor(out=buf[:, g * 256:(g + 1) * 256], in0=pb[g // 2][:, (g % 2) * 256:(g % 2 + 1) * 256],
                                                                       scalar=xbcS[:, 16 + g, s_:s_ + 1], in1=buf[:, g * 256:(g + 1) * 256], op0=ALU.mult, op1=ALU.add),
                         reads=[PR(g // 2), R("B_tok"), streg[sl]], writes=[streg[sl]])
                P.op("act", lambda e: e.activation(out=hstb[:], in_=buf, func=AF.Copy), reads=[streg[sl]], writes=[R("hstb", 0, 8)])
                DMA("sp", o_ss[s_], buf, "st_ss%d" % sl, reads=[streg[sl]], writes=[R("o_ss", s_, s_ + 1)])

            def samp_A2(s_):
                for g in range(NG):
                    P.op("pe", lambda e, g=g: e.matmul(pb[4 + g // 2][0:NS, (g % 2) * 256:(g % 2 + 1) * 256], lhsT=CmaskS[:, g, s_, :], rhs=hstb[:, g * 256:(g + 1) * 256], start=True, stop=True),
                         reads=[R("xbc", 0, 4), R("hstb", 0, 8)], writes=[PR(4 + g // 2)])

            def samp_B(s_):
                for q in range(4):
                    P.op("dve", lambda e, q=q: e.tensor_tensor(out=ysS[:, q * 512:(q + 1) * 512], in0=pb[4 + q][0:NS, 0:512], in1=ysS[:, q * 512:(q + 1) * 512], op=ALU.add),
                         reads=[PR(4 + q), R(SN, 0, 4)], writes=[R(SN, 0, 4)])

            samp_L(0)
            samp_L(1)
            samp_A(0)
            samp_A2(0)
            for s_ in range(NS):
                if s_ + 2 < NS:
                    samp_L(s_ + 2)
                if s_ + 1 < NS:
                    samp_A(s_ + 1)
                samp_B(s_)
                if s_ + 1 < NS:
                    samp_A2(s_ + 1)
            P.op("dve", lambda e: e.tensor_tensor(out=tmpDx.rearrange("p (h q) -> p h q", h=NH), in0=x_tokS.rearrange("p (h q) -> p h q", h=NH),
                                                  in1=rowb[0:NS, RB_DSK:RB_DSK + 32].unsqueeze(2).to_broadcast([NS, NH, HP]), op=ALU.mult), reads=[R("xdt1"), R("rowb")], writes=[R("hT", 0, 8)])
            P.op("dve", lambda e: e.tensor_tensor(out=ysS, in0=ysS, in1=tmpDx, op=ALU.add), reads=[R(SN, 0, 4), R("hT", 0, 8)], writes=[R(SN, 0, 4)])
            P.op("dve", lambda e: e.tensor_tensor(out=ysS, in0=ysS, in1=szS, op=ALU.mult), reads=[R(SN, 0, 4), R("xdte", 0, 8)], writes=[R(SN, 0, 4)])
            P.op("act", lambda e: e.activation(out=junkS, in_=ysS, func=AF.Square, accum_out=ssS[:, 1:2]), reads=[R(SN, 0, 4)], writes=[R("hT", 0, 8), R("ssq", 0, 8)])
            P.op("act", lambda e: e.activation(out=rsS[:, 5:6], in_=ssS[:, 1:2], func=AF.Ln, scale=1.0 / DI, bias=epsc[0:NS, 0:1]), reads=[R("ssq", 0, 8), R("epsc")], writes=[R("rsq", 0, 8)])
            P.op("act", lambda e: e.activation(out=rsS[:, 1:2], in_=rsS[:, 5:6], func=AF.Exp, scale=-0.5), reads=[R("rsq", 0, 8)], writes=[R("rsq", 0, 8)])
            P.op("dve", lambda e: e.scalar_tensor_tensor(out=ynS, in0=ysS, scalar=rsS[:, 1:2], in1=snwb[0:NS, :], op0=ALU.mult, op1=ALU.mult),
                 reads=[R(SN, 0, 4), R("rsq", 0, 8), R("snwb")], writes=[R("hT", 0, 8)])
            for fc in range(16):
                P.op("pe", lambda e, fc=fc: e.transpose(out=PBF(0)[:, fc * NS:(fc + 1) * NS], in_=ynS[:, fc * 128:(fc + 1) * 128], identity=identb[0:NS, 0:NS]),
                     reads=[R("hT", 0, 8), R("identb")], writes=[PR(0)])
            P.op("act", lambda e: e.activation(out=ynTs, in_=PBF(0)[:, 0:256].rearrange("p (c s) -> p c s", c=16), func=AF.Copy), reads=[PR(0)], writes=[RCB])
            for t in range(4):
                wt, wr = load_w(w_ssd[:, :, t * 256:(t + 1) * 256], 16, 256, tid=('ssd', t))
                for j in range(2):
                    oc = 2 * t + j
                    for kc in range(16):
                        P.op("pe", lambda e, j=j, kc=kc, oc=oc, wt=wt: e.matmul(pb[1][:, oc * NS:(oc + 1) * NS], lhsT=wt[:, kc, j * 128:(j + 1) * 128], rhs=ynTs[:, kc, :], start=(kc == 0), stop=(kc == 15)),
                             reads=[wr, RCB], writes=[PR(1)])
            P.op("dve", lambda e: e.tensor_tensor(out=ptmp, in0=pb[1][:, 0:128].rearrange("p (c s) -> p c s", c=8), in1=gbS, op=ALU.mult), reads=[PR(1), RAM], writes=[RAM])
            P.op("dve", lambda e: e.tensor_tensor(out=mixTs, in0=ptmp, in1=amS, op=ALU.add), reads=[RAM], writes=[RCB])
            tmpR = ysS[:, 0:512]
            for t in range(2):
                wt, wr = load_w(w_out[:, :, t * 512:(t + 1) * 512], 8, 512, tid=('out', t))
                for kc in range(8):
                    P.op("pe", lambda e, kc=kc, wt=wt, t=t: e.matmul(pb[2 + t][0:NS, 0:512], lhsT=mixTs[:, kc, :], rhs=wt[:, kc, :], start=(kc == 0), stop=(kc == 7)), reads=[wr, RCB], writes=[PR(2 + t)])
                P.op("dve", lambda e, t=t: e.tensor_tensor(out=tmpR, in0=pb[2 + t][0:NS, 0:512], in1=gs1[:, t * 512:(t + 1) * 512], op=ALU.mult), reads=[PR(2 + t), R(SN, 0, 4)], writes=[R(SN, 0, 4)])
                P.op("dve", lambda e, t=t: e.tensor_tensor(out=xs_tok[:, t * 512:(t + 1) * 512], in0=xs_tok[:, t * 512:(t + 1) * 512], in1=tmpR, op=ALU.add), reads=[R(SN, 0, 4), R("yst0")], writes=[R("yst0")])
            s_norm_T(a2T, 24)
            for t in range(11):
                wt, wr = load_w(w_ffi[:, :, t * 512:(t + 1) * 512], 8, 512, tid=('ffi', t))
                fbk = (1, 4, 5, 6)[t % 4]
                for q in range(4):
                    for kc in range(8):
                        P.op("pe", lambda e, q=q, kc=kc, wt=wt, fbk=fbk: e.matmul(pb[fbk][:, q * NS:(q + 1) * NS], lhsT=wt[:, kc, q * 128:(q + 1) * 128], rhs=hTs[:, kc, :], start=(kc == 0), stop=(kc == 7)),
                             reads=[wr, RCB], writes=[PR(fbk)])
                P.op("act", lambda e, fbk=fbk: e.activation(out=sgS, in_=pb[fbk][:, 0:2 * NS], func=AF.Silu), reads=[PR(fbk)], writes=[RAM])
                P.op("dve", lambda e, t=t, fbk=fbk: e.tensor_tensor(out=fTs[:, 2 * t:2 * t + 2, :], in0=pb[fbk][:, 2 * NS:4 * NS].rearrange("p (c s) -> p c s", c=2), in1=sgS.rearrange("p (c s) -> p c s", c=2), op=ALU.mult),
                     reads=[PR(fbk), RAM], writes=[RCB])
            for half in range(2):
                for kg in range(3):
                    nk = 8 if kg < 2 else 6
                    wt, wr = load_w(w_ffo[:, kg * 8:kg * 8 + nk, half * 512:(half + 1) * 512], nk, 512, tid=('ffo', kg, half))
                    for kc in range(nk):
                        P.op("pe", lambda e, kc=kc, kg=kg, nk=nk, wt=wt, half=half: e.matmul(pb[2 + half][0:NS, 0:512], lhsT=fTs[:, kg * 8 + kc, :], rhs=wt[:, kc, :], start=(kg == 0 and kc == 0), stop=(kg == 2 and kc == nk - 1)),
                             reads=[wr, RCB], writes=[PR(2 + half)])
                P.op("dve", lambda e, half=half: e.tensor_tensor(out=tmpR, in0=pb[2 + half][0:NS, 0:512], in1=gs2[:, half * 512:(half + 1) * 512], op=ALU.mult), reads=[PR(2 + half), R(SN, 0, 4)], writes=[R(SN, 0, 4)])
                P.op("dve", lambda e, half=half: e.tensor_tensor(out=xs_tok[:, half * 512:(half + 1) * 512], in0=xs_tok[:, half * 512:(half + 1) * 512], in1=tmpR, op=ALU.add), reads=[R(SN, 0, 4), R("yst0")], writes=[R("yst0")])
            P.op("act", lambda e: e.activation(out=junkS[:, 0:D], in_=xs_tok, func=AF.Square, accum_out=ssS[:, 2:3]), reads=[R("yst0")], writes=[R("hT", 0, 8), R("ssq", 0, 8)])
            P.op("act", lambda e: e.activation(out=rsS[:, 6:7], in_=ssS[:, 2:3], func=AF.Ln, scale=1.0 / D, bias=epsc[0:NS, 0:1]), reads=[R("ssq", 0, 8), R("epsc")], writes=[R("rsq", 0, 8)])
            P.op("act", lambda e: e.activation(out=rsS[:, 2:3], in_=rsS[:, 6:7], func=AF.Exp, scale=-0.5), reads=[R("rsq", 0, 8)], writes=[R("rsq", 0, 8)])
            P.op("dve", lambda e: e.scalar_tensor_tensor(out=xs_tok, in0=xs_tok, scalar=rsS[:, 2:3], in1=rowb[0:NS, RB_FNW:RB_FNW + D], op0=ALU.mult, op1=ALU.mult),
                 reads=[R("yst0"), R("rsq", 0, 8), R("rowb")], writes=[R("yst0")])
            DMA("sp", o_ys, xs_tok, "st_ys", reads=[R("yst0")], writes=[R("o_ys")])

        P.op("sp", None, reads=[R("o_y", 0, 4 * NBLK), R("o_pp"), R("o_cp"), R("o_sp"), R("o_ys"), R("o_ps"), R("o_cs"), R("o_ss", 0, NS)])

        P.analyze()
        sems_e = {e: es.enter_context(nc.semaphore("se_" + e)) for e in ENGS}
        sems_d = {k: es.enter_context(nc.semaphore("sd_" + k)) for k in sorted(dma_keys)}
        P.emit(sems_e, sems_d)
    return nc


def _tile_k(w):
    K, N = w.shape
    return np.ascontiguousarray(w.reshape(K // 128, 128, N).transpose(1, 0, 2))


def _fm(v):
    return np.ascontiguousarray(v.reshape(-1, 128).T)


_NC_CACHE = {}


def kernel(x_prompt, x_sample, c_prompt, c_sample, state_pool, state_conv, state_ssm, w_ada, b_ada, norm1_w,
           w_in, w_pool, pool_scale, conv_w, conv_b, dt_bias, A_log, D_skip, ssd_norm_w, w_ssd_proj, w_out,
           norm2_w, w_ffn_in, w_ffn_out, final_norm_w):
    f = np.float32
    n = 8
    x_prompt = np.asarray(x_prompt, f)
    pvec = np.zeros((128, PV_N), f)
    pvec[:, PV_N1W:PV_N1W + 8] = _fm(np.asarray(norm1_w[0], f))
    pvec[:, PV_PSC:PV_PSC + 8] = _fm(np.asarray(pool_scale[0], f))
    cw = np.asarray(conv_w[0], f)
    for k in range(4):
        pvec[:, PV_CW + 32 * k:PV_CW + 32 * k + 32] = _fm(cw[k])
    pvec[:, PV_CB:PV_CB + 32] = _fm(np.asarray(conv_b[0], f))
    pvec[:, PV_N2W:PV_N2W + 8] = _fm(np.asarray(norm2_w[0], f))
    pvec[:, PV_BADA:PV_BADA + 48] = _fm(np.asarray(b_ada[0], f))
    pvec[:, PV_DF:PV_DF + 16] = _fm(np.repeat(np.asarray(D_skip[0], f), HP))
    rowb = np.zeros((128, RB_N), f)
    rowb[:, RB_FNW:RB_FNW + D] = np.asarray(final_norm_w, f)[None, :]
    bg = np.zeros((128, 2 * D), f)
    bg[:, 0:D] = np.asarray(b_ada[0], f)[None, 2 * D:3 * D]
    bg[:, D:2 * D] = np.asarray(b_ada[0], f)[None, 5 * D:6 * D]
    rowb[:, RB_DSK:RB_DSK + 32] = np.asarray(D_skip[0], f)[None, :]
    rowb[:, RB_ALOG:RB_ALOG + 32] = np.asarray(A_log[0], f)[None, :]
    rowb[:, RB_DTB:RB_DTB + 32] = np.asarray(dt_bias[0], f)[None, :]
    snw = np.ascontiguousarray(np.broadcast_to(np.asarray(ssd_norm_w[0], f)[None, :], (128, DI)))
    w_ada_t = _tile_k(np.asarray(w_ada[0], f))
    w_in_t = _tile_k(np.asarray(w_in[0], f))
    wp = np.asarray(w_pool[0], f)
    w_pool_t = np.ascontiguousarray(np.stack([_tile_k(wp[g]) for g in range(4)], axis=1))
    w_ssd_t = _tile_k(np.asarray(w_ssd_proj[0], f))
    w_out_t = _tile_k(np.asarray(w_out[0], f))
    wfi = np.asarray(w_ffn_in[0], f)
    perm = np.concatenate([np.concatenate([np.arange(256 * t, 256 * t + 256), DFF + np.arange(256 * t, 256 * t + 256)]) for t in range(11)])
    w_ffi_t = _tile_k(np.ascontiguousarray(wfi[:, perm]))
    w_ffo_t = _tile_k(np.asarray(w_ffn_out[0], f))

    in_maps = []
    for b in range(n):
        s0, s1 = NS * b, NS * (b + 1)
        c17 = np.concatenate([np.asarray(c_prompt[b:b + 1], f), np.asarray(c_sample[s0:s1], f)], axis=0)
        cT = np.ascontiguousarray(c17.T.reshape(8, 128, 17).transpose(1, 0, 2))
        sp = np.asarray(state_pool[0, s0:s1], f)
        sp_t = np.ascontiguousarray(sp.reshape(NS, 15, 8, 128).transpose(3, 2, 0, 1))
        sc = np.asarray(state_conv[0, s0:s1], f)
        sc_t = np.ascontiguousarray(sc.reshape(NS, 3, 32, 128).transpose(3, 2, 0, 1))
        ss = np.asarray(state_ssm[0, s0:s1], f)
        ss_t = np.ascontiguousarray(ss.reshape(NS, DI, DST).transpose(0, 2, 1))
        in_maps.append({
            "xp": np.ascontiguousarray(x_prompt[b]),
            "xsm": np.ascontiguousarray(np.asarray(x_sample[s0:s1, 0], f)),
            "cT": cT, "pvec": pvec, "rowb": rowb, "snw": snw, "bg": bg,
            "w_ada": w_ada_t, "w_in": w_in_t, "w_pool": w_pool_t, "w_ssd": w_ssd_t, "w_out": w_out_t,
            "w_ffi": w_ffi_t, "w_ffo": w_ffo_t,
            "st_pool": sp_t, "st_conv": sc_t, "st_ssm": ss_t,
        })
    nblk = int(os.environ.get("K_NBLK", NBLK))
    ncores = int(os.environ.get("K_CORES", n))
    if "nc" not in _NC_CACHE:
        _NC_CACHE["nc"] = build_nc(nblk=nblk)
    nc = _NC_CACHE["nc"]
    res = run_bass_kernel_spmd(nc, in_maps[:ncores], core_ids=list(range(ncores)))
    rs = list(res.results)
    while len(rs) < n:
        rs.append({k: np.zeros_like(v) for k, v in rs[0].items()})
    y_prompt = np.stack([rs[b]["o_y"] for b in range(n)], axis=0)
    y_sample = np.concatenate([rs[b]["o_ys"] for b in range(n)], axis=0)[:, None, :]
    pool_p = np.stack([rs[b]["o_pp"].transpose(2, 1, 0).reshape(15, D) for b in range(n)], axis=0)[None]
    conv_p = np.stack([rs[b]["o_cp"].transpose(2, 1, 0).reshape(3, CONV) for b in range(n)], axis=0)[None]
    ssm_p = np.stack([rs[b]["o_sp"].T.reshape(NH, HP, DST) for b in range(n)], axis=0)[None]
    pool_s = np.concatenate([rs[b]["o_ps"].transpose(2, 3, 1, 0).reshape(NS, 15, D) for b in range(n)], axis=0)[None]
    conv_s = np.concatenate([rs[b]["o_cs"].transpose(2, 3, 1, 0).reshape(NS, 3, CONV) for b in range(n)], axis=0)[None]
    ssm_s = np.concatenate([rs[b]["o_ss"].transpose(0, 2, 1).reshape(NS, NH, HP, DST) for b in range(n)], axis=0)[None]
    return (np.ascontiguousarray(y_prompt, dtype=f), np.ascontiguousarray(y_sample, dtype=f),
            np.ascontiguousarray(pool_p, dtype=f), np.ascontiguousarray(conv_p, dtype=f),
            np.ascontiguousarray(ssm_p, dtype=f), np.ascontiguousarray(pool_s, dtype=f),
            np.ascontiguousarray(conv_s, dtype=f), np.ascontiguousarray(ssm_s, dtype=f))
```
